# Optimizing a Trainium2 kernel written in Bass

```python
import jax
import jax.numpy as jnp
from jax import lax
import numpy as np

D_MODEL = 1024
BATCH = 8
SEQ = 2048
DEPTH = 4

N_MIXERS = 3
N_RWKV = (DEPTH + 2) // N_MIXERS
N_RET = (DEPTH + 1) // N_MIXERS
N_MOBA = DEPTH // N_MIXERS
RWKV_HEAD = 64
RWKV_HEADS = D_MODEL // RWKV_HEAD
RWKV_DECAY_LORA = 64
RWKV_AAA_LORA = 64
RWKV_GATE_LORA = 160
RWKV_GN_EPS = 64e-5
RET_HEADS = 4
RET_QK_DIM = D_MODEL // RET_HEADS
RET_V_DIM = 2 * D_MODEL // RET_HEADS
RET_CHUNK = 128
RET_ROPE_BASE = 10000.0
RET_GN_EPS = 1e-5
MOBA_HEADS = 16
MOBA_HEAD_DIM = D_MODEL // MOBA_HEADS
MOBA_BLOCK = 256
MOBA_TOPK = 3
MOBA_Q_CHUNK = 16
FFN_DIM = 2816
CONV_WIDTH = 3
LN_EPS = 1e-5
DEEPNORM_ALPHA = (2 * DEPTH) ** 0.25
DEEPNORM_BETA = (8 * DEPTH) ** -0.25

kernel_name = 'hybrid_rwkv7_retnet_moba_convffn'


def _layer_norm(x, g, b):
    xf = x.astype(jnp.float32)
    mu = jnp.mean(xf, -1, keepdims=True)
    var = jnp.mean(jnp.square(xf - mu), -1, keepdims=True)
    return ((xf - mu) * lax.rsqrt(var + LN_EPS) * g + b).astype(x.dtype)


def _head_norm(y, g, b, eps):
    yf = y.astype(jnp.float32)
    mu = jnp.mean(yf, -1, keepdims=True)
    var = jnp.mean(jnp.square(yf - mu), -1, keepdims=True)
    yn = ((yf - mu) * lax.rsqrt(var + eps)).reshape(*y.shape[:-2], -1)
    return yn * g + b


def _rwkv7_time_mix(x, mix, w_rkv, w0, w1, w2, a0, a1, a2, g1, g2, k_k, k_a, r_k, gn_g, gn_b, w_o):
    bsz, t_len, d = x.shape
    h, n = RWKV_HEADS, RWKV_HEAD
    f32 = jnp.float32
    xx = jnp.pad(x, ((0, 0), (1, 0), (0, 0)))[:, :-1] - x
    xr, xw, xk, xv, xa, xg = (x + xx * mix[i] for i in range(6))
    r = xr @ w_rkv[0]
    k = xk @ w_rkv[1]
    v = xv @ w_rkv[2]
    logw = -jax.nn.softplus(-(w0 + jnp.tanh(xw @ w1) @ w2).astype(f32)) - 0.5
    decay = jnp.exp(-jnp.exp(logw))
    a = jax.nn.sigmoid((a0 + (xa @ a1) @ a2).astype(f32))
    gate = jax.nn.sigmoid(xg @ g1) @ g2
    heads = lambda t: t.reshape(bsz, t_len, h, n).astype(f32)
    kk = heads(k * k_k)
    kk = kk / jnp.maximum(jnp.sqrt(jnp.sum(kk * kk, -1, keepdims=True)), 1e-12)
    k = k * (1.0 + (a - 1.0) * k_a)
    r_h, k_h, v_h, w_h, a_h = heads(r), heads(k), heads(v), heads(decay), heads(a)

    def step(state, inp):
        r_t, w_t, k_t, v_t, kk_t, a_t = inp
        sa = jnp.einsum('bhij,bhj->bhi', state, -kk_t)
        state = (state * w_t[:, :, None, :]
                 + sa[..., None] * (kk_t * a_t)[:, :, None, :]
                 + v_t[..., None] * k_t[:, :, None, :])
        return state, jnp.einsum('bhij,bhj->bhi', state, r_t)

    xs = tuple(jnp.moveaxis(t, 1, 0) for t in (r_h, w_h, k_h, v_h, kk, a_h))
    s0 = jnp.zeros((bsz, h, n, n), f32)
    _, y = lax.scan(step, s0, xs)
    y = jnp.moveaxis(y, 0, 1)
    y = _head_norm(y, gn_g, gn_b, RWKV_GN_EPS)
    bonus = (jnp.sum(r_h * k_h * r_k, -1, keepdims=True) * v_h).reshape(bsz, t_len, d)
    return ((y + bonus) * gate).astype(x.dtype) @ w_o


def _rotate_every_two(x, pos):
    d = x.shape[-1]
    inv = 1.0 / (RET_ROPE_BASE ** jnp.linspace(0.0, 1.0, d // 2, dtype=jnp.float32))
    ang = pos[:, None].astype(jnp.float32) * inv[None, :]
    cos = jnp.cos(ang)[None, :, None, :]
    sin = jnp.sin(ang)[None, :, None, :]
    xf = x.astype(jnp.float32).reshape(*x.shape[:-1], d // 2, 2)
    x1, x2 = xf[..., 0], xf[..., 1]
    return jnp.stack([x1 * cos - x2 * sin, x2 * cos + x1 * sin], -1).reshape(x.shape)


def _retention(x, w_in, gn_g, gn_b, w_o):
    bsz, t_len, d = x.shape
    h, dk, dv, c = RET_HEADS, RET_QK_DIM, RET_V_DIM, RET_CHUNK
    n_chunks = t_len // c
    f32 = jnp.float32
    proj = x @ w_in
    q, k, v, gate = jnp.split(proj, [d, 2 * d, 4 * d], axis=-1)
    pos = jnp.arange(t_len)
    q = _rotate_every_two(q.reshape(bsz, t_len, h, dk), pos)
    k = _rotate_every_two(k.reshape(bsz, t_len, h, dk), pos) * (dk ** -0.5)
    v = v.reshape(bsz, t_len, h, dv).astype(f32)
    log_gamma = jnp.log(1.0 - 2.0 ** (-5.0 - jnp.arange(h, dtype=f32)))
    idx = jnp.arange(c, dtype=f32)
    rel = idx[:, None] - idx[None, :]
    inner_decay = jnp.where(rel >= 0, jnp.exp(log_gamma[:, None, None] * jnp.maximum(rel, 0.0)), 0.0)
    cross_decay = jnp.exp(log_gamma[:, None] * (idx + 1.0))[None, :, :, None]
    state_decay = jnp.exp(log_gamma[:, None] * (c - 1.0 - idx))[None, :, :, None]
    chunk_decay = jnp.exp(log_gamma * c)[None, :, None, None]
    to_chunks = lambda t: t.reshape(bsz, n_chunks, c, h, t.shape[-1]).transpose(1, 0, 3, 2, 4)

    def step(state, inp):
        q_c, k_c, v_c = inp
        s = jnp.einsum('bhnd,bhmd->bhnm', q_c, k_c) * inner_decay
        o = (jnp.einsum('bhnm,bhme->bhne', s, v_c)
             + jnp.einsum('bhnd,bhde->bhne', q_c, state) * cross_decay)
        state = state * chunk_decay + jnp.einsum('bhmd,bhme->bhde', k_c * state_decay, v_c)
        return state, o

    s0 = jnp.zeros((bsz, h, dk, dv), f32)
    _, o = lax.scan(step, s0, (to_chunks(q), to_chunks(k), to_chunks(v)))
    o = o.transpose(1, 0, 3, 2, 4).reshape(bsz, t_len, h, dv)
    o = _head_norm(o, gn_g, gn_b, RET_GN_EPS)
    return (jax.nn.silu(gate.astype(f32)) * o).astype(x.dtype) @ w_o


def _moba_attention(x, w_qkv, w_o):
    bsz, t_len, d = x.shape
    h, hd, blk, qc = MOBA_HEADS, MOBA_HEAD_DIM, MOBA_BLOCK, MOBA_Q_CHUNK
    f32 = jnp.float32
    q, k, v = (t.reshape(bsz, t_len, h, hd).transpose(0, 2, 1, 3) for t in jnp.split(x @ w_qkv, 3, axis=-1))
    n_blk = -(-t_len // blk)
    pad = ((0, 0), (0, 0), (0, n_blk * blk - t_len), (0, 0))
    k_pad, v_pad = jnp.pad(k, pad), jnp.pad(v, pad)
    kb = k_pad.reshape(bsz, h, n_blk, blk, hd)
    vb = v_pad.reshape(bsz, h, n_blk, blk, hd)
    k_mean = jnp.mean(kb.astype(f32), axis=3)
    q_blk = jnp.arange(t_len) // blk
    gate = jnp.einsum('bhtd,bhnd->bhtn', q.astype(f32), k_mean)
    past = jnp.arange(n_blk)[None, :] < q_blk[:, None]
    gate = jnp.where(past, gate, -jnp.inf)
    topk = min(MOBA_TOPK, n_blk)
    _, sel = lax.top_k(gate, topk)
    sel_valid = jnp.arange(topk)[None, :] < q_blk[:, None]
    scale = hd ** -0.5
    bi = jnp.arange(bsz)[:, None, None, None]
    hi = jnp.arange(h)[None, :, None, None]

    def chunk(ci):
        t0 = ci * qc
        q_c = lax.dynamic_slice_in_dim(q, t0, qc, axis=2).astype(f32)
        sel_c = lax.dynamic_slice_in_dim(sel, t0, qc, axis=2)
        valid_c = lax.dynamic_slice_in_dim(sel_valid, t0, qc, axis=0)
        k_sel = kb[bi, hi, sel_c]
        v_sel = vb[bi, hi, sel_c]
        s_sel = jnp.einsum('bhqd,bhqskd->bhqsk', q_c, k_sel) * scale
        s_sel = jnp.where(valid_c[None, None, :, :, None], s_sel, -jnp.inf)
        own0 = (t0 // blk) * blk
        k_own = lax.dynamic_slice_in_dim(k_pad, own0, blk, axis=2)
        v_own = lax.dynamic_slice_in_dim(v_pad, own0, blk, axis=2)
        s_own = jnp.einsum('bhqd,bhkd->bhqk', q_c, k_own) * scale
        causal = (own0 + jnp.arange(blk))[None, :] <= (t0 + jnp.arange(qc))[:, None]
        s_own = jnp.where(causal, s_own, -jnp.inf)
        logits = jnp.concatenate([s_sel.reshape(bsz, h, qc, topk * blk), s_own], axis=-1)
        p = jax.nn.softmax(logits, axis=-1)
        p_sel = p[..., :topk * blk].reshape(bsz, h, qc, topk, blk)
        p_own = p[..., topk * blk:]
        return (jnp.einsum('bhqsk,bhqskd->bhqd', p_sel, v_sel)
                + jnp.einsum('bhqk,bhkd->bhqd', p_own, v_own))

    o = lax.map(chunk, jnp.arange(t_len // qc))
    o = o.transpose(1, 0, 3, 2, 4).reshape(bsz, t_len, d)
    return o.astype(x.dtype) @ w_o


def _conv_ffn(x, w_in, conv_w, conv_b, w_out):
    hdn = x @ w_in
    hdn = lax.conv_general_dilated(
        hdn, conv_w[:, None, :].astype(hdn.dtype), window_strides=(1,),
        padding=[(CONV_WIDTH - 1, 0)], dimension_numbers=('NWC', 'WIO', 'NWC'),
        feature_group_count=hdn.shape[-1]) + conv_b
    u, g = jnp.split(hdn, 2, axis=-1)
    return (jax.nn.silu(g) * u) @ w_out


def setup_inputs(seed: int = 0) -> dict:
    key = jax.random.key(seed)
    keys = iter(jax.random.split(key, 40))
    f32 = jnp.float32
    nrm = lambda shape, s: jax.random.normal(next(keys), shape, f32) * s
    uni = lambda shape, lo, hi: jax.random.uniform(next(keys), shape, f32, lo, hi)
    d, ff = D_MODEL, FFN_DIM
    return {
        'x': nrm((BATCH, SEQ, d), 1.0),
        'rwkv_mix': uni((N_RWKV, 6, d), 0.0, 1.0),
        'rwkv_w_rkv': nrm((N_RWKV, 3, d, d), d ** -0.5),
        'rwkv_w0': uni((N_RWKV, d), -5.0, 0.0),
        'rwkv_w1': nrm((N_RWKV, d, RWKV_DECAY_LORA), d ** -0.5),
        'rwkv_w2': nrm((N_RWKV, RWKV_DECAY_LORA, d), 0.1 * RWKV_DECAY_LORA ** -0.5),
        'rwkv_a0': nrm((N_RWKV, d), 0.5),
        'rwkv_a1': nrm((N_RWKV, d, RWKV_AAA_LORA), d ** -0.5),
        'rwkv_a2': nrm((N_RWKV, RWKV_AAA_LORA, d), 0.1 * RWKV_AAA_LORA ** -0.5),
        'rwkv_g1': nrm((N_RWKV, d, RWKV_GATE_LORA), d ** -0.5),
        'rwkv_g2': nrm((N_RWKV, RWKV_GATE_LORA, d), RWKV_GATE_LORA ** -0.5),
        'rwkv_k_k': 0.85 + nrm((N_RWKV, d), 0.05),
        'rwkv_k_a': 1.0 + nrm((N_RWKV, d), 0.05),
        'rwkv_r_k': nrm((N_RWKV, RWKV_HEADS, RWKV_HEAD), 0.1),
        'rwkv_gn_g': 1.0 + nrm((N_RWKV, d), 0.02),
        'rwkv_gn_b': nrm((N_RWKV, d), 0.02),
        'rwkv_w_o': nrm((N_RWKV, d, d), DEEPNORM_BETA * d ** -0.5),
        'ret_w_in': nrm((N_RET, d, 6 * d), d ** -0.5),
        'ret_gn_g': 1.0 + nrm((N_RET, 2 * d), 0.02),
        'ret_gn_b': nrm((N_RET, 2 * d), 0.02),
        'ret_w_o': nrm((N_RET, 2 * d, d), DEEPNORM_BETA * (2 * d) ** -0.5),
        'moba_w_qkv': nrm((N_MOBA, d, 3 * d), d ** -0.5),
        'moba_w_o': nrm((N_MOBA, d, d), DEEPNORM_BETA * d ** -0.5),
        'ffn_w_in': nrm((DEPTH, d, 2 * ff), d ** -0.5),
        'ffn_conv_w': nrm((DEPTH, CONV_WIDTH, 2 * ff), CONV_WIDTH ** -0.5),
        'ffn_conv_b': nrm((DEPTH, 2 * ff), 0.02),
        'ffn_w_out': nrm((DEPTH, ff, d), DEEPNORM_BETA * ff ** -0.5),
        'ln1_g': 1.0 + nrm((DEPTH, d), 0.02),
        'ln1_b': nrm((DEPTH, d), 0.02),
        'ln2_g': 1.0 + nrm((DEPTH, d), 0.02),
        'ln2_b': nrm((DEPTH, d), 0.02),
    }


def reference(x, rwkv_mix, rwkv_w_rkv, rwkv_w0, rwkv_w1, rwkv_w2, rwkv_a0, rwkv_a1, rwkv_a2,
              rwkv_g1, rwkv_g2, rwkv_k_k, rwkv_k_a, rwkv_r_k, rwkv_gn_g, rwkv_gn_b, rwkv_w_o,
              ret_w_in, ret_gn_g, ret_gn_b, ret_w_o, moba_w_qkv, moba_w_o,
              ffn_w_in, ffn_conv_w, ffn_conv_b, ffn_w_out, ln1_g, ln1_b, ln2_g, ln2_b):
    for i in range(DEPTH):
        kind, j = i % N_MIXERS, i // N_MIXERS
        if kind == 0:
            mixed = _rwkv7_time_mix(x, rwkv_mix[j], rwkv_w_rkv[j], rwkv_w0[j], rwkv_w1[j], rwkv_w2[j],
                                    rwkv_a0[j], rwkv_a1[j], rwkv_a2[j], rwkv_g1[j], rwkv_g2[j],
                                    rwkv_k_k[j], rwkv_k_a[j], rwkv_r_k[j], rwkv_gn_g[j], rwkv_gn_b[j],
                                    rwkv_w_o[j])
        elif kind == 1:
            mixed = _retention(x, ret_w_in[j], ret_gn_g[j], ret_gn_b[j], ret_w_o[j])
        else:
            mixed = _moba_attention(x, moba_w_qkv[j], moba_w_o[j])
        x = _layer_norm(DEEPNORM_ALPHA * x + mixed, ln1_g[i], ln1_b[i])
        x = _layer_norm(DEEPNORM_ALPHA * x + _conv_ffn(x, ffn_w_in[i], ffn_conv_w[i], ffn_conv_b[i], ffn_w_out[i]),
                        ln2_g[i], ln2_b[i])
    return x
```

```python
import numpy as np
import concourse.bass as bass
import concourse.mybir as mybir
from concourse.bass_utils import run_bass_kernel_spmd
from contextlib import ExitStack

_DBG = {}
F32 = mybir.dt.float32
BF16 = mybir.dt.bfloat16
AF = mybir.ActivationFunctionType
ALU = mybir.AluOpType
AX = mybir.AxisListType

ENGS = ("pe", "act", "dve", "pool", "sp")


class T:
    def __init__(self, S, t, name):
        self.S = S
        self.t = t
        self.name = name
        self.st = {}
        self.is_psum = False
        self.dsem = None
        self.dcnt = 0

    def __getitem__(self, idx):
        return self.t[idx]


class R:
    __slots__ = ("tile", "key")
    def __init__(self, tile, key=None):
        self.tile = tile
        self.key = key


class Sched:
    def __init__(self, nc):
        self.nc = nc
        self.es = ExitStack()
        self.ins = {e: [] for e in ENGS}
        self.cnt = {e: 0 for e in ENGS}
        self.seen = {e: {} for e in ENGS}
        self.sems = {}
        self.needed = {e: set() for e in ENGS}
        self.nsem = 0
        for e in ENGS:
            self.sems[("e", e)] = self.es.enter_context(nc.semaphore("sem_" + e))
        self.dma_pool = []
        self.out_dma = []
        self.pend = {}
        self.scopes = []

    def _es(self):
        return self.scopes[-1] if self.scopes else self.es

    def push(self):
        self.scopes.append(ExitStack())

    def pop(self):
        self.barrier()
        self.scopes.pop().close()

    def sbuf(self, name, shape, dt):
        self.uid = getattr(self, "uid", 0) + 1
        name = "%s_%d" % (name, self.uid)
        t = self._es().enter_context(self.nc.sbuf_tensor(name, list(shape), dt))
        return T(self, t, name)

    def psum(self, name, shape, dt=F32):
        self.uid = getattr(self, "uid", 0) + 1
        name = "%s_%d" % (name, self.uid)
        t = self._es().enter_context(self.nc.psum_tensor(name, list(shape), dt))
        tt = T(self, t, name)
        tt.is_psum = True
        return tt

    def new_dsem(self, name):
        self.nsem += 1
        key = ("d", self.nsem)
        self.sems[key] = self.es.enter_context(self.nc.semaphore("dsem%d_%s" % (self.nsem, name)))
        return key

    def _states(self, ref, create=True):
        tile, key = ref.tile, ref.key
        if key is None:
            if None not in tile.st:
                tile.st[None] = [None, {}]
            return list(tile.st.values())
        out = []
        if None in tile.st:
            out.append(tile.st[None])
        if key not in tile.st:
            tile.st[key] = [None, {}]
        out.append(tile.st[key])
        return out

    def _collect(self, eng, reads, writes):
        waits = {}
        def need(w, same_ok=False):
            if w is None:
                return
            k, v = w
            if same_ok and k == ("e", eng) and eng == "pe":
                return
            if waits.get(k, 0) < v:
                waits[k] = v
        for r in reads:
            for st in self._states(r):
                need(st[0])
                if r.tile.is_psum:
                    for rk, rv in st[1].items():
                        if rk != ("e", eng):
                            need((rk, rv))
        for w in writes:
            for st in self._states(w):
                need(st[0])
                for rk, rv in st[1].items():
                    if rk == ("e", eng):
                        continue
                    need((rk, rv))
        final = []
        for k, v in waits.items():
            if k == ("e", eng) and eng == "pe":
                continue
            if self.seen[eng].get(k, 0) < v:
                self.seen[eng][k] = v
                final.append((k, v))
                if k[0] == "e":
                    self.needed[k[1]].add(v)
        return final

    def _commit(self, tag, reads, writes):
        for r in reads:
            if r.key is None:
                for k, s in r.tile.st.items():
                    if s[1].get(tag[0], 0) < tag[1]:
                        s[1][tag[0]] = tag[1]
            else:
                s = r.tile.st[r.key]
                if s[1].get(tag[0], 0) < tag[1]:
                    s[1][tag[0]] = tag[1]
        for w in writes:
            if w.key is None:
                w.tile.st = {None: [tag, {}]}
            else:
                w.tile.st[w.key] = [tag, {}]

    def i(self, eng, meth, reads=(), writes=(), **kw):
        return self.op(eng, (meth, kw), reads, writes)

    def _norm(self, refs):
        out = []
        for r in refs:
            if not isinstance(r, R):
                r = R(r)
            if r.tile.is_psum and r.key is not None:
                r = R(r.tile)
            out.append(r)
        return out

    def op(self, eng, fn, reads=(), writes=()):
        reads = self._norm(reads)
        writes = self._norm(writes)
        waits = self._collect(eng, reads, writes)
        self.cnt[eng] += 1
        idx = self.cnt[eng]
        self.ins[eng].append([fn, waits, idx, None])
        self._commit((("e", eng), idx), reads, writes)
        return idx

    def dma(self, q, out_ap, in_ap, reads=(), writes=(), out_final=False, owner=None, **kw):
        reads = [r if isinstance(r, R) else R(r) for r in reads]
        writes = [w if isinstance(w, R) else R(w) for w in writes]
        waits = self._collect(q, reads, writes)
        if owner is None:
            owner = (writes[0] if writes else reads[0]).tile
        if owner.dsem is None:
            owner.dsem = self.new_dsem(owner.name)
        owner.dcnt += 16
        tag = (owner.dsem, owner.dcnt)
        self.cnt[q] += 1
        idx = self.cnt[q]
        self.ins[q].append([lambda e: e.dma_start(out=out_ap, in_=in_ap, **kw), waits, idx, tag])
        self._commit(tag, reads, writes)
        self.pend[tag[0]] = tag[1]
        if out_final:
            self.out_dma.append(tag)
        return tag

    def barrier(self):
        for f in ENGS:
            if self.cnt[f] and (self.ins[f][-1][0] is None or self.ins[f][-1][3] is not None):
                self.cnt[f] += 1
                self.ins[f].append([None, [], self.cnt[f], None])
        for e in ENGS:
            waits = []
            for f in ENGS:
                if f == e or self.cnt[f] == 0:
                    continue
                v = self.cnt[f]
                if self.seen[e].get(("e", f), 0) < v:
                    self.seen[e][("e", f)] = v
                    waits.append((("e", f), v))
                    self.needed[f].add(v)
            for k, v in self.pend.items():
                if self.seen[e].get(k, 0) < v:
                    self.seen[e][k] = v
                    waits.append((k, v))
            if waits:
                self.cnt[e] += 1
                self.ins[e].append([None, waits, self.cnt[e], None])

    def emit(self):
        nc = self.nc
        fwd = {}
        for k, v in self.out_dma:
            fwd[k] = max(fwd.get(k, 0), v)
        fw = list(fwd.items())
        self.cnt["sp"] += 1
        self.ins["sp"].append([None, fw, self.cnt["sp"], None])
        rank = {}
        for e in ENGS:
            s = sorted(self.needed[e])
            rank[e] = {v: i + 1 for i, v in enumerate(s)}
        engmap = {"pe": "tensor", "act": "scalar", "dve": "vector", "pool": "gpsimd", "sp": "sync"}
        with nc.Block() as block:
            for e in ENGS:
                lst = self.ins[e]
                if not lst:
                    continue
                def body(eng, e=e, lst=lst):
                    for fn, waits, idx, dtag in lst:
                        for k, v in waits:
                            if k[0] == "e":
                                eng.wait_ge(self.sems[k], rank[k[1]][v])
                            else:
                                eng.wait_ge(self.sems[k], v)
                        if fn is None:
                            if idx in self.needed[e]:
                                eng.nop().then_inc(self.sems[("e", e)], 1)
                            continue
                        ins = fn(eng) if callable(fn) else getattr(eng, fn[0])(**fn[1])
                        if dtag is not None:
                            ins.then_inc(self.sems[dtag[0]], 16)
                            if idx in self.needed[e]:
                                raise RuntimeError("dma instr needed as engine milestone")
                        elif idx in self.needed[e]:
                            ins.then_inc(self.sems[("e", e)], 1)
                getattr(block, engmap[e])(body)
        self.es.close()

D = 1024; T_SEQ = 2048; DEPTH = 4; FF = 2816; NCH = 8; NPAIR = 22
ALPHA = float((2 * DEPTH) ** 0.25)
LN_EPS = 1e-5

WEIGHT_SHAPES = {
    "rwkv_w_rkv": [2, 3, 1024, 1024], "rwkv_w1": [2, 1024, 64], "rwkv_w2": [2, 64, 1024],
    "rwkv_a1": [2, 1024, 64], "rwkv_a2": [2, 64, 1024], "rwkv_g1": [2, 1024, 160], "rwkv_g2": [2, 160, 1024],
    "rwkv_w_o": [2, 1024, 1024], "ret_w_in": [1, 1024, 6144], "ret_w_o": [1, 2048, 1024],
    "moba_w_qkv": [1, 1024, 3072], "moba_w_o": [1, 1024, 1024],
    "ffn_w_in": [4, 1024, 5632], "ffn_w_out": [4, 2816, 1024],
}


def _fm(v):
    v = np.asarray(v, np.float32).reshape(-1, 128)
    return np.ascontiguousarray(v.T)


def vec_layout():
    off = {}
    n = 0
    def add(name, cols):
        nonlocal n
        off[name] = n
        n += cols
    for i in range(4):
        for nm in ("ln1_g", "ln1_b", "ln2_g", "ln2_b"):
            add("%s%d" % (nm, i), 8)
        for k in range(3):
            add("cw%d_%d" % (k, i), 44)
        add("cb_%d" % i, 44)
    for j in range(2):
        for m in range(6):
            add("mix%d_%d" % (m, j), 8)
        for nm in ("w0", "a0", "k_k", "k_a", "r_k"):
            add("%s_%d" % (nm, j), 8)
    add("ret_gn_g", 16)
    add("ret_gn_b", 16)
    return off, n


def pack_vecs(inp):
    off, n = vec_layout()
    out = np.zeros((128, n), np.float32)
    def put(name, v):
        a = _fm(v)
        out[:, off[name]:off[name] + a.shape[1]] = a
    for i in range(4):
        for nm in ("ln1_g", "ln1_b", "ln2_g", "ln2_b"):
            put("%s%d" % (nm, i), inp[nm][i])
        for k in range(3):
            put("cw%d_%d" % (k, i), inp["ffn_conv_w"][i, k])
        put("cb_%d" % i, inp["ffn_conv_b"][i])
    for j in range(2):
        for m in range(6):
            put("mix%d_%d" % (m, j), inp["rwkv_mix"][j, m])
        put("w0_%d" % j, inp["rwkv_w0"][j]); put("a0_%d" % j, inp["rwkv_a0"][j])
        put("k_k_%d" % j, inp["rwkv_k_k"][j]); put("k_a_%d" % j, inp["rwkv_k_a"][j])
        put("r_k_%d" % j, inp["rwkv_r_k"][j].reshape(-1))
    put("ret_gn_g", inp["ret_gn_g"][0]); put("ret_gn_b", inp["ret_gn_b"][0])
    return out


class Ctx:
    pass


def build_program(plan, x_in_bf=True):
    nc = bass.Bass("TRN2", target_bir_lowering=False)
    S = Sched(nc)
    g = Ctx()
    g.nc, g.S = nc, S
    g.W = {k: nc.dram_tensor(k, shp, F32, kind="ExternalInput").ap() for k, shp in WEIGHT_SHAPES.items()}
    voff, nv = vec_layout()
    g.voff = voff
    xT_d = nc.dram_tensor("xT", [D, T_SEQ], F32, kind="ExternalInput").ap()
    vecs_d = nc.dram_tensor("vecs", [128, nv], F32, kind="ExternalInput").ap()
    consts_d = nc.dram_tensor("consts", [128, 1024], F32, kind="ExternalInput").ap()
    g.rows_d = nc.dram_tensor("rows", [4, 128, 1024], F32, kind="ExternalInput").ap()
    g.rtab_d = nc.dram_tensor("rtab", [2, 128, 2, T_SEQ], F32, kind="ExternalInput").ap()
    g.rmask_d = nc.dram_tensor("rmask", [4, 128, 384], F32, kind="ExternalInput").ap()
    g.ret_sw_d = nc.dram_tensor("ret_sw", [1024, 2048], F32, kind="ExternalInput").ap()
    g.mk_d = nc.dram_tensor("mk", [128, 3, 512], F32, kind="ExternalInput").ap()
    out_d = nc.dram_tensor("outT", [D, T_SEQ], F32, kind="ExternalOutput").ap()

    g.x = S.sbuf("x", [128, NCH, T_SEQ], F32)
    g.xb = S.sbuf("xb", [128, NCH, T_SEQ], BF16)
    g.vecs = S.sbuf("vecs", [128, nv], F32)
    g.cf = S.sbuf("cf", [128, 1024], F32)
    g.cb = S.sbuf("cb", [128, 1024], BF16)
    xv = xT_d.rearrange("(c p) t -> p c t", p=128)
    for c in range(NCH):
        S.dma("sp", g.x[:, c, :], xv[:, c, :], writes=[R(g.x, None)])
    for c in range(NCH):
        S.dma("pool", g.xb[:, c, :], xv[:, c, :], writes=[R(g.xb, None)])
    S.dma("sp", g.vecs[:], vecs_d, writes=[g.vecs])
    S.dma("sp", g.cf[:], consts_d, writes=[g.cf])
    S.dma("pool", g.cb[:], consts_d, writes=[g.cb])
    g.mk = S.sbuf("mk", [128, 3, 512], BF16)
    S.dma("pool", g.mk[:], g.mk_d, writes=[g.mk])
    g.ones32 = g.cf[:, 0:128]
    g.identb = g.cb[:, 128:256]

    for step in plan:
        kind = step[0]
        if kind == "ffn":
            ffn_phase(g, step[1])
        elif kind == "ln":
            S.push()
            layer_norm(g, step[1], range(4))
            S.pop()
        elif kind == "moba":
            moba_phase(g, step[1])
        elif kind == "ret":
            ret_phase(g, step[1])
        elif kind == "rwkv":
            rwkv_phase(g, step[1], step[2])
        else:
            raise ValueError(kind)

    S.barrier()
    ov = out_d.rearrange("(c p) t -> p c t", p=128)
    outsem = T(S, None, "outsem")
    for c in range(NCH):
        S.dma("sp", ov[:, c, :], g.x[:, c, :], reads=[R(g.x, None)], out_final=True)
    S.emit()
    return nc


def vcol(g, name, c=0):
    o = g.voff[name] + c
    return g.vecs[:, o:o + 1]


def layer_norm(g, pref, tts, width=512, write_xb=True):
    S = g.S
    gname, bname = pref
    x, xb = g.x, g.xb
    sq = [S.sbuf("lnsq", [128, 512], F32) for _ in range(2)]
    mean = S.sbuf("lnmean", [128, 512], F32)
    msq = S.sbuf("lnmsq", [128, 512], F32)
    rstd = S.sbuf("lnrstd", [128, 512], F32)
    tmp = [S.sbuf("lntmp", [128, 512], F32) for _ in range(2)]
    ps_s = S.psum("lnps", [128, 512])
    ps_q = S.psum("lnpq", [128, 512])
    ones = g.ones32
    W_ = width
    for tix in tts:
        ts = slice(tix * W_, (tix + 1) * W_)
        tt = (tix * W_) // 512
        for c in range(NCH):
            q = sq[c % 2]
            S.i("act", "activation", [R(x, (c, tt))], [q], out=q[:, 0:W_], in_=x[:, c, ts], func=AF.Square)
            S.i("pe", "matmul", [R(x, (c, tt)), g.cf], [ps_s], out=ps_s[:, 0:W_], lhsT=ones, rhs=x[:, c, ts], start=(c == 0), stop=(c == NCH - 1))
            S.i("pe", "matmul", [q, g.cf], [ps_q], out=ps_q[:, 0:W_], lhsT=ones, rhs=q[:, 0:W_], start=(c == 0), stop=(c == NCH - 1))
        S.i("dve", "tensor_scalar", [ps_s], [mean], out=mean[:, 0:W_], in0=ps_s[:, 0:W_], scalar1=1.0 / D, scalar2=None, op0=ALU.mult)
        S.i("dve", "tensor_tensor", [mean], [msq], out=msq[:, 0:W_], in0=mean[:, 0:W_], in1=mean[:, 0:W_], op=ALU.mult)
        S.i("dve", "scalar_tensor_tensor", [ps_q, msq], [msq], out=msq[:, 0:W_], in0=ps_q[:, 0:W_], scalar=1.0 / D, in1=msq[:, 0:W_], op0=ALU.mult, op1=ALU.subtract)
        S.i("dve", "tensor_scalar", [msq], [msq], out=msq[:, 0:W_], in0=msq[:, 0:W_], scalar1=LN_EPS, scalar2=None, op0=ALU.add)
        S.i("act", "activation", [msq], [rstd], out=rstd[:, 0:W_], in_=msq[:, 0:W_], func=AF.Sqrt)
        S.i("dve", "reciprocal", [rstd], [rstd], out=rstd[:, 0:W_], in_=rstd[:, 0:W_])
        for c in range(NCH):
            tm = tmp[c % 2]
            S.i("dve", "tensor_tensor", [R(x, (c, tt)), mean], [tm], out=tm[:, 0:W_], in0=x[:, c, ts], in1=mean[:, 0:W_], op=ALU.subtract)
            S.i("pool", "tensor_tensor", [tm, rstd], [tm], out=tm[:, 0:W_], in0=tm[:, 0:W_], in1=rstd[:, 0:W_], op=ALU.mult)
            S.i("act", "activation", [tm, g.vecs], [R(x, (c, tt))], out=x[:, c, ts], in_=tm[:, 0:W_], func=AF.Identity,
                bias=vcol(g, bname, c), scale=vcol(g, gname, c))
            if write_xb:
                S.i("pool", "tensor_copy", [R(x, (c, tt))], [R(xb, (c, tt))], out=xb[:, c, ts], in_=x[:, c, ts])


def ffn_phase(g, i):
    S = g.S
    x, xb = g.x, g.xb
    S.push()
    W_in = g.W["ffn_w_in"][i].rearrange("(k p) (two f) -> p k two f", p=128, two=2)
    W_out = g.W["ffn_w_out"][i].rearrange("(k p) n -> p k n", p=128)
    TB = 1024
    a = S.sbuf("ffa", [128, NPAIR, TB], BF16)
    hu = S.sbuf("hu", [128, TB + 2], F32)
    hg = S.sbuf("hg", [128, TB + 2], F32)
    au = S.sbuf("au", [128, TB], F32)
    ag = S.sbuf("ag", [128, TB], F32)
    tg = S.sbuf("tg", [128, TB], F32)
    halo = S.sbuf("halo", [128, 2 * NPAIR, 2], F32)
    wps = [[S.sbuf("wp", [128, NCH, 128], BF16) for _ in range(2)] for _ in range(2)]
    wos = [S.sbuf("wo", [128, NPAIR, 128], BF16) for _ in range(2)]
    pb = [S.psum("ffps", [128, 512]) for _ in range(8)]
    cw = lambda k, j: vcol(g, "cw%d_%d" % (k, i), j)
    cbv = lambda j: vcol(g, "cb_%d" % i, j)
    nw = 0
    for blk in range(2):
        for j in range(NPAIR):
            wp = wps[nw % 2]; nw += 1
            for ug in range(2):
                S.dma("pool", wp[ug][:], W_in[:, :, ug, j * 128:(j + 1) * 128], writes=[wp[ug]])
            banks = pb[(j % 2) * 4:(j % 2) * 4 + 4]
            for ug in range(2):
                for h in range(2):
                    bk = banks[ug * 2 + h]
                    tt = blk * 2 + h
                    for k in range(NCH):
                        S.i("pe", "matmul", [wp[ug], R(xb, (k, tt))], [bk], out=bk[:], lhsT=wp[ug][:, k, :],
                            rhs=xb[:, k, tt * 512:(tt + 1) * 512], start=(k == 0), stop=(k == NCH - 1))
            for ug, hb in ((0, hu), (1, hg)):
                for h in range(2):
                    bk = banks[ug * 2 + h]
                    S.i("act", "activation", [bk], [R(hb, h)], out=hb[:, 2 + h * 512:2 + (h + 1) * 512], in_=bk[:], func=AF.Copy)
                if blk == 0:
                    S.i("pool", "memset", [], [R(hb, "halo")], ap=hb[:, 0:2], constant=0.0)
                else:
                    S.i("pool", "tensor_copy", [R(halo, (ug, j))], [R(hb, "halo")], out=hb[:, 0:2], in_=halo[:, ug * NPAIR + j, :])
            ju, jg = j, NPAIR + j
            S.i("dve", "tensor_scalar", [hu, g.vecs], [au], out=au[:], in0=hu[:, 2:TB + 2], scalar1=cw(2, ju), scalar2=cbv(ju), op0=ALU.mult, op1=ALU.add)
            S.i("dve", "scalar_tensor_tensor", [hu, au], [au], out=au[:], in0=hu[:, 1:TB + 1], scalar=cw(1, ju), in1=au[:], op0=ALU.mult, op1=ALU.add)
            S.i("dve", "scalar_tensor_tensor", [hu, au], [au], out=au[:], in0=hu[:, 0:TB], scalar=cw(0, ju), in1=au[:], op0=ALU.mult, op1=ALU.add)
            S.i("pool", "tensor_scalar", [hg, g.vecs], [ag], out=ag[:], in0=hg[:, 2:TB + 2], scalar1=cw(2, jg), scalar2=cbv(jg), op0=ALU.mult, op1=ALU.add)
            for k in (1, 0):
                S.i("pool", "tensor_scalar", [hg], [tg], out=tg[:], in0=hg[:, k:TB + k], scalar1=cw(k, jg), scalar2=None, op0=ALU.mult)
                S.i("pool", "tensor_tensor", [ag, tg], [ag], out=ag[:], in0=ag[:], in1=tg[:], op=ALU.add)
            S.i("act", "activation", [ag], [ag], out=ag[:], in_=ag[:], func=AF.Silu)
            S.i("dve", "tensor_tensor", [au, ag], [R(a, j)], out=a[:, j, :], in0=au[:], in1=ag[:], op=ALU.mult)
            if blk == 0:
                for ug, hb in ((0, hu), (1, hg)):
                    S.i("pool", "tensor_copy", [hb], [R(halo, (ug, j))], out=halo[:, ug * NPAIR + j, :], in_=hb[:, TB:TB + 2])
        for m in range(NCH):
            wo = wos[m % 2]
            S.dma("pool", wo[:], W_out[:, :, m * 128:(m + 1) * 128], writes=[wo])
            for h in range(2):
                bk = pb[(m * 2 + h) % 8]
                tt = blk * 2 + h
                for k in range(NPAIR):
                    S.i("pe", "matmul", [wo, R(a, k)], [bk], out=bk[:], lhsT=wo[:, k, :], rhs=a[:, k, h * 512:(h + 1) * 512],
                        start=(k == 0), stop=(k == NPAIR - 1))
                xs = x[:, m, tt * 512:(tt + 1) * 512]
                S.i("dve", "scalar_tensor_tensor", [bk, R(x, (m, tt))], [R(x, (m, tt))], out=xs, in0=xs, scalar=ALPHA, in1=bk[:],
                    op0=ALU.mult, op1=ALU.add)
    S.pop()
    S.push()
    layer_norm(g, ("ln2_g%d" % i, "ln2_b%d" % i), range(4))
    S.pop()

def moba_phase(g, i):
    S = g.S
    x, xb = g.x, g.xb
    S.push()
    Wqkv = g.W["moba_w_qkv"][0].rearrange("(k p) n -> p k n", p=128)
    Wo = g.W["moba_w_o"][0].rearrange("(k p) n -> p k n", p=128)
    ogT = S.sbuf("ogT", [128, NCH, T_SEQ], BF16)
    wsl = [[S.sbuf("mw", [128, NCH, 128], BF16) for _ in range(3)] for _ in range(2)]
    qkv = [[S.sbuf("mqkv", [128, T_SEQ], BF16) for _ in range(3)] for _ in range(2)]
    vtms = [S.sbuf("vtm", [128, 16, 128], BF16) for _ in range(2)]
    ksum = S.sbuf("ksum", [128, 8], F32)
    kmean = S.sbuf("kmean", [128, 8], BF16)
    P = S.sbuf("mP", [128, T_SEQ], BF16)
    PT = S.sbuf("mPT", [128, 16, 128], BF16)
    sd = S.sbuf("msd", [128, 128], F32)
    g8 = S.sbuf("g8", [128, 8], F32)
    top8 = S.sbuf("top8", [128, 8], F32)
    mb = S.sbuf("mb", [128, 8], F32)
    b8 = S.sbuf("b8", [128, 8], F32)
    nm = S.sbuf("nm", [128, 4], F32)
    rs = S.sbuf("rs", [128, 12], F32)
    rinv = S.sbuf("rinv", [128, 2], F32)
    otm = S.sbuf("otm", [128, 128], BF16)
    wos = [S.sbuf("mwo", [128, NCH, 128], BF16) for _ in range(2)]
    sc = S.psum("msc", [128, 2048])
    pT = S.psum("mpT", [128, 1024], BF16)
    gps = S.psum("mgps", [128, 512])
    ov = S.psum("mov", [128, 512])
    pj = S.psum("mpj", [128, 512])
    tri = g.cf[:, 384:512]
    if _DBG.get('dbg_memset'):
        S.i('pool', 'memset', [], [ogT], ap=ogT[:], constant=0.0)
    ident = g.identb
    for c in range(_DBG.get('moba_pairs', NCH)):
        ws = wsl[c % 2]
        qT, kT, vT = qkv[c % 2]
        vtm = vtms[c % 2]
        for w in range(3):
            S.dma("pool", ws[w][:], Wqkv[:, :, w * 1024 + c * 128: w * 1024 + (c + 1) * 128], writes=[ws[w]])
        for w, dst in ((0, qT), (1, kT), (2, vT)):
            for tt in range(4):
                for k in range(NCH):
                    S.i("pe", "matmul", [ws[w], R(xb, (k, tt))], [pj], out=pj[:], lhsT=ws[w][:, k, :], rhs=xb[:, k, tt * 512:(tt + 1) * 512],
                        start=(k == 0), stop=(k == NCH - 1))
                if w == 0:
                    S.i("act", "activation", [pj], [R(dst, tt)], out=dst[:, tt * 512:(tt + 1) * 512], in_=pj[:], func=AF.Copy, scale=0.125)
                else:
                    S.i("act", "activation", [pj], [R(dst, tt)], out=dst[:, tt * 512:(tt + 1) * 512], in_=pj[:], func=AF.Copy)
                if w == 1 and _DBG.get('moba_stage', 9) >= 0.2:
                    S.i("dve", "tensor_reduce", [pj], [R(ksum, tt)], out=ksum[:, 2 * tt:2 * tt + 2],
                        in_=pj[:].rearrange("p (b k) -> p b k", b=2), axis=AX.X, op=ALU.add)
        if _DBG.get('moba_stage', 9) >= 0.2:
            S.i("dve", "tensor_scalar", [ksum], [kmean], out=kmean[:], in0=ksum[:], scalar1=1.0 / 256.0, scalar2=None, op0=ALU.mult)
        for b in range(2 if _DBG.get('moba_stage', 9) >= 0.3 else 0):
            for t8 in range(8):
                kt = b * 8 + t8
                S.i("pe", "transpose", [vT, g.cb], [pT], out=pT[:, t8 * 128:(t8 + 1) * 128], in_=vT[:, kt * 128:(kt + 1) * 128], identity=ident)
            S.i("dve", "tensor_copy", [pT], [R(vtm, b)], out=vtm[:, b * 8:(b + 1) * 8, :].rearrange("p a b -> p (a b)"), in_=pT[:])
        stg = _DBG.get('moba_stage', 9)
        for qt in _DBG.get('moba_qts', range(16)):
            if stg < 2:
                break
            qb = qt // 2
            nkt = qt + 1
            qs = slice(qt * 128, (qt + 1) * 128)
            for hh in range(2):
                ps = slice(hh * 64, (hh + 1) * 64)
                use_thr = qb >= 4
                if use_thr:
                    S.i("pe", "matmul", [qT, kmean], [gps], out=gps[:, 0:8], lhsT=qT[ps, qs], rhs=kmean[ps, 0:8], start=True, stop=True)
                    S.i("pool", "memset", [], [R(g8, "pad")], ap=g8[:, qb:8], constant=-1.0e30)
                    S.i("dve", "tensor_copy", [gps], [R(g8, "val")], out=g8[:, 0:qb], in_=gps[:, 0:qb])
                    S.i("dve", "max", [g8], [top8], out=top8[:], in_=g8[:])
                    S.i("dve", "tensor_scalar", [g8, top8], [mb], out=mb[:], in0=g8[:], scalar1=top8[:, 2:3], scalar2=30000.0,
                        op0=ALU.is_ge, op1=ALU.mult)
                ncol = nkt * 128
                for b in range((ncol + 511) // 512):
                    w_ = min(512, ncol - b * 512)
                    S.i("pe", "matmul", [qT, kT], [sc], out=sc[:, b * 512:b * 512 + w_], lhsT=qT[ps, qs], rhs=kT[ps, b * 512:b * 512 + w_],
                        start=True, stop=True)
                S.i("dve", "tensor_tensor", [sc, g.cf], [sd], out=sd[:], in0=sc[:, qt * 128:(qt + 1) * 128], in1=tri, op=ALU.add)
                S.i("dve", "tensor_reduce", [sd], [R(nm, 0)], out=nm[:, 0:1], in_=sd[:], axis=AX.X, op=ALU.max, negate=True)
                if qt > 0:
                    S.i("dve", "tensor_reduce", [sc], [R(nm, 1)], out=nm[:, 1:2], in_=sc[:, 0:qt * 128], axis=AX.X, op=ALU.max, negate=True)
                    S.i("dve", "tensor_tensor", [nm], [R(nm, 2)], out=nm[:, 2:3], in0=nm[:, 0:1], in1=nm[:, 1:2], op=ALU.min)
                    negm = nm[:, 2:3]
                else:
                    negm = nm[:, 0:1]
                if stg < 3:
                    continue
                if use_thr:
                    S.i("dve", "tensor_scalar", [mb, nm], [b8], out=b8[:], in0=mb[:], scalar1=negm, scalar2=-30000.0, op0=ALU.add, op1=ALU.add)
                npz = 0
                if use_thr:
                    for n in range(qb):
                        S.i("act", "activation", [sc, b8], [R(P, n), R(rs, npz)], out=P[:, n * 256:(n + 1) * 256], in_=sc[:, n * 256:(n + 1) * 256],
                            func=AF.Exp, bias=b8[:, n:n + 1], scale=1.0, accum_out=rs[:, npz:npz + 1])
                        npz += 1
                    if qt % 2 == 1:
                        S.i("act", "activation", [sc, nm], [R(P, "o"), R(rs, npz)], out=P[:, (qt - 1) * 128:qt * 128], in_=sc[:, (qt - 1) * 128:qt * 128],
                            func=AF.Exp, bias=negm, scale=1.0, accum_out=rs[:, npz:npz + 1])
                        npz += 1
                elif qt > 0:
                    S.i("act", "activation", [sc, nm], [R(P, "past"), R(rs, npz)], out=P[:, 0:qt * 128], in_=sc[:, 0:qt * 128],
                        func=AF.Exp, bias=negm, scale=1.0, accum_out=rs[:, npz:npz + 1])
                    npz += 1
                S.i("act", "activation", [sd, nm], [R(P, "d"), R(rs, npz)], out=P[:, qt * 128:(qt + 1) * 128], in_=sd[:],
                    func=AF.Exp, bias=negm, scale=1.0, accum_out=rs[:, npz:npz + 1])
                npz += 1
                S.i("dve", "tensor_reduce", [rs], [R(rs, 11)], out=rs[:, 11:12], in_=rs[:, 0:npz], axis=AX.X, op=ALU.add)
                S.i("dve", "reciprocal", [R(rs, 11)], [R(rinv, hh)], out=rinv[:, hh:hh + 1], in_=rs[:, 11:12])
                if stg < 4:
                    continue
                for b in range((nkt + 7) // 8):
                    n8 = min(8, nkt - b * 8)
                    for t8 in range(n8):
                        kt = b * 8 + t8
                        S.i("pe", "transpose", [P, g.cb], [pT], out=pT[:, t8 * 128:(t8 + 1) * 128], in_=P[:, kt * 128:(kt + 1) * 128], identity=ident)
                    S.i("dve", "tensor_copy", [pT], [R(PT, b)], out=PT[:, b * 8:b * 8 + n8, :].rearrange("p a b -> p (a b)"), in_=pT[:, 0:n8 * 128])
                for kt in range(nkt):
                    S.i("pe", "matmul", [PT, vtm], [R(ov, hh)], out=ov[:, hh * 64:(hh + 1) * 64], lhsT=PT[:, kt, :], rhs=vtm[:, kt, hh * 64:(hh + 1) * 64],
                        start=(kt == 0), stop=(kt == nkt - 1))
                S.i("dve", "tensor_scalar", [R(ov, hh), R(rinv, hh)], [R(otm, hh)], out=otm[:, hh * 64:(hh + 1) * 64], in0=ov[:, hh * 64:(hh + 1) * 64],
                    scalar1=rinv[:, hh:hh + 1], scalar2=None, op0=ALU.mult)
            if stg < 5:
                continue
            S.i("pe", "transpose", [otm, g.cb], [pT], out=pT[:, 0:128], in_=otm[:], identity=ident)
            S.i("act", "activation", [pT], [R(ogT, (c, qt))], out=ogT[:, c, qs], in_=pT[:, 0:128], func=AF.Copy)
    for m in range(NCH):
        wo = wos[m % 2]
        S.dma("pool", wo[:], Wo[:, :, m * 128:(m + 1) * 128], writes=[wo])
        for tt in range(4):
            for k in range(NCH):
                S.i("pe", "matmul", [wo, ogT], [pj], out=pj[:], lhsT=wo[:, k, :], rhs=ogT[:, k, tt * 512:(tt + 1) * 512], start=(k == 0), stop=(k == NCH - 1))
            xs = x[:, m, tt * 512:(tt + 1) * 512]
            S.i("dve", "scalar_tensor_tensor", [pj, R(x, (m, tt))], [R(x, (m, tt))], out=xs, in0=xs, scalar=ALPHA, in1=pj[:], op0=ALU.mult, op1=ALU.add)
    S.pop()
    S.push()
    layer_norm(g, ("ln1_g%d" % i, "ln1_b%d" % i), range(4))
    S.pop()

def ret_phase(g, i):
    S = g.S
    x, xb = g.x, g.xb
    S.push()
    Win = g.W["ret_w_in"][0].rearrange("(k p) n -> p k n", p=128)
    Wsw = g.ret_sw_d.rearrange("(k p) n -> p k n", p=128)
    Wo = g.W["ret_w_o"][0].rearrange("(k p) n -> p k n", p=128)
    wq = S.sbuf("rwq", [128, NCH, 256], BF16); wqs = S.sbuf("rwqs", [128, NCH, 256], BF16)
    wk = S.sbuf("rwk", [128, NCH, 256], BF16); wks = S.sbuf("rwks", [128, NCH, 256], BF16)
    wv = S.sbuf("rwv", [128, NCH, 512], BF16); wg = S.sbuf("rwg", [128, NCH, 512], BF16)
    wo = S.sbuf("rwo", [128, 4, 1024], BF16)
    rm = S.sbuf("rrm", [128, 384], F32)
    tabs = [S.sbuf("rtab", [128, 2, 2, 512], F32) for _ in range(2)]
    qrot = S.sbuf("qrot", [128, 2, 512], BF16); krot = S.sbuf("krot", [128, 2, 512], BF16)
    vT = S.sbuf("rvT", [128, 4, 512], BF16); sgT = S.sbuf("rsgT", [128, 4, 512], BF16)
    vtm = S.sbuf("rvtm", [128, 4, 512], BF16); ktm = S.sbuf("rktm", [128, 4, 256], BF16)
    og = S.sbuf("rog", [128, 4, 512], BF16)
    S32 = S.sbuf("rS32", [128, 2, 512], F32); Sb = S.sbuf("rSb", [128, 2, 512], BF16)
    t1 = S.sbuf("rt1", [128, 512], F32); t2 = S.sbuf("rt2", [128, 512], F32)
    ST = S.sbuf("rST", [128, 128], BF16); qcd = S.sbuf("rqcd", [128, 2, 128], BF16)
    osb = S.sbuf("rosb", [128, 512], F32); osq = S.sbuf("rosq", [128, 512], F32)
    mean = S.sbuf("rmean", [128, 128], F32); msq = S.sbuf("rmsq", [128, 128], F32); rstd = S.sbuf("rrstd", [128, 128], F32)
    tn = [S.sbuf("rtn", [128, 128], F32) for _ in range(2)]
    pA = S.psum("rpA", [128, 512]); pB = S.psum("rpB", [128, 512])
    sps = S.psum("rsps", [128, 512]); ops = S.psum("rops", [128, 512])
    ups = [S.psum("rups", [128, 512]) for _ in range(2)]
    pT = S.psum("rpT", [128, 1024], BF16)
    pst = S.psum("rpst", [128, 512])
    ident = g.identb
    ones = g.ones32
    ntab = 0
    for h in range(4):
        S.dma("pool", wq[:], Win[:, :, h * 256:(h + 1) * 256], writes=[wq])
        S.dma("pool", wqs[:], Wsw[:, :, h * 256:(h + 1) * 256], writes=[wqs])
        S.dma("pool", wk[:], Win[:, :, 1024 + h * 256:1024 + (h + 1) * 256], writes=[wk])
        S.dma("pool", wks[:], Wsw[:, :, 1024 + h * 256:1024 + (h + 1) * 256], writes=[wks])
        S.dma("pool", wv[:], Win[:, :, 2048 + h * 512:2048 + (h + 1) * 512], writes=[wv])
        S.dma("pool", wg[:], Win[:, :, 4096 + h * 512:4096 + (h + 1) * 512], writes=[wg])
        S.dma("pool", wo[:], Wo[:, h * 4:(h + 1) * 4, :], writes=[wo])
        S.dma("sp", rm[:], g.rmask_d[h], writes=[rm])
        S.i("pool", "memset", [], [S32], ap=S32[:], constant=0.0)
        S.i("pool", "memset", [], [Sb], ap=Sb[:], constant=0.0)
        for tt in range(4):
            ts = slice(tt * 512, (tt + 1) * 512)
            tab = tabs[ntab % 2]; ntab += 1
            for cs_ in range(2):
                S.dma("sp", tab[:, cs_, :, :], g.rtab_d[cs_][:, :, ts], writes=[R(tab, None)])
            for (wa, wb, dst) in ((wq, wqs, qrot), (wk, wks, krot)):
                for dc in range(2):
                    for k in range(NCH):
                        S.i("pe", "matmul", [wa, R(xb, (k, tt))], [pA], out=pA[:], lhsT=wa[:, k, dc * 128:(dc + 1) * 128], rhs=xb[:, k, ts],
                            start=(k == 0), stop=(k == NCH - 1))
                    for k in range(NCH):
                        S.i("pe", "matmul", [wb, R(xb, (k, tt))], [pB], out=pB[:], lhsT=wb[:, k, dc * 128:(dc + 1) * 128], rhs=xb[:, k, ts],
                            start=(k == 0), stop=(k == NCH - 1))
                    S.i("dve", "tensor_tensor", [pA, tab], [t1], out=t1[:], in0=pA[:], in1=tab[:, 0, dc, :], op=ALU.mult)
                    S.i("dve", "tensor_tensor", [pB, tab], [t2], out=t2[:], in0=pB[:], in1=tab[:, 1, dc, :], op=ALU.mult)
                    S.i("pool", "tensor_tensor", [t1, t2], [R(dst, dc)], out=dst[:, dc, :], in0=t1[:], in1=t2[:], op=ALU.add)
            for ec in range(4):
                for k in range(NCH):
                    S.i("pe", "matmul", [wv, R(xb, (k, tt))], [pA], out=pA[:], lhsT=wv[:, k, ec * 128:(ec + 1) * 128], rhs=xb[:, k, ts],
                        start=(k == 0), stop=(k == NCH - 1))
                S.i("act", "activation", [pA], [R(vT, ec)], out=vT[:, ec, :], in_=pA[:], func=AF.Copy)
                for k in range(NCH):
                    S.i("pe", "matmul", [wg, R(xb, (k, tt))], [pB], out=pB[:], lhsT=wg[:, k, ec * 128:(ec + 1) * 128], rhs=xb[:, k, ts],
                        start=(k == 0), stop=(k == NCH - 1))
                S.i("act", "activation", [pB], [R(sgT, ec)], out=sgT[:, ec, :], in_=pB[:], func=AF.Silu)
            for n in range(4):
                for ec in range(4):
                    idx = (n % 2) * 4 + ec
                    S.i("pe", "transpose", [vT, g.cb], [pT], out=pT[:, idx * 128:(idx + 1) * 128], in_=vT[:, ec, n * 128:(n + 1) * 128], identity=ident)
                if n % 2 == 1:
                    S.i("dve", "tensor_copy", [pT], [R(vtm, n // 2)], out=vtm[:, n - 1:n + 1, :].rearrange("p a b -> p (a b)"), in_=pT[:])
            for n in range(4):
                for dc in range(2):
                    idx = n * 2 + dc
                    S.i("pe", "transpose", [krot, g.cb], [pT], out=pT[:, idx * 128:(idx + 1) * 128], in_=krot[:, dc, n * 128:(n + 1) * 128], identity=ident)
            S.i("dve", "tensor_scalar", [pT, rm], [ktm], out=ktm[:].rearrange("p a b -> p (a b)"), in0=pT[:], scalar1=rm[:, 256:257], scalar2=None, op0=ALU.mult)
            for n in range(4):
                cs = slice(n * 128, (n + 1) * 128)
                for dc in range(2):
                    S.i("pe", "matmul", [krot, qrot], [sps], out=sps[:, 0:128], lhsT=krot[:, dc, cs], rhs=qrot[:, dc, cs], start=(dc == 0), stop=(dc == 1))
                S.i("dve", "tensor_tensor", [sps, rm], [ST], out=ST[:], in0=sps[:, 0:128], in1=rm[:, 0:128], op=ALU.mult)
                for dc in range(2):
                    S.i("pool", "tensor_tensor", [qrot, rm], [R(qcd, dc)], out=qcd[:, dc, :], in0=qrot[:, dc, cs], in1=rm[:, 128:256], op=ALU.mult)
                for ec in range(4):
                    es = slice(ec * 128, (ec + 1) * 128)
                    S.i("pe", "matmul", [vtm, ST], [ops], out=ops[:, es], lhsT=vtm[:, n, es], rhs=ST[:], start=True, stop=False)
                    for dc in range(2):
                        S.i("pe", "matmul", [Sb, qcd], [ops], out=ops[:, es], lhsT=Sb[:, dc, es], rhs=qcd[:, dc, :], start=False, stop=(dc == 1))
                for dc in range(2):
                    S.i("pe", "matmul", [ktm, vtm], [ups[dc]], out=ups[dc][:], lhsT=ktm[:, n, dc * 128:(dc + 1) * 128], rhs=vtm[:, n, :], start=True, stop=True)
                    S.i("dve", "scalar_tensor_tensor", [S32, ups[dc], rm], [R(S32, dc)], out=S32[:, dc, :], in0=S32[:, dc, :], scalar=rm[:, 257:258], in1=ups[dc][:],
                        op0=ALU.mult, op1=ALU.add)
                    S.i("act", "activation", [R(S32, dc)], [R(Sb, dc)], out=Sb[:, dc, :], in_=S32[:, dc, :], func=AF.Copy)
                S.i("act", "activation", [ops], [osb], out=osb[:], in_=ops[:], func=AF.Copy)
                S.i("act", "activation", [ops], [osq], out=osq[:], in_=ops[:], func=AF.Square)
                for ec in range(4):
                    S.i("pe", "matmul", [osb, g.cf], [R(pst, 0)], out=pst[:, 0:128], lhsT=ones, rhs=osb[:, ec * 128:(ec + 1) * 128], start=(ec == 0), stop=(ec == 3))
                for ec in range(4):
                    S.i("pe", "matmul", [osq, g.cf], [R(pst, 1)], out=pst[:, 128:256], lhsT=ones, rhs=osq[:, ec * 128:(ec + 1) * 128], start=(ec == 0), stop=(ec == 3))
                S.i("dve", "tensor_scalar", [R(pst, 0)], [mean], out=mean[:], in0=pst[:, 0:128], scalar1=1.0 / 512, scalar2=None, op0=ALU.mult)
                S.i("dve", "tensor_tensor", [mean], [msq], out=msq[:], in0=mean[:], in1=mean[:], op=ALU.mult)
                S.i("dve", "scalar_tensor_tensor", [R(pst, 1), msq], [msq], out=msq[:], in0=pst[:, 128:256], scalar=1.0 / 512, in1=msq[:], op0=ALU.mult, op1=ALU.subtract)
                S.i("dve", "tensor_scalar", [msq], [msq], out=msq[:], in0=msq[:], scalar1=1e-5, scalar2=None, op0=ALU.add)
                S.i("act", "activation", [msq], [rstd], out=rstd[:], in_=msq[:], func=AF.Sqrt)
                S.i("dve", "reciprocal", [rstd], [rstd], out=rstd[:], in_=rstd[:])
                for ec in range(4):
                    t_ = tn[ec % 2]
                    col = h * 4 + ec
                    S.i("dve", "tensor_tensor", [osb, mean], [t_], out=t_[:], in0=osb[:, ec * 128:(ec + 1) * 128], in1=mean[:], op=ALU.subtract)
                    S.i("pool", "tensor_tensor", [t_, rstd], [t_], out=t_[:], in0=t_[:], in1=rstd[:], op=ALU.mult)
                    S.i("act", "activation", [t_, g.vecs], [t_], out=t_[:], in_=t_[:], func=AF.Identity, bias=vcol(g, "ret_gn_b", col), scale=vcol(g, "ret_gn_g", col))
                    S.i("dve", "tensor_tensor", [t_, R(sgT, ec)], [R(og, (ec, n))], out=og[:, ec, cs], in0=t_[:], in1=sgT[:, ec, cs], op=ALU.mult)
            for m in range(NCH):
                pw = pA if m % 2 == 0 else pB
                for k in range(4):
                    S.i("pe", "matmul", [wo, og], [pw], out=pw[:], lhsT=wo[:, k, m * 128:(m + 1) * 128], rhs=og[:, k, :], start=(k == 0), stop=(k == 3))
                xs = x[:, m, ts]
                if h == 0:
                    S.i("dve", "scalar_tensor_tensor", [pw, R(x, (m, tt))], [R(x, (m, tt))], out=xs, in0=xs, scalar=ALPHA, in1=pw[:], op0=ALU.mult, op1=ALU.add)
                else:
                    S.i("dve", "tensor_tensor", [pw, R(x, (m, tt))], [R(x, (m, tt))], out=xs, in0=xs, in1=pw[:], op=ALU.add)
    S.pop()
    S.push()
    layer_norm(g, ("ln1_g%d" % i, "ln1_b%d" % i), range(4))
    S.pop()

def rwkv_phase(g, i, j):
    S = g.S
    x, xb = g.x, g.xb
    S.push()
    TW = 256
    NT_ = T_SEQ // TW
    W = g.W
    Wrkv = [W["rwkv_w_rkv"][j, w].rearrange("(k p) n -> p k n", p=128) for w in range(3)]
    Wo = W["rwkv_w_o"][j].rearrange("(k p) n -> p k n", p=128)
    V = lambda name, c=0: vcol(g, "%s_%d" % (name, j), c)
    w1 = S.sbuf("w1", [128, NCH, 64], BF16); a1 = S.sbuf("a1", [128, NCH, 64], BF16); g1 = S.sbuf("g1", [128, NCH, 160], BF16)
    w2 = S.sbuf("w2", [64, 1024], BF16); a2 = S.sbuf("a2", [64, 1024], BF16); g2 = S.sbuf("g2", [128, 2, 1024], BF16)
    S.dma("pool", w1[:], W["rwkv_w1"][j].rearrange("(k p) n -> p k n", p=128), writes=[w1])
    S.dma("pool", a1[:], W["rwkv_a1"][j].rearrange("(k p) n -> p k n", p=128), writes=[a1])
    S.dma("pool", g1[:], W["rwkv_g1"][j].rearrange("(k p) n -> p k n", p=128), writes=[g1])
    S.dma("pool", w2[:], W["rwkv_w2"][j], writes=[w2])
    S.dma("pool", a2[:], W["rwkv_a2"][j], writes=[a2])
    S.dma("pool", g2[:, 0, :], W["rwkv_g2"][j][0:128, :], writes=[R(g2, None)])
    S.dma("pool", g2[0:32, 1, :], W["rwkv_g2"][j][128:160, :], writes=[R(g2, None)])
    rows = S.sbuf("gnrows", [128, 2, 1024], BF16)
    S.dma("pool", rows[:, 0, :], g.rows_d[2 * j], writes=[R(rows, None)])
    S.dma("pool", rows[:, 1, :], g.rows_d[2 * j + 1], writes=[R(rows, None)])
    om = S.sbuf("om", [128, 56], F32)
    mo = g.voff["mix0_%d" % j]
    S.i("dve", "tensor_scalar", [g.vecs], [om], out=om[:, 0:48], in0=g.vecs[:, mo:mo + 48], scalar1=-1.0, scalar2=1.0, op0=ALU.mult, op1=ALU.add)
    ko = g.voff["k_a_%d" % j]
    S.i("dve", "tensor_scalar", [g.vecs], [om], out=om[:, 48:56], in0=g.vecs[:, ko:ko + 8], scalar1=-1.0, scalar2=1.0, op0=ALU.mult, op1=ALU.add)
    xm = [S.sbuf("xm", [128, NCH, TW], BF16) for _ in range(2)]
    mtmp = [S.sbuf("mtmp", [128, TW], F32) for _ in range(2)]
    xlast = S.sbuf("xlast", [128, NCH, 2], F32)
    hw = S.sbuf("hw", [64, TW], BF16); ha = S.sbuf("ha", [64, TW], BF16); sg = S.sbuf("sg", [128, 2, TW], BF16)
    wsl = [S.sbuf("rwsl", [128, NCH, 128], BF16) for _ in range(4)]
    tA = S.sbuf("tA", [128, TW], F32); tld = S.sbuf("tld", [128, TW], F32); tlnP = S.sbuf("tlnP", [128, TW], F32)
    td3 = S.sbuf("td3", [128, TW], F32); te1 = S.sbuf("te1", [128, TW], F32); te2 = S.sbuf("te2", [128, TW], F32); te3 = S.sbuf("te3", [128, TW], F32)
    tkk = S.sbuf("tkk", [128, TW], F32); tkk2 = S.sbuf("tkk2", [128, TW], BF16); trn = S.sbuf("trn", [128, TW], F32)
    tkkn = S.sbuf("tkkn", [128, TW], F32); tt_ = S.sbuf("tt", [128, TW], F32); tkm = S.sbuf("tkm", [128, TW], F32); ttb = S.sbuf("ttb", [128, TW], F32)
    PC = S.sbuf("PC", [128, NCH, 2], F32)
    def FM(a, c, sl):
        return xb[:, c, a * TW + sl.start:a * TW + sl.stop]
    FMK = lambda a, c: R(xb, ("fm", a, c))
    Lk = [S.sbuf("Lk", [128, 4, 128], BF16) for _ in range(2)]; Mk = [S.sbuf("Mk", [128, 4, 128], BF16) for _ in range(2)]
    NTt = [S.sbuf("NT", [128, 4, 128], BF16) for _ in range(2)]
    Mak = S.sbuf("Mak", [128, 4, 128], BF16)
    Mrb = S.sbuf("Mrb", [128, 16, 128], BF16); Mrk = S.sbuf("Mrk", [128, 16, 128], BF16)
    Atm = S.sbuf("Atm", [128, 1024], BF16); Btm = S.sbuf("Btm", [128, 1024], BF16); Ktm = S.sbuf("Ktm", [128, 1024], BF16); Vtm = S.sbuf("Vtm", [128, 1024], BF16)
    AhT = S.sbuf("AhT", [128, NCH, 128], BF16); Xb = S.sbuf("Xb", [128, 256], BF16); Vhat = S.sbuf("Vhat", [128, 1024], F32)
    Ub = S.sbuf("Ub", [128, 1024], BF16); ST = S.sbuf("ST", [128, NCH, 64], BF16); STs = S.sbuf("STs", [128, NCH, 64], F32)
    yn = S.sbuf("yn", [128, 1024], F32); st4 = S.sbuf("st4", [128, 4, 16], F32); bsum = S.sbuf("bsum", [128, 16], F32)
    ogtm = S.sbuf("ogtm", [128, 1024], BF16); ogT = S.sbuf("ogT", [128, NCH, TW], BF16)
    PB = [S.psum("rp", [128, 512]) for _ in range(7)]
    pT = S.psum("rpT", [128, 1024], BF16)
    ident = g.identb
    blk1 = g.cb[:, 256:384]
    hind = g.cb[:, 768:770]
    ones = g.ones32
    S.i("pool", "memset", [], [ST], ap=ST[:], constant=0.0)
    nws = 0
    stg = _DBG.get('rw_stage', 99)
    for tix in range(_DBG.get('rw_tiles', NT_)):
        t0 = tix * TW
        tt = t0 // 512
        def mix(m, dst):
            for c in range(NCH):
                tm = mtmp[c % 2]
                mc = vcol(g, "mix%d_%d" % (m, j), c)
                oc = om[:, m * 8 + c:m * 8 + c + 1]
                if tix == 0:
                    S.i("pool", "memset", [], [tm], ap=tm[:, 0:1], constant=0.0)
                else:
                    S.i("pool", "tensor_scalar", [xlast, g.vecs], [tm], out=tm[:, 0:1], in0=xlast[:, c, 0:1], scalar1=mc, scalar2=None, op0=ALU.mult)
                S.i("pool", "tensor_scalar", [R(x, (c, tt)), g.vecs], [tm], out=tm[:, 1:TW], in0=x[:, c, t0:t0 + TW - 1], scalar1=mc, scalar2=None, op0=ALU.mult)
                S.i("dve", "scalar_tensor_tensor", [R(x, (c, tt)), om, tm], [R(dst, c)], out=dst[:, c, :], in0=x[:, c, t0:t0 + TW], scalar=oc, in1=tm[:],
                    op0=ALU.mult, op1=ALU.add)
        if stg < 2:
            continue
        nx = [0]
        def nxm():
            nx[0] += 1
            return xm[nx[0] % 2]
        d_ = nxm(); mix(1, d_)
        for k in range(NCH):
            S.i("pe", "matmul", [w1, R(d_, k)], [PB[0]], out=PB[0][0:64, 0:TW], lhsT=w1[:, k, :], rhs=d_[:, k, :], start=(k == 0), stop=(k == NCH - 1))
        S.i("act", "activation", [PB[0]], [hw], out=hw[:], in_=PB[0][0:64, 0:TW], func=AF.Tanh)
        d_ = nxm(); mix(4, d_)
        for k in range(NCH):
            S.i("pe", "matmul", [a1, R(d_, k)], [PB[1]], out=PB[1][0:64, 0:TW], lhsT=a1[:, k, :], rhs=d_[:, k, :], start=(k == 0), stop=(k == NCH - 1))
        S.i("act", "activation", [PB[1]], [ha], out=ha[:], in_=PB[1][0:64, 0:TW], func=AF.Copy)
        d_ = nxm(); mix(5, d_)
        for (lo, hi, kc) in ((0, 128, 0), (128, 160, 1)):
            for k in range(NCH):
                S.i("pe", "matmul", [g1, R(d_, k)], [PB[2]], out=PB[2][0:hi - lo, 0:TW], lhsT=g1[:, k, lo:hi], rhs=d_[:, k, :], start=(k == 0), stop=(k == NCH - 1))
            S.i("act", "activation", [PB[2]], [R(sg, kc)], out=sg[0:hi - lo, kc, :], in_=PB[2][0:hi - lo, 0:TW], func=AF.Sigmoid)
        xr = nxm(); mix(0, xr)
        xk = nxm(); mix(2, xk)
        for c in range(NCH):
            S.i("pool", "tensor_copy", [R(x, (c, tt))], [R(xlast, c)], out=xlast[:, c, 1:2], in_=x[:, c, t0 + TW - 1:t0 + TW])
        if stg < 3:
            continue
        for c in range(NCH):
            wr = wsl[nws % 4]; nws += 1
            wk_ = wsl[nws % 4]; nws += 1
            S.dma("pool", wr[:], Wrkv[0][:, :, c * 128:(c + 1) * 128], writes=[wr])
            S.dma("pool", wk_[:], Wrkv[1][:, :, c * 128:(c + 1) * 128], writes=[wk_])
            rp, kp, zw, za, ssp = PB[0], PB[1], PB[2], PB[3], PB[4]
            for k in range(NCH):
                S.i("pe", "matmul", [wr, R(xr, k)], [rp], out=rp[:, 0:TW], lhsT=wr[:, k, :], rhs=xr[:, k, :], start=(k == 0), stop=(k == NCH - 1))
            for k in range(NCH):
                S.i("pe", "matmul", [wk_, R(xk, k)], [kp], out=kp[:, 0:TW], lhsT=wk_[:, k, :], rhs=xk[:, k, :], start=(k == 0), stop=(k == NCH - 1))
            S.i("pe", "matmul", [w2, hw], [zw], out=zw[:, 0:TW], lhsT=w2[:, c * 128:(c + 1) * 128], rhs=hw[:], start=True, stop=True)
            S.i("pe", "matmul", [a2, ha], [za], out=za[:, 0:TW], lhsT=a2[:, c * 128:(c + 1) * 128], rhs=ha[:], start=True, stop=True)
            S.i("act", "activation", [za, g.vecs], [tA], out=tA[:], in_=za[:, 0:TW], func=AF.Sigmoid, bias=V("a0", c), scale=1.0)
            S.i("act", "activation", [zw, g.vecs], [tld], out=tld[:], in_=zw[:, 0:TW], func=AF.Sigmoid, bias=V("w0", c), scale=1.0)
            S.i("dve", "tensor_scalar", [tld], [tld], out=tld[:], in0=tld[:], scalar1=-0.6065306597126334, scalar2=None, op0=ALU.mult)
            for n in range(2):
                cs = slice(n * 128, (n + 1) * 128)
                S.i("dve", "tensor_tensor_scan", [tld, g.cf], [R(tlnP, n)], out=tlnP[:, cs], data0=ones, data1=tld[:, cs], initial=0.0, op0=ALU.mult, op1=ALU.add)
            S.i("dve", "tensor_tensor", [tlnP, tld], [td3], out=td3[:], in0=tlnP[:], in1=tld[:], op=ALU.subtract)
            S.i("act", "activation", [tlnP], [te1], out=te1[:], in_=tlnP[:], func=AF.Exp)
            S.i("act", "activation", [tlnP], [te2], out=te2[:], in_=tlnP[:], func=AF.Exp, scale=-1.0)
            S.i("act", "activation", [td3], [te3], out=te3[:], in_=td3[:], func=AF.Exp)
            for n in range(2):
                S.i("pool", "tensor_copy", [te1], [R(PC, (c, n))], out=PC[:, c, n:n + 1], in_=te1[:, n * 128 + 127:n * 128 + 128])
            S.i("dve", "tensor_scalar", [kp, g.vecs], [tkk], out=tkk[:], in0=kp[:, 0:TW], scalar1=V("k_k", c), scalar2=None, op0=ALU.mult)
            S.i("pool", "tensor_tensor", [tkk], [tkk2], out=tkk2[:], in0=tkk[:], in1=tkk[:], op=ALU.mult)
            S.i("pe", "matmul", [tkk2, g.cb], [ssp], out=ssp[:, 0:TW], lhsT=blk1, rhs=tkk2[:], start=True, stop=True)
            S.i("act", "activation", [ssp], [trn], out=trn[:], in_=ssp[:, 0:TW], func=AF.Sqrt)
            S.i("dve", "tensor_scalar", [trn], [trn], out=trn[:], in0=trn[:], scalar1=1e-12, scalar2=None, op0=ALU.max)
            S.i("dve", "reciprocal", [trn], [trn], out=trn[:], in_=trn[:])
            S.i("dve", "tensor_tensor", [tkk, trn], [tkkn], out=tkkn[:], in0=tkk[:], in1=trn[:], op=ALU.mult)
            S.i("dve", "tensor_scalar", [tA, g.vecs, om], [tt_], out=tt_[:], in0=tA[:], scalar1=V("k_a", c), scalar2=om[:, 48 + c:49 + c], op0=ALU.mult, op1=ALU.add)
            S.i("dve", "tensor_tensor", [kp, tt_], [tkm], out=tkm[:], in0=kp[:, 0:TW], in1=tt_[:], op=ALU.mult)
            full = slice(0, TW)
            S.i("dve", "scalar_tensor_tensor", [tkkn, te3], [FMK(0, c)], out=FM(0, c, full), in0=tkkn[:], scalar=-1.0, in1=te3[:], op0=ALU.mult, op1=ALU.mult)
            S.i("pool", "tensor_tensor", [tkkn, tA], [ttb], out=ttb[:], in0=tkkn[:], in1=tA[:], op=ALU.mult)
            S.i("dve", "tensor_tensor", [ttb, te2], [FMK(1, c)], out=FM(1, c, full), in0=ttb[:], in1=te2[:], op=ALU.mult)
            S.i("pool", "tensor_tensor", [tkm, te2], [FMK(2, c)], out=FM(2, c, full), in0=tkm[:], in1=te2[:], op=ALU.mult)
            S.i("dve", "tensor_tensor", [rp, te1], [FMK(3, c)], out=FM(3, c, full), in0=rp[:, 0:TW], in1=te1[:], op=ALU.mult)
            S.i("dve", "scalar_tensor_tensor", [rp, g.vecs, tkm], [FMK(4, c)], out=FM(4, c, full), in0=rp[:, 0:TW], scalar=V("r_k", c), in1=tkm[:], op0=ALU.mult, op1=ALU.mult)
        if stg < 4:
            continue
        xv = nxm(); mix(3, xv)
        for c in range(NCH):
            wv_ = wsl[nws % 4]; nws += 1
            S.dma("pool", wv_[:], Wrkv[2][:, :, c * 128:(c + 1) * 128], writes=[wv_])
            vp = PB[5 + (c % 2)]
            for k in range(NCH):
                S.i("pe", "matmul", [wv_, R(xv, k)], [vp], out=vp[:, 0:TW], lhsT=wv_[:, k, :], rhs=xv[:, k, :], start=(k == 0), stop=(k == NCH - 1))
            S.i("act", "activation", [vp], [FMK(5, c)], out=FM(5, c, slice(0, TW)), in_=vp[:, 0:TW], func=AF.Copy)
        if stg < 5:
            continue
        for n in range(2):
            cs = slice(n * 128, (n + 1) * 128)
            for a_, dst in ((0, Atm), (1, Btm), (2, Ktm), (5, Vtm)):
                for c in range(NCH):
                    S.i("pe", "transpose", [FMK(a_, c), g.cb], [pT], out=pT[:, c * 128:(c + 1) * 128], in_=FM(a_, c, cs), identity=ident)
                S.i("dve" if a_ in (0, 2) else "act", "tensor_copy" if a_ in (0, 2) else "activation", [pT], [dst],
                    **(dict(out=dst[:], in_=pT[:]) if a_ in (0, 2) else dict(out=dst[:], in_=pT[:], func=AF.Copy)))
            if stg < 6:
                continue
            for hg in range(_DBG.get('rw_hg', 4)):
                heads = [4 * hg + q for q in range(4)]
                def HP(h):
                    return slice((h % 2) * 64, (h % 2) * 64 + 64), h // 2
                specs = [(1, 0, 0, Mk[0], None), (0, 1, 2, Lk[0], None), (2, 0, 0, Mak, None), (1, 3, 1, Mrb, hg), (2, 3, 1, Mrk, hg)]
                for si, (la, ra, mki, dst, full16) in enumerate(specs):
                    dview = dst[:] if full16 is None else dst[:, hg * 4:(hg + 1) * 4, :]
                    wkey = dst if full16 is None else R(dst, hg)
                    for par in range(2):
                        bk = PB[(2 * si + par) % 6]
                        for qq in range(2):
                            h = heads[2 * qq + par]
                            ps_, c = HP(h)
                            S.i("pe", "matmul", [FMK(la, c), FMK(ra, c)], [bk], out=bk[:, qq * 128:(qq + 1) * 128], lhsT=FM(la, c, cs)[ps_, :], rhs=FM(ra, c, cs)[ps_, :],
                                start=True, stop=True)
                        S.i("dve", "tensor_tensor", [bk, g.mk], [wkey], out=dview[:, par::2, :], in0=bk[:, 0:256].rearrange("p (a b) -> p a b", b=128),
                            in1=g.mk[:, mki, 0:256].rearrange("p (a b) -> p a b", b=128), op=ALU.mult)
                if stg < 7:
                    continue
                cur = 0
                S.i("pool", "tensor_copy", [Mk[0]], [NTt[0]], out=NTt[0][:], in_=Mk[0][:])
                ntc = 0
                for lev in range(6):
                    Lc, Mc = Lk[cur], Mk[cur]
                    Ln, Mn = Lk[1 - cur], Mk[1 - cur]
                    NTc, NTn = NTt[ntc], NTt[1 - ntc]
                    for q in range(4):
                        S.i("pe", "matmul", [Mc, Lc], [PB[5]], out=PB[5][:, q * 128:(q + 1) * 128], lhsT=Mc[:, q, :], rhs=Lc[:, q, :], start=True, stop=True)
                    for q in range(4):
                        S.i("pe", "matmul", [Lc, Mc], [PB[6]], out=PB[6][:, q * 128:(q + 1) * 128], lhsT=Lc[:, q, :], rhs=Mc[:, q, :], start=True, stop=True)
                    S.i("act", "activation", [PB[5]], [Ln], out=Ln[:].rearrange("p a b -> p (a b)"), in_=PB[5][:], func=AF.Copy)
                    S.i("dve", "tensor_copy", [PB[6]], [Mn], out=Mn[:].rearrange("p a b -> p (a b)"), in_=PB[6][:])
                    bk = PB[lev % 2]
                    for q in range(4):
                        S.i("pe", "matmul", [NTc, g.cb], [bk], out=bk[:, q * 128:(q + 1) * 128], lhsT=ident, rhs=NTc[:, q, :], start=True, stop=False)
                        S.i("pe", "matmul", [Mn, g.cb], [bk], out=bk[:, q * 128:(q + 1) * 128], lhsT=ident, rhs=Mn[:, q, :], start=False, stop=False)
                        S.i("pe", "matmul", [Ln, NTc], [bk], out=bk[:, q * 128:(q + 1) * 128], lhsT=Ln[:, q, :], rhs=NTc[:, q, :], start=False, stop=True)
                    if lev % 2 == 0:
                        S.i("act", "activation", [bk], [NTn], out=NTn[:].rearrange("p a b -> p (a b)"), in_=bk[:], func=AF.Copy)
                    else:
                        S.i("dve", "tensor_copy", [bk], [NTn], out=NTn[:].rearrange("p a b -> p (a b)"), in_=bk[:])
                    cur = 1 - cur
                    ntc = 1 - ntc
                NTf = NTt[ntc]
                if stg < 8:
                    continue
                for q, h in enumerate(heads):
                    S.i("pe", "matmul", [Mak, Vtm], [PB[2]], out=PB[2][:, q * 64:(q + 1) * 64], lhsT=Mak[:, q, :], rhs=Vtm[:, h * 64:(h + 1) * 64], start=True, stop=True)
                S.i("act", "activation", [PB[2]], [Xb], out=Xb[:], in_=PB[2][:, 0:256], func=AF.Copy)
                for q, h in enumerate(heads):
                    S.i("pe", "matmul", [Xb, g.cb], [PB[3]], out=PB[3][:, q * 64:(q + 1) * 64], lhsT=ident, rhs=Xb[:, q * 64:(q + 1) * 64], start=True, stop=False)
                    S.i("pe", "matmul", [NTf, Xb], [PB[3]], out=PB[3][:, q * 64:(q + 1) * 64], lhsT=NTf[:, q, :], rhs=Xb[:, q * 64:(q + 1) * 64], start=False, stop=True)
                S.i("dve", "tensor_copy", [PB[3]], [R(Vhat, hg)], out=Vhat[:, hg * 256:(hg + 1) * 256], in_=PB[3][:, 0:256])
                for q, h in enumerate(heads):
                    ps_, c = HP(h)
                    cc = (c % 2) * 128
                    S.i("pe", "matmul", [Atm, NTf], [PB[6]], out=PB[6][ps_, cc:cc + 128], lhsT=Atm[:, h * 64:(h + 1) * 64], rhs=NTf[:, q, :], start=True, stop=True)
                S.i("dve", "tensor_tensor", [PB[6], FMK(0, 2 * hg), FMK(0, 2 * hg + 1)], [R(AhT, hg)], out=AhT[:, 2 * hg:2 * hg + 2, :],
                    in0=PB[6][:, 0:256].rearrange("p (a b) -> p a b", b=128), in1=xb[:, 2 * hg:2 * hg + 2, cs.start:cs.stop], op=ALU.add)
            if stg < 9:
                continue
            Ubv = Ub[:].rearrange("p (h n) -> p h n", n=64)
            Vhv = Vhat[:].rearrange("p (h n) -> p h n", n=64)
            for par in range(2):
                bk = PB[par]
                for hh in range(8):
                    h = 2 * hh + par
                    ps_, c = slice(par * 64, par * 64 + 64), h // 2
                    S.i("pe", "matmul", [AhT, ST], [bk], out=bk[:, hh * 64:(hh + 1) * 64], lhsT=AhT[ps_, c, :], rhs=ST[ps_, c, :], start=True, stop=True)
            for par in range(2):
                S.i("dve", "tensor_tensor", [PB[par], Vhat], [R(Ub, par)], out=Ubv[:, par::2, :], in0=PB[par][:].rearrange("p (h n) -> p h n", n=64),
                    in1=Vhv[:, par::2, :], op=ALU.add)
            for c in range(NCH):
                S.i("pool", "tensor_scalar", [R(ST, c), PC], [R(STs, c)], out=STs[:, c, :], in0=ST[:, c, :], scalar1=PC[:, c, n:n + 1], scalar2=None, op0=ALU.mult)
            for par in range(2):
                bk = PB[2 + par]
                for hh in range(8):
                    h = 2 * hh + par
                    ps_, c = slice(par * 64, par * 64 + 64), h // 2
                    o_ = bk[:, hh * 64:(hh + 1) * 64]
                    S.i("pe", "matmul", [FMK(3, c), ST], [bk], out=o_, lhsT=FM(3, c, cs)[ps_, :], rhs=ST[ps_, c, :], start=True, stop=False)
                    S.i("pe", "matmul", [Mrk, Vtm], [bk], out=o_, lhsT=Mrk[:, h, :], rhs=Vtm[:, h * 64:(h + 1) * 64], start=False, stop=False)
                    S.i("pe", "matmul", [Mrb, Ub], [bk], out=o_, lhsT=Mrb[:, h, :], rhs=Ub[:, h * 64:(h + 1) * 64], start=False, stop=True)
            for h in range(16):
                ps_, c = slice((h % 2) * 64, (h % 2) * 64 + 64), h // 2
                o_ = PB[4][ps_, c * 64:(c + 1) * 64]
                S.i("pe", "matmul", [Ktm, Vtm], [PB[4]], out=o_, lhsT=Ktm[:, h * 64:(h + 1) * 64], rhs=Vtm[:, h * 64:(h + 1) * 64], start=True, stop=False)
                S.i("pe", "matmul", [Btm, Ub], [PB[4]], out=o_, lhsT=Btm[:, h * 64:(h + 1) * 64], rhs=Ub[:, h * 64:(h + 1) * 64], start=False, stop=True)
            for c in range(NCH):
                S.i("dve", "scalar_tensor_tensor", [PB[4], PC, R(STs, c)], [R(ST, c)], out=ST[:, c, :], in0=PB[4][:, c * 64:(c + 1) * 64], scalar=PC[:, c, n:n + 1],
                    in1=STs[:, c, :], op0=ALU.mult, op1=ALU.add)
            for b in range(2):
                S.i("dve", "tensor_reduce", [PB[2 + b]], [R(st4, ("s", b))], out=st4[:, 0, b::2], in_=PB[2 + b][:].rearrange("p (h n) -> p h n", n=64), axis=AX.X, op=ALU.add)
                S.i("act", "activation", [PB[2 + b]], [R(yn, None)], out=yn[:, b * 512:(b + 1) * 512], in_=PB[2 + b][:], func=AF.Square)
                S.i("dve", "tensor_reduce", [R(yn, None)], [R(st4, ("q", b))], out=st4[:, 1, b::2], in_=yn[:, b * 512:(b + 1) * 512].rearrange("p (h n) -> p h n", n=64), axis=AX.X, op=ALU.add)
            S.i("dve", "tensor_scalar", [st4], [R(st4, "m")], out=st4[:, 2, :], in0=st4[:, 0, :], scalar1=1.0 / 64, scalar2=None, op0=ALU.mult)
            S.i("dve", "tensor_tensor", [st4], [R(st4, "v")], out=st4[:, 3, :], in0=st4[:, 2, :], in1=st4[:, 2, :], op=ALU.mult)
            S.i("dve", "scalar_tensor_tensor", [st4], [R(st4, "v")], out=st4[:, 3, :], in0=st4[:, 1, :], scalar=1.0 / 64, in1=st4[:, 3, :], op0=ALU.mult, op1=ALU.subtract)
            S.i("dve", "tensor_scalar", [st4], [R(st4, "v")], out=st4[:, 3, :], in0=st4[:, 3, :], scalar1=64e-5, scalar2=None, op0=ALU.add)
            S.i("act", "activation", [st4], [R(st4, "v")], out=st4[:, 3, :], in_=st4[:, 3, :], func=AF.Sqrt)
            S.i("dve", "reciprocal", [st4], [R(st4, "v")], out=st4[:, 3, :], in_=st4[:, 3, :])
            for c in range(NCH):
                S.i("pe", "matmul", [FMK(4, c), g.cb], [PB[5]], out=PB[5][:, 2 * c:2 * c + 2], lhsT=FM(4, c, cs), rhs=hind, start=True, stop=True)
            S.i("dve", "tensor_copy", [PB[5]], [bsum], out=bsum[:], in_=PB[5][:, 0:16])
            for h in range(16):
                b = h % 2
                hs = slice(h * 64, (h + 1) * 64)
                S.i("dve", "tensor_scalar", [PB[2 + b], st4], [R(yn, h)], out=yn[:, hs], in0=PB[2 + b][:, (h // 2) * 64:(h // 2 + 1) * 64],
                    scalar1=st4[:, 2, h:h + 1], scalar2=st4[:, 3, h:h + 1], op0=ALU.subtract, op1=ALU.mult)
            S.i("pool", "tensor_tensor", [yn, rows], [yn], out=yn[:], in0=yn[:], in1=rows[:, 0, :], op=ALU.mult)
            S.i("pool", "tensor_tensor", [yn, rows], [yn], out=yn[:], in0=yn[:], in1=rows[:, 1, :], op=ALU.add)
            for h in range(16):
                hs = slice(h * 64, (h + 1) * 64)
                S.i("dve", "scalar_tensor_tensor", [Vtm, bsum, yn], [R(yn, h)], out=yn[:, hs], in0=Vtm[:, hs], scalar=bsum[:, h:h + 1], in1=yn[:, hs], op0=ALU.mult, op1=ALU.add)
            for b in range(2):
                S.i("pe", "matmul", [sg, g2], [PB[b]], out=PB[b][:], lhsT=sg[:, 0, cs], rhs=g2[:, 0, b * 512:(b + 1) * 512], start=True, stop=False)
                S.i("pe", "matmul", [sg, g2], [PB[b]], out=PB[b][:], lhsT=sg[0:32, 1, cs], rhs=g2[0:32, 1, b * 512:(b + 1) * 512], start=False, stop=True)
                S.i("dve", "tensor_tensor", [yn, PB[b]], [R(ogtm, b)], out=ogtm[:, b * 512:(b + 1) * 512], in0=yn[:, b * 512:(b + 1) * 512], in1=PB[b][:], op=ALU.mult)
            for c in range(NCH):
                S.i("pe", "transpose", [ogtm, g.cb], [pT], out=pT[:, c * 128:(c + 1) * 128], in_=ogtm[:, c * 128:(c + 1) * 128], identity=ident)
            for c in range(NCH):
                S.i("act", "activation", [pT], [R(ogT, (c, n))], out=ogT[:, c, cs], in_=pT[:, c * 128:(c + 1) * 128], func=AF.Copy)
        if stg < 11:
            continue
        for m in range(NCH):
            wo = wsl[nws % 4]; nws += 1
            S.dma("pool", wo[:], Wo[:, :, m * 128:(m + 1) * 128], writes=[wo])
            bk = PB[5 + (m % 2)]
            for k in range(NCH):
                S.i("pe", "matmul", [wo, ogT], [bk], out=bk[:, 0:TW], lhsT=wo[:, k, :], rhs=ogT[:, k, :], start=(k == 0), stop=(k == NCH - 1))
            xs = x[:, m, t0:t0 + TW]
            S.i("dve", "scalar_tensor_tensor", [bk, R(x, (m, tt))], [R(x, (m, tt))], out=xs, in0=xs, scalar=ALPHA, in1=bk[:, 0:TW], op0=ALU.mult, op1=ALU.add)
        for c in range(NCH):
            S.i("pool", "tensor_copy", [xlast], [R(xlast, c)], out=xlast[:, c, 0:1], in_=xlast[:, c, 1:2])
    S.pop()
    S.push()
    layer_norm(g, ("ln1_g%d" % i, "ln1_b%d" % i), range(4))
    S.pop()

def make_consts():
    c = np.zeros((128, 1024), np.float32)
    c[:, 0:128] = 1.0
    c[:, 128:256] = np.eye(128, dtype=np.float32)
    bo = np.zeros((128, 128), np.float32); bo[:64, :64] = 1.0; bo[64:, 64:] = 1.0
    c[:, 256:384] = bo
    t = np.arange(128)
    c[:, 384:512] = np.where(t[None, :] <= t[:, None], 0.0, -30000.0)
    c[:, 512:640] = (t[:, None] < t[None, :]).astype(np.float32)
    c[:, 640:768] = (t[:, None] <= t[None, :]).astype(np.float32)
    c[:64, 768] = 1.0; c[64:, 769] = 1.0
    return c


def make_ret_tables():
    dk = 256
    inv = (1.0 / (np.float32(10000.0) ** np.linspace(0.0, 1.0, dk // 2, dtype=np.float32))).astype(np.float32)
    pos = np.arange(T_SEQ, dtype=np.float32)
    ang = (pos[:, None] * inv[None, :]).astype(np.float32)
    cos = np.cos(ang).astype(np.float32); sin = np.sin(ang).astype(np.float32)
    f = np.arange(dk)
    cosf = cos[:, f // 2].T
    sgn = np.where(f % 2 == 0, -1.0, 1.0).astype(np.float32)
    sinf = (sin[:, f // 2].T * sgn[:, None]).astype(np.float32)
    tab = np.stack([cosf.reshape(2, 128, T_SEQ).transpose(1, 0, 2), sinf.reshape(2, 128, T_SEQ).transpose(1, 0, 2)])
    m = np.zeros((4, 128, 384), np.float32)
    idx = np.arange(128, dtype=np.float64)
    for h in range(4):
        lg = np.log(1.0 - 2.0 ** (-5.0 - h))
        rel = idx[None, :] - idx[:, None]
        m[h, :, 0:128] = np.where(rel >= 0, np.exp(lg * np.maximum(rel, 0)), 0.0) / 16.0
        m[h, :, 128:256] = np.exp(lg * (idx + 1.0))[None, :]
        m[h, :, 256] = np.exp(lg * (127.0 - idx)) / 16.0
        m[h, :, 257] = np.exp(lg * 128.0)
    return np.ascontiguousarray(tab.astype(np.float32)), m


FULL_PLAN = [("rwkv", 0, 0), ("ffn", 0), ("ret", 1), ("ffn", 1), ("moba", 2), ("ffn", 2), ("rwkv", 3, 1), ("ffn", 3)]
_PLAN = FULL_PLAN
_NC_CACHE = {}


def host_inputs(inputs):
    shared = {k: np.ascontiguousarray(np.asarray(inputs[k], np.float32)) for k in WEIGHT_SHAPES}
    shared["vecs"] = pack_vecs(inputs)
    shared["consts"] = make_consts()
    rows = np.zeros((4, 128, 1024), np.float32)
    for j in range(2):
        rows[2 * j] = np.broadcast_to(np.asarray(inputs["rwkv_gn_g"][j], np.float32)[None, :], (128, 1024))
        rows[2 * j + 1] = np.broadcast_to(np.asarray(inputs["rwkv_gn_b"][j], np.float32)[None, :], (128, 1024))
    shared["rows"] = rows
    ti = np.arange(128)
    mk = np.zeros((128, 3, 512), np.float32)
    mk[:, 0, :] = np.tile((ti[:, None] < ti[None, :]).astype(np.float32), (1, 4))
    mk[:, 1, :] = np.tile((ti[:, None] <= ti[None, :]).astype(np.float32), (1, 4))
    mk[:, 2, :] = np.tile((ti[:, None] > ti[None, :]).astype(np.float32), (1, 4))
    shared["mk"] = mk
    tab, msk = make_ret_tables()
    shared["rtab"] = tab
    shared["rmask"] = msk
    wqk = np.asarray(inputs["ret_w_in"][0][:, :2048], np.float32)
    shared["ret_sw"] = np.ascontiguousarray(wqk.reshape(1024, 1024, 2)[:, :, ::-1].reshape(1024, 2048))
    return shared


def run_plan(plan, inputs, x_full, n_cores=8):
    key = repr(plan)
    if key not in _NC_CACHE:
        _NC_CACHE[key] = build_program(plan)
    nc = _NC_CACHE[key]
    shared = host_inputs(inputs)
    in_maps = []
    for b in range(n_cores):
        m = dict(shared)
        m["xT"] = np.ascontiguousarray(np.asarray(x_full[b], np.float32).T)
        in_maps.append(m)
    res = run_bass_kernel_spmd(nc, in_maps, core_ids=list(range(n_cores)))
    return np.stack([np.ascontiguousarray(r["outT"].T) for r in res.results]).astype(np.float32)


def kernel(**inputs):
    return run_plan(_PLAN, inputs, inputs["x"], 8)
```

```python
import numpy as np
import concourse.bass as bass
import concourse.mybir as mybir
from concourse.bass_utils import run_bass_kernel_spmd
from contextlib import ExitStack

_DBG = {}
F32 = mybir.dt.float32
BF16 = mybir.dt.bfloat16
AF = mybir.ActivationFunctionType
ALU = mybir.AluOpType
AX = mybir.AxisListType

ENGS = ("pe", "act", "dve", "pool", "sp")


class T:
    def __init__(self, S, t, name):
        self.S = S
        self.t = t
        self.name = name
        self.st = {}
        self.is_psum = False
        self.dsem = None
        self.dcnt = 0

    def __getitem__(self, idx):
        return self.t[idx]


class R:
    __slots__ = ("tile", "key")
    def __init__(self, tile, key=None):
        self.tile = tile
        self.key = key


class Sched:
    def __init__(self, nc):
        self.nc = nc
        self.es = ExitStack()
        self.ins = {e: [] for e in ENGS}
        self.cnt = {e: 0 for e in ENGS}
        self.seen = {e: {} for e in ENGS}
        self.sems = {}
        self.needed = {e: set() for e in ENGS}
        self.nsem = 0
        for e in ENGS:
            self.sems[("e", e)] = self.es.enter_context(nc.semaphore("sem_" + e))
        self.dma_pool = []
        self.out_dma = []
        self.pend = {}
        self.scopes = []

    def _es(self):
        return self.scopes[-1] if self.scopes else self.es

    def push(self):
        self.scopes.append(ExitStack())

    def pop(self):
        self.barrier()
        self.scopes.pop().close()

    def sbuf(self, name, shape, dt):
        self.uid = getattr(self, "uid", 0) + 1
        name = "%s_%d" % (name, self.uid)
        t = self._es().enter_context(self.nc.sbuf_tensor(name, list(shape), dt))
        return T(self, t, name)

    def psum(self, name, shape, dt=F32):
        self.uid = getattr(self, "uid", 0) + 1
        name = "%s_%d" % (name, self.uid)
        t = self._es().enter_context(self.nc.psum_tensor(name, list(shape), dt))
        tt = T(self, t, name)
        tt.is_psum = True
        return tt

    def new_dsem(self, name):
        self.nsem += 1
        key = ("d", self.nsem)
        self.sems[key] = self.es.enter_context(self.nc.semaphore("dsem%d_%s" % (self.nsem, name)))
        return key

    def _states(self, ref, create=True):
        tile, key = ref.tile, ref.key
        if key is None:
            if None not in tile.st:
                tile.st[None] = [None, {}]
            return list(tile.st.values())
        out = []
        if None in tile.st:
            out.append(tile.st[None])
        if key not in tile.st:
            tile.st[key] = [None, {}]
        out.append(tile.st[key])
        return out

    def _collect(self, eng, reads, writes):
        waits = {}
        def need(w, same_ok=False):
            if w is None:
                return
            k, v = w
            if same_ok and k == ("e", eng) and eng == "pe":
                return
            if waits.get(k, 0) < v:
                waits[k] = v
        for r in reads:
            for st in self._states(r):
                need(st[0])
                if r.tile.is_psum:
                    for rk, rv in st[1].items():
                        if rk != ("e", eng):
                            need((rk, rv))
        for w in writes:
            for st in self._states(w):
                need(st[0])
                for rk, rv in st[1].items():
                    if rk == ("e", eng) and eng == "pe":
                        continue
                    need((rk, rv))
        final = []
        for k, v in waits.items():
            if k == ("e", eng) and eng == "pe":
                continue
            if self.seen[eng].get(k, 0) < v:
                self.seen[eng][k] = v
                final.append((k, v))
                if k[0] == "e":
                    self.needed[k[1]].add(v)
        return final

    def _commit(self, tag, reads, writes):
        for r in reads:
            if r.key is None:
                for k, s in r.tile.st.items():
                    if s[1].get(tag[0], 0) < tag[1]:
                        s[1][tag[0]] = tag[1]
            else:
                s = r.tile.st[r.key]
                if s[1].get(tag[0], 0) < tag[1]:
                    s[1][tag[0]] = tag[1]
        for w in writes:
            if w.key is None:
                w.tile.st = {None: [tag, {}]}
            else:
                w.tile.st[w.key] = [tag, {}]

    def i(self, eng, meth, reads=(), writes=(), **kw):
        return self.op(eng, (meth, kw), reads, writes)

    def _norm(self, refs):
        out = []
        for r in refs:
            if not isinstance(r, R):
                r = R(r)
            if r.tile.is_psum and r.key is not None:
                r = R(r.tile)
            out.append(r)
        return out

    def op(self, eng, fn, reads=(), writes=()):
        reads = self._norm(reads)
        writes = self._norm(writes)
        waits = self._collect(eng, reads, writes)
        self.cnt[eng] += 1
        idx = self.cnt[eng]
        self.ins[eng].append([fn, waits, idx, None])
        self._commit((("e", eng), idx), reads, writes)
        return idx

    def dma(self, q, out_ap, in_ap, reads=(), writes=(), out_final=False, owner=None, **kw):
        reads = [r if isinstance(r, R) else R(r) for r in reads]
        writes = [w if isinstance(w, R) else R(w) for w in writes]
        waits = self._collect(q, reads, writes)
        if owner is None:
            owner = (writes[0] if writes else reads[0]).tile
        if owner.dsem is None:
            owner.dsem = self.new_dsem(owner.name)
        owner.dcnt += 16
        tag = (owner.dsem, owner.dcnt)
        self.cnt[q] += 1
        idx = self.cnt[q]
        self.ins[q].append([lambda e: e.dma_start(out=out_ap, in_=in_ap, **kw), waits, idx, tag])
        self._commit(tag, reads, writes)
        self.pend[tag[0]] = tag[1]
        if out_final:
            self.out_dma.append(tag)
        return tag

    def barrier(self):
        for f in ENGS:
            if self.cnt[f] and (self.ins[f][-1][0] is None or self.ins[f][-1][3] is not None):
                self.cnt[f] += 1
                self.ins[f].append([None, [], self.cnt[f], None])
        for e in ENGS:
            waits = []
            for f in ENGS:
                if f == e or self.cnt[f] == 0:
                    continue
                v = self.cnt[f]
                if self.seen[e].get(("e", f), 0) < v:
                    self.seen[e][("e", f)] = v
                    waits.append((("e", f), v))
                    self.needed[f].add(v)
            for k, v in self.pend.items():
                if self.seen[e].get(k, 0) < v:
                    self.seen[e][k] = v
                    waits.append((k, v))
            if waits:
                self.cnt[e] += 1
                self.ins[e].append([None, waits, self.cnt[e], None])

    def emit(self):
        nc = self.nc
        fwd = {}
        for k, v in self.out_dma:
            fwd[k] = max(fwd.get(k, 0), v)
        fw = list(fwd.items())
        self.cnt["sp"] += 1
        self.ins["sp"].append([None, fw, self.cnt["sp"], None])
        rank = {}
        for e in ENGS:
            s = sorted(self.needed[e])
            rank[e] = {v: i + 1 for i, v in enumerate(s)}
        engmap = {"pe": "tensor", "act": "scalar", "dve": "vector", "pool": "gpsimd", "sp": "sync"}
        with nc.Block() as block:
            for e in ENGS:
                lst = self.ins[e]
                if not lst:
                    continue
                def body(eng, e=e, lst=lst):
                    for fn, waits, idx, dtag in lst:
                        for k, v in waits:
                            if k[0] == "e":
                                eng.wait_ge(self.sems[k], rank[k[1]][v])
                            else:
                                eng.wait_ge(self.sems[k], v)
                        if fn is None:
                            if idx in self.needed[e]:
                                eng.nop().then_inc(self.sems[("e", e)], 1)
                            continue
                        ins = fn(eng) if callable(fn) else getattr(eng, fn[0])(**fn[1])
                        if dtag is not None:
                            ins.then_inc(self.sems[dtag[0]], 16)
                            if idx in self.needed[e]:
                                raise RuntimeError("dma instr needed as engine milestone")
                        elif idx in self.needed[e]:
                            ins.then_inc(self.sems[("e", e)], 1)
                getattr(block, engmap[e])(body)
        self.es.close()

D = 1024; T_SEQ = 2048; DEPTH = 4; FF = 2816; NCH = 8; NPAIR = 22
ALPHA = float((2 * DEPTH) ** 0.25)
LN_EPS = 1e-5

WEIGHT_SHAPES = {
    "rwkv_w_rkv": [2, 3, 1024, 1024], "rwkv_w1": [2, 1024, 64], "rwkv_w2": [2, 64, 1024],
    "rwkv_a1": [2, 1024, 64], "rwkv_a2": [2, 64, 1024], "rwkv_g1": [2, 1024, 160], "rwkv_g2": [2, 160, 1024],
    "rwkv_w_o": [2, 1024, 1024], "ret_w_in": [1, 1024, 6144], "ret_w_o": [1, 2048, 1024],
    "moba_w_qkv": [1, 1024, 3072], "moba_w_o": [1, 1024, 1024],
    "ffn_w_in": [4, 1024, 5632], "ffn_w_out": [4, 2816, 1024],
}


def _fm(v):
    v = np.asarray(v, np.float32).reshape(-1, 128)
    return np.ascontiguousarray(v.T)


def vec_layout():
    off = {}
    n = 0
    def add(name, cols):
        nonlocal n
        off[name] = n
        n += cols
    for i in range(4):
        for nm in ("ln1_g", "ln1_b", "ln2_g", "ln2_b"):
            add("%s%d" % (nm, i), 8)
        for k in range(3):
            add("cw%d_%d" % (k, i), 44)
        add("cb_%d" % i, 44)
    for j in range(2):
        for m in range(6):
            add("mix%d_%d" % (m, j), 8)
        for nm in ("w0", "a0", "k_k", "k_a", "r_k"):
            add("%s_%d" % (nm, j), 8)
    add("ret_gn_g", 16)
    add("ret_gn_b", 16)
    return off, n


def pack_vecs(inp):
    off, n = vec_layout()
    out = np.zeros((128, n), np.float32)
    def put(name, v):
        a = _fm(v)
        out[:, off[name]:off[name] + a.shape[1]] = a
    for i in range(4):
        for nm in ("ln1_g", "ln1_b", "ln2_g", "ln2_b"):
            put("%s%d" % (nm, i), inp[nm][i])
        for k in range(3):
            put("cw%d_%d" % (k, i), inp["ffn_conv_w"][i, k])
        put("cb_%d" % i, inp["ffn_conv_b"][i])
    for j in range(2):
        for m in range(6):
            put("mix%d_%d" % (m, j), inp["rwkv_mix"][j, m])
        put("w0_%d" % j, inp["rwkv_w0"][j]); put("a0_%d" % j, inp["rwkv_a0"][j])
        put("k_k_%d" % j, inp["rwkv_k_k"][j]); put("k_a_%d" % j, inp["rwkv_k_a"][j])
        put("r_k_%d" % j, inp["rwkv_r_k"][j].reshape(-1))
    put("ret_gn_g", inp["ret_gn_g"][0]); put("ret_gn_b", inp["ret_gn_b"][0])
    return out


class Ctx:
    pass


def build_program(plan, x_in_bf=True):
    nc = bass.Bass("TRN2", target_bir_lowering=False)
    S = Sched(nc)
    g = Ctx()
    g.nc, g.S = nc, S
    g.W = {k: nc.dram_tensor(k, shp, F32, kind="ExternalInput").ap() for k, shp in WEIGHT_SHAPES.items()}
    voff, nv = vec_layout()
    g.voff = voff
    xT_d = nc.dram_tensor("xT", [D, T_SEQ], F32, kind="ExternalInput").ap()
    vecs_d = nc.dram_tensor("vecs", [128, nv], F32, kind="ExternalInput").ap()
    consts_d = nc.dram_tensor("consts", [128, 1024], F32, kind="ExternalInput").ap()
    g.rows_d = nc.dram_tensor("rows", [4, 128, 1024], F32, kind="ExternalInput").ap()
    g.rtab_d = nc.dram_tensor("rtab", [2, 128, 2, T_SEQ], F32, kind="ExternalInput").ap()
    g.rmask_d = nc.dram_tensor("rmask", [4, 128, 384], F32, kind="ExternalInput").ap()
    g.ret_sw_d = nc.dram_tensor("ret_sw", [1024, 2048], F32, kind="ExternalInput").ap()
    g.mk_d = nc.dram_tensor("mk", [128, 3, 512], F32, kind="ExternalInput").ap()
    out_d = nc.dram_tensor("outT", [D, T_SEQ], F32, kind="ExternalOutput").ap()

    g.x = S.sbuf("x", [128, NCH, T_SEQ], F32)
    g.xb = S.sbuf("xb", [128, NCH, T_SEQ], BF16)
    g.vecs = S.sbuf("vecs", [128, nv], F32)
    g.cf = S.sbuf("cf", [128, 1024], F32)
    g.cb = S.sbuf("cb", [128, 1024], BF16)
    xv = xT_d.rearrange("(c p) t -> p c t", p=128)
    for c in range(NCH):
        S.dma("sp", g.x[:, c, :], xv[:, c, :], writes=[R(g.x, None)])
    for c in range(NCH):
        S.dma("pool", g.xb[:, c, :], xv[:, c, :], writes=[R(g.xb, None)])
    S.dma("sp", g.vecs[:], vecs_d, writes=[g.vecs])
    S.dma("sp", g.cf[:], consts_d, writes=[g.cf])
    S.dma("pool", g.cb[:], consts_d, writes=[g.cb])
    g.mk = S.sbuf("mk", [128, 3, 512], BF16)
    S.dma("pool", g.mk[:], g.mk_d, writes=[g.mk])
    g.ones32 = g.cf[:, 0:128]
    g.identb = g.cb[:, 128:256]

    for step in plan:
        kind = step[0]
        if kind == "ffn":
            ffn_phase(g, step[1])
        elif kind == "ln":
            S.push()
            layer_norm(g, step[1], range(4))
            S.pop()
        elif kind == "moba":
            moba_phase(g, step[1])
        elif kind == "ret":
            ret_phase(g, step[1])
        elif kind == "rwkv":
            rwkv_phase(g, step[1], step[2])
        else:
            raise ValueError(kind)

    S.barrier()
    ov = out_d.rearrange("(c p) t -> p c t", p=128)
    outsem = T(S, None, "outsem")
    for c in range(NCH):
        S.dma("sp", ov[:, c, :], g.x[:, c, :], reads=[R(g.x, None)], out_final=True)
    S.emit()
    return nc


def vcol(g, name, c=0):
    o = g.voff[name] + c
    return g.vecs[:, o:o + 1]


def layer_norm(g, pref, tts, width=512, write_xb=True):
    S = g.S
    gname, bname = pref
    x, xb = g.x, g.xb
    sq = [S.sbuf("lnsq", [128, 512], F32) for _ in range(2)]
    mean = S.sbuf("lnmean", [128, 512], F32)
    msq = S.sbuf("lnmsq", [128, 512], F32)
    rstd = S.sbuf("lnrstd", [128, 512], F32)
    tmp = [S.sbuf("lntmp", [128, 512], F32) for _ in range(3)]
    ps_s = S.psum("lnps", [128, 512])
    ps_q = S.psum("lnpq", [128, 512])
    ones = g.ones32
    W_ = width
    for tix in tts:
        ts = slice(tix * W_, (tix + 1) * W_)
        tt = (tix * W_) // 512
        for c in range(NCH):
            q = sq[c % 2]
            S.i("act", "activation", [R(x, (c, tt))], [q], out=q[:, 0:W_], in_=x[:, c, ts], func=AF.Square)
            S.i("pe", "matmul", [R(x, (c, tt)), g.cf], [ps_s], out=ps_s[:, 0:W_], lhsT=ones, rhs=x[:, c, ts], start=(c == 0), stop=(c == NCH - 1))
            S.i("pe", "matmul", [q, g.cf], [ps_q], out=ps_q[:, 0:W_], lhsT=ones, rhs=q[:, 0:W_], start=(c == 0), stop=(c == NCH - 1))
        S.i("dve", "tensor_scalar", [ps_s], [mean], out=mean[:, 0:W_], in0=ps_s[:, 0:W_], scalar1=1.0 / D, scalar2=None, op0=ALU.mult)
        S.i("dve", "tensor_tensor", [mean], [msq], out=msq[:, 0:W_], in0=mean[:, 0:W_], in1=mean[:, 0:W_], op=ALU.mult)
        S.i("dve", "scalar_tensor_tensor", [ps_q, msq], [msq], out=msq[:, 0:W_], in0=ps_q[:, 0:W_], scalar=1.0 / D, in1=msq[:, 0:W_], op0=ALU.mult, op1=ALU.subtract)
        S.i("dve", "tensor_scalar", [msq], [msq], out=msq[:, 0:W_], in0=msq[:, 0:W_], scalar1=LN_EPS, scalar2=None, op0=ALU.add)
        S.i("act", "activation", [msq], [rstd], out=rstd[:, 0:W_], in_=msq[:, 0:W_], func=AF.Sqrt)
        S.i("dve", "reciprocal", [rstd], [rstd], out=rstd[:, 0:W_], in_=rstd[:, 0:W_])
        for c in range(NCH):
            tm = tmp[c % 3]
            S.i("dve", "tensor_tensor", [R(x, (c, tt)), mean], [tm], out=tm[:, 0:W_], in0=x[:, c, ts], in1=mean[:, 0:W_], op=ALU.subtract)
            S.i("dve", "tensor_tensor", [tm, rstd], [tm], out=tm[:, 0:W_], in0=tm[:, 0:W_], in1=rstd[:, 0:W_], op=ALU.mult)
            S.i("act", "activation", [tm, g.vecs], [R(x, (c, tt))], out=x[:, c, ts], in_=tm[:, 0:W_], func=AF.Identity,
                bias=vcol(g, bname, c), scale=vcol(g, gname, c))
            if write_xb:
                S.i("dve", "tensor_scalar", [tm, g.vecs], [R(xb, (c, tt))], out=xb[:, c, ts], in0=tm[:, 0:W_], scalar1=vcol(g, gname, c),
                    scalar2=vcol(g, bname, c), op0=ALU.mult, op1=ALU.add)


def ffn_phase(g, i):
    S = g.S
    x, xb = g.x, g.xb
    S.push()
    W_in = g.W["ffn_w_in"][i].rearrange("(k p) (two f) -> p k two f", p=128, two=2)
    W_out = g.W["ffn_w_out"][i].rearrange("(k p) n -> p k n", p=128)
    TB = 1024
    a = S.sbuf("ffa", [128, NPAIR, TB], BF16)
    hus = [S.sbuf("hu", [128, TB + 2], F32) for _ in range(2)]
    hgs = [S.sbuf("hg", [128, TB + 2], F32) for _ in range(2)]
    aus = [S.sbuf("au", [128, TB], F32) for _ in range(2)]
    ags = [S.sbuf("ag", [128, TB], F32) for _ in range(2)]
    halo = S.sbuf("halo", [128, 2 * NPAIR, 2], F32)
    wps = [[S.sbuf("wp", [128, NCH, 128], BF16) for _ in range(2)] for _ in range(2)]
    wos = [S.sbuf("wo", [128, NPAIR, 128], BF16) for _ in range(2)]
    pb = [S.psum("ffps", [128, 512]) for _ in range(8)]
    cw = lambda k, j: vcol(g, "cw%d_%d" % (k, i), j)
    cbv = lambda j: vcol(g, "cb_%d" % i, j)
    nw = 0
    for blk in range(2):
        for j in range(NPAIR):
            wp = wps[nw % 2]; nw += 1
            hu, hg, au, ag = hus[j % 2], hgs[j % 2], aus[j % 2], ags[j % 2]
            for ug in range(2):
                S.dma("pool", wp[ug][:], W_in[:, :, ug, j * 128:(j + 1) * 128], writes=[wp[ug]])
            banks = pb[(j % 2) * 4:(j % 2) * 4 + 4]
            for ug in range(2):
                for h in range(2):
                    bk = banks[ug * 2 + h]
                    tt = blk * 2 + h
                    for k in range(NCH):
                        S.i("pe", "matmul", [wp[ug], R(xb, (k, tt))], [bk], out=bk[:], lhsT=wp[ug][:, k, :],
                            rhs=xb[:, k, tt * 512:(tt + 1) * 512], start=(k == 0), stop=(k == NCH - 1))
            for ug, hb in ((0, hu), (1, hg)):
                for h in range(2):
                    bk = banks[ug * 2 + h]
                    S.i("act", "activation", [bk], [R(hb, h)], out=hb[:, 2 + h * 512:2 + (h + 1) * 512], in_=bk[:], func=AF.Copy)
                if blk == 0:
                    S.i("pool", "memset", [], [R(hb, "halo")], ap=hb[:, 0:2], constant=0.0)
                else:
                    S.i("pool", "tensor_copy", [R(halo, (ug, j))], [R(hb, "halo")], out=hb[:, 0:2], in_=halo[:, ug * NPAIR + j, :])
            for (hb, ab_, jj) in ((hu, au, j), (hg, ag, NPAIR + j)):
                S.i("act", "activation", [hb, g.vecs], [ab_], out=ab_[:], in_=hb[:, 2:TB + 2], func=AF.Identity, bias=cbv(jj), scale=cw(2, jj))
                S.i("dve", "scalar_tensor_tensor", [hb, ab_, g.vecs], [ab_], out=ab_[:], in0=hb[:, 1:TB + 1], scalar=cw(1, jj), in1=ab_[:], op0=ALU.mult, op1=ALU.add)
                S.i("dve", "scalar_tensor_tensor", [hb, ab_, g.vecs], [ab_], out=ab_[:], in0=hb[:, 0:TB], scalar=cw(0, jj), in1=ab_[:], op0=ALU.mult, op1=ALU.add)
            S.i("act", "activation", [ag], [ag], out=ag[:], in_=ag[:], func=AF.Silu)
            S.i("dve", "tensor_tensor", [au, ag], [R(a, j)], out=a[:, j, :], in0=au[:], in1=ag[:], op=ALU.mult)
            if blk == 0:
                for ug, hb in ((0, hu), (1, hg)):
                    S.i("pool", "tensor_copy", [hb], [R(halo, (ug, j))], out=halo[:, ug * NPAIR + j, :], in_=hb[:, TB:TB + 2])
        for m in range(NCH):
            wo = wos[m % 2]
            S.dma("pool", wo[:], W_out[:, :, m * 128:(m + 1) * 128], writes=[wo])
            for h in range(2):
                bk = pb[(m * 2 + h) % 8]
                tt = blk * 2 + h
                for k in range(NPAIR):
                    S.i("pe", "matmul", [wo, R(a, k)], [bk], out=bk[:], lhsT=wo[:, k, :], rhs=a[:, k, h * 512:(h + 1) * 512],
                        start=(k == 0), stop=(k == NPAIR - 1))
                xs = x[:, m, tt * 512:(tt + 1) * 512]
                S.i("dve", "scalar_tensor_tensor", [bk, R(x, (m, tt))], [R(x, (m, tt))], out=xs, in0=xs, scalar=ALPHA, in1=bk[:],
                    op0=ALU.mult, op1=ALU.add)
    S.pop()
    S.push()
    layer_norm(g, ("ln2_g%d" % i, "ln2_b%d" % i), range(4))
    S.pop()

def moba_phase(g, i):
    S = g.S
    x, xb = g.x, g.xb
    S.push()
    Wqkv = g.W["moba_w_qkv"][0].rearrange("(k p) n -> p k n", p=128)
    Wo = g.W["moba_w_o"][0].rearrange("(k p) n -> p k n", p=128)
    ogT = S.sbuf("ogT", [128, NCH, T_SEQ], BF16)
    wsl = [[S.sbuf("mw", [128, NCH, 128], BF16) for _ in range(3)] for _ in range(2)]
    qkv = [[S.sbuf("mqkv", [128, T_SEQ], BF16) for _ in range(3)] for _ in range(2)]
    vtms = [S.sbuf("vtm", [128, 16, 128], BF16) for _ in range(2)]
    ksum = S.sbuf("ksum", [128, 8], F32)
    kmean = S.sbuf("kmean", [128, 8], BF16)
    P = S.sbuf("mP", [128, T_SEQ], BF16)
    PT = S.sbuf("mPT", [128, 16, 128], BF16)
    sd = S.sbuf("msd", [128, 128], F32)
    g8 = S.sbuf("g8", [128, 8], F32)
    top8 = S.sbuf("top8", [128, 8], F32)
    mb = S.sbuf("mb", [128, 8], F32)
    b8 = S.sbuf("b8", [128, 8], F32)
    nm = S.sbuf("nm", [128, 4], F32)
    rs = S.sbuf("rs", [128, 12], F32)
    rinv = S.sbuf("rinv", [128, 2], F32)
    otm = S.sbuf("otm", [128, 128], BF16)
    wos = [S.sbuf("mwo", [128, NCH, 128], BF16) for _ in range(2)]
    sc = S.psum("msc", [128, 2048])
    pT = S.psum("mpT", [128, 1024], BF16)
    gps = S.psum("mgps", [128, 512])
    ov = S.psum("mov", [128, 512])
    pj = S.psum("mpj", [128, 512])
    tri = g.cf[:, 384:512]
    if _DBG.get('dbg_memset'):
        S.i('pool', 'memset', [], [ogT], ap=ogT[:], constant=0.0)
    ident = g.identb
    for c in range(_DBG.get('moba_pairs', NCH)):
        ws = wsl[c % 2]
        qT, kT, vT = qkv[c % 2]
        vtm = vtms[c % 2]
        for w in range(3):
            S.dma("pool", ws[w][:], Wqkv[:, :, w * 1024 + c * 128: w * 1024 + (c + 1) * 128], writes=[ws[w]])
        for w, dst in ((0, qT), (1, kT), (2, vT)):
            for tt in range(4):
                for k in range(NCH):
                    S.i("pe", "matmul", [ws[w], R(xb, (k, tt))], [pj], out=pj[:], lhsT=ws[w][:, k, :], rhs=xb[:, k, tt * 512:(tt + 1) * 512],
                        start=(k == 0), stop=(k == NCH - 1))
                if w == 0:
                    S.i("act", "activation", [pj], [R(dst, tt)], out=dst[:, tt * 512:(tt + 1) * 512], in_=pj[:], func=AF.Copy, scale=0.125)
                else:
                    S.i("act", "activation", [pj], [R(dst, tt)], out=dst[:, tt * 512:(tt + 1) * 512], in_=pj[:], func=AF.Copy)
                if w == 1 and _DBG.get('moba_stage', 9) >= 0.2:
                    S.i("dve", "tensor_reduce", [pj], [R(ksum, tt)], out=ksum[:, 2 * tt:2 * tt + 2],
                        in_=pj[:].rearrange("p (b k) -> p b k", b=2), axis=AX.X, op=ALU.add)
        if _DBG.get('moba_stage', 9) >= 0.2:
            S.i("dve", "tensor_scalar", [ksum], [kmean], out=kmean[:], in0=ksum[:], scalar1=1.0 / 256.0, scalar2=None, op0=ALU.mult)
        for b in range(2 if _DBG.get('moba_stage', 9) >= 0.3 else 0):
            for t8 in range(8):
                kt = b * 8 + t8
                S.i("pe", "transpose", [vT, g.cb], [pT], out=pT[:, t8 * 128:(t8 + 1) * 128], in_=vT[:, kt * 128:(kt + 1) * 128], identity=ident)
            S.i("dve", "tensor_copy", [pT], [R(vtm, b)], out=vtm[:, b * 8:(b + 1) * 8, :].rearrange("p a b -> p (a b)"), in_=pT[:])
        stg = _DBG.get('moba_stage', 9)
        for qt in _DBG.get('moba_qts', range(16)):
            if stg < 2:
                break
            qb = qt // 2
            nkt = qt + 1
            qs = slice(qt * 128, (qt + 1) * 128)
            for hh in range(2):
                ps = slice(hh * 64, (hh + 1) * 64)
                use_thr = qb >= 4
                if use_thr:
                    S.i("pe", "matmul", [qT, kmean], [gps], out=gps[:, 0:8], lhsT=qT[ps, qs], rhs=kmean[ps, 0:8], start=True, stop=True)
                    S.i("pool", "memset", [], [R(g8, "pad")], ap=g8[:, qb:8], constant=-1.0e30)
                    S.i("dve", "tensor_copy", [gps], [R(g8, "val")], out=g8[:, 0:qb], in_=gps[:, 0:qb])
                    S.i("dve", "max", [g8], [top8], out=top8[:], in_=g8[:])
                    S.i("dve", "tensor_scalar", [g8, top8], [mb], out=mb[:], in0=g8[:], scalar1=top8[:, 2:3], scalar2=30000.0,
                        op0=ALU.is_ge, op1=ALU.mult)
                ncol = nkt * 128
                for b in range((ncol + 511) // 512):
                    w_ = min(512, ncol - b * 512)
                    S.i("pe", "matmul", [qT, kT], [sc], out=sc[:, b * 512:b * 512 + w_], lhsT=qT[ps, qs], rhs=kT[ps, b * 512:b * 512 + w_],
                        start=True, stop=True)
                S.i("dve", "tensor_tensor", [sc, g.cf], [sd], out=sd[:], in0=sc[:, qt * 128:(qt + 1) * 128], in1=tri, op=ALU.add)
                S.i("dve", "tensor_reduce", [sd], [R(nm, 0)], out=nm[:, 0:1], in_=sd[:], axis=AX.X, op=ALU.max, negate=True)
                if qt > 0:
                    S.i("dve", "tensor_reduce", [sc], [R(nm, 1)], out=nm[:, 1:2], in_=sc[:, 0:qt * 128], axis=AX.X, op=ALU.max, negate=True)
                    S.i("dve", "tensor_tensor", [nm], [R(nm, 2)], out=nm[:, 2:3], in0=nm[:, 0:1], in1=nm[:, 1:2], op=ALU.min)
                    negm = nm[:, 2:3]
                else:
                    negm = nm[:, 0:1]
                if stg < 3:
                    continue
                if use_thr:
                    S.i("dve", "tensor_scalar", [mb, nm], [b8], out=b8[:], in0=mb[:], scalar1=negm, scalar2=-30000.0, op0=ALU.add, op1=ALU.add)
                npz = 0
                if use_thr:
                    for n in range(qb):
                        S.i("act", "activation", [sc, b8], [R(P, n), R(rs, npz)], out=P[:, n * 256:(n + 1) * 256], in_=sc[:, n * 256:(n + 1) * 256],
                            func=AF.Exp, bias=b8[:, n:n + 1], scale=1.0, accum_out=rs[:, npz:npz + 1])
                        npz += 1
                    if qt % 2 == 1:
                        S.i("act", "activation", [sc, nm], [R(P, "o"), R(rs, npz)], out=P[:, (qt - 1) * 128:qt * 128], in_=sc[:, (qt - 1) * 128:qt * 128],
                            func=AF.Exp, bias=negm, scale=1.0, accum_out=rs[:, npz:npz + 1])
                        npz += 1
                elif qt > 0:
                    S.i("act", "activation", [sc, nm], [R(P, "past"), R(rs, npz)], out=P[:, 0:qt * 128], in_=sc[:, 0:qt * 128],
                        func=AF.Exp, bias=negm, scale=1.0, accum_out=rs[:, npz:npz + 1])
                    npz += 1
                S.i("act", "activation", [sd, nm], [R(P, "d"), R(rs, npz)], out=P[:, qt * 128:(qt + 1) * 128], in_=sd[:],
                    func=AF.Exp, bias=negm, scale=1.0, accum_out=rs[:, npz:npz + 1])
                npz += 1
                S.i("dve", "tensor_reduce", [rs], [R(rs, 11)], out=rs[:, 11:12], in_=rs[:, 0:npz], axis=AX.X, op=ALU.add)
                S.i("dve", "reciprocal", [R(rs, 11)], [R(rinv, hh)], out=rinv[:, hh:hh + 1], in_=rs[:, 11:12])
                if stg < 4:
                    continue
                for b in range((nkt + 7) // 8):
                    n8 = min(8, nkt - b * 8)
                    for t8 in range(n8):
                        kt = b * 8 + t8
                        S.i("pe", "transpose", [P, g.cb], [pT], out=pT[:, t8 * 128:(t8 + 1) * 128], in_=P[:, kt * 128:(kt + 1) * 128], identity=ident)
                    S.i("dve", "tensor_copy", [pT], [R(PT, b)], out=PT[:, b * 8:b * 8 + n8, :].rearrange("p a b -> p (a b)"), in_=pT[:, 0:n8 * 128])
                for kt in range(nkt):
                    S.i("pe", "matmul", [PT, vtm], [R(ov, hh)], out=ov[:, hh * 64:(hh + 1) * 64], lhsT=PT[:, kt, :], rhs=vtm[:, kt, hh * 64:(hh + 1) * 64],
                        start=(kt == 0), stop=(kt == nkt - 1))
                S.i("dve", "tensor_scalar", [R(ov, hh), R(rinv, hh)], [R(otm, hh)], out=otm[:, hh * 64:(hh + 1) * 64], in0=ov[:, hh * 64:(hh + 1) * 64],
                    scalar1=rinv[:, hh:hh + 1], scalar2=None, op0=ALU.mult)
            if stg < 5:
                continue
            S.i("pe", "transpose", [otm, g.cb], [pT], out=pT[:, 0:128], in_=otm[:], identity=ident)
            S.i("act", "activation", [pT], [R(ogT, (c, qt))], out=ogT[:, c, qs], in_=pT[:, 0:128], func=AF.Copy)
    for m in range(NCH):
        wo = wos[m % 2]
        S.dma("pool", wo[:], Wo[:, :, m * 128:(m + 1) * 128], writes=[wo])
        for tt in range(4):
            for k in range(NCH):
                S.i("pe", "matmul", [wo, ogT], [pj], out=pj[:], lhsT=wo[:, k, :], rhs=ogT[:, k, tt * 512:(tt + 1) * 512], start=(k == 0), stop=(k == NCH - 1))
            xs = x[:, m, tt * 512:(tt + 1) * 512]
            S.i("dve", "scalar_tensor_tensor", [pj, R(x, (m, tt))], [R(x, (m, tt))], out=xs, in0=xs, scalar=ALPHA, in1=pj[:], op0=ALU.mult, op1=ALU.add)
    S.pop()
    S.push()
    layer_norm(g, ("ln1_g%d" % i, "ln1_b%d" % i), range(4))
    S.pop()

def ret_phase(g, i):
    S = g.S
    x, xb = g.x, g.xb
    S.push()
    Win = g.W["ret_w_in"][0].rearrange("(k p) n -> p k n", p=128)
    Wsw = g.ret_sw_d.rearrange("(k p) n -> p k n", p=128)
    Wo = g.W["ret_w_o"][0].rearrange("(k p) n -> p k n", p=128)
    wq = S.sbuf("rwq", [128, NCH, 256], BF16); wqs = S.sbuf("rwqs", [128, NCH, 256], BF16)
    wk = S.sbuf("rwk", [128, NCH, 256], BF16); wks = S.sbuf("rwks", [128, NCH, 256], BF16)
    wv = S.sbuf("rwv", [128, NCH, 512], BF16); wg = S.sbuf("rwg", [128, NCH, 512], BF16)
    wo = S.sbuf("rwo", [128, 4, 1024], BF16)
    rm = S.sbuf("rrm", [128, 384], F32)
    tabs = [S.sbuf("rtab", [128, 2, 2, 512], F32) for _ in range(2)]
    qrot = S.sbuf("qrot", [128, 2, 512], BF16); krot = S.sbuf("krot", [128, 2, 512], BF16)
    vT = S.sbuf("rvT", [128, 4, 512], BF16); sgT = S.sbuf("rsgT", [128, 4, 512], BF16)
    vtm = S.sbuf("rvtm", [128, 4, 512], BF16); ktm = S.sbuf("rktm", [128, 4, 256], BF16)
    og = S.sbuf("rog", [128, 4, 512], BF16)
    S32 = S.sbuf("rS32", [128, 2, 512], F32); Sb = S.sbuf("rSb", [128, 2, 512], BF16)
    t1 = S.sbuf("rt1", [128, 512], F32); t2 = S.sbuf("rt2", [128, 512], F32)
    ST = S.sbuf("rST", [128, 128], BF16); qcd = S.sbuf("rqcd", [128, 2, 128], BF16)
    osb = S.sbuf("rosb", [128, 512], F32); osq = S.sbuf("rosq", [128, 512], F32)
    mean = S.sbuf("rmean", [128, 128], F32); msq = S.sbuf("rmsq", [128, 128], F32); rstd = S.sbuf("rrstd", [128, 128], F32)
    tn = [S.sbuf("rtn", [128, 128], F32) for _ in range(2)]
    pA = S.psum("rpA", [128, 512]); pB = S.psum("rpB", [128, 512])
    sps = S.psum("rsps", [128, 512]); ops = S.psum("rops", [128, 512])
    ups = [S.psum("rups", [128, 512]) for _ in range(2)]
    pT = S.psum("rpT", [128, 1024], BF16)
    pst = S.psum("rpst", [128, 512])
    ident = g.identb
    ones = g.ones32
    ntab = 0
    for h in range(4):
        S.dma("pool", wq[:], Win[:, :, h * 256:(h + 1) * 256], writes=[wq])
        S.dma("pool", wqs[:], Wsw[:, :, h * 256:(h + 1) * 256], writes=[wqs])
        S.dma("pool", wk[:], Win[:, :, 1024 + h * 256:1024 + (h + 1) * 256], writes=[wk])
        S.dma("pool", wks[:], Wsw[:, :, 1024 + h * 256:1024 + (h + 1) * 256], writes=[wks])
        S.dma("pool", wv[:], Win[:, :, 2048 + h * 512:2048 + (h + 1) * 512], writes=[wv])
        S.dma("pool", wg[:], Win[:, :, 4096 + h * 512:4096 + (h + 1) * 512], writes=[wg])
        S.dma("pool", wo[:], Wo[:, h * 4:(h + 1) * 4, :], writes=[wo])
        S.dma("sp", rm[:], g.rmask_d[h], writes=[rm])
        S.i("pool", "memset", [], [S32], ap=S32[:], constant=0.0)
        S.i("pool", "memset", [], [Sb], ap=Sb[:], constant=0.0)
        for tt in range(4):
            ts = slice(tt * 512, (tt + 1) * 512)
            tab = tabs[ntab % 2]; ntab += 1
            for cs_ in range(2):
                S.dma("sp", tab[:, cs_, :, :], g.rtab_d[cs_][:, :, ts], writes=[R(tab, None)])
            for (wa, wb, dst) in ((wq, wqs, qrot), (wk, wks, krot)):
                for dc in range(2):
                    for k in range(NCH):
                        S.i("pe", "matmul", [wa, R(xb, (k, tt))], [pA], out=pA[:], lhsT=wa[:, k, dc * 128:(dc + 1) * 128], rhs=xb[:, k, ts],
                            start=(k == 0), stop=(k == NCH - 1))
                    for k in range(NCH):
                        S.i("pe", "matmul", [wb, R(xb, (k, tt))], [pB], out=pB[:], lhsT=wb[:, k, dc * 128:(dc + 1) * 128], rhs=xb[:, k, ts],
                            start=(k == 0), stop=(k == NCH - 1))
                    S.i("dve", "tensor_tensor", [pA, tab], [t1], out=t1[:], in0=pA[:], in1=tab[:, 0, dc, :], op=ALU.mult)
                    S.i("dve", "tensor_tensor", [pB, tab], [t2], out=t2[:], in0=pB[:], in1=tab[:, 1, dc, :], op=ALU.mult)
                    S.i("pool", "tensor_tensor", [t1, t2], [R(dst, dc)], out=dst[:, dc, :], in0=t1[:], in1=t2[:], op=ALU.add)
            for ec in range(4):
                for k in range(NCH):
                    S.i("pe", "matmul", [wv, R(xb, (k, tt))], [pA], out=pA[:], lhsT=wv[:, k, ec * 128:(ec + 1) * 128], rhs=xb[:, k, ts],
                        start=(k == 0), stop=(k == NCH - 1))
                S.i("act", "activation", [pA], [R(vT, ec)], out=vT[:, ec, :], in_=pA[:], func=AF.Copy)
                for k in range(NCH):
                    S.i("pe", "matmul", [wg, R(xb, (k, tt))], [pB], out=pB[:], lhsT=wg[:, k, ec * 128:(ec + 1) * 128], rhs=xb[:, k, ts],
                        start=(k == 0), stop=(k == NCH - 1))
                S.i("act", "activation", [pB], [R(sgT, ec)], out=sgT[:, ec, :], in_=pB[:], func=AF.Silu)
            for n in range(4):
                for ec in range(4):
                    idx = (n % 2) * 4 + ec
                    S.i("pe", "transpose", [vT, g.cb], [pT], out=pT[:, idx * 128:(idx + 1) * 128], in_=vT[:, ec, n * 128:(n + 1) * 128], identity=ident)
                if n % 2 == 1:
                    S.i("dve", "tensor_copy", [pT], [R(vtm, n // 2)], out=vtm[:, n - 1:n + 1, :].rearrange("p a b -> p (a b)"), in_=pT[:])
            for n in range(4):
                for dc in range(2):
                    idx = n * 2 + dc
                    S.i("pe", "transpose", [krot, g.cb], [pT], out=pT[:, idx * 128:(idx + 1) * 128], in_=krot[:, dc, n * 128:(n + 1) * 128], identity=ident)
            S.i("dve", "tensor_scalar", [pT, rm], [ktm], out=ktm[:].rearrange("p a b -> p (a b)"), in0=pT[:], scalar1=rm[:, 256:257], scalar2=None, op0=ALU.mult)
            for n in range(4):
                cs = slice(n * 128, (n + 1) * 128)
                for dc in range(2):
                    S.i("pe", "matmul", [krot, qrot], [sps], out=sps[:, 0:128], lhsT=krot[:, dc, cs], rhs=qrot[:, dc, cs], start=(dc == 0), stop=(dc == 1))
                S.i("dve", "tensor_tensor", [sps, rm], [ST], out=ST[:], in0=sps[:, 0:128], in1=rm[:, 0:128], op=ALU.mult)
                for dc in range(2):
                    S.i("pool", "tensor_tensor", [qrot, rm], [R(qcd, dc)], out=qcd[:, dc, :], in0=qrot[:, dc, cs], in1=rm[:, 128:256], op=ALU.mult)
                for ec in range(4):
                    es = slice(ec * 128, (ec + 1) * 128)
                    S.i("pe", "matmul", [vtm, ST], [ops], out=ops[:, es], lhsT=vtm[:, n, es], rhs=ST[:], start=True, stop=False)
                    for dc in range(2):
                        S.i("pe", "matmul", [Sb, qcd], [ops], out=ops[:, es], lhsT=Sb[:, dc, es], rhs=qcd[:, dc, :], start=False, stop=(dc == 1))
                for dc in range(2):
                    S.i("pe", "matmul", [ktm, vtm], [ups[dc]], out=ups[dc][:], lhsT=ktm[:, n, dc * 128:(dc + 1) * 128], rhs=vtm[:, n, :], start=True, stop=True)
                    S.i("dve", "scalar_tensor_tensor", [S32, ups[dc], rm], [R(S32, dc)], out=S32[:, dc, :], in0=S32[:, dc, :], scalar=rm[:, 257:258], in1=ups[dc][:],
                        op0=ALU.mult, op1=ALU.add)
                    S.i("act", "activation", [R(S32, dc)], [R(Sb, dc)], out=Sb[:, dc, :], in_=S32[:, dc, :], func=AF.Copy)
                S.i("act", "activation", [ops], [osb], out=osb[:], in_=ops[:], func=AF.Copy)
                S.i("act", "activation", [ops], [osq], out=osq[:], in_=ops[:], func=AF.Square)
                for ec in range(4):
                    S.i("pe", "matmul", [osb, g.cf], [R(pst, 0)], out=pst[:, 0:128], lhsT=ones, rhs=osb[:, ec * 128:(ec + 1) * 128], start=(ec == 0), stop=(ec == 3))
                for ec in range(4):
                    S.i("pe", "matmul", [osq, g.cf], [R(pst, 1)], out=pst[:, 128:256], lhsT=ones, rhs=osq[:, ec * 128:(ec + 1) * 128], start=(ec == 0), stop=(ec == 3))
                S.i("dve", "tensor_scalar", [R(pst, 0)], [mean], out=mean[:], in0=pst[:, 0:128], scalar1=1.0 / 512, scalar2=None, op0=ALU.mult)
                S.i("dve", "tensor_tensor", [mean], [msq], out=msq[:], in0=mean[:], in1=mean[:], op=ALU.mult)
                S.i("dve", "scalar_tensor_tensor", [R(pst, 1), msq], [msq], out=msq[:], in0=pst[:, 128:256], scalar=1.0 / 512, in1=msq[:], op0=ALU.mult, op1=ALU.subtract)
                S.i("dve", "tensor_scalar", [msq], [msq], out=msq[:], in0=msq[:], scalar1=1e-5, scalar2=None, op0=ALU.add)
                S.i("act", "activation", [msq], [rstd], out=rstd[:], in_=msq[:], func=AF.Sqrt)
                S.i("dve", "reciprocal", [rstd], [rstd], out=rstd[:], in_=rstd[:])
                for ec in range(4):
                    t_ = tn[ec % 2]
                    col = h * 4 + ec
                    S.i("dve", "tensor_tensor", [osb, mean], [t_], out=t_[:], in0=osb[:, ec * 128:(ec + 1) * 128], in1=mean[:], op=ALU.subtract)
                    S.i("pool", "tensor_tensor", [t_, rstd], [t_], out=t_[:], in0=t_[:], in1=rstd[:], op=ALU.mult)
                    S.i("act", "activation", [t_, g.vecs], [t_], out=t_[:], in_=t_[:], func=AF.Identity, bias=vcol(g, "ret_gn_b", col), scale=vcol(g, "ret_gn_g", col))
                    S.i("dve", "tensor_tensor", [t_, R(sgT, ec)], [R(og, (ec, n))], out=og[:, ec, cs], in0=t_[:], in1=sgT[:, ec, cs], op=ALU.mult)
            for m in range(NCH):
                pw = pA if m % 2 == 0 else pB
                for k in range(4):
                    S.i("pe", "matmul", [wo, og], [pw], out=pw[:], lhsT=wo[:, k, m * 128:(m + 1) * 128], rhs=og[:, k, :], start=(k == 0), stop=(k == 3))
                xs = x[:, m, ts]
                if h == 0:
                    S.i("dve", "scalar_tensor_tensor", [pw, R(x, (m, tt))], [R(x, (m, tt))], out=xs, in0=xs, scalar=ALPHA, in1=pw[:], op0=ALU.mult, op1=ALU.add)
                else:
                    S.i("dve", "tensor_tensor", [pw, R(x, (m, tt))], [R(x, (m, tt))], out=xs, in0=xs, in1=pw[:], op=ALU.add)
    S.pop()
    S.push()
    layer_norm(g, ("ln1_g%d" % i, "ln1_b%d" % i), range(4))
    S.pop()

def rwkv_phase(g, i, j):
    S = g.S
    x, xb = g.x, g.xb
    S.push()
    TW = 256
    NT_ = T_SEQ // TW
    W = g.W
    Wrkv = [W["rwkv_w_rkv"][j, w].rearrange("(k p) n -> p k n", p=128) for w in range(3)]
    Wo = W["rwkv_w_o"][j].rearrange("(k p) n -> p k n", p=128)
    V = lambda name, c=0: vcol(g, "%s_%d" % (name, j), c)
    w1 = S.sbuf("w1", [128, NCH, 64], BF16); a1 = S.sbuf("a1", [128, NCH, 64], BF16); g1 = S.sbuf("g1", [128, NCH, 160], BF16)
    w2 = S.sbuf("w2", [64, 1024], BF16); a2 = S.sbuf("a2", [64, 1024], BF16); g2 = S.sbuf("g2", [128, 2, 1024], BF16)
    S.dma("pool", w1[:], W["rwkv_w1"][j].rearrange("(k p) n -> p k n", p=128), writes=[w1])
    S.dma("pool", a1[:], W["rwkv_a1"][j].rearrange("(k p) n -> p k n", p=128), writes=[a1])
    S.dma("pool", g1[:], W["rwkv_g1"][j].rearrange("(k p) n -> p k n", p=128), writes=[g1])
    S.dma("pool", w2[:], W["rwkv_w2"][j], writes=[w2])
    S.dma("pool", a2[:], W["rwkv_a2"][j], writes=[a2])
    S.dma("pool", g2[:, 0, :], W["rwkv_g2"][j][0:128, :], writes=[R(g2, None)])
    S.dma("pool", g2[0:32, 1, :], W["rwkv_g2"][j][128:160, :], writes=[R(g2, None)])
    rows = S.sbuf("gnrows", [128, 2, 1024], BF16)
    S.dma("pool", rows[:, 0, :], g.rows_d[2 * j], writes=[R(rows, None)])
    S.dma("pool", rows[:, 1, :], g.rows_d[2 * j + 1], writes=[R(rows, None)])
    om = S.sbuf("om", [128, 56], F32)
    mo = g.voff["mix0_%d" % j]
    S.i("dve", "tensor_scalar", [g.vecs], [om], out=om[:, 0:48], in0=g.vecs[:, mo:mo + 48], scalar1=-1.0, scalar2=1.0, op0=ALU.mult, op1=ALU.add)
    ko = g.voff["k_a_%d" % j]
    S.i("dve", "tensor_scalar", [g.vecs], [om], out=om[:, 48:56], in0=g.vecs[:, ko:ko + 8], scalar1=-1.0, scalar2=1.0, op0=ALU.mult, op1=ALU.add)
    xx = S.sbuf("xx", [128, NCH, TW], F32)
    xlast = S.sbuf("xlast", [128, NCH, 2], F32)
    S.i("pool", "memset", [], [xlast], ap=xlast[:], constant=0.0)
    hw = S.sbuf("hw", [64, TW], BF16); ha = S.sbuf("ha", [64, TW], BF16); sg = S.sbuf("sg", [128, 2, TW], BF16)
    wsl = [S.sbuf("rwsl", [128, NCH, 128], BF16) for _ in range(4)]
    TS = [dict(), dict()]
    for nm in ("tA", "tsw", "tkk"):
        for q_ in range(2):
            TS[q_][nm] = S.sbuf(nm, [128, TW], F32)
    for q_ in range(2):
        TS[q_]["tkk2"] = S.sbuf("tkk2", [128, TW], BF16)
    for nm in ("tcs", "te1", "te2", "te3", "trn", "tt"):
        t_ = S.sbuf(nm, [128, TW], F32)
        TS[0][nm] = t_; TS[1][nm] = t_
    PC = S.sbuf("PC", [128, NCH, 2], F32)
    def FM(a, c, sl):
        return xb[:, c, a * TW + sl.start:a * TW + sl.stop]
    FMK = lambda a, c: R(xb, ("fm", a, c))
    Lk = [[S.sbuf("Lk", [128, 4, 128], BF16) for _ in range(2)] for _ in range(2)]
    Mk = [[S.sbuf("Mk", [128, 4, 128], BF16) for _ in range(2)] for _ in range(2)]
    NTt = [[S.sbuf("NT", [128, 4, 128], BF16) for _ in range(2)] for _ in range(2)]
    Mak = [S.sbuf("Mak", [128, 4, 128], BF16) for _ in range(2)]
    Mrb = S.sbuf("Mrb", [128, 16, 128], BF16); Mrk = S.sbuf("Mrk", [128, 16, 128], BF16)
    Atm = S.sbuf("Atm", [128, 1024], BF16); Btm = S.sbuf("Btm", [128, 1024], BF16); Ktm = S.sbuf("Ktm", [128, 1024], BF16); Vtm = S.sbuf("Vtm", [128, 1024], BF16)
    AhT = S.sbuf("AhT", [128, NCH, 128], BF16); Xb = S.sbuf("Xb", [128, 256], BF16); Vhat = S.sbuf("Vhat", [128, 1024], BF16)
    Ub = S.sbuf("Ub", [128, 1024], BF16); ST = S.sbuf("ST", [128, NCH, 64], BF16); STs = S.sbuf("STs", [128, NCH, 64], F32)
    yn = S.sbuf("yn", [128, 1024], F32); st4 = S.sbuf("st4", [128, 4, 16], F32); bsum = S.sbuf("bsum", [128, 16], F32)
    ogtm = S.sbuf("ogtm", [128, 1024], BF16); ogT = S.sbuf("ogT", [128, NCH, TW], BF16)
    PB = [S.psum("rp", [128, 512]) for _ in range(7)]
    pT = S.psum("rpT", [128, 1024], BF16)
    ident = g.identb
    blk1 = g.cb[:, 256:384]
    hind = g.cb[:, 768:770]
    ones = g.ones32
    S.i("pool", "memset", [], [ST], ap=ST[:], constant=0.0)
    nws = 0
    stg = _DBG.get('rw_stage', 99)
    for tix in range(_DBG.get('rw_tiles', NT_)):
        t0 = tix * TW
        tt = t0 // 512
        S.i("dve", "tensor_tensor", [R(x, None)], [R(xx, "m")], out=xx[:, :, 1:TW], in0=x[:, :, t0:t0 + TW - 1], in1=x[:, :, t0 + 1:t0 + TW], op=ALU.subtract)
        S.i("dve", "tensor_tensor", [R(x, None), xlast], [R(xx, "0")], out=xx[:, :, 0:1], in0=xlast[:, :, tix % 2:tix % 2 + 1], in1=x[:, :, t0:t0 + 1], op=ALU.subtract)
        S.i("dve", "tensor_copy", [R(x, None)], [xlast], out=xlast[:, :, (tix + 1) % 2:(tix + 1) % 2 + 1], in_=x[:, :, t0 + TW - 1:t0 + TW])
        def mix(m, dst):
            for c in range(NCH):
                mc = vcol(g, "mix%d_%d" % (m, j), c)
                S.i("dve", "scalar_tensor_tensor", [R(x, (c, tt)), xx, g.vecs], [dst.key(c)], out=dst.ap(c), in0=xx[:, c, :], scalar=mc, in1=x[:, c, t0:t0 + TW],
                    op0=ALU.mult, op1=ALU.add)
        nx = [0]
        class XM:
            def __init__(self, slot):
                self.slot = slot
            def ap(self, c):
                return xb[:, c, 1536 + self.slot * TW:1536 + (self.slot + 1) * TW]
            def key(self, c):
                return R(xb, ("xm", self.slot, c))
        def nxm():
            nx[0] += 1
            return XM(nx[0] % 2)
        d_ = nxm(); mix(1, d_)
        for k in range(NCH):
            S.i("pe", "matmul", [w1, d_.key(k)], [PB[0]], out=PB[0][0:64, 0:TW], lhsT=w1[:, k, :], rhs=d_.ap(k), start=(k == 0), stop=(k == NCH - 1))
        S.i("act", "activation", [PB[0]], [hw], out=hw[:], in_=PB[0][0:64, 0:TW], func=AF.Tanh)
        d_ = nxm(); mix(4, d_)
        for k in range(NCH):
            S.i("pe", "matmul", [a1, d_.key(k)], [PB[1]], out=PB[1][0:64, 0:TW], lhsT=a1[:, k, :], rhs=d_.ap(k), start=(k == 0), stop=(k == NCH - 1))
        S.i("act", "activation", [PB[1]], [ha], out=ha[:], in_=PB[1][0:64, 0:TW], func=AF.Copy)
        d_ = nxm(); mix(5, d_)
        for (lo, hi, kc) in ((0, 128, 0), (128, 160, 1)):
            for k in range(NCH):
                S.i("pe", "matmul", [g1, d_.key(k)], [PB[2]], out=PB[2][0:hi - lo, 0:TW], lhsT=g1[:, k, lo:hi], rhs=d_.ap(k), start=(k == 0), stop=(k == NCH - 1))
            S.i("act", "activation", [PB[2]], [R(sg, kc)], out=sg[0:hi - lo, kc, :], in_=PB[2][0:hi - lo, 0:TW], func=AF.Sigmoid)
        xr = nxm(); mix(0, xr)
        xk = nxm(); mix(2, xk)
        if stg < 3:
            continue
        for c in range(NCH):
            wr = wsl[nws % 4]; nws += 1
            wk_ = wsl[nws % 4]; nws += 1
            S.dma("pool", wr[:], Wrkv[0][:, :, c * 128:(c + 1) * 128], writes=[wr])
            S.dma("pool", wk_[:], Wrkv[1][:, :, c * 128:(c + 1) * 128], writes=[wk_])
            d = TS[c % 2]
            tA, tsw, tcs, te1, te2, te3, tkk, tkk2, trn, tt_ = (d[k_] for k_ in ("tA", "tsw", "tcs", "te1", "te2", "te3", "tkk", "tkk2", "trn", "tt"))
            td3 = tsw; tkkn = tkk; ttb = tkk; tkm = tt_
            if c % 2 == 0:
                rp, kp, zw, za, ssp = PB[0], PB[1], PB[2], PB[3], PB[4]
            else:
                rp, kp, zw, za, ssp = PB[5], PB[6], PB[2], PB[3], PB[4]
            for k in range(NCH):
                S.i("pe", "matmul", [wr, xr.key(k)], [rp], out=rp[:, 0:TW], lhsT=wr[:, k, :], rhs=xr.ap(k), start=(k == 0), stop=(k == NCH - 1))
            for k in range(NCH):
                S.i("pe", "matmul", [wk_, xk.key(k)], [kp], out=kp[:, 0:TW], lhsT=wk_[:, k, :], rhs=xk.ap(k), start=(k == 0), stop=(k == NCH - 1))
            S.i("pe", "matmul", [w2, hw], [zw], out=zw[:, 0:TW], lhsT=w2[:, c * 128:(c + 1) * 128], rhs=hw[:], start=True, stop=True)
            S.i("pe", "matmul", [a2, ha], [za], out=za[:, 0:TW], lhsT=a2[:, c * 128:(c + 1) * 128], rhs=ha[:], start=True, stop=True)
            S.i("act", "activation", [za, g.vecs], [tA], out=tA[:], in_=za[:, 0:TW], func=AF.Sigmoid, bias=V("a0", c), scale=1.0)
            S.i("act", "activation", [zw, g.vecs], [tsw], out=tsw[:], in_=zw[:, 0:TW], func=AF.Sigmoid, bias=V("w0", c), scale=1.0)
            for n in range(2):
                cs = slice(n * 128, (n + 1) * 128)
                S.i("dve", "tensor_tensor_scan", [tsw, g.cf], [R(tcs, n)], out=tcs[:, cs], data0=ones, data1=tsw[:, cs], initial=0.0, op0=ALU.mult, op1=ALU.add)
            S.i("dve", "tensor_tensor", [tcs, tsw], [td3], out=td3[:], in0=tcs[:], in1=tsw[:], op=ALU.subtract)
            LD = 0.6065306597126334
            S.i("act", "activation", [tcs], [te1], out=te1[:], in_=tcs[:], func=AF.Exp, scale=-LD)
            S.i("act", "activation", [tcs], [te2], out=te2[:], in_=tcs[:], func=AF.Exp, scale=LD)
            S.i("act", "activation", [td3], [te3], out=te3[:], in_=td3[:], func=AF.Exp, scale=-LD)
            S.i("act", "activation", [te1], [R(PC, c)], out=PC[:, c, :], in_=te1[:, 127:TW:128], func=AF.Copy)
            S.i("dve", "tensor_scalar", [kp, g.vecs], [tkk], out=tkk[:], in0=kp[:, 0:TW], scalar1=V("k_k", c), scalar2=None, op0=ALU.mult)
            S.i("act", "activation", [tkk], [tkk2], out=tkk2[:], in_=tkk[:], func=AF.Square)
            S.i("pe", "matmul", [tkk2, g.cb], [ssp], out=ssp[:, 0:TW], lhsT=blk1, rhs=tkk2[:], start=True, stop=True)
            S.i("act", "activation", [ssp], [trn], out=trn[:], in_=ssp[:, 0:TW], func=AF.Sqrt)
            S.i("dve", "tensor_scalar", [trn], [trn], out=trn[:], in0=trn[:], scalar1=1e-12, scalar2=None, op0=ALU.max)
            S.i("dve", "reciprocal", [trn], [trn], out=trn[:], in_=trn[:])
            S.i("dve", "tensor_tensor", [tkk, trn], [tkkn], out=tkkn[:], in0=tkk[:], in1=trn[:], op=ALU.mult)
            S.i("dve", "tensor_scalar", [tA, g.vecs, om], [tt_], out=tt_[:], in0=tA[:], scalar1=V("k_a", c), scalar2=om[:, 48 + c:49 + c], op0=ALU.mult, op1=ALU.add)
            S.i("dve", "tensor_tensor", [kp, tt_], [tkm], out=tkm[:], in0=kp[:, 0:TW], in1=tt_[:], op=ALU.mult)
            full = slice(0, TW)
            S.i("dve", "scalar_tensor_tensor", [tkkn, te3], [FMK(0, c)], out=FM(0, c, full), in0=tkkn[:], scalar=-1.0, in1=te3[:], op0=ALU.mult, op1=ALU.mult)
            S.i("dve", "tensor_tensor", [tkkn, tA], [ttb], out=ttb[:], in0=tkkn[:], in1=tA[:], op=ALU.mult)
            S.i("dve", "tensor_tensor", [ttb, te2], [FMK(1, c)], out=FM(1, c, full), in0=ttb[:], in1=te2[:], op=ALU.mult)
            S.i("dve", "tensor_tensor", [tkm, te2], [FMK(2, c)], out=FM(2, c, full), in0=tkm[:], in1=te2[:], op=ALU.mult)
            S.i("dve", "tensor_tensor", [rp, te1], [FMK(3, c)], out=FM(3, c, full), in0=rp[:, 0:TW], in1=te1[:], op=ALU.mult)
            S.i("dve", "scalar_tensor_tensor", [rp, g.vecs, tkm], [FMK(4, c)], out=FM(4, c, full), in0=rp[:, 0:TW], scalar=V("r_k", c), in1=tkm[:], op0=ALU.mult, op1=ALU.mult)
        xv = nxm(); mix(3, xv)
        for c in range(NCH):
            wv_ = wsl[nws % 4]; nws += 1
            S.dma("pool", wv_[:], Wrkv[2][:, :, c * 128:(c + 1) * 128], writes=[wv_])
            vp = PB[5 + (c % 2)]
            for k in range(NCH):
                S.i("pe", "matmul", [wv_, xv.key(k)], [vp], out=vp[:, 0:TW], lhsT=wv_[:, k, :], rhs=xv.ap(k), start=(k == 0), stop=(k == NCH - 1))
            S.i("act", "activation", [vp], [FMK(5, c)], out=FM(5, c, slice(0, TW)), in_=vp[:, 0:TW], func=AF.Copy)
        if stg < 5:
            continue
        for n in range(2):
            cs = slice(n * 128, (n + 1) * 128)
            for a_, dst in ((0, Atm), (1, Btm), (2, Ktm), (5, Vtm)):
                for c in range(NCH):
                    S.i("pe", "transpose", [FMK(a_, c), g.cb], [pT], out=pT[:, c * 128:(c + 1) * 128], in_=FM(a_, c, cs), identity=ident)
                S.i("dve" if a_ in (0, 2) else "act", "tensor_copy" if a_ in (0, 2) else "activation", [pT], [dst],
                    **(dict(out=dst[:], in_=pT[:]) if a_ in (0, 2) else dict(out=dst[:], in_=pT[:], func=AF.Copy)))
            if stg < 6:
                continue
            for hgp in range(2):
                grp = [2 * hgp, 2 * hgp + 1]
                def HP(h):
                    return slice((h % 2) * 64, (h % 2) * 64 + 64), h // 2
                for gi, hg in enumerate(grp):
                    heads = [4 * hg + q for q in range(4)]
                    specs = [(1, 0, 0, Mk[gi][0], None), (0, 1, 2, Lk[gi][0], None), (2, 0, 0, Mak[gi], None), (1, 3, 1, Mrb, hg), (2, 3, 1, Mrk, hg)]
                    for si, (la, ra, mki, dst, full16) in enumerate(specs):
                        dview = dst[:] if full16 is None else dst[:, hg * 4:(hg + 1) * 4, :]
                        wkey = dst if full16 is None else R(dst, hg)
                        for par in range(2):
                            bk = PB[(2 * si + par) % 6]
                            for qq in range(2):
                                h = heads[2 * qq + par]
                                ps_, c = HP(h)
                                S.i("pe", "matmul", [FMK(la, c), FMK(ra, c)], [bk], out=bk[:, qq * 128:(qq + 1) * 128], lhsT=FM(la, c, cs)[ps_, :], rhs=FM(ra, c, cs)[ps_, :],
                                    start=True, stop=True)
                            S.i("dve", "tensor_tensor", [bk, g.mk], [wkey], out=dview[:, par::2, :],
                                in0=bk[:, 0:256].rearrange("p (a b) -> p a b", b=128), in1=g.mk[:, mki, 0:256].rearrange("p (a b) -> p a b", b=128), op=ALU.mult)
                if stg < 7:
                    continue
                cur = [0, 0]
                NTin = [Mk[0][0], Mk[1][0]]
                for r in range(7):
                    for gi in range(2):
                        Lc, Mc = Lk[gi][cur[gi]], Mk[gi][cur[gi]]
                        Ln, Mn = Lk[gi][1 - cur[gi]], Mk[gi][1 - cur[gi]]
                        bN, bL, bM = PB[3 * gi], PB[3 * gi + 1], PB[3 * gi + 2]
                        if r >= 1:
                            NTo = NTt[gi][r % 2]
                            for q in range(4):
                                S.i("pe", "matmul", [NTin[gi], g.cb], [bN], out=bN[:, q * 128:(q + 1) * 128], lhsT=ident, rhs=NTin[gi][:, q, :], start=True, stop=False)
                                S.i("pe", "matmul", [Mc, g.cb], [bN], out=bN[:, q * 128:(q + 1) * 128], lhsT=ident, rhs=Mc[:, q, :], start=False, stop=False)
                                S.i("pe", "matmul", [Lc, NTin[gi]], [bN], out=bN[:, q * 128:(q + 1) * 128], lhsT=Lc[:, q, :], rhs=NTin[gi][:, q, :], start=False, stop=True)
                        if r < 6:
                            for q in range(4):
                                S.i("pe", "matmul", [Mc, Lc], [bL], out=bL[:, q * 128:(q + 1) * 128], lhsT=Mc[:, q, :], rhs=Lc[:, q, :], start=True, stop=True)
                            for q in range(4):
                                S.i("pe", "matmul", [Lc, Mc], [bM], out=bM[:, q * 128:(q + 1) * 128], lhsT=Lc[:, q, :], rhs=Mc[:, q, :], start=True, stop=True)
                        if r >= 1:
                            if gi == 0:
                                S.i("act", "activation", [bN], [NTo], out=NTo[:].rearrange("p a b -> p (a b)"), in_=bN[:], func=AF.Copy)
                            else:
                                S.i("dve", "tensor_copy", [bN], [NTo], out=NTo[:].rearrange("p a b -> p (a b)"), in_=bN[:])
                            NTin[gi] = NTo
                        if r < 6:
                            S.i("act", "activation", [bL], [Ln], out=Ln[:].rearrange("p a b -> p (a b)"), in_=bL[:], func=AF.Copy)
                            S.i("dve", "tensor_copy", [bM], [Mn], out=Mn[:].rearrange("p a b -> p (a b)"), in_=bM[:])
                            cur[gi] = 1 - cur[gi]
                if stg < 8:
                    continue
                for gi, hg in enumerate(grp):
                    heads = [4 * hg + q for q in range(4)]
                    NTf = NTin[gi]
                    for q, h in enumerate(heads):
                        S.i("pe", "matmul", [Mak[gi], Vtm], [PB[6]], out=PB[6][:, q * 64:(q + 1) * 64], lhsT=Mak[gi][:, q, :], rhs=Vtm[:, h * 64:(h + 1) * 64], start=True, stop=True)
                    S.i("act", "activation", [PB[6]], [Xb], out=Xb[:], in_=PB[6][:, 0:256], func=AF.Copy)
                    for q, h in enumerate(heads):
                        S.i("pe", "matmul", [Xb, g.cb], [PB[6]], out=PB[6][:, 256 + q * 64:256 + (q + 1) * 64], lhsT=ident, rhs=Xb[:, q * 64:(q + 1) * 64], start=True, stop=False)
                        S.i("pe", "matmul", [NTf, Xb], [PB[6]], out=PB[6][:, 256 + q * 64:256 + (q + 1) * 64], lhsT=NTf[:, q, :], rhs=Xb[:, q * 64:(q + 1) * 64], start=False, stop=True)
                    S.i("dve", "tensor_copy", [PB[6]], [R(Vhat, hg)], out=Vhat[:, hg * 256:(hg + 1) * 256], in_=PB[6][:, 256:512])
                    bA = PB[gi]
                    for q, h in enumerate(heads):
                        ps_, c = HP(h)
                        cc = (c % 2) * 128
                        S.i("pe", "matmul", [Atm, NTf], [bA], out=bA[ps_, cc:cc + 128], lhsT=Atm[:, h * 64:(h + 1) * 64], rhs=NTf[:, q, :], start=True, stop=True)
                    S.i("dve", "tensor_tensor", [bA, FMK(0, 2 * hg), FMK(0, 2 * hg + 1)], [R(AhT, hg)], out=AhT[:, 2 * hg:2 * hg + 2, :],
                        in0=bA[:, 0:256].rearrange("p (a b) -> p a b", b=128), in1=xb[:, 2 * hg:2 * hg + 2, cs.start:cs.stop], op=ALU.add)
            if stg < 9:
                continue
            Ubv = Ub[:].rearrange("p (h n) -> p h n", n=64)
            Vhv = Vhat[:].rearrange("p (h n) -> p h n", n=64)
            for par in range(2):
                bk = PB[par]
                for hh in range(8):
                    h = 2 * hh + par
                    ps_, c = slice(par * 64, par * 64 + 64), h // 2
                    S.i("pe", "matmul", [AhT, ST], [bk], out=bk[:, hh * 64:(hh + 1) * 64], lhsT=AhT[ps_, c, :], rhs=ST[ps_, c, :], start=True, stop=True)
            for par in range(2):
                S.i("dve", "tensor_tensor", [PB[par], Vhat], [R(Ub, par)], out=Ubv[:, par::2, :], in0=PB[par][:].rearrange("p (h n) -> p h n", n=64),
                    in1=Vhv[:, par::2, :], op=ALU.add)
            for c in range(NCH):
                S.i("act", "activation", [R(ST, c), PC], [R(STs, c)], out=STs[:, c, :], in_=ST[:, c, :], func=AF.Identity, scale=PC[:, c, n:n + 1])
            for par in range(2):
                bk = PB[2 + par]
                for hh in range(8):
                    h = 2 * hh + par
                    ps_, c = slice(par * 64, par * 64 + 64), h // 2
                    o_ = bk[:, hh * 64:(hh + 1) * 64]
                    S.i("pe", "matmul", [FMK(3, c), ST], [bk], out=o_, lhsT=FM(3, c, cs)[ps_, :], rhs=ST[ps_, c, :], start=True, stop=False)
                    S.i("pe", "matmul", [Mrk, Vtm], [bk], out=o_, lhsT=Mrk[:, h, :], rhs=Vtm[:, h * 64:(h + 1) * 64], start=False, stop=False)
                    S.i("pe", "matmul", [Mrb, Ub], [bk], out=o_, lhsT=Mrb[:, h, :], rhs=Ub[:, h * 64:(h + 1) * 64], start=False, stop=True)
            for h in range(16):
                ps_, c = slice((h % 2) * 64, (h % 2) * 64 + 64), h // 2
                o_ = PB[4][ps_, c * 64:(c + 1) * 64]
                S.i("pe", "matmul", [Ktm, Vtm], [PB[4]], out=o_, lhsT=Ktm[:, h * 64:(h + 1) * 64], rhs=Vtm[:, h * 64:(h + 1) * 64], start=True, stop=False)
                S.i("pe", "matmul", [Btm, Ub], [PB[4]], out=o_, lhsT=Btm[:, h * 64:(h + 1) * 64], rhs=Ub[:, h * 64:(h + 1) * 64], start=False, stop=True)
            for c in range(NCH):
                S.i("dve", "scalar_tensor_tensor", [PB[4], PC, R(STs, c)], [R(ST, c)], out=ST[:, c, :], in0=PB[4][:, c * 64:(c + 1) * 64], scalar=PC[:, c, n:n + 1],
                    in1=STs[:, c, :], op0=ALU.mult, op1=ALU.add)
            for b in range(2):
                S.i("dve", "tensor_reduce", [PB[2 + b]], [R(st4, ("s", b))], out=st4[:, 0, b::2], in_=PB[2 + b][:].rearrange("p (h n) -> p h n", n=64), axis=AX.X, op=ALU.add)
                S.i("act", "activation", [PB[2 + b]], [R(yn, None)], out=yn[:, b * 512:(b + 1) * 512], in_=PB[2 + b][:], func=AF.Square)
                S.i("dve", "tensor_reduce", [R(yn, None)], [R(st4, ("q", b))], out=st4[:, 1, b::2], in_=yn[:, b * 512:(b + 1) * 512].rearrange("p (h n) -> p h n", n=64), axis=AX.X, op=ALU.add)
            S.i("dve", "tensor_scalar", [st4], [R(st4, "m")], out=st4[:, 2, :], in0=st4[:, 0, :], scalar1=1.0 / 64, scalar2=None, op0=ALU.mult)
            S.i("dve", "tensor_tensor", [st4], [R(st4, "v")], out=st4[:, 3, :], in0=st4[:, 2, :], in1=st4[:, 2, :], op=ALU.mult)
            S.i("dve", "scalar_tensor_tensor", [st4], [R(st4, "v")], out=st4[:, 3, :], in0=st4[:, 1, :], scalar=1.0 / 64, in1=st4[:, 3, :], op0=ALU.mult, op1=ALU.subtract)
            S.i("dve", "tensor_scalar", [st4], [R(st4, "v")], out=st4[:, 3, :], in0=st4[:, 3, :], scalar1=64e-5, scalar2=None, op0=ALU.add)
            S.i("act", "activation", [st4], [R(st4, "v")], out=st4[:, 3, :], in_=st4[:, 3, :], func=AF.Sqrt)
            S.i("dve", "reciprocal", [st4], [R(st4, "v")], out=st4[:, 3, :], in_=st4[:, 3, :])
            for c in range(NCH):
                S.i("pe", "matmul", [FMK(4, c), g.cb], [PB[5]], out=PB[5][:, 2 * c:2 * c + 2], lhsT=FM(4, c, cs), rhs=hind, start=True, stop=True)
            S.i("dve", "tensor_copy", [PB[5]], [bsum], out=bsum[:], in_=PB[5][:, 0:16])
            for h in range(16):
                b = h % 2
                hs = slice(h * 64, (h + 1) * 64)
                S.i("dve", "tensor_scalar", [PB[2 + b], st4], [R(yn, h)], out=yn[:, hs], in0=PB[2 + b][:, (h // 2) * 64:(h // 2 + 1) * 64],
                    scalar1=st4[:, 2, h:h + 1], scalar2=st4[:, 3, h:h + 1], op0=ALU.subtract, op1=ALU.mult)
            S.i("dve", "tensor_tensor", [yn, rows], [yn], out=yn[:], in0=yn[:], in1=rows[:, 0, :], op=ALU.mult)
            S.i("dve", "tensor_tensor", [yn, rows], [yn], out=yn[:], in0=yn[:], in1=rows[:, 1, :], op=ALU.add)
            for h in range(16):
                hs = slice(h * 64, (h + 1) * 64)
                S.i("dve", "scalar_tensor_tensor", [Vtm, bsum, yn], [R(yn, h)], out=yn[:, hs], in0=Vtm[:, hs], scalar=bsum[:, h:h + 1], in1=yn[:, hs], op0=ALU.mult, op1=ALU.add)
            for b in range(2):
                S.i("pe", "matmul", [sg, g2], [PB[b]], out=PB[b][:], lhsT=sg[:, 0, cs], rhs=g2[:, 0, b * 512:(b + 1) * 512], start=True, stop=False)
                S.i("pe", "matmul", [sg, g2], [PB[b]], out=PB[b][:], lhsT=sg[0:32, 1, cs], rhs=g2[0:32, 1, b * 512:(b + 1) * 512], start=False, stop=True)
                S.i("dve", "tensor_tensor", [yn, PB[b]], [R(ogtm, b)], out=ogtm[:, b * 512:(b + 1) * 512], in0=yn[:, b * 512:(b + 1) * 512], in1=PB[b][:], op=ALU.mult)
            for c in range(NCH):
                S.i("pe", "transpose", [ogtm, g.cb], [pT], out=pT[:, c * 128:(c + 1) * 128], in_=ogtm[:, c * 128:(c + 1) * 128], identity=ident)
            for c in range(NCH):
                S.i("act", "activation", [pT], [R(ogT, (c, n))], out=ogT[:, c, cs], in_=pT[:, c * 128:(c + 1) * 128], func=AF.Copy)
        if stg < 11:
            continue
        for m in range(NCH):
            wo = wsl[nws % 4]; nws += 1
            S.dma("pool", wo[:], Wo[:, :, m * 128:(m + 1) * 128], writes=[wo])
            bk = PB[5 + (m % 2)]
            for k in range(NCH):
                S.i("pe", "matmul", [wo, ogT], [bk], out=bk[:, 0:TW], lhsT=wo[:, k, :], rhs=ogT[:, k, :], start=(k == 0), stop=(k == NCH - 1))
            xs = x[:, m, t0:t0 + TW]
            S.i("dve", "scalar_tensor_tensor", [bk, R(x, (m, tt))], [R(x, (m, tt))], out=xs, in0=xs, scalar=ALPHA, in1=bk[:, 0:TW], op0=ALU.mult, op1=ALU.add)
    S.pop()
    S.push()
    layer_norm(g, ("ln1_g%d" % i, "ln1_b%d" % i), range(4))
    S.pop()

def make_consts():
    c = np.zeros((128, 1024), np.float32)
    c[:, 0:128] = 1.0
    c[:, 128:256] = np.eye(128, dtype=np.float32)
    bo = np.zeros((128, 128), np.float32); bo[:64, :64] = 1.0; bo[64:, 64:] = 1.0
    c[:, 256:384] = bo
    t = np.arange(128)
    c[:, 384:512] = np.where(t[None, :] <= t[:, None], 0.0, -30000.0)
    c[:, 512:640] = (t[:, None] < t[None, :]).astype(np.float32)
    c[:, 640:768] = (t[:, None] <= t[None, :]).astype(np.float32)
    c[:64, 768] = 1.0; c[64:, 769] = 1.0
    return c


def make_ret_tables():
    dk = 256
    inv = (1.0 / (np.float32(10000.0) ** np.linspace(0.0, 1.0, dk // 2, dtype=np.float32))).astype(np.float32)
    pos = np.arange(T_SEQ, dtype=np.float32)
    ang = (pos[:, None] * inv[None, :]).astype(np.float32)
    cos = np.cos(ang).astype(np.float32); sin = np.sin(ang).astype(np.float32)
    f = np.arange(dk)
    cosf = cos[:, f // 2].T
    sgn = np.where(f % 2 == 0, -1.0, 1.0).astype(np.float32)
    sinf = (sin[:, f // 2].T * sgn[:, None]).astype(np.float32)
    tab = np.stack([cosf.reshape(2, 128, T_SEQ).transpose(1, 0, 2), sinf.reshape(2, 128, T_SEQ).transpose(1, 0, 2)])
    m = np.zeros((4, 128, 384), np.float32)
    idx = np.arange(128, dtype=np.float64)
    for h in range(4):
        lg = np.log(1.0 - 2.0 ** (-5.0 - h))
        rel = idx[None, :] - idx[:, None]
        m[h, :, 0:128] = np.where(rel >= 0, np.exp(lg * np.maximum(rel, 0)), 0.0) / 16.0
        m[h, :, 128:256] = np.exp(lg * (idx + 1.0))[None, :]
        m[h, :, 256] = np.exp(lg * (127.0 - idx)) / 16.0
        m[h, :, 257] = np.exp(lg * 128.0)
    return np.ascontiguousarray(tab.astype(np.float32)), m


FULL_PLAN = [("rwkv", 0, 0), ("ffn", 0), ("ret", 1), ("ffn", 1), ("moba", 2), ("ffn", 2), ("rwkv", 3, 1), ("ffn", 3)]
_PLAN = FULL_PLAN
_NC_CACHE = {}


def host_inputs(inputs):
    shared = {k: np.ascontiguousarray(np.asarray(inputs[k], np.float32)) for k in WEIGHT_SHAPES}
    shared["vecs"] = pack_vecs(inputs)
    shared["consts"] = make_consts()
    rows = np.zeros((4, 128, 1024), np.float32)
    for j in range(2):
        rows[2 * j] = np.broadcast_to(np.asarray(inputs["rwkv_gn_g"][j], np.float32)[None, :], (128, 1024))
        rows[2 * j + 1] = np.broadcast_to(np.asarray(inputs["rwkv_gn_b"][j], np.float32)[None, :], (128, 1024))
    shared["rows"] = rows
    ti = np.arange(128)
    mk = np.zeros((128, 3, 512), np.float32)
    mk[:, 0, :] = np.tile((ti[:, None] < ti[None, :]).astype(np.float32), (1, 4))
    mk[:, 1, :] = np.tile((ti[:, None] <= ti[None, :]).astype(np.float32), (1, 4))
    mk[:, 2, :] = np.tile((ti[:, None] > ti[None, :]).astype(np.float32), (1, 4))
    shared["mk"] = mk
    tab, msk = make_ret_tables()
    shared["rtab"] = tab
    shared["rmask"] = msk
    wqk = np.asarray(inputs["ret_w_in"][0][:, :2048], np.float32)
    shared["ret_sw"] = np.ascontiguousarray(wqk.reshape(1024, 1024, 2)[:, :, ::-1].reshape(1024, 2048))
    return shared


def run_plan(plan, inputs, x_full, n_cores=8):
    key = repr(plan)
    if key not in _NC_CACHE:
        _NC_CACHE[key] = build_program(plan)
    nc = _NC_CACHE[key]
    shared = host_inputs(inputs)
    in_maps = []
    for b in range(n_cores):
        m = dict(shared)
        m["xT"] = np.ascontiguousarray(np.asarray(x_full[b], np.float32).T)
        in_maps.append(m)
    res = run_bass_kernel_spmd(nc, in_maps, core_ids=list(range(n_cores)))
    return np.stack([np.ascontiguousarray(r["outT"].T) for r in res.results]).astype(np.float32)


def kernel(**inputs):
    return run_plan(_PLAN, inputs, inputs["x"], 8)
```

```python
import numpy as np
import concourse.bass as bass
import concourse.mybir as mybir
from concourse.bass_utils import run_bass_kernel_spmd
from contextlib import ExitStack

_DBG = {}
F32 = mybir.dt.float32
BF16 = mybir.dt.bfloat16
AF = mybir.ActivationFunctionType
ALU = mybir.AluOpType
AX = mybir.AxisListType

ENGS = ("pe", "act", "dve", "pool", "sp")


class T:
    def __init__(self, S, t, name):
        self.S = S
        self.t = t
        self.name = name
        self.st = {}
        self.is_psum = False
        self.dsem = None
        self.dcnt = 0

    def __getitem__(self, idx):
        return self.t[idx]


class R:
    __slots__ = ("tile", "key")
    def __init__(self, tile, key=None):
        self.tile = tile
        self.key = key


class Sched:
    def __init__(self, nc):
        self.nc = nc
        self.es = ExitStack()
        self.ins = {e: [] for e in ENGS}
        self.cnt = {e: 0 for e in ENGS}
        self.seen = {e: {} for e in ENGS}
        self.sems = {}
        self.needed = {e: set() for e in ENGS}
        self.nsem = 0
        for e in ENGS:
            self.sems[("e", e)] = self.es.enter_context(nc.semaphore("sem_" + e))
        self.dma_pool = []
        self.out_dma = []
        self.pend = {}
        self.scopes = []

    def _es(self):
        return self.scopes[-1] if self.scopes else self.es

    def push(self):
        self.scopes.append(ExitStack())

    def pop(self):
        self.barrier()
        self.scopes.pop().close()

    def sbuf(self, name, shape, dt):
        self.uid = getattr(self, "uid", 0) + 1
        name = "%s_%d" % (name, self.uid)
        t = self._es().enter_context(self.nc.sbuf_tensor(name, list(shape), dt))
        return T(self, t, name)

    def psum(self, name, shape, dt=F32):
        self.uid = getattr(self, "uid", 0) + 1
        name = "%s_%d" % (name, self.uid)
        t = self._es().enter_context(self.nc.psum_tensor(name, list(shape), dt))
        tt = T(self, t, name)
        tt.is_psum = True
        return tt

    def new_dsem(self, name):
        self.nsem += 1
        key = ("d", self.nsem)
        self.sems[key] = self.es.enter_context(self.nc.semaphore("dsem%d_%s" % (self.nsem, name)))
        return key

    def _states(self, ref, create=True):
        tile, key = ref.tile, ref.key
        if key is None:
            if None not in tile.st:
                tile.st[None] = [None, {}]
            return list(tile.st.values())
        out = []
        if None in tile.st:
            out.append(tile.st[None])
        if key not in tile.st:
            tile.st[key] = [None, {}]
        out.append(tile.st[key])
        return out

    def _collect(self, eng, reads, writes):
        waits = {}
        def need(w, same_ok=False):
            if w is None:
                return
            k, v = w
            if same_ok and k == ("e", eng) and eng == "pe":
                return
            if waits.get(k, 0) < v:
                waits[k] = v
        for r in reads:
            for st in self._states(r):
                need(st[0])
                if r.tile.is_psum:
                    for rk, rv in st[1].items():
                        if rk != ("e", eng):
                            need((rk, rv))
        for w in writes:
            for st in self._states(w):
                need(st[0])
                for rk, rv in st[1].items():
                    if rk == ("e", eng) and eng == "pe":
                        continue
                    need((rk, rv))
        final = []
        for k, v in waits.items():
            if k == ("e", eng) and eng == "pe":
                continue
            if self.seen[eng].get(k, 0) < v:
                self.seen[eng][k] = v
                final.append((k, v))
                if k[0] == "e":
                    self.needed[k[1]].add(v)
        return final

    def _commit(self, tag, reads, writes):
        for r in reads:
            if r.key is None:
                for k, s in r.tile.st.items():
                    if s[1].get(tag[0], 0) < tag[1]:
                        s[1][tag[0]] = tag[1]
            else:
                s = r.tile.st[r.key]
                if s[1].get(tag[0], 0) < tag[1]:
                    s[1][tag[0]] = tag[1]
        for w in writes:
            if w.key is None:
                w.tile.st = {None: [tag, {}]}
            else:
                w.tile.st[w.key] = [tag, {}]

    def i(self, eng, meth, reads=(), writes=(), **kw):
        return self.op(eng, (meth, kw), reads, writes)

    def _norm(self, refs):
        out = []
        for r in refs:
            if not isinstance(r, R):
                r = R(r)
            if r.tile.is_psum and r.key is not None:
                r = R(r.tile)
            out.append(r)
        return out

    def op(self, eng, fn, reads=(), writes=()):
        reads = self._norm(reads)
        writes = self._norm(writes)
        waits = self._collect(eng, reads, writes)
        self.cnt[eng] += 1
        idx = self.cnt[eng]
        self.ins[eng].append([fn, waits, idx, None])
        self._commit((("e", eng), idx), reads, writes)
        return idx

    def dma(self, q, out_ap, in_ap, reads=(), writes=(), out_final=False, owner=None, **kw):
        reads = [r if isinstance(r, R) else R(r) for r in reads]
        writes = [w if isinstance(w, R) else R(w) for w in writes]
        waits = self._collect(q, reads, writes)
        if owner is None:
            owner = (writes[0] if writes else reads[0]).tile
        if owner.dsem is None:
            owner.dsem = self.new_dsem(owner.name)
        owner.dcnt += 16
        tag = (owner.dsem, owner.dcnt)
        self.cnt[q] += 1
        idx = self.cnt[q]
        self.ins[q].append([lambda e: e.dma_start(out=out_ap, in_=in_ap, **kw), waits, idx, tag])
        self._commit(tag, reads, writes)
        self.pend[tag[0]] = tag[1]
        if out_final:
            self.out_dma.append(tag)
        return tag

    def barrier(self):
        for f in ENGS:
            if self.cnt[f] and (self.ins[f][-1][0] is None or self.ins[f][-1][3] is not None):
                self.cnt[f] += 1
                self.ins[f].append([None, [], self.cnt[f], None])
        for e in ENGS:
            waits = []
            for f in ENGS:
                if f == e or self.cnt[f] == 0:
                    continue
                v = self.cnt[f]
                if self.seen[e].get(("e", f), 0) < v:
                    self.seen[e][("e", f)] = v
                    waits.append((("e", f), v))
                    self.needed[f].add(v)
            for k, v in self.pend.items():
                if self.seen[e].get(k, 0) < v:
                    self.seen[e][k] = v
                    waits.append((k, v))
            if waits:
                self.cnt[e] += 1
                self.ins[e].append([None, waits, self.cnt[e], None])

    def emit(self):
        nc = self.nc
        fwd = {}
        for k, v in self.out_dma:
            fwd[k] = max(fwd.get(k, 0), v)
        fw = list(fwd.items())
        self.cnt["sp"] += 1
        self.ins["sp"].append([None, fw, self.cnt["sp"], None])
        rank = {}
        for e in ENGS:
            s = sorted(self.needed[e])
            rank[e] = {v: i + 1 for i, v in enumerate(s)}
        engmap = {"pe": "tensor", "act": "scalar", "dve": "vector", "pool": "gpsimd", "sp": "sync"}
        with nc.Block() as block:
            for e in ENGS:
                lst = self.ins[e]
                if not lst:
                    continue
                def body(eng, e=e, lst=lst):
                    for fn, waits, idx, dtag in lst:
                        for k, v in waits:
                            if k[0] == "e":
                                eng.wait_ge(self.sems[k], rank[k[1]][v])
                            else:
                                eng.wait_ge(self.sems[k], v)
                        if fn is None:
                            if idx in self.needed[e]:
                                eng.nop().then_inc(self.sems[("e", e)], 1)
                            continue
                        ins = fn(eng) if callable(fn) else getattr(eng, fn[0])(**fn[1])
                        if dtag is not None:
                            ins.then_inc(self.sems[dtag[0]], 16)
                            if idx in self.needed[e]:
                                raise RuntimeError("dma instr needed as engine milestone")
                        elif idx in self.needed[e]:
                            ins.then_inc(self.sems[("e", e)], 1)
                getattr(block, engmap[e])(body)
        self.es.close()

D = 1024; T_SEQ = 2048; DEPTH = 4; FF = 2816; NCH = 8; NPAIR = 22
ALPHA = float((2 * DEPTH) ** 0.25)
LN_EPS = 1e-5

WEIGHT_SHAPES = {
    "rwkv_w_rkv": [2, 3, 1024, 1024], "rwkv_w1": [2, 1024, 64], "rwkv_w2": [2, 64, 1024],
    "rwkv_a1": [2, 1024, 64], "rwkv_a2": [2, 64, 1024], "rwkv_g1": [2, 1024, 160], "rwkv_g2": [2, 160, 1024],
    "rwkv_w_o": [2, 1024, 1024], "ret_w_in": [1, 1024, 6144], "ret_w_o": [1, 2048, 1024],
    "moba_w_qkv": [1, 1024, 3072], "moba_w_o": [1, 1024, 1024],
    "ffn_w_in": [4, 1024, 5632], "ffn_w_out": [4, 2816, 1024],
}


def _fm(v):
    v = np.asarray(v, np.float32).reshape(-1, 128)
    return np.ascontiguousarray(v.T)


def vec_layout():
    off = {}
    n = 0
    def add(name, cols):
        nonlocal n
        off[name] = n
        n += cols
    for i in range(4):
        for nm in ("ln1_g", "ln1_b", "ln2_g", "ln2_b"):
            add("%s%d" % (nm, i), 8)
        for k in range(3):
            add("cw%d_%d" % (k, i), 44)
        add("cb_%d" % i, 44)
    for j in range(2):
        for m in range(6):
            add("mix%d_%d" % (m, j), 8)
        for nm in ("w0", "a0", "k_k", "k_a", "r_k"):
            add("%s_%d" % (nm, j), 8)
    add("ret_gn_g", 16)
    add("ret_gn_b", 16)
    return off, n


def pack_vecs(inp):
    off, n = vec_layout()
    out = np.zeros((128, n), np.float32)
    def put(name, v):
        a = _fm(v)
        out[:, off[name]:off[name] + a.shape[1]] = a
    for i in range(4):
        for nm in ("ln1_g", "ln1_b", "ln2_g", "ln2_b"):
            put("%s%d" % (nm, i), inp[nm][i])
        for k in range(3):
            put("cw%d_%d" % (k, i), inp["ffn_conv_w"][i, k])
        put("cb_%d" % i, inp["ffn_conv_b"][i])
    for j in range(2):
        for m in range(6):
            put("mix%d_%d" % (m, j), inp["rwkv_mix"][j, m])
        put("w0_%d" % j, inp["rwkv_w0"][j]); put("a0_%d" % j, inp["rwkv_a0"][j])
        put("k_k_%d" % j, inp["rwkv_k_k"][j]); put("k_a_%d" % j, inp["rwkv_k_a"][j])
        put("r_k_%d" % j, inp["rwkv_r_k"][j].reshape(-1))
    put("ret_gn_g", inp["ret_gn_g"][0]); put("ret_gn_b", inp["ret_gn_b"][0])
    return out


class Ctx:
    pass


def build_program(plan, x_in_bf=True):
    nc = bass.Bass("TRN2", target_bir_lowering=False)
    S = Sched(nc)
    g = Ctx()
    g.nc, g.S = nc, S
    g.W = {k: nc.dram_tensor(k, shp, F32, kind="ExternalInput").ap() for k, shp in WEIGHT_SHAPES.items()}
    voff, nv = vec_layout()
    g.voff = voff
    xT_d = nc.dram_tensor("xT", [D, T_SEQ], F32, kind="ExternalInput").ap()
    vecs_d = nc.dram_tensor("vecs", [128, nv], F32, kind="ExternalInput").ap()
    consts_d = nc.dram_tensor("consts", [128, 1024], F32, kind="ExternalInput").ap()
    g.rows_d = nc.dram_tensor("rows", [4, 128, 1024], F32, kind="ExternalInput").ap()
    g.rtab_d = nc.dram_tensor("rtab", [2, 128, 2, T_SEQ], F32, kind="ExternalInput").ap()
    g.rmask_d = nc.dram_tensor("rmask", [4, 128, 384], F32, kind="ExternalInput").ap()
    g.ret_sw_d = nc.dram_tensor("ret_sw", [1024, 2048], F32, kind="ExternalInput").ap()
    g.mk_d = nc.dram_tensor("mk", [128, 3, 512], F32, kind="ExternalInput").ap()
    out_d = nc.dram_tensor("outT", [D, T_SEQ], F32, kind="ExternalOutput").ap()

    g.x = S.sbuf("x", [128, NCH, T_SEQ], F32)
    g.xb = S.sbuf("xb", [128, NCH, T_SEQ], BF16)
    g.vecs = S.sbuf("vecs", [128, nv], F32)
    g.cf = S.sbuf("cf", [128, 256], F32)
    g.cb = S.sbuf("cb", [128, 512], BF16)
    xv = xT_d.rearrange("(c p) t -> p c t", p=128)
    for c in range(NCH):
        S.dma("sp", g.x[:, c, :], xv[:, c, :], writes=[R(g.x, None)])
    if plan[0][0] != "rwkv":
        for c in range(NCH):
            S.dma("pool", g.xb[:, c, :], xv[:, c, :], writes=[R(g.xb, None)])
    S.dma("sp", g.vecs[:], vecs_d, writes=[g.vecs])
    S.dma("sp", g.cf[:, 0:128], consts_d[:, 0:128], writes=[R(g.cf, None)])
    S.dma("sp", g.cf[:, 128:256], consts_d[:, 384:512], writes=[R(g.cf, None)])
    S.i("pool", "memset", [], [g.cb], ap=g.cb[:], constant=0.0)
    S.dma("pool", g.cb[:, 0:256], consts_d[:, 128:384], writes=[R(g.cb, None)])
    S.dma("pool", g.cb[:, 256:258], consts_d[:, 768:770], writes=[R(g.cb, None)])
    g.mk = S.sbuf("mk", [128, 3, 512], BF16)
    S.dma("pool", g.mk[:], g.mk_d, writes=[g.mk])
    g.ones32 = g.cf[:, 0:128]
    g.identb = g.cb[:, 0:128]

    for step in plan:
        kind = step[0]
        if kind == "ffn":
            ffn_phase(g, step[1])
        elif kind == "ln":
            S.push()
            layer_norm(g, step[1], range(4))
            S.pop()
        elif kind == "moba":
            moba_phase(g, step[1])
        elif kind == "ret":
            ret_phase(g, step[1])
        elif kind == "rwkv":
            rwkv_phase(g, step[1], step[2])
        else:
            raise ValueError(kind)

    S.barrier()
    ov = out_d.rearrange("(c p) t -> p c t", p=128)
    outsem = T(S, None, "outsem")
    for c in range(NCH):
        S.dma("sp", ov[:, c, :], g.x[:, c, :], reads=[R(g.x, None)], out_final=True)
    S.emit()
    return nc


def vcol(g, name, c=0):
    o = g.voff[name] + c
    return g.vecs[:, o:o + 1]


def layer_norm(g, pref, tts, width=512, write_xb=True):
    S = g.S
    gname, bname = pref
    x, xb = g.x, g.xb
    sq = [S.sbuf("lnsq", [128, 512], F32) for _ in range(2)]
    mean = S.sbuf("lnmean", [128, 512], F32)
    msq = S.sbuf("lnmsq", [128, 512], F32)
    rstd = S.sbuf("lnrstd", [128, 512], F32)
    tmp = [S.sbuf("lntmp", [128, 512], F32) for _ in range(3)]
    ps_s = S.psum("lnps", [128, 512])
    ps_q = S.psum("lnpq", [128, 512])
    ones = g.ones32
    W_ = width
    for tix in tts:
        ts = slice(tix * W_, (tix + 1) * W_)
        tt = (tix * W_) // 512
        for c in range(NCH):
            q = sq[c % 2]
            S.i("act", "activation", [R(x, (c, tt))], [q], out=q[:, 0:W_], in_=x[:, c, ts], func=AF.Square)
            S.i("pe", "matmul", [R(x, (c, tt)), g.cf], [ps_s], out=ps_s[:, 0:W_], lhsT=ones, rhs=x[:, c, ts], start=(c == 0), stop=(c == NCH - 1))
            S.i("pe", "matmul", [q, g.cf], [ps_q], out=ps_q[:, 0:W_], lhsT=ones, rhs=q[:, 0:W_], start=(c == 0), stop=(c == NCH - 1))
        S.i("dve", "tensor_scalar", [ps_s], [mean], out=mean[:, 0:W_], in0=ps_s[:, 0:W_], scalar1=1.0 / D, scalar2=None, op0=ALU.mult)
        S.i("dve", "tensor_tensor", [mean], [msq], out=msq[:, 0:W_], in0=mean[:, 0:W_], in1=mean[:, 0:W_], op=ALU.mult)
        S.i("dve", "scalar_tensor_tensor", [ps_q, msq], [msq], out=msq[:, 0:W_], in0=ps_q[:, 0:W_], scalar=1.0 / D, in1=msq[:, 0:W_], op0=ALU.mult, op1=ALU.subtract)
        S.i("dve", "tensor_scalar", [msq], [msq], out=msq[:, 0:W_], in0=msq[:, 0:W_], scalar1=LN_EPS, scalar2=None, op0=ALU.add)
        S.i("act", "activation", [msq], [rstd], out=rstd[:, 0:W_], in_=msq[:, 0:W_], func=AF.Sqrt)
        S.i("dve", "reciprocal", [rstd], [rstd], out=rstd[:, 0:W_], in_=rstd[:, 0:W_])
        for c in range(NCH):
            tm = tmp[c % 3]
            S.i("dve", "tensor_tensor", [R(x, (c, tt)), mean], [tm], out=tm[:, 0:W_], in0=x[:, c, ts], in1=mean[:, 0:W_], op=ALU.subtract)
            S.i("dve", "tensor_tensor", [tm, rstd], [tm], out=tm[:, 0:W_], in0=tm[:, 0:W_], in1=rstd[:, 0:W_], op=ALU.mult)
            S.i("act", "activation", [tm, g.vecs], [R(x, (c, tt))], out=x[:, c, ts], in_=tm[:, 0:W_], func=AF.Identity,
                bias=vcol(g, bname, c), scale=vcol(g, gname, c))
            if write_xb:
                S.i("dve", "tensor_scalar", [tm, g.vecs], [R(xb, (c, tt))], out=xb[:, c, ts], in0=tm[:, 0:W_], scalar1=vcol(g, gname, c),
                    scalar2=vcol(g, bname, c), op0=ALU.mult, op1=ALU.add)


def ffn_phase(g, i):
    S = g.S
    x, xb = g.x, g.xb
    S.push()
    W_in = g.W["ffn_w_in"][i].rearrange("(k p) (two f) -> p k two f", p=128, two=2)
    W_out = g.W["ffn_w_out"][i].rearrange("(k p) n -> p k n", p=128)
    TB = 1024
    a = S.sbuf("ffa", [128, NPAIR, TB], BF16)
    hus = [S.sbuf("hu", [128, TB + 2], F32) for _ in range(2)]
    hgs = [S.sbuf("hg", [128, TB + 2], F32) for _ in range(2)]
    aus = [S.sbuf("au", [128, TB], F32) for _ in range(2)]
    ags = [S.sbuf("ag", [128, TB], F32) for _ in range(2)]
    halo = S.sbuf("halo", [128, 2 * NPAIR, 2], F32)
    wps = [[S.sbuf("wp", [128, NCH, 128], BF16) for _ in range(2)] for _ in range(2)]
    wos = [S.sbuf("wo", [128, NPAIR, 128], BF16) for _ in range(2)]
    pb = [S.psum("ffps", [128, 512]) for _ in range(8)]
    cw = lambda k, j: vcol(g, "cw%d_%d" % (k, i), j)
    cbv = lambda j: vcol(g, "cb_%d" % i, j)
    nw = 0
    for blk in range(2):
        for j in range(NPAIR):
            wp = wps[nw % 2]; nw += 1
            hu, hg, au, ag = hus[j % 2], hgs[j % 2], aus[j % 2], ags[j % 2]
            for ug in range(2):
                S.dma("pool", wp[ug][:], W_in[:, :, ug, j * 128:(j + 1) * 128], writes=[wp[ug]])
            banks = pb[(j % 2) * 4:(j % 2) * 4 + 4]
            for ug in range(2):
                for h in range(2):
                    bk = banks[ug * 2 + h]
                    tt = blk * 2 + h
                    for k in range(NCH):
                        S.i("pe", "matmul", [wp[ug], R(xb, (k, tt))], [bk], out=bk[:], lhsT=wp[ug][:, k, :],
                            rhs=xb[:, k, tt * 512:(tt + 1) * 512], start=(k == 0), stop=(k == NCH - 1))
            for ug, hb in ((0, hu), (1, hg)):
                for h in range(2):
                    bk = banks[ug * 2 + h]
                    S.i("act", "activation", [bk], [R(hb, h)], out=hb[:, 2 + h * 512:2 + (h + 1) * 512], in_=bk[:], func=AF.Copy)
                if blk == 0:
                    S.i("pool", "memset", [], [R(hb, "halo")], ap=hb[:, 0:2], constant=0.0)
                else:
                    S.i("pool", "tensor_copy", [R(halo, (ug, j))], [R(hb, "halo")], out=hb[:, 0:2], in_=halo[:, ug * NPAIR + j, :])
            for (hb, ab_, jj) in ((hu, au, j), (hg, ag, NPAIR + j)):
                S.i("act", "activation", [hb, g.vecs], [ab_], out=ab_[:], in_=hb[:, 2:TB + 2], func=AF.Identity, bias=cbv(jj), scale=cw(2, jj))
                S.i("dve", "scalar_tensor_tensor", [hb, ab_, g.vecs], [ab_], out=ab_[:], in0=hb[:, 1:TB + 1], scalar=cw(1, jj), in1=ab_[:], op0=ALU.mult, op1=ALU.add)
                S.i("dve", "scalar_tensor_tensor", [hb, ab_, g.vecs], [ab_], out=ab_[:], in0=hb[:, 0:TB], scalar=cw(0, jj), in1=ab_[:], op0=ALU.mult, op1=ALU.add)
            S.i("act", "activation", [ag], [ag], out=ag[:], in_=ag[:], func=AF.Silu)
            S.i("dve", "tensor_tensor", [au, ag], [R(a, j)], out=a[:, j, :], in0=au[:], in1=ag[:], op=ALU.mult)
            if blk == 0:
                for ug, hb in ((0, hu), (1, hg)):
                    S.i("pool", "tensor_copy", [hb], [R(halo, (ug, j))], out=halo[:, ug * NPAIR + j, :], in_=hb[:, TB:TB + 2])
        for m in range(NCH):
            wo = wos[m % 2]
            S.dma("pool", wo[:], W_out[:, :, m * 128:(m + 1) * 128], writes=[wo])
            for h in range(2):
                bk = pb[(m * 2 + h) % 8]
                tt = blk * 2 + h
                for k in range(NPAIR):
                    S.i("pe", "matmul", [wo, R(a, k)], [bk], out=bk[:], lhsT=wo[:, k, :], rhs=a[:, k, h * 512:(h + 1) * 512],
                        start=(k == 0), stop=(k == NPAIR - 1))
                xs = x[:, m, tt * 512:(tt + 1) * 512]
                S.i("dve", "scalar_tensor_tensor", [bk, R(x, (m, tt))], [R(x, (m, tt))], out=xs, in0=xs, scalar=ALPHA, in1=bk[:],
                    op0=ALU.mult, op1=ALU.add)
    S.pop()
    S.push()
    layer_norm(g, ("ln2_g%d" % i, "ln2_b%d" % i), range(4))
    S.pop()

def moba_phase(g, i):
    S = g.S
    x, xb = g.x, g.xb
    S.push()
    Wqkv = g.W["moba_w_qkv"][0].rearrange("(k p) n -> p k n", p=128)
    Wo = g.W["moba_w_o"][0].rearrange("(k p) n -> p k n", p=128)
    ogT = S.sbuf("ogT", [128, NCH, T_SEQ], BF16)
    wsl = [[S.sbuf("mw", [128, NCH, 128], BF16) for _ in range(3)] for _ in range(2)]
    qkv = [[S.sbuf("mqkv", [128, T_SEQ], BF16) for _ in range(3)] for _ in range(2)]
    vtms = [S.sbuf("vtm", [128, 16, 128], BF16) for _ in range(2)]
    ksum = S.sbuf("ksum", [128, 8], F32)
    kmean = S.sbuf("kmean", [128, 8], BF16)
    P = S.sbuf("mP", [128, T_SEQ], BF16)
    PT = S.sbuf("mPT", [128, 16, 128], BF16)
    sd = S.sbuf("msd", [128, 128], F32)
    g8 = S.sbuf("g8", [128, 8], F32)
    top8 = S.sbuf("top8", [128, 8], F32)
    mb = S.sbuf("mb", [128, 8], F32)
    b8 = S.sbuf("b8", [128, 8], F32)
    nm = S.sbuf("nm", [128, 4], F32)
    rs = S.sbuf("rs", [128, 12], F32)
    rinv = S.sbuf("rinv", [128, 2], F32)
    otm = S.sbuf("otm", [128, 128], BF16)
    wos = [S.sbuf("mwo", [128, NCH, 128], BF16) for _ in range(2)]
    sc = S.psum("msc", [128, 2048])
    pT = S.psum("mpT", [128, 1024], BF16)
    gps = S.psum("mgps", [128, 512])
    ov = S.psum("mov", [128, 512])
    pj = S.psum("mpj", [128, 512])
    tri = g.cf[:, 128:256]
    if _DBG.get('dbg_memset'):
        S.i('pool', 'memset', [], [ogT], ap=ogT[:], constant=0.0)
    ident = g.identb
    for c in range(_DBG.get('moba_pairs', NCH)):
        ws = wsl[c % 2]
        qT, kT, vT = qkv[c % 2]
        vtm = vtms[c % 2]
        for w in range(3):
            S.dma("pool", ws[w][:], Wqkv[:, :, w * 1024 + c * 128: w * 1024 + (c + 1) * 128], writes=[ws[w]])
        for w, dst in ((0, qT), (1, kT), (2, vT)):
            for tt in range(4):
                for k in range(NCH):
                    S.i("pe", "matmul", [ws[w], R(xb, (k, tt))], [pj], out=pj[:], lhsT=ws[w][:, k, :], rhs=xb[:, k, tt * 512:(tt + 1) * 512],
                        start=(k == 0), stop=(k == NCH - 1))
                if w == 0:
                    S.i("act", "activation", [pj], [R(dst, tt)], out=dst[:, tt * 512:(tt + 1) * 512], in_=pj[:], func=AF.Copy, scale=0.125)
                else:
                    S.i("act", "activation", [pj], [R(dst, tt)], out=dst[:, tt * 512:(tt + 1) * 512], in_=pj[:], func=AF.Copy)
                if w == 1 and _DBG.get('moba_stage', 9) >= 0.2:
                    S.i("dve", "tensor_reduce", [pj], [R(ksum, tt)], out=ksum[:, 2 * tt:2 * tt + 2],
                        in_=pj[:].rearrange("p (b k) -> p b k", b=2), axis=AX.X, op=ALU.add)
        if _DBG.get('moba_stage', 9) >= 0.2:
            S.i("dve", "tensor_scalar", [ksum], [kmean], out=kmean[:], in0=ksum[:], scalar1=1.0 / 256.0, scalar2=None, op0=ALU.mult)
        for b in range(2 if _DBG.get('moba_stage', 9) >= 0.3 else 0):
            for t8 in range(8):
                kt = b * 8 + t8
                S.i("pe", "transpose", [vT, g.cb], [pT], out=pT[:, t8 * 128:(t8 + 1) * 128], in_=vT[:, kt * 128:(kt + 1) * 128], identity=ident)
            S.i("dve", "tensor_copy", [pT], [R(vtm, b)], out=vtm[:, b * 8:(b + 1) * 8, :].rearrange("p a b -> p (a b)"), in_=pT[:])
        stg = _DBG.get('moba_stage', 9)
        for qt in _DBG.get('moba_qts', range(16)):
            if stg < 2:
                break
            qb = qt // 2
            nkt = qt + 1
            qs = slice(qt * 128, (qt + 1) * 128)
            for hh in range(2):
                ps = slice(hh * 64, (hh + 1) * 64)
                use_thr = qb >= 4
                if use_thr:
                    S.i("pe", "matmul", [qT, kmean], [gps], out=gps[:, 0:8], lhsT=qT[ps, qs], rhs=kmean[ps, 0:8], start=True, stop=True)
                    S.i("pool", "memset", [], [R(g8, "pad")], ap=g8[:, qb:8], constant=-1.0e30)
                    S.i("dve", "tensor_copy", [gps], [R(g8, "val")], out=g8[:, 0:qb], in_=gps[:, 0:qb])
                    S.i("dve", "max", [g8], [top8], out=top8[:], in_=g8[:])
                    S.i("dve", "tensor_scalar", [g8, top8], [mb], out=mb[:], in0=g8[:], scalar1=top8[:, 2:3], scalar2=30000.0,
                        op0=ALU.is_ge, op1=ALU.mult)
                ncol = nkt * 128
                for b in range((ncol + 511) // 512):
                    w_ = min(512, ncol - b * 512)
                    S.i("pe", "matmul", [qT, kT], [sc], out=sc[:, b * 512:b * 512 + w_], lhsT=qT[ps, qs], rhs=kT[ps, b * 512:b * 512 + w_],
                        start=True, stop=True)
                S.i("dve", "tensor_tensor", [sc, g.cf], [sd], out=sd[:], in0=sc[:, qt * 128:(qt + 1) * 128], in1=tri, op=ALU.add)
                S.i("dve", "tensor_reduce", [sd], [R(nm, 0)], out=nm[:, 0:1], in_=sd[:], axis=AX.X, op=ALU.max, negate=True)
                if qt > 0:
                    S.i("dve", "tensor_reduce", [sc], [R(nm, 1)], out=nm[:, 1:2], in_=sc[:, 0:qt * 128], axis=AX.X, op=ALU.max, negate=True)
                    S.i("dve", "tensor_tensor", [nm], [R(nm, 2)], out=nm[:, 2:3], in0=nm[:, 0:1], in1=nm[:, 1:2], op=ALU.min)
                    negm = nm[:, 2:3]
                else:
                    negm = nm[:, 0:1]
                if stg < 3:
                    continue
                if use_thr:
                    S.i("dve", "tensor_scalar", [mb, nm], [b8], out=b8[:], in0=mb[:], scalar1=negm, scalar2=-30000.0, op0=ALU.add, op1=ALU.add)
                npz = 0
                if use_thr:
                    for n in range(qb):
                        S.i("act", "activation", [sc, b8], [R(P, n), R(rs, npz)], out=P[:, n * 256:(n + 1) * 256], in_=sc[:, n * 256:(n + 1) * 256],
                            func=AF.Exp, bias=b8[:, n:n + 1], scale=1.0, accum_out=rs[:, npz:npz + 1])
                        npz += 1
                    if qt % 2 == 1:
                        S.i("act", "activation", [sc, nm], [R(P, "o"), R(rs, npz)], out=P[:, (qt - 1) * 128:qt * 128], in_=sc[:, (qt - 1) * 128:qt * 128],
                            func=AF.Exp, bias=negm, scale=1.0, accum_out=rs[:, npz:npz + 1])
                        npz += 1
                elif qt > 0:
                    S.i("act", "activation", [sc, nm], [R(P, "past"), R(rs, npz)], out=P[:, 0:qt * 128], in_=sc[:, 0:qt * 128],
                        func=AF.Exp, bias=negm, scale=1.0, accum_out=rs[:, npz:npz + 1])
                    npz += 1
                S.i("act", "activation", [sd, nm], [R(P, "d"), R(rs, npz)], out=P[:, qt * 128:(qt + 1) * 128], in_=sd[:],
                    func=AF.Exp, bias=negm, scale=1.0, accum_out=rs[:, npz:npz + 1])
                npz += 1
                S.i("dve", "tensor_reduce", [rs], [R(rs, 11)], out=rs[:, 11:12], in_=rs[:, 0:npz], axis=AX.X, op=ALU.add)
                S.i("dve", "reciprocal", [R(rs, 11)], [R(rinv, hh)], out=rinv[:, hh:hh + 1], in_=rs[:, 11:12])
                if stg < 4:
                    continue
                for b in range((nkt + 7) // 8):
                    n8 = min(8, nkt - b * 8)
                    for t8 in range(n8):
                        kt = b * 8 + t8
                        S.i("pe", "transpose", [P, g.cb], [pT], out=pT[:, t8 * 128:(t8 + 1) * 128], in_=P[:, kt * 128:(kt + 1) * 128], identity=ident)
                    S.i("dve", "tensor_copy", [pT], [R(PT, b)], out=PT[:, b * 8:b * 8 + n8, :].rearrange("p a b -> p (a b)"), in_=pT[:, 0:n8 * 128])
                for kt in range(nkt):
                    S.i("pe", "matmul", [PT, vtm], [R(ov, hh)], out=ov[:, hh * 64:(hh + 1) * 64], lhsT=PT[:, kt, :], rhs=vtm[:, kt, hh * 64:(hh + 1) * 64],
                        start=(kt == 0), stop=(kt == nkt - 1))
                S.i("dve", "tensor_scalar", [R(ov, hh), R(rinv, hh)], [R(otm, hh)], out=otm[:, hh * 64:(hh + 1) * 64], in0=ov[:, hh * 64:(hh + 1) * 64],
                    scalar1=rinv[:, hh:hh + 1], scalar2=None, op0=ALU.mult)
            if stg < 5:
                continue
            S.i("pe", "transpose", [otm, g.cb], [pT], out=pT[:, 0:128], in_=otm[:], identity=ident)
            S.i("act", "activation", [pT], [R(ogT, (c, qt))], out=ogT[:, c, qs], in_=pT[:, 0:128], func=AF.Copy)
    for m in range(NCH):
        wo = wos[m % 2]
        S.dma("pool", wo[:], Wo[:, :, m * 128:(m + 1) * 128], writes=[wo])
        for tt in range(4):
            for k in range(NCH):
                S.i("pe", "matmul", [wo, ogT], [pj], out=pj[:], lhsT=wo[:, k, :], rhs=ogT[:, k, tt * 512:(tt + 1) * 512], start=(k == 0), stop=(k == NCH - 1))
            xs = x[:, m, tt * 512:(tt + 1) * 512]
            S.i("dve", "scalar_tensor_tensor", [pj, R(x, (m, tt))], [R(x, (m, tt))], out=xs, in0=xs, scalar=ALPHA, in1=pj[:], op0=ALU.mult, op1=ALU.add)
    S.pop()
    S.push()
    layer_norm(g, ("ln1_g%d" % i, "ln1_b%d" % i), range(4))
    S.pop()

def ret_phase(g, i):
    S = g.S
    x, xb = g.x, g.xb
    S.push()
    Win = g.W["ret_w_in"][0].rearrange("(k p) n -> p k n", p=128)
    Wsw = g.ret_sw_d.rearrange("(k p) n -> p k n", p=128)
    Wo = g.W["ret_w_o"][0].rearrange("(k p) n -> p k n", p=128)
    wq = S.sbuf("rwq", [128, NCH, 256], BF16); wqs = S.sbuf("rwqs", [128, NCH, 256], BF16)
    wk = S.sbuf("rwk", [128, NCH, 256], BF16); wks = S.sbuf("rwks", [128, NCH, 256], BF16)
    wv = S.sbuf("rwv", [128, NCH, 512], BF16); wg = S.sbuf("rwg", [128, NCH, 512], BF16)
    wo = S.sbuf("rwo", [128, 4, 1024], BF16)
    rm = S.sbuf("rrm", [128, 384], F32)
    tabs = [S.sbuf("rtab", [128, 2, 2, 512], F32) for _ in range(2)]
    qrot = S.sbuf("qrot", [128, 2, 512], BF16); krot = S.sbuf("krot", [128, 2, 512], BF16)
    vT = S.sbuf("rvT", [128, 4, 512], BF16); sgT = S.sbuf("rsgT", [128, 4, 512], BF16)
    vtm = S.sbuf("rvtm", [128, 4, 512], BF16); ktm = S.sbuf("rktm", [128, 4, 256], BF16)
    og = S.sbuf("rog", [128, 4, 512], BF16)
    S32 = S.sbuf("rS32", [128, 2, 512], F32); Sb = S.sbuf("rSb", [128, 2, 512], BF16)
    t1 = S.sbuf("rt1", [128, 512], F32); t2 = S.sbuf("rt2", [128, 512], F32)
    ST = S.sbuf("rST", [128, 128], BF16); qcd = S.sbuf("rqcd", [128, 2, 128], BF16)
    osbs = [S.sbuf("rosb", [128, 512], F32) for _ in range(2)]; osqs = [S.sbuf("rosq", [128, 512], F32) for _ in range(2)]
    means = [S.sbuf("rmean", [128, 128], F32) for _ in range(2)]; msqs = [S.sbuf("rmsq", [128, 128], F32) for _ in range(2)]; rstds = [S.sbuf("rrstd", [128, 128], F32) for _ in range(2)]
    nchunk = 0
    pA = S.psum("rpA", [128, 512]); pB = S.psum("rpB", [128, 512])
    opss = [S.psum("rops", [128, 512]) for _ in range(2)]
    ups = [S.psum("rups", [128, 512]) for _ in range(2)]
    pT = S.psum("rpT", [128, 1024], BF16)
    pst = S.psum("rpst", [128, 512])
    sps = pst
    ident = g.identb
    ones = g.ones32
    ntab = 0
    for h in range(4):
        S.dma("pool", wq[:], Win[:, :, h * 256:(h + 1) * 256], writes=[wq])
        S.dma("pool", wqs[:], Wsw[:, :, h * 256:(h + 1) * 256], writes=[wqs])
        S.dma("pool", wk[:], Win[:, :, 1024 + h * 256:1024 + (h + 1) * 256], writes=[wk])
        S.dma("pool", wks[:], Wsw[:, :, 1024 + h * 256:1024 + (h + 1) * 256], writes=[wks])
        S.dma("pool", wv[:], Win[:, :, 2048 + h * 512:2048 + (h + 1) * 512], writes=[wv])
        S.dma("pool", wg[:], Win[:, :, 4096 + h * 512:4096 + (h + 1) * 512], writes=[wg])
        S.dma("pool", wo[:], Wo[:, h * 4:(h + 1) * 4, :], writes=[wo])
        S.dma("sp", rm[:], g.rmask_d[h], writes=[rm])
        S.i("pool", "memset", [], [S32], ap=S32[:], constant=0.0)
        S.i("pool", "memset", [], [Sb], ap=Sb[:], constant=0.0)
        for tt in range(4):
            ts = slice(tt * 512, (tt + 1) * 512)
            tab = tabs[ntab % 2]; ntab += 1
            for cs_ in range(2):
                S.dma("sp", tab[:, cs_, :, :], g.rtab_d[cs_][:, :, ts], writes=[R(tab, None)])
            for (wa, wb, dst) in ((wq, wqs, qrot), (wk, wks, krot)):
                for dc in range(2):
                    for k in range(NCH):
                        S.i("pe", "matmul", [wa, R(xb, (k, tt))], [pA], out=pA[:], lhsT=wa[:, k, dc * 128:(dc + 1) * 128], rhs=xb[:, k, ts],
                            start=(k == 0), stop=(k == NCH - 1))
                    for k in range(NCH):
                        S.i("pe", "matmul", [wb, R(xb, (k, tt))], [pB], out=pB[:], lhsT=wb[:, k, dc * 128:(dc + 1) * 128], rhs=xb[:, k, ts],
                            start=(k == 0), stop=(k == NCH - 1))
                    S.i("dve", "tensor_tensor", [pA, tab], [t1], out=t1[:], in0=pA[:], in1=tab[:, 0, dc, :], op=ALU.mult)
                    S.i("dve", "tensor_tensor", [pB, tab], [t2], out=t2[:], in0=pB[:], in1=tab[:, 1, dc, :], op=ALU.mult)
                    S.i("pool", "tensor_tensor", [t1, t2], [R(dst, dc)], out=dst[:, dc, :], in0=t1[:], in1=t2[:], op=ALU.add)
            for ec in range(4):
                for k in range(NCH):
                    S.i("pe", "matmul", [wv, R(xb, (k, tt))], [pA], out=pA[:], lhsT=wv[:, k, ec * 128:(ec + 1) * 128], rhs=xb[:, k, ts],
                        start=(k == 0), stop=(k == NCH - 1))
                S.i("act", "activation", [pA], [R(vT, ec)], out=vT[:, ec, :], in_=pA[:], func=AF.Copy)
                for k in range(NCH):
                    S.i("pe", "matmul", [wg, R(xb, (k, tt))], [pB], out=pB[:], lhsT=wg[:, k, ec * 128:(ec + 1) * 128], rhs=xb[:, k, ts],
                        start=(k == 0), stop=(k == NCH - 1))
                S.i("act", "activation", [pB], [R(sgT, ec)], out=sgT[:, ec, :], in_=pB[:], func=AF.Silu)
            for n in range(4):
                for ec in range(4):
                    idx = (n % 2) * 4 + ec
                    S.i("pe", "transpose", [vT, g.cb], [pT], out=pT[:, idx * 128:(idx + 1) * 128], in_=vT[:, ec, n * 128:(n + 1) * 128], identity=ident)
                if n % 2 == 1:
                    S.i("dve", "tensor_copy", [pT], [R(vtm, n // 2)], out=vtm[:, n - 1:n + 1, :].rearrange("p a b -> p (a b)"), in_=pT[:])
            for n in range(4):
                for dc in range(2):
                    idx = n * 2 + dc
                    S.i("pe", "transpose", [krot, g.cb], [pT], out=pT[:, idx * 128:(idx + 1) * 128], in_=krot[:, dc, n * 128:(n + 1) * 128], identity=ident)
            S.i("dve", "tensor_scalar", [pT, rm], [ktm], out=ktm[:].rearrange("p a b -> p (a b)"), in0=pT[:], scalar1=rm[:, 256:257], scalar2=None, op0=ALU.mult)
            def partA(n):
                cs = slice(n * 128, (n + 1) * 128)
                opsn = opss[n % 2]
                for dc in range(2):
                    S.i("pe", "matmul", [krot, qrot], [sps], out=sps[:, 256:384], lhsT=krot[:, dc, cs], rhs=qrot[:, dc, cs], start=(dc == 0), stop=(dc == 1))
                S.i("dve", "tensor_tensor", [sps, rm], [ST], out=ST[:], in0=sps[:, 256:384], in1=rm[:, 0:128], op=ALU.mult)
                S.i("pool", "tensor_tensor", [qrot, rm], [qcd], out=qcd[:], in0=qrot[:, :, cs], in1=rm[:, 128:256].unsqueeze(1).to_broadcast([128, 2, 128]), op=ALU.mult)
                for ec in range(4):
                    es = slice(ec * 128, (ec + 1) * 128)
                    S.i("pe", "matmul", [vtm, ST], [opsn], out=opsn[:, es], lhsT=vtm[:, n, es], rhs=ST[:], start=True, stop=False)
                    for dc in range(2):
                        S.i("pe", "matmul", [Sb, qcd], [opsn], out=opsn[:, es], lhsT=Sb[:, dc, es], rhs=qcd[:, dc, :], start=False, stop=(dc == 1))
                for dc in range(2):
                    S.i("pe", "matmul", [ktm, vtm], [ups[dc]], out=ups[dc][:], lhsT=ktm[:, n, dc * 128:(dc + 1) * 128], rhs=vtm[:, n, :], start=True, stop=True)
                    S.i("dve", "scalar_tensor_tensor", [S32, ups[dc], rm], [R(S32, dc)], out=S32[:, dc, :], in0=S32[:, dc, :], scalar=rm[:, 257:258], in1=ups[dc][:],
                        op0=ALU.mult, op1=ALU.add)
                    S.i("act", "activation", [R(S32, dc)], [R(Sb, dc)], out=Sb[:, dc, :], in_=S32[:, dc, :], func=AF.Copy)

            def partB(n):
                cs = slice(n * 128, (n + 1) * 128)
                opsn = opss[n % 2]
                osb, osq, mean, msq, rstd = osbs[n % 2], osqs[n % 2], means[n % 2], msqs[n % 2], rstds[n % 2]
                S.i("act", "activation", [opsn], [osb], out=osb[:], in_=opsn[:], func=AF.Copy)
                S.i("act", "activation", [opsn], [osq], out=osq[:], in_=opsn[:], func=AF.Square)
                for ec in range(4):
                    S.i("pe", "matmul", [osb, g.cf], [R(pst, 0)], out=pst[:, 0:128], lhsT=ones, rhs=osb[:, ec * 128:(ec + 1) * 128], start=(ec == 0), stop=(ec == 3))
                for ec in range(4):
                    S.i("pe", "matmul", [osq, g.cf], [R(pst, 1)], out=pst[:, 128:256], lhsT=ones, rhs=osq[:, ec * 128:(ec + 1) * 128], start=(ec == 0), stop=(ec == 3))
                S.i("dve", "tensor_scalar", [R(pst, 0)], [mean], out=mean[:], in0=pst[:, 0:128], scalar1=1.0 / 512, scalar2=None, op0=ALU.mult)
                S.i("dve", "tensor_tensor", [mean], [msq], out=msq[:], in0=mean[:], in1=mean[:], op=ALU.mult)
                S.i("dve", "scalar_tensor_tensor", [R(pst, 1), msq], [msq], out=msq[:], in0=pst[:, 128:256], scalar=1.0 / 512, in1=msq[:], op0=ALU.mult, op1=ALU.subtract)
                S.i("dve", "tensor_scalar", [msq], [msq], out=msq[:], in0=msq[:], scalar1=1e-5, scalar2=None, op0=ALU.add)
                S.i("act", "activation", [msq], [rstd], out=rstd[:], in_=msq[:], func=AF.Sqrt)
                S.i("dve", "reciprocal", [rstd], [rstd], out=rstd[:], in_=rstd[:])
                mb_ = mean[:].unsqueeze(1).to_broadcast([128, 4, 128])
                rb_ = rstd[:].unsqueeze(1).to_broadcast([128, 4, 128])
                o3 = osb[:].rearrange("p (a b) -> p a b", b=128)
                S.i("dve", "tensor_tensor", [osb, mean], [osb], out=o3, in0=o3, in1=mb_, op=ALU.subtract)
                S.i("dve", "tensor_tensor", [osb, rstd], [osb], out=o3, in0=o3, in1=rb_, op=ALU.mult)
                for ec in range(4):
                    col = h * 4 + ec
                    S.i("act", "activation", [osb, g.vecs], [R(osq, ec)], out=osq[:, ec * 128:(ec + 1) * 128], in_=osb[:, ec * 128:(ec + 1) * 128], func=AF.Identity,
                        bias=vcol(g, "ret_gn_b", col), scale=vcol(g, "ret_gn_g", col))
                S.i("dve", "tensor_tensor", [osq, sgT], [R(og, n)], out=og[:, :, cs], in0=osq[:].rearrange("p (a b) -> p a b", b=128), in1=sgT[:, :, cs], op=ALU.mult)

            partA(0)
            for n in range(4):
                if n + 1 < 4:
                    partA(n + 1)
                partB(n)
            for m in range(NCH):
                pw = pA if m % 2 == 0 else pB
                for k in range(4):
                    S.i("pe", "matmul", [wo, og], [pw], out=pw[:], lhsT=wo[:, k, m * 128:(m + 1) * 128], rhs=og[:, k, :], start=(k == 0), stop=(k == 3))
                xs = x[:, m, ts]
                if h == 0:
                    S.i("dve", "scalar_tensor_tensor", [pw, R(x, (m, tt))], [R(x, (m, tt))], out=xs, in0=xs, scalar=ALPHA, in1=pw[:], op0=ALU.mult, op1=ALU.add)
                else:
                    S.i("dve", "tensor_tensor", [pw, R(x, (m, tt))], [R(x, (m, tt))], out=xs, in0=xs, in1=pw[:], op=ALU.add)
    S.pop()
    S.push()
    layer_norm(g, ("ln1_g%d" % i, "ln1_b%d" % i), range(4))
    S.pop()

def rwkv_phase(g, i, j):
    S = g.S
    x, xb = g.x, g.xb
    S.push()
    TW = 256
    NT_ = T_SEQ // TW
    W = g.W
    Wrkv = [W["rwkv_w_rkv"][j, w].rearrange("(k p) n -> p k n", p=128) for w in range(3)]
    Wo = W["rwkv_w_o"][j].rearrange("(k p) n -> p k n", p=128)
    V = lambda name, c=0: vcol(g, "%s_%d" % (name, j), c)
    w1 = S.sbuf("w1", [128, NCH, 64], BF16); a1 = S.sbuf("a1", [128, NCH, 64], BF16); g1 = S.sbuf("g1", [128, NCH, 160], BF16)
    w2 = S.sbuf("w2", [64, 1024], BF16); a2 = S.sbuf("a2", [64, 1024], BF16); g2 = S.sbuf("g2", [128, 2, 1024], BF16)
    S.dma("pool", w1[:], W["rwkv_w1"][j].rearrange("(k p) n -> p k n", p=128), writes=[w1])
    S.dma("pool", a1[:], W["rwkv_a1"][j].rearrange("(k p) n -> p k n", p=128), writes=[a1])
    S.dma("pool", g1[:], W["rwkv_g1"][j].rearrange("(k p) n -> p k n", p=128), writes=[g1])
    S.dma("pool", w2[:], W["rwkv_w2"][j], writes=[w2])
    S.dma("pool", a2[:], W["rwkv_a2"][j], writes=[a2])
    S.dma("pool", g2[:, 0, :], W["rwkv_g2"][j][0:128, :], writes=[R(g2, None)])
    S.dma("pool", g2[0:32, 1, :], W["rwkv_g2"][j][128:160, :], writes=[R(g2, None)])
    rows = S.sbuf("gnrows", [128, 2, 1024], BF16)
    S.dma("pool", rows[:, 0, :], g.rows_d[2 * j], writes=[R(rows, None)])
    S.dma("pool", rows[:, 1, :], g.rows_d[2 * j + 1], writes=[R(rows, None)])
    om = S.sbuf("om", [128, 56], F32)
    mo = g.voff["mix0_%d" % j]
    S.i("dve", "tensor_scalar", [g.vecs], [om], out=om[:, 0:48], in0=g.vecs[:, mo:mo + 48], scalar1=-1.0, scalar2=1.0, op0=ALU.mult, op1=ALU.add)
    ko = g.voff["k_a_%d" % j]
    S.i("dve", "tensor_scalar", [g.vecs], [om], out=om[:, 48:56], in0=g.vecs[:, ko:ko + 8], scalar1=-1.0, scalar2=1.0, op0=ALU.mult, op1=ALU.add)
    xx = S.sbuf("xx", [128, NCH, TW], F32)
    xlast = S.sbuf("xlast", [128, NCH, 2], F32)
    S.i("pool", "memset", [], [xlast], ap=xlast[:], constant=0.0)
    hw = S.sbuf("hw", [64, TW], BF16); ha = S.sbuf("ha", [64, TW], BF16); sg = S.sbuf("sg", [128, 2, TW], BF16)
    wsl = [S.sbuf("rwsl", [128, NCH, 128], BF16) for _ in range(3)]
    TS = [dict(), dict()]
    for nm in ("tA", "tsw", "tkk", "tcs", "te1", "te2", "te3", "trn", "tt"):
        for q_ in range(2):
            TS[q_][nm] = S.sbuf(nm, [128, TW], F32)
    for q_ in range(2):
        TS[q_]["tkk2"] = S.sbuf("tkk2", [128, TW], BF16)
    PC = S.sbuf("PC", [128, NCH, 2], F32)
    def FM(a, c, sl):
        return xb[:, c, a * TW + sl.start:a * TW + sl.stop]
    FMK = lambda a, c: R(xb, ("fm", a, c))
    Lk = [[S.sbuf("Lk", [128, 4, 128], BF16) for _ in range(2)] for _ in range(2)]
    Mk = [[S.sbuf("Mk", [128, 4, 128], BF16) for _ in range(2)] for _ in range(2)]
    NTt = [[S.sbuf("NT", [128, 4, 128], BF16) for _ in range(2)] for _ in range(2)]
    Mak = [S.sbuf("Mak", [128, 4, 128], BF16) for _ in range(2)]
    Mrb = S.sbuf("Mrb", [128, 16, 128], BF16); Mrk = S.sbuf("Mrk", [128, 16, 128], BF16)
    Atm = S.sbuf("Atm", [128, 1024], BF16); Btm = S.sbuf("Btm", [128, 1024], BF16); Ktm = S.sbuf("Ktm", [128, 1024], BF16); Vtm = S.sbuf("Vtm", [128, 1024], BF16)
    AhT = S.sbuf("AhT", [128, NCH, 128], BF16); Xb = S.sbuf("Xb", [128, 256], BF16); Vhat = S.sbuf("Vhat", [128, 1024], BF16)
    Ub = S.sbuf("Ub", [128, 1024], BF16); ST = S.sbuf("ST", [128, NCH, 64], BF16); STs = S.sbuf("STs", [128, NCH, 64], F32)
    yn = S.sbuf("yn", [128, 1024], F32); st4 = S.sbuf("st4", [128, 4, 16], F32); bsum = S.sbuf("bsum", [128, 16], F32)
    ogtm = S.sbuf("ogtm", [128, 1024], BF16); ogT = S.sbuf("ogT", [128, NCH, TW], BF16)
    PB = [S.psum("rp", [128, 512]) for _ in range(7)]
    pT = S.psum("rpT", [128, 1024], BF16)
    ident = g.identb
    blk1 = g.cb[:, 128:256]
    hind = g.cb[:, 256:258]
    ones = g.ones32
    S.i("pool", "memset", [], [ST], ap=ST[:], constant=0.0)
    nws = 0
    stg = _DBG.get('rw_stage', 99)
    for tix in range(_DBG.get('rw_tiles', NT_)):
        t0 = tix * TW
        tt = t0 // 512
        S.i("dve", "tensor_tensor", [R(x, None)], [R(xx, "m")], out=xx[:, :, 1:TW], in0=x[:, :, t0:t0 + TW - 1], in1=x[:, :, t0 + 1:t0 + TW], op=ALU.subtract)
        S.i("dve", "tensor_tensor", [R(x, None), xlast], [R(xx, "0")], out=xx[:, :, 0:1], in0=xlast[:, :, tix % 2:tix % 2 + 1], in1=x[:, :, t0:t0 + 1], op=ALU.subtract)
        S.i("dve", "tensor_copy", [R(x, None)], [xlast], out=xlast[:, :, (tix + 1) % 2:(tix + 1) % 2 + 1], in_=x[:, :, t0 + TW - 1:t0 + TW])
        def mix(m, dst):
            for c in range(NCH):
                mc = vcol(g, "mix%d_%d" % (m, j), c)
                S.i("dve", "scalar_tensor_tensor", [R(x, (c, tt)), xx, g.vecs], [dst.key(c)], out=dst.ap(c), in0=xx[:, c, :], scalar=mc, in1=x[:, c, t0:t0 + TW],
                    op0=ALU.mult, op1=ALU.add)
        nx = [0]
        class XM:
            def __init__(self, slot):
                self.slot = slot
            def ap(self, c):
                return xb[:, c, 1536 + self.slot * TW:1536 + (self.slot + 1) * TW]
            def key(self, c):
                return R(xb, ("xm", self.slot, c))
        def nxm():
            nx[0] += 1
            return XM(nx[0] % 2)
        d_ = nxm(); mix(1, d_)
        for k in range(NCH):
            S.i("pe", "matmul", [w1, d_.key(k)], [PB[0]], out=PB[0][0:64, 0:TW], lhsT=w1[:, k, :], rhs=d_.ap(k), start=(k == 0), stop=(k == NCH - 1))
        S.i("act", "activation", [PB[0]], [hw], out=hw[:], in_=PB[0][0:64, 0:TW], func=AF.Tanh)
        d_ = nxm(); mix(4, d_)
        for k in range(NCH):
            S.i("pe", "matmul", [a1, d_.key(k)], [PB[1]], out=PB[1][0:64, 0:TW], lhsT=a1[:, k, :], rhs=d_.ap(k), start=(k == 0), stop=(k == NCH - 1))
        S.i("act", "activation", [PB[1]], [ha], out=ha[:], in_=PB[1][0:64, 0:TW], func=AF.Copy)
        d_ = nxm(); mix(5, d_)
        for (lo, hi, kc) in ((0, 128, 0), (128, 160, 1)):
            for k in range(NCH):
                S.i("pe", "matmul", [g1, d_.key(k)], [PB[2]], out=PB[2][0:hi - lo, 0:TW], lhsT=g1[:, k, lo:hi], rhs=d_.ap(k), start=(k == 0), stop=(k == NCH - 1))
            S.i("act", "activation", [PB[2]], [R(sg, kc)], out=sg[0:hi - lo, kc, :], in_=PB[2][0:hi - lo, 0:TW], func=AF.Sigmoid)
        xr = nxm(); mix(0, xr)
        xk = nxm(); mix(2, xk)
        if stg < 3:
            continue
        for c in range(NCH):
            wr = wsl[nws % 3]; nws += 1
            wk_ = wsl[nws % 3]; nws += 1
            S.dma("pool", wr[:], Wrkv[0][:, :, c * 128:(c + 1) * 128], writes=[wr])
            S.dma("pool", wk_[:], Wrkv[1][:, :, c * 128:(c + 1) * 128], writes=[wk_])
            d = TS[c % 2]
            tA, tsw, tcs, te1, te2, te3, tkk, tkk2, trn, tt_ = (d[k_] for k_ in ("tA", "tsw", "tcs", "te1", "te2", "te3", "tkk", "tkk2", "trn", "tt"))
            td3 = tsw; tkkn = tkk; ttb = tkk; tkm = tt_
            if c % 2 == 0:
                rp, kp, zw, za, ssp = PB[0], PB[1], PB[2], PB[3], PB[4]
            else:
                rp, kp, zw, za, ssp = PB[5], PB[6], PB[2], PB[3], PB[4]
            for k in range(NCH):
                S.i("pe", "matmul", [wr, xr.key(k)], [rp], out=rp[:, 0:TW], lhsT=wr[:, k, :], rhs=xr.ap(k), start=(k == 0), stop=(k == NCH - 1))
            for k in range(NCH):
                S.i("pe", "matmul", [wk_, xk.key(k)], [kp], out=kp[:, 0:TW], lhsT=wk_[:, k, :], rhs=xk.ap(k), start=(k == 0), stop=(k == NCH - 1))
            S.i("pe", "matmul", [w2, hw], [zw], out=zw[:, 0:TW], lhsT=w2[:, c * 128:(c + 1) * 128], rhs=hw[:], start=True, stop=True)
            S.i("pe", "matmul", [a2, ha], [za], out=za[:, 0:TW], lhsT=a2[:, c * 128:(c + 1) * 128], rhs=ha[:], start=True, stop=True)
            S.i("act", "activation", [za, g.vecs], [tA], out=tA[:], in_=za[:, 0:TW], func=AF.Sigmoid, bias=V("a0", c), scale=1.0)
            S.i("act", "activation", [zw, g.vecs], [tsw], out=tsw[:], in_=zw[:, 0:TW], func=AF.Sigmoid, bias=V("w0", c), scale=1.0)
            for n in range(2):
                cs = slice(n * 128, (n + 1) * 128)
                S.i("dve", "tensor_tensor_scan", [tsw, g.cf], [R(tcs, n)], out=tcs[:, cs], data0=ones, data1=tsw[:, cs], initial=0.0, op0=ALU.mult, op1=ALU.add)
            S.i("dve", "tensor_tensor", [tcs, tsw], [td3], out=td3[:], in0=tcs[:], in1=tsw[:], op=ALU.subtract)
            LD = 0.6065306597126334
            S.i("act", "activation", [tcs], [te1], out=te1[:], in_=tcs[:], func=AF.Exp, scale=-LD)
            S.i("act", "activation", [tcs], [te2], out=te2[:], in_=tcs[:], func=AF.Exp, scale=LD)
            S.i("act", "activation", [td3], [te3], out=te3[:], in_=td3[:], func=AF.Exp, scale=-LD)
            S.i("act", "activation", [te1], [R(PC, c)], out=PC[:, c, :], in_=te1[:, 127:TW:128], func=AF.Copy)
            S.i("dve", "tensor_scalar", [kp, g.vecs], [tkk], out=tkk[:], in0=kp[:, 0:TW], scalar1=V("k_k", c), scalar2=None, op0=ALU.mult)
            S.i("act", "activation", [tkk], [tkk2], out=tkk2[:], in_=tkk[:], func=AF.Square)
            S.i("pe", "matmul", [tkk2, g.cb], [ssp], out=ssp[:, 0:TW], lhsT=blk1, rhs=tkk2[:], start=True, stop=True)
            S.i("act", "activation", [ssp], [trn], out=trn[:], in_=ssp[:, 0:TW], func=AF.Sqrt)
            S.i("dve", "tensor_scalar", [trn], [trn], out=trn[:], in0=trn[:], scalar1=1e-12, scalar2=None, op0=ALU.max)
            S.i("dve", "reciprocal", [trn], [trn], out=trn[:], in_=trn[:])
            S.i("dve", "tensor_tensor", [tkk, trn], [tkkn], out=tkkn[:], in0=tkk[:], in1=trn[:], op=ALU.mult)
            S.i("dve", "tensor_scalar", [tA, g.vecs, om], [tt_], out=tt_[:], in0=tA[:], scalar1=V("k_a", c), scalar2=om[:, 48 + c:49 + c], op0=ALU.mult, op1=ALU.add)
            S.i("dve", "tensor_tensor", [kp, tt_], [tkm], out=tkm[:], in0=kp[:, 0:TW], in1=tt_[:], op=ALU.mult)
            full = slice(0, TW)
            S.i("dve", "scalar_tensor_tensor", [tkkn, te3], [FMK(0, c)], out=FM(0, c, full), in0=tkkn[:], scalar=-1.0, in1=te3[:], op0=ALU.mult, op1=ALU.mult)
            S.i("dve", "tensor_tensor", [tkkn, tA], [ttb], out=ttb[:], in0=tkkn[:], in1=tA[:], op=ALU.mult)
            S.i("dve", "tensor_tensor", [ttb, te2], [FMK(1, c)], out=FM(1, c, full), in0=ttb[:], in1=te2[:], op=ALU.mult)
            S.i("dve", "tensor_tensor", [tkm, te2], [FMK(2, c)], out=FM(2, c, full), in0=tkm[:], in1=te2[:], op=ALU.mult)
            S.i("dve", "tensor_tensor", [rp, te1], [FMK(3, c)], out=FM(3, c, full), in0=rp[:, 0:TW], in1=te1[:], op=ALU.mult)
            S.i("dve", "scalar_tensor_tensor", [rp, g.vecs, tkm], [FMK(4, c)], out=FM(4, c, full), in0=rp[:, 0:TW], scalar=V("r_k", c), in1=tkm[:], op0=ALU.mult, op1=ALU.mult)
        xv = nxm(); mix(3, xv)
        for c in range(NCH):
            wv_ = wsl[nws % 3]; nws += 1
            S.dma("pool", wv_[:], Wrkv[2][:, :, c * 128:(c + 1) * 128], writes=[wv_])
            vp = PB[5 + (c % 2)]
            for k in range(NCH):
                S.i("pe", "matmul", [wv_, xv.key(k)], [vp], out=vp[:, 0:TW], lhsT=wv_[:, k, :], rhs=xv.ap(k), start=(k == 0), stop=(k == NCH - 1))
            S.i("act", "activation", [vp], [FMK(5, c)], out=FM(5, c, slice(0, TW)), in_=vp[:, 0:TW], func=AF.Copy)
        if stg < 5:
            continue
        for n in range(2):
            cs = slice(n * 128, (n + 1) * 128)
            for a_, dst in ((0, Atm), (1, Btm), (2, Ktm), (5, Vtm)):
                for c in range(NCH):
                    S.i("pe", "transpose", [FMK(a_, c), g.cb], [pT], out=pT[:, c * 128:(c + 1) * 128], in_=FM(a_, c, cs), identity=ident)
                S.i("dve" if a_ in (0, 2) else "act", "tensor_copy" if a_ in (0, 2) else "activation", [pT], [dst],
                    **(dict(out=dst[:], in_=pT[:]) if a_ in (0, 2) else dict(out=dst[:], in_=pT[:], func=AF.Copy)))
            if stg < 6:
                continue
            for hgp in range(2):
                grp = [2 * hgp, 2 * hgp + 1]
                def HP(h):
                    return slice((h % 2) * 64, (h % 2) * 64 + 64), h // 2
                for gi, hg in enumerate(grp):
                    heads = [4 * hg + q for q in range(4)]
                    specs = [(1, 0, 0, Mk[gi][0], None), (0, 1, 2, Lk[gi][0], None), (2, 0, 0, Mak[gi], None), (1, 3, 1, Mrb, hg), (2, 3, 1, Mrk, hg)]
                    for si, (la, ra, mki, dst, full16) in enumerate(specs):
                        dview = dst[:] if full16 is None else dst[:, hg * 4:(hg + 1) * 4, :]
                        wkey = dst if full16 is None else R(dst, hg)
                        for par in range(2):
                            bk = PB[(2 * si + par) % 6]
                            for qq in range(2):
                                h = heads[2 * qq + par]
                                ps_, c = HP(h)
                                S.i("pe", "matmul", [FMK(la, c), FMK(ra, c)], [bk], out=bk[:, qq * 128:(qq + 1) * 128], lhsT=FM(la, c, cs)[ps_, :], rhs=FM(ra, c, cs)[ps_, :],
                                    start=True, stop=True)
                            S.i("dve", "tensor_tensor", [bk, g.mk], [wkey], out=dview[:, par::2, :],
                                in0=bk[:, 0:256].rearrange("p (a b) -> p a b", b=128), in1=g.mk[:, mki, 0:256].rearrange("p (a b) -> p a b", b=128), op=ALU.mult)
                if stg < 7:
                    continue
                cur = [0, 0]
                NTin = [Mk[0][0], Mk[1][0]]
                for r in range(7):
                    for gi in range(2):
                        Lc, Mc = Lk[gi][cur[gi]], Mk[gi][cur[gi]]
                        Ln, Mn = Lk[gi][1 - cur[gi]], Mk[gi][1 - cur[gi]]
                        bN, bL, bM = PB[3 * gi], PB[3 * gi + 1], PB[3 * gi + 2]
                        if r >= 1:
                            NTo = NTt[gi][r % 2]
                            for q in range(4):
                                S.i("pe", "matmul", [NTin[gi], g.cb], [bN], out=bN[:, q * 128:(q + 1) * 128], lhsT=ident, rhs=NTin[gi][:, q, :], start=True, stop=False)
                                S.i("pe", "matmul", [Mc, g.cb], [bN], out=bN[:, q * 128:(q + 1) * 128], lhsT=ident, rhs=Mc[:, q, :], start=False, stop=False)
                                S.i("pe", "matmul", [Lc, NTin[gi]], [bN], out=bN[:, q * 128:(q + 1) * 128], lhsT=Lc[:, q, :], rhs=NTin[gi][:, q, :], start=False, stop=True)
                        if r < 6:
                            for q in range(4):
                                S.i("pe", "matmul", [Mc, Lc], [bL], out=bL[:, q * 128:(q + 1) * 128], lhsT=Mc[:, q, :], rhs=Lc[:, q, :], start=True, stop=True)
                            for q in range(4):
                                S.i("pe", "matmul", [Lc, Mc], [bM], out=bM[:, q * 128:(q + 1) * 128], lhsT=Lc[:, q, :], rhs=Mc[:, q, :], start=True, stop=True)
                        if r >= 1:
                            if gi == 0:
                                S.i("act", "activation", [bN], [NTo], out=NTo[:].rearrange("p a b -> p (a b)"), in_=bN[:], func=AF.Copy)
                            else:
                                S.i("dve", "tensor_copy", [bN], [NTo], out=NTo[:].rearrange("p a b -> p (a b)"), in_=bN[:])
                            NTin[gi] = NTo
                        if r < 6:
                            S.i("act", "activation", [bL], [Ln], out=Ln[:].rearrange("p a b -> p (a b)"), in_=bL[:], func=AF.Copy)
                            S.i("dve", "tensor_copy", [bM], [Mn], out=Mn[:].rearrange("p a b -> p (a b)"), in_=bM[:])
                            cur[gi] = 1 - cur[gi]
                if stg < 8:
                    continue
                for gi, hg in enumerate(grp):
                    heads = [4 * hg + q for q in range(4)]
                    NTf = NTin[gi]
                    for q, h in enumerate(heads):
                        S.i("pe", "matmul", [Mak[gi], Vtm], [PB[6]], out=PB[6][:, q * 64:(q + 1) * 64], lhsT=Mak[gi][:, q, :], rhs=Vtm[:, h * 64:(h + 1) * 64], start=True, stop=True)
                    S.i("act", "activation", [PB[6]], [Xb], out=Xb[:], in_=PB[6][:, 0:256], func=AF.Copy)
                    for q, h in enumerate(heads):
                        S.i("pe", "matmul", [Xb, g.cb], [PB[6]], out=PB[6][:, 256 + q * 64:256 + (q + 1) * 64], lhsT=ident, rhs=Xb[:, q * 64:(q + 1) * 64], start=True, stop=False)
                        S.i("pe", "matmul", [NTf, Xb], [PB[6]], out=PB[6][:, 256 + q * 64:256 + (q + 1) * 64], lhsT=NTf[:, q, :], rhs=Xb[:, q * 64:(q + 1) * 64], start=False, stop=True)
                    S.i("dve", "tensor_copy", [PB[6]], [R(Vhat, hg)], out=Vhat[:, hg * 256:(hg + 1) * 256], in_=PB[6][:, 256:512])
                    bA = PB[gi]
                    for q, h in enumerate(heads):
                        ps_, c = HP(h)
                        cc = (c % 2) * 128
                        S.i("pe", "matmul", [Atm, NTf], [bA], out=bA[ps_, cc:cc + 128], lhsT=Atm[:, h * 64:(h + 1) * 64], rhs=NTf[:, q, :], start=True, stop=True)
                    S.i("dve", "tensor_tensor", [bA, FMK(0, 2 * hg), FMK(0, 2 * hg + 1)], [R(AhT, hg)], out=AhT[:, 2 * hg:2 * hg + 2, :],
                        in0=bA[:, 0:256].rearrange("p (a b) -> p a b", b=128), in1=xb[:, 2 * hg:2 * hg + 2, cs.start:cs.stop], op=ALU.add)
            if stg < 9:
                continue
            Ubv = Ub[:].rearrange("p (h n) -> p h n", n=64)
            Vhv = Vhat[:].rearrange("p (h n) -> p h n", n=64)
            for par in range(2):
                bk = PB[par]
                for hh in range(8):
                    h = 2 * hh + par
                    ps_, c = slice(par * 64, par * 64 + 64), h // 2
                    S.i("pe", "matmul", [AhT, ST], [bk], out=bk[:, hh * 64:(hh + 1) * 64], lhsT=AhT[ps_, c, :], rhs=ST[ps_, c, :], start=True, stop=True)
            for par in range(2):
                S.i("dve", "tensor_tensor", [PB[par], Vhat], [R(Ub, par)], out=Ubv[:, par::2, :], in0=PB[par][:].rearrange("p (h n) -> p h n", n=64),
                    in1=Vhv[:, par::2, :], op=ALU.add)
            for c in range(NCH):
                S.i("act", "activation", [R(ST, c), PC], [R(STs, c)], out=STs[:, c, :], in_=ST[:, c, :], func=AF.Identity, scale=PC[:, c, n:n + 1])
            for par in range(2):
                bk = PB[2 + par]
                for hh in range(8):
                    h = 2 * hh + par
                    ps_, c = slice(par * 64, par * 64 + 64), h // 2
                    o_ = bk[:, hh * 64:(hh + 1) * 64]
                    S.i("pe", "matmul", [FMK(3, c), ST], [bk], out=o_, lhsT=FM(3, c, cs)[ps_, :], rhs=ST[ps_, c, :], start=True, stop=False)
                    S.i("pe", "matmul", [Mrk, Vtm], [bk], out=o_, lhsT=Mrk[:, h, :], rhs=Vtm[:, h * 64:(h + 1) * 64], start=False, stop=False)
                    S.i("pe", "matmul", [Mrb, Ub], [bk], out=o_, lhsT=Mrb[:, h, :], rhs=Ub[:, h * 64:(h + 1) * 64], start=False, stop=True)
            for h in range(16):
                ps_, c = slice((h % 2) * 64, (h % 2) * 64 + 64), h // 2
                o_ = PB[4][ps_, c * 64:(c + 1) * 64]
                S.i("pe", "matmul", [Ktm, Vtm], [PB[4]], out=o_, lhsT=Ktm[:, h * 64:(h + 1) * 64], rhs=Vtm[:, h * 64:(h + 1) * 64], start=True, stop=False)
                S.i("pe", "matmul", [Btm, Ub], [PB[4]], out=o_, lhsT=Btm[:, h * 64:(h + 1) * 64], rhs=Ub[:, h * 64:(h + 1) * 64], start=False, stop=True)
            for c in range(NCH):
                S.i("dve", "scalar_tensor_tensor", [PB[4], PC, R(STs, c)], [R(ST, c)], out=ST[:, c, :], in0=PB[4][:, c * 64:(c + 1) * 64], scalar=PC[:, c, n:n + 1],
                    in1=STs[:, c, :], op0=ALU.mult, op1=ALU.add)
            for b in range(2):
                S.i("dve", "tensor_reduce", [PB[2 + b]], [R(st4, ("s", b))], out=st4[:, 0, b::2], in_=PB[2 + b][:].rearrange("p (h n) -> p h n", n=64), axis=AX.X, op=ALU.add)
                S.i("act", "activation", [PB[2 + b]], [R(yn, None)], out=yn[:, b * 512:(b + 1) * 512], in_=PB[2 + b][:], func=AF.Square)
                S.i("dve", "tensor_reduce", [R(yn, None)], [R(st4, ("q", b))], out=st4[:, 1, b::2], in_=yn[:, b * 512:(b + 1) * 512].rearrange("p (h n) -> p h n", n=64), axis=AX.X, op=ALU.add)
            S.i("dve", "tensor_scalar", [st4], [R(st4, "m")], out=st4[:, 2, :], in0=st4[:, 0, :], scalar1=1.0 / 64, scalar2=None, op0=ALU.mult)
            S.i("dve", "tensor_tensor", [st4], [R(st4, "v")], out=st4[:, 3, :], in0=st4[:, 2, :], in1=st4[:, 2, :], op=ALU.mult)
            S.i("dve", "scalar_tensor_tensor", [st4], [R(st4, "v")], out=st4[:, 3, :], in0=st4[:, 1, :], scalar=1.0 / 64, in1=st4[:, 3, :], op0=ALU.mult, op1=ALU.subtract)
            S.i("dve", "tensor_scalar", [st4], [R(st4, "v")], out=st4[:, 3, :], in0=st4[:, 3, :], scalar1=64e-5, scalar2=None, op0=ALU.add)
            S.i("act", "activation", [st4], [R(st4, "v")], out=st4[:, 3, :], in_=st4[:, 3, :], func=AF.Sqrt)
            S.i("dve", "reciprocal", [st4], [R(st4, "v")], out=st4[:, 3, :], in_=st4[:, 3, :])
            for c in range(NCH):
                S.i("pe", "matmul", [FMK(4, c), g.cb], [PB[5]], out=PB[5][:, 2 * c:2 * c + 2], lhsT=FM(4, c, cs), rhs=hind, start=True, stop=True)
            S.i("dve", "tensor_copy", [PB[5]], [bsum], out=bsum[:], in_=PB[5][:, 0:16])
            for h in range(16):
                b = h % 2
                hs = slice(h * 64, (h + 1) * 64)
                S.i("dve", "tensor_scalar", [PB[2 + b], st4], [R(yn, h)], out=yn[:, hs], in0=PB[2 + b][:, (h // 2) * 64:(h // 2 + 1) * 64],
                    scalar1=st4[:, 2, h:h + 1], scalar2=st4[:, 3, h:h + 1], op0=ALU.subtract, op1=ALU.mult)
            S.i("dve", "tensor_tensor", [yn, rows], [yn], out=yn[:], in0=yn[:], in1=rows[:, 0, :], op=ALU.mult)
            S.i("dve", "tensor_tensor", [yn, rows], [yn], out=yn[:], in0=yn[:], in1=rows[:, 1, :], op=ALU.add)
            for h in range(16):
                hs = slice(h * 64, (h + 1) * 64)
                S.i("dve", "scalar_tensor_tensor", [Vtm, bsum, yn], [R(yn, h)], out=yn[:, hs], in0=Vtm[:, hs], scalar=bsum[:, h:h + 1], in1=yn[:, hs], op0=ALU.mult, op1=ALU.add)
            for b in range(2):
                S.i("pe", "matmul", [sg, g2], [PB[b]], out=PB[b][:], lhsT=sg[:, 0, cs], rhs=g2[:, 0, b * 512:(b + 1) * 512], start=True, stop=False)
                S.i("pe", "matmul", [sg, g2], [PB[b]], out=PB[b][:], lhsT=sg[0:32, 1, cs], rhs=g2[0:32, 1, b * 512:(b + 1) * 512], start=False, stop=True)
                S.i("dve", "tensor_tensor", [yn, PB[b]], [R(ogtm, b)], out=ogtm[:, b * 512:(b + 1) * 512], in0=yn[:, b * 512:(b + 1) * 512], in1=PB[b][:], op=ALU.mult)
            for c in range(NCH):
                S.i("pe", "transpose", [ogtm, g.cb], [pT], out=pT[:, c * 128:(c + 1) * 128], in_=ogtm[:, c * 128:(c + 1) * 128], identity=ident)
            for c in range(NCH):
                S.i("act", "activation", [pT], [R(ogT, (c, n))], out=ogT[:, c, cs], in_=pT[:, c * 128:(c + 1) * 128], func=AF.Copy)
        if stg < 11:
            continue
        for m in range(NCH):
            wo = wsl[nws % 3]; nws += 1
            S.dma("pool", wo[:], Wo[:, :, m * 128:(m + 1) * 128], writes=[wo])
            bk = PB[5 + (m % 2)]
            for k in range(NCH):
                S.i("pe", "matmul", [wo, ogT], [bk], out=bk[:, 0:TW], lhsT=wo[:, k, :], rhs=ogT[:, k, :], start=(k == 0), stop=(k == NCH - 1))
            xs = x[:, m, t0:t0 + TW]
            S.i("dve", "scalar_tensor_tensor", [bk, R(x, (m, tt))], [R(x, (m, tt))], out=xs, in0=xs, scalar=ALPHA, in1=bk[:, 0:TW], op0=ALU.mult, op1=ALU.add)
    S.pop()
    S.push()
    layer_norm(g, ("ln1_g%d" % i, "ln1_b%d" % i), range(4))
    S.pop()

def make_consts():
    c = np.zeros((128, 1024), np.float32)
    c[:, 0:128] = 1.0
    c[:, 128:256] = np.eye(128, dtype=np.float32)
    bo = np.zeros((128, 128), np.float32); bo[:64, :64] = 1.0; bo[64:, 64:] = 1.0
    c[:, 256:384] = bo
    t = np.arange(128)
    c[:, 384:512] = np.where(t[None, :] <= t[:, None], 0.0, -30000.0)
    c[:, 512:640] = (t[:, None] < t[None, :]).astype(np.float32)
    c[:, 640:768] = (t[:, None] <= t[None, :]).astype(np.float32)
    c[:64, 768] = 1.0; c[64:, 769] = 1.0
    return c


def make_ret_tables():
    dk = 256
    inv = (1.0 / (np.float32(10000.0) ** np.linspace(0.0, 1.0, dk // 2, dtype=np.float32))).astype(np.float32)
    pos = np.arange(T_SEQ, dtype=np.float32)
    ang = (pos[:, None] * inv[None, :]).astype(np.float32)
    cos = np.cos(ang).astype(np.float32); sin = np.sin(ang).astype(np.float32)
    f = np.arange(dk)
    cosf = cos[:, f // 2].T
    sgn = np.where(f % 2 == 0, -1.0, 1.0).astype(np.float32)
    sinf = (sin[:, f // 2].T * sgn[:, None]).astype(np.float32)
    tab = np.stack([cosf.reshape(2, 128, T_SEQ).transpose(1, 0, 2), sinf.reshape(2, 128, T_SEQ).transpose(1, 0, 2)])
    m = np.zeros((4, 128, 384), np.float32)
    idx = np.arange(128, dtype=np.float64)
    for h in range(4):
        lg = np.log(1.0 - 2.0 ** (-5.0 - h))
        rel = idx[None, :] - idx[:, None]
        m[h, :, 0:128] = np.where(rel >= 0, np.exp(lg * np.maximum(rel, 0)), 0.0) / 16.0
        m[h, :, 128:256] = np.exp(lg * (idx + 1.0))[None, :]
        m[h, :, 256] = np.exp(lg * (127.0 - idx)) / 16.0
        m[h, :, 257] = np.exp(lg * 128.0)
    return np.ascontiguousarray(tab.astype(np.float32)), m


FULL_PLAN = [("rwkv", 0, 0), ("ffn", 0), ("ret", 1), ("ffn", 1), ("moba", 2), ("ffn", 2), ("rwkv", 3, 1), ("ffn", 3)]
_PLAN = FULL_PLAN
_NC_CACHE = {}


def host_inputs(inputs):
    shared = {k: np.ascontiguousarray(np.asarray(inputs[k], np.float32)) for k in WEIGHT_SHAPES}
    shared["vecs"] = pack_vecs(inputs)
    shared["consts"] = make_consts()
    rows = np.zeros((4, 128, 1024), np.float32)
    for j in range(2):
        rows[2 * j] = np.broadcast_to(np.asarray(inputs["rwkv_gn_g"][j], np.float32)[None, :], (128, 1024))
        rows[2 * j + 1] = np.broadcast_to(np.asarray(inputs["rwkv_gn_b"][j], np.float32)[None, :], (128, 1024))
    shared["rows"] = rows
    ti = np.arange(128)
    mk = np.zeros((128, 3, 512), np.float32)
    mk[:, 0, :] = np.tile((ti[:, None] < ti[None, :]).astype(np.float32), (1, 4))
    mk[:, 1, :] = np.tile((ti[:, None] <= ti[None, :]).astype(np.float32), (1, 4))
    mk[:, 2, :] = np.tile((ti[:, None] > ti[None, :]).astype(np.float32), (1, 4))
    shared["mk"] = mk
    tab, msk = make_ret_tables()
    shared["rtab"] = tab
    shared["rmask"] = msk
    wqk = np.asarray(inputs["ret_w_in"][0][:, :2048], np.float32)
    shared["ret_sw"] = np.ascontiguousarray(wqk.reshape(1024, 1024, 2)[:, :, ::-1].reshape(1024, 2048))
    return shared


def run_plan(plan, inputs, x_full, n_cores=8):
    key = repr(plan)
    if key not in _NC_CACHE:
        _NC_CACHE[key] = build_program(plan)
    nc = _NC_CACHE[key]
    shared = host_inputs(inputs)
    in_maps = []
    for b in range(n_cores):
        m = dict(shared)
        m["xT"] = np.ascontiguousarray(np.asarray(x_full[b], np.float32).T)
        in_maps.append(m)
    res = run_bass_kernel_spmd(nc, in_maps, core_ids=list(range(n_cores)))
    return np.stack([np.ascontiguousarray(r["outT"].T) for r in res.results]).astype(np.float32)


def kernel(**inputs):
    return run_plan(_PLAN, inputs, inputs["x"], 8)
```

```python
import numpy as np
import concourse.bass as bass
import concourse.mybir as mybir
from concourse.bass_utils import run_bass_kernel_spmd
from contextlib import ExitStack

_DBG = {}
F32 = mybir.dt.float32
BF16 = mybir.dt.bfloat16
AF = mybir.ActivationFunctionType
ALU = mybir.AluOpType
AX = mybir.AxisListType

ENGS = ("pe", "act", "dve", "pool", "sp")


class T:
    def __init__(self, S, t, name):
        self.S = S
        self.t = t
        self.name = name
        self.st = {}
        self.is_psum = False
        self.dsem = None
        self.dcnt = 0

    def __getitem__(self, idx):
        return self.t[idx]


class R:
    __slots__ = ("tile", "key")
    def __init__(self, tile, key=None):
        self.tile = tile
        self.key = key


class Sched:
    def __init__(self, nc):
        self.nc = nc
        self.es = ExitStack()
        self.ins = {e: [] for e in ENGS}
        self.cnt = {e: 0 for e in ENGS}
        self.seen = {e: {} for e in ENGS}
        self.sems = {}
        self.needed = {e: set() for e in ENGS}
        self.nsem = 0
        for e in ENGS:
            self.sems[("e", e)] = self.es.enter_context(nc.semaphore("sem_" + e))
        self.dma_pool = []
        self.out_dma = []
        self.pend = {}
        self.scopes = []

    def _es(self):
        return self.scopes[-1] if self.scopes else self.es

    def push(self):
        self.scopes.append(ExitStack())

    def pop(self):
        self.barrier()
        self.scopes.pop().close()

    def sbuf(self, name, shape, dt):
        self.uid = getattr(self, "uid", 0) + 1
        name = "%s_%d" % (name, self.uid)
        t = self._es().enter_context(self.nc.sbuf_tensor(name, list(shape), dt))
        return T(self, t, name)

    def psum(self, name, shape, dt=F32):
        self.uid = getattr(self, "uid", 0) + 1
        name = "%s_%d" % (name, self.uid)
        t = self._es().enter_context(self.nc.psum_tensor(name, list(shape), dt))
        tt = T(self, t, name)
        tt.is_psum = True
        return tt

    def new_dsem(self, name):
        self.nsem += 1
        key = ("d", self.nsem)
        self.sems[key] = self.es.enter_context(self.nc.semaphore("dsem%d_%s" % (self.nsem, name)))
        return key

    def _states(self, ref, create=True):
        tile, key = ref.tile, ref.key
        if key is None:
            if None not in tile.st:
                tile.st[None] = [None, {}]
            return list(tile.st.values())
        out = []
        if None in tile.st:
            out.append(tile.st[None])
        if key not in tile.st:
            tile.st[key] = [None, {}]
        out.append(tile.st[key])
        return out

    def _collect(self, eng, reads, writes):
        waits = {}
        def need(w, same_ok=False):
            if w is None:
                return
            k, v = w
            if same_ok and k == ("e", eng) and eng == "pe":
                return
            if waits.get(k, 0) < v:
                waits[k] = v
        for r in reads:
            for st in self._states(r):
                need(st[0])
                if r.tile.is_psum:
                    for rk, rv in st[1].items():
                        if rk != ("e", eng):
                            need((rk, rv))
        for w in writes:
            for st in self._states(w):
                need(st[0])
                for rk, rv in st[1].items():
                    if rk == ("e", eng) and eng == "pe":
                        continue
                    need((rk, rv))
        final = []
        for k, v in waits.items():
            if k == ("e", eng) and eng == "pe":
                continue
            if self.seen[eng].get(k, 0) < v:
                self.seen[eng][k] = v
                final.append((k, v))
                if k[0] == "e":
                    self.needed[k[1]].add(v)
        return final

    def _commit(self, tag, reads, writes):
        for r in reads:
            if r.key is None:
                for k, s in r.tile.st.items():
                    if s[1].get(tag[0], 0) < tag[1]:
                        s[1][tag[0]] = tag[1]
            else:
                s = r.tile.st[r.key]
                if s[1].get(tag[0], 0) < tag[1]:
                    s[1][tag[0]] = tag[1]
        for w in writes:
            if w.key is None:
                w.tile.st = {None: [tag, {}]}
            else:
                w.tile.st[w.key] = [tag, {}]

    def i(self, eng, meth, reads=(), writes=(), **kw):
        return self.op(eng, (meth, kw), reads, writes)

    def _norm(self, refs):
        out = []
        for r in refs:
            if not isinstance(r, R):
                r = R(r)
            if r.tile.is_psum and r.key is not None:
                r = R(r.tile)
            out.append(r)
        return out

    def op(self, eng, fn, reads=(), writes=()):
        reads = self._norm(reads)
        writes = self._norm(writes)
        waits = self._collect(eng, reads, writes)
        self.cnt[eng] += 1
        idx = self.cnt[eng]
        self.ins[eng].append([fn, waits, idx, None])
        self._commit((("e", eng), idx), reads, writes)
        return idx

    def dma(self, q, out_ap, in_ap, reads=(), writes=(), out_final=False, owner=None, **kw):
        reads = [r if isinstance(r, R) else R(r) for r in reads]
        writes = [w if isinstance(w, R) else R(w) for w in writes]
        waits = self._collect(q, reads, writes)
        if owner is None:
            owner = (writes[0] if writes else reads[0]).tile
        if owner.dsem is None:
            owner.dsem = self.new_dsem(owner.name)
        owner.dcnt += 16
        tag = (owner.dsem, owner.dcnt)
        self.cnt[q] += 1
        idx = self.cnt[q]
        self.ins[q].append([lambda e: e.dma_start(out=out_ap, in_=in_ap, **kw), waits, idx, tag])
        self._commit(tag, reads, writes)
        self.pend[tag[0]] = tag[1]
        if out_final:
            self.out_dma.append(tag)
        return tag

    def barrier(self):
        for f in ENGS:
            if self.cnt[f] and (self.ins[f][-1][0] is None or self.ins[f][-1][3] is not None):
                self.cnt[f] += 1
                self.ins[f].append([None, [], self.cnt[f], None])
        for e in ENGS:
            waits = []
            for f in ENGS:
                if f == e or self.cnt[f] == 0:
                    continue
                v = self.cnt[f]
                if self.seen[e].get(("e", f), 0) < v:
                    self.seen[e][("e", f)] = v
                    waits.append((("e", f), v))
                    self.needed[f].add(v)
            for k, v in self.pend.items():
                if self.seen[e].get(k, 0) < v:
                    self.seen[e][k] = v
                    waits.append((k, v))
            if waits:
                self.cnt[e] += 1
                self.ins[e].append([None, waits, self.cnt[e], None])

    def emit(self):
        nc = self.nc
        fwd = {}
        for k, v in self.out_dma:
            fwd[k] = max(fwd.get(k, 0), v)
        fw = list(fwd.items())
        self.cnt["sp"] += 1
        self.ins["sp"].append([None, fw, self.cnt["sp"], None])
        rank = {}
        for e in ENGS:
            s = sorted(self.needed[e])
            rank[e] = {v: i + 1 for i, v in enumerate(s)}
        engmap = {"pe": "tensor", "act": "scalar", "dve": "vector", "pool": "gpsimd", "sp": "sync"}
        with nc.Block() as block:
            for e in ENGS:
                lst = self.ins[e]
                if not lst:
                    continue
                def body(eng, e=e, lst=lst):
                    for fn, waits, idx, dtag in lst:
                        for k, v in waits:
                            if k[0] == "e":
                                eng.wait_ge(self.sems[k], rank[k[1]][v])
                            else:
                                eng.wait_ge(self.sems[k], v)
                        if fn is None:
                            if idx in self.needed[e]:
                                eng.nop().then_inc(self.sems[("e", e)], 1)
                            continue
                        ins = fn(eng) if callable(fn) else getattr(eng, fn[0])(**fn[1])
                        if dtag is not None:
                            ins.then_inc(self.sems[dtag[0]], 16)
                            if idx in self.needed[e]:
                                raise RuntimeError("dma instr needed as engine milestone")
                        elif idx in self.needed[e]:
                            ins.then_inc(self.sems[("e", e)], 1)
                getattr(block, engmap[e])(body)
        self.es.close()

D = 1024; T_SEQ = 2048; DEPTH = 4; FF = 2816; NCH = 8; NPAIR = 22
ALPHA = float((2 * DEPTH) ** 0.25)
LN_EPS = 1e-5

WEIGHT_SHAPES = {
    "rwkv_w_rkv": [2, 3, 1024, 1024], "rwkv_w1": [2, 1024, 64], "rwkv_w2": [2, 64, 1024],
    "rwkv_a1": [2, 1024, 64], "rwkv_a2": [2, 64, 1024], "rwkv_g1": [2, 1024, 160], "rwkv_g2": [2, 160, 1024],
    "rwkv_w_o": [2, 1024, 1024], "ret_w_in": [1, 1024, 6144], "ret_w_o": [1, 2048, 1024],
    "moba_w_qkv": [1, 1024, 3072], "moba_w_o": [1, 1024, 1024],
    "ffn_w_in": [4, 1024, 5632], "ffn_w_out": [4, 2816, 1024],
}


def _fm(v):
    v = np.asarray(v, np.float32).reshape(-1, 128)
    return np.ascontiguousarray(v.T)


def vec_layout():
    off = {}
    n = 0
    def add(name, cols):
        nonlocal n
        off[name] = n
        n += cols
    for i in range(4):
        for nm in ("ln1_g", "ln1_b", "ln2_g", "ln2_b"):
            add("%s%d" % (nm, i), 8)
        for k in range(3):
            add("cw%d_%d" % (k, i), 44)
        add("cb_%d" % i, 44)
    for j in range(2):
        for m in range(6):
            add("mix%d_%d" % (m, j), 8)
        for nm in ("w0", "a0", "k_k", "k_a", "r_k"):
            add("%s_%d" % (nm, j), 8)
    add("ret_gn_g", 16)
    add("ret_gn_b", 16)
    return off, n


def pack_vecs(inp):
    off, n = vec_layout()
    out = np.zeros((128, n), np.float32)
    def put(name, v):
        a = _fm(v)
        out[:, off[name]:off[name] + a.shape[1]] = a
    for i in range(4):
        for nm in ("ln1_g", "ln1_b", "ln2_g", "ln2_b"):
            put("%s%d" % (nm, i), inp[nm][i])
        for k in range(3):
            put("cw%d_%d" % (k, i), inp["ffn_conv_w"][i, k])
        put("cb_%d" % i, inp["ffn_conv_b"][i])
    for j in range(2):
        for m in range(6):
            put("mix%d_%d" % (m, j), inp["rwkv_mix"][j, m])
        put("w0_%d" % j, inp["rwkv_w0"][j]); put("a0_%d" % j, inp["rwkv_a0"][j])
        put("k_k_%d" % j, inp["rwkv_k_k"][j]); put("k_a_%d" % j, inp["rwkv_k_a"][j])
        put("r_k_%d" % j, inp["rwkv_r_k"][j].reshape(-1))
    put("ret_gn_g", inp["ret_gn_g"][0]); put("ret_gn_b", inp["ret_gn_b"][0])
    return out


class Ctx:
    pass


def build_program(plan, x_in_bf=True):
    nc = bass.Bass("TRN2", target_bir_lowering=False)
    S = Sched(nc)
    g = Ctx()
    g.nc, g.S = nc, S
    g.W = {k: nc.dram_tensor(k, shp, F32, kind="ExternalInput").ap() for k, shp in WEIGHT_SHAPES.items()}
    voff, nv = vec_layout()
    g.voff = voff
    xT_d = nc.dram_tensor("xT", [D, T_SEQ], F32, kind="ExternalInput").ap()
    vecs_d = nc.dram_tensor("vecs", [128, nv], F32, kind="ExternalInput").ap()
    consts_d = nc.dram_tensor("consts", [128, 1024], F32, kind="ExternalInput").ap()
    g.rows_d = nc.dram_tensor("rows", [4, 128, 1024], F32, kind="ExternalInput").ap()
    g.rtab_d = nc.dram_tensor("rtab", [2, 128, 2, T_SEQ], F32, kind="ExternalInput").ap()
    g.rmask_d = nc.dram_tensor("rmask", [4, 128, 384], F32, kind="ExternalInput").ap()
    g.ret_sw_d = nc.dram_tensor("ret_sw", [1024, 2048], F32, kind="ExternalInput").ap()
    g.mk_d = nc.dram_tensor("mk", [128, 3, 512], F32, kind="ExternalInput").ap()
    out_d = nc.dram_tensor("outT", [D, T_SEQ], F32, kind="ExternalOutput").ap()

    g.x = S.sbuf("x", [128, NCH, T_SEQ], F32)
    g.xb = S.sbuf("xb", [128, NCH, T_SEQ], BF16)
    g.vecs = S.sbuf("vecs", [128, nv], F32)
    g.cf = S.sbuf("cf", [128, 256], F32)
    g.cb = S.sbuf("cb", [128, 512], BF16)
    xv = xT_d.rearrange("(c p) t -> p c t", p=128)
    for c in range(NCH):
        S.dma("sp", g.x[:, c, :], xv[:, c, :], writes=[R(g.x, None)])
    if plan[0][0] != "rwkv":
        for c in range(NCH):
            S.dma("pool", g.xb[:, c, :], xv[:, c, :], writes=[R(g.xb, None)])
    S.dma("sp", g.vecs[:], vecs_d, writes=[g.vecs])
    S.dma("sp", g.cf[:, 0:128], consts_d[:, 0:128], writes=[R(g.cf, None)])
    S.dma("sp", g.cf[:, 128:256], consts_d[:, 384:512], writes=[R(g.cf, None)])
    S.i("pool", "memset", [], [g.cb], ap=g.cb[:], constant=0.0)
    S.dma("pool", g.cb[:, 0:256], consts_d[:, 128:384], writes=[R(g.cb, None)])
    S.dma("pool", g.cb[:, 256:258], consts_d[:, 768:770], writes=[R(g.cb, None)])
    g.mk = S.sbuf("mk", [128, 3, 512], BF16)
    S.dma("pool", g.mk[:], g.mk_d, writes=[g.mk])
    g.ones32 = g.cf[:, 0:128]
    g.identb = g.cb[:, 0:128]

    for step in plan:
        kind = step[0]
        if kind == "ffn":
            ffn_phase(g, step[1])
        elif kind == "ln":
            S.push()
            layer_norm(g, step[1], range(4))
            S.pop()
        elif kind == "moba":
            moba_phase(g, step[1])
        elif kind == "ret":
            ret_phase(g, step[1])
        elif kind == "rwkv":
            rwkv_phase(g, step[1], step[2])
        else:
            raise ValueError(kind)

    S.barrier()
    ov = out_d.rearrange("(c p) t -> p c t", p=128)
    outsem = T(S, None, "outsem")
    for c in range(NCH):
        S.dma("sp", ov[:, c, :], g.x[:, c, :], reads=[R(g.x, None)], out_final=True)
    S.emit()
    return nc


def vcol(g, name, c=0):
    o = g.voff[name] + c
    return g.vecs[:, o:o + 1]


def layer_norm(g, pref, tts, width=512, write_xb=True):
    S = g.S
    gname, bname = pref
    x, xb = g.x, g.xb
    sq = [S.sbuf("lnsq", [128, 512], F32) for _ in range(2)]
    means = [S.sbuf("lnmean", [128, 512], F32) for _ in range(2)]
    msqs = [S.sbuf("lnmsq", [128, 512], F32) for _ in range(2)]
    rstds = [S.sbuf("lnrstd", [128, 512], F32) for _ in range(2)]
    tmp = [S.sbuf("lntmp", [128, 512], F32) for _ in range(3)]
    ps_ss = [S.psum("lnps", [128, 512]) for _ in range(2)]
    ps_qs = [S.psum("lnpq", [128, 512]) for _ in range(2)]
    ones = g.ones32
    W_ = width
    for tix in tts:
        ts = slice(tix * W_, (tix + 1) * W_)
        tt = (tix * W_) // 512
        mean, msq, rstd, ps_s, ps_q = means[tix % 2], msqs[tix % 2], rstds[tix % 2], ps_ss[tix % 2], ps_qs[tix % 2]
        for c in range(NCH):
            q = sq[c % 2]
            S.i("act", "activation", [R(x, (c, tt))], [q], out=q[:, 0:W_], in_=x[:, c, ts], func=AF.Square)
            S.i("pe", "matmul", [R(x, (c, tt)), g.cf], [ps_s], out=ps_s[:, 0:W_], lhsT=ones, rhs=x[:, c, ts], start=(c == 0), stop=(c == NCH - 1))
            S.i("pe", "matmul", [q, g.cf], [ps_q], out=ps_q[:, 0:W_], lhsT=ones, rhs=q[:, 0:W_], start=(c == 0), stop=(c == NCH - 1))
        S.i("dve", "tensor_scalar", [ps_s], [mean], out=mean[:, 0:W_], in0=ps_s[:, 0:W_], scalar1=1.0 / D, scalar2=None, op0=ALU.mult)
        S.i("dve", "tensor_tensor", [mean], [msq], out=msq[:, 0:W_], in0=mean[:, 0:W_], in1=mean[:, 0:W_], op=ALU.mult)
        S.i("dve", "scalar_tensor_tensor", [ps_q, msq], [msq], out=msq[:, 0:W_], in0=ps_q[:, 0:W_], scalar=1.0 / D, in1=msq[:, 0:W_], op0=ALU.mult, op1=ALU.subtract)
        S.i("dve", "tensor_scalar", [msq], [msq], out=msq[:, 0:W_], in0=msq[:, 0:W_], scalar1=LN_EPS, scalar2=None, op0=ALU.add)
        S.i("act", "activation", [msq], [rstd], out=rstd[:, 0:W_], in_=msq[:, 0:W_], func=AF.Sqrt)
        S.i("dve", "reciprocal", [rstd], [rstd], out=rstd[:, 0:W_], in_=rstd[:, 0:W_])
        for c in range(NCH):
            tm = tmp[c % 3]
            S.i("dve", "tensor_tensor", [R(x, (c, tt)), mean], [tm], out=tm[:, 0:W_], in0=x[:, c, ts], in1=mean[:, 0:W_], op=ALU.subtract)
            S.i("dve", "tensor_tensor", [tm, rstd], [tm], out=tm[:, 0:W_], in0=tm[:, 0:W_], in1=rstd[:, 0:W_], op=ALU.mult)
            S.i("act", "activation", [tm, g.vecs], [R(x, (c, tt))], out=x[:, c, ts], in_=tm[:, 0:W_], func=AF.Identity,
                bias=vcol(g, bname, c), scale=vcol(g, gname, c))
            if write_xb:
                S.i("dve", "tensor_scalar", [tm, g.vecs], [R(xb, (c, tt))], out=xb[:, c, ts], in0=tm[:, 0:W_], scalar1=vcol(g, gname, c),
                    scalar2=vcol(g, bname, c), op0=ALU.mult, op1=ALU.add)


def ffn_phase(g, i):
    S = g.S
    x, xb = g.x, g.xb
    S.push()
    W_in = g.W["ffn_w_in"][i].rearrange("(k p) (two f) -> p k two f", p=128, two=2)
    W_out = g.W["ffn_w_out"][i].rearrange("(k p) n -> p k n", p=128)
    TB = 1024
    a = S.sbuf("ffa", [128, NPAIR, TB], BF16)
    hus = [S.sbuf("hu", [128, TB + 2], F32) for _ in range(2)]
    hgs = [S.sbuf("hg", [128, TB + 2], F32) for _ in range(2)]
    aus = [S.sbuf("au", [128, TB], F32) for _ in range(2)]
    ags = [S.sbuf("ag", [128, TB], F32) for _ in range(2)]
    halo = S.sbuf("halo", [128, 2 * NPAIR, 2], F32)
    wps = [[S.sbuf("wp", [128, NCH, 128], BF16) for _ in range(2)] for _ in range(2)]
    wos = [S.sbuf("wo", [128, NPAIR, 128], BF16) for _ in range(2)]
    pb = [S.psum("ffps", [128, 512]) for _ in range(8)]
    cw = lambda k, j: vcol(g, "cw%d_%d" % (k, i), j)
    cbv = lambda j: vcol(g, "cb_%d" % i, j)
    nw = 0
    for blk in range(2):
        for j in range(NPAIR):
            if j == 0:
                for ug in range(2):
                    S.dma("pool", wps[nw % 2][ug][:], W_in[:, :, ug, 0:128], writes=[wps[nw % 2][ug]])
            wp = wps[nw % 2]; nw += 1
            hu, hg, au, ag = hus[j % 2], hgs[j % 2], aus[j % 2], ags[j % 2]
            if j + 1 < NPAIR:
                for ug in range(2):
                    S.dma("pool", wps[nw % 2][ug][:], W_in[:, :, ug, (j + 1) * 128:(j + 2) * 128], writes=[wps[nw % 2][ug]])
            elif True:
                S.dma("pool", wos[0][:], W_out[:, :, 0:128], writes=[wos[0]])
            banks = pb[(j % 2) * 4:(j % 2) * 4 + 4]
            for ug in range(2):
                for h in range(2):
                    bk = banks[ug * 2 + h]
                    tt = blk * 2 + h
                    for k in range(NCH):
                        S.i("pe", "matmul", [wp[ug], R(xb, (k, tt))], [bk], out=bk[:], lhsT=wp[ug][:, k, :],
                            rhs=xb[:, k, tt * 512:(tt + 1) * 512], start=(k == 0), stop=(k == NCH - 1))
            for ug, hb in ((0, hu), (1, hg)):
                for h in range(2):
                    bk = banks[ug * 2 + h]
                    S.i("act", "activation", [bk], [R(hb, h)], out=hb[:, 2 + h * 512:2 + (h + 1) * 512], in_=bk[:], func=AF.Copy)
                if blk == 0:
                    S.i("dve", "memset", [], [R(hb, "halo")], ap=hb[:, 0:2], constant=0.0)
                else:
                    S.i("dve", "tensor_copy", [R(halo, (ug, j))], [R(hb, "halo")], out=hb[:, 0:2], in_=halo[:, ug * NPAIR + j, :])
            for (hb, ab_, jj) in ((hu, au, j), (hg, ag, NPAIR + j)):
                S.i("act", "activation", [hb, g.vecs], [ab_], out=ab_[:], in_=hb[:, 2:TB + 2], func=AF.Identity, bias=cbv(jj), scale=cw(2, jj))
                S.i("dve", "scalar_tensor_tensor", [hb, ab_, g.vecs], [ab_], out=ab_[:], in0=hb[:, 1:TB + 1], scalar=cw(1, jj), in1=ab_[:], op0=ALU.mult, op1=ALU.add)
                S.i("dve", "scalar_tensor_tensor", [hb, ab_, g.vecs], [ab_], out=ab_[:], in0=hb[:, 0:TB], scalar=cw(0, jj), in1=ab_[:], op0=ALU.mult, op1=ALU.add)
            S.i("act", "activation", [ag], [ag], out=ag[:], in_=ag[:], func=AF.Silu)
            S.i("dve", "tensor_tensor", [au, ag], [R(a, j)], out=a[:, j, :], in0=au[:], in1=ag[:], op=ALU.mult)
            if blk == 0:
                for ug, hb in ((0, hu), (1, hg)):
                    S.i("dve", "tensor_copy", [hb], [R(halo, (ug, j))], out=halo[:, ug * NPAIR + j, :], in_=hb[:, TB:TB + 2])
        for m in range(NCH):
            wo = wos[m % 2]
            if m + 1 < NCH:
                S.dma("pool", wos[(m + 1) % 2][:], W_out[:, :, (m + 1) * 128:(m + 2) * 128], writes=[wos[(m + 1) % 2]])
            for h in range(2):
                bk = pb[(m * 2 + h) % 8]
                tt = blk * 2 + h
                for k in range(NPAIR):
                    S.i("pe", "matmul", [wo, R(a, k)], [bk], out=bk[:], lhsT=wo[:, k, :], rhs=a[:, k, h * 512:(h + 1) * 512],
                        start=(k == 0), stop=(k == NPAIR - 1))
                xs = x[:, m, tt * 512:(tt + 1) * 512]
                S.i("dve", "scalar_tensor_tensor", [bk, R(x, (m, tt))], [R(x, (m, tt))], out=xs, in0=xs, scalar=ALPHA, in1=bk[:],
                    op0=ALU.mult, op1=ALU.add)
    S.pop()
    S.push()
    layer_norm(g, ("ln2_g%d" % i, "ln2_b%d" % i), range(4))
    S.pop()

def moba_phase(g, i):
    S = g.S
    x, xb = g.x, g.xb
    S.push()
    Wqkv = g.W["moba_w_qkv"][0].rearrange("(k p) n -> p k n", p=128)
    Wo = g.W["moba_w_o"][0].rearrange("(k p) n -> p k n", p=128)
    ogT = S.sbuf("ogT", [128, NCH, T_SEQ], BF16)
    wsl = [[S.sbuf("mw", [128, NCH, 128], BF16) for _ in range(3)] for _ in range(2)]
    qkv = [[S.sbuf("mqkv", [128, T_SEQ], BF16) for _ in range(3)] for _ in range(2)]
    vtms = [S.sbuf("vtm", [128, 16, 128], BF16) for _ in range(2)]
    ksum = S.sbuf("ksum", [128, 8], F32)
    kmean = S.sbuf("kmean", [128, 8], BF16)
    P = S.sbuf("mP", [128, T_SEQ], BF16)
    PT = S.sbuf("mPT", [128, 16, 128], BF16)
    sd = S.sbuf("msd", [128, 128], F32)
    g8 = S.sbuf("g8", [128, 8], F32)
    top8 = S.sbuf("top8", [128, 8], F32)
    mb = S.sbuf("mb", [128, 8], F32)
    b8 = S.sbuf("b8", [128, 8], F32)
    nm = S.sbuf("nm", [128, 4], F32)
    rs = S.sbuf("rs", [128, 12], F32)
    rinv = S.sbuf("rinv", [128, 2], F32)
    otm = S.sbuf("otm", [128, 128], BF16)
    wos = [S.sbuf("mwo", [128, NCH, 128], BF16) for _ in range(2)]
    sc = S.psum("msc", [128, 2048])
    pT = S.psum("mpT", [128, 1024], BF16)
    gps = S.psum("mgps", [128, 512])
    ov = S.psum("mov", [128, 512])
    pj = S.psum("mpj", [128, 512])
    tri = g.cf[:, 128:256]
    if _DBG.get('dbg_memset'):
        S.i('pool', 'memset', [], [ogT], ap=ogT[:], constant=0.0)
    ident = g.identb
    for c in range(_DBG.get('moba_pairs', NCH)):
        ws = wsl[c % 2]
        qT, kT, vT = qkv[c % 2]
        vtm = vtms[c % 2]
        for w in range(3):
            S.dma("pool", ws[w][:], Wqkv[:, :, w * 1024 + c * 128: w * 1024 + (c + 1) * 128], writes=[ws[w]])
        for w, dst in ((0, qT), (1, kT), (2, vT)):
            for tt in range(4):
                for k in range(NCH):
                    S.i("pe", "matmul", [ws[w], R(xb, (k, tt))], [pj], out=pj[:], lhsT=ws[w][:, k, :], rhs=xb[:, k, tt * 512:(tt + 1) * 512],
                        start=(k == 0), stop=(k == NCH - 1))
                if w == 0:
                    S.i("act", "activation", [pj], [R(dst, tt)], out=dst[:, tt * 512:(tt + 1) * 512], in_=pj[:], func=AF.Copy, scale=0.125)
                else:
                    S.i("act", "activation", [pj], [R(dst, tt)], out=dst[:, tt * 512:(tt + 1) * 512], in_=pj[:], func=AF.Copy)
                if w == 1 and _DBG.get('moba_stage', 9) >= 0.2:
                    S.i("dve", "tensor_reduce", [pj], [R(ksum, tt)], out=ksum[:, 2 * tt:2 * tt + 2],
                        in_=pj[:].rearrange("p (b k) -> p b k", b=2), axis=AX.X, op=ALU.add)
        if _DBG.get('moba_stage', 9) >= 0.2:
            S.i("dve", "tensor_scalar", [ksum], [kmean], out=kmean[:], in0=ksum[:], scalar1=1.0 / 256.0, scalar2=None, op0=ALU.mult)
        for b in range(2 if _DBG.get('moba_stage', 9) >= 0.3 else 0):
            for t8 in range(8):
                kt = b * 8 + t8
                S.i("pe", "transpose", [vT, g.cb], [pT], out=pT[:, t8 * 128:(t8 + 1) * 128], in_=vT[:, kt * 128:(kt + 1) * 128], identity=ident)
            S.i("dve", "tensor_copy", [pT], [R(vtm, b)], out=vtm[:, b * 8:(b + 1) * 8, :].rearrange("p a b -> p (a b)"), in_=pT[:])
        stg = _DBG.get('moba_stage', 9)
        for qt in _DBG.get('moba_qts', range(16)):
            if stg < 2:
                break
            qb = qt // 2
            nkt = qt + 1
            qs = slice(qt * 128, (qt + 1) * 128)
            for hh in range(2):
                ps = slice(hh * 64, (hh + 1) * 64)
                use_thr = qb >= 4
                if use_thr:
                    S.i("pe", "matmul", [qT, kmean], [gps], out=gps[:, 0:8], lhsT=qT[ps, qs], rhs=kmean[ps, 0:8], start=True, stop=True)
                    S.i("pool", "memset", [], [R(g8, "pad")], ap=g8[:, qb:8], constant=-1.0e30)
                    S.i("dve", "tensor_copy", [gps], [R(g8, "val")], out=g8[:, 0:qb], in_=gps[:, 0:qb])
                    S.i("dve", "max", [g8], [top8], out=top8[:], in_=g8[:])
                    S.i("dve", "tensor_scalar", [g8, top8], [mb], out=mb[:], in0=g8[:], scalar1=top8[:, 2:3], scalar2=30000.0,
                        op0=ALU.is_ge, op1=ALU.mult)
                ncol = nkt * 128
                for b in range((ncol + 511) // 512):
                    w_ = min(512, ncol - b * 512)
                    S.i("pe", "matmul", [qT, kT], [sc], out=sc[:, b * 512:b * 512 + w_], lhsT=qT[ps, qs], rhs=kT[ps, b * 512:b * 512 + w_],
                        start=True, stop=True)
                S.i("dve", "tensor_tensor", [sc, g.cf], [sd], out=sd[:], in0=sc[:, qt * 128:(qt + 1) * 128], in1=tri, op=ALU.add)
                S.i("dve", "tensor_reduce", [sd], [R(nm, 0)], out=nm[:, 0:1], in_=sd[:], axis=AX.X, op=ALU.max, negate=True)
                if qt > 0:
                    S.i("dve", "tensor_reduce", [sc], [R(nm, 1)], out=nm[:, 1:2], in_=sc[:, 0:qt * 128], axis=AX.X, op=ALU.max, negate=True)
                    S.i("dve", "tensor_tensor", [nm], [R(nm, 2)], out=nm[:, 2:3], in0=nm[:, 0:1], in1=nm[:, 1:2], op=ALU.min)
                    negm = nm[:, 2:3]
                else:
                    negm = nm[:, 0:1]
                if stg < 3:
                    continue
                if use_thr:
                    S.i("dve", "tensor_scalar", [mb, nm], [b8], out=b8[:], in0=mb[:], scalar1=negm, scalar2=-30000.0, op0=ALU.add, op1=ALU.add)
                npz = 0
                if use_thr:
                    for n in range(qb):
                        S.i("act", "activation", [sc, b8], [R(P, n), R(rs, npz)], out=P[:, n * 256:(n + 1) * 256], in_=sc[:, n * 256:(n + 1) * 256],
                            func=AF.Exp, bias=b8[:, n:n + 1], scale=1.0, accum_out=rs[:, npz:npz + 1])
                        npz += 1
                    if qt % 2 == 1:
                        S.i("act", "activation", [sc, nm], [R(P, "o"), R(rs, npz)], out=P[:, (qt - 1) * 128:qt * 128], in_=sc[:, (qt - 1) * 128:qt * 128],
                            func=AF.Exp, bias=negm, scale=1.0, accum_out=rs[:, npz:npz + 1])
                        npz += 1
                elif qt > 0:
                    S.i("act", "activation", [sc, nm], [R(P, "past"), R(rs, npz)], out=P[:, 0:qt * 128], in_=sc[:, 0:qt * 128],
                        func=AF.Exp, bias=negm, scale=1.0, accum_out=rs[:, npz:npz + 1])
                    npz += 1
                S.i("act", "activation", [sd, nm], [R(P, "d"), R(rs, npz)], out=P[:, qt * 128:(qt + 1) * 128], in_=sd[:],
                    func=AF.Exp, bias=negm, scale=1.0, accum_out=rs[:, npz:npz + 1])
                npz += 1
                S.i("dve", "tensor_reduce", [rs], [R(rs, 11)], out=rs[:, 11:12], in_=rs[:, 0:npz], axis=AX.X, op=ALU.add)
                S.i("dve", "reciprocal", [R(rs, 11)], [R(rinv, hh)], out=rinv[:, hh:hh + 1], in_=rs[:, 11:12])
                if stg < 4:
                    continue
                for b in range((nkt + 7) // 8):
                    n8 = min(8, nkt - b * 8)
                    for t8 in range(n8):
                        kt = b * 8 + t8
                        S.i("pe", "transpose", [P, g.cb], [pT], out=pT[:, t8 * 128:(t8 + 1) * 128], in_=P[:, kt * 128:(kt + 1) * 128], identity=ident)
                    S.i("dve", "tensor_copy", [pT], [R(PT, b)], out=PT[:, b * 8:b * 8 + n8, :].rearrange("p a b -> p (a b)"), in_=pT[:, 0:n8 * 128])
                for kt in range(nkt):
                    S.i("pe", "matmul", [PT, vtm], [R(ov, hh)], out=ov[:, hh * 64:(hh + 1) * 64], lhsT=PT[:, kt, :], rhs=vtm[:, kt, hh * 64:(hh + 1) * 64],
                        start=(kt == 0), stop=(kt == nkt - 1))
                S.i("dve", "tensor_scalar", [R(ov, hh), R(rinv, hh)], [R(otm, hh)], out=otm[:, hh * 64:(hh + 1) * 64], in0=ov[:, hh * 64:(hh + 1) * 64],
                    scalar1=rinv[:, hh:hh + 1], scalar2=None, op0=ALU.mult)
            if stg < 5:
                continue
            S.i("pe", "transpose", [otm, g.cb], [pT], out=pT[:, 0:128], in_=otm[:], identity=ident)
            S.i("act", "activation", [pT], [R(ogT, (c, qt))], out=ogT[:, c, qs], in_=pT[:, 0:128], func=AF.Copy)
    for m in range(NCH):
        wo = wos[m % 2]
        S.dma("pool", wo[:], Wo[:, :, m * 128:(m + 1) * 128], writes=[wo])
        for tt in range(4):
            for k in range(NCH):
                S.i("pe", "matmul", [wo, ogT], [pj], out=pj[:], lhsT=wo[:, k, :], rhs=ogT[:, k, tt * 512:(tt + 1) * 512], start=(k == 0), stop=(k == NCH - 1))
            xs = x[:, m, tt * 512:(tt + 1) * 512]
            S.i("dve", "scalar_tensor_tensor", [pj, R(x, (m, tt))], [R(x, (m, tt))], out=xs, in0=xs, scalar=ALPHA, in1=pj[:], op0=ALU.mult, op1=ALU.add)
    S.pop()
    S.push()
    layer_norm(g, ("ln1_g%d" % i, "ln1_b%d" % i), range(4))
    S.pop()

def ret_phase(g, i):
    S = g.S
    x, xb = g.x, g.xb
    S.push()
    Win = g.W["ret_w_in"][0].rearrange("(k p) n -> p k n", p=128)
    Wsw = g.ret_sw_d.rearrange("(k p) n -> p k n", p=128)
    Wo = g.W["ret_w_o"][0].rearrange("(k p) n -> p k n", p=128)
    wq = S.sbuf("rwq", [128, NCH, 256], BF16); wqs = S.sbuf("rwqs", [128, NCH, 256], BF16)
    wk = S.sbuf("rwk", [128, NCH, 256], BF16); wks = S.sbuf("rwks", [128, NCH, 256], BF16)
    wv = S.sbuf("rwv", [128, NCH, 512], BF16); wg = S.sbuf("rwg", [128, NCH, 512], BF16)
    wo = S.sbuf("rwo", [128, 4, 1024], BF16)
    rm = S.sbuf("rrm", [128, 384], F32)
    tabs = [S.sbuf("rtab", [128, 2, 2, 512], F32) for _ in range(2)]
    qrot = S.sbuf("qrot", [128, 2, 512], BF16); krot = S.sbuf("krot", [128, 2, 512], BF16)
    vT = S.sbuf("rvT", [128, 4, 512], BF16); sgT = S.sbuf("rsgT", [128, 4, 512], BF16)
    vtm = S.sbuf("rvtm", [128, 4, 512], BF16); ktm = S.sbuf("rktm", [128, 4, 256], BF16)
    og = S.sbuf("rog", [128, 4, 512], BF16)
    S32 = S.sbuf("rS32", [128, 2, 512], F32); Sb = S.sbuf("rSb", [128, 2, 512], BF16)
    t1 = S.sbuf("rt1", [128, 512], F32); t2 = S.sbuf("rt2", [128, 512], F32)
    ST = S.sbuf("rST", [128, 128], BF16); qcd = S.sbuf("rqcd", [128, 2, 128], BF16)
    osbs = [S.sbuf("rosb", [128, 512], F32) for _ in range(2)]; osqs = [S.sbuf("rosq", [128, 512], F32) for _ in range(2)]
    means = [S.sbuf("rmean", [128, 128], F32) for _ in range(2)]; msqs = [S.sbuf("rmsq", [128, 128], F32) for _ in range(2)]; rstds = [S.sbuf("rrstd", [128, 128], F32) for _ in range(2)]
    nchunk = 0
    pA = S.psum("rpA", [128, 512]); pB = S.psum("rpB", [128, 512])
    opss = [S.psum("rops", [128, 512]) for _ in range(2)]
    ups = [S.psum("rups", [128, 512]) for _ in range(2)]
    pT = S.psum("rpT", [128, 1024], BF16)
    pst = S.psum("rpst", [128, 512])
    sps = pst
    ident = g.identb
    ones = g.ones32
    ntab = 0
    for h in range(4):
        S.dma("pool", wq[:], Win[:, :, h * 256:(h + 1) * 256], writes=[wq])
        S.dma("pool", wqs[:], Wsw[:, :, h * 256:(h + 1) * 256], writes=[wqs])
        S.dma("pool", wk[:], Win[:, :, 1024 + h * 256:1024 + (h + 1) * 256], writes=[wk])
        S.dma("pool", wks[:], Wsw[:, :, 1024 + h * 256:1024 + (h + 1) * 256], writes=[wks])
        S.dma("pool", wv[:], Win[:, :, 2048 + h * 512:2048 + (h + 1) * 512], writes=[wv])
        S.dma("pool", wg[:], Win[:, :, 4096 + h * 512:4096 + (h + 1) * 512], writes=[wg])
        S.dma("pool", wo[:], Wo[:, h * 4:(h + 1) * 4, :], writes=[wo])
        S.dma("sp", rm[:], g.rmask_d[h], writes=[rm])
        S.i("pool", "memset", [], [S32], ap=S32[:], constant=0.0)
        S.i("pool", "memset", [], [Sb], ap=Sb[:], constant=0.0)
        for tt in range(4):
            ts = slice(tt * 512, (tt + 1) * 512)
            tab = tabs[ntab % 2]; ntab += 1
            for cs_ in range(2):
                S.dma("sp", tab[:, cs_, :, :], g.rtab_d[cs_][:, :, ts], writes=[R(tab, None)])
            for (wa, wb, dst) in ((wq, wqs, qrot), (wk, wks, krot)):
                for dc in range(2):
                    for k in range(NCH):
                        S.i("pe", "matmul", [wa, R(xb, (k, tt))], [pA], out=pA[:], lhsT=wa[:, k, dc * 128:(dc + 1) * 128], rhs=xb[:, k, ts],
                            start=(k == 0), stop=(k == NCH - 1))
                    for k in range(NCH):
                        S.i("pe", "matmul", [wb, R(xb, (k, tt))], [pB], out=pB[:], lhsT=wb[:, k, dc * 128:(dc + 1) * 128], rhs=xb[:, k, ts],
                            start=(k == 0), stop=(k == NCH - 1))
                    S.i("dve", "tensor_tensor", [pA, tab], [t1], out=t1[:], in0=pA[:], in1=tab[:, 0, dc, :], op=ALU.mult)
                    S.i("dve", "tensor_tensor", [pB, tab], [t2], out=t2[:], in0=pB[:], in1=tab[:, 1, dc, :], op=ALU.mult)
                    S.i("pool", "tensor_tensor", [t1, t2], [R(dst, dc)], out=dst[:, dc, :], in0=t1[:], in1=t2[:], op=ALU.add)
            for ec in range(4):
                for k in range(NCH):
                    S.i("pe", "matmul", [wv, R(xb, (k, tt))], [pA], out=pA[:], lhsT=wv[:, k, ec * 128:(ec + 1) * 128], rhs=xb[:, k, ts],
                        start=(k == 0), stop=(k == NCH - 1))
                S.i("act", "activation", [pA], [R(vT, ec)], out=vT[:, ec, :], in_=pA[:], func=AF.Copy)
                for k in range(NCH):
                    S.i("pe", "matmul", [wg, R(xb, (k, tt))], [pB], out=pB[:], lhsT=wg[:, k, ec * 128:(ec + 1) * 128], rhs=xb[:, k, ts],
                        start=(k == 0), stop=(k == NCH - 1))
                S.i("act", "activation", [pB], [R(sgT, ec)], out=sgT[:, ec, :], in_=pB[:], func=AF.Silu)
            for n in range(4):
                for ec in range(4):
                    idx = (n % 2) * 4 + ec
                    S.i("pe", "transpose", [vT, g.cb], [pT], out=pT[:, idx * 128:(idx + 1) * 128], in_=vT[:, ec, n * 128:(n + 1) * 128], identity=ident)
                if n % 2 == 1:
                    S.i("dve", "tensor_copy", [pT], [R(vtm, n // 2)], out=vtm[:, n - 1:n + 1, :].rearrange("p a b -> p (a b)"), in_=pT[:])
            for n in range(4):
                for dc in range(2):
                    idx = n * 2 + dc
                    S.i("pe", "transpose", [krot, g.cb], [pT], out=pT[:, idx * 128:(idx + 1) * 128], in_=krot[:, dc, n * 128:(n + 1) * 128], identity=ident)
            S.i("dve", "tensor_scalar", [pT, rm], [ktm], out=ktm[:].rearrange("p a b -> p (a b)"), in0=pT[:], scalar1=rm[:, 256:257], scalar2=None, op0=ALU.mult)
            def partA(n):
                cs = slice(n * 128, (n + 1) * 128)
                opsn = opss[n % 2]
                for dc in range(2):
                    S.i("pe", "matmul", [krot, qrot], [sps], out=sps[:, 256:384], lhsT=krot[:, dc, cs], rhs=qrot[:, dc, cs], start=(dc == 0), stop=(dc == 1))
                S.i("dve", "tensor_tensor", [sps, rm], [ST], out=ST[:], in0=sps[:, 256:384], in1=rm[:, 0:128], op=ALU.mult)
                S.i("pool", "tensor_tensor", [qrot, rm], [qcd], out=qcd[:], in0=qrot[:, :, cs], in1=rm[:, 128:256].unsqueeze(1).to_broadcast([128, 2, 128]), op=ALU.mult)
                for ec in range(4):
                    es = slice(ec * 128, (ec + 1) * 128)
                    S.i("pe", "matmul", [vtm, ST], [opsn], out=opsn[:, es], lhsT=vtm[:, n, es], rhs=ST[:], start=True, stop=False)
                    for dc in range(2):
                        S.i("pe", "matmul", [Sb, qcd], [opsn], out=opsn[:, es], lhsT=Sb[:, dc, es], rhs=qcd[:, dc, :], start=False, stop=(dc == 1))
                for dc in range(2):
                    S.i("pe", "matmul", [ktm, vtm], [ups[dc]], out=ups[dc][:], lhsT=ktm[:, n, dc * 128:(dc + 1) * 128], rhs=vtm[:, n, :], start=True, stop=True)
                    S.i("dve", "scalar_tensor_tensor", [S32, ups[dc], rm], [R(S32, dc)], out=S32[:, dc, :], in0=S32[:, dc, :], scalar=rm[:, 257:258], in1=ups[dc][:],
                        op0=ALU.mult, op1=ALU.add)
                    S.i("act", "activation", [R(S32, dc)], [R(Sb, dc)], out=Sb[:, dc, :], in_=S32[:, dc, :], func=AF.Copy)

            def partB(n):
                cs = slice(n * 128, (n + 1) * 128)
                opsn = opss[n % 2]
                osb, osq, mean, msq, rstd = osbs[n % 2], osqs[n % 2], means[n % 2], msqs[n % 2], rstds[n % 2]
                S.i("act", "activation", [opsn], [osb], out=osb[:], in_=opsn[:], func=AF.Copy)
                S.i("act", "activation", [opsn], [osq], out=osq[:], in_=opsn[:], func=AF.Square)
                for ec in range(4):
                    S.i("pe", "matmul", [osb, g.cf], [R(pst, 0)], out=pst[:, 0:128], lhsT=ones, rhs=osb[:, ec * 128:(ec + 1) * 128], start=(ec == 0), stop=(ec == 3))
                for ec in range(4):
                    S.i("pe", "matmul", [osq, g.cf], [R(pst, 1)], out=pst[:, 128:256], lhsT=ones, rhs=osq[:, ec * 128:(ec + 1) * 128], start=(ec == 0), stop=(ec == 3))
                S.i("dve", "tensor_scalar", [R(pst, 0)], [mean], out=mean[:], in0=pst[:, 0:128], scalar1=1.0 / 512, scalar2=None, op0=ALU.mult)
                S.i("dve", "tensor_tensor", [mean], [msq], out=msq[:], in0=mean[:], in1=mean[:], op=ALU.mult)
                S.i("dve", "scalar_tensor_tensor", [R(pst, 1), msq], [msq], out=msq[:], in0=pst[:, 128:256], scalar=1.0 / 512, in1=msq[:], op0=ALU.mult, op1=ALU.subtract)
                S.i("dve", "tensor_scalar", [msq], [msq], out=msq[:], in0=msq[:], scalar1=1e-5, scalar2=None, op0=ALU.add)
                S.i("act", "activation", [msq], [rstd], out=rstd[:], in_=msq[:], func=AF.Sqrt)
                S.i("dve", "reciprocal", [rstd], [rstd], out=rstd[:], in_=rstd[:])
                mb_ = mean[:].unsqueeze(1).to_broadcast([128, 4, 128])
                rb_ = rstd[:].unsqueeze(1).to_broadcast([128, 4, 128])
                o3 = osb[:].rearrange("p (a b) -> p a b", b=128)
                S.i("dve", "tensor_tensor", [osb, mean], [osb], out=o3, in0=o3, in1=mb_, op=ALU.subtract)
                S.i("dve", "tensor_tensor", [osb, rstd], [osb], out=o3, in0=o3, in1=rb_, op=ALU.mult)
                for ec in range(4):
                    col = h * 4 + ec
                    S.i("act", "activation", [osb, g.vecs], [R(osq, ec)], out=osq[:, ec * 128:(ec + 1) * 128], in_=osb[:, ec * 128:(ec + 1) * 128], func=AF.Identity,
                        bias=vcol(g, "ret_gn_b", col), scale=vcol(g, "ret_gn_g", col))
                S.i("dve", "tensor_tensor", [osq, sgT], [R(og, n)], out=og[:, :, cs], in0=osq[:].rearrange("p (a b) -> p a b", b=128), in1=sgT[:, :, cs], op=ALU.mult)

            partA(0)
            for n in range(4):
                if n + 1 < 4:
                    partA(n + 1)
                partB(n)
            for m in range(NCH):
                pw = pA if m % 2 == 0 else pB
                for k in range(4):
                    S.i("pe", "matmul", [wo, og], [pw], out=pw[:], lhsT=wo[:, k, m * 128:(m + 1) * 128], rhs=og[:, k, :], start=(k == 0), stop=(k == 3))
                xs = x[:, m, ts]
                if h == 0:
                    S.i("dve", "scalar_tensor_tensor", [pw, R(x, (m, tt))], [R(x, (m, tt))], out=xs, in0=xs, scalar=ALPHA, in1=pw[:], op0=ALU.mult, op1=ALU.add)
                else:
                    S.i("dve", "tensor_tensor", [pw, R(x, (m, tt))], [R(x, (m, tt))], out=xs, in0=xs, in1=pw[:], op=ALU.add)
    S.pop()
    S.push()
    layer_norm(g, ("ln1_g%d" % i, "ln1_b%d" % i), range(4))
    S.pop()

def rwkv_phase(g, i, j):
    S = g.S
    x, xb = g.x, g.xb
    S.push()
    TW = 256
    NT_ = T_SEQ // TW
    W = g.W
    Wrkv = [W["rwkv_w_rkv"][j, w].rearrange("(k p) n -> p k n", p=128) for w in range(3)]
    Wo = W["rwkv_w_o"][j].rearrange("(k p) n -> p k n", p=128)
    V = lambda name, c=0: vcol(g, "%s_%d" % (name, j), c)
    w1 = S.sbuf("w1", [128, NCH, 64], BF16); a1 = S.sbuf("a1", [128, NCH, 64], BF16); g1 = S.sbuf("g1", [128, NCH, 160], BF16)
    w2 = S.sbuf("w2", [64, 1024], BF16); a2 = S.sbuf("a2", [64, 1024], BF16); g2 = S.sbuf("g2", [128, 2, 1024], BF16)
    S.dma("pool", w1[:], W["rwkv_w1"][j].rearrange("(k p) n -> p k n", p=128), writes=[w1])
    S.dma("pool", a1[:], W["rwkv_a1"][j].rearrange("(k p) n -> p k n", p=128), writes=[a1])
    S.dma("pool", g1[:], W["rwkv_g1"][j].rearrange("(k p) n -> p k n", p=128), writes=[g1])
    S.dma("pool", w2[:], W["rwkv_w2"][j], writes=[w2])
    S.dma("pool", a2[:], W["rwkv_a2"][j], writes=[a2])
    S.dma("pool", g2[:, 0, :], W["rwkv_g2"][j][0:128, :], writes=[R(g2, None)])
    S.dma("pool", g2[0:32, 1, :], W["rwkv_g2"][j][128:160, :], writes=[R(g2, None)])
    rows = S.sbuf("gnrows", [128, 2, 1024], BF16)
    S.dma("pool", rows[:, 0, :], g.rows_d[2 * j], writes=[R(rows, None)])
    S.dma("pool", rows[:, 1, :], g.rows_d[2 * j + 1], writes=[R(rows, None)])
    om = S.sbuf("om", [128, 56], F32)
    mo = g.voff["mix0_%d" % j]
    S.i("dve", "tensor_scalar", [g.vecs], [om], out=om[:, 0:48], in0=g.vecs[:, mo:mo + 48], scalar1=-1.0, scalar2=1.0, op0=ALU.mult, op1=ALU.add)
    ko = g.voff["k_a_%d" % j]
    S.i("dve", "tensor_scalar", [g.vecs], [om], out=om[:, 48:56], in0=g.vecs[:, ko:ko + 8], scalar1=-1.0, scalar2=1.0, op0=ALU.mult, op1=ALU.add)
    xx = S.sbuf("xx", [128, NCH, TW], F32)
    xlast = S.sbuf("xlast", [128, NCH, 2], F32)
    S.i("pool", "memset", [], [xlast], ap=xlast[:], constant=0.0)
    hw = S.sbuf("hw", [64, TW], BF16); ha = S.sbuf("ha", [64, TW], BF16); sg = S.sbuf("sg", [128, 2, TW], BF16)
    wsl = [S.sbuf("rwsl", [128, NCH, 128], BF16) for _ in range(3)]
    TS = [dict(), dict()]
    for nm in ("tA", "tsw", "tkk", "tcs", "te1", "te2", "te3", "trn", "tt"):
        for q_ in range(2):
            TS[q_][nm] = S.sbuf(nm, [128, TW], F32)
    for q_ in range(2):
        TS[q_]["tkk2"] = S.sbuf("tkk2", [128, TW], BF16)
    PC = S.sbuf("PC", [128, NCH, 2], F32)
    def FM(a, c, sl):
        return xb[:, c, a * TW + sl.start:a * TW + sl.stop]
    FMK = lambda a, c: R(xb, ("fm", a, c))
    Lk = [[S.sbuf("Lk", [128, 4, 128], BF16) for _ in range(2)] for _ in range(2)]
    Mk = [[S.sbuf("Mk", [128, 4, 128], BF16) for _ in range(2)] for _ in range(2)]
    NTt = [[S.sbuf("NT", [128, 4, 128], BF16) for _ in range(2)] for _ in range(2)]
    Mak = [S.sbuf("Mak", [128, 4, 128], BF16) for _ in range(2)]
    Mrb = S.sbuf("Mrb", [128, 16, 128], BF16); Mrk = S.sbuf("Mrk", [128, 16, 128], BF16)
    Atm = S.sbuf("Atm", [128, 1024], BF16); Btm = S.sbuf("Btm", [128, 1024], BF16); Ktm = S.sbuf("Ktm", [128, 1024], BF16); Vtm = S.sbuf("Vtm", [128, 1024], BF16)
    AhT = S.sbuf("AhT", [128, NCH, 128], BF16); Xb = S.sbuf("Xb", [128, 256], BF16); Vhat = S.sbuf("Vhat", [128, 1024], BF16)
    Ub = S.sbuf("Ub", [128, 1024], BF16); ST = S.sbuf("ST", [128, NCH, 64], BF16); STs = S.sbuf("STs", [128, NCH, 64], F32)
    yn = S.sbuf("yn", [128, 1024], F32); st4 = S.sbuf("st4", [128, 4, 16], F32); bsum = S.sbuf("bsum", [128, 16], F32)
    ogtm = S.sbuf("ogtm", [128, 1024], BF16); ogT = S.sbuf("ogT", [128, NCH, TW], BF16)
    PB = [S.psum("rp", [128, 512]) for _ in range(7)]
    pT = S.psum("rpT", [128, 1024], BF16)
    ident = g.identb
    blk1 = g.cb[:, 128:256]
    hind = g.cb[:, 256:258]
    ones = g.ones32
    S.i("pool", "memset", [], [ST], ap=ST[:], constant=0.0)
    nws = 0
    stg = _DBG.get('rw_stage', 99)
    for tix in range(_DBG.get('rw_tiles', NT_)):
        t0 = tix * TW
        tt = t0 // 512
        S.i("dve", "tensor_tensor", [R(x, None)], [R(xx, "m")], out=xx[:, :, 1:TW], in0=x[:, :, t0:t0 + TW - 1], in1=x[:, :, t0 + 1:t0 + TW], op=ALU.subtract)
        S.i("dve", "tensor_tensor", [R(x, None), xlast], [R(xx, "0")], out=xx[:, :, 0:1], in0=xlast[:, :, tix % 2:tix % 2 + 1], in1=x[:, :, t0:t0 + 1], op=ALU.subtract)
        S.i("dve", "tensor_copy", [R(x, None)], [xlast], out=xlast[:, :, (tix + 1) % 2:(tix + 1) % 2 + 1], in_=x[:, :, t0 + TW - 1:t0 + TW])
        def mix(m, dst):
            for c in range(NCH):
                mc = vcol(g, "mix%d_%d" % (m, j), c)
                S.i("dve", "scalar_tensor_tensor", [R(x, (c, tt)), xx, g.vecs], [dst.key(c)], out=dst.ap(c), in0=xx[:, c, :], scalar=mc, in1=x[:, c, t0:t0 + TW],
                    op0=ALU.mult, op1=ALU.add)
        nx = [0]
        class XM:
            def __init__(self, slot):
                self.slot = slot
            def ap(self, c):
                return xb[:, c, 1536 + self.slot * TW:1536 + (self.slot + 1) * TW]
            def key(self, c):
                return R(xb, ("xm", self.slot, c))
        def nxm():
            nx[0] += 1
            return XM(nx[0] % 2)
        d_ = nxm(); mix(1, d_)
        for k in range(NCH):
            S.i("pe", "matmul", [w1, d_.key(k)], [PB[0]], out=PB[0][0:64, 0:TW], lhsT=w1[:, k, :], rhs=d_.ap(k), start=(k == 0), stop=(k == NCH - 1))
        S.i("act", "activation", [PB[0]], [hw], out=hw[:], in_=PB[0][0:64, 0:TW], func=AF.Tanh)
        d_ = nxm(); mix(4, d_)
        for k in range(NCH):
            S.i("pe", "matmul", [a1, d_.key(k)], [PB[1]], out=PB[1][0:64, 0:TW], lhsT=a1[:, k, :], rhs=d_.ap(k), start=(k == 0), stop=(k == NCH - 1))
        S.i("act", "activation", [PB[1]], [ha], out=ha[:], in_=PB[1][0:64, 0:TW], func=AF.Copy)
        d_ = nxm(); mix(5, d_)
        for (lo, hi, kc) in ((0, 128, 0), (128, 160, 1)):
            for k in range(NCH):
                S.i("pe", "matmul", [g1, d_.key(k)], [PB[2]], out=PB[2][0:hi - lo, 0:TW], lhsT=g1[:, k, lo:hi], rhs=d_.ap(k), start=(k == 0), stop=(k == NCH - 1))
            S.i("act", "activation", [PB[2]], [R(sg, kc)], out=sg[0:hi - lo, kc, :], in_=PB[2][0:hi - lo, 0:TW], func=AF.Sigmoid)
        xr = nxm(); mix(0, xr)
        xk = nxm(); mix(2, xk)
        if stg < 3:
            continue
        for c in range(NCH):
            wr = wsl[nws % 3]; nws += 1
            wk_ = wsl[nws % 3]; nws += 1
            S.dma("pool", wr[:], Wrkv[0][:, :, c * 128:(c + 1) * 128], writes=[wr])
            S.dma("pool", wk_[:], Wrkv[1][:, :, c * 128:(c + 1) * 128], writes=[wk_])
            d = TS[c % 2]
            tA, tsw, tcs, te1, te2, te3, tkk, tkk2, trn, tt_ = (d[k_] for k_ in ("tA", "tsw", "tcs", "te1", "te2", "te3", "tkk", "tkk2", "trn", "tt"))
            td3 = tsw; tkkn = tkk; ttb = tkk; tkm = tt_
            if c % 2 == 0:
                rp, kp, zw, za, ssp = PB[0], PB[1], PB[2], PB[3], PB[4]
            else:
                rp, kp, zw, za, ssp = PB[5], PB[6], PB[2], PB[3], PB[4]
            for k in range(NCH):
                S.i("pe", "matmul", [wr, xr.key(k)], [rp], out=rp[:, 0:TW], lhsT=wr[:, k, :], rhs=xr.ap(k), start=(k == 0), stop=(k == NCH - 1))
            for k in range(NCH):
                S.i("pe", "matmul", [wk_, xk.key(k)], [kp], out=kp[:, 0:TW], lhsT=wk_[:, k, :], rhs=xk.ap(k), start=(k == 0), stop=(k == NCH - 1))
            S.i("pe", "matmul", [w2, hw], [zw], out=zw[:, 0:TW], lhsT=w2[:, c * 128:(c + 1) * 128], rhs=hw[:], start=True, stop=True)
            S.i("pe", "matmul", [a2, ha], [za], out=za[:, 0:TW], lhsT=a2[:, c * 128:(c + 1) * 128], rhs=ha[:], start=True, stop=True)
            S.i("act", "activation", [za, g.vecs], [tA], out=tA[:], in_=za[:, 0:TW], func=AF.Sigmoid, bias=V("a0", c), scale=1.0)
            S.i("act", "activation", [zw, g.vecs], [tsw], out=tsw[:], in_=zw[:, 0:TW], func=AF.Sigmoid, bias=V("w0", c), scale=1.0)
            for n in range(2):
                cs = slice(n * 128, (n + 1) * 128)
                S.i("dve", "tensor_tensor_scan", [tsw, g.cf], [R(tcs, n)], out=tcs[:, cs], data0=ones, data1=tsw[:, cs], initial=0.0, op0=ALU.mult, op1=ALU.add)
            S.i("dve", "tensor_tensor", [tcs, tsw], [td3], out=td3[:], in0=tcs[:], in1=tsw[:], op=ALU.subtract)
            LD = 0.6065306597126334
            S.i("act", "activation", [tcs], [te1], out=te1[:], in_=tcs[:], func=AF.Exp, scale=-LD)
            S.i("act", "activation", [tcs], [te2], out=te2[:], in_=tcs[:], func=AF.Exp, scale=LD)
            S.i("act", "activation", [td3], [te3], out=te3[:], in_=td3[:], func=AF.Exp, scale=-LD)
            S.i("act", "activation", [te1], [R(PC, c)], out=PC[:, c, :], in_=te1[:, 127:TW:128], func=AF.Copy)
            S.i("dve", "tensor_scalar", [kp, g.vecs], [tkk], out=tkk[:], in0=kp[:, 0:TW], scalar1=V("k_k", c), scalar2=None, op0=ALU.mult)
            S.i("act", "activation", [tkk], [tkk2], out=tkk2[:], in_=tkk[:], func=AF.Square)
            S.i("pe", "matmul", [tkk2, g.cb], [ssp], out=ssp[:, 0:TW], lhsT=blk1, rhs=tkk2[:], start=True, stop=True)
            S.i("act", "activation", [ssp], [trn], out=trn[:], in_=ssp[:, 0:TW], func=AF.Sqrt)
            S.i("dve", "tensor_scalar", [trn], [trn], out=trn[:], in0=trn[:], scalar1=1e-12, scalar2=None, op0=ALU.max)
            S.i("dve", "reciprocal", [trn], [trn], out=trn[:], in_=trn[:])
            S.i("dve", "tensor_tensor", [tkk, trn], [tkkn], out=tkkn[:], in0=tkk[:], in1=trn[:], op=ALU.mult)
            S.i("dve", "tensor_scalar", [tA, g.vecs, om], [tt_], out=tt_[:], in0=tA[:], scalar1=V("k_a", c), scalar2=om[:, 48 + c:49 + c], op0=ALU.mult, op1=ALU.add)
            S.i("dve", "tensor_tensor", [kp, tt_], [tkm], out=tkm[:], in0=kp[:, 0:TW], in1=tt_[:], op=ALU.mult)
            full = slice(0, TW)
            S.i("dve", "scalar_tensor_tensor", [tkkn, te3], [FMK(0, c)], out=FM(0, c, full), in0=tkkn[:], scalar=-1.0, in1=te3[:], op0=ALU.mult, op1=ALU.mult)
            S.i("dve", "tensor_tensor", [tkkn, tA], [ttb], out=ttb[:], in0=tkkn[:], in1=tA[:], op=ALU.mult)
            S.i("dve", "tensor_tensor", [ttb, te2], [FMK(1, c)], out=FM(1, c, full), in0=ttb[:], in1=te2[:], op=ALU.mult)
            S.i("dve", "tensor_tensor", [tkm, te2], [FMK(2, c)], out=FM(2, c, full), in0=tkm[:], in1=te2[:], op=ALU.mult)
            S.i("dve", "tensor_tensor", [rp, te1], [FMK(3, c)], out=FM(3, c, full), in0=rp[:, 0:TW], in1=te1[:], op=ALU.mult)
            S.i("dve", "scalar_tensor_tensor", [rp, g.vecs, tkm], [FMK(4, c)], out=FM(4, c, full), in0=rp[:, 0:TW], scalar=V("r_k", c), in1=tkm[:], op0=ALU.mult, op1=ALU.mult)
        xv = nxm(); mix(3, xv)
        for c in range(NCH):
            wv_ = wsl[nws % 3]; nws += 1
            S.dma("pool", wv_[:], Wrkv[2][:, :, c * 128:(c + 1) * 128], writes=[wv_])
            vp = PB[5 + (c % 2)]
            for k in range(NCH):
                S.i("pe", "matmul", [wv_, xv.key(k)], [vp], out=vp[:, 0:TW], lhsT=wv_[:, k, :], rhs=xv.ap(k), start=(k == 0), stop=(k == NCH - 1))
            S.i("act", "activation", [vp], [FMK(5, c)], out=FM(5, c, slice(0, TW)), in_=vp[:, 0:TW], func=AF.Copy)
        if stg < 5:
            continue
        for n in range(2):
            cs = slice(n * 128, (n + 1) * 128)
            for a_, dst in ((0, Atm), (1, Btm), (2, Ktm), (5, Vtm)):
                for c in range(NCH):
                    S.i("pe", "transpose", [FMK(a_, c), g.cb], [pT], out=pT[:, c * 128:(c + 1) * 128], in_=FM(a_, c, cs), identity=ident)
                S.i("dve" if a_ in (0, 2) else "act", "tensor_copy" if a_ in (0, 2) else "activation", [pT], [dst],
                    **(dict(out=dst[:], in_=pT[:]) if a_ in (0, 2) else dict(out=dst[:], in_=pT[:], func=AF.Copy)))
            if stg < 6:
                continue
            for hgp in range(2):
                grp = [2 * hgp, 2 * hgp + 1]
                def HP(h):
                    return slice((h % 2) * 64, (h % 2) * 64 + 64), h // 2
                for gi, hg in enumerate(grp):
                    heads = [4 * hg + q for q in range(4)]
                    specs = [(1, 0, 0, Mk[gi][0], None), (0, 1, 2, Lk[gi][0], None), (2, 0, 0, Mak[gi], None), (1, 3, 1, Mrb, hg), (2, 3, 1, Mrk, hg)]
                    for si, (la, ra, mki, dst, full16) in enumerate(specs):
                        dview = dst[:] if full16 is None else dst[:, hg * 4:(hg + 1) * 4, :]
                        wkey = dst if full16 is None else R(dst, hg)
                        for par in range(2):
                            bk = PB[(2 * si + par) % 6]
                            for qq in range(2):
                                h = heads[2 * qq + par]
                                ps_, c = HP(h)
                                S.i("pe", "matmul", [FMK(la, c), FMK(ra, c)], [bk], out=bk[:, qq * 128:(qq + 1) * 128], lhsT=FM(la, c, cs)[ps_, :], rhs=FM(ra, c, cs)[ps_, :],
                                    start=True, stop=True)
                            S.i("dve", "tensor_tensor", [bk, g.mk], [wkey], out=dview[:, par::2, :],
                                in0=bk[:, 0:256].rearrange("p (a b) -> p a b", b=128), in1=g.mk[:, mki, 0:256].rearrange("p (a b) -> p a b", b=128), op=ALU.mult)
                if stg < 7:
                    continue
                cur = [0, 0]
                NTin = [Mk[0][0], Mk[1][0]]
                for r in range(7):
                    for gi in range(2):
                        Lc, Mc = Lk[gi][cur[gi]], Mk[gi][cur[gi]]
                        Ln, Mn = Lk[gi][1 - cur[gi]], Mk[gi][1 - cur[gi]]
                        bN, bL, bM = PB[3 * gi], PB[3 * gi + 1], PB[3 * gi + 2]
                        if r >= 1:
                            NTo = NTt[gi][r % 2]
                            for q in range(4):
                                S.i("pe", "matmul", [NTin[gi], g.cb], [bN], out=bN[:, q * 128:(q + 1) * 128], lhsT=ident, rhs=NTin[gi][:, q, :], start=True, stop=False)
                                S.i("pe", "matmul", [Mc, g.cb], [bN], out=bN[:, q * 128:(q + 1) * 128], lhsT=ident, rhs=Mc[:, q, :], start=False, stop=False)
                                S.i("pe", "matmul", [Lc, NTin[gi]], [bN], out=bN[:, q * 128:(q + 1) * 128], lhsT=Lc[:, q, :], rhs=NTin[gi][:, q, :], start=False, stop=True)
                        if r < 6:
                            for q in range(4):
                                S.i("pe", "matmul", [Mc, Lc], [bL], out=bL[:, q * 128:(q + 1) * 128], lhsT=Mc[:, q, :], rhs=Lc[:, q, :], start=True, stop=True)
                            for q in range(4):
                                S.i("pe", "matmul", [Lc, Mc], [bM], out=bM[:, q * 128:(q + 1) * 128], lhsT=Lc[:, q, :], rhs=Mc[:, q, :], start=True, stop=True)
                        if r >= 1:
                            if gi == 0:
                                S.i("act", "activation", [bN], [NTo], out=NTo[:].rearrange("p a b -> p (a b)"), in_=bN[:], func=AF.Copy)
                            else:
                                S.i("dve", "tensor_copy", [bN], [NTo], out=NTo[:].rearrange("p a b -> p (a b)"), in_=bN[:])
                            NTin[gi] = NTo
                        if r < 6:
                            S.i("act", "activation", [bL], [Ln], out=Ln[:].rearrange("p a b -> p (a b)"), in_=bL[:], func=AF.Copy)
                            S.i("dve", "tensor_copy", [bM], [Mn], out=Mn[:].rearrange("p a b -> p (a b)"), in_=bM[:])
                            cur[gi] = 1 - cur[gi]
                if stg < 8:
                    continue
                for gi, hg in enumerate(grp):
                    heads = [4 * hg + q for q in range(4)]
                    NTf = NTin[gi]
                    for q, h in enumerate(heads):
                        S.i("pe", "matmul", [Mak[gi], Vtm], [PB[6]], out=PB[6][:, q * 64:(q + 1) * 64], lhsT=Mak[gi][:, q, :], rhs=Vtm[:, h * 64:(h + 1) * 64], start=True, stop=True)
                    S.i("act", "activation", [PB[6]], [Xb], out=Xb[:], in_=PB[6][:, 0:256], func=AF.Copy)
                    for q, h in enumerate(heads):
                        S.i("pe", "matmul", [Xb, g.cb], [PB[6]], out=PB[6][:, 256 + q * 64:256 + (q + 1) * 64], lhsT=ident, rhs=Xb[:, q * 64:(q + 1) * 64], start=True, stop=False)
                        S.i("pe", "matmul", [NTf, Xb], [PB[6]], out=PB[6][:, 256 + q * 64:256 + (q + 1) * 64], lhsT=NTf[:, q, :], rhs=Xb[:, q * 64:(q + 1) * 64], start=False, stop=True)
                    S.i("dve", "tensor_copy", [PB[6]], [R(Vhat, hg)], out=Vhat[:, hg * 256:(hg + 1) * 256], in_=PB[6][:, 256:512])
                    bA = PB[gi]
                    for q, h in enumerate(heads):
                        ps_, c = HP(h)
                        cc = (c % 2) * 128
                        S.i("pe", "matmul", [Atm, NTf], [bA], out=bA[ps_, cc:cc + 128], lhsT=Atm[:, h * 64:(h + 1) * 64], rhs=NTf[:, q, :], start=True, stop=True)
                    S.i("dve", "tensor_tensor", [bA, FMK(0, 2 * hg), FMK(0, 2 * hg + 1)], [R(AhT, hg)], out=AhT[:, 2 * hg:2 * hg + 2, :],
                        in0=bA[:, 0:256].rearrange("p (a b) -> p a b", b=128), in1=xb[:, 2 * hg:2 * hg + 2, cs.start:cs.stop], op=ALU.add)
            if stg < 9:
                continue
            Ubv = Ub[:].rearrange("p (h n) -> p h n", n=64)
            Vhv = Vhat[:].rearrange("p (h n) -> p h n", n=64)
            for par in range(2):
                bk = PB[par]
                for hh in range(8):
                    h = 2 * hh + par
                    ps_, c = slice(par * 64, par * 64 + 64), h // 2
                    S.i("pe", "matmul", [AhT, ST], [bk], out=bk[:, hh * 64:(hh + 1) * 64], lhsT=AhT[ps_, c, :], rhs=ST[ps_, c, :], start=True, stop=True)
            for par in range(2):
                S.i("dve", "tensor_tensor", [PB[par], Vhat], [R(Ub, par)], out=Ubv[:, par::2, :], in0=PB[par][:].rearrange("p (h n) -> p h n", n=64),
                    in1=Vhv[:, par::2, :], op=ALU.add)
            for c in range(NCH):
                S.i("act", "activation", [R(ST, c), PC], [R(STs, c)], out=STs[:, c, :], in_=ST[:, c, :], func=AF.Identity, scale=PC[:, c, n:n + 1])
            for par in range(2):
                bk = PB[2 + par]
                for hh in range(8):
                    h = 2 * hh + par
                    ps_, c = slice(par * 64, par * 64 + 64), h // 2
                    o_ = bk[:, hh * 64:(hh + 1) * 64]
                    S.i("pe", "matmul", [FMK(3, c), ST], [bk], out=o_, lhsT=FM(3, c, cs)[ps_, :], rhs=ST[ps_, c, :], start=True, stop=False)
                    S.i("pe", "matmul", [Mrk, Vtm], [bk], out=o_, lhsT=Mrk[:, h, :], rhs=Vtm[:, h * 64:(h + 1) * 64], start=False, stop=False)
                    S.i("pe", "matmul", [Mrb, Ub], [bk], out=o_, lhsT=Mrb[:, h, :], rhs=Ub[:, h * 64:(h + 1) * 64], start=False, stop=True)
            for h in range(16):
                ps_, c = slice((h % 2) * 64, (h % 2) * 64 + 64), h // 2
                o_ = PB[4][ps_, c * 64:(c + 1) * 64]
                S.i("pe", "matmul", [Ktm, Vtm], [PB[4]], out=o_, lhsT=Ktm[:, h * 64:(h + 1) * 64], rhs=Vtm[:, h * 64:(h + 1) * 64], start=True, stop=False)
                S.i("pe", "matmul", [Btm, Ub], [PB[4]], out=o_, lhsT=Btm[:, h * 64:(h + 1) * 64], rhs=Ub[:, h * 64:(h + 1) * 64], start=False, stop=True)
            for c in range(NCH):
                S.i("dve", "scalar_tensor_tensor", [PB[4], PC, R(STs, c)], [R(ST, c)], out=ST[:, c, :], in0=PB[4][:, c * 64:(c + 1) * 64], scalar=PC[:, c, n:n + 1],
                    in1=STs[:, c, :], op0=ALU.mult, op1=ALU.add)
            for b in range(2):
                S.i("dve", "tensor_reduce", [PB[2 + b]], [R(st4, ("s", b))], out=st4[:, 0, b::2], in_=PB[2 + b][:].rearrange("p (h n) -> p h n", n=64), axis=AX.X, op=ALU.add)
                S.i("act", "activation", [PB[2 + b]], [R(yn, None)], out=yn[:, b * 512:(b + 1) * 512], in_=PB[2 + b][:], func=AF.Square)
                S.i("dve", "tensor_reduce", [R(yn, None)], [R(st4, ("q", b))], out=st4[:, 1, b::2], in_=yn[:, b * 512:(b + 1) * 512].rearrange("p (h n) -> p h n", n=64), axis=AX.X, op=ALU.add)
            S.i("dve", "tensor_scalar", [st4], [R(st4, "m")], out=st4[:, 2, :], in0=st4[:, 0, :], scalar1=1.0 / 64, scalar2=None, op0=ALU.mult)
            S.i("dve", "tensor_tensor", [st4], [R(st4, "v")], out=st4[:, 3, :], in0=st4[:, 2, :], in1=st4[:, 2, :], op=ALU.mult)
            S.i("dve", "scalar_tensor_tensor", [st4], [R(st4, "v")], out=st4[:, 3, :], in0=st4[:, 1, :], scalar=1.0 / 64, in1=st4[:, 3, :], op0=ALU.mult, op1=ALU.subtract)
            S.i("dve", "tensor_scalar", [st4], [R(st4, "v")], out=st4[:, 3, :], in0=st4[:, 3, :], scalar1=64e-5, scalar2=None, op0=ALU.add)
            S.i("act", "activation", [st4], [R(st4, "v")], out=st4[:, 3, :], in_=st4[:, 3, :], func=AF.Sqrt)
            S.i("dve", "reciprocal", [st4], [R(st4, "v")], out=st4[:, 3, :], in_=st4[:, 3, :])
            for c in range(NCH):
                S.i("pe", "matmul", [FMK(4, c), g.cb], [PB[5]], out=PB[5][:, 2 * c:2 * c + 2], lhsT=FM(4, c, cs), rhs=hind, start=True, stop=True)
            S.i("dve", "tensor_copy", [PB[5]], [bsum], out=bsum[:], in_=PB[5][:, 0:16])
            for h in range(16):
                b = h % 2
                hs = slice(h * 64, (h + 1) * 64)
                S.i("dve", "tensor_scalar", [PB[2 + b], st4], [R(yn, h)], out=yn[:, hs], in0=PB[2 + b][:, (h // 2) * 64:(h // 2 + 1) * 64],
                    scalar1=st4[:, 2, h:h + 1], scalar2=st4[:, 3, h:h + 1], op0=ALU.subtract, op1=ALU.mult)
            S.i("dve", "tensor_tensor", [yn, rows], [yn], out=yn[:], in0=yn[:], in1=rows[:, 0, :], op=ALU.mult)
            S.i("dve", "tensor_tensor", [yn, rows], [yn], out=yn[:], in0=yn[:], in1=rows[:, 1, :], op=ALU.add)
            for h in range(16):
                hs = slice(h * 64, (h + 1) * 64)
                S.i("dve", "scalar_tensor_tensor", [Vtm, bsum, yn], [R(yn, h)], out=yn[:, hs], in0=Vtm[:, hs], scalar=bsum[:, h:h + 1], in1=yn[:, hs], op0=ALU.mult, op1=ALU.add)
            for b in range(2):
                S.i("pe", "matmul", [sg, g2], [PB[b]], out=PB[b][:], lhsT=sg[:, 0, cs], rhs=g2[:, 0, b * 512:(b + 1) * 512], start=True, stop=False)
                S.i("pe", "matmul", [sg, g2], [PB[b]], out=PB[b][:], lhsT=sg[0:32, 1, cs], rhs=g2[0:32, 1, b * 512:(b + 1) * 512], start=False, stop=True)
                S.i("dve", "tensor_tensor", [yn, PB[b]], [R(ogtm, b)], out=ogtm[:, b * 512:(b + 1) * 512], in0=yn[:, b * 512:(b + 1) * 512], in1=PB[b][:], op=ALU.mult)
            for c in range(NCH):
                S.i("pe", "transpose", [ogtm, g.cb], [pT], out=pT[:, c * 128:(c + 1) * 128], in_=ogtm[:, c * 128:(c + 1) * 128], identity=ident)
            for c in range(NCH):
                S.i("act", "activation", [pT], [R(ogT, (c, n))], out=ogT[:, c, cs], in_=pT[:, c * 128:(c + 1) * 128], func=AF.Copy)
        if stg < 11:
            continue
        for m in range(NCH):
            wo = wsl[nws % 3]; nws += 1
            S.dma("pool", wo[:], Wo[:, :, m * 128:(m + 1) * 128], writes=[wo])
            bk = PB[5 + (m % 2)]
            for k in range(NCH):
                S.i("pe", "matmul", [wo, ogT], [bk], out=bk[:, 0:TW], lhsT=wo[:, k, :], rhs=ogT[:, k, :], start=(k == 0), stop=(k == NCH - 1))
            xs = x[:, m, t0:t0 + TW]
            S.i("dve", "scalar_tensor_tensor", [bk, R(x, (m, tt))], [R(x, (m, tt))], out=xs, in0=xs, scalar=ALPHA, in1=bk[:, 0:TW], op0=ALU.mult, op1=ALU.add)
    S.pop()
    S.push()
    layer_norm(g, ("ln1_g%d" % i, "ln1_b%d" % i), range(4))
    S.pop()

def make_consts():
    c = np.zeros((128, 1024), np.float32)
    c[:, 0:128] = 1.0
    c[:, 128:256] = np.eye(128, dtype=np.float32)
    bo = np.zeros((128, 128), np.float32); bo[:64, :64] = 1.0; bo[64:, 64:] = 1.0
    c[:, 256:384] = bo
    t = np.arange(128)
    c[:, 384:512] = np.where(t[None, :] <= t[:, None], 0.0, -30000.0)
    c[:, 512:640] = (t[:, None] < t[None, :]).astype(np.float32)
    c[:, 640:768] = (t[:, None] <= t[None, :]).astype(np.float32)
    c[:64, 768] = 1.0; c[64:, 769] = 1.0
    return c


def make_ret_tables():
    dk = 256
    inv = (1.0 / (np.float32(10000.0) ** np.linspace(0.0, 1.0, dk // 2, dtype=np.float32))).astype(np.float32)
    pos = np.arange(T_SEQ, dtype=np.float32)
    ang = (pos[:, None] * inv[None, :]).astype(np.float32)
    cos = np.cos(ang).astype(np.float32); sin = np.sin(ang).astype(np.float32)
    f = np.arange(dk)
    cosf = cos[:, f // 2].T
    sgn = np.where(f % 2 == 0, -1.0, 1.0).astype(np.float32)
    sinf = (sin[:, f // 2].T * sgn[:, None]).astype(np.float32)
    tab = np.stack([cosf.reshape(2, 128, T_SEQ).transpose(1, 0, 2), sinf.reshape(2, 128, T_SEQ).transpose(1, 0, 2)])
    m = np.zeros((4, 128, 384), np.float32)
    idx = np.arange(128, dtype=np.float64)
    for h in range(4):
        lg = np.log(1.0 - 2.0 ** (-5.0 - h))
        rel = idx[None, :] - idx[:, None]
        m[h, :, 0:128] = np.where(rel >= 0, np.exp(lg * np.maximum(rel, 0)), 0.0) / 16.0
        m[h, :, 128:256] = np.exp(lg * (idx + 1.0))[None, :]
        m[h, :, 256] = np.exp(lg * (127.0 - idx)) / 16.0
        m[h, :, 257] = np.exp(lg * 128.0)
    return np.ascontiguousarray(tab.astype(np.float32)), m


FULL_PLAN = [("rwkv", 0, 0), ("ffn", 0), ("ret", 1), ("ffn", 1), ("moba", 2), ("ffn", 2), ("rwkv", 3, 1), ("ffn", 3)]
_PLAN = FULL_PLAN
_NC_CACHE = {}


def host_inputs(inputs):
    shared = {k: np.ascontiguousarray(np.asarray(inputs[k], np.float32)) for k in WEIGHT_SHAPES}
    shared["vecs"] = pack_vecs(inputs)
    shared["consts"] = make_consts()
    rows = np.zeros((4, 128, 1024), np.float32)
    for j in range(2):
        rows[2 * j] = np.broadcast_to(np.asarray(inputs["rwkv_gn_g"][j], np.float32)[None, :], (128, 1024))
        rows[2 * j + 1] = np.broadcast_to(np.asarray(inputs["rwkv_gn_b"][j], np.float32)[None, :], (128, 1024))
    shared["rows"] = rows
    ti = np.arange(128)
    mk = np.zeros((128, 3, 512), np.float32)
    mk[:, 0, :] = np.tile((ti[:, None] < ti[None, :]).astype(np.float32), (1, 4))
    mk[:, 1, :] = np.tile((ti[:, None] <= ti[None, :]).astype(np.float32), (1, 4))
    mk[:, 2, :] = np.tile((ti[:, None] > ti[None, :]).astype(np.float32), (1, 4))
    shared["mk"] = mk
    tab, msk = make_ret_tables()
    shared["rtab"] = tab
    shared["rmask"] = msk
    wqk = np.asarray(inputs["ret_w_in"][0][:, :2048], np.float32)
    shared["ret_sw"] = np.ascontiguousarray(wqk.reshape(1024, 1024, 2)[:, :, ::-1].reshape(1024, 2048))
    return shared


def run_plan(plan, inputs, x_full, n_cores=8):
    key = repr(plan)
    if key not in _NC_CACHE:
        _NC_CACHE[key] = build_program(plan)
    nc = _NC_CACHE[key]
    shared = host_inputs(inputs)
    in_maps = []
    for b in range(n_cores):
        m = dict(shared)
        m["xT"] = np.ascontiguousarray(np.asarray(x_full[b], np.float32).T)
        in_maps.append(m)
    res = run_bass_kernel_spmd(nc, in_maps, core_ids=list(range(n_cores)))
    return np.stack([np.ascontiguousarray(r["outT"].T) for r in res.results]).astype(np.float32)


def kernel(**inputs):
    return run_plan(_PLAN, inputs, inputs["x"], 8)
```

```python
import numpy as np
import concourse.bass as bass
import concourse.mybir as mybir
from concourse.bass_utils import run_bass_kernel_spmd
from contextlib import ExitStack

_DBG = {}
F32 = mybir.dt.float32
BF16 = mybir.dt.bfloat16
AF = mybir.ActivationFunctionType
ALU = mybir.AluOpType
AX = mybir.AxisListType

ENGS = ("pe", "act", "dve", "pool", "sp")


class T:
    def __init__(self, S, t, name):
        self.S = S
        self.t = t
        self.name = name
        self.st = {}
        self.is_psum = False
        self.dsem = None
        self.dcnt = 0

    def __getitem__(self, idx):
        return self.t[idx]


class R:
    __slots__ = ("tile", "key")
    def __init__(self, tile, key=None):
        self.tile = tile
        self.key = key


class Sched:
    def __init__(self, nc):
        self.nc = nc
        self.es = ExitStack()
        self.ins = {e: [] for e in ENGS}
        self.cnt = {e: 0 for e in ENGS}
        self.seen = {e: {} for e in ENGS}
        self.sems = {}
        self.needed = {e: set() for e in ENGS}
        self.nsem = 0
        for e in ENGS:
            self.sems[("e", e)] = self.es.enter_context(nc.semaphore("sem_" + e))
        self.dma_pool = []
        self.out_dma = []
        self.pend = {}
        self.scopes = []

    def _es(self):
        return self.scopes[-1] if self.scopes else self.es

    def push(self):
        self.scopes.append(ExitStack())

    def pop(self):
        self.barrier()
        self.scopes.pop().close()

    def sbuf(self, name, shape, dt):
        self.uid = getattr(self, "uid", 0) + 1
        name = "%s_%d" % (name, self.uid)
        t = self._es().enter_context(self.nc.sbuf_tensor(name, list(shape), dt))
        return T(self, t, name)

    def psum(self, name, shape, dt=F32):
        self.uid = getattr(self, "uid", 0) + 1
        name = "%s_%d" % (name, self.uid)
        t = self._es().enter_context(self.nc.psum_tensor(name, list(shape), dt))
        tt = T(self, t, name)
        tt.is_psum = True
        return tt

    def new_dsem(self, name):
        self.nsem += 1
        key = ("d", self.nsem)
        self.sems[key] = self.es.enter_context(self.nc.semaphore("dsem%d_%s" % (self.nsem, name)))
        return key

    def _states(self, ref, create=True):
        tile, key = ref.tile, ref.key
        if key is None:
            if None not in tile.st:
                tile.st[None] = [None, {}]
            return list(tile.st.values())
        out = []
        if None in tile.st:
            out.append(tile.st[None])
        if key not in tile.st:
            tile.st[key] = [None, {}]
        out.append(tile.st[key])
        return out

    def _collect(self, eng, reads, writes):
        waits = {}
        def need(w, same_ok=False):
            if w is None:
                return
            k, v = w
            if same_ok and k == ("e", eng) and eng == "pe":
                return
            if waits.get(k, 0) < v:
                waits[k] = v
        for r in reads:
            for st in self._states(r):
                need(st[0])
                if r.tile.is_psum:
                    for rk, rv in st[1].items():
                        if rk != ("e", eng):
                            need((rk, rv))
        for w in writes:
            for st in self._states(w):
                need(st[0])
                for rk, rv in st[1].items():
                    if rk == ("e", eng) and eng == "pe":
                        continue
                    need((rk, rv))
        final = []
        for k, v in waits.items():
            if k == ("e", eng) and eng == "pe":
                continue
            if self.seen[eng].get(k, 0) < v:
                self.seen[eng][k] = v
                final.append((k, v))
                if k[0] == "e":
                    self.needed[k[1]].add(v)
        return final

    def _commit(self, tag, reads, writes):
        for r in reads:
            if r.key is None:
                for k, s in r.tile.st.items():
                    if s[1].get(tag[0], 0) < tag[1]:
                        s[1][tag[0]] = tag[1]
            else:
                s = r.tile.st[r.key]
                if s[1].get(tag[0], 0) < tag[1]:
                    s[1][tag[0]] = tag[1]
        for w in writes:
            if w.key is None:
                w.tile.st = {None: [tag, {}]}
            else:
                w.tile.st[w.key] = [tag, {}]

    def i(self, eng, meth, reads=(), writes=(), **kw):
        return self.op(eng, (meth, kw), reads, writes)

    def _norm(self, refs):
        out = []
        for r in refs:
            if not isinstance(r, R):
                r = R(r)
            if r.tile.is_psum and r.key is not None:
                r = R(r.tile)
            out.append(r)
        return out

    def op(self, eng, fn, reads=(), writes=()):
        reads = self._norm(reads)
        writes = self._norm(writes)
        waits = self._collect(eng, reads, writes)
        self.cnt[eng] += 1
        idx = self.cnt[eng]
        self.ins[eng].append([fn, waits, idx, None])
        self._commit((("e", eng), idx), reads, writes)
        return idx

    def dma(self, q, out_ap, in_ap, reads=(), writes=(), out_final=False, owner=None, **kw):
        reads = [r if isinstance(r, R) else R(r) for r in reads]
        writes = [w if isinstance(w, R) else R(w) for w in writes]
        waits = self._collect(q, reads, writes)
        if owner is None:
            owner = (writes[0] if writes else reads[0]).tile
        if owner.dsem is None:
            owner.dsem = self.new_dsem(owner.name)
        owner.dcnt += 16
        tag = (owner.dsem, owner.dcnt)
        self.cnt[q] += 1
        idx = self.cnt[q]
        self.ins[q].append([lambda e: e.dma_start(out=out_ap, in_=in_ap, **kw), waits, idx, tag])
        self._commit(tag, reads, writes)
        self.pend[tag[0]] = tag[1]
        if out_final:
            self.out_dma.append(tag)
        return tag

    def barrier(self):
        for f in ENGS:
            if self.cnt[f] and (self.ins[f][-1][0] is None or self.ins[f][-1][3] is not None):
                self.cnt[f] += 1
                self.ins[f].append([None, [], self.cnt[f], None])
        for e in ENGS:
            waits = []
            for f in ENGS:
                if f == e or self.cnt[f] == 0:
                    continue
                v = self.cnt[f]
                if self.seen[e].get(("e", f), 0) < v:
                    self.seen[e][("e", f)] = v
                    waits.append((("e", f), v))
                    self.needed[f].add(v)
            for k, v in self.pend.items():
                if self.seen[e].get(k, 0) < v:
                    self.seen[e][k] = v
                    waits.append((k, v))
            if waits:
                self.cnt[e] += 1
                self.ins[e].append([None, waits, self.cnt[e], None])

    def emit(self):
        nc = self.nc
        fwd = {}
        for k, v in self.out_dma:
            fwd[k] = max(fwd.get(k, 0), v)
        fw = list(fwd.items())
        self.cnt["sp"] += 1
        self.ins["sp"].append([None, fw, self.cnt["sp"], None])
        rank = {}
        for e in ENGS:
            s = sorted(self.needed[e])
            rank[e] = {v: i + 1 for i, v in enumerate(s)}
        engmap = {"pe": "tensor", "act": "scalar", "dve": "vector", "pool": "gpsimd", "sp": "sync"}
        with nc.Block() as block:
            for e in ENGS:
                lst = self.ins[e]
                if not lst:
                    continue
                def body(eng, e=e, lst=lst):
                    for fn, waits, idx, dtag in lst:
                        for k, v in waits:
                            if k[0] == "e":
                                eng.wait_ge(self.sems[k], rank[k[1]][v])
                            else:
                                eng.wait_ge(self.sems[k], v)
                        if fn is None:
                            if idx in self.needed[e]:
                                eng.nop().then_inc(self.sems[("e", e)], 1)
                            continue
                        ins = fn(eng) if callable(fn) else getattr(eng, fn[0])(**fn[1])
                        if dtag is not None:
                            ins.then_inc(self.sems[dtag[0]], 16)
                            if idx in self.needed[e]:
                                raise RuntimeError("dma instr needed as engine milestone")
                        elif idx in self.needed[e]:
                            ins.then_inc(self.sems[("e", e)], 1)
                getattr(block, engmap[e])(body)
        self.es.close()

D = 1024; T_SEQ = 2048; DEPTH = 4; FF = 2816; NCH = 8; NPAIR = 22
ALPHA = float((2 * DEPTH) ** 0.25)
LN_EPS = 1e-5

WEIGHT_SHAPES = {
    "rwkv_w_rkv": [2, 3, 1024, 1024], "rwkv_w1": [2, 1024, 64], "rwkv_w2": [2, 64, 1024],
    "rwkv_a1": [2, 1024, 64], "rwkv_a2": [2, 64, 1024], "rwkv_g1": [2, 1024, 160], "rwkv_g2": [2, 160, 1024],
    "rwkv_w_o": [2, 1024, 1024], "ret_w_in": [1, 1024, 6144], "ret_w_o": [1, 2048, 1024],
    "moba_w_qkv": [1, 1024, 3072], "moba_w_o": [1, 1024, 1024],
    "ffn_w_in": [4, 1024, 5632], "ffn_w_out": [4, 2816, 1024],
}


def _fm(v):
    v = np.asarray(v, np.float32).reshape(-1, 128)
    return np.ascontiguousarray(v.T)


def vec_layout():
    off = {}
    n = 0
    def add(name, cols):
        nonlocal n
        off[name] = n
        n += cols
    for i in range(4):
        for nm in ("ln1_g", "ln1_b", "ln2_g", "ln2_b"):
            add("%s%d" % (nm, i), 8)
        for k in range(3):
            add("cw%d_%d" % (k, i), 44)
        add("cb_%d" % i, 44)
    for j in range(2):
        for m in range(6):
            add("mix%d_%d" % (m, j), 8)
        for nm in ("w0", "a0", "k_k", "k_a", "r_k"):
            add("%s_%d" % (nm, j), 8)
    add("ret_gn_g", 16)
    add("ret_gn_b", 16)
    return off, n


def pack_vecs(inp):
    off, n = vec_layout()
    out = np.zeros((128, n), np.float32)
    def put(name, v):
        a = _fm(v)
        out[:, off[name]:off[name] + a.shape[1]] = a
    for i in range(4):
        for nm in ("ln1_g", "ln1_b", "ln2_g", "ln2_b"):
            put("%s%d" % (nm, i), inp[nm][i])
        for k in range(3):
            put("cw%d_%d" % (k, i), inp["ffn_conv_w"][i, k])
        put("cb_%d" % i, inp["ffn_conv_b"][i])
    for j in range(2):
        for m in range(6):
            put("mix%d_%d" % (m, j), inp["rwkv_mix"][j, m])
        put("w0_%d" % j, inp["rwkv_w0"][j]); put("a0_%d" % j, inp["rwkv_a0"][j])
        put("k_k_%d" % j, inp["rwkv_k_k"][j]); put("k_a_%d" % j, inp["rwkv_k_a"][j])
        put("r_k_%d" % j, inp["rwkv_r_k"][j].reshape(-1))
    put("ret_gn_g", inp["ret_gn_g"][0]); put("ret_gn_b", inp["ret_gn_b"][0])
    return out


class Ctx:
    pass


def build_program(plan, x_in_bf=True):
    nc = bass.Bass("TRN2", target_bir_lowering=False)
    S = Sched(nc)
    g = Ctx()
    g.nc, g.S = nc, S
    g.W = {k: nc.dram_tensor(k, shp, F32, kind="ExternalInput").ap() for k, shp in WEIGHT_SHAPES.items()}
    voff, nv = vec_layout()
    g.voff = voff
    xT_d = nc.dram_tensor("xT", [D, T_SEQ], F32, kind="ExternalInput").ap()
    vecs_d = nc.dram_tensor("vecs", [128, nv], F32, kind="ExternalInput").ap()
    consts_d = nc.dram_tensor("consts", [128, 1024], F32, kind="ExternalInput").ap()
    g.rows_d = nc.dram_tensor("rows", [4, 128, 1024], F32, kind="ExternalInput").ap()
    g.rtab_d = nc.dram_tensor("rtab", [2, 128, 2, T_SEQ], F32, kind="ExternalInput").ap()
    g.rmask_d = nc.dram_tensor("rmask", [4, 128, 384], F32, kind="ExternalInput").ap()
    g.ret_sw_d = nc.dram_tensor("ret_sw", [1024, 2048], F32, kind="ExternalInput").ap()
    g.mk_d = nc.dram_tensor("mk", [128, 3, 512], F32, kind="ExternalInput").ap()
    out_d = nc.dram_tensor("outT", [D, T_SEQ], F32, kind="ExternalOutput").ap()

    g.x = S.sbuf("x", [128, NCH, T_SEQ], F32)
    g.xb = S.sbuf("xb", [128, NCH, T_SEQ], BF16)
    g.vecs = S.sbuf("vecs", [128, nv], F32)
    g.cf = S.sbuf("cf", [128, 256], F32)
    g.cb = S.sbuf("cb", [128, 512], BF16)
    xv = xT_d.rearrange("(c p) t -> p c t", p=128)
    for c in range(NCH):
        S.dma("sp", g.x[:, c, :], xv[:, c, :], writes=[R(g.x, None)])
    if plan[0][0] != "rwkv":
        for c in range(NCH):
            S.dma("pool", g.xb[:, c, :], xv[:, c, :], writes=[R(g.xb, None)])
    S.dma("sp", g.vecs[:], vecs_d, writes=[g.vecs])
    S.dma("sp", g.cf[:, 0:128], consts_d[:, 0:128], writes=[R(g.cf, None)])
    S.dma("sp", g.cf[:, 128:256], consts_d[:, 384:512], writes=[R(g.cf, None)])
    S.i("pool", "memset", [], [g.cb], ap=g.cb[:], constant=0.0)
    S.dma("pool", g.cb[:, 0:256], consts_d[:, 128:384], writes=[R(g.cb, None)])
    S.dma("pool", g.cb[:, 256:258], consts_d[:, 768:770], writes=[R(g.cb, None)])
    g.mk = S.sbuf("mk", [128, 3, 512], BF16)
    S.dma("pool", g.mk[:], g.mk_d, writes=[g.mk])
    g.ones32 = g.cf[:, 0:128]
    g.identb = g.cb[:, 0:128]

    for step in plan:
        kind = step[0]
        if kind == "ffn":
            ffn_phase(g, step[1])
        elif kind == "ln":
            S.push()
            layer_norm(g, step[1], range(4))
            S.pop()
        elif kind == "moba":
            moba_phase(g, step[1])
        elif kind == "ret":
            ret_phase(g, step[1])
        elif kind == "rwkv":
            rwkv_phase(g, step[1], step[2])
        else:
            raise ValueError(kind)

    S.barrier()
    ov = out_d.rearrange("(c p) t -> p c t", p=128)
    outsem = T(S, None, "outsem")
    for c in range(NCH):
        S.dma("sp", ov[:, c, :], g.x[:, c, :], reads=[R(g.x, None)], out_final=True)
    S.emit()
    return nc


def vcol(g, name, c=0):
    o = g.voff[name] + c
    return g.vecs[:, o:o + 1]


def layer_norm(g, pref, tts, width=512, write_xb=True):
    S = g.S
    gname, bname = pref
    x, xb = g.x, g.xb
    sq = [S.sbuf("lnsq", [128, 512], F32) for _ in range(2)]
    means = [S.sbuf("lnmean", [128, 512], F32) for _ in range(2)]
    msqs = [S.sbuf("lnmsq", [128, 512], F32) for _ in range(2)]
    rstds = [S.sbuf("lnrstd", [128, 512], F32) for _ in range(2)]
    tmp = [S.sbuf("lntmp", [128, 512], F32) for _ in range(3)]
    ps_ss = [S.psum("lnps", [128, 512]) for _ in range(2)]
    ps_qs = [S.psum("lnpq", [128, 512]) for _ in range(2)]
    ones = g.ones32
    W_ = width
    for tix in tts:
        ts = slice(tix * W_, (tix + 1) * W_)
        tt = (tix * W_) // 512
        mean, msq, rstd, ps_s, ps_q = means[tix % 2], msqs[tix % 2], rstds[tix % 2], ps_ss[tix % 2], ps_qs[tix % 2]
        for c in range(NCH):
            q = sq[c % 2]
            S.i("act", "activation", [R(x, (c, tt))], [q], out=q[:, 0:W_], in_=x[:, c, ts], func=AF.Square)
            S.i("pe", "matmul", [R(x, (c, tt)), g.cf], [ps_s], out=ps_s[:, 0:W_], lhsT=ones, rhs=x[:, c, ts], start=(c == 0), stop=(c == NCH - 1))
            S.i("pe", "matmul", [q, g.cf], [ps_q], out=ps_q[:, 0:W_], lhsT=ones, rhs=q[:, 0:W_], start=(c == 0), stop=(c == NCH - 1))
        S.i("dve", "tensor_scalar", [ps_s], [mean], out=mean[:, 0:W_], in0=ps_s[:, 0:W_], scalar1=1.0 / D, scalar2=None, op0=ALU.mult)
        S.i("dve", "tensor_tensor", [mean], [msq], out=msq[:, 0:W_], in0=mean[:, 0:W_], in1=mean[:, 0:W_], op=ALU.mult)
        S.i("dve", "scalar_tensor_tensor", [ps_q, msq], [msq], out=msq[:, 0:W_], in0=ps_q[:, 0:W_], scalar=1.0 / D, in1=msq[:, 0:W_], op0=ALU.mult, op1=ALU.subtract)
        S.i("dve", "tensor_scalar", [msq], [msq], out=msq[:, 0:W_], in0=msq[:, 0:W_], scalar1=LN_EPS, scalar2=None, op0=ALU.add)
        S.i("act", "activation", [msq], [rstd], out=rstd[:, 0:W_], in_=msq[:, 0:W_], func=AF.Ln)
        S.i("act", "activation", [rstd], [rstd], out=rstd[:, 0:W_], in_=rstd[:, 0:W_], func=AF.Exp, scale=-0.5)
        for c in range(NCH):
            tm = tmp[c % 3]
            S.i("dve", "tensor_tensor", [R(x, (c, tt)), mean], [tm], out=tm[:, 0:W_], in0=x[:, c, ts], in1=mean[:, 0:W_], op=ALU.subtract)
            S.i("dve", "tensor_tensor", [tm, rstd], [tm], out=tm[:, 0:W_], in0=tm[:, 0:W_], in1=rstd[:, 0:W_], op=ALU.mult)
            S.i("act", "activation", [tm, g.vecs], [R(x, (c, tt))], out=x[:, c, ts], in_=tm[:, 0:W_], func=AF.Identity,
                bias=vcol(g, bname, c), scale=vcol(g, gname, c))
            if write_xb:
                S.i("dve", "tensor_scalar", [tm, g.vecs], [R(xb, (c, tt))], out=xb[:, c, ts], in0=tm[:, 0:W_], scalar1=vcol(g, gname, c),
                    scalar2=vcol(g, bname, c), op0=ALU.mult, op1=ALU.add)


def ffn_phase(g, i):
    S = g.S
    x, xb = g.x, g.xb
    S.push()
    W_in = g.W["ffn_w_in"][i].rearrange("(k p) (two f) -> p k two f", p=128, two=2)
    W_out = g.W["ffn_w_out"][i].rearrange("(k p) n -> p k n", p=128)
    TB = 1024
    a = S.sbuf("ffa", [128, NPAIR, TB], BF16)
    hus = [S.sbuf("hu", [128, TB + 2], F32) for _ in range(2)]
    hgs = [S.sbuf("hg", [128, TB + 2], F32) for _ in range(2)]
    aus = [S.sbuf("au", [128, TB], F32) for _ in range(2)]
    ags = [S.sbuf("ag", [128, TB], F32) for _ in range(2)]
    halo = S.sbuf("halo", [128, 2 * NPAIR, 2], F32)
    wps = [[S.sbuf("wp", [128, NCH, 128], BF16) for _ in range(2)] for _ in range(2)]
    wos = [S.sbuf("wo", [128, NPAIR, 128], BF16) for _ in range(2)]
    pb = [S.psum("ffps", [128, 512]) for _ in range(8)]
    cw = lambda k, j: vcol(g, "cw%d_%d" % (k, i), j)
    cbv = lambda j: vcol(g, "cb_%d" % i, j)
    nw = 0
    for blk in range(2):
        for j in range(NPAIR):
            if j == 0:
                for ug in range(2):
                    S.dma("pool", wps[nw % 2][ug][:], W_in[:, :, ug, 0:128], writes=[wps[nw % 2][ug]])
            wp = wps[nw % 2]; nw += 1
            hu, hg, au, ag = hus[j % 2], hgs[j % 2], aus[j % 2], ags[j % 2]
            if j + 1 < NPAIR:
                for ug in range(2):
                    S.dma("pool", wps[nw % 2][ug][:], W_in[:, :, ug, (j + 1) * 128:(j + 2) * 128], writes=[wps[nw % 2][ug]])
            elif True:
                S.dma("pool", wos[0][:], W_out[:, :, 0:128], writes=[wos[0]])
            banks = pb[(j % 2) * 4:(j % 2) * 4 + 4]
            for ug in range(2):
                for h in range(2):
                    bk = banks[ug * 2 + h]
                    tt = blk * 2 + h
                    for k in range(NCH):
                        S.i("pe", "matmul", [wp[ug], R(xb, (k, tt))], [bk], out=bk[:], lhsT=wp[ug][:, k, :],
                            rhs=xb[:, k, tt * 512:(tt + 1) * 512], start=(k == 0), stop=(k == NCH - 1))
            for ug, hb in ((0, hu), (1, hg)):
                for h in range(2):
                    bk = banks[ug * 2 + h]
                    S.i("act", "activation", [bk], [R(hb, h)], out=hb[:, 2 + h * 512:2 + (h + 1) * 512], in_=bk[:], func=AF.Copy)
                if blk == 0:
                    S.i("dve", "memset", [], [R(hb, "halo")], ap=hb[:, 0:2], constant=0.0)
                else:
                    S.i("dve", "tensor_copy", [R(halo, (ug, j))], [R(hb, "halo")], out=hb[:, 0:2], in_=halo[:, ug * NPAIR + j, :])
            for (hb, ab_, jj) in ((hu, au, j), (hg, ag, NPAIR + j)):
                S.i("act", "activation", [hb, g.vecs], [ab_], out=ab_[:], in_=hb[:, 2:TB + 2], func=AF.Identity, bias=cbv(jj), scale=cw(2, jj))
                S.i("dve", "scalar_tensor_tensor", [hb, ab_, g.vecs], [ab_], out=ab_[:], in0=hb[:, 1:TB + 1], scalar=cw(1, jj), in1=ab_[:], op0=ALU.mult, op1=ALU.add)
                S.i("dve", "scalar_tensor_tensor", [hb, ab_, g.vecs], [ab_], out=ab_[:], in0=hb[:, 0:TB], scalar=cw(0, jj), in1=ab_[:], op0=ALU.mult, op1=ALU.add)
            S.i("act", "activation", [ag], [ag], out=ag[:], in_=ag[:], func=AF.Silu)
            S.i("dve", "tensor_tensor", [au, ag], [R(a, j)], out=a[:, j, :], in0=au[:], in1=ag[:], op=ALU.mult)
            if blk == 0:
                for ug, hb in ((0, hu), (1, hg)):
                    S.i("dve", "tensor_copy", [hb], [R(halo, (ug, j))], out=halo[:, ug * NPAIR + j, :], in_=hb[:, TB:TB + 2])
        for m in range(NCH):
            wo = wos[m % 2]
            if m + 1 < NCH:
                S.dma("pool", wos[(m + 1) % 2][:], W_out[:, :, (m + 1) * 128:(m + 2) * 128], writes=[wos[(m + 1) % 2]])
            for h in range(2):
                bk = pb[(m * 2 + h) % 8]
                tt = blk * 2 + h
                for k in range(NPAIR):
                    S.i("pe", "matmul", [wo, R(a, k)], [bk], out=bk[:], lhsT=wo[:, k, :], rhs=a[:, k, h * 512:(h + 1) * 512],
                        start=(k == 0), stop=(k == NPAIR - 1))
                xs = x[:, m, tt * 512:(tt + 1) * 512]
                S.i("dve", "scalar_tensor_tensor", [bk, R(x, (m, tt))], [R(x, (m, tt))], out=xs, in0=xs, scalar=ALPHA, in1=bk[:],
                    op0=ALU.mult, op1=ALU.add)
    S.pop()
    S.push()
    layer_norm(g, ("ln2_g%d" % i, "ln2_b%d" % i), range(4))
    S.pop()

def moba_phase(g, i):
    S = g.S
    x, xb = g.x, g.xb
    S.push()
    Wqkv = g.W["moba_w_qkv"][0].rearrange("(k p) n -> p k n", p=128)
    Wo = g.W["moba_w_o"][0].rearrange("(k p) n -> p k n", p=128)
    ogT = S.sbuf("ogT", [128, NCH, T_SEQ], BF16)
    wsl = [[S.sbuf("mw", [128, NCH, 128], BF16) for _ in range(3)] for _ in range(2)]
    qkv = [[S.sbuf("mqkv", [128, T_SEQ], BF16) for _ in range(3)] for _ in range(2)]
    vtms = [S.sbuf("vtm", [128, 16, 128], BF16) for _ in range(2)]
    ksum = S.sbuf("ksum", [128, 8], F32)
    kmean = S.sbuf("kmean", [128, 8], BF16)
    P = S.sbuf("mP", [128, T_SEQ], BF16)
    PT = S.sbuf("mPT", [128, 16, 128], BF16)
    sd = S.sbuf("msd", [128, 128], F32)
    g8 = S.sbuf("g8", [128, 8], F32)
    top8 = S.sbuf("top8", [128, 8], F32)
    mb = S.sbuf("mb", [128, 8], F32)
    b8 = S.sbuf("b8", [128, 8], F32)
    nm = S.sbuf("nm", [128, 4], F32)
    rs = S.sbuf("rs", [128, 12], F32)
    rinv = S.sbuf("rinv", [128, 2], F32)
    otm = S.sbuf("otm", [128, 128], BF16)
    wos = [S.sbuf("mwo", [128, NCH, 128], BF16) for _ in range(2)]
    sc = S.psum("msc", [128, 2048])
    pT = S.psum("mpT", [128, 1024], BF16)
    gps = S.psum("mgps", [128, 512])
    ov = S.psum("mov", [128, 512])
    pj = S.psum("mpj", [128, 512])
    tri = g.cf[:, 128:256]
    if _DBG.get('dbg_memset'):
        S.i('pool', 'memset', [], [ogT], ap=ogT[:], constant=0.0)
    ident = g.identb
    for c in range(_DBG.get('moba_pairs', NCH)):
        ws = wsl[c % 2]
        qT, kT, vT = qkv[c % 2]
        vtm = vtms[c % 2]
        for w in range(3):
            S.dma("pool", ws[w][:], Wqkv[:, :, w * 1024 + c * 128: w * 1024 + (c + 1) * 128], writes=[ws[w]])
        for w, dst in ((0, qT), (1, kT), (2, vT)):
            for tt in range(4):
                for k in range(NCH):
                    S.i("pe", "matmul", [ws[w], R(xb, (k, tt))], [pj], out=pj[:], lhsT=ws[w][:, k, :], rhs=xb[:, k, tt * 512:(tt + 1) * 512],
                        start=(k == 0), stop=(k == NCH - 1))
                if w == 0:
                    S.i("act", "activation", [pj], [R(dst, tt)], out=dst[:, tt * 512:(tt + 1) * 512], in_=pj[:], func=AF.Copy, scale=0.125)
                else:
                    S.i("act", "activation", [pj], [R(dst, tt)], out=dst[:, tt * 512:(tt + 1) * 512], in_=pj[:], func=AF.Copy)
                if w == 1 and _DBG.get('moba_stage', 9) >= 0.2:
                    S.i("dve", "tensor_reduce", [pj], [R(ksum, tt)], out=ksum[:, 2 * tt:2 * tt + 2],
                        in_=pj[:].rearrange("p (b k) -> p b k", b=2), axis=AX.X, op=ALU.add)
        if _DBG.get('moba_stage', 9) >= 0.2:
            S.i("dve", "tensor_scalar", [ksum], [kmean], out=kmean[:], in0=ksum[:], scalar1=1.0 / 256.0, scalar2=None, op0=ALU.mult)
        for b in range(2 if _DBG.get('moba_stage', 9) >= 0.3 else 0):
            for t8 in range(8):
                kt = b * 8 + t8
                S.i("pe", "transpose", [vT, g.cb], [pT], out=pT[:, t8 * 128:(t8 + 1) * 128], in_=vT[:, kt * 128:(kt + 1) * 128], identity=ident)
            S.i("dve", "tensor_copy", [pT], [R(vtm, b)], out=vtm[:, b * 8:(b + 1) * 8, :].rearrange("p a b -> p (a b)"), in_=pT[:])
        stg = _DBG.get('moba_stage', 9)
        for qt in _DBG.get('moba_qts', range(16)):
            if stg < 2:
                break
            qb = qt // 2
            nkt = qt + 1
            qs = slice(qt * 128, (qt + 1) * 128)
            for hh in range(2):
                ps = slice(hh * 64, (hh + 1) * 64)
                use_thr = qb >= 4
                if use_thr:
                    S.i("pe", "matmul", [qT, kmean], [gps], out=gps[:, 0:8], lhsT=qT[ps, qs], rhs=kmean[ps, 0:8], start=True, stop=True)
                    S.i("pool", "memset", [], [R(g8, "pad")], ap=g8[:, qb:8], constant=-1.0e30)
                    S.i("dve", "tensor_copy", [gps], [R(g8, "val")], out=g8[:, 0:qb], in_=gps[:, 0:qb])
                    S.i("dve", "max", [g8], [top8], out=top8[:], in_=g8[:])
                    S.i("dve", "tensor_scalar", [g8, top8], [mb], out=mb[:], in0=g8[:], scalar1=top8[:, 2:3], scalar2=30000.0,
                        op0=ALU.is_ge, op1=ALU.mult)
                ncol = nkt * 128
                for b in range((ncol + 511) // 512):
                    w_ = min(512, ncol - b * 512)
                    S.i("pe", "matmul", [qT, kT], [sc], out=sc[:, b * 512:b * 512 + w_], lhsT=qT[ps, qs], rhs=kT[ps, b * 512:b * 512 + w_],
                        start=True, stop=True)
                S.i("dve", "tensor_tensor", [sc, g.cf], [sd], out=sd[:], in0=sc[:, qt * 128:(qt + 1) * 128], in1=tri, op=ALU.add)
                S.i("dve", "tensor_reduce", [sd], [R(nm, 0)], out=nm[:, 0:1], in_=sd[:], axis=AX.X, op=ALU.max, negate=True)
                if qt > 0:
                    S.i("dve", "tensor_reduce", [sc], [R(nm, 1)], out=nm[:, 1:2], in_=sc[:, 0:qt * 128], axis=AX.X, op=ALU.max, negate=True)
                    S.i("dve", "tensor_tensor", [nm], [R(nm, 2)], out=nm[:, 2:3], in0=nm[:, 0:1], in1=nm[:, 1:2], op=ALU.min)
                    negm = nm[:, 2:3]
                else:
                    negm = nm[:, 0:1]
                if stg < 3:
                    continue
                if use_thr:
                    S.i("dve", "tensor_scalar", [mb, nm], [b8], out=b8[:], in0=mb[:], scalar1=negm, scalar2=-30000.0, op0=ALU.add, op1=ALU.add)
                npz = 0
                if use_thr:
                    for n in range(qb):
                        S.i("act", "activation", [sc, b8], [R(P, n), R(rs, npz)], out=P[:, n * 256:(n + 1) * 256], in_=sc[:, n * 256:(n + 1) * 256],
                            func=AF.Exp, bias=b8[:, n:n + 1], scale=1.0, accum_out=rs[:, npz:npz + 1])
                        npz += 1
                    if qt % 2 == 1:
                        S.i("act", "activation", [sc, nm], [R(P, "o"), R(rs, npz)], out=P[:, (qt - 1) * 128:qt * 128], in_=sc[:, (qt - 1) * 128:qt * 128],
                            func=AF.Exp, bias=negm, scale=1.0, accum_out=rs[:, npz:npz + 1])
                        npz += 1
                elif qt > 0:
                    S.i("act", "activation", [sc, nm], [R(P, "past"), R(rs, npz)], out=P[:, 0:qt * 128], in_=sc[:, 0:qt * 128],
                        func=AF.Exp, bias=negm, scale=1.0, accum_out=rs[:, npz:npz + 1])
                    npz += 1
                S.i("act", "activation", [sd, nm], [R(P, "d"), R(rs, npz)], out=P[:, qt * 128:(qt + 1) * 128], in_=sd[:],
                    func=AF.Exp, bias=negm, scale=1.0, accum_out=rs[:, npz:npz + 1])
                npz += 1
                S.i("dve", "tensor_reduce", [rs], [R(rs, 11)], out=rs[:, 11:12], in_=rs[:, 0:npz], axis=AX.X, op=ALU.add)
                S.i("dve", "reciprocal", [R(rs, 11)], [R(rinv, hh)], out=rinv[:, hh:hh + 1], in_=rs[:, 11:12])
                if stg < 4:
                    continue
                for b in range((nkt + 7) // 8):
                    n8 = min(8, nkt - b * 8)
                    for t8 in range(n8):
                        kt = b * 8 + t8
                        S.i("pe", "transpose", [P, g.cb], [pT], out=pT[:, t8 * 128:(t8 + 1) * 128], in_=P[:, kt * 128:(kt + 1) * 128], identity=ident)
                    S.i("dve", "tensor_copy", [pT], [R(PT, b)], out=PT[:, b * 8:b * 8 + n8, :].rearrange("p a b -> p (a b)"), in_=pT[:, 0:n8 * 128])
                for kt in range(nkt):
                    S.i("pe", "matmul", [PT, vtm], [R(ov, hh)], out=ov[:, hh * 64:(hh + 1) * 64], lhsT=PT[:, kt, :], rhs=vtm[:, kt, hh * 64:(hh + 1) * 64],
                        start=(kt == 0), stop=(kt == nkt - 1))
                S.i("dve", "tensor_scalar", [R(ov, hh), R(rinv, hh)], [R(otm, hh)], out=otm[:, hh * 64:(hh + 1) * 64], in0=ov[:, hh * 64:(hh + 1) * 64],
                    scalar1=rinv[:, hh:hh + 1], scalar2=None, op0=ALU.mult)
            if stg < 5:
                continue
            S.i("pe", "transpose", [otm, g.cb], [pT], out=pT[:, 0:128], in_=otm[:], identity=ident)
            S.i("act", "activation", [pT], [R(ogT, (c, qt))], out=ogT[:, c, qs], in_=pT[:, 0:128], func=AF.Copy)
    for m in range(NCH):
        wo = wos[m % 2]
        S.dma("pool", wo[:], Wo[:, :, m * 128:(m + 1) * 128], writes=[wo])
        for tt in range(4):
            for k in range(NCH):
                S.i("pe", "matmul", [wo, ogT], [pj], out=pj[:], lhsT=wo[:, k, :], rhs=ogT[:, k, tt * 512:(tt + 1) * 512], start=(k == 0), stop=(k == NCH - 1))
            xs = x[:, m, tt * 512:(tt + 1) * 512]
            S.i("dve", "scalar_tensor_tensor", [pj, R(x, (m, tt))], [R(x, (m, tt))], out=xs, in0=xs, scalar=ALPHA, in1=pj[:], op0=ALU.mult, op1=ALU.add)
    S.pop()
    S.push()
    layer_norm(g, ("ln1_g%d" % i, "ln1_b%d" % i), range(4))
    S.pop()

def ret_phase(g, i):
    S = g.S
    x, xb = g.x, g.xb
    S.push()
    Win = g.W["ret_w_in"][0].rearrange("(k p) n -> p k n", p=128)
    Wsw = g.ret_sw_d.rearrange("(k p) n -> p k n", p=128)
    Wo = g.W["ret_w_o"][0].rearrange("(k p) n -> p k n", p=128)
    wq = S.sbuf("rwq", [128, NCH, 256], BF16); wqs = S.sbuf("rwqs", [128, NCH, 256], BF16)
    wk = S.sbuf("rwk", [128, NCH, 256], BF16); wks = S.sbuf("rwks", [128, NCH, 256], BF16)
    wv = S.sbuf("rwv", [128, NCH, 512], BF16); wg = S.sbuf("rwg", [128, NCH, 512], BF16)
    wo = S.sbuf("rwo", [128, 4, 1024], BF16)
    rm = S.sbuf("rrm", [128, 384], F32)
    tabs = [S.sbuf("rtab", [128, 2, 2, 512], F32) for _ in range(2)]
    qrot = S.sbuf("qrot", [128, 2, 512], BF16); krot = S.sbuf("krot", [128, 2, 512], BF16)
    vT = S.sbuf("rvT", [128, 4, 512], BF16); sgT = S.sbuf("rsgT", [128, 4, 512], BF16)
    vtm = S.sbuf("rvtm", [128, 4, 512], BF16); ktm = S.sbuf("rktm", [128, 4, 256], BF16)
    og = S.sbuf("rog", [128, 4, 512], BF16)
    S32 = S.sbuf("rS32", [128, 2, 512], F32); Sb = S.sbuf("rSb", [128, 2, 512], BF16)
    t1 = S.sbuf("rt1", [128, 512], F32); t2 = S.sbuf("rt2", [128, 512], F32)
    ST = S.sbuf("rST", [128, 128], BF16); qcd = S.sbuf("rqcd", [128, 2, 128], BF16)
    osbs = [S.sbuf("rosb", [128, 512], F32) for _ in range(2)]; osqs = [S.sbuf("rosq", [128, 512], F32) for _ in range(2)]
    means = [S.sbuf("rmean", [128, 128], F32) for _ in range(2)]; msqs = [S.sbuf("rmsq", [128, 128], F32) for _ in range(2)]; rstds = [S.sbuf("rrstd", [128, 128], F32) for _ in range(2)]
    nchunk = 0
    pA = S.psum("rpA", [128, 512]); pB = S.psum("rpB", [128, 512])
    opss = [S.psum("rops", [128, 512]) for _ in range(2)]
    ups = [S.psum("rups", [128, 512]) for _ in range(2)]
    pT = S.psum("rpT", [128, 1024], BF16)
    pst = S.psum("rpst", [128, 512])
    sps = pst
    ident = g.identb
    ones = g.ones32
    ntab = 0
    for h in range(4):
        S.dma("pool", wq[:], Win[:, :, h * 256:(h + 1) * 256], writes=[wq])
        S.dma("pool", wqs[:], Wsw[:, :, h * 256:(h + 1) * 256], writes=[wqs])
        S.dma("pool", wk[:], Win[:, :, 1024 + h * 256:1024 + (h + 1) * 256], writes=[wk])
        S.dma("pool", wks[:], Wsw[:, :, 1024 + h * 256:1024 + (h + 1) * 256], writes=[wks])
        S.dma("pool", wv[:], Win[:, :, 2048 + h * 512:2048 + (h + 1) * 512], writes=[wv])
        S.dma("pool", wg[:], Win[:, :, 4096 + h * 512:4096 + (h + 1) * 512], writes=[wg])
        S.dma("pool", wo[:], Wo[:, h * 4:(h + 1) * 4, :], writes=[wo])
        S.dma("sp", rm[:], g.rmask_d[h], writes=[rm])
        S.i("pool", "memset", [], [S32], ap=S32[:], constant=0.0)
        S.i("pool", "memset", [], [Sb], ap=Sb[:], constant=0.0)
        for tt in range(4):
            ts = slice(tt * 512, (tt + 1) * 512)
            tab = tabs[ntab % 2]; ntab += 1
            for cs_ in range(2):
                S.dma("sp", tab[:, cs_, :, :], g.rtab_d[cs_][:, :, ts], writes=[R(tab, None)])
            for (wa, wb, dst) in ((wq, wqs, qrot), (wk, wks, krot)):
                for dc in range(2):
                    for k in range(NCH):
                        S.i("pe", "matmul", [wa, R(xb, (k, tt))], [pA], out=pA[:], lhsT=wa[:, k, dc * 128:(dc + 1) * 128], rhs=xb[:, k, ts],
                            start=(k == 0), stop=(k == NCH - 1))
                    for k in range(NCH):
                        S.i("pe", "matmul", [wb, R(xb, (k, tt))], [pB], out=pB[:], lhsT=wb[:, k, dc * 128:(dc + 1) * 128], rhs=xb[:, k, ts],
                            start=(k == 0), stop=(k == NCH - 1))
                    S.i("dve", "tensor_tensor", [pA, tab], [t1], out=t1[:], in0=pA[:], in1=tab[:, 0, dc, :], op=ALU.mult)
                    S.i("dve", "tensor_tensor", [pB, tab], [t2], out=t2[:], in0=pB[:], in1=tab[:, 1, dc, :], op=ALU.mult)
                    S.i("pool", "tensor_tensor", [t1, t2], [R(dst, dc)], out=dst[:, dc, :], in0=t1[:], in1=t2[:], op=ALU.add)
            for ec in range(4):
                for k in range(NCH):
                    S.i("pe", "matmul", [wv, R(xb, (k, tt))], [pA], out=pA[:], lhsT=wv[:, k, ec * 128:(ec + 1) * 128], rhs=xb[:, k, ts],
                        start=(k == 0), stop=(k == NCH - 1))
                S.i("act", "activation", [pA], [R(vT, ec)], out=vT[:, ec, :], in_=pA[:], func=AF.Copy)
                for k in range(NCH):
                    S.i("pe", "matmul", [wg, R(xb, (k, tt))], [pB], out=pB[:], lhsT=wg[:, k, ec * 128:(ec + 1) * 128], rhs=xb[:, k, ts],
                        start=(k == 0), stop=(k == NCH - 1))
                S.i("act", "activation", [pB], [R(sgT, ec)], out=sgT[:, ec, :], in_=pB[:], func=AF.Silu)
            for n in range(4):
                for ec in range(4):
                    idx = (n % 2) * 4 + ec
                    S.i("pe", "transpose", [vT, g.cb], [pT], out=pT[:, idx * 128:(idx + 1) * 128], in_=vT[:, ec, n * 128:(n + 1) * 128], identity=ident)
                if n % 2 == 1:
                    S.i("dve", "tensor_copy", [pT], [R(vtm, n // 2)], out=vtm[:, n - 1:n + 1, :].rearrange("p a b -> p (a b)"), in_=pT[:])
            for n in range(4):
                for dc in range(2):
                    idx = n * 2 + dc
                    S.i("pe", "transpose", [krot, g.cb], [pT], out=pT[:, idx * 128:(idx + 1) * 128], in_=krot[:, dc, n * 128:(n + 1) * 128], identity=ident)
            S.i("dve", "tensor_scalar", [pT, rm], [ktm], out=ktm[:].rearrange("p a b -> p (a b)"), in0=pT[:], scalar1=rm[:, 256:257], scalar2=None, op0=ALU.mult)
            def partA(n):
                cs = slice(n * 128, (n + 1) * 128)
                opsn = opss[n % 2]
                for dc in range(2):
                    S.i("pe", "matmul", [krot, qrot], [sps], out=sps[:, 256:384], lhsT=krot[:, dc, cs], rhs=qrot[:, dc, cs], start=(dc == 0), stop=(dc == 1))
                S.i("dve", "tensor_tensor", [sps, rm], [ST], out=ST[:], in0=sps[:, 256:384], in1=rm[:, 0:128], op=ALU.mult)
                S.i("pool", "tensor_tensor", [qrot, rm], [qcd], out=qcd[:], in0=qrot[:, :, cs], in1=rm[:, 128:256].unsqueeze(1).to_broadcast([128, 2, 128]), op=ALU.mult)
                for ec in range(4):
                    es = slice(ec * 128, (ec + 1) * 128)
                    S.i("pe", "matmul", [vtm, ST], [opsn], out=opsn[:, es], lhsT=vtm[:, n, es], rhs=ST[:], start=True, stop=False)
                    for dc in range(2):
                        S.i("pe", "matmul", [Sb, qcd], [opsn], out=opsn[:, es], lhsT=Sb[:, dc, es], rhs=qcd[:, dc, :], start=False, stop=(dc == 1))
                for dc in range(2):
                    S.i("pe", "matmul", [ktm, vtm], [ups[dc]], out=ups[dc][:], lhsT=ktm[:, n, dc * 128:(dc + 1) * 128], rhs=vtm[:, n, :], start=True, stop=True)
                    S.i("dve", "scalar_tensor_tensor", [S32, ups[dc], rm], [R(S32, dc)], out=S32[:, dc, :], in0=S32[:, dc, :], scalar=rm[:, 257:258], in1=ups[dc][:],
                        op0=ALU.mult, op1=ALU.add)
                    S.i("act", "activation", [R(S32, dc)], [R(Sb, dc)], out=Sb[:, dc, :], in_=S32[:, dc, :], func=AF.Copy)

            def partB(n):
                cs = slice(n * 128, (n + 1) * 128)
                opsn = opss[n % 2]
                osb, osq, mean, msq, rstd = osbs[n % 2], osqs[n % 2], means[n % 2], msqs[n % 2], rstds[n % 2]
                S.i("act", "activation", [opsn], [osb], out=osb[:], in_=opsn[:], func=AF.Copy)
                S.i("act", "activation", [opsn], [osq], out=osq[:], in_=opsn[:], func=AF.Square)
                for ec in range(4):
                    S.i("pe", "matmul", [osb, g.cf], [R(pst, 0)], out=pst[:, 0:128], lhsT=ones, rhs=osb[:, ec * 128:(ec + 1) * 128], start=(ec == 0), stop=(ec == 3))
                for ec in range(4):
                    S.i("pe", "matmul", [osq, g.cf], [R(pst, 1)], out=pst[:, 128:256], lhsT=ones, rhs=osq[:, ec * 128:(ec + 1) * 128], start=(ec == 0), stop=(ec == 3))
                S.i("dve", "tensor_scalar", [R(pst, 0)], [mean], out=mean[:], in0=pst[:, 0:128], scalar1=1.0 / 512, scalar2=None, op0=ALU.mult)
                S.i("dve", "tensor_tensor", [mean], [msq], out=msq[:], in0=mean[:], in1=mean[:], op=ALU.mult)
                S.i("dve", "scalar_tensor_tensor", [R(pst, 1), msq], [msq], out=msq[:], in0=pst[:, 128:256], scalar=1.0 / 512, in1=msq[:], op0=ALU.mult, op1=ALU.subtract)
                S.i("dve", "tensor_scalar", [msq], [msq], out=msq[:], in0=msq[:], scalar1=1e-5, scalar2=None, op0=ALU.add)
                S.i("act", "activation", [msq], [rstd], out=rstd[:], in_=msq[:], func=AF.Ln)
                S.i("act", "activation", [rstd], [rstd], out=rstd[:], in_=rstd[:], func=AF.Exp, scale=-0.5)
                mb_ = mean[:].unsqueeze(1).to_broadcast([128, 4, 128])
                rb_ = rstd[:].unsqueeze(1).to_broadcast([128, 4, 128])
                o3 = osb[:].rearrange("p (a b) -> p a b", b=128)
                S.i("dve", "tensor_tensor", [osb, mean], [osb], out=o3, in0=o3, in1=mb_, op=ALU.subtract)
                S.i("dve", "tensor_tensor", [osb, rstd], [osb], out=o3, in0=o3, in1=rb_, op=ALU.mult)
                for ec in range(4):
                    col = h * 4 + ec
                    S.i("act", "activation", [osb, g.vecs], [R(osq, ec)], out=osq[:, ec * 128:(ec + 1) * 128], in_=osb[:, ec * 128:(ec + 1) * 128], func=AF.Identity,
                        bias=vcol(g, "ret_gn_b", col), scale=vcol(g, "ret_gn_g", col))
                S.i("dve", "tensor_tensor", [osq, sgT], [R(og, n)], out=og[:, :, cs], in0=osq[:].rearrange("p (a b) -> p a b", b=128), in1=sgT[:, :, cs], op=ALU.mult)

            partA(0)
            for n in range(4):
                if n + 1 < 4:
                    partA(n + 1)
                partB(n)
            for m in range(NCH):
                pw = pA if m % 2 == 0 else pB
                for k in range(4):
                    S.i("pe", "matmul", [wo, og], [pw], out=pw[:], lhsT=wo[:, k, m * 128:(m + 1) * 128], rhs=og[:, k, :], start=(k == 0), stop=(k == 3))
                xs = x[:, m, ts]
                if h == 0:
                    S.i("dve", "scalar_tensor_tensor", [pw, R(x, (m, tt))], [R(x, (m, tt))], out=xs, in0=xs, scalar=ALPHA, in1=pw[:], op0=ALU.mult, op1=ALU.add)
                else:
                    S.i("dve", "tensor_tensor", [pw, R(x, (m, tt))], [R(x, (m, tt))], out=xs, in0=xs, in1=pw[:], op=ALU.add)
    S.pop()
    S.push()
    layer_norm(g, ("ln1_g%d" % i, "ln1_b%d" % i), range(4))
    S.pop()

def rwkv_phase(g, i, j):
    S = g.S
    x, xb = g.x, g.xb
    S.push()
    TW = 256
    NT_ = T_SEQ // TW
    W = g.W
    Wrkv = [W["rwkv_w_rkv"][j, w].rearrange("(k p) n -> p k n", p=128) for w in range(3)]
    Wo = W["rwkv_w_o"][j].rearrange("(k p) n -> p k n", p=128)
    V = lambda name, c=0: vcol(g, "%s_%d" % (name, j), c)
    w1 = S.sbuf("w1", [128, NCH, 64], BF16); a1 = S.sbuf("a1", [128, NCH, 64], BF16); g1 = S.sbuf("g1", [128, NCH, 160], BF16)
    w2 = S.sbuf("w2", [64, 1024], BF16); a2 = S.sbuf("a2", [64, 1024], BF16); g2 = S.sbuf("g2", [128, 2, 1024], BF16)
    S.dma("pool", w1[:], W["rwkv_w1"][j].rearrange("(k p) n -> p k n", p=128), writes=[w1])
    S.dma("pool", a1[:], W["rwkv_a1"][j].rearrange("(k p) n -> p k n", p=128), writes=[a1])
    S.dma("pool", g1[:], W["rwkv_g1"][j].rearrange("(k p) n -> p k n", p=128), writes=[g1])
    S.dma("pool", w2[:], W["rwkv_w2"][j], writes=[w2])
    S.dma("pool", a2[:], W["rwkv_a2"][j], writes=[a2])
    S.dma("pool", g2[:, 0, :], W["rwkv_g2"][j][0:128, :], writes=[R(g2, None)])
    S.dma("pool", g2[0:32, 1, :], W["rwkv_g2"][j][128:160, :], writes=[R(g2, None)])
    rows = S.sbuf("gnrows", [128, 2, 1024], BF16)
    S.dma("pool", rows[:, 0, :], g.rows_d[2 * j], writes=[R(rows, None)])
    S.dma("pool", rows[:, 1, :], g.rows_d[2 * j + 1], writes=[R(rows, None)])
    om = S.sbuf("om", [128, 72], F32)
    mo = g.voff["mix0_%d" % j]
    S.i("dve", "tensor_scalar", [g.vecs], [om], out=om[:, 0:48], in0=g.vecs[:, mo:mo + 48], scalar1=-1.0, scalar2=1.0, op0=ALU.mult, op1=ALU.add)
    ko = g.voff["k_a_%d" % j]
    S.i("dve", "tensor_scalar", [g.vecs], [om], out=om[:, 48:56], in0=g.vecs[:, ko:ko + 8], scalar1=-1.0, scalar2=1.0, op0=ALU.mult, op1=ALU.add)
    ao = g.voff["a0_%d" % j]; wo_ = g.voff["w0_%d" % j]
    S.i("dve", "tensor_scalar", [g.vecs], [om], out=om[:, 56:64], in0=g.vecs[:, ao:ao + 8], scalar1=0.5, scalar2=None, op0=ALU.mult)
    S.i("dve", "tensor_scalar", [g.vecs], [om], out=om[:, 64:72], in0=g.vecs[:, wo_:wo_ + 8], scalar1=0.5, scalar2=None, op0=ALU.mult)
    xx = S.sbuf("xx", [128, NCH, TW], F32)
    xlast = S.sbuf("xlast", [128, NCH, 2], F32)
    S.i("pool", "memset", [], [xlast], ap=xlast[:], constant=0.0)
    hw = S.sbuf("hw", [64, TW], BF16); ha = S.sbuf("ha", [64, TW], BF16); sg = S.sbuf("sg", [128, 2, TW], BF16); sgf = S.sbuf("sgf", [128, 2, TW], BF16)
    NSL = _DBG.get('rw_nsl', 4)
    wsl = [S.sbuf("rwsl", [128, NCH, 128], BF16) for _ in range(NSL)]
    wseq = []
    for tix_ in range(NT_):
        for c_ in range(NCH):
            wseq.append(Wrkv[0][:, :, c_ * 128:(c_ + 1) * 128]); wseq.append(Wrkv[1][:, :, c_ * 128:(c_ + 1) * 128])
        for c_ in range(NCH):
            wseq.append(Wrkv[2][:, :, c_ * 128:(c_ + 1) * 128])
        for m_ in range(NCH):
            wseq.append(Wo[:, :, m_ * 128:(m_ + 1) * 128])
    wst = {"issued": 0, "next": 0}
    def wget():
        i_ = wst["next"]; wst["next"] += 1
        while wst["issued"] < min(len(wseq), i_ + NSL - 1):
            q_ = wst["issued"]
            S.dma("pool", wsl[q_ % NSL][:], wseq[q_], writes=[wsl[q_ % NSL]])
            wst["issued"] += 1
        return wsl[i_ % NSL]
    TS = [dict(), dict()]
    for nm in ("tA", "tsw", "tkk", "tcs", "te1", "te2", "te3"):
        for q_ in range(2):
            TS[q_][nm] = S.sbuf(nm, [128, TW], F32)
    for nm in ("trn", "tt"):
        t_ = S.sbuf(nm, [128, TW], F32)
        TS[0][nm] = t_; TS[1][nm] = t_
    for q_ in range(2):
        TS[q_]["tkk2"] = S.sbuf("tkk2", [128, TW], BF16)
    PC = S.sbuf("PC", [128, NCH, 2], F32)
    def FM(a, c, sl):
        return xb[:, c, a * TW + sl.start:a * TW + sl.stop]
    FMK = lambda a, c: R(xb, ("fm", a, c))
    Lk = [[S.sbuf("Lk", [128, 4, 128], BF16) for _ in range(2)] for _ in range(2)]
    Mk = [[S.sbuf("Mk", [128, 4, 128], BF16) for _ in range(2)] for _ in range(2)]
    NTt = [[S.sbuf("NT", [128, 4, 128], BF16) for _ in range(2)] for _ in range(2)]
    Mak = [S.sbuf("Mak", [128, 4, 128], BF16) for _ in range(2)]
    Mrb = S.sbuf("Mrb", [128, 16, 128], BF16); Mrk = S.sbuf("Mrk", [128, 16, 128], BF16)
    Atm = S.sbuf("Atm", [128, 1024], BF16); Btm = S.sbuf("Btm", [128, 1024], BF16); Ktm = S.sbuf("Ktm", [128, 1024], BF16); Vtm = S.sbuf("Vtm", [128, 1024], BF16)
    AhT = S.sbuf("AhT", [128, NCH, 128], BF16); Xb = S.sbuf("Xb", [128, 256], BF16); Vhat = S.sbuf("Vhat", [128, 1024], BF16)
    Ub = S.sbuf("Ub", [128, 1024], BF16); ST = S.sbuf("ST", [128, NCH, 64], BF16); STs = S.sbuf("STs", [128, NCH, 64], F32)
    yn = S.sbuf("yn", [128, 1024], F32); st4 = S.sbuf("st4", [128, 4, 16], F32); bsum = S.sbuf("bsum", [128, 16], F32)
    ogtm = S.sbuf("ogtm", [128, 1024], BF16); ogT = S.sbuf("ogT", [128, NCH, TW], BF16)
    PB = [S.psum("rp", [128, 512]) for _ in range(7)]
    pT = S.psum("rpT", [128, 1024], BF16)
    ident = g.identb
    blk1 = g.cb[:, 128:256]
    hind = g.cb[:, 256:258]
    ones = g.ones32
    S.i("pool", "memset", [], [ST], ap=ST[:], constant=0.0)
    nws = 0
    stg = _DBG.get('rw_stage', 99)
    for tix in range(_DBG.get('rw_tiles', NT_)):
        t0 = tix * TW
        tt = t0 // 512
        S.i("dve", "tensor_tensor", [R(x, None)], [R(xx, "m")], out=xx[:, :, 1:TW], in0=x[:, :, t0:t0 + TW - 1], in1=x[:, :, t0 + 1:t0 + TW], op=ALU.subtract)
        S.i("dve", "tensor_tensor", [R(x, None), xlast], [R(xx, "0")], out=xx[:, :, 0:1], in0=xlast[:, :, tix % 2:tix % 2 + 1], in1=x[:, :, t0:t0 + 1], op=ALU.subtract)
        S.i("dve", "tensor_copy", [R(x, None)], [xlast], out=xlast[:, :, (tix + 1) % 2:(tix + 1) % 2 + 1], in_=x[:, :, t0 + TW - 1:t0 + TW])
        def mix(m, dst):
            for c in range(NCH):
                mc = vcol(g, "mix%d_%d" % (m, j), c)
                S.i("dve", "scalar_tensor_tensor", [R(x, (c, tt)), xx, g.vecs], [dst.key(c)], out=dst.ap(c), in0=xx[:, c, :], scalar=mc, in1=x[:, c, t0:t0 + TW],
                    op0=ALU.mult, op1=ALU.add)
        nx = [0]
        class XM:
            def __init__(self, slot):
                self.slot = slot
            def ap(self, c):
                return xb[:, c, 1536 + self.slot * TW:1536 + (self.slot + 1) * TW]
            def key(self, c):
                return R(xb, ("xm", self.slot, c))
        def nxm():
            nx[0] += 1
            return XM(nx[0] % 2)
        d_ = nxm(); mix(1, d_)
        for k in range(NCH):
            S.i("pe", "matmul", [w1, d_.key(k)], [PB[0]], out=PB[0][0:64, 0:TW], lhsT=w1[:, k, :], rhs=d_.ap(k), start=(k == 0), stop=(k == NCH - 1))
        S.i("act", "activation", [PB[0]], [hw], out=hw[:], in_=PB[0][0:64, 0:TW], func=AF.Tanh)
        d_ = nxm(); mix(4, d_)
        for k in range(NCH):
            S.i("pe", "matmul", [a1, d_.key(k)], [PB[1]], out=PB[1][0:64, 0:TW], lhsT=a1[:, k, :], rhs=d_.ap(k), start=(k == 0), stop=(k == NCH - 1))
        S.i("act", "activation", [PB[1]], [ha], out=ha[:], in_=PB[1][0:64, 0:TW], func=AF.Copy)
        d_ = nxm(); mix(5, d_)
        for (lo, hi, kc) in ((0, 128, 0), (128, 160, 1)):
            for k in range(NCH):
                S.i("pe", "matmul", [g1, d_.key(k)], [PB[2]], out=PB[2][0:hi - lo, 0:TW], lhsT=g1[:, k, lo:hi], rhs=d_.ap(k), start=(k == 0), stop=(k == NCH - 1))
            S.i("act", "activation", [PB[2]], [R(sgf, kc)], out=sgf[0:hi - lo, kc, :], in_=PB[2][0:hi - lo, 0:TW], func=AF.Tanh, scale=0.5)
            S.i("dve", "tensor_scalar", [R(sgf, kc)], [R(sg, kc)], out=sg[0:hi - lo, kc, :], in0=sgf[0:hi - lo, kc, :], scalar1=0.5, scalar2=0.5, op0=ALU.mult, op1=ALU.add)
        xr = nxm(); mix(0, xr)
        xk = nxm(); mix(2, xk)
        if stg < 3:
            continue
        for c in range(NCH):
            wr = wget()
            wk_ = wget()
            d = TS[c % 2]
            tA, tsw, tcs, te1, te2, te3, tkk, tkk2, trn, tt_ = (d[k_] for k_ in ("tA", "tsw", "tcs", "te1", "te2", "te3", "tkk", "tkk2", "trn", "tt"))
            td3 = tsw; tkkn = tkk; ttb = tkk; tkm = tt_
            if c % 2 == 0:
                rp, kp, zw, za, ssp = PB[0], PB[1], PB[2], PB[3], PB[4]
            else:
                rp, kp, zw, za, ssp = PB[5], PB[6], PB[2], PB[3], PB[4]
            for k in range(NCH):
                S.i("pe", "matmul", [wr, xr.key(k)], [rp], out=rp[:, 0:TW], lhsT=wr[:, k, :], rhs=xr.ap(k), start=(k == 0), stop=(k == NCH - 1))
            for k in range(NCH):
                S.i("pe", "matmul", [wk_, xk.key(k)], [kp], out=kp[:, 0:TW], lhsT=wk_[:, k, :], rhs=xk.ap(k), start=(k == 0), stop=(k == NCH - 1))
            S.i("pe", "matmul", [w2, hw], [zw], out=zw[:, 0:TW], lhsT=w2[:, c * 128:(c + 1) * 128], rhs=hw[:], start=True, stop=True)
            S.i("pe", "matmul", [a2, ha], [za], out=za[:, 0:TW], lhsT=a2[:, c * 128:(c + 1) * 128], rhs=ha[:], start=True, stop=True)
            S.i("act", "activation", [za, om], [tA], out=tA[:], in_=za[:, 0:TW], func=AF.Tanh, bias=om[:, 56 + c:57 + c], scale=0.5)
            S.i("act", "activation", [zw, om], [tsw], out=tsw[:], in_=zw[:, 0:TW], func=AF.Tanh, bias=om[:, 64 + c:65 + c], scale=0.5)
            S.i("dve", "tensor_scalar", [tA], [tA], out=tA[:], in0=tA[:], scalar1=0.5, scalar2=0.5, op0=ALU.mult, op1=ALU.add)
            S.i("dve", "tensor_scalar", [tsw], [tsw], out=tsw[:], in0=tsw[:], scalar1=0.5, scalar2=0.5, op0=ALU.mult, op1=ALU.add)
            for n in range(2):
                cs = slice(n * 128, (n + 1) * 128)
                S.i("dve", "tensor_tensor_scan", [tsw, g.cf], [R(tcs, n)], out=tcs[:, cs], data0=ones, data1=tsw[:, cs], initial=0.0, op0=ALU.mult, op1=ALU.add)
            S.i("dve", "tensor_tensor", [tcs, tsw], [td3], out=td3[:], in0=tcs[:], in1=tsw[:], op=ALU.subtract)
            LD = 0.6065306597126334
            S.i("act", "activation", [tcs], [te1], out=te1[:], in_=tcs[:], func=AF.Exp, scale=-LD)
            S.i("act", "activation", [tcs], [te2], out=te2[:], in_=tcs[:], func=AF.Exp, scale=LD)
            S.i("act", "activation", [td3], [te3], out=te3[:], in_=td3[:], func=AF.Exp, scale=-LD)
            S.i("act", "activation", [te1], [R(PC, c)], out=PC[:, c, :], in_=te1[:, 127:TW:128], func=AF.Copy)
            S.i("dve", "tensor_scalar", [kp, g.vecs], [tkk], out=tkk[:], in0=kp[:, 0:TW], scalar1=V("k_k", c), scalar2=None, op0=ALU.mult)
            S.i("act", "activation", [tkk], [tkk2], out=tkk2[:], in_=tkk[:], func=AF.Square)
            S.i("pe", "matmul", [tkk2, g.cb], [ssp], out=ssp[:, 0:TW], lhsT=blk1, rhs=tkk2[:], start=True, stop=True)
            S.i("dve", "tensor_scalar", [ssp], [trn], out=trn[:], in0=ssp[:, 0:TW], scalar1=1e-24, scalar2=None, op0=ALU.max)
            S.i("act", "activation", [trn], [trn], out=trn[:], in_=trn[:], func=AF.Ln)
            S.i("act", "activation", [trn], [trn], out=trn[:], in_=trn[:], func=AF.Exp, scale=-0.5)
            S.i("dve", "tensor_tensor", [tkk, trn], [tkkn], out=tkkn[:], in0=tkk[:], in1=trn[:], op=ALU.mult)
            S.i("dve", "tensor_scalar", [tA, g.vecs, om], [tt_], out=tt_[:], in0=tA[:], scalar1=V("k_a", c), scalar2=om[:, 48 + c:49 + c], op0=ALU.mult, op1=ALU.add)
            S.i("dve", "tensor_tensor", [kp, tt_], [tkm], out=tkm[:], in0=kp[:, 0:TW], in1=tt_[:], op=ALU.mult)
            full = slice(0, TW)
            S.i("dve", "scalar_tensor_tensor", [tkkn, te3], [FMK(0, c)], out=FM(0, c, full), in0=tkkn[:], scalar=-1.0, in1=te3[:], op0=ALU.mult, op1=ALU.mult)
            S.i("dve", "tensor_tensor", [tkkn, tA], [ttb], out=ttb[:], in0=tkkn[:], in1=tA[:], op=ALU.mult)
            S.i("dve", "tensor_tensor", [ttb, te2], [FMK(1, c)], out=FM(1, c, full), in0=ttb[:], in1=te2[:], op=ALU.mult)
            S.i("dve", "tensor_tensor", [tkm, te2], [FMK(2, c)], out=FM(2, c, full), in0=tkm[:], in1=te2[:], op=ALU.mult)
            S.i("dve", "tensor_tensor", [rp, te1], [FMK(3, c)], out=FM(3, c, full), in0=rp[:, 0:TW], in1=te1[:], op=ALU.mult)
            S.i("dve", "scalar_tensor_tensor", [rp, g.vecs, tkm], [FMK(4, c)], out=FM(4, c, full), in0=rp[:, 0:TW], scalar=V("r_k", c), in1=tkm[:], op0=ALU.mult, op1=ALU.mult)
        xv = nxm(); mix(3, xv)
        for c in range(NCH):
            wv_ = wget()
            vp = PB[5 + (c % 2)]
            for k in range(NCH):
                S.i("pe", "matmul", [wv_, xv.key(k)], [vp], out=vp[:, 0:TW], lhsT=wv_[:, k, :], rhs=xv.ap(k), start=(k == 0), stop=(k == NCH - 1))
            S.i("act", "activation", [vp], [FMK(5, c)], out=FM(5, c, slice(0, TW)), in_=vp[:, 0:TW], func=AF.Copy)
        if stg < 5:
            continue
        for n in range(2):
            cs = slice(n * 128, (n + 1) * 128)
            for a_, dst in ((0, Atm), (1, Btm), (2, Ktm), (5, Vtm)):
                for c in range(NCH):
                    S.i("pe", "transpose", [FMK(a_, c), g.cb], [pT], out=pT[:, c * 128:(c + 1) * 128], in_=FM(a_, c, cs), identity=ident)
                S.i("dve" if a_ in (0, 2) else "act", "tensor_copy" if a_ in (0, 2) else "activation", [pT], [dst],
                    **(dict(out=dst[:], in_=pT[:]) if a_ in (0, 2) else dict(out=dst[:], in_=pT[:], func=AF.Copy)))
            if stg < 6:
                continue
            for hgp in range(2):
                grp = [2 * hgp, 2 * hgp + 1]
                def HP(h):
                    return slice((h % 2) * 64, (h % 2) * 64 + 64), h // 2
                for gi, hg in enumerate(grp):
                    heads = [4 * hg + q for q in range(4)]
                    specs = [(1, 0, 0, Mk[gi][0], None), (0, 1, 2, Lk[gi][0], None), (2, 0, 0, Mak[gi], None), (1, 3, 1, Mrb, hg), (2, 3, 1, Mrk, hg)]
                    for si, (la, ra, mki, dst, full16) in enumerate(specs):
                        dview = dst[:] if full16 is None else dst[:, hg * 4:(hg + 1) * 4, :]
                        wkey = dst if full16 is None else R(dst, hg)
                        for par in range(2):
                            bk = PB[(2 * si + par) % 6]
                            for qq in range(2):
                                h = heads[2 * qq + par]
                                ps_, c = HP(h)
                                S.i("pe", "matmul", [FMK(la, c), FMK(ra, c)], [bk], out=bk[:, qq * 128:(qq + 1) * 128], lhsT=FM(la, c, cs)[ps_, :], rhs=FM(ra, c, cs)[ps_, :],
                                    start=True, stop=True)
                            S.i("dve", "tensor_tensor", [bk, g.mk], [wkey], out=dview[:, par::2, :],
                                in0=bk[:, 0:256].rearrange("p (a b) -> p a b", b=128), in1=g.mk[:, mki, 0:256].rearrange("p (a b) -> p a b", b=128), op=ALU.mult)
                if stg < 7:
                    continue
                cur = [0, 0]
                NTin = [Mk[0][0], Mk[1][0]]
                for r in range(7):
                    for gi in range(2):
                        Lc, Mc = Lk[gi][cur[gi]], Mk[gi][cur[gi]]
                        Ln, Mn = Lk[gi][1 - cur[gi]], Mk[gi][1 - cur[gi]]
                        bN, bL, bM = PB[3 * gi], PB[3 * gi + 1], PB[3 * gi + 2]
                        if r >= 1:
                            NTo = NTt[gi][r % 2]
                            for q in range(4):
                                S.i("pe", "matmul", [NTin[gi], g.cb], [bN], out=bN[:, q * 128:(q + 1) * 128], lhsT=ident, rhs=NTin[gi][:, q, :], start=True, stop=False)
                                S.i("pe", "matmul", [Mc, g.cb], [bN], out=bN[:, q * 128:(q + 1) * 128], lhsT=ident, rhs=Mc[:, q, :], start=False, stop=False)
                                S.i("pe", "matmul", [Lc, NTin[gi]], [bN], out=bN[:, q * 128:(q + 1) * 128], lhsT=Lc[:, q, :], rhs=NTin[gi][:, q, :], start=False, stop=True)
                        if r < 6:
                            for q in range(4):
                                S.i("pe", "matmul", [Mc, Lc], [bL], out=bL[:, q * 128:(q + 1) * 128], lhsT=Mc[:, q, :], rhs=Lc[:, q, :], start=True, stop=True)
                            for q in range(4):
                                S.i("pe", "matmul", [Lc, Mc], [bM], out=bM[:, q * 128:(q + 1) * 128], lhsT=Lc[:, q, :], rhs=Mc[:, q, :], start=True, stop=True)
                        if r >= 1:
                            if gi == 0:
                                S.i("act", "activation", [bN], [NTo], out=NTo[:].rearrange("p a b -> p (a b)"), in_=bN[:], func=AF.Copy)
                            else:
                                S.i("dve", "tensor_copy", [bN], [NTo], out=NTo[:].rearrange("p a b -> p (a b)"), in_=bN[:])
                            NTin[gi] = NTo
                        if r < 6:
                            S.i("act", "activation", [bL], [Ln], out=Ln[:].rearrange("p a b -> p (a b)"), in_=bL[:], func=AF.Copy)
                            S.i("dve", "tensor_copy", [bM], [Mn], out=Mn[:].rearrange("p a b -> p (a b)"), in_=bM[:])
                            cur[gi] = 1 - cur[gi]
                if stg < 8:
                    continue
                for gi, hg in enumerate(grp):
                    heads = [4 * hg + q for q in range(4)]
                    NTf = NTin[gi]
                    for q, h in enumerate(heads):
                        S.i("pe", "matmul", [Mak[gi], Vtm], [PB[6]], out=PB[6][:, q * 64:(q + 1) * 64], lhsT=Mak[gi][:, q, :], rhs=Vtm[:, h * 64:(h + 1) * 64], start=True, stop=True)
                    S.i("act", "activation", [PB[6]], [Xb], out=Xb[:], in_=PB[6][:, 0:256], func=AF.Copy)
                    for q, h in enumerate(heads):
                        S.i("pe", "matmul", [Xb, g.cb], [PB[6]], out=PB[6][:, 256 + q * 64:256 + (q + 1) * 64], lhsT=ident, rhs=Xb[:, q * 64:(q + 1) * 64], start=True, stop=False)
                        S.i("pe", "matmul", [NTf, Xb], [PB[6]], out=PB[6][:, 256 + q * 64:256 + (q + 1) * 64], lhsT=NTf[:, q, :], rhs=Xb[:, q * 64:(q + 1) * 64], start=False, stop=True)
                    S.i("dve", "tensor_copy", [PB[6]], [R(Vhat, hg)], out=Vhat[:, hg * 256:(hg + 1) * 256], in_=PB[6][:, 256:512])
                    bA = PB[gi]
                    for q, h in enumerate(heads):
                        ps_, c = HP(h)
                        cc = (c % 2) * 128
                        S.i("pe", "matmul", [Atm, NTf], [bA], out=bA[ps_, cc:cc + 128], lhsT=Atm[:, h * 64:(h + 1) * 64], rhs=NTf[:, q, :], start=True, stop=True)
                    S.i("dve", "tensor_tensor", [bA, FMK(0, 2 * hg), FMK(0, 2 * hg + 1)], [R(AhT, hg)], out=AhT[:, 2 * hg:2 * hg + 2, :],
                        in0=bA[:, 0:256].rearrange("p (a b) -> p a b", b=128), in1=xb[:, 2 * hg:2 * hg + 2, cs.start:cs.stop], op=ALU.add)
            if stg < 9:
                continue
            Ubv = Ub[:].rearrange("p (h n) -> p h n", n=64)
            Vhv = Vhat[:].rearrange("p (h n) -> p h n", n=64)
            for par in range(2):
                bk = PB[par]
                for hh in range(8):
                    h = 2 * hh + par
                    ps_, c = slice(par * 64, par * 64 + 64), h // 2
                    S.i("pe", "matmul", [AhT, ST], [bk], out=bk[:, hh * 64:(hh + 1) * 64], lhsT=AhT[ps_, c, :], rhs=ST[ps_, c, :], start=True, stop=True)
            for par in range(2):
                S.i("dve", "tensor_tensor", [PB[par], Vhat], [R(Ub, par)], out=Ubv[:, par::2, :], in0=PB[par][:].rearrange("p (h n) -> p h n", n=64),
                    in1=Vhv[:, par::2, :], op=ALU.add)
            for c in range(NCH):
                S.i("act", "activation", [R(ST, c), PC], [R(STs, c)], out=STs[:, c, :], in_=ST[:, c, :], func=AF.Identity, scale=PC[:, c, n:n + 1])
            for par in range(2):
                bk = PB[2 + par]
                for hh in range(8):
                    h = 2 * hh + par
                    ps_, c = slice(par * 64, par * 64 + 64), h // 2
                    o_ = bk[:, hh * 64:(hh + 1) * 64]
                    S.i("pe", "matmul", [FMK(3, c), ST], [bk], out=o_, lhsT=FM(3, c, cs)[ps_, :], rhs=ST[ps_, c, :], start=True, stop=False)
                    S.i("pe", "matmul", [Mrk, Vtm], [bk], out=o_, lhsT=Mrk[:, h, :], rhs=Vtm[:, h * 64:(h + 1) * 64], start=False, stop=False)
                    S.i("pe", "matmul", [Mrb, Ub], [bk], out=o_, lhsT=Mrb[:, h, :], rhs=Ub[:, h * 64:(h + 1) * 64], start=False, stop=True)
            for h in range(16):
                ps_, c = slice((h % 2) * 64, (h % 2) * 64 + 64), h // 2
                o_ = PB[4][ps_, c * 64:(c + 1) * 64]
                S.i("pe", "matmul", [Ktm, Vtm], [PB[4]], out=o_, lhsT=Ktm[:, h * 64:(h + 1) * 64], rhs=Vtm[:, h * 64:(h + 1) * 64], start=True, stop=False)
                S.i("pe", "matmul", [Btm, Ub], [PB[4]], out=o_, lhsT=Btm[:, h * 64:(h + 1) * 64], rhs=Ub[:, h * 64:(h + 1) * 64], start=False, stop=True)
            for c in range(NCH):
                S.i("dve", "scalar_tensor_tensor", [PB[4], PC, R(STs, c)], [R(ST, c)], out=ST[:, c, :], in0=PB[4][:, c * 64:(c + 1) * 64], scalar=PC[:, c, n:n + 1],
                    in1=STs[:, c, :], op0=ALU.mult, op1=ALU.add)
            for b in range(2):
                S.i("dve", "tensor_reduce", [PB[2 + b]], [R(st4, ("s", b))], out=st4[:, 0, b::2], in_=PB[2 + b][:].rearrange("p (h n) -> p h n", n=64), axis=AX.X, op=ALU.add)
                S.i("act", "activation", [PB[2 + b]], [R(yn, None)], out=yn[:, b * 512:(b + 1) * 512], in_=PB[2 + b][:], func=AF.Square)
                S.i("dve", "tensor_reduce", [R(yn, None)], [R(st4, ("q", b))], out=st4[:, 1, b::2], in_=yn[:, b * 512:(b + 1) * 512].rearrange("p (h n) -> p h n", n=64), axis=AX.X, op=ALU.add)
            S.i("dve", "tensor_scalar", [st4], [R(st4, "m")], out=st4[:, 2, :], in0=st4[:, 0, :], scalar1=1.0 / 64, scalar2=None, op0=ALU.mult)
            S.i("dve", "tensor_tensor", [st4], [R(st4, "v")], out=st4[:, 3, :], in0=st4[:, 2, :], in1=st4[:, 2, :], op=ALU.mult)
            S.i("dve", "scalar_tensor_tensor", [st4], [R(st4, "v")], out=st4[:, 3, :], in0=st4[:, 1, :], scalar=1.0 / 64, in1=st4[:, 3, :], op0=ALU.mult, op1=ALU.subtract)
            S.i("dve", "tensor_scalar", [st4], [R(st4, "v")], out=st4[:, 3, :], in0=st4[:, 3, :], scalar1=64e-5, scalar2=None, op0=ALU.add)
            S.i("act", "activation", [st4], [R(st4, "v")], out=st4[:, 3, :], in_=st4[:, 3, :], func=AF.Ln)
            S.i("act", "activation", [st4], [R(st4, "v")], out=st4[:, 3, :], in_=st4[:, 3, :], func=AF.Exp, scale=-0.5)
            for c in range(NCH):
                S.i("pe", "matmul", [FMK(4, c), g.cb], [PB[5]], out=PB[5][:, 2 * c:2 * c + 2], lhsT=FM(4, c, cs), rhs=hind, start=True, stop=True)
            S.i("dve", "tensor_copy", [PB[5]], [bsum], out=bsum[:], in_=PB[5][:, 0:16])
            for h in range(16):
                b = h % 2
                hs = slice(h * 64, (h + 1) * 64)
                S.i("dve", "tensor_scalar", [PB[2 + b], st4], [R(yn, h)], out=yn[:, hs], in0=PB[2 + b][:, (h // 2) * 64:(h // 2 + 1) * 64],
                    scalar1=st4[:, 2, h:h + 1], scalar2=st4[:, 3, h:h + 1], op0=ALU.subtract, op1=ALU.mult)
            S.i("dve", "tensor_tensor", [yn, rows], [yn], out=yn[:], in0=yn[:], in1=rows[:, 0, :], op=ALU.mult)
            S.i("dve", "tensor_tensor", [yn, rows], [yn], out=yn[:], in0=yn[:], in1=rows[:, 1, :], op=ALU.add)
            for h in range(16):
                hs = slice(h * 64, (h + 1) * 64)
                S.i("dve", "scalar_tensor_tensor", [Vtm, bsum, yn], [R(yn, h)], out=yn[:, hs], in0=Vtm[:, hs], scalar=bsum[:, h:h + 1], in1=yn[:, hs], op0=ALU.mult, op1=ALU.add)
            for b in range(2):
                S.i("pe", "matmul", [sg, g2], [PB[b]], out=PB[b][:], lhsT=sg[:, 0, cs], rhs=g2[:, 0, b * 512:(b + 1) * 512], start=True, stop=False)
                S.i("pe", "matmul", [sg, g2], [PB[b]], out=PB[b][:], lhsT=sg[0:32, 1, cs], rhs=g2[0:32, 1, b * 512:(b + 1) * 512], start=False, stop=True)
                S.i("dve", "tensor_tensor", [yn, PB[b]], [R(ogtm, b)], out=ogtm[:, b * 512:(b + 1) * 512], in0=yn[:, b * 512:(b + 1) * 512], in1=PB[b][:], op=ALU.mult)
            for c in range(NCH):
                S.i("pe", "transpose", [ogtm, g.cb], [pT], out=pT[:, c * 128:(c + 1) * 128], in_=ogtm[:, c * 128:(c + 1) * 128], identity=ident)
            for c in range(NCH):
                S.i("act", "activation", [pT], [R(ogT, (c, n))], out=ogT[:, c, cs], in_=pT[:, c * 128:(c + 1) * 128], func=AF.Copy)
        if stg < 11:
            continue
        for m in range(NCH):
            wo = wget()
            bk = PB[5 + (m % 2)]
            for k in range(NCH):
                S.i("pe", "matmul", [wo, ogT], [bk], out=bk[:, 0:TW], lhsT=wo[:, k, :], rhs=ogT[:, k, :], start=(k == 0), stop=(k == NCH - 1))
            xs = x[:, m, t0:t0 + TW]
            S.i("dve", "scalar_tensor_tensor", [bk, R(x, (m, tt))], [R(x, (m, tt))], out=xs, in0=xs, scalar=ALPHA, in1=bk[:, 0:TW], op0=ALU.mult, op1=ALU.add)
    S.pop()
    S.push()
    layer_norm(g, ("ln1_g%d" % i, "ln1_b%d" % i), range(4))
    S.pop()

def make_consts():
    c = np.zeros((128, 1024), np.float32)
    c[:, 0:128] = 1.0
    c[:, 128:256] = np.eye(128, dtype=np.float32)
    bo = np.zeros((128, 128), np.float32); bo[:64, :64] = 1.0; bo[64:, 64:] = 1.0
    c[:, 256:384] = bo
    t = np.arange(128)
    c[:, 384:512] = np.where(t[None, :] <= t[:, None], 0.0, -30000.0)
    c[:, 512:640] = (t[:, None] < t[None, :]).astype(np.float32)
    c[:, 640:768] = (t[:, None] <= t[None, :]).astype(np.float32)
    c[:64, 768] = 1.0; c[64:, 769] = 1.0
    return c


def make_ret_tables():
    dk = 256
    inv = (1.0 / (np.float32(10000.0) ** np.linspace(0.0, 1.0, dk // 2, dtype=np.float32))).astype(np.float32)
    pos = np.arange(T_SEQ, dtype=np.float32)
    ang = (pos[:, None] * inv[None, :]).astype(np.float32)
    cos = np.cos(ang).astype(np.float32); sin = np.sin(ang).astype(np.float32)
    f = np.arange(dk)
    cosf = cos[:, f // 2].T
    sgn = np.where(f % 2 == 0, -1.0, 1.0).astype(np.float32)
    sinf = (sin[:, f // 2].T * sgn[:, None]).astype(np.float32)
    tab = np.stack([cosf.reshape(2, 128, T_SEQ).transpose(1, 0, 2), sinf.reshape(2, 128, T_SEQ).transpose(1, 0, 2)])
    m = np.zeros((4, 128, 384), np.float32)
    idx = np.arange(128, dtype=np.float64)
    for h in range(4):
        lg = np.log(1.0 - 2.0 ** (-5.0 - h))
        rel = idx[None, :] - idx[:, None]
        m[h, :, 0:128] = np.where(rel >= 0, np.exp(lg * np.maximum(rel, 0)), 0.0) / 16.0
        m[h, :, 128:256] = np.exp(lg * (idx + 1.0))[None, :]
        m[h, :, 256] = np.exp(lg * (127.0 - idx)) / 16.0
        m[h, :, 257] = np.exp(lg * 128.0)
    return np.ascontiguousarray(tab.astype(np.float32)), m


FULL_PLAN = [("rwkv", 0, 0), ("ffn", 0), ("ret", 1), ("ffn", 1), ("moba", 2), ("ffn", 2), ("rwkv", 3, 1), ("ffn", 3)]
_PLAN = FULL_PLAN
_NC_CACHE = {}


def host_inputs(inputs):
    shared = {k: np.ascontiguousarray(np.asarray(inputs[k], np.float32)) for k in WEIGHT_SHAPES}
    shared["vecs"] = pack_vecs(inputs)
    shared["consts"] = make_consts()
    rows = np.zeros((4, 128, 1024), np.float32)
    for j in range(2):
        rows[2 * j] = np.broadcast_to(np.asarray(inputs["rwkv_gn_g"][j], np.float32)[None, :], (128, 1024))
        rows[2 * j + 1] = np.broadcast_to(np.asarray(inputs["rwkv_gn_b"][j], np.float32)[None, :], (128, 1024))
    shared["rows"] = rows
    ti = np.arange(128)
    mk = np.zeros((128, 3, 512), np.float32)
    mk[:, 0, :] = np.tile((ti[:, None] < ti[None, :]).astype(np.float32), (1, 4))
    mk[:, 1, :] = np.tile((ti[:, None] <= ti[None, :]).astype(np.float32), (1, 4))
    mk[:, 2, :] = np.tile((ti[:, None] > ti[None, :]).astype(np.float32), (1, 4))
    shared["mk"] = mk
    tab, msk = make_ret_tables()
    shared["rtab"] = tab
    shared["rmask"] = msk
    wqk = np.asarray(inputs["ret_w_in"][0][:, :2048], np.float32)
    shared["ret_sw"] = np.ascontiguousarray(wqk.reshape(1024, 1024, 2)[:, :, ::-1].reshape(1024, 2048))
    return shared


def run_plan(plan, inputs, x_full, n_cores=8):
    key = repr(plan)
    if key not in _NC_CACHE:
        _NC_CACHE[key] = build_program(plan)
    nc = _NC_CACHE[key]
    shared = host_inputs(inputs)
    in_maps = []
    for b in range(n_cores):
        m = dict(shared)
        m["xT"] = np.ascontiguousarray(np.asarray(x_full[b], np.float32).T)
        in_maps.append(m)
    res = run_bass_kernel_spmd(nc, in_maps, core_ids=list(range(n_cores)))
    return np.stack([np.ascontiguousarray(r["outT"].T) for r in res.results]).astype(np.float32)


def kernel(**inputs):
    return run_plan(_PLAN, inputs, inputs["x"], 8)
```

```python
import numpy as np
import concourse.bass as bass
import concourse.mybir as mybir
from concourse.bass_utils import run_bass_kernel_spmd
from contextlib import ExitStack

_DBG = {}
F32 = mybir.dt.float32
BF16 = mybir.dt.bfloat16
AF = mybir.ActivationFunctionType
ALU = mybir.AluOpType
AX = mybir.AxisListType

ENGS = ("pe", "act", "dve", "pool", "sp")


class T:
    def __init__(self, S, t, name):
        self.S = S
        self.t = t
        self.name = name
        self.st = {}
        self.is_psum = False
        self.dsem = None
        self.dcnt = 0

    def __getitem__(self, idx):
        return self.t[idx]


class R:
    __slots__ = ("tile", "key")
    def __init__(self, tile, key=None):
        self.tile = tile
        self.key = key


class Sched:
    def __init__(self, nc):
        self.nc = nc
        self.es = ExitStack()
        self.ins = {e: [] for e in ENGS}
        self.cnt = {e: 0 for e in ENGS}
        self.seen = {e: {} for e in ENGS}
        self.sems = {}
        self.needed = {e: set() for e in ENGS}
        self.nsem = 0
        for e in ENGS:
            self.sems[("e", e)] = self.es.enter_context(nc.semaphore("sem_" + e))
        self.dma_pool = []
        self.out_dma = []
        self.pend = {}
        self.scopes = []

    def _es(self):
        return self.scopes[-1] if self.scopes else self.es

    def push(self):
        self.scopes.append(ExitStack())

    def pop(self):
        self.barrier()
        self.scopes.pop().close()

    def sbuf(self, name, shape, dt):
        self.uid = getattr(self, "uid", 0) + 1
        name = "%s_%d" % (name, self.uid)
        t = self._es().enter_context(self.nc.sbuf_tensor(name, list(shape), dt))
        return T(self, t, name)

    def psum(self, name, shape, dt=F32):
        self.uid = getattr(self, "uid", 0) + 1
        name = "%s_%d" % (name, self.uid)
        t = self._es().enter_context(self.nc.psum_tensor(name, list(shape), dt))
        tt = T(self, t, name)
        tt.is_psum = True
        return tt

    def new_dsem(self, name):
        self.nsem += 1
        key = ("d", self.nsem)
        self.sems[key] = self.es.enter_context(self.nc.semaphore("dsem%d_%s" % (self.nsem, name)))
        return key

    def _states(self, ref, create=True):
        tile, key = ref.tile, ref.key
        if key is None:
            if None not in tile.st:
                tile.st[None] = [None, {}]
            return list(tile.st.values())
        out = []
        if None in tile.st:
            out.append(tile.st[None])
        if key not in tile.st:
            tile.st[key] = [None, {}]
        out.append(tile.st[key])
        return out

    def _collect(self, eng, reads, writes):
        waits = {}
        def need(w, same_ok=False):
            if w is None:
                return
            k, v = w
            if same_ok and k == ("e", eng) and eng == "pe":
                return
            if waits.get(k, 0) < v:
                waits[k] = v
        for r in reads:
            for st in self._states(r):
                need(st[0])
                if r.tile.is_psum:
                    for rk, rv in st[1].items():
                        if rk != ("e", eng):
                            need((rk, rv))
        for w in writes:
            for st in self._states(w):
                need(st[0])
                for rk, rv in st[1].items():
                    if rk == ("e", eng) and eng == "pe":
                        continue
                    need((rk, rv))
        final = []
        for k, v in waits.items():
            if k == ("e", eng) and eng == "pe":
                continue
            if self.seen[eng].get(k, 0) < v:
                self.seen[eng][k] = v
                final.append((k, v))
                if k[0] == "e":
                    self.needed[k[1]].add(v)
        return final

    def _commit(self, tag, reads, writes):
        for r in reads:
            if r.key is None:
                for k, s in r.tile.st.items():
                    if s[1].get(tag[0], 0) < tag[1]:
                        s[1][tag[0]] = tag[1]
            else:
                s = r.tile.st[r.key]
                if s[1].get(tag[0], 0) < tag[1]:
                    s[1][tag[0]] = tag[1]
        for w in writes:
            if w.key is None:
                w.tile.st = {None: [tag, {}]}
            else:
                w.tile.st[w.key] = [tag, {}]

    def i(self, eng, meth, reads=(), writes=(), **kw):
        return self.op(eng, (meth, kw), reads, writes)

    def _norm(self, refs):
        out = []
        for r in refs:
            if not isinstance(r, R):
                r = R(r)
            if r.tile.is_psum and r.key is not None:
                r = R(r.tile)
            out.append(r)
        return out

    def op(self, eng, fn, reads=(), writes=()):
        reads = self._norm(reads)
        writes = self._norm(writes)
        waits = self._collect(eng, reads, writes)
        self.cnt[eng] += 1
        idx = self.cnt[eng]
        self.ins[eng].append([fn, waits, idx, None])
        self._commit((("e", eng), idx), reads, writes)
        return idx

    def dma(self, q, out_ap, in_ap, reads=(), writes=(), out_final=False, owner=None, **kw):
        reads = [r if isinstance(r, R) else R(r) for r in reads]
        writes = [w if isinstance(w, R) else R(w) for w in writes]
        waits = self._collect(q, reads, writes)
        if owner is None:
            owner = (writes[0] if writes else reads[0]).tile
        if owner.dsem is None:
            owner.dsem = self.new_dsem(owner.name)
        owner.dcnt += 16
        tag = (owner.dsem, owner.dcnt)
        self.cnt[q] += 1
        idx = self.cnt[q]
        self.ins[q].append([lambda e: e.dma_start(out=out_ap, in_=in_ap, **kw), waits, idx, tag])
        self._commit(tag, reads, writes)
        self.pend[tag[0]] = tag[1]
        if out_final:
            self.out_dma.append(tag)
        return tag

    def barrier(self):
        for f in ENGS:
            if self.cnt[f] and (self.ins[f][-1][0] is None or self.ins[f][-1][3] is not None):
                self.cnt[f] += 1
                self.ins[f].append([None, [], self.cnt[f], None])
        for e in ENGS:
            waits = []
            for f in ENGS:
                if f == e or self.cnt[f] == 0:
                    continue
                v = self.cnt[f]
                if self.seen[e].get(("e", f), 0) < v:
                    self.seen[e][("e", f)] = v
                    waits.append((("e", f), v))
                    self.needed[f].add(v)
            for k, v in self.pend.items():
                if self.seen[e].get(k, 0) < v:
                    self.seen[e][k] = v
                    waits.append((k, v))
            if waits:
                self.cnt[e] += 1
                self.ins[e].append([None, waits, self.cnt[e], None])

    def emit(self):
        nc = self.nc
        fwd = {}
        for k, v in self.out_dma:
            fwd[k] = max(fwd.get(k, 0), v)
        fw = list(fwd.items())
        self.cnt["sp"] += 1
        self.ins["sp"].append([None, fw, self.cnt["sp"], None])
        rank = {}
        for e in ENGS:
            s = sorted(self.needed[e])
            rank[e] = {v: i + 1 for i, v in enumerate(s)}
        engmap = {"pe": "tensor", "act": "scalar", "dve": "vector", "pool": "gpsimd", "sp": "sync"}
        with nc.Block() as block:
            for e in ENGS:
                lst = self.ins[e]
                if not lst:
                    continue
                def body(eng, e=e, lst=lst):
                    for fn, waits, idx, dtag in lst:
                        for k, v in waits:
                            if k[0] == "e":
                                eng.wait_ge(self.sems[k], rank[k[1]][v])
                            else:
                                eng.wait_ge(self.sems[k], v)
                        if fn is None:
                            if idx in self.needed[e]:
                                eng.nop().then_inc(self.sems[("e", e)], 1)
                            continue
                        ins = fn(eng) if callable(fn) else getattr(eng, fn[0])(**fn[1])
                        if dtag is not None:
                            ins.then_inc(self.sems[dtag[0]], 16)
                            if idx in self.needed[e]:
                                raise RuntimeError("dma instr needed as engine milestone")
                        elif idx in self.needed[e]:
                            ins.then_inc(self.sems[("e", e)], 1)
                getattr(block, engmap[e])(body)
        self.es.close()

D = 1024; T_SEQ = 2048; DEPTH = 4; FF = 2816; NCH = 8; NPAIR = 22
ALPHA = float((2 * DEPTH) ** 0.25)
LN_EPS = 1e-5

WEIGHT_SHAPES = {
    "rwkv_w_rkv": [2, 3, 1024, 1024], "rwkv_w1": [2, 1024, 64], "rwkv_w2": [2, 64, 1024],
    "rwkv_a1": [2, 1024, 64], "rwkv_a2": [2, 64, 1024], "rwkv_g1": [2, 1024, 160], "rwkv_g2": [2, 160, 1024],
    "rwkv_w_o": [2, 1024, 1024], "ret_w_in": [1, 1024, 6144], "ret_w_o": [1, 2048, 1024],
    "moba_w_qkv": [1, 1024, 3072], "moba_w_o": [1, 1024, 1024],
    "ffn_w_in": [4, 1024, 5632], "ffn_w_out": [4, 2816, 1024],
}


def _fm(v):
    v = np.asarray(v, np.float32).reshape(-1, 128)
    return np.ascontiguousarray(v.T)


def vec_layout():
    off = {}
    n = 0
    def add(name, cols):
        nonlocal n
        off[name] = n
        n += cols
    for i in range(4):
        for nm in ("ln1_g", "ln1_b", "ln2_g", "ln2_b"):
            add("%s%d" % (nm, i), 8)
        for k in range(3):
            add("cw%d_%d" % (k, i), 44)
        add("cb_%d" % i, 44)
    for j in range(2):
        for m in range(6):
            add("mix%d_%d" % (m, j), 8)
        for nm in ("w0", "a0", "k_k", "k_a", "r_k"):
            add("%s_%d" % (nm, j), 8)
    add("ret_gn_g", 16)
    add("ret_gn_b", 16)
    return off, n


def pack_vecs(inp):
    off, n = vec_layout()
    out = np.zeros((128, n), np.float32)
    def put(name, v):
        a = _fm(v)
        out[:, off[name]:off[name] + a.shape[1]] = a
    for i in range(4):
        for nm in ("ln1_g", "ln1_b", "ln2_g", "ln2_b"):
            put("%s%d" % (nm, i), inp[nm][i])
        for k in range(3):
            put("cw%d_%d" % (k, i), inp["ffn_conv_w"][i, k])
        put("cb_%d" % i, inp["ffn_conv_b"][i])
    for j in range(2):
        for m in range(6):
            put("mix%d_%d" % (m, j), inp["rwkv_mix"][j, m])
        put("w0_%d" % j, inp["rwkv_w0"][j]); put("a0_%d" % j, inp["rwkv_a0"][j])
        put("k_k_%d" % j, inp["rwkv_k_k"][j]); put("k_a_%d" % j, inp["rwkv_k_a"][j])
        put("r_k_%d" % j, inp["rwkv_r_k"][j].reshape(-1))
    put("ret_gn_g", inp["ret_gn_g"][0]); put("ret_gn_b", inp["ret_gn_b"][0])
    return out


class Ctx:
    pass


def build_program(plan, x_in_bf=True):
    nc = bass.Bass("TRN2", target_bir_lowering=False)
    S = Sched(nc)
    g = Ctx()
    g.nc, g.S = nc, S
    g.W = {k: nc.dram_tensor(k, shp, F32, kind="ExternalInput").ap() for k, shp in WEIGHT_SHAPES.items()}
    voff, nv = vec_layout()
    g.voff = voff
    xT_d = nc.dram_tensor("xT", [D, T_SEQ], F32, kind="ExternalInput").ap()
    vecs_d = nc.dram_tensor("vecs", [128, nv], F32, kind="ExternalInput").ap()
    consts_d = nc.dram_tensor("consts", [128, 1024], F32, kind="ExternalInput").ap()
    g.rows_d = nc.dram_tensor("rows", [4, 128, 1024], F32, kind="ExternalInput").ap()
    g.rtab_d = nc.dram_tensor("rtab", [2, 128, 2, T_SEQ], F32, kind="ExternalInput").ap()
    g.rmask_d = nc.dram_tensor("rmask", [4, 128, 384], F32, kind="ExternalInput").ap()
    g.ret_sw_d = nc.dram_tensor("ret_sw", [1024, 2048], F32, kind="ExternalInput").ap()
    g.mk_d = nc.dram_tensor("mk", [128, 3, 512], F32, kind="ExternalInput").ap()
    out_d = nc.dram_tensor("outT", [D, T_SEQ], F32, kind="ExternalOutput").ap()

    g.x = S.sbuf("x", [128, NCH, T_SEQ], F32)
    g.xb = S.sbuf("xb", [128, NCH, T_SEQ], BF16)
    g.vecs = S.sbuf("vecs", [128, nv], F32)
    g.cf = S.sbuf("cf", [128, 256], F32)
    g.cb = S.sbuf("cb", [128, 512], BF16)
    xv = xT_d.rearrange("(c p) t -> p c t", p=128)
    for c in range(NCH):
        S.dma("sp", g.x[:, c, :], xv[:, c, :], writes=[R(g.x, None)])
    if plan[0][0] != "rwkv":
        for c in range(NCH):
            S.dma("pool", g.xb[:, c, :], xv[:, c, :], writes=[R(g.xb, None)])
    S.dma("sp", g.vecs[:], vecs_d, writes=[g.vecs])
    S.dma("sp", g.cf[:, 0:128], consts_d[:, 0:128], writes=[R(g.cf, None)])
    S.dma("sp", g.cf[:, 128:256], consts_d[:, 384:512], writes=[R(g.cf, None)])
    S.i("pool", "memset", [], [g.cb], ap=g.cb[:], constant=0.0)
    S.dma("pool", g.cb[:, 0:256], consts_d[:, 128:384], writes=[R(g.cb, None)])
    S.dma("pool", g.cb[:, 256:258], consts_d[:, 768:770], writes=[R(g.cb, None)])
    g.mk = S.sbuf("mk", [128, 3, 512], BF16)
    S.dma("pool", g.mk[:], g.mk_d, writes=[g.mk])
    g.ones32 = g.cf[:, 0:128]
    g.identb = g.cb[:, 0:128]

    for step in plan:
        kind = step[0]
        if kind == "ffn":
            ffn_phase(g, step[1])
        elif kind == "ln":
            S.push()
            layer_norm(g, step[1], range(4))
            S.pop()
        elif kind == "moba":
            moba_phase(g, step[1])
        elif kind == "ret":
            ret_phase(g, step[1])
        elif kind == "rwkv":
            rwkv_phase(g, step[1], step[2])
        else:
            raise ValueError(kind)

    S.barrier()
    ov = out_d.rearrange("(c p) t -> p c t", p=128)
    outsem = T(S, None, "outsem")
    for c in range(NCH):
        S.dma("sp", ov[:, c, :], g.x[:, c, :], reads=[R(g.x, None)], out_final=True)
    S.emit()
    return nc


def vcol(g, name, c=0):
    o = g.voff[name] + c
    return g.vecs[:, o:o + 1]


def layer_norm(g, pref, tts, width=512, write_xb=True):
    S = g.S
    gname, bname = pref
    x, xb = g.x, g.xb
    tts = list(tts)
    nt = len(tts)
    sq = [S.sbuf("lnsq", [128, 512], F32) for _ in range(2)]
    means = [S.sbuf("lnmean", [128, 512], F32) for _ in range(nt)]
    msqs = [S.sbuf("lnmsq", [128, 512], F32) for _ in range(2)]
    rstds = [S.sbuf("lnrstd", [128, 512], F32) for _ in range(nt)]
    tmp = [S.sbuf("lntmp", [128, 512], F32) for _ in range(3)]
    ps_ss = [S.psum("lnps", [128, 512]) for _ in range(2)]
    ps_qs = [S.psum("lnpq", [128, 512]) for _ in range(2)]
    ones = g.ones32
    W_ = width
    def stats(ii, tix):
        ts = slice(tix * W_, (tix + 1) * W_)
        tt = (tix * W_) // 512
        mean, msq, rstd, ps_s, ps_q = means[ii], msqs[ii % 2], rstds[ii], ps_ss[ii % 2], ps_qs[ii % 2]
        for c in range(NCH):
            q = sq[c % 2]
            S.i("act", "activation", [R(x, (c, tt))], [q], out=q[:, 0:W_], in_=x[:, c, ts], func=AF.Square)
            S.i("pe", "matmul", [R(x, (c, tt)), g.cf], [ps_s], out=ps_s[:, 0:W_], lhsT=ones, rhs=x[:, c, ts], start=(c == 0), stop=(c == NCH - 1))
            S.i("pe", "matmul", [q, g.cf], [ps_q], out=ps_q[:, 0:W_], lhsT=ones, rhs=q[:, 0:W_], start=(c == 0), stop=(c == NCH - 1))
        S.i("dve", "tensor_scalar", [ps_s], [mean], out=mean[:, 0:W_], in0=ps_s[:, 0:W_], scalar1=1.0 / D, scalar2=None, op0=ALU.mult)
        S.i("dve", "tensor_tensor", [mean], [msq], out=msq[:, 0:W_], in0=mean[:, 0:W_], in1=mean[:, 0:W_], op=ALU.mult)
        S.i("dve", "scalar_tensor_tensor", [ps_q, msq], [msq], out=msq[:, 0:W_], in0=ps_q[:, 0:W_], scalar=1.0 / D, in1=msq[:, 0:W_], op0=ALU.mult, op1=ALU.subtract)
        S.i("dve", "tensor_scalar", [msq], [msq], out=msq[:, 0:W_], in0=msq[:, 0:W_], scalar1=LN_EPS, scalar2=None, op0=ALU.add)
        S.i("act", "activation", [msq], [rstd], out=rstd[:, 0:W_], in_=msq[:, 0:W_], func=AF.Ln)
        S.i("act", "activation", [rstd], [rstd], out=rstd[:, 0:W_], in_=rstd[:, 0:W_], func=AF.Exp, scale=-0.5)
    def norm(ii, tix):
        ts = slice(tix * W_, (tix + 1) * W_)
        tt = (tix * W_) // 512
        mean, rstd = means[ii], rstds[ii]
        for c in range(NCH):
            tm = tmp[c % 3]
            S.i("dve", "tensor_tensor", [R(x, (c, tt)), mean], [tm], out=tm[:, 0:W_], in0=x[:, c, ts], in1=mean[:, 0:W_], op=ALU.subtract)
            S.i("dve", "tensor_tensor", [tm, rstd], [tm], out=tm[:, 0:W_], in0=tm[:, 0:W_], in1=rstd[:, 0:W_], op=ALU.mult)
            S.i("act", "activation", [tm, g.vecs], [R(x, (c, tt))], out=x[:, c, ts], in_=tm[:, 0:W_], func=AF.Identity,
                bias=vcol(g, bname, c), scale=vcol(g, gname, c))
            if write_xb:
                S.i("dve", "tensor_scalar", [tm, g.vecs], [R(xb, (c, tt))], out=xb[:, c, ts], in0=tm[:, 0:W_], scalar1=vcol(g, gname, c),
                    scalar2=vcol(g, bname, c), op0=ALU.mult, op1=ALU.add)
    stats(0, tts[0])
    for ii in range(nt):
        if ii + 1 < nt:
            stats(ii + 1, tts[ii + 1])
        norm(ii, tts[ii])


def ffn_phase(g, i):
    S = g.S
    x, xb = g.x, g.xb
    S.push()
    W_in = g.W["ffn_w_in"][i].rearrange("(k p) (two f) -> p k two f", p=128, two=2)
    W_out = g.W["ffn_w_out"][i].rearrange("(k p) n -> p k n", p=128)
    TB = 1024
    a = S.sbuf("ffa", [128, NPAIR, TB], BF16)
    hus = [S.sbuf("hu", [128, TB + 2], F32) for _ in range(2)]
    hgs = [S.sbuf("hg", [128, TB + 2], F32) for _ in range(2)]
    aus = [S.sbuf("au", [128, TB], F32) for _ in range(2)]
    ags = [S.sbuf("ag", [128, TB], F32) for _ in range(2)]
    halo = S.sbuf("halo", [128, 2 * NPAIR, 2], F32)
    wps = [[S.sbuf("wp", [128, NCH, 128], BF16) for _ in range(2)] for _ in range(2)]
    wos = [S.sbuf("wo", [128, NPAIR, 128], BF16) for _ in range(2)]
    pb = [S.psum("ffps", [128, 512]) for _ in range(8)]
    cw = lambda k, j: vcol(g, "cw%d_%d" % (k, i), j)
    cbv = lambda j: vcol(g, "cb_%d" % i, j)
    nw = 0
    for blk in range(2):
        for j in range(NPAIR):
            if j == 0:
                for ug in range(2):
                    S.dma("pool", wps[nw % 2][ug][:], W_in[:, :, ug, 0:128], writes=[wps[nw % 2][ug]])
            wp = wps[nw % 2]; nw += 1
            hu, hg, au, ag = hus[j % 2], hgs[j % 2], aus[j % 2], ags[j % 2]
            if j + 1 < NPAIR:
                for ug in range(2):
                    S.dma("pool", wps[nw % 2][ug][:], W_in[:, :, ug, (j + 1) * 128:(j + 2) * 128], writes=[wps[nw % 2][ug]])
            elif True:
                S.dma("pool", wos[0][:], W_out[:, :, 0:128], writes=[wos[0]])
            banks = pb[(j % 2) * 4:(j % 2) * 4 + 4]
            for ug in range(2):
                for h in range(2):
                    bk = banks[ug * 2 + h]
                    tt = blk * 2 + h
                    for k in range(NCH):
                        S.i("pe", "matmul", [wp[ug], R(xb, (k, tt))], [bk], out=bk[:], lhsT=wp[ug][:, k, :],
                            rhs=xb[:, k, tt * 512:(tt + 1) * 512], start=(k == 0), stop=(k == NCH - 1))
            for ug, hb in ((0, hu), (1, hg)):
                for h in range(2):
                    bk = banks[ug * 2 + h]
                    S.i("act", "activation", [bk], [R(hb, h)], out=hb[:, 2 + h * 512:2 + (h + 1) * 512], in_=bk[:], func=AF.Copy)
                if blk == 0:
                    S.i("dve", "memset", [], [R(hb, "halo")], ap=hb[:, 0:2], constant=0.0)
                else:
                    S.i("dve", "tensor_copy", [R(halo, (ug, j))], [R(hb, "halo")], out=hb[:, 0:2], in_=halo[:, ug * NPAIR + j, :])
            for (hb, ab_, jj) in ((hu, au, j), (hg, ag, NPAIR + j)):
                S.i("act", "activation", [hb, g.vecs], [ab_], out=ab_[:], in_=hb[:, 2:TB + 2], func=AF.Identity, bias=cbv(jj), scale=cw(2, jj))
                S.i("dve", "scalar_tensor_tensor", [hb, ab_, g.vecs], [ab_], out=ab_[:], in0=hb[:, 1:TB + 1], scalar=cw(1, jj), in1=ab_[:], op0=ALU.mult, op1=ALU.add)
                S.i("dve", "scalar_tensor_tensor", [hb, ab_, g.vecs], [ab_], out=ab_[:], in0=hb[:, 0:TB], scalar=cw(0, jj), in1=ab_[:], op0=ALU.mult, op1=ALU.add)
            S.i("act", "activation", [ag], [ag], out=ag[:], in_=ag[:], func=AF.Silu)
            S.i("dve", "tensor_tensor", [au, ag], [R(a, j)], out=a[:, j, :], in0=au[:], in1=ag[:], op=ALU.mult)
            if blk == 0:
                for ug, hb in ((0, hu), (1, hg)):
                    S.i("dve", "tensor_copy", [hb], [R(halo, (ug, j))], out=halo[:, ug * NPAIR + j, :], in_=hb[:, TB:TB + 2])
        for m in range(NCH):
            wo = wos[m % 2]
            if m + 1 < NCH:
                S.dma("pool", wos[(m + 1) % 2][:], W_out[:, :, (m + 1) * 128:(m + 2) * 128], writes=[wos[(m + 1) % 2]])
            for h in range(2):
                bk = pb[(m * 2 + h) % 8]
                tt = blk * 2 + h
                for k in range(NPAIR):
                    S.i("pe", "matmul", [wo, R(a, k)], [bk], out=bk[:], lhsT=wo[:, k, :], rhs=a[:, k, h * 512:(h + 1) * 512],
                        start=(k == 0), stop=(k == NPAIR - 1))
                xs = x[:, m, tt * 512:(tt + 1) * 512]
                S.i("dve", "scalar_tensor_tensor", [bk, R(x, (m, tt))], [R(x, (m, tt))], out=xs, in0=xs, scalar=ALPHA, in1=bk[:],
                    op0=ALU.mult, op1=ALU.add)
    S.pop()
    S.push()
    layer_norm(g, ("ln2_g%d" % i, "ln2_b%d" % i), range(4))
    S.pop()

def moba_phase(g, i):
    S = g.S
    x, xb = g.x, g.xb
    S.push()
    Wqkv = g.W["moba_w_qkv"][0].rearrange("(k p) n -> p k n", p=128)
    Wo = g.W["moba_w_o"][0].rearrange("(k p) n -> p k n", p=128)
    ogT = S.sbuf("ogT", [128, NCH, T_SEQ], BF16)
    wsl = [[S.sbuf("mw", [128, NCH, 128], BF16) for _ in range(3)] for _ in range(2)]
    qkv = [[S.sbuf("mqkv", [128, T_SEQ], BF16) for _ in range(3)] for _ in range(2)]
    vtms = [S.sbuf("vtm", [128, 16, 128], BF16) for _ in range(2)]
    ksum = S.sbuf("ksum", [128, 8], F32)
    kmean = S.sbuf("kmean", [128, 8], BF16)
    P = S.sbuf("mP", [128, T_SEQ], BF16)
    PT = S.sbuf("mPT", [128, 16, 128], BF16)
    sd = S.sbuf("msd", [128, 128], F32)
    g8 = S.sbuf("g8", [128, 8], F32)
    top8 = S.sbuf("top8", [128, 8], F32)
    mb = S.sbuf("mb", [128, 8], F32)
    b8 = S.sbuf("b8", [128, 8], F32)
    nm = S.sbuf("nm", [128, 4], F32)
    rs = S.sbuf("rs", [128, 12], F32)
    rinv = S.sbuf("rinv", [128, 2], F32)
    otm = S.sbuf("otm", [128, 128], BF16)
    wos = [S.sbuf("mwo", [128, NCH, 128], BF16) for _ in range(2)]
    sc = S.psum("msc", [128, 2048])
    pT = S.psum("mpT", [128, 1024], BF16)
    gps = S.psum("mgps", [128, 512])
    ov = S.psum("mov", [128, 512])
    pj = S.psum("mpj", [128, 512])
    tri = g.cf[:, 128:256]
    if _DBG.get('dbg_memset'):
        S.i('pool', 'memset', [], [ogT], ap=ogT[:], constant=0.0)
    ident = g.identb
    for c in range(_DBG.get('moba_pairs', NCH)):
        ws = wsl[c % 2]
        qT, kT, vT = qkv[c % 2]
        vtm = vtms[c % 2]
        for w in range(3):
            S.dma("pool", ws[w][:], Wqkv[:, :, w * 1024 + c * 128: w * 1024 + (c + 1) * 128], writes=[ws[w]])
        for w, dst in ((0, qT), (1, kT), (2, vT)):
            for tt in range(4):
                for k in range(NCH):
                    S.i("pe", "matmul", [ws[w], R(xb, (k, tt))], [pj], out=pj[:], lhsT=ws[w][:, k, :], rhs=xb[:, k, tt * 512:(tt + 1) * 512],
                        start=(k == 0), stop=(k == NCH - 1))
                if w == 0:
                    S.i("act", "activation", [pj], [R(dst, tt)], out=dst[:, tt * 512:(tt + 1) * 512], in_=pj[:], func=AF.Copy, scale=0.125)
                else:
                    S.i("act", "activation", [pj], [R(dst, tt)], out=dst[:, tt * 512:(tt + 1) * 512], in_=pj[:], func=AF.Copy)
                if w == 1 and _DBG.get('moba_stage', 9) >= 0.2:
                    S.i("dve", "tensor_reduce", [pj], [R(ksum, tt)], out=ksum[:, 2 * tt:2 * tt + 2],
                        in_=pj[:].rearrange("p (b k) -> p b k", b=2), axis=AX.X, op=ALU.add)
        if _DBG.get('moba_stage', 9) >= 0.2:
            S.i("dve", "tensor_scalar", [ksum], [kmean], out=kmean[:], in0=ksum[:], scalar1=1.0 / 256.0, scalar2=None, op0=ALU.mult)
        for b in range(2 if _DBG.get('moba_stage', 9) >= 0.3 else 0):
            for t8 in range(8):
                kt = b * 8 + t8
                S.i("pe", "transpose", [vT, g.cb], [pT], out=pT[:, t8 * 128:(t8 + 1) * 128], in_=vT[:, kt * 128:(kt + 1) * 128], identity=ident)
            S.i("dve", "tensor_copy", [pT], [R(vtm, b)], out=vtm[:, b * 8:(b + 1) * 8, :].rearrange("p a b -> p (a b)"), in_=pT[:])
        stg = _DBG.get('moba_stage', 9)
        for qt in _DBG.get('moba_qts', range(16)):
            if stg < 2:
                break
            qb = qt // 2
            nkt = qt + 1
            qs = slice(qt * 128, (qt + 1) * 128)
            for hh in range(2):
                ps = slice(hh * 64, (hh + 1) * 64)
                use_thr = qb >= 4
                if use_thr:
                    S.i("pe", "matmul", [qT, kmean], [gps], out=gps[:, 0:8], lhsT=qT[ps, qs], rhs=kmean[ps, 0:8], start=True, stop=True)
                    S.i("pool", "memset", [], [R(g8, "pad")], ap=g8[:, qb:8], constant=-1.0e30)
                    S.i("dve", "tensor_copy", [gps], [R(g8, "val")], out=g8[:, 0:qb], in_=gps[:, 0:qb])
                    S.i("dve", "max", [g8], [top8], out=top8[:], in_=g8[:])
                    S.i("dve", "tensor_scalar", [g8, top8], [mb], out=mb[:], in0=g8[:], scalar1=top8[:, 2:3], scalar2=30000.0,
                        op0=ALU.is_ge, op1=ALU.mult)
                ncol = nkt * 128
                for b in range((ncol + 511) // 512):
                    w_ = min(512, ncol - b * 512)
                    S.i("pe", "matmul", [qT, kT], [sc], out=sc[:, b * 512:b * 512 + w_], lhsT=qT[ps, qs], rhs=kT[ps, b * 512:b * 512 + w_],
                        start=True, stop=True)
                S.i("dve", "tensor_tensor", [sc, g.cf], [sd], out=sd[:], in0=sc[:, qt * 128:(qt + 1) * 128], in1=tri, op=ALU.add)
                S.i("dve", "tensor_reduce", [sd], [R(nm, 0)], out=nm[:, 0:1], in_=sd[:], axis=AX.X, op=ALU.max, negate=True)
                if qt > 0:
                    S.i("dve", "tensor_reduce", [sc], [R(nm, 1)], out=nm[:, 1:2], in_=sc[:, 0:qt * 128], axis=AX.X, op=ALU.max, negate=True)
                    S.i("dve", "tensor_tensor", [nm], [R(nm, 2)], out=nm[:, 2:3], in0=nm[:, 0:1], in1=nm[:, 1:2], op=ALU.min)
                    negm = nm[:, 2:3]
                else:
                    negm = nm[:, 0:1]
                if stg < 3:
                    continue
                if use_thr:
                    S.i("dve", "tensor_scalar", [mb, nm], [b8], out=b8[:], in0=mb[:], scalar1=negm, scalar2=-30000.0, op0=ALU.add, op1=ALU.add)
                npz = 0
                if use_thr:
                    for n in range(qb):
                        S.i("act", "activation", [sc, b8], [R(P, n), R(rs, npz)], out=P[:, n * 256:(n + 1) * 256], in_=sc[:, n * 256:(n + 1) * 256],
                            func=AF.Exp, bias=b8[:, n:n + 1], scale=1.0, accum_out=rs[:, npz:npz + 1])
                        npz += 1
                    if qt % 2 == 1:
                        S.i("act", "activation", [sc, nm], [R(P, "o"), R(rs, npz)], out=P[:, (qt - 1) * 128:qt * 128], in_=sc[:, (qt - 1) * 128:qt * 128],
                            func=AF.Exp, bias=negm, scale=1.0, accum_out=rs[:, npz:npz + 1])
                        npz += 1
                elif qt > 0:
                    S.i("act", "activation", [sc, nm], [R(P, "past"), R(rs, npz)], out=P[:, 0:qt * 128], in_=sc[:, 0:qt * 128],
                        func=AF.Exp, bias=negm, scale=1.0, accum_out=rs[:, npz:npz + 1])
                    npz += 1
                S.i("act", "activation", [sd, nm], [R(P, "d"), R(rs, npz)], out=P[:, qt * 128:(qt + 1) * 128], in_=sd[:],
                    func=AF.Exp, bias=negm, scale=1.0, accum_out=rs[:, npz:npz + 1])
                npz += 1
                S.i("dve", "tensor_reduce", [rs], [R(rs, 11)], out=rs[:, 11:12], in_=rs[:, 0:npz], axis=AX.X, op=ALU.add)
                S.i("dve", "reciprocal", [R(rs, 11)], [R(rinv, hh)], out=rinv[:, hh:hh + 1], in_=rs[:, 11:12])
                if stg < 4:
                    continue
                for b in range((nkt + 7) // 8):
                    n8 = min(8, nkt - b * 8)
                    for t8 in range(n8):
                        kt = b * 8 + t8
                        S.i("pe", "transpose", [P, g.cb], [pT], out=pT[:, t8 * 128:(t8 + 1) * 128], in_=P[:, kt * 128:(kt + 1) * 128], identity=ident)
                    S.i("dve", "tensor_copy", [pT], [R(PT, b)], out=PT[:, b * 8:b * 8 + n8, :].rearrange("p a b -> p (a b)"), in_=pT[:, 0:n8 * 128])
                for kt in range(nkt):
                    S.i("pe", "matmul", [PT, vtm], [R(ov, hh)], out=ov[:, hh * 64:(hh + 1) * 64], lhsT=PT[:, kt, :], rhs=vtm[:, kt, hh * 64:(hh + 1) * 64],
                        start=(kt == 0), stop=(kt == nkt - 1))
                S.i("dve", "tensor_scalar", [R(ov, hh), R(rinv, hh)], [R(otm, hh)], out=otm[:, hh * 64:(hh + 1) * 64], in0=ov[:, hh * 64:(hh + 1) * 64],
                    scalar1=rinv[:, hh:hh + 1], scalar2=None, op0=ALU.mult)
            if stg < 5:
                continue
            S.i("pe", "transpose", [otm, g.cb], [pT], out=pT[:, 0:128], in_=otm[:], identity=ident)
            S.i("act", "activation", [pT], [R(ogT, (c, qt))], out=ogT[:, c, qs], in_=pT[:, 0:128], func=AF.Copy)
    for m in range(NCH):
        wo = wos[m % 2]
        S.dma("pool", wo[:], Wo[:, :, m * 128:(m + 1) * 128], writes=[wo])
        for tt in range(4):
            for k in range(NCH):
                S.i("pe", "matmul", [wo, ogT], [pj], out=pj[:], lhsT=wo[:, k, :], rhs=ogT[:, k, tt * 512:(tt + 1) * 512], start=(k == 0), stop=(k == NCH - 1))
            xs = x[:, m, tt * 512:(tt + 1) * 512]
            S.i("dve", "scalar_tensor_tensor", [pj, R(x, (m, tt))], [R(x, (m, tt))], out=xs, in0=xs, scalar=ALPHA, in1=pj[:], op0=ALU.mult, op1=ALU.add)
    S.pop()
    S.push()
    layer_norm(g, ("ln1_g%d" % i, "ln1_b%d" % i), range(4))
    S.pop()

def ret_phase(g, i):
    S = g.S
    x, xb = g.x, g.xb
    S.push()
    Win = g.W["ret_w_in"][0].rearrange("(k p) n -> p k n", p=128)
    Wsw = g.ret_sw_d.rearrange("(k p) n -> p k n", p=128)
    Wo = g.W["ret_w_o"][0].rearrange("(k p) n -> p k n", p=128)
    wq = S.sbuf("rwq", [128, NCH, 256], BF16); wqs = S.sbuf("rwqs", [128, NCH, 256], BF16)
    wk = S.sbuf("rwk", [128, NCH, 256], BF16); wks = S.sbuf("rwks", [128, NCH, 256], BF16)
    wv = S.sbuf("rwv", [128, NCH, 512], BF16); wg = S.sbuf("rwg", [128, NCH, 512], BF16)
    wo = S.sbuf("rwo", [128, 4, 1024], BF16)
    rm = S.sbuf("rrm", [128, 384], F32)
    tabs = [S.sbuf("rtab", [128, 2, 2, 512], F32) for _ in range(2)]
    qrot = S.sbuf("qrot", [128, 2, 512], BF16); krot = S.sbuf("krot", [128, 2, 512], BF16)
    vT = S.sbuf("rvT", [128, 4, 512], BF16); sgT = S.sbuf("rsgT", [128, 4, 512], BF16)
    vtm = S.sbuf("rvtm", [128, 4, 512], BF16); ktm = S.sbuf("rktm", [128, 4, 256], BF16)
    og = S.sbuf("rog", [128, 4, 512], BF16)
    S32 = S.sbuf("rS32", [128, 2, 512], F32); Sb = S.sbuf("rSb", [128, 2, 512], BF16)
    t1 = S.sbuf("rt1", [128, 512], F32); t2 = S.sbuf("rt2", [128, 512], F32)
    ST = S.sbuf("rST", [128, 128], BF16); qcd = S.sbuf("rqcd", [128, 2, 128], BF16)
    osbs = [S.sbuf("rosb", [128, 512], F32) for _ in range(2)]; osqs = [S.sbuf("rosq", [128, 512], F32) for _ in range(2)]
    means = [S.sbuf("rmean", [128, 128], F32) for _ in range(2)]; msqs = [S.sbuf("rmsq", [128, 128], F32) for _ in range(2)]; rstds = [S.sbuf("rrstd", [128, 128], F32) for _ in range(2)]
    nchunk = 0
    pA = S.psum("rpA", [128, 512]); pB = S.psum("rpB", [128, 512])
    opss = [S.psum("rops", [128, 512]) for _ in range(2)]
    ups = [S.psum("rups", [128, 512]) for _ in range(2)]
    pT = S.psum("rpT", [128, 1024], BF16)
    pst = S.psum("rpst", [128, 512])
    sps = pst
    ident = g.identb
    ones = g.ones32
    ntab = 0
    for h in range(4):
        S.dma("pool", wq[:], Win[:, :, h * 256:(h + 1) * 256], writes=[wq])
        S.dma("pool", wqs[:], Wsw[:, :, h * 256:(h + 1) * 256], writes=[wqs])
        S.dma("pool", wk[:], Win[:, :, 1024 + h * 256:1024 + (h + 1) * 256], writes=[wk])
        S.dma("pool", wks[:], Wsw[:, :, 1024 + h * 256:1024 + (h + 1) * 256], writes=[wks])
        S.dma("pool", wv[:], Win[:, :, 2048 + h * 512:2048 + (h + 1) * 512], writes=[wv])
        S.dma("pool", wg[:], Win[:, :, 4096 + h * 512:4096 + (h + 1) * 512], writes=[wg])
        S.dma("pool", wo[:], Wo[:, h * 4:(h + 1) * 4, :], writes=[wo])
        S.dma("sp", rm[:], g.rmask_d[h], writes=[rm])
        S.i("pool", "memset", [], [S32], ap=S32[:], constant=0.0)
        S.i("pool", "memset", [], [Sb], ap=Sb[:], constant=0.0)
        for tt in range(4):
            ts = slice(tt * 512, (tt + 1) * 512)
            tab = tabs[ntab % 2]; ntab += 1
            for cs_ in range(2):
                S.dma("sp", tab[:, cs_, :, :], g.rtab_d[cs_][:, :, ts], writes=[R(tab, None)])
            for (wa, wb, dst) in ((wq, wqs, qrot), (wk, wks, krot)):
                for dc in range(2):
                    for k in range(NCH):
                        S.i("pe", "matmul", [wa, R(xb, (k, tt))], [pA], out=pA[:], lhsT=wa[:, k, dc * 128:(dc + 1) * 128], rhs=xb[:, k, ts],
                            start=(k == 0), stop=(k == NCH - 1))
                    for k in range(NCH):
                        S.i("pe", "matmul", [wb, R(xb, (k, tt))], [pB], out=pB[:], lhsT=wb[:, k, dc * 128:(dc + 1) * 128], rhs=xb[:, k, ts],
                            start=(k == 0), stop=(k == NCH - 1))
                    S.i("dve", "tensor_tensor", [pA, tab], [t1], out=t1[:], in0=pA[:], in1=tab[:, 0, dc, :], op=ALU.mult)
                    S.i("dve", "tensor_tensor", [pB, tab], [t2], out=t2[:], in0=pB[:], in1=tab[:, 1, dc, :], op=ALU.mult)
                    S.i("pool", "tensor_tensor", [t1, t2], [R(dst, dc)], out=dst[:, dc, :], in0=t1[:], in1=t2[:], op=ALU.add)
            for ec in range(4):
                for k in range(NCH):
                    S.i("pe", "matmul", [wv, R(xb, (k, tt))], [pA], out=pA[:], lhsT=wv[:, k, ec * 128:(ec + 1) * 128], rhs=xb[:, k, ts],
                        start=(k == 0), stop=(k == NCH - 1))
                S.i("act", "activation", [pA], [R(vT, ec)], out=vT[:, ec, :], in_=pA[:], func=AF.Copy)
                for k in range(NCH):
                    S.i("pe", "matmul", [wg, R(xb, (k, tt))], [pB], out=pB[:], lhsT=wg[:, k, ec * 128:(ec + 1) * 128], rhs=xb[:, k, ts],
                        start=(k == 0), stop=(k == NCH - 1))
                S.i("act", "activation", [pB], [R(sgT, ec)], out=sgT[:, ec, :], in_=pB[:], func=AF.Silu)
            for n in range(4):
                for ec in range(4):
                    idx = (n % 2) * 4 + ec
                    S.i("pe", "transpose", [vT, g.cb], [pT], out=pT[:, idx * 128:(idx + 1) * 128], in_=vT[:, ec, n * 128:(n + 1) * 128], identity=ident)
                if n % 2 == 1:
                    S.i("dve", "tensor_copy", [pT], [R(vtm, n // 2)], out=vtm[:, n - 1:n + 1, :].rearrange("p a b -> p (a b)"), in_=pT[:])
            for n in range(4):
                for dc in range(2):
                    idx = n * 2 + dc
                    S.i("pe", "transpose", [krot, g.cb], [pT], out=pT[:, idx * 128:(idx + 1) * 128], in_=krot[:, dc, n * 128:(n + 1) * 128], identity=ident)
            S.i("dve", "tensor_scalar", [pT, rm], [ktm], out=ktm[:].rearrange("p a b -> p (a b)"), in0=pT[:], scalar1=rm[:, 256:257], scalar2=None, op0=ALU.mult)
            def partA(n):
                cs = slice(n * 128, (n + 1) * 128)
                opsn = opss[n % 2]
                for dc in range(2):
                    S.i("pe", "matmul", [krot, qrot], [sps], out=sps[:, 256:384], lhsT=krot[:, dc, cs], rhs=qrot[:, dc, cs], start=(dc == 0), stop=(dc == 1))
                S.i("dve", "tensor_tensor", [sps, rm], [ST], out=ST[:], in0=sps[:, 256:384], in1=rm[:, 0:128], op=ALU.mult)
                S.i("pool", "tensor_tensor", [qrot, rm], [qcd], out=qcd[:], in0=qrot[:, :, cs], in1=rm[:, 128:256].unsqueeze(1).to_broadcast([128, 2, 128]), op=ALU.mult)
                for ec in range(4):
                    es = slice(ec * 128, (ec + 1) * 128)
                    S.i("pe", "matmul", [vtm, ST], [opsn], out=opsn[:, es], lhsT=vtm[:, n, es], rhs=ST[:], start=True, stop=False)
                    for dc in range(2):
                        S.i("pe", "matmul", [Sb, qcd], [opsn], out=opsn[:, es], lhsT=Sb[:, dc, es], rhs=qcd[:, dc, :], start=False, stop=(dc == 1))
                for dc in range(2):
                    S.i("pe", "matmul", [ktm, vtm], [ups[dc]], out=ups[dc][:], lhsT=ktm[:, n, dc * 128:(dc + 1) * 128], rhs=vtm[:, n, :], start=True, stop=True)
                    S.i("dve", "scalar_tensor_tensor", [S32, ups[dc], rm], [R(S32, dc)], out=S32[:, dc, :], in0=S32[:, dc, :], scalar=rm[:, 257:258], in1=ups[dc][:],
                        op0=ALU.mult, op1=ALU.add)
                    S.i("act", "activation", [R(S32, dc)], [R(Sb, dc)], out=Sb[:, dc, :], in_=S32[:, dc, :], func=AF.Copy)

            def partB(n):
                cs = slice(n * 128, (n + 1) * 128)
                opsn = opss[n % 2]
                osb, osq, mean, msq, rstd = osbs[n % 2], osqs[n % 2], means[n % 2], msqs[n % 2], rstds[n % 2]
                S.i("act", "activation", [opsn], [osb], out=osb[:], in_=opsn[:], func=AF.Copy)
                S.i("act", "activation", [opsn], [osq], out=osq[:], in_=opsn[:], func=AF.Square)
                for ec in range(4):
                    S.i("pe", "matmul", [osb, g.cf], [R(pst, 0)], out=pst[:, 0:128], lhsT=ones, rhs=osb[:, ec * 128:(ec + 1) * 128], start=(ec == 0), stop=(ec == 3))
                for ec in range(4):
                    S.i("pe", "matmul", [osq, g.cf], [R(pst, 1)], out=pst[:, 128:256], lhsT=ones, rhs=osq[:, ec * 128:(ec + 1) * 128], start=(ec == 0), stop=(ec == 3))
                S.i("dve", "tensor_scalar", [R(pst, 0)], [mean], out=mean[:], in0=pst[:, 0:128], scalar1=1.0 / 512, scalar2=None, op0=ALU.mult)
                S.i("dve", "tensor_tensor", [mean], [msq], out=msq[:], in0=mean[:], in1=mean[:], op=ALU.mult)
                S.i("dve", "scalar_tensor_tensor", [R(pst, 1), msq], [msq], out=msq[:], in0=pst[:, 128:256], scalar=1.0 / 512, in1=msq[:], op0=ALU.mult, op1=ALU.subtract)
                S.i("dve", "tensor_scalar", [msq], [msq], out=msq[:], in0=msq[:], scalar1=1e-5, scalar2=None, op0=ALU.add)
                S.i("act", "activation", [msq], [rstd], out=rstd[:], in_=msq[:], func=AF.Ln)
                S.i("act", "activation", [rstd], [rstd], out=rstd[:], in_=rstd[:], func=AF.Exp, scale=-0.5)
                mb_ = mean[:].unsqueeze(1).to_broadcast([128, 4, 128])
                rb_ = rstd[:].unsqueeze(1).to_broadcast([128, 4, 128])
                o3 = osb[:].rearrange("p (a b) -> p a b", b=128)
                S.i("dve", "tensor_tensor", [osb, mean], [osb], out=o3, in0=o3, in1=mb_, op=ALU.subtract)
                S.i("dve", "tensor_tensor", [osb, rstd], [osb], out=o3, in0=o3, in1=rb_, op=ALU.mult)
                for ec in range(4):
                    col = h * 4 + ec
                    S.i("act", "activation", [osb, g.vecs], [R(osq, ec)], out=osq[:, ec * 128:(ec + 1) * 128], in_=osb[:, ec * 128:(ec + 1) * 128], func=AF.Identity,
                        bias=vcol(g, "ret_gn_b", col), scale=vcol(g, "ret_gn_g", col))
                S.i("dve", "tensor_tensor", [osq, sgT], [R(og, n)], out=og[:, :, cs], in0=osq[:].rearrange("p (a b) -> p a b", b=128), in1=sgT[:, :, cs], op=ALU.mult)

            partA(0)
            for n in range(4):
                if n + 1 < 4:
                    partA(n + 1)
                partB(n)
            for m in range(NCH):
                pw = pA if m % 2 == 0 else pB
                for k in range(4):
                    S.i("pe", "matmul", [wo, og], [pw], out=pw[:], lhsT=wo[:, k, m * 128:(m + 1) * 128], rhs=og[:, k, :], start=(k == 0), stop=(k == 3))
                xs = x[:, m, ts]
                if h == 0:
                    S.i("dve", "scalar_tensor_tensor", [pw, R(x, (m, tt))], [R(x, (m, tt))], out=xs, in0=xs, scalar=ALPHA, in1=pw[:], op0=ALU.mult, op1=ALU.add)
                else:
                    S.i("dve", "tensor_tensor", [pw, R(x, (m, tt))], [R(x, (m, tt))], out=xs, in0=xs, in1=pw[:], op=ALU.add)
    S.pop()
    S.push()
    layer_norm(g, ("ln1_g%d" % i, "ln1_b%d" % i), range(4))
    S.pop()

def rwkv_phase(g, i, j):
    S = g.S
    x, xb = g.x, g.xb
    S.push()
    TW = 256
    NT_ = T_SEQ // TW
    W = g.W
    Wrkv = [W["rwkv_w_rkv"][j, w].rearrange("(k p) n -> p k n", p=128) for w in range(3)]
    Wo = W["rwkv_w_o"][j].rearrange("(k p) n -> p k n", p=128)
    V = lambda name, c=0: vcol(g, "%s_%d" % (name, j), c)
    w1 = S.sbuf("w1", [128, NCH, 64], BF16); a1 = S.sbuf("a1", [128, NCH, 64], BF16); g1 = S.sbuf("g1", [128, NCH, 160], BF16)
    w2 = S.sbuf("w2", [64, 1024], BF16); a2 = S.sbuf("a2", [64, 1024], BF16); g2 = S.sbuf("g2", [128, 2, 1024], BF16)
    S.dma("pool", w1[:], W["rwkv_w1"][j].rearrange("(k p) n -> p k n", p=128), writes=[w1])
    S.dma("pool", a1[:], W["rwkv_a1"][j].rearrange("(k p) n -> p k n", p=128), writes=[a1])
    S.dma("pool", g1[:], W["rwkv_g1"][j].rearrange("(k p) n -> p k n", p=128), writes=[g1])
    S.dma("pool", w2[:], W["rwkv_w2"][j], writes=[w2])
    S.dma("pool", a2[:], W["rwkv_a2"][j], writes=[a2])
    S.dma("pool", g2[:, 0, :], W["rwkv_g2"][j][0:128, :], writes=[R(g2, None)])
    S.dma("pool", g2[0:32, 1, :], W["rwkv_g2"][j][128:160, :], writes=[R(g2, None)])
    rows = S.sbuf("gnrows", [128, 2, 1024], BF16)
    S.dma("pool", rows[:, 0, :], g.rows_d[2 * j], writes=[R(rows, None)])
    S.dma("pool", rows[:, 1, :], g.rows_d[2 * j + 1], writes=[R(rows, None)])
    om = S.sbuf("om", [128, 72], F32)
    mo = g.voff["mix0_%d" % j]
    S.i("dve", "tensor_scalar", [g.vecs], [om], out=om[:, 0:48], in0=g.vecs[:, mo:mo + 48], scalar1=-1.0, scalar2=1.0, op0=ALU.mult, op1=ALU.add)
    ko = g.voff["k_a_%d" % j]
    S.i("dve", "tensor_scalar", [g.vecs], [om], out=om[:, 48:56], in0=g.vecs[:, ko:ko + 8], scalar1=-1.0, scalar2=1.0, op0=ALU.mult, op1=ALU.add)
    ao = g.voff["a0_%d" % j]; wo_ = g.voff["w0_%d" % j]
    S.i("dve", "tensor_scalar", [g.vecs], [om], out=om[:, 56:64], in0=g.vecs[:, ao:ao + 8], scalar1=0.5, scalar2=None, op0=ALU.mult)
    S.i("dve", "tensor_scalar", [g.vecs], [om], out=om[:, 64:72], in0=g.vecs[:, wo_:wo_ + 8], scalar1=0.5, scalar2=None, op0=ALU.mult)
    xx = S.sbuf("xx", [128, NCH, TW], F32)
    xlast = S.sbuf("xlast", [128, NCH, 2], F32)
    S.i("pool", "memset", [], [xlast], ap=xlast[:], constant=0.0)
    hw = S.sbuf("hw", [64, TW], BF16); ha = S.sbuf("ha", [64, TW], BF16); sg = S.sbuf("sg", [128, 2, TW], BF16); sgf = S.sbuf("sgf", [128, 2, TW], BF16)
    NSL = _DBG.get('rw_nsl', 4)
    wsl = [S.sbuf("rwsl", [128, NCH, 128], BF16) for _ in range(NSL)]
    wseq = []
    for tix_ in range(NT_):
        for c_ in range(NCH):
            wseq.append(Wrkv[0][:, :, c_ * 128:(c_ + 1) * 128]); wseq.append(Wrkv[1][:, :, c_ * 128:(c_ + 1) * 128])
        for c_ in range(NCH):
            wseq.append(Wrkv[2][:, :, c_ * 128:(c_ + 1) * 128])
        for m_ in range(NCH):
            wseq.append(Wo[:, :, m_ * 128:(m_ + 1) * 128])
    wst = {"issued": 0, "next": 0}
    def wget():
        i_ = wst["next"]; wst["next"] += 1
        while wst["issued"] < min(len(wseq), i_ + NSL - 1):
            q_ = wst["issued"]
            S.dma("pool", wsl[q_ % NSL][:], wseq[q_], writes=[wsl[q_ % NSL]])
            wst["issued"] += 1
        return wsl[i_ % NSL]
    TS = [dict(), dict()]
    for nm in ("tA", "tsw", "tkk", "tcs", "te1", "te2", "te3"):
        for q_ in range(2):
            TS[q_][nm] = S.sbuf(nm, [128, TW], F32)
    for nm in ("trn", "tt"):
        t_ = S.sbuf(nm, [128, TW], F32)
        TS[0][nm] = t_; TS[1][nm] = t_
    for q_ in range(2):
        TS[q_]["tkk2"] = S.sbuf("tkk2", [128, TW], BF16)
    PC = S.sbuf("PC", [128, NCH, 2], F32)
    def FM(a, c, sl):
        return xb[:, c, a * TW + sl.start:a * TW + sl.stop]
    FMK = lambda a, c: R(xb, ("fm", a, c))
    Lk = [[S.sbuf("Lk", [128, 4, 128], BF16) for _ in range(2)] for _ in range(2)]
    Mk = [[S.sbuf("Mk", [128, 4, 128], BF16) for _ in range(2)] for _ in range(2)]
    NTt = [[S.sbuf("NT", [128, 4, 128], BF16) for _ in range(2)] for _ in range(2)]
    Mak = [S.sbuf("Mak", [128, 4, 128], BF16) for _ in range(2)]
    Mrb = S.sbuf("Mrb", [128, 16, 128], BF16); Mrk = S.sbuf("Mrk", [128, 16, 128], BF16)
    Atm = S.sbuf("Atm", [128, 1024], BF16); Btm = S.sbuf("Btm", [128, 1024], BF16); Ktm = S.sbuf("Ktm", [128, 1024], BF16); Vtm = S.sbuf("Vtm", [128, 1024], BF16)
    AhT = S.sbuf("AhT", [128, NCH, 128], BF16); Xb = S.sbuf("Xb", [128, 256], BF16); Vhat = S.sbuf("Vhat", [128, 1024], BF16)
    Ub = S.sbuf("Ub", [128, 1024], BF16); ST = S.sbuf("ST", [128, NCH, 64], BF16); STs = S.sbuf("STs", [128, NCH, 64], F32)
    yn = S.sbuf("yn", [128, 1024], F32); st4 = S.sbuf("st4", [128, 4, 16], F32); bsum = S.sbuf("bsum", [128, 16], F32)
    ogtm = S.sbuf("ogtm", [128, 1024], BF16); ogT = S.sbuf("ogT", [128, NCH, TW], BF16)
    PB = [S.psum("rp", [128, 512]) for _ in range(7)]
    pT = S.psum("rpT", [128, 1024], BF16)
    ident = g.identb
    blk1 = g.cb[:, 128:256]
    hind = g.cb[:, 256:258]
    ones = g.ones32
    S.i("pool", "memset", [], [ST], ap=ST[:], constant=0.0)
    nws = 0
    stg = _DBG.get('rw_stage', 99)
    for tix in range(_DBG.get('rw_tiles', NT_)):
        t0 = tix * TW
        tt = t0 // 512
        S.i("dve", "tensor_tensor", [R(x, None)], [R(xx, "m")], out=xx[:, :, 1:TW], in0=x[:, :, t0:t0 + TW - 1], in1=x[:, :, t0 + 1:t0 + TW], op=ALU.subtract)
        S.i("dve", "tensor_tensor", [R(x, None), xlast], [R(xx, "0")], out=xx[:, :, 0:1], in0=xlast[:, :, tix % 2:tix % 2 + 1], in1=x[:, :, t0:t0 + 1], op=ALU.subtract)
        S.i("dve", "tensor_copy", [R(x, None)], [xlast], out=xlast[:, :, (tix + 1) % 2:(tix + 1) % 2 + 1], in_=x[:, :, t0 + TW - 1:t0 + TW])
        def mix(m, dst):
            for c in range(NCH):
                mc = vcol(g, "mix%d_%d" % (m, j), c)
                S.i("dve", "scalar_tensor_tensor", [R(x, (c, tt)), xx, g.vecs], [dst.key(c)], out=dst.ap(c), in0=xx[:, c, :], scalar=mc, in1=x[:, c, t0:t0 + TW],
                    op0=ALU.mult, op1=ALU.add)
        nx = [0]
        class XM:
            def __init__(self, slot):
                self.slot = slot
            def ap(self, c):
                return xb[:, c, 1536 + self.slot * TW:1536 + (self.slot + 1) * TW]
            def key(self, c):
                return R(xb, ("xm", self.slot, c))
        def nxm():
            nx[0] += 1
            return XM(nx[0] % 2)
        d_ = nxm(); mix(1, d_)
        for k in range(NCH):
            S.i("pe", "matmul", [w1, d_.key(k)], [PB[0]], out=PB[0][0:64, 0:TW], lhsT=w1[:, k, :], rhs=d_.ap(k), start=(k == 0), stop=(k == NCH - 1))
        S.i("act", "activation", [PB[0]], [hw], out=hw[:], in_=PB[0][0:64, 0:TW], func=AF.Tanh)
        d_ = nxm(); mix(4, d_)
        for k in range(NCH):
            S.i("pe", "matmul", [a1, d_.key(k)], [PB[1]], out=PB[1][0:64, 0:TW], lhsT=a1[:, k, :], rhs=d_.ap(k), start=(k == 0), stop=(k == NCH - 1))
        S.i("act", "activation", [PB[1]], [ha], out=ha[:], in_=PB[1][0:64, 0:TW], func=AF.Copy)
        d_ = nxm(); mix(5, d_)
        for (lo, hi, kc) in ((0, 128, 0), (128, 160, 1)):
            for k in range(NCH):
                S.i("pe", "matmul", [g1, d_.key(k)], [PB[2]], out=PB[2][0:hi - lo, 0:TW], lhsT=g1[:, k, lo:hi], rhs=d_.ap(k), start=(k == 0), stop=(k == NCH - 1))
            S.i("act", "activation", [PB[2]], [R(sgf, kc)], out=sgf[0:hi - lo, kc, :], in_=PB[2][0:hi - lo, 0:TW], func=AF.Tanh, scale=0.5)
            S.i("dve", "tensor_scalar", [R(sgf, kc)], [R(sg, kc)], out=sg[0:hi - lo, kc, :], in0=sgf[0:hi - lo, kc, :], scalar1=0.5, scalar2=0.5, op0=ALU.mult, op1=ALU.add)
        xr = nxm(); mix(0, xr)
        xk = nxm(); mix(2, xk)
        if stg < 3:
            continue
        for c in range(NCH):
            wr = wget()
            wk_ = wget()
            d = TS[c % 2]
            tA, tsw, tcs, te1, te2, te3, tkk, tkk2, trn, tt_ = (d[k_] for k_ in ("tA", "tsw", "tcs", "te1", "te2", "te3", "tkk", "tkk2", "trn", "tt"))
            td3 = tsw; tkkn = tkk; ttb = tkk; tkm = tt_
            if c % 2 == 0:
                rp, kp, zw, za, ssp = PB[0], PB[1], PB[2], PB[3], PB[4]
            else:
                rp, kp, zw, za, ssp = PB[5], PB[6], PB[2], PB[3], PB[4]
            for k in range(NCH):
                S.i("pe", "matmul", [wr, xr.key(k)], [rp], out=rp[:, 0:TW], lhsT=wr[:, k, :], rhs=xr.ap(k), start=(k == 0), stop=(k == NCH - 1))
            for k in range(NCH):
                S.i("pe", "matmul", [wk_, xk.key(k)], [kp], out=kp[:, 0:TW], lhsT=wk_[:, k, :], rhs=xk.ap(k), start=(k == 0), stop=(k == NCH - 1))
            S.i("pe", "matmul", [w2, hw], [zw], out=zw[:, 0:TW], lhsT=w2[:, c * 128:(c + 1) * 128], rhs=hw[:], start=True, stop=True)
            S.i("pe", "matmul", [a2, ha], [za], out=za[:, 0:TW], lhsT=a2[:, c * 128:(c + 1) * 128], rhs=ha[:], start=True, stop=True)
            S.i("act", "activation", [za, om], [tA], out=tA[:], in_=za[:, 0:TW], func=AF.Tanh, bias=om[:, 56 + c:57 + c], scale=0.5)
            S.i("act", "activation", [zw, om], [tsw], out=tsw[:], in_=zw[:, 0:TW], func=AF.Tanh, bias=om[:, 64 + c:65 + c], scale=0.5)
            S.i("dve", "tensor_scalar", [tA], [tA], out=tA[:], in0=tA[:], scalar1=0.5, scalar2=0.5, op0=ALU.mult, op1=ALU.add)
            S.i("dve", "tensor_scalar", [tsw], [tsw], out=tsw[:], in0=tsw[:], scalar1=0.5, scalar2=0.5, op0=ALU.mult, op1=ALU.add)
            for n in range(2):
                cs = slice(n * 128, (n + 1) * 128)
                S.i("dve", "tensor_tensor_scan", [tsw, g.cf], [R(tcs, n)], out=tcs[:, cs], data0=ones, data1=tsw[:, cs], initial=0.0, op0=ALU.mult, op1=ALU.add)
            S.i("dve", "tensor_tensor", [tcs, tsw], [td3], out=td3[:], in0=tcs[:], in1=tsw[:], op=ALU.subtract)
            LD = 0.6065306597126334
            S.i("act", "activation", [tcs], [te1], out=te1[:], in_=tcs[:], func=AF.Exp, scale=-LD)
            S.i("act", "activation", [tcs], [te2], out=te2[:], in_=tcs[:], func=AF.Exp, scale=LD)
            S.i("act", "activation", [td3], [te3], out=te3[:], in_=td3[:], func=AF.Exp, scale=-LD)
            S.i("act", "activation", [te1], [R(PC, c)], out=PC[:, c, :], in_=te1[:, 127:TW:128], func=AF.Copy)
            S.i("dve", "tensor_scalar", [kp, g.vecs], [tkk], out=tkk[:], in0=kp[:, 0:TW], scalar1=V("k_k", c), scalar2=None, op0=ALU.mult)
            S.i("act", "activation", [tkk], [tkk2], out=tkk2[:], in_=tkk[:], func=AF.Square)
            S.i("pe", "matmul", [tkk2, g.cb], [ssp], out=ssp[:, 0:TW], lhsT=blk1, rhs=tkk2[:], start=True, stop=True)
            S.i("dve", "tensor_scalar", [ssp], [trn], out=trn[:], in0=ssp[:, 0:TW], scalar1=1e-24, scalar2=None, op0=ALU.max)
            S.i("act", "activation", [trn], [trn], out=trn[:], in_=trn[:], func=AF.Ln)
            S.i("act", "activation", [trn], [trn], out=trn[:], in_=trn[:], func=AF.Exp, scale=-0.5)
            S.i("dve", "tensor_tensor", [tkk, trn], [tkkn], out=tkkn[:], in0=tkk[:], in1=trn[:], op=ALU.mult)
            S.i("dve", "tensor_scalar", [tA, g.vecs, om], [tt_], out=tt_[:], in0=tA[:], scalar1=V("k_a", c), scalar2=om[:, 48 + c:49 + c], op0=ALU.mult, op1=ALU.add)
            S.i("dve", "tensor_tensor", [kp, tt_], [tkm], out=tkm[:], in0=kp[:, 0:TW], in1=tt_[:], op=ALU.mult)
            full = slice(0, TW)
            S.i("dve", "scalar_tensor_tensor", [tkkn, te3], [FMK(0, c)], out=FM(0, c, full), in0=tkkn[:], scalar=-1.0, in1=te3[:], op0=ALU.mult, op1=ALU.mult)
            S.i("dve", "tensor_tensor", [tkkn, tA], [ttb], out=ttb[:], in0=tkkn[:], in1=tA[:], op=ALU.mult)
            S.i("dve", "tensor_tensor", [ttb, te2], [FMK(1, c)], out=FM(1, c, full), in0=ttb[:], in1=te2[:], op=ALU.mult)
            S.i("dve", "tensor_tensor", [tkm, te2], [FMK(2, c)], out=FM(2, c, full), in0=tkm[:], in1=te2[:], op=ALU.mult)
            S.i("dve", "tensor_tensor", [rp, te1], [FMK(3, c)], out=FM(3, c, full), in0=rp[:, 0:TW], in1=te1[:], op=ALU.mult)
            S.i("dve", "scalar_tensor_tensor", [rp, g.vecs, tkm], [FMK(4, c)], out=FM(4, c, full), in0=rp[:, 0:TW], scalar=V("r_k", c), in1=tkm[:], op0=ALU.mult, op1=ALU.mult)
        xv = nxm(); mix(3, xv)
        for c in range(NCH):
            wv_ = wget()
            vp = PB[5 + (c % 2)]
            for k in range(NCH):
                S.i("pe", "matmul", [wv_, xv.key(k)], [vp], out=vp[:, 0:TW], lhsT=wv_[:, k, :], rhs=xv.ap(k), start=(k == 0), stop=(k == NCH - 1))
            S.i("act", "activation", [vp], [FMK(5, c)], out=FM(5, c, slice(0, TW)), in_=vp[:, 0:TW], func=AF.Copy)
        if stg < 5:
            continue
        for n in range(2):
            cs = slice(n * 128, (n + 1) * 128)
            for a_, dst in ((0, Atm), (1, Btm), (2, Ktm), (5, Vtm)):
                for c in range(NCH):
                    S.i("pe", "transpose", [FMK(a_, c), g.cb], [pT], out=pT[:, c * 128:(c + 1) * 128], in_=FM(a_, c, cs), identity=ident)
                S.i("dve" if a_ in (0, 2) else "act", "tensor_copy" if a_ in (0, 2) else "activation", [pT], [dst],
                    **(dict(out=dst[:], in_=pT[:]) if a_ in (0, 2) else dict(out=dst[:], in_=pT[:], func=AF.Copy)))
            if stg < 6:
                continue
            for hgp in range(2):
                grp = [2 * hgp, 2 * hgp + 1]
                def HP(h):
                    return slice((h % 2) * 64, (h % 2) * 64 + 64), h // 2
                for gi, hg in enumerate(grp):
                    heads = [4 * hg + q for q in range(4)]
                    specs = [(1, 0, 0, Mk[gi][0], None), (0, 1, 2, Lk[gi][0], None), (2, 0, 0, Mak[gi], None), (1, 3, 1, Mrb, hg), (2, 3, 1, Mrk, hg)]
                    for si, (la, ra, mki, dst, full16) in enumerate(specs):
                        dview = dst[:] if full16 is None else dst[:, hg * 4:(hg + 1) * 4, :]
                        wkey = dst if full16 is None else R(dst, hg)
                        for par in range(2):
                            bk = PB[(2 * si + par) % 6]
                            for qq in range(2):
                                h = heads[2 * qq + par]
                                ps_, c = HP(h)
                                S.i("pe", "matmul", [FMK(la, c), FMK(ra, c)], [bk], out=bk[:, qq * 128:(qq + 1) * 128], lhsT=FM(la, c, cs)[ps_, :], rhs=FM(ra, c, cs)[ps_, :],
                                    start=True, stop=True)
                            S.i("dve", "tensor_tensor", [bk, g.mk], [wkey], out=dview[:, par::2, :],
                                in0=bk[:, 0:256].rearrange("p (a b) -> p a b", b=128), in1=g.mk[:, mki, 0:256].rearrange("p (a b) -> p a b", b=128), op=ALU.mult)
                if stg < 7:
                    continue
                cur = [0, 0]
                NTin = [Mk[0][0], Mk[1][0]]
                for r in range(7):
                    for gi in range(2):
                        Lc, Mc = Lk[gi][cur[gi]], Mk[gi][cur[gi]]
                        Ln, Mn = Lk[gi][1 - cur[gi]], Mk[gi][1 - cur[gi]]
                        bN, bL, bM = PB[3 * gi], PB[3 * gi + 1], PB[3 * gi + 2]
                        if r >= 1:
                            NTo = NTt[gi][r % 2]
                            for q in range(4):
                                S.i("pe", "matmul", [NTin[gi], g.cb], [bN], out=bN[:, q * 128:(q + 1) * 128], lhsT=ident, rhs=NTin[gi][:, q, :], start=True, stop=False)
                                S.i("pe", "matmul", [Mc, g.cb], [bN], out=bN[:, q * 128:(q + 1) * 128], lhsT=ident, rhs=Mc[:, q, :], start=False, stop=False)
                                S.i("pe", "matmul", [Lc, NTin[gi]], [bN], out=bN[:, q * 128:(q + 1) * 128], lhsT=Lc[:, q, :], rhs=NTin[gi][:, q, :], start=False, stop=True)
                        if r < 6:
                            for q in range(4):
                                S.i("pe", "matmul", [Mc, Lc], [bL], out=bL[:, q * 128:(q + 1) * 128], lhsT=Mc[:, q, :], rhs=Lc[:, q, :], start=True, stop=True)
                            for q in range(4):
                                S.i("pe", "matmul", [Lc, Mc], [bM], out=bM[:, q * 128:(q + 1) * 128], lhsT=Lc[:, q, :], rhs=Mc[:, q, :], start=True, stop=True)
                        if r >= 1:
                            if gi == 0:
                                S.i("act", "activation", [bN], [NTo], out=NTo[:].rearrange("p a b -> p (a b)"), in_=bN[:], func=AF.Copy)
                            else:
                                S.i("dve", "tensor_copy", [bN], [NTo], out=NTo[:].rearrange("p a b -> p (a b)"), in_=bN[:])
                            NTin[gi] = NTo
                        if r < 6:
                            S.i("act", "activation", [bL], [Ln], out=Ln[:].rearrange("p a b -> p (a b)"), in_=bL[:], func=AF.Copy)
                            S.i("dve", "tensor_copy", [bM], [Mn], out=Mn[:].rearrange("p a b -> p (a b)"), in_=bM[:])
                            cur[gi] = 1 - cur[gi]
                if stg < 8:
                    continue
                for gi, hg in enumerate(grp):
                    heads = [4 * hg + q for q in range(4)]
                    NTf = NTin[gi]
                    for q, h in enumerate(heads):
                        S.i("pe", "matmul", [Mak[gi], Vtm], [PB[6]], out=PB[6][:, q * 64:(q + 1) * 64], lhsT=Mak[gi][:, q, :], rhs=Vtm[:, h * 64:(h + 1) * 64], start=True, stop=True)
                    S.i("act", "activation", [PB[6]], [Xb], out=Xb[:], in_=PB[6][:, 0:256], func=AF.Copy)
                    for q, h in enumerate(heads):
                        S.i("pe", "matmul", [Xb, g.cb], [PB[6]], out=PB[6][:, 256 + q * 64:256 + (q + 1) * 64], lhsT=ident, rhs=Xb[:, q * 64:(q + 1) * 64], start=True, stop=False)
                        S.i("pe", "matmul", [NTf, Xb], [PB[6]], out=PB[6][:, 256 + q * 64:256 + (q + 1) * 64], lhsT=NTf[:, q, :], rhs=Xb[:, q * 64:(q + 1) * 64], start=False, stop=True)
                    S.i("dve", "tensor_copy", [PB[6]], [R(Vhat, hg)], out=Vhat[:, hg * 256:(hg + 1) * 256], in_=PB[6][:, 256:512])
                    bA = PB[gi]
                    for q, h in enumerate(heads):
                        ps_, c = HP(h)
                        cc = (c % 2) * 128
                        S.i("pe", "matmul", [Atm, NTf], [bA], out=bA[ps_, cc:cc + 128], lhsT=Atm[:, h * 64:(h + 1) * 64], rhs=NTf[:, q, :], start=True, stop=True)
                    S.i("dve", "tensor_tensor", [bA, FMK(0, 2 * hg), FMK(0, 2 * hg + 1)], [R(AhT, hg)], out=AhT[:, 2 * hg:2 * hg + 2, :],
                        in0=bA[:, 0:256].rearrange("p (a b) -> p a b", b=128), in1=xb[:, 2 * hg:2 * hg + 2, cs.start:cs.stop], op=ALU.add)
            if stg < 9:
                continue
            Ubv = Ub[:].rearrange("p (h n) -> p h n", n=64)
            Vhv = Vhat[:].rearrange("p (h n) -> p h n", n=64)
            for par in range(2):
                bk = PB[par]
                for hh in range(8):
                    h = 2 * hh + par
                    ps_, c = slice(par * 64, par * 64 + 64), h // 2
                    S.i("pe", "matmul", [AhT, ST], [bk], out=bk[:, hh * 64:(hh + 1) * 64], lhsT=AhT[ps_, c, :], rhs=ST[ps_, c, :], start=True, stop=True)
            for par in range(2):
                S.i("dve", "tensor_tensor", [PB[par], Vhat], [R(Ub, par)], out=Ubv[:, par::2, :], in0=PB[par][:].rearrange("p (h n) -> p h n", n=64),
                    in1=Vhv[:, par::2, :], op=ALU.add)
            for c in range(NCH):
                S.i("act", "activation", [R(ST, c), PC], [R(STs, c)], out=STs[:, c, :], in_=ST[:, c, :], func=AF.Identity, scale=PC[:, c, n:n + 1])
            for par in range(2):
                bk = PB[2 + par]
                for hh in range(8):
                    h = 2 * hh + par
                    ps_, c = slice(par * 64, par * 64 + 64), h // 2
                    o_ = bk[:, hh * 64:(hh + 1) * 64]
                    S.i("pe", "matmul", [FMK(3, c), ST], [bk], out=o_, lhsT=FM(3, c, cs)[ps_, :], rhs=ST[ps_, c, :], start=True, stop=False)
                    S.i("pe", "matmul", [Mrk, Vtm], [bk], out=o_, lhsT=Mrk[:, h, :], rhs=Vtm[:, h * 64:(h + 1) * 64], start=False, stop=False)
                    S.i("pe", "matmul", [Mrb, Ub], [bk], out=o_, lhsT=Mrb[:, h, :], rhs=Ub[:, h * 64:(h + 1) * 64], start=False, stop=True)
            for h in range(16):
                ps_, c = slice((h % 2) * 64, (h % 2) * 64 + 64), h // 2
                o_ = PB[4][ps_, c * 64:(c + 1) * 64]
                S.i("pe", "matmul", [Ktm, Vtm], [PB[4]], out=o_, lhsT=Ktm[:, h * 64:(h + 1) * 64], rhs=Vtm[:, h * 64:(h + 1) * 64], start=True, stop=False)
                S.i("pe", "matmul", [Btm, Ub], [PB[4]], out=o_, lhsT=Btm[:, h * 64:(h + 1) * 64], rhs=Ub[:, h * 64:(h + 1) * 64], start=False, stop=True)
            for c in range(NCH):
                S.i("dve", "scalar_tensor_tensor", [PB[4], PC, R(STs, c)], [R(ST, c)], out=ST[:, c, :], in0=PB[4][:, c * 64:(c + 1) * 64], scalar=PC[:, c, n:n + 1],
                    in1=STs[:, c, :], op0=ALU.mult, op1=ALU.add)
            for b in range(2):
                S.i("dve", "tensor_reduce", [PB[2 + b]], [R(st4, ("s", b))], out=st4[:, 0, b::2], in_=PB[2 + b][:].rearrange("p (h n) -> p h n", n=64), axis=AX.X, op=ALU.add)
                S.i("act", "activation", [PB[2 + b]], [R(yn, None)], out=yn[:, b * 512:(b + 1) * 512], in_=PB[2 + b][:], func=AF.Square)
                S.i("dve", "tensor_reduce", [R(yn, None)], [R(st4, ("q", b))], out=st4[:, 1, b::2], in_=yn[:, b * 512:(b + 1) * 512].rearrange("p (h n) -> p h n", n=64), axis=AX.X, op=ALU.add)
            S.i("dve", "tensor_scalar", [st4], [R(st4, "m")], out=st4[:, 2, :], in0=st4[:, 0, :], scalar1=1.0 / 64, scalar2=None, op0=ALU.mult)
            S.i("dve", "tensor_tensor", [st4], [R(st4, "v")], out=st4[:, 3, :], in0=st4[:, 2, :], in1=st4[:, 2, :], op=ALU.mult)
            S.i("dve", "scalar_tensor_tensor", [st4], [R(st4, "v")], out=st4[:, 3, :], in0=st4[:, 1, :], scalar=1.0 / 64, in1=st4[:, 3, :], op0=ALU.mult, op1=ALU.subtract)
            S.i("dve", "tensor_scalar", [st4], [R(st4, "v")], out=st4[:, 3, :], in0=st4[:, 3, :], scalar1=64e-5, scalar2=None, op0=ALU.add)
            S.i("act", "activation", [st4], [R(st4, "v")], out=st4[:, 3, :], in_=st4[:, 3, :], func=AF.Ln)
            S.i("act", "activation", [st4], [R(st4, "v")], out=st4[:, 3, :], in_=st4[:, 3, :], func=AF.Exp, scale=-0.5)
            for c in range(NCH):
                S.i("pe", "matmul", [FMK(4, c), g.cb], [PB[5]], out=PB[5][:, 2 * c:2 * c + 2], lhsT=FM(4, c, cs), rhs=hind, start=True, stop=True)
            S.i("dve", "tensor_copy", [PB[5]], [bsum], out=bsum[:], in_=PB[5][:, 0:16])
            for h in range(16):
                b = h % 2
                hs = slice(h * 64, (h + 1) * 64)
                S.i("dve", "tensor_scalar", [PB[2 + b], st4], [R(yn, h)], out=yn[:, hs], in0=PB[2 + b][:, (h // 2) * 64:(h // 2 + 1) * 64],
                    scalar1=st4[:, 2, h:h + 1], scalar2=st4[:, 3, h:h + 1], op0=ALU.subtract, op1=ALU.mult)
            S.i("dve", "tensor_tensor", [yn, rows], [yn], out=yn[:], in0=yn[:], in1=rows[:, 0, :], op=ALU.mult)
            S.i("dve", "tensor_tensor", [yn, rows], [yn], out=yn[:], in0=yn[:], in1=rows[:, 1, :], op=ALU.add)
            for h in range(16):
                hs = slice(h * 64, (h + 1) * 64)
                S.i("dve", "scalar_tensor_tensor", [Vtm, bsum, yn], [R(yn, h)], out=yn[:, hs], in0=Vtm[:, hs], scalar=bsum[:, h:h + 1], in1=yn[:, hs], op0=ALU.mult, op1=ALU.add)
            for b in range(2):
                S.i("pe", "matmul", [sg, g2], [PB[b]], out=PB[b][:], lhsT=sg[:, 0, cs], rhs=g2[:, 0, b * 512:(b + 1) * 512], start=True, stop=False)
                S.i("pe", "matmul", [sg, g2], [PB[b]], out=PB[b][:], lhsT=sg[0:32, 1, cs], rhs=g2[0:32, 1, b * 512:(b + 1) * 512], start=False, stop=True)
                S.i("dve", "tensor_tensor", [yn, PB[b]], [R(ogtm, b)], out=ogtm[:, b * 512:(b + 1) * 512], in0=yn[:, b * 512:(b + 1) * 512], in1=PB[b][:], op=ALU.mult)
            for c in range(NCH):
                S.i("pe", "transpose", [ogtm, g.cb], [pT], out=pT[:, c * 128:(c + 1) * 128], in_=ogtm[:, c * 128:(c + 1) * 128], identity=ident)
            for c in range(NCH):
                S.i("act", "activation", [pT], [R(ogT, (c, n))], out=ogT[:, c, cs], in_=pT[:, c * 128:(c + 1) * 128], func=AF.Copy)
        if stg < 11:
            continue
        for m in range(NCH):
            wo = wget()
            bk = PB[5 + (m % 2)]
            for k in range(NCH):
                S.i("pe", "matmul", [wo, ogT], [bk], out=bk[:, 0:TW], lhsT=wo[:, k, :], rhs=ogT[:, k, :], start=(k == 0), stop=(k == NCH - 1))
            xs = x[:, m, t0:t0 + TW]
            S.i("dve", "scalar_tensor_tensor", [bk, R(x, (m, tt))], [R(x, (m, tt))], out=xs, in0=xs, scalar=ALPHA, in1=bk[:, 0:TW], op0=ALU.mult, op1=ALU.add)
    S.pop()
    S.push()
    layer_norm(g, ("ln1_g%d" % i, "ln1_b%d" % i), range(4))
    S.pop()

def make_consts():
    c = np.zeros((128, 1024), np.float32)
    c[:, 0:128] = 1.0
    c[:, 128:256] = np.eye(128, dtype=np.float32)
    bo = np.zeros((128, 128), np.float32); bo[:64, :64] = 1.0; bo[64:, 64:] = 1.0
    c[:, 256:384] = bo
    t = np.arange(128)
    c[:, 384:512] = np.where(t[None, :] <= t[:, None], 0.0, -30000.0)
    c[:, 512:640] = (t[:, None] < t[None, :]).astype(np.float32)
    c[:, 640:768] = (t[:, None] <= t[None, :]).astype(np.float32)
    c[:64, 768] = 1.0; c[64:, 769] = 1.0
    return c


def make_ret_tables():
    dk = 256
    inv = (1.0 / (np.float32(10000.0) ** np.linspace(0.0, 1.0, dk // 2, dtype=np.float32))).astype(np.float32)
    pos = np.arange(T_SEQ, dtype=np.float32)
    ang = (pos[:, None] * inv[None, :]).astype(np.float32)
    cos = np.cos(ang).astype(np.float32); sin = np.sin(ang).astype(np.float32)
    f = np.arange(dk)
    cosf = cos[:, f // 2].T
    sgn = np.where(f % 2 == 0, -1.0, 1.0).astype(np.float32)
    sinf = (sin[:, f // 2].T * sgn[:, None]).astype(np.float32)
    tab = np.stack([cosf.reshape(2, 128, T_SEQ).transpose(1, 0, 2), sinf.reshape(2, 128, T_SEQ).transpose(1, 0, 2)])
    m = np.zeros((4, 128, 384), np.float32)
    idx = np.arange(128, dtype=np.float64)
    for h in range(4):
        lg = np.log(1.0 - 2.0 ** (-5.0 - h))
        rel = idx[None, :] - idx[:, None]
        m[h, :, 0:128] = np.where(rel >= 0, np.exp(lg * np.maximum(rel, 0)), 0.0) / 16.0
        m[h, :, 128:256] = np.exp(lg * (idx + 1.0))[None, :]
        m[h, :, 256] = np.exp(lg * (127.0 - idx)) / 16.0
        m[h, :, 257] = np.exp(lg * 128.0)
    return np.ascontiguousarray(tab.astype(np.float32)), m


FULL_PLAN = [("rwkv", 0, 0), ("ffn", 0), ("ret", 1), ("ffn", 1), ("moba", 2), ("ffn", 2), ("rwkv", 3, 1), ("ffn", 3)]
_PLAN = FULL_PLAN
_NC_CACHE = {}


def host_inputs(inputs):
    shared = {k: np.ascontiguousarray(np.asarray(inputs[k], np.float32)) for k in WEIGHT_SHAPES}
    shared["vecs"] = pack_vecs(inputs)
    shared["consts"] = make_consts()
    rows = np.zeros((4, 128, 1024), np.float32)
    for j in range(2):
        rows[2 * j] = np.broadcast_to(np.asarray(inputs["rwkv_gn_g"][j], np.float32)[None, :], (128, 1024))
        rows[2 * j + 1] = np.broadcast_to(np.asarray(inputs["rwkv_gn_b"][j], np.float32)[None, :], (128, 1024))
    shared["rows"] = rows
    ti = np.arange(128)
    mk = np.zeros((128, 3, 512), np.float32)
    mk[:, 0, :] = np.tile((ti[:, None] < ti[None, :]).astype(np.float32), (1, 4))
    mk[:, 1, :] = np.tile((ti[:, None] <= ti[None, :]).astype(np.float32), (1, 4))
    mk[:, 2, :] = np.tile((ti[:, None] > ti[None, :]).astype(np.float32), (1, 4))
    shared["mk"] = mk
    tab, msk = make_ret_tables()
    shared["rtab"] = tab
    shared["rmask"] = msk
    wqk = np.asarray(inputs["ret_w_in"][0][:, :2048], np.float32)
    shared["ret_sw"] = np.ascontiguousarray(wqk.reshape(1024, 1024, 2)[:, :, ::-1].reshape(1024, 2048))
    return shared


def run_plan(plan, inputs, x_full, n_cores=8):
    key = repr(plan)
    if key not in _NC_CACHE:
        _NC_CACHE[key] = build_program(plan)
    nc = _NC_CACHE[key]
    shared = host_inputs(inputs)
    in_maps = []
    for b in range(n_cores):
        m = dict(shared)
        m["xT"] = np.ascontiguousarray(np.asarray(x_full[b], np.float32).T)
        in_maps.append(m)
    res = run_bass_kernel_spmd(nc, in_maps, core_ids=list(range(n_cores)))
    return np.stack([np.ascontiguousarray(r["outT"].T) for r in res.results]).astype(np.float32)


def kernel(**inputs):
    return run_plan(_PLAN, inputs, inputs["x"], 8)
```

```python
import numpy as np
import concourse.bass as bass
import concourse.mybir as mybir
from concourse.bass_utils import run_bass_kernel_spmd
from contextlib import ExitStack

_DBG = {}
F32 = mybir.dt.float32
BF16 = mybir.dt.bfloat16
AF = mybir.ActivationFunctionType
ALU = mybir.AluOpType
AX = mybir.AxisListType

ENGS = ("pe", "act", "dve", "pool", "sp")


class T:
    def __init__(self, S, t, name):
        self.S = S
        self.t = t
        self.name = name
        self.st = {}
        self.is_psum = False
        self.dsem = None
        self.dcnt = 0

    def __getitem__(self, idx):
        return self.t[idx]


class R:
    __slots__ = ("tile", "key")
    def __init__(self, tile, key=None):
        self.tile = tile
        self.key = key


class Sched:
    def __init__(self, nc):
        self.nc = nc
        self.es = ExitStack()
        self.ins = {e: [] for e in ENGS}
        self.cnt = {e: 0 for e in ENGS}
        self.seen = {e: {} for e in ENGS}
        self.sems = {}
        self.needed = {e: set() for e in ENGS}
        self.nsem = 0
        for e in ENGS:
            self.sems[("e", e)] = self.es.enter_context(nc.semaphore("sem_" + e))
        self.dma_pool = []
        self.out_dma = []
        self.pend = {}
        self.scopes = []

    def _es(self):
        return self.scopes[-1] if self.scopes else self.es

    def push(self):
        self.scopes.append(ExitStack())

    def pop(self):
        self.barrier()
        self.scopes.pop().close()

    def sbuf(self, name, shape, dt):
        self.uid = getattr(self, "uid", 0) + 1
        name = "%s_%d" % (name, self.uid)
        t = self._es().enter_context(self.nc.sbuf_tensor(name, list(shape), dt))
        return T(self, t, name)

    def psum(self, name, shape, dt=F32):
        self.uid = getattr(self, "uid", 0) + 1
        name = "%s_%d" % (name, self.uid)
        t = self._es().enter_context(self.nc.psum_tensor(name, list(shape), dt))
        tt = T(self, t, name)
        tt.is_psum = True
        return tt

    def new_dsem(self, name):
        self.nsem += 1
        key = ("d", self.nsem)
        self.sems[key] = self.es.enter_context(self.nc.semaphore("dsem%d_%s" % (self.nsem, name)))
        return key

    def _states(self, ref, create=True):
        tile, key = ref.tile, ref.key
        if key is None:
            if None not in tile.st:
                tile.st[None] = [None, {}]
            return list(tile.st.values())
        out = []
        if None in tile.st:
            out.append(tile.st[None])
        if key not in tile.st:
            tile.st[key] = [None, {}]
        out.append(tile.st[key])
        return out

    def _collect(self, eng, reads, writes):
        waits = {}
        def need(w, same_ok=False):
            if w is None:
                return
            k, v = w
            if same_ok and k == ("e", eng) and eng == "pe":
                return
            if waits.get(k, 0) < v:
                waits[k] = v
        for r in reads:
            for st in self._states(r):
                need(st[0])
                if r.tile.is_psum:
                    for rk, rv in st[1].items():
                        if rk != ("e", eng):
                            need((rk, rv))
        for w in writes:
            for st in self._states(w):
                need(st[0])
                for rk, rv in st[1].items():
                    if rk == ("e", eng) and eng == "pe":
                        continue
                    need((rk, rv))
        final = []
        for k, v in waits.items():
            if k == ("e", eng) and eng == "pe":
                continue
            if self.seen[eng].get(k, 0) < v:
                self.seen[eng][k] = v
                final.append((k, v))
                if k[0] == "e":
                    self.needed[k[1]].add(v)
        return final

    def _commit(self, tag, reads, writes):
        for r in reads:
            if r.key is None:
                for k, s in r.tile.st.items():
                    if s[1].get(tag[0], 0) < tag[1]:
                        s[1][tag[0]] = tag[1]
            else:
                s = r.tile.st[r.key]
                if s[1].get(tag[0], 0) < tag[1]:
                    s[1][tag[0]] = tag[1]
        for w in writes:
            if w.key is None:
                w.tile.st = {None: [tag, {}]}
            else:
                w.tile.st[w.key] = [tag, {}]

    def i(self, eng, meth, reads=(), writes=(), **kw):
        return self.op(eng, (meth, kw), reads, writes)

    def _norm(self, refs):
        out = []
        for r in refs:
            if not isinstance(r, R):
                r = R(r)
            if r.tile.is_psum and r.key is not None:
                r = R(r.tile)
            out.append(r)
        return out

    def op(self, eng, fn, reads=(), writes=()):
        reads = self._norm(reads)
        writes = self._norm(writes)
        waits = self._collect(eng, reads, writes)
        self.cnt[eng] += 1
        idx = self.cnt[eng]
        self.ins[eng].append([fn, waits, idx, None])
        self._commit((("e", eng), idx), reads, writes)
        return idx

    def dma(self, q, out_ap, in_ap, reads=(), writes=(), out_final=False, owner=None, **kw):
        reads = [r if isinstance(r, R) else R(r) for r in reads]
        writes = [w if isinstance(w, R) else R(w) for w in writes]
        waits = self._collect(q, reads, writes)
        if owner is None:
            owner = (writes[0] if writes else reads[0]).tile
        if owner.dsem is None:
            owner.dsem = self.new_dsem(owner.name)
        owner.dcnt += 16
        tag = (owner.dsem, owner.dcnt)
        self.cnt[q] += 1
        idx = self.cnt[q]
        self.ins[q].append([lambda e: e.dma_start(out=out_ap, in_=in_ap, **kw), waits, idx, tag])
        self._commit(tag, reads, writes)
        self.pend[tag[0]] = tag[1]
        if out_final:
            self.out_dma.append(tag)
        return tag

    def barrier(self):
        for f in ENGS:
            if self.cnt[f] and (self.ins[f][-1][0] is None or self.ins[f][-1][3] is not None):
                self.cnt[f] += 1
                self.ins[f].append([None, [], self.cnt[f], None])
        for e in ENGS:
            waits = []
            for f in ENGS:
                if f == e or self.cnt[f] == 0:
                    continue
                v = self.cnt[f]
                if self.seen[e].get(("e", f), 0) < v:
                    self.seen[e][("e", f)] = v
                    waits.append((("e", f), v))
                    self.needed[f].add(v)
            for k, v in self.pend.items():
                if self.seen[e].get(k, 0) < v:
                    self.seen[e][k] = v
                    waits.append((k, v))
            if waits:
                self.cnt[e] += 1
                self.ins[e].append([None, waits, self.cnt[e], None])

    def emit(self):
        nc = self.nc
        fwd = {}
        for k, v in self.out_dma:
            fwd[k] = max(fwd.get(k, 0), v)
        fw = list(fwd.items())
        self.cnt["sp"] += 1
        self.ins["sp"].append([None, fw, self.cnt["sp"], None])
        rank = {}
        for e in ENGS:
            s = sorted(self.needed[e])
            rank[e] = {v: i + 1 for i, v in enumerate(s)}
        engmap = {"pe": "tensor", "act": "scalar", "dve": "vector", "pool": "gpsimd", "sp": "sync"}
        with nc.Block() as block:
            for e in ENGS:
                lst = self.ins[e]
                if not lst:
                    continue
                def body(eng, e=e, lst=lst):
                    for fn, waits, idx, dtag in lst:
                        for k, v in waits:
                            if k[0] == "e":
                                eng.wait_ge(self.sems[k], rank[k[1]][v])
                            else:
                                eng.wait_ge(self.sems[k], v)
                        if fn is None:
                            if idx in self.needed[e]:
                                eng.nop().then_inc(self.sems[("e", e)], 1)
                            continue
                        ins = fn(eng) if callable(fn) else getattr(eng, fn[0])(**fn[1])
                        if dtag is not None:
                            ins.then_inc(self.sems[dtag[0]], 16)
                            if idx in self.needed[e]:
                                raise RuntimeError("dma instr needed as engine milestone")
                        elif idx in self.needed[e]:
                            ins.then_inc(self.sems[("e", e)], 1)
                getattr(block, engmap[e])(body)
        self.es.close()

D = 1024; T_SEQ = 2048; DEPTH = 4; FF = 2816; NCH = 8; NPAIR = 22
ALPHA = float((2 * DEPTH) ** 0.25)
LN_EPS = 1e-5

WEIGHT_SHAPES = {
    "rwkv_w_rkv": [2, 3, 1024, 1024], "rwkv_w1": [2, 1024, 64], "rwkv_w2": [2, 64, 1024],
    "rwkv_a1": [2, 1024, 64], "rwkv_a2": [2, 64, 1024], "rwkv_g1": [2, 1024, 160], "rwkv_g2": [2, 160, 1024],
    "rwkv_w_o": [2, 1024, 1024], "ret_w_in": [1, 1024, 6144], "ret_w_o": [1, 2048, 1024],
    "moba_w_qkv": [1, 1024, 3072], "moba_w_o": [1, 1024, 1024],
    "ffn_w_in": [4, 1024, 5632], "ffn_w_out": [4, 2816, 1024],
}


def _fm(v):
    v = np.asarray(v, np.float32).reshape(-1, 128)
    return np.ascontiguousarray(v.T)


def vec_layout():
    off = {}
    n = 0
    def add(name, cols):
        nonlocal n
        off[name] = n
        n += cols
    for i in range(4):
        for nm in ("ln1_g", "ln1_b", "ln2_g", "ln2_b"):
            add("%s%d" % (nm, i), 8)
        for k in range(3):
            add("cw%d_%d" % (k, i), 44)
        add("cb_%d" % i, 44)
    for j in range(2):
        for m in range(6):
            add("mix%d_%d" % (m, j), 8)
        for nm in ("w0", "a0", "k_k", "k_a", "r_k"):
            add("%s_%d" % (nm, j), 8)
    add("ret_gn_g", 16)
    add("ret_gn_b", 16)
    return off, n


def pack_vecs(inp):
    off, n = vec_layout()
    out = np.zeros((128, n), np.float32)
    def put(name, v):
        a = _fm(v)
        out[:, off[name]:off[name] + a.shape[1]] = a
    for i in range(4):
        for nm in ("ln1_g", "ln1_b", "ln2_g", "ln2_b"):
            put("%s%d" % (nm, i), inp[nm][i])
        for k in range(3):
            put("cw%d_%d" % (k, i), inp["ffn_conv_w"][i, k])
        put("cb_%d" % i, inp["ffn_conv_b"][i])
    for j in range(2):
        for m in range(6):
            put("mix%d_%d" % (m, j), inp["rwkv_mix"][j, m])
        put("w0_%d" % j, inp["rwkv_w0"][j]); put("a0_%d" % j, inp["rwkv_a0"][j])
        put("k_k_%d" % j, inp["rwkv_k_k"][j]); put("k_a_%d" % j, inp["rwkv_k_a"][j])
        put("r_k_%d" % j, inp["rwkv_r_k"][j].reshape(-1))
    put("ret_gn_g", inp["ret_gn_g"][0]); put("ret_gn_b", inp["ret_gn_b"][0])
    return out


class Ctx:
    pass


def build_program(plan, x_in_bf=True):
    nc = bass.Bass("TRN2", target_bir_lowering=False)
    S = Sched(nc)
    g = Ctx()
    g.nc, g.S = nc, S
    g.W = {k: nc.dram_tensor(k, shp, F32, kind="ExternalInput").ap() for k, shp in WEIGHT_SHAPES.items()}
    voff, nv = vec_layout()
    g.voff = voff
    xT_d = nc.dram_tensor("xT", [D, T_SEQ], F32, kind="ExternalInput").ap()
    vecs_d = nc.dram_tensor("vecs", [128, nv], F32, kind="ExternalInput").ap()
    consts_d = nc.dram_tensor("consts", [128, 1024], F32, kind="ExternalInput").ap()
    g.rows_d = nc.dram_tensor("rows", [4, 128, 1024], F32, kind="ExternalInput").ap()
    g.rtab_d = nc.dram_tensor("rtab", [2, 128, 2, T_SEQ], F32, kind="ExternalInput").ap()
    g.rmask_d = nc.dram_tensor("rmask", [4, 128, 384], F32, kind="ExternalInput").ap()
    g.ret_sw_d = nc.dram_tensor("ret_sw", [1024, 2048], F32, kind="ExternalInput").ap()
    g.mk_d = nc.dram_tensor("mk", [128, 3, 512], F32, kind="ExternalInput").ap()
    out_d = nc.dram_tensor("outT", [D, T_SEQ], F32, kind="ExternalOutput").ap()

    g.x = S.sbuf("x", [128, NCH, T_SEQ], F32)
    g.xb = S.sbuf("xb", [128, NCH, T_SEQ], BF16)
    g.vecs = S.sbuf("vecs", [128, nv], F32)
    g.cf = S.sbuf("cf", [128, 256], F32)
    g.cb = S.sbuf("cb", [128, 512], BF16)
    xv = xT_d.rearrange("(c p) t -> p c t", p=128)
    for c in range(NCH):
        S.dma("sp", g.x[:, c, :], xv[:, c, :], writes=[R(g.x, None)])
    if plan[0][0] != "rwkv":
        for c in range(NCH):
            S.dma("pool", g.xb[:, c, :], xv[:, c, :], writes=[R(g.xb, None)])
    S.dma("sp", g.vecs[:], vecs_d, writes=[g.vecs])
    S.dma("sp", g.cf[:, 0:128], consts_d[:, 0:128], writes=[R(g.cf, None)])
    S.dma("sp", g.cf[:, 128:256], consts_d[:, 384:512], writes=[R(g.cf, None)])
    S.i("pool", "memset", [], [g.cb], ap=g.cb[:], constant=0.0)
    S.dma("pool", g.cb[:, 0:256], consts_d[:, 128:384], writes=[R(g.cb, None)])
    S.dma("pool", g.cb[:, 256:258], consts_d[:, 768:770], writes=[R(g.cb, None)])
    g.mk = S.sbuf("mk", [128, 3, 512], BF16)
    S.dma("pool", g.mk[:], g.mk_d, writes=[g.mk])
    g.ones32 = g.cf[:, 0:128]
    g.identb = g.cb[:, 0:128]

    for step in plan:
        kind = step[0]
        if kind == "ffn":
            ffn_phase(g, step[1])
        elif kind == "ln":
            S.push()
            layer_norm(g, step[1], range(4))
            S.pop()
        elif kind == "moba":
            moba_phase(g, step[1])
        elif kind == "ret":
            ret_phase(g, step[1])
        elif kind == "rwkv":
            rwkv_phase(g, step[1], step[2])
        else:
            raise ValueError(kind)

    S.barrier()
    ov = out_d.rearrange("(c p) t -> p c t", p=128)
    outsem = T(S, None, "outsem")
    for c in range(NCH):
        S.dma("sp", ov[:, c, :], g.x[:, c, :], reads=[R(g.x, None)], out_final=True)
    S.emit()
    return nc


def vcol(g, name, c=0):
    o = g.voff[name] + c
    return g.vecs[:, o:o + 1]


def layer_norm(g, pref, tts, width=512, write_xb=True):
    S = g.S
    gname, bname = pref
    x, xb = g.x, g.xb
    tts = list(tts)
    nt = len(tts)
    sq = [S.sbuf("lnsq", [128, 512], F32) for _ in range(2)]
    means = [S.sbuf("lnmean", [128, 512], F32) for _ in range(nt)]
    msqs = [S.sbuf("lnmsq", [128, 512], F32) for _ in range(2)]
    rstds = [S.sbuf("lnrstd", [128, 512], F32) for _ in range(nt)]
    tmp = [S.sbuf("lntmp", [128, 512], F32) for _ in range(3)]
    ps_ss = [S.psum("lnps", [128, 512]) for _ in range(2)]
    ps_qs = [S.psum("lnpq", [128, 512]) for _ in range(2)]
    ones = g.ones32
    W_ = width
    def stats(ii, tix):
        ts = slice(tix * W_, (tix + 1) * W_)
        tt = (tix * W_) // 512
        mean, msq, rstd, ps_s, ps_q = means[ii], msqs[ii % 2], rstds[ii], ps_ss[ii % 2], ps_qs[ii % 2]
        for c in range(NCH):
            q = sq[c % 2]
            S.i("act", "activation", [R(x, (c, tt))], [q], out=q[:, 0:W_], in_=x[:, c, ts], func=AF.Square)
            S.i("pe", "matmul", [R(x, (c, tt)), g.cf], [ps_s], out=ps_s[:, 0:W_], lhsT=ones, rhs=x[:, c, ts], start=(c == 0), stop=(c == NCH - 1))
            S.i("pe", "matmul", [q, g.cf], [ps_q], out=ps_q[:, 0:W_], lhsT=ones, rhs=q[:, 0:W_], start=(c == 0), stop=(c == NCH - 1))
        S.i("dve", "tensor_scalar", [ps_s], [mean], out=mean[:, 0:W_], in0=ps_s[:, 0:W_], scalar1=1.0 / D, scalar2=None, op0=ALU.mult)
        S.i("dve", "tensor_tensor", [mean], [msq], out=msq[:, 0:W_], in0=mean[:, 0:W_], in1=mean[:, 0:W_], op=ALU.mult)
        S.i("dve", "scalar_tensor_tensor", [ps_q, msq], [msq], out=msq[:, 0:W_], in0=ps_q[:, 0:W_], scalar=1.0 / D, in1=msq[:, 0:W_], op0=ALU.mult, op1=ALU.subtract)
        S.i("dve", "tensor_scalar", [msq], [msq], out=msq[:, 0:W_], in0=msq[:, 0:W_], scalar1=LN_EPS, scalar2=None, op0=ALU.add)
        S.i("act", "activation", [msq], [rstd], out=rstd[:, 0:W_], in_=msq[:, 0:W_], func=AF.Ln)
        S.i("act", "activation", [rstd], [rstd], out=rstd[:, 0:W_], in_=rstd[:, 0:W_], func=AF.Exp, scale=-0.5)
    def norm(ii, tix):
        ts = slice(tix * W_, (tix + 1) * W_)
        tt = (tix * W_) // 512
        mean, rstd = means[ii], rstds[ii]
        for c in range(NCH):
            tm = tmp[c % 3]
            S.i("dve", "tensor_tensor", [R(x, (c, tt)), mean], [tm], out=tm[:, 0:W_], in0=x[:, c, ts], in1=mean[:, 0:W_], op=ALU.subtract)
            S.i("dve", "tensor_tensor", [tm, rstd], [tm], out=tm[:, 0:W_], in0=tm[:, 0:W_], in1=rstd[:, 0:W_], op=ALU.mult)
            S.i("act", "activation", [tm, g.vecs], [R(x, (c, tt))], out=x[:, c, ts], in_=tm[:, 0:W_], func=AF.Identity,
                bias=vcol(g, bname, c), scale=vcol(g, gname, c))
            if write_xb:
                S.i("dve", "tensor_scalar", [tm, g.vecs], [R(xb, (c, tt))], out=xb[:, c, ts], in0=tm[:, 0:W_], scalar1=vcol(g, gname, c),
                    scalar2=vcol(g, bname, c), op0=ALU.mult, op1=ALU.add)
    stats(0, tts[0])
    for ii in range(nt):
        if ii + 1 < nt:
            stats(ii + 1, tts[ii + 1])
        norm(ii, tts[ii])


def ffn_phase(g, i):
    S = g.S
    x, xb = g.x, g.xb
    S.push()
    W_in = g.W["ffn_w_in"][i].rearrange("(k p) (two f) -> p k two f", p=128, two=2)
    W_out = g.W["ffn_w_out"][i].rearrange("(k p) n -> p k n", p=128)
    TB = 1024
    a = S.sbuf("ffa", [128, NPAIR, TB], BF16)
    hus = [S.sbuf("hu", [128, TB + 2], F32) for _ in range(2)]
    hgs = [S.sbuf("hg", [128, TB + 2], F32) for _ in range(2)]
    aus = [S.sbuf("au", [128, TB], F32) for _ in range(2)]
    ags = [S.sbuf("ag", [128, TB], F32) for _ in range(2)]
    halo = S.sbuf("halo", [128, 2 * NPAIR, 2], F32)
    wps = [[S.sbuf("wp", [128, NCH, 128], BF16) for _ in range(2)] for _ in range(2)]
    wos = [S.sbuf("wo", [128, NPAIR, 128], BF16) for _ in range(2)]
    pb = [S.psum("ffps", [128, 512]) for _ in range(8)]
    cw = lambda k, j: vcol(g, "cw%d_%d" % (k, i), j)
    cbv = lambda j: vcol(g, "cb_%d" % i, j)
    nw = 0
    for blk in range(2):
        for j in range(NPAIR):
            if j == 0:
                for ug in range(2):
                    S.dma("pool", wps[nw % 2][ug][:], W_in[:, :, ug, 0:128], writes=[wps[nw % 2][ug]])
            wp = wps[nw % 2]; nw += 1
            hu, hg, au, ag = hus[j % 2], hgs[j % 2], aus[j % 2], ags[j % 2]
            if j + 1 < NPAIR:
                for ug in range(2):
                    S.dma("pool", wps[nw % 2][ug][:], W_in[:, :, ug, (j + 1) * 128:(j + 2) * 128], writes=[wps[nw % 2][ug]])
            elif True:
                S.dma("pool", wos[0][:], W_out[:, :, 0:128], writes=[wos[0]])
            banks = pb[(j % 2) * 4:(j % 2) * 4 + 4]
            for ug in range(2):
                for h in range(2):
                    bk = banks[ug * 2 + h]
                    tt = blk * 2 + h
                    for k in range(NCH):
                        S.i("pe", "matmul", [wp[ug], R(xb, (k, tt))], [bk], out=bk[:], lhsT=wp[ug][:, k, :],
                            rhs=xb[:, k, tt * 512:(tt + 1) * 512], start=(k == 0), stop=(k == NCH - 1))
            for ug, hb in ((0, hu), (1, hg)):
                for h in range(2):
                    bk = banks[ug * 2 + h]
                    S.i("act", "activation", [bk], [R(hb, h)], out=hb[:, 2 + h * 512:2 + (h + 1) * 512], in_=bk[:], func=AF.Copy)
                if blk == 0:
                    S.i("dve", "memset", [], [R(hb, "halo")], ap=hb[:, 0:2], constant=0.0)
                else:
                    S.i("dve", "tensor_copy", [R(halo, (ug, j))], [R(hb, "halo")], out=hb[:, 0:2], in_=halo[:, ug * NPAIR + j, :])
            for (hb, ab_, jj) in ((hu, au, j), (hg, ag, NPAIR + j)):
                S.i("act", "activation", [hb, g.vecs], [ab_], out=ab_[:], in_=hb[:, 2:TB + 2], func=AF.Identity, bias=cbv(jj), scale=cw(2, jj))
                S.i("dve", "scalar_tensor_tensor", [hb, ab_, g.vecs], [ab_], out=ab_[:], in0=hb[:, 1:TB + 1], scalar=cw(1, jj), in1=ab_[:], op0=ALU.mult, op1=ALU.add)
                S.i("dve", "scalar_tensor_tensor", [hb, ab_, g.vecs], [ab_], out=ab_[:], in0=hb[:, 0:TB], scalar=cw(0, jj), in1=ab_[:], op0=ALU.mult, op1=ALU.add)
            S.i("act", "activation", [ag], [ag], out=ag[:], in_=ag[:], func=AF.Silu)
            S.i("dve", "tensor_tensor", [au, ag], [R(a, j)], out=a[:, j, :], in0=au[:], in1=ag[:], op=ALU.mult)
            if blk == 0:
                for ug, hb in ((0, hu), (1, hg)):
                    S.i("dve", "tensor_copy", [hb], [R(halo, (ug, j))], out=halo[:, ug * NPAIR + j, :], in_=hb[:, TB:TB + 2])
        for m in range(NCH):
            wo = wos[m % 2]
            if m + 1 < NCH:
                S.dma("pool", wos[(m + 1) % 2][:], W_out[:, :, (m + 1) * 128:(m + 2) * 128], writes=[wos[(m + 1) % 2]])
            for h in range(2):
                bk = pb[(m * 2 + h) % 8]
                tt = blk * 2 + h
                for k in range(NPAIR):
                    S.i("pe", "matmul", [wo, R(a, k)], [bk], out=bk[:], lhsT=wo[:, k, :], rhs=a[:, k, h * 512:(h + 1) * 512],
                        start=(k == 0), stop=(k == NPAIR - 1))
                xs = x[:, m, tt * 512:(tt + 1) * 512]
                S.i("dve", "scalar_tensor_tensor", [bk, R(x, (m, tt))], [R(x, (m, tt))], out=xs, in0=xs, scalar=ALPHA, in1=bk[:],
                    op0=ALU.mult, op1=ALU.add)
    S.pop()
    S.push()
    layer_norm(g, ("ln2_g%d" % i, "ln2_b%d" % i), range(4))
    S.pop()

def moba_phase(g, i):
    S = g.S
    x, xb = g.x, g.xb
    S.push()
    Wqkv = g.W["moba_w_qkv"][0].rearrange("(k p) n -> p k n", p=128)
    Wo = g.W["moba_w_o"][0].rearrange("(k p) n -> p k n", p=128)
    ogT = S.sbuf("ogT", [128, NCH, T_SEQ], BF16)
    wsl = [[S.sbuf("mw", [128, NCH, 128], BF16) for _ in range(3)] for _ in range(2)]
    qkv = [[S.sbuf("mqkv", [128, T_SEQ], BF16) for _ in range(3)] for _ in range(2)]
    vtms = [S.sbuf("vtm", [128, 16, 128], BF16) for _ in range(2)]
    ksum = S.sbuf("ksum", [128, 8], F32)
    kmean = S.sbuf("kmean", [128, 8], BF16)
    P = S.sbuf("mP", [128, T_SEQ], BF16)
    PT = S.sbuf("mPT", [128, 16, 128], BF16)
    sd = S.sbuf("msd", [128, 128], F32)
    g8 = S.sbuf("g8", [128, 8], F32)
    top8 = S.sbuf("top8", [128, 8], F32)
    mb = S.sbuf("mb", [128, 8], F32)
    b8 = S.sbuf("b8", [128, 8], F32)
    nm = S.sbuf("nm", [128, 4], F32)
    rs = S.sbuf("rs", [128, 12], F32)
    rinv = S.sbuf("rinv", [128, 2], F32)
    otm = S.sbuf("otm", [128, 128], BF16)
    wos = [S.sbuf("mwo", [128, NCH, 128], BF16) for _ in range(2)]
    sc = S.psum("msc", [128, 2048])
    pT = S.psum("mpT", [128, 1024], BF16)
    gps = S.psum("mgps", [128, 512])
    ov = S.psum("mov", [128, 512])
    pj = S.psum("mpj", [128, 512])
    tri = g.cf[:, 128:256]
    if _DBG.get('dbg_memset'):
        S.i('pool', 'memset', [], [ogT], ap=ogT[:], constant=0.0)
    ident = g.identb
    for c in range(_DBG.get('moba_pairs', NCH)):
        ws = wsl[c % 2]
        qT, kT, vT = qkv[c % 2]
        vtm = vtms[c % 2]
        for w in range(3):
            S.dma("pool", ws[w][:], Wqkv[:, :, w * 1024 + c * 128: w * 1024 + (c + 1) * 128], writes=[ws[w]])
        for w, dst in ((0, qT), (1, kT), (2, vT)):
            for tt in range(4):
                for k in range(NCH):
                    S.i("pe", "matmul", [ws[w], R(xb, (k, tt))], [pj], out=pj[:], lhsT=ws[w][:, k, :], rhs=xb[:, k, tt * 512:(tt + 1) * 512],
                        start=(k == 0), stop=(k == NCH - 1))
                if w == 0:
                    S.i("act", "activation", [pj], [R(dst, tt)], out=dst[:, tt * 512:(tt + 1) * 512], in_=pj[:], func=AF.Copy, scale=0.125)
                else:
                    S.i("act", "activation", [pj], [R(dst, tt)], out=dst[:, tt * 512:(tt + 1) * 512], in_=pj[:], func=AF.Copy)
                if w == 1 and _DBG.get('moba_stage', 9) >= 0.2:
                    S.i("dve", "tensor_reduce", [pj], [R(ksum, tt)], out=ksum[:, 2 * tt:2 * tt + 2],
                        in_=pj[:].rearrange("p (b k) -> p b k", b=2), axis=AX.X, op=ALU.add)
        if _DBG.get('moba_stage', 9) >= 0.2:
            S.i("dve", "tensor_scalar", [ksum], [kmean], out=kmean[:], in0=ksum[:], scalar1=1.0 / 256.0, scalar2=None, op0=ALU.mult)
        for b in range(2 if _DBG.get('moba_stage', 9) >= 0.3 else 0):
            for t8 in range(8):
                kt = b * 8 + t8
                S.i("pe", "transpose", [vT, g.cb], [pT], out=pT[:, t8 * 128:(t8 + 1) * 128], in_=vT[:, kt * 128:(kt + 1) * 128], identity=ident)
            S.i("dve", "tensor_copy", [pT], [R(vtm, b)], out=vtm[:, b * 8:(b + 1) * 8, :].rearrange("p a b -> p (a b)"), in_=pT[:])
        stg = _DBG.get('moba_stage', 9)
        for qt in _DBG.get('moba_qts', range(16)):
            if stg < 2:
                break
            qb = qt // 2
            nkt = qt + 1
            qs = slice(qt * 128, (qt + 1) * 128)
            for hh in range(2):
                ps = slice(hh * 64, (hh + 1) * 64)
                use_thr = qb >= 4
                if use_thr:
                    S.i("pe", "matmul", [qT, kmean], [gps], out=gps[:, 0:8], lhsT=qT[ps, qs], rhs=kmean[ps, 0:8], start=True, stop=True)
                    S.i("pool", "memset", [], [R(g8, "pad")], ap=g8[:, qb:8], constant=-1.0e30)
                    S.i("dve", "tensor_copy", [gps], [R(g8, "val")], out=g8[:, 0:qb], in_=gps[:, 0:qb])
                    S.i("dve", "max", [g8], [top8], out=top8[:], in_=g8[:])
                    S.i("dve", "tensor_scalar", [g8, top8], [mb], out=mb[:], in0=g8[:], scalar1=top8[:, 2:3], scalar2=30000.0,
                        op0=ALU.is_ge, op1=ALU.mult)
                ncol = nkt * 128
                for b in range((ncol + 511) // 512):
                    w_ = min(512, ncol - b * 512)
                    S.i("pe", "matmul", [qT, kT], [sc], out=sc[:, b * 512:b * 512 + w_], lhsT=qT[ps, qs], rhs=kT[ps, b * 512:b * 512 + w_],
                        start=True, stop=True)
                S.i("dve", "tensor_tensor", [sc, g.cf], [sc], out=sc[:, qt * 128:(qt + 1) * 128], in0=sc[:, qt * 128:(qt + 1) * 128], in1=tri, op=ALU.add)
                S.i("dve", "tensor_reduce", [sc], [R(nm, 0)], out=nm[:, 0:1], in_=sc[:, 0:ncol], axis=AX.X, op=ALU.max, negate=True)
                negm = nm[:, 0:1]
                if stg < 3:
                    continue
                if use_thr:
                    S.i("dve", "tensor_scalar", [mb, nm], [b8], out=b8[:], in0=mb[:], scalar1=negm, scalar2=-30000.0, op0=ALU.add, op1=ALU.add)
                npz = 0
                if use_thr:
                    for n in range(qb):
                        S.i("act", "activation", [sc, b8], [R(P, n), R(rs, npz)], out=P[:, n * 256:(n + 1) * 256], in_=sc[:, n * 256:(n + 1) * 256],
                            func=AF.Exp, bias=b8[:, n:n + 1], scale=1.0, accum_out=rs[:, npz:npz + 1])
                        npz += 1
                    S.i("act", "activation", [sc, nm], [R(P, "o"), R(rs, npz)], out=P[:, qb * 256:ncol], in_=sc[:, qb * 256:ncol],
                        func=AF.Exp, bias=negm, scale=1.0, accum_out=rs[:, npz:npz + 1])
                    npz += 1
                else:
                    S.i("act", "activation", [sc, nm], [R(P, "past"), R(rs, npz)], out=P[:, 0:ncol], in_=sc[:, 0:ncol],
                        func=AF.Exp, bias=negm, scale=1.0, accum_out=rs[:, npz:npz + 1])
                    npz += 1
                S.i("dve", "tensor_reduce", [rs], [R(rs, 11)], out=rs[:, 11:12], in_=rs[:, 0:npz], axis=AX.X, op=ALU.add)
                S.i("dve", "reciprocal", [R(rs, 11)], [R(rinv, hh)], out=rinv[:, hh:hh + 1], in_=rs[:, 11:12])
                if stg < 4:
                    continue
                for b in range((nkt + 7) // 8):
                    n8 = min(8, nkt - b * 8)
                    for t8 in range(n8):
                        kt = b * 8 + t8
                        S.i("pe", "transpose", [P, g.cb], [pT], out=pT[:, t8 * 128:(t8 + 1) * 128], in_=P[:, kt * 128:(kt + 1) * 128], identity=ident)
                    S.i("dve", "tensor_copy", [pT], [R(PT, b)], out=PT[:, b * 8:b * 8 + n8, :].rearrange("p a b -> p (a b)"), in_=pT[:, 0:n8 * 128])
                for kt in range(nkt):
                    S.i("pe", "matmul", [PT, vtm], [R(ov, hh)], out=ov[:, hh * 64:(hh + 1) * 64], lhsT=PT[:, kt, :], rhs=vtm[:, kt, hh * 64:(hh + 1) * 64],
                        start=(kt == 0), stop=(kt == nkt - 1))
                S.i("dve", "tensor_scalar", [R(ov, hh), R(rinv, hh)], [R(otm, hh)], out=otm[:, hh * 64:(hh + 1) * 64], in0=ov[:, hh * 64:(hh + 1) * 64],
                    scalar1=rinv[:, hh:hh + 1], scalar2=None, op0=ALU.mult)
            if stg < 5:
                continue
            S.i("pe", "transpose", [otm, g.cb], [pT], out=pT[:, 0:128], in_=otm[:], identity=ident)
            S.i("act", "activation", [pT], [R(ogT, (c, qt))], out=ogT[:, c, qs], in_=pT[:, 0:128], func=AF.Copy)
    for m in range(NCH):
        wo = wos[m % 2]
        S.dma("pool", wo[:], Wo[:, :, m * 128:(m + 1) * 128], writes=[wo])
        for tt in range(4):
            for k in range(NCH):
                S.i("pe", "matmul", [wo, ogT], [pj], out=pj[:], lhsT=wo[:, k, :], rhs=ogT[:, k, tt * 512:(tt + 1) * 512], start=(k == 0), stop=(k == NCH - 1))
            xs = x[:, m, tt * 512:(tt + 1) * 512]
            S.i("dve", "scalar_tensor_tensor", [pj, R(x, (m, tt))], [R(x, (m, tt))], out=xs, in0=xs, scalar=ALPHA, in1=pj[:], op0=ALU.mult, op1=ALU.add)
    S.pop()
    S.push()
    layer_norm(g, ("ln1_g%d" % i, "ln1_b%d" % i), range(4))
    S.pop()

def ret_phase(g, i):
    S = g.S
    x, xb = g.x, g.xb
    S.push()
    Win = g.W["ret_w_in"][0].rearrange("(k p) n -> p k n", p=128)
    Wsw = g.ret_sw_d.rearrange("(k p) n -> p k n", p=128)
    Wo = g.W["ret_w_o"][0].rearrange("(k p) n -> p k n", p=128)
    wq = S.sbuf("rwq", [128, NCH, 256], BF16); wqs = S.sbuf("rwqs", [128, NCH, 256], BF16)
    wk = S.sbuf("rwk", [128, NCH, 256], BF16); wks = S.sbuf("rwks", [128, NCH, 256], BF16)
    wv = S.sbuf("rwv", [128, NCH, 512], BF16); wg = S.sbuf("rwg", [128, NCH, 512], BF16)
    wo = S.sbuf("rwo", [128, 4, 1024], BF16)
    rm = S.sbuf("rrm", [128, 384], F32)
    tabs = [S.sbuf("rtab", [128, 2, 2, 512], F32) for _ in range(2)]
    qrot = S.sbuf("qrot", [128, 2, 512], BF16); krot = S.sbuf("krot", [128, 2, 512], BF16)
    vT = S.sbuf("rvT", [128, 4, 512], BF16); sgT = S.sbuf("rsgT", [128, 4, 512], BF16)
    vtm = S.sbuf("rvtm", [128, 4, 512], BF16); ktm = S.sbuf("rktm", [128, 4, 256], BF16)
    og = S.sbuf("rog", [128, 4, 512], BF16)
    S32 = S.sbuf("rS32", [128, 2, 512], F32); Sb = S.sbuf("rSb", [128, 2, 512], BF16)
    t1 = S.sbuf("rt1", [128, 512], F32); t2 = S.sbuf("rt2", [128, 512], F32)
    ST = S.sbuf("rST", [128, 128], BF16); qcd = S.sbuf("rqcd", [128, 2, 128], BF16)
    osbs = [S.sbuf("rosb", [128, 512], F32) for _ in range(2)]; osqs = [S.sbuf("rosq", [128, 512], F32) for _ in range(2)]
    means = [S.sbuf("rmean", [128, 128], F32) for _ in range(2)]; msqs = [S.sbuf("rmsq", [128, 128], F32) for _ in range(2)]; rstds = [S.sbuf("rrstd", [128, 128], F32) for _ in range(2)]
    nchunk = 0
    pA = S.psum("rpA", [128, 512]); pB = S.psum("rpB", [128, 512])
    opss = [S.psum("rops", [128, 512]) for _ in range(2)]
    ups = [S.psum("rups", [128, 512]) for _ in range(2)]
    pT = S.psum("rpT", [128, 1024], BF16)
    pst = S.psum("rpst", [128, 512])
    sps = pst
    ident = g.identb
    ones = g.ones32
    ntab = 0
    for h in range(4):
        S.dma("pool", wq[:], Win[:, :, h * 256:(h + 1) * 256], writes=[wq])
        S.dma("pool", wqs[:], Wsw[:, :, h * 256:(h + 1) * 256], writes=[wqs])
        S.dma("pool", wk[:], Win[:, :, 1024 + h * 256:1024 + (h + 1) * 256], writes=[wk])
        S.dma("pool", wks[:], Wsw[:, :, 1024 + h * 256:1024 + (h + 1) * 256], writes=[wks])
        S.dma("pool", wv[:], Win[:, :, 2048 + h * 512:2048 + (h + 1) * 512], writes=[wv])
        S.dma("pool", wg[:], Win[:, :, 4096 + h * 512:4096 + (h + 1) * 512], writes=[wg])
        S.dma("pool", wo[:], Wo[:, h * 4:(h + 1) * 4, :], writes=[wo])
        S.dma("sp", rm[:], g.rmask_d[h], writes=[rm])
        S.i("pool", "memset", [], [S32], ap=S32[:], constant=0.0)
        S.i("pool", "memset", [], [Sb], ap=Sb[:], constant=0.0)
        for tt in range(4):
            ts = slice(tt * 512, (tt + 1) * 512)
            tab = tabs[ntab % 2]; ntab += 1
            for cs_ in range(2):
                S.dma("sp", tab[:, cs_, :, :], g.rtab_d[cs_][:, :, ts], writes=[R(tab, None)])
            for (wa, wb, dst) in ((wq, wqs, qrot), (wk, wks, krot)):
                for dc in range(2):
                    for k in range(NCH):
                        S.i("pe", "matmul", [wa, R(xb, (k, tt))], [pA], out=pA[:], lhsT=wa[:, k, dc * 128:(dc + 1) * 128], rhs=xb[:, k, ts],
                            start=(k == 0), stop=(k == NCH - 1))
                    for k in range(NCH):
                        S.i("pe", "matmul", [wb, R(xb, (k, tt))], [pB], out=pB[:], lhsT=wb[:, k, dc * 128:(dc + 1) * 128], rhs=xb[:, k, ts],
                            start=(k == 0), stop=(k == NCH - 1))
                    S.i("dve", "tensor_tensor", [pA, tab], [t1], out=t1[:], in0=pA[:], in1=tab[:, 0, dc, :], op=ALU.mult)
                    S.i("dve", "tensor_tensor", [pB, tab], [t2], out=t2[:], in0=pB[:], in1=tab[:, 1, dc, :], op=ALU.mult)
                    S.i("pool", "tensor_tensor", [t1, t2], [R(dst, dc)], out=dst[:, dc, :], in0=t1[:], in1=t2[:], op=ALU.add)
            for ec in range(4):
                for k in range(NCH):
                    S.i("pe", "matmul", [wv, R(xb, (k, tt))], [pA], out=pA[:], lhsT=wv[:, k, ec * 128:(ec + 1) * 128], rhs=xb[:, k, ts],
                        start=(k == 0), stop=(k == NCH - 1))
                S.i("act", "activation", [pA], [R(vT, ec)], out=vT[:, ec, :], in_=pA[:], func=AF.Copy)
                for k in range(NCH):
                    S.i("pe", "matmul", [wg, R(xb, (k, tt))], [pB], out=pB[:], lhsT=wg[:, k, ec * 128:(ec + 1) * 128], rhs=xb[:, k, ts],
                        start=(k == 0), stop=(k == NCH - 1))
                S.i("act", "activation", [pB], [R(sgT, ec)], out=sgT[:, ec, :], in_=pB[:], func=AF.Silu)
            for n in range(4):
                for ec in range(4):
                    idx = (n % 2) * 4 + ec
                    S.i("pe", "transpose", [vT, g.cb], [pT], out=pT[:, idx * 128:(idx + 1) * 128], in_=vT[:, ec, n * 128:(n + 1) * 128], identity=ident)
                if n % 2 == 1:
                    S.i("dve", "tensor_copy", [pT], [R(vtm, n // 2)], out=vtm[:, n - 1:n + 1, :].rearrange("p a b -> p (a b)"), in_=pT[:])
            for n in range(4):
                for dc in range(2):
                    idx = n * 2 + dc
                    S.i("pe", "transpose", [krot, g.cb], [pT], out=pT[:, idx * 128:(idx + 1) * 128], in_=krot[:, dc, n * 128:(n + 1) * 128], identity=ident)
            S.i("dve", "tensor_scalar", [pT, rm], [ktm], out=ktm[:].rearrange("p a b -> p (a b)"), in0=pT[:], scalar1=rm[:, 256:257], scalar2=None, op0=ALU.mult)
            def partA(n):
                cs = slice(n * 128, (n + 1) * 128)
                opsn = opss[n % 2]
                for dc in range(2):
                    S.i("pe", "matmul", [krot, qrot], [sps], out=sps[:, 256:384], lhsT=krot[:, dc, cs], rhs=qrot[:, dc, cs], start=(dc == 0), stop=(dc == 1))
                S.i("dve", "tensor_tensor", [sps, rm], [ST], out=ST[:], in0=sps[:, 256:384], in1=rm[:, 0:128], op=ALU.mult)
                S.i("pool", "tensor_tensor", [qrot, rm], [qcd], out=qcd[:], in0=qrot[:, :, cs], in1=rm[:, 128:256].unsqueeze(1).to_broadcast([128, 2, 128]), op=ALU.mult)
                for ec in range(4):
                    es = slice(ec * 128, (ec + 1) * 128)
                    S.i("pe", "matmul", [vtm, ST], [opsn], out=opsn[:, es], lhsT=vtm[:, n, es], rhs=ST[:], start=True, stop=False)
                    for dc in range(2):
                        S.i("pe", "matmul", [Sb, qcd], [opsn], out=opsn[:, es], lhsT=Sb[:, dc, es], rhs=qcd[:, dc, :], start=False, stop=(dc == 1))
                for dc in range(2):
                    S.i("pe", "matmul", [ktm, vtm], [ups[dc]], out=ups[dc][:], lhsT=ktm[:, n, dc * 128:(dc + 1) * 128], rhs=vtm[:, n, :], start=True, stop=True)
                    S.i("dve", "scalar_tensor_tensor", [S32, ups[dc], rm], [R(S32, dc)], out=S32[:, dc, :], in0=S32[:, dc, :], scalar=rm[:, 257:258], in1=ups[dc][:],
                        op0=ALU.mult, op1=ALU.add)
                    S.i("act", "activation", [R(S32, dc)], [R(Sb, dc)], out=Sb[:, dc, :], in_=S32[:, dc, :], func=AF.Copy)

            def partB(n):
                cs = slice(n * 128, (n + 1) * 128)
                opsn = opss[n % 2]
                osb, osq, mean, msq, rstd = osbs[n % 2], osqs[n % 2], means[n % 2], msqs[n % 2], rstds[n % 2]
                S.i("act", "activation", [opsn], [osb], out=osb[:], in_=opsn[:], func=AF.Copy)
                S.i("act", "activation", [opsn], [osq], out=osq[:], in_=opsn[:], func=AF.Square)
                for ec in range(4):
                    S.i("pe", "matmul", [osb, g.cf], [R(pst, 0)], out=pst[:, 0:128], lhsT=ones, rhs=osb[:, ec * 128:(ec + 1) * 128], start=(ec == 0), stop=(ec == 3))
                for ec in range(4):
                    S.i("pe", "matmul", [osq, g.cf], [R(pst, 1)], out=pst[:, 128:256], lhsT=ones, rhs=osq[:, ec * 128:(ec + 1) * 128], start=(ec == 0), stop=(ec == 3))
                S.i("dve", "tensor_scalar", [R(pst, 0)], [mean], out=mean[:], in0=pst[:, 0:128], scalar1=1.0 / 512, scalar2=None, op0=ALU.mult)
                S.i("dve", "tensor_tensor", [mean], [msq], out=msq[:], in0=mean[:], in1=mean[:], op=ALU.mult)
                S.i("dve", "scalar_tensor_tensor", [R(pst, 1), msq], [msq], out=msq[:], in0=pst[:, 128:256], scalar=1.0 / 512, in1=msq[:], op0=ALU.mult, op1=ALU.subtract)
                S.i("dve", "tensor_scalar", [msq], [msq], out=msq[:], in0=msq[:], scalar1=1e-5, scalar2=None, op0=ALU.add)
                S.i("act", "activation", [msq], [rstd], out=rstd[:], in_=msq[:], func=AF.Ln)
                S.i("act", "activation", [rstd], [rstd], out=rstd[:], in_=rstd[:], func=AF.Exp, scale=-0.5)
                mb_ = mean[:].unsqueeze(1).to_broadcast([128, 4, 128])
                rb_ = rstd[:].unsqueeze(1).to_broadcast([128, 4, 128])
                o3 = osb[:].rearrange("p (a b) -> p a b", b=128)
                S.i("dve", "tensor_tensor", [osb, mean], [osb], out=o3, in0=o3, in1=mb_, op=ALU.subtract)
                S.i("dve", "tensor_tensor", [osb, rstd], [osb], out=o3, in0=o3, in1=rb_, op=ALU.mult)
                for ec in range(4):
                    col = h * 4 + ec
                    S.i("act", "activation", [osb, g.vecs], [R(osq, ec)], out=osq[:, ec * 128:(ec + 1) * 128], in_=osb[:, ec * 128:(ec + 1) * 128], func=AF.Identity,
                        bias=vcol(g, "ret_gn_b", col), scale=vcol(g, "ret_gn_g", col))
                S.i("dve", "tensor_tensor", [osq, sgT], [R(og, n)], out=og[:, :, cs], in0=osq[:].rearrange("p (a b) -> p a b", b=128), in1=sgT[:, :, cs], op=ALU.mult)

            partA(0)
            for n in range(4):
                if n + 1 < 4:
                    partA(n + 1)
                partB(n)
            for m in range(NCH):
                pw = pA if m % 2 == 0 else pB
                for k in range(4):
                    S.i("pe", "matmul", [wo, og], [pw], out=pw[:], lhsT=wo[:, k, m * 128:(m + 1) * 128], rhs=og[:, k, :], start=(k == 0), stop=(k == 3))
                xs = x[:, m, ts]
                if h == 0:
                    S.i("dve", "scalar_tensor_tensor", [pw, R(x, (m, tt))], [R(x, (m, tt))], out=xs, in0=xs, scalar=ALPHA, in1=pw[:], op0=ALU.mult, op1=ALU.add)
                else:
                    S.i("dve", "tensor_tensor", [pw, R(x, (m, tt))], [R(x, (m, tt))], out=xs, in0=xs, in1=pw[:], op=ALU.add)
    S.pop()
    S.push()
    layer_norm(g, ("ln1_g%d" % i, "ln1_b%d" % i), range(4))
    S.pop()

def rwkv_phase(g, i, j):
    S = g.S
    x, xb = g.x, g.xb
    S.push()
    TW = 256
    NT_ = T_SEQ // TW
    W = g.W
    Wrkv = [W["rwkv_w_rkv"][j, w].rearrange("(k p) n -> p k n", p=128) for w in range(3)]
    Wo = W["rwkv_w_o"][j].rearrange("(k p) n -> p k n", p=128)
    V = lambda name, c=0: vcol(g, "%s_%d" % (name, j), c)
    w1 = S.sbuf("w1", [128, NCH, 64], BF16); a1 = S.sbuf("a1", [128, NCH, 64], BF16); g1 = S.sbuf("g1", [128, NCH, 160], BF16)
    w2 = S.sbuf("w2", [64, 1024], BF16); a2 = S.sbuf("a2", [64, 1024], BF16); g2 = S.sbuf("g2", [128, 2, 1024], BF16)
    S.dma("pool", w1[:], W["rwkv_w1"][j].rearrange("(k p) n -> p k n", p=128), writes=[w1])
    S.dma("pool", a1[:], W["rwkv_a1"][j].rearrange("(k p) n -> p k n", p=128), writes=[a1])
    S.dma("pool", g1[:], W["rwkv_g1"][j].rearrange("(k p) n -> p k n", p=128), writes=[g1])
    S.dma("pool", w2[:], W["rwkv_w2"][j], writes=[w2])
    S.dma("pool", a2[:], W["rwkv_a2"][j], writes=[a2])
    S.dma("pool", g2[:, 0, :], W["rwkv_g2"][j][0:128, :], writes=[R(g2, None)])
    S.dma("pool", g2[0:32, 1, :], W["rwkv_g2"][j][128:160, :], writes=[R(g2, None)])
    rows = S.sbuf("gnrows", [128, 2, 1024], BF16)
    S.dma("pool", rows[:, 0, :], g.rows_d[2 * j], writes=[R(rows, None)])
    S.dma("pool", rows[:, 1, :], g.rows_d[2 * j + 1], writes=[R(rows, None)])
    om = S.sbuf("om", [128, 72], F32)
    mo = g.voff["mix0_%d" % j]
    S.i("dve", "tensor_scalar", [g.vecs], [om], out=om[:, 0:48], in0=g.vecs[:, mo:mo + 48], scalar1=-1.0, scalar2=1.0, op0=ALU.mult, op1=ALU.add)
    ko = g.voff["k_a_%d" % j]
    S.i("dve", "tensor_scalar", [g.vecs], [om], out=om[:, 48:56], in0=g.vecs[:, ko:ko + 8], scalar1=-1.0, scalar2=1.0, op0=ALU.mult, op1=ALU.add)
    ao = g.voff["a0_%d" % j]; wo_ = g.voff["w0_%d" % j]
    S.i("dve", "tensor_scalar", [g.vecs], [om], out=om[:, 56:64], in0=g.vecs[:, ao:ao + 8], scalar1=0.5, scalar2=None, op0=ALU.mult)
    S.i("dve", "tensor_scalar", [g.vecs], [om], out=om[:, 64:72], in0=g.vecs[:, wo_:wo_ + 8], scalar1=0.5, scalar2=None, op0=ALU.mult)
    xx = S.sbuf("xx", [128, NCH, TW], F32)
    xlast = S.sbuf("xlast", [128, NCH, 2], F32)
    S.i("pool", "memset", [], [xlast], ap=xlast[:], constant=0.0)
    hw = S.sbuf("hw", [64, TW], BF16); ha = S.sbuf("ha", [64, TW], BF16); sg = S.sbuf("sg", [128, 2, TW], BF16); sgf = S.sbuf("sgf", [128, 2, TW], BF16)
    NSL = _DBG.get('rw_nsl', 4)
    wsl = [S.sbuf("rwsl", [128, NCH, 128], BF16) for _ in range(NSL)]
    wseq = []
    for tix_ in range(NT_):
        for c_ in range(NCH):
            wseq.append(Wrkv[0][:, :, c_ * 128:(c_ + 1) * 128]); wseq.append(Wrkv[1][:, :, c_ * 128:(c_ + 1) * 128])
        for c_ in range(NCH):
            wseq.append(Wrkv[2][:, :, c_ * 128:(c_ + 1) * 128])
        for m_ in range(NCH):
            wseq.append(Wo[:, :, m_ * 128:(m_ + 1) * 128])
    wst = {"issued": 0, "next": 0}
    def wget():
        i_ = wst["next"]; wst["next"] += 1
        while wst["issued"] < min(len(wseq), i_ + NSL - 1):
            q_ = wst["issued"]
            S.dma("pool", wsl[q_ % NSL][:], wseq[q_], writes=[wsl[q_ % NSL]])
            wst["issued"] += 1
        return wsl[i_ % NSL]
    TS = [dict(), dict()]
    for nm in ("tA", "tsw", "tkk", "tcs", "te1", "te2", "te3"):
        for q_ in range(2):
            TS[q_][nm] = S.sbuf(nm, [128, TW], F32)
    for nm in ("trn", "tt"):
        t_ = S.sbuf(nm, [128, TW], F32)
        TS[0][nm] = t_; TS[1][nm] = t_
    for q_ in range(2):
        TS[q_]["tkk2"] = S.sbuf("tkk2", [128, TW], BF16)
    PC = S.sbuf("PC", [128, NCH, 2], F32)
    def FM(a, c, sl):
        return xb[:, c, a * TW + sl.start:a * TW + sl.stop]
    FMK = lambda a, c: R(xb, ("fm", a, c))
    Lk = [[S.sbuf("Lk", [128, 4, 128], BF16) for _ in range(2)] for _ in range(2)]
    Mk = [[S.sbuf("Mk", [128, 4, 128], BF16) for _ in range(2)] for _ in range(2)]
    NTt = [[S.sbuf("NT", [128, 4, 128], BF16) for _ in range(2)] for _ in range(2)]
    Mak = [S.sbuf("Mak", [128, 4, 128], BF16) for _ in range(2)]
    Mrb = S.sbuf("Mrb", [128, 16, 128], BF16); Mrk = S.sbuf("Mrk", [128, 16, 128], BF16)
    Atm = S.sbuf("Atm", [128, 1024], BF16); Btm = S.sbuf("Btm", [128, 1024], BF16); Ktm = S.sbuf("Ktm", [128, 1024], BF16); Vtm = S.sbuf("Vtm", [128, 1024], BF16)
    AhT = S.sbuf("AhT", [128, NCH, 128], BF16); Xb = S.sbuf("Xb", [128, 256], BF16); Vhat = S.sbuf("Vhat", [128, 1024], BF16)
    Ub = S.sbuf("Ub", [128, 1024], BF16); ST = S.sbuf("ST", [128, NCH, 64], BF16); STs = S.sbuf("STs", [128, NCH, 64], F32)
    yn = S.sbuf("yn", [128, 1024], F32); st4 = S.sbuf("st4", [128, 4, 16], F32); bsum = S.sbuf("bsum", [128, 16], F32)
    ogtm = S.sbuf("ogtm", [128, 1024], BF16); ogT = S.sbuf("ogT", [128, NCH, TW], BF16)
    PB = [S.psum("rp", [128, 512]) for _ in range(7)]
    pT = S.psum("rpT", [128, 1024], BF16)
    ident = g.identb
    blk1 = g.cb[:, 128:256]
    hind = g.cb[:, 256:258]
    ones = g.ones32
    S.i("pool", "memset", [], [ST], ap=ST[:], constant=0.0)
    nws = 0
    stg = _DBG.get('rw_stage', 99)
    for tix in range(_DBG.get('rw_tiles', NT_)):
        t0 = tix * TW
        tt = t0 // 512
        S.i("dve", "tensor_tensor", [R(x, None)], [R(xx, "m")], out=xx[:, :, 1:TW], in0=x[:, :, t0:t0 + TW - 1], in1=x[:, :, t0 + 1:t0 + TW], op=ALU.subtract)
        S.i("dve", "tensor_tensor", [R(x, None), xlast], [R(xx, "0")], out=xx[:, :, 0:1], in0=xlast[:, :, tix % 2:tix % 2 + 1], in1=x[:, :, t0:t0 + 1], op=ALU.subtract)
        S.i("dve", "tensor_copy", [R(x, None)], [xlast], out=xlast[:, :, (tix + 1) % 2:(tix + 1) % 2 + 1], in_=x[:, :, t0 + TW - 1:t0 + TW])
        def mix(m, dst):
            for c in range(NCH):
                mc = vcol(g, "mix%d_%d" % (m, j), c)
                S.i("dve", "scalar_tensor_tensor", [R(x, (c, tt)), xx, g.vecs], [dst.key(c)], out=dst.ap(c), in0=xx[:, c, :], scalar=mc, in1=x[:, c, t0:t0 + TW],
                    op0=ALU.mult, op1=ALU.add)
        nx = [0]
        class XM:
            def __init__(self, slot):
                self.slot = slot
            def ap(self, c):
                return xb[:, c, 1536 + self.slot * TW:1536 + (self.slot + 1) * TW]
            def key(self, c):
                return R(xb, ("xm", self.slot, c))
        def nxm():
            nx[0] += 1
            return XM(nx[0] % 2)
        d_ = nxm(); mix(1, d_)
        for k in range(NCH):
            S.i("pe", "matmul", [w1, d_.key(k)], [PB[0]], out=PB[0][0:64, 0:TW], lhsT=w1[:, k, :], rhs=d_.ap(k), start=(k == 0), stop=(k == NCH - 1))
        S.i("act", "activation", [PB[0]], [hw], out=hw[:], in_=PB[0][0:64, 0:TW], func=AF.Tanh)
        d_ = nxm(); mix(4, d_)
        for k in range(NCH):
            S.i("pe", "matmul", [a1, d_.key(k)], [PB[1]], out=PB[1][0:64, 0:TW], lhsT=a1[:, k, :], rhs=d_.ap(k), start=(k == 0), stop=(k == NCH - 1))
        S.i("act", "activation", [PB[1]], [ha], out=ha[:], in_=PB[1][0:64, 0:TW], func=AF.Copy)
        d_ = nxm(); mix(5, d_)
        for (lo, hi, kc) in ((0, 128, 0), (128, 160, 1)):
            for k in range(NCH):
                S.i("pe", "matmul", [g1, d_.key(k)], [PB[2]], out=PB[2][0:hi - lo, 0:TW], lhsT=g1[:, k, lo:hi], rhs=d_.ap(k), start=(k == 0), stop=(k == NCH - 1))
            S.i("act", "activation", [PB[2]], [R(sgf, kc)], out=sgf[0:hi - lo, kc, :], in_=PB[2][0:hi - lo, 0:TW], func=AF.Tanh, scale=0.5)
            S.i("dve", "tensor_scalar", [R(sgf, kc)], [R(sg, kc)], out=sg[0:hi - lo, kc, :], in0=sgf[0:hi - lo, kc, :], scalar1=0.5, scalar2=0.5, op0=ALU.mult, op1=ALU.add)
        xr = nxm(); mix(0, xr)
        xk = nxm(); mix(2, xk)
        if stg < 3:
            continue
        for c in range(NCH):
            wr = wget()
            wk_ = wget()
            d = TS[c % 2]
            tA, tsw, tcs, te1, te2, te3, tkk, tkk2, trn, tt_ = (d[k_] for k_ in ("tA", "tsw", "tcs", "te1", "te2", "te3", "tkk", "tkk2", "trn", "tt"))
            td3 = tsw; tkkn = tkk; ttb = tkk; tkm = tt_
            if c % 2 == 0:
                rp, kp, zw, za, ssp = PB[0], PB[1], PB[2], PB[3], PB[4]
            else:
                rp, kp, zw, za, ssp = PB[5], PB[6], PB[2], PB[3], PB[4]
            for k in range(NCH):
                S.i("pe", "matmul", [wr, xr.key(k)], [rp], out=rp[:, 0:TW], lhsT=wr[:, k, :], rhs=xr.ap(k), start=(k == 0), stop=(k == NCH - 1))
            for k in range(NCH):
                S.i("pe", "matmul", [wk_, xk.key(k)], [kp], out=kp[:, 0:TW], lhsT=wk_[:, k, :], rhs=xk.ap(k), start=(k == 0), stop=(k == NCH - 1))
            S.i("pe", "matmul", [w2, hw], [zw], out=zw[:, 0:TW], lhsT=w2[:, c * 128:(c + 1) * 128], rhs=hw[:], start=True, stop=True)
            S.i("pe", "matmul", [a2, ha], [za], out=za[:, 0:TW], lhsT=a2[:, c * 128:(c + 1) * 128], rhs=ha[:], start=True, stop=True)
            S.i("act", "activation", [za, om], [tA], out=tA[:], in_=za[:, 0:TW], func=AF.Tanh, bias=om[:, 56 + c:57 + c], scale=0.5)
            S.i("act", "activation", [zw, om], [tsw], out=tsw[:], in_=zw[:, 0:TW], func=AF.Tanh, bias=om[:, 64 + c:65 + c], scale=0.5)
            S.i("dve", "tensor_scalar", [tA], [tA], out=tA[:], in0=tA[:], scalar1=0.5, scalar2=0.5, op0=ALU.mult, op1=ALU.add)
            S.i("dve", "tensor_scalar", [tsw], [tsw], out=tsw[:], in0=tsw[:], scalar1=0.5, scalar2=0.5, op0=ALU.mult, op1=ALU.add)
            for n in range(2):
                cs = slice(n * 128, (n + 1) * 128)
                S.i("dve", "tensor_tensor_scan", [tsw, g.cf], [R(tcs, n)], out=tcs[:, cs], data0=ones, data1=tsw[:, cs], initial=0.0, op0=ALU.mult, op1=ALU.add)
            S.i("dve", "tensor_tensor", [tcs, tsw], [td3], out=td3[:], in0=tcs[:], in1=tsw[:], op=ALU.subtract)
            LD = 0.6065306597126334
            S.i("act", "activation", [tcs], [te1], out=te1[:], in_=tcs[:], func=AF.Exp, scale=-LD)
            S.i("act", "activation", [tcs], [te2], out=te2[:], in_=tcs[:], func=AF.Exp, scale=LD)
            S.i("act", "activation", [td3], [te3], out=te3[:], in_=td3[:], func=AF.Exp, scale=-LD)
            S.i("act", "activation", [te1], [R(PC, c)], out=PC[:, c, :], in_=te1[:, 127:TW:128], func=AF.Copy)
            S.i("dve", "tensor_scalar", [kp, g.vecs], [tkk], out=tkk[:], in0=kp[:, 0:TW], scalar1=V("k_k", c), scalar2=None, op0=ALU.mult)
            S.i("act", "activation", [tkk], [tkk2], out=tkk2[:], in_=tkk[:], func=AF.Square)
            S.i("pe", "matmul", [tkk2, g.cb], [ssp], out=ssp[:, 0:TW], lhsT=blk1, rhs=tkk2[:], start=True, stop=True)
            S.i("dve", "tensor_scalar", [ssp], [trn], out=trn[:], in0=ssp[:, 0:TW], scalar1=1e-24, scalar2=None, op0=ALU.max)
            S.i("act", "activation", [trn], [trn], out=trn[:], in_=trn[:], func=AF.Ln)
            S.i("act", "activation", [trn], [trn], out=trn[:], in_=trn[:], func=AF.Exp, scale=-0.5)
            S.i("dve", "tensor_tensor", [tkk, trn], [tkkn], out=tkkn[:], in0=tkk[:], in1=trn[:], op=ALU.mult)
            S.i("dve", "tensor_scalar", [tA, g.vecs, om], [tt_], out=tt_[:], in0=tA[:], scalar1=V("k_a", c), scalar2=om[:, 48 + c:49 + c], op0=ALU.mult, op1=ALU.add)
            S.i("dve", "tensor_tensor", [kp, tt_], [tkm], out=tkm[:], in0=kp[:, 0:TW], in1=tt_[:], op=ALU.mult)
            full = slice(0, TW)
            S.i("dve", "scalar_tensor_tensor", [tkkn, te3], [FMK(0, c)], out=FM(0, c, full), in0=tkkn[:], scalar=-1.0, in1=te3[:], op0=ALU.mult, op1=ALU.mult)
            S.i("dve", "tensor_tensor", [tkkn, tA], [ttb], out=ttb[:], in0=tkkn[:], in1=tA[:], op=ALU.mult)
            S.i("dve", "tensor_tensor", [ttb, te2], [FMK(1, c)], out=FM(1, c, full), in0=ttb[:], in1=te2[:], op=ALU.mult)
            S.i("dve", "tensor_tensor", [tkm, te2], [FMK(2, c)], out=FM(2, c, full), in0=tkm[:], in1=te2[:], op=ALU.mult)
            S.i("dve", "tensor_tensor", [rp, te1], [FMK(3, c)], out=FM(3, c, full), in0=rp[:, 0:TW], in1=te1[:], op=ALU.mult)
            S.i("dve", "scalar_tensor_tensor", [rp, g.vecs, tkm], [FMK(4, c)], out=FM(4, c, full), in0=rp[:, 0:TW], scalar=V("r_k", c), in1=tkm[:], op0=ALU.mult, op1=ALU.mult)
        xv = nxm(); mix(3, xv)
        for c in range(NCH):
            wv_ = wget()
            vp = PB[5 + (c % 2)]
            for k in range(NCH):
                S.i("pe", "matmul", [wv_, xv.key(k)], [vp], out=vp[:, 0:TW], lhsT=wv_[:, k, :], rhs=xv.ap(k), start=(k == 0), stop=(k == NCH - 1))
            S.i("act", "activation", [vp], [FMK(5, c)], out=FM(5, c, slice(0, TW)), in_=vp[:, 0:TW], func=AF.Copy)
        if stg < 5:
            continue
        for n in range(2):
            cs = slice(n * 128, (n + 1) * 128)
            for a_, dst in ((0, Atm), (1, Btm), (2, Ktm), (5, Vtm)):
                for c in range(NCH):
                    S.i("pe", "transpose", [FMK(a_, c), g.cb], [pT], out=pT[:, c * 128:(c + 1) * 128], in_=FM(a_, c, cs), identity=ident)
                S.i("dve" if a_ in (0, 2) else "act", "tensor_copy" if a_ in (0, 2) else "activation", [pT], [dst],
                    **(dict(out=dst[:], in_=pT[:]) if a_ in (0, 2) else dict(out=dst[:], in_=pT[:], func=AF.Copy)))
            if stg < 6:
                continue
            for hgp in range(2):
                grp = [2 * hgp, 2 * hgp + 1]
                def HP(h):
                    return slice((h % 2) * 64, (h % 2) * 64 + 64), h // 2
                for gi, hg in enumerate(grp):
                    heads = [4 * hg + q for q in range(4)]
                    specs = [(1, 0, 0, Mk[gi][0], None), (0, 1, 2, Lk[gi][0], None), (2, 0, 0, Mak[gi], None), (1, 3, 1, Mrb, hg), (2, 3, 1, Mrk, hg)]
                    for si, (la, ra, mki, dst, full16) in enumerate(specs):
                        dview = dst[:] if full16 is None else dst[:, hg * 4:(hg + 1) * 4, :]
                        wkey = dst if full16 is None else R(dst, hg)
                        for par in range(2):
                            bk = PB[(2 * si + par) % 6]
                            for qq in range(2):
                                h = heads[2 * qq + par]
                                ps_, c = HP(h)
                                S.i("pe", "matmul", [FMK(la, c), FMK(ra, c)], [bk], out=bk[:, qq * 128:(qq + 1) * 128], lhsT=FM(la, c, cs)[ps_, :], rhs=FM(ra, c, cs)[ps_, :],
                                    start=True, stop=True)
                            S.i("dve", "tensor_tensor", [bk, g.mk], [wkey], out=dview[:, par::2, :],
                                in0=bk[:, 0:256].rearrange("p (a b) -> p a b", b=128), in1=g.mk[:, mki, 0:256].rearrange("p (a b) -> p a b", b=128), op=ALU.mult)
                if stg < 7:
                    continue
                cur = [0, 0]
                NTin = [Mk[0][0], Mk[1][0]]
                for r in range(7):
                    for gi in range(2):
                        Lc, Mc = Lk[gi][cur[gi]], Mk[gi][cur[gi]]
                        Ln, Mn = Lk[gi][1 - cur[gi]], Mk[gi][1 - cur[gi]]
                        bN, bL, bM = PB[3 * gi], PB[3 * gi + 1], PB[3 * gi + 2]
                        if r >= 1:
                            NTo = NTt[gi][r % 2]
                            for q in range(4):
                                S.i("pe", "matmul", [NTin[gi], g.cb], [bN], out=bN[:, q * 128:(q + 1) * 128], lhsT=ident, rhs=NTin[gi][:, q, :], start=True, stop=False)
                                S.i("pe", "matmul", [Mc, g.cb], [bN], out=bN[:, q * 128:(q + 1) * 128], lhsT=ident, rhs=Mc[:, q, :], start=False, stop=False)
                                S.i("pe", "matmul", [Lc, NTin[gi]], [bN], out=bN[:, q * 128:(q + 1) * 128], lhsT=Lc[:, q, :], rhs=NTin[gi][:, q, :], start=False, stop=True)
                        if r < 6:
                            for q in range(4):
                                S.i("pe", "matmul", [Mc, Lc], [bL], out=bL[:, q * 128:(q + 1) * 128], lhsT=Mc[:, q, :], rhs=Lc[:, q, :], start=True, stop=True)
                            for q in range(4):
                                S.i("pe", "matmul", [Lc, Mc], [bM], out=bM[:, q * 128:(q + 1) * 128], lhsT=Lc[:, q, :], rhs=Mc[:, q, :], start=True, stop=True)
                        if r >= 1:
                            if gi == 0:
                                S.i("act", "activation", [bN], [NTo], out=NTo[:].rearrange("p a b -> p (a b)"), in_=bN[:], func=AF.Copy)
                            else:
                                S.i("dve", "tensor_copy", [bN], [NTo], out=NTo[:].rearrange("p a b -> p (a b)"), in_=bN[:])
                            NTin[gi] = NTo
                        if r < 6:
                            S.i("act", "activation", [bL], [Ln], out=Ln[:].rearrange("p a b -> p (a b)"), in_=bL[:], func=AF.Copy)
                            S.i("dve", "tensor_copy", [bM], [Mn], out=Mn[:].rearrange("p a b -> p (a b)"), in_=bM[:])
                            cur[gi] = 1 - cur[gi]
                if stg < 8:
                    continue
                for gi, hg in enumerate(grp):
                    heads = [4 * hg + q for q in range(4)]
                    NTf = NTin[gi]
                    for q, h in enumerate(heads):
                        S.i("pe", "matmul", [Mak[gi], Vtm], [PB[6]], out=PB[6][:, q * 64:(q + 1) * 64], lhsT=Mak[gi][:, q, :], rhs=Vtm[:, h * 64:(h + 1) * 64], start=True, stop=True)
                    S.i("act", "activation", [PB[6]], [Xb], out=Xb[:], in_=PB[6][:, 0:256], func=AF.Copy)
                    for q, h in enumerate(heads):
                        S.i("pe", "matmul", [Xb, g.cb], [PB[6]], out=PB[6][:, 256 + q * 64:256 + (q + 1) * 64], lhsT=ident, rhs=Xb[:, q * 64:(q + 1) * 64], start=True, stop=False)
                        S.i("pe", "matmul", [NTf, Xb], [PB[6]], out=PB[6][:, 256 + q * 64:256 + (q + 1) * 64], lhsT=NTf[:, q, :], rhs=Xb[:, q * 64:(q + 1) * 64], start=False, stop=True)
                    S.i("dve", "tensor_copy", [PB[6]], [R(Vhat, hg)], out=Vhat[:, hg * 256:(hg + 1) * 256], in_=PB[6][:, 256:512])
                    bA = PB[gi]
                    for q, h in enumerate(heads):
                        ps_, c = HP(h)
                        cc = (c % 2) * 128
                        S.i("pe", "matmul", [Atm, NTf], [bA], out=bA[ps_, cc:cc + 128], lhsT=Atm[:, h * 64:(h + 1) * 64], rhs=NTf[:, q, :], start=True, stop=True)
                    S.i("dve", "tensor_tensor", [bA, FMK(0, 2 * hg), FMK(0, 2 * hg + 1)], [R(AhT, hg)], out=AhT[:, 2 * hg:2 * hg + 2, :],
                        in0=bA[:, 0:256].rearrange("p (a b) -> p a b", b=128), in1=xb[:, 2 * hg:2 * hg + 2, cs.start:cs.stop], op=ALU.add)
            if stg < 9:
                continue
            Ubv = Ub[:].rearrange("p (h n) -> p h n", n=64)
            Vhv = Vhat[:].rearrange("p (h n) -> p h n", n=64)
            for par in range(2):
                bk = PB[par]
                for hh in range(8):
                    h = 2 * hh + par
                    ps_, c = slice(par * 64, par * 64 + 64), h // 2
                    S.i("pe", "matmul", [AhT, ST], [bk], out=bk[:, hh * 64:(hh + 1) * 64], lhsT=AhT[ps_, c, :], rhs=ST[ps_, c, :], start=True, stop=True)
            for par in range(2):
                S.i("dve", "tensor_tensor", [PB[par], Vhat], [R(Ub, par)], out=Ubv[:, par::2, :], in0=PB[par][:].rearrange("p (h n) -> p h n", n=64),
                    in1=Vhv[:, par::2, :], op=ALU.add)
            for c in range(NCH):
                S.i("act", "activation", [R(ST, c), PC], [R(STs, c)], out=STs[:, c, :], in_=ST[:, c, :], func=AF.Identity, scale=PC[:, c, n:n + 1])
            for par in range(2):
                bk = PB[2 + par]
                for hh in range(8):
                    h = 2 * hh + par
                    ps_, c = slice(par * 64, par * 64 + 64), h // 2
                    o_ = bk[:, hh * 64:(hh + 1) * 64]
                    S.i("pe", "matmul", [FMK(3, c), ST], [bk], out=o_, lhsT=FM(3, c, cs)[ps_, :], rhs=ST[ps_, c, :], start=True, stop=False)
                    S.i("pe", "matmul", [Mrk, Vtm], [bk], out=o_, lhsT=Mrk[:, h, :], rhs=Vtm[:, h * 64:(h + 1) * 64], start=False, stop=False)
                    S.i("pe", "matmul", [Mrb, Ub], [bk], out=o_, lhsT=Mrb[:, h, :], rhs=Ub[:, h * 64:(h + 1) * 64], start=False, stop=True)
            for h in range(16):
                ps_, c = slice((h % 2) * 64, (h % 2) * 64 + 64), h // 2
                o_ = PB[4][ps_, c * 64:(c + 1) * 64]
                S.i("pe", "matmul", [Ktm, Vtm], [PB[4]], out=o_, lhsT=Ktm[:, h * 64:(h + 1) * 64], rhs=Vtm[:, h * 64:(h + 1) * 64], start=True, stop=False)
                S.i("pe", "matmul", [Btm, Ub], [PB[4]], out=o_, lhsT=Btm[:, h * 64:(h + 1) * 64], rhs=Ub[:, h * 64:(h + 1) * 64], start=False, stop=True)
            for c in range(NCH):
                S.i("dve", "scalar_tensor_tensor", [PB[4], PC, R(STs, c)], [R(ST, c)], out=ST[:, c, :], in0=PB[4][:, c * 64:(c + 1) * 64], scalar=PC[:, c, n:n + 1],
                    in1=STs[:, c, :], op0=ALU.mult, op1=ALU.add)
            for b in range(2):
                S.i("dve", "tensor_reduce", [PB[2 + b]], [R(st4, ("s", b))], out=st4[:, 0, b::2], in_=PB[2 + b][:].rearrange("p (h n) -> p h n", n=64), axis=AX.X, op=ALU.add)
                S.i("act", "activation", [PB[2 + b]], [R(yn, None)], out=yn[:, b * 512:(b + 1) * 512], in_=PB[2 + b][:], func=AF.Square)
                S.i("dve", "tensor_reduce", [R(yn, None)], [R(st4, ("q", b))], out=st4[:, 1, b::2], in_=yn[:, b * 512:(b + 1) * 512].rearrange("p (h n) -> p h n", n=64), axis=AX.X, op=ALU.add)
            S.i("dve", "tensor_scalar", [st4], [R(st4, "m")], out=st4[:, 2, :], in0=st4[:, 0, :], scalar1=1.0 / 64, scalar2=None, op0=ALU.mult)
            S.i("dve", "tensor_tensor", [st4], [R(st4, "v")], out=st4[:, 3, :], in0=st4[:, 2, :], in1=st4[:, 2, :], op=ALU.mult)
            S.i("dve", "scalar_tensor_tensor", [st4], [R(st4, "v")], out=st4[:, 3, :], in0=st4[:, 1, :], scalar=1.0 / 64, in1=st4[:, 3, :], op0=ALU.mult, op1=ALU.subtract)
            S.i("dve", "tensor_scalar", [st4], [R(st4, "v")], out=st4[:, 3, :], in0=st4[:, 3, :], scalar1=64e-5, scalar2=None, op0=ALU.add)
            S.i("act", "activation", [st4], [R(st4, "v")], out=st4[:, 3, :], in_=st4[:, 3, :], func=AF.Ln)
            S.i("act", "activation", [st4], [R(st4, "v")], out=st4[:, 3, :], in_=st4[:, 3, :], func=AF.Exp, scale=-0.5)
            for c in range(NCH):
                S.i("pe", "matmul", [FMK(4, c), g.cb], [PB[5]], out=PB[5][:, 2 * c:2 * c + 2], lhsT=FM(4, c, cs), rhs=hind, start=True, stop=True)
            S.i("dve", "tensor_copy", [PB[5]], [bsum], out=bsum[:], in_=PB[5][:, 0:16])
            for h in range(16):
                b = h % 2
                hs = slice(h * 64, (h + 1) * 64)
                S.i("dve", "tensor_scalar", [PB[2 + b], st4], [R(yn, h)], out=yn[:, hs], in0=PB[2 + b][:, (h // 2) * 64:(h // 2 + 1) * 64],
                    scalar1=st4[:, 2, h:h + 1], scalar2=st4[:, 3, h:h + 1], op0=ALU.subtract, op1=ALU.mult)
            S.i("dve", "tensor_tensor", [yn, rows], [yn], out=yn[:], in0=yn[:], in1=rows[:, 0, :], op=ALU.mult)
            S.i("dve", "tensor_tensor", [yn, rows], [yn], out=yn[:], in0=yn[:], in1=rows[:, 1, :], op=ALU.add)
            for h in range(16):
                hs = slice(h * 64, (h + 1) * 64)
                S.i("dve", "scalar_tensor_tensor", [Vtm, bsum, yn], [R(yn, h)], out=yn[:, hs], in0=Vtm[:, hs], scalar=bsum[:, h:h + 1], in1=yn[:, hs], op0=ALU.mult, op1=ALU.add)
            for b in range(2):
                S.i("pe", "matmul", [sg, g2], [PB[b]], out=PB[b][:], lhsT=sg[:, 0, cs], rhs=g2[:, 0, b * 512:(b + 1) * 512], start=True, stop=False)
                S.i("pe", "matmul", [sg, g2], [PB[b]], out=PB[b][:], lhsT=sg[0:32, 1, cs], rhs=g2[0:32, 1, b * 512:(b + 1) * 512], start=False, stop=True)
                S.i("dve", "tensor_tensor", [yn, PB[b]], [R(ogtm, b)], out=ogtm[:, b * 512:(b + 1) * 512], in0=yn[:, b * 512:(b + 1) * 512], in1=PB[b][:], op=ALU.mult)
            for c in range(NCH):
                S.i("pe", "transpose", [ogtm, g.cb], [pT], out=pT[:, c * 128:(c + 1) * 128], in_=ogtm[:, c * 128:(c + 1) * 128], identity=ident)
            for c in range(NCH):
                S.i("act", "activation", [pT], [R(ogT, (c, n))], out=ogT[:, c, cs], in_=pT[:, c * 128:(c + 1) * 128], func=AF.Copy)
        if stg < 11:
            continue
        for m in range(NCH):
            wo = wget()
            bk = PB[5 + (m % 2)]
            for k in range(NCH):
                S.i("pe", "matmul", [wo, ogT], [bk], out=bk[:, 0:TW], lhsT=wo[:, k, :], rhs=ogT[:, k, :], start=(k == 0), stop=(k == NCH - 1))
            xs = x[:, m, t0:t0 + TW]
            S.i("dve", "scalar_tensor_tensor", [bk, R(x, (m, tt))], [R(x, (m, tt))], out=xs, in0=xs, scalar=ALPHA, in1=bk[:, 0:TW], op0=ALU.mult, op1=ALU.add)
    S.pop()
    S.push()
    layer_norm(g, ("ln1_g%d" % i, "ln1_b%d" % i), range(4))
    S.pop()

def make_consts():
    c = np.zeros((128, 1024), np.float32)
    c[:, 0:128] = 1.0
    c[:, 128:256] = np.eye(128, dtype=np.float32)
    bo = np.zeros((128, 128), np.float32); bo[:64, :64] = 1.0; bo[64:, 64:] = 1.0
    c[:, 256:384] = bo
    t = np.arange(128)
    c[:, 384:512] = np.where(t[None, :] <= t[:, None], 0.0, -30000.0)
    c[:, 512:640] = (t[:, None] < t[None, :]).astype(np.float32)
    c[:, 640:768] = (t[:, None] <= t[None, :]).astype(np.float32)
    c[:64, 768] = 1.0; c[64:, 769] = 1.0
    return c


def make_ret_tables():
    dk = 256
    inv = (1.0 / (np.float32(10000.0) ** np.linspace(0.0, 1.0, dk // 2, dtype=np.float32))).astype(np.float32)
    pos = np.arange(T_SEQ, dtype=np.float32)
    ang = (pos[:, None] * inv[None, :]).astype(np.float32)
    cos = np.cos(ang).astype(np.float32); sin = np.sin(ang).astype(np.float32)
    f = np.arange(dk)
    cosf = cos[:, f // 2].T
    sgn = np.where(f % 2 == 0, -1.0, 1.0).astype(np.float32)
    sinf = (sin[:, f // 2].T * sgn[:, None]).astype(np.float32)
    tab = np.stack([cosf.reshape(2, 128, T_SEQ).transpose(1, 0, 2), sinf.reshape(2, 128, T_SEQ).transpose(1, 0, 2)])
    m = np.zeros((4, 128, 384), np.float32)
    idx = np.arange(128, dtype=np.float64)
    for h in range(4):
        lg = np.log(1.0 - 2.0 ** (-5.0 - h))
        rel = idx[None, :] - idx[:, None]
        m[h, :, 0:128] = np.where(rel >= 0, np.exp(lg * np.maximum(rel, 0)), 0.0) / 16.0
        m[h, :, 128:256] = np.exp(lg * (idx + 1.0))[None, :]
        m[h, :, 256] = np.exp(lg * (127.0 - idx)) / 16.0
        m[h, :, 257] = np.exp(lg * 128.0)
    return np.ascontiguousarray(tab.astype(np.float32)), m


FULL_PLAN = [("rwkv", 0, 0), ("ffn", 0), ("ret", 1), ("ffn", 1), ("moba", 2), ("ffn", 2), ("rwkv", 3, 1), ("ffn", 3)]
_PLAN = FULL_PLAN
_NC_CACHE = {}


def host_inputs(inputs):
    shared = {k: np.ascontiguousarray(np.asarray(inputs[k], np.float32)) for k in WEIGHT_SHAPES}
    shared["vecs"] = pack_vecs(inputs)
    shared["consts"] = make_consts()
    rows = np.zeros((4, 128, 1024), np.float32)
    for j in range(2):
        rows[2 * j] = np.broadcast_to(np.asarray(inputs["rwkv_gn_g"][j], np.float32)[None, :], (128, 1024))
        rows[2 * j + 1] = np.broadcast_to(np.asarray(inputs["rwkv_gn_b"][j], np.float32)[None, :], (128, 1024))
    shared["rows"] = rows
    ti = np.arange(128)
    mk = np.zeros((128, 3, 512), np.float32)
    mk[:, 0, :] = np.tile((ti[:, None] < ti[None, :]).astype(np.float32), (1, 4))
    mk[:, 1, :] = np.tile((ti[:, None] <= ti[None, :]).astype(np.float32), (1, 4))
    mk[:, 2, :] = np.tile((ti[:, None] > ti[None, :]).astype(np.float32), (1, 4))
    shared["mk"] = mk
    tab, msk = make_ret_tables()
    shared["rtab"] = tab
    shared["rmask"] = msk
    wqk = np.asarray(inputs["ret_w_in"][0][:, :2048], np.float32)
    shared["ret_sw"] = np.ascontiguousarray(wqk.reshape(1024, 1024, 2)[:, :, ::-1].reshape(1024, 2048))
    return shared


def run_plan(plan, inputs, x_full, n_cores=8):
    key = repr(plan)
    if key not in _NC_CACHE:
        _NC_CACHE[key] = build_program(plan)
    nc = _NC_CACHE[key]
    shared = host_inputs(inputs)
    in_maps = []
    for b in range(n_cores):
        m = dict(shared)
        m["xT"] = np.ascontiguousarray(np.asarray(x_full[b], np.float32).T)
        in_maps.append(m)
    res = run_bass_kernel_spmd(nc, in_maps, core_ids=list(range(n_cores)))
    return np.stack([np.ascontiguousarray(r["outT"].T) for r in res.results]).astype(np.float32)


def kernel(**inputs):
    return run_plan(_PLAN, inputs, inputs["x"], 8)
```

```python
import numpy as np
import concourse.bass as bass
import concourse.mybir as mybir
from concourse.bass_utils import run_bass_kernel_spmd
from contextlib import ExitStack

_DBG = {}
F32 = mybir.dt.float32
BF16 = mybir.dt.bfloat16
AF = mybir.ActivationFunctionType
ALU = mybir.AluOpType
AX = mybir.AxisListType

ENGS = ("pe", "act", "dve", "pool", "sp")


class T:
    def __init__(self, S, t, name):
        self.S = S
        self.t = t
        self.name = name
        self.st = {}
        self.is_psum = False
        self.dsem = None
        self.dcnt = 0

    def __getitem__(self, idx):
        return self.t[idx]


class R:
    __slots__ = ("tile", "key")
    def __init__(self, tile, key=None):
        self.tile = tile
        self.key = key


class Sched:
    def __init__(self, nc):
        self.nc = nc
        self.es = ExitStack()
        self.ins = {e: [] for e in ENGS}
        self.cnt = {e: 0 for e in ENGS}
        self.seen = {e: {} for e in ENGS}
        self.sems = {}
        self.needed = {e: set() for e in ENGS}
        self.nsem = 0
        for e in ENGS:
            self.sems[("e", e)] = self.es.enter_context(nc.semaphore("sem_" + e))
        self.dma_pool = []
        self.out_dma = []
        self.pend = {}
        self.scopes = []

    def _es(self):
        return self.scopes[-1] if self.scopes else self.es

    def push(self):
        self.scopes.append(ExitStack())

    def pop(self):
        self.barrier()
        self.scopes.pop().close()

    def sbuf(self, name, shape, dt):
        self.uid = getattr(self, "uid", 0) + 1
        name = "%s_%d" % (name, self.uid)
        t = self._es().enter_context(self.nc.sbuf_tensor(name, list(shape), dt))
        return T(self, t, name)

    def psum(self, name, shape, dt=F32):
        self.uid = getattr(self, "uid", 0) + 1
        name = "%s_%d" % (name, self.uid)
        t = self._es().enter_context(self.nc.psum_tensor(name, list(shape), dt))
        tt = T(self, t, name)
        tt.is_psum = True
        return tt

    def new_dsem(self, name):
        self.nsem += 1
        key = ("d", self.nsem)
        self.sems[key] = self.es.enter_context(self.nc.semaphore("dsem%d_%s" % (self.nsem, name)))
        return key

    def _states(self, ref, create=True):
        tile, key = ref.tile, ref.key
        if key is None:
            if None not in tile.st:
                tile.st[None] = [None, {}]
            return list(tile.st.values())
        out = []
        if None in tile.st:
            out.append(tile.st[None])
        if key not in tile.st:
            tile.st[key] = [None, {}]
        out.append(tile.st[key])
        return out

    def _collect(self, eng, reads, writes):
        waits = {}
        def need(w, same_ok=False):
            if w is None:
                return
            k, v = w
            if same_ok and k == ("e", eng) and eng == "pe":
                return
            if waits.get(k, 0) < v:
                waits[k] = v
        for r in reads:
            for st in self._states(r):
                need(st[0])
                if r.tile.is_psum:
                    for rk, rv in st[1].items():
                        if rk != ("e", eng):
                            need((rk, rv))
        for w in writes:
            for st in self._states(w):
                need(st[0])
                for rk, rv in st[1].items():
                    if rk == ("e", eng) and eng == "pe":
                        continue
                    need((rk, rv))
        final = []
        for k, v in waits.items():
            if k == ("e", eng) and eng == "pe":
                continue
            if self.seen[eng].get(k, 0) < v:
                self.seen[eng][k] = v
                final.append((k, v))
                if k[0] == "e":
                    self.needed[k[1]].add(v)
        return final

    def _commit(self, tag, reads, writes):
        for r in reads:
            if r.key is None:
                for k, s in r.tile.st.items():
                    if s[1].get(tag[0], 0) < tag[1]:
                        s[1][tag[0]] = tag[1]
            else:
                s = r.tile.st[r.key]
                if s[1].get(tag[0], 0) < tag[1]:
                    s[1][tag[0]] = tag[1]
        for w in writes:
            if w.key is None:
                w.tile.st = {None: [tag, {}]}
            else:
                w.tile.st[w.key] = [tag, {}]

    def i(self, eng, meth, reads=(), writes=(), **kw):
        return self.op(eng, (meth, kw), reads, writes)

    def _norm(self, refs):
        out = []
        for r in refs:
            if not isinstance(r, R):
                r = R(r)
            if r.tile.is_psum and r.key is not None:
                r = R(r.tile)
            out.append(r)
        return out

    def op(self, eng, fn, reads=(), writes=()):
        reads = self._norm(reads)
        writes = self._norm(writes)
        waits = self._collect(eng, reads, writes)
        self.cnt[eng] += 1
        idx = self.cnt[eng]
        self.ins[eng].append([fn, waits, idx, None])
        self._commit((("e", eng), idx), reads, writes)
        return idx

    def dma(self, q, out_ap, in_ap, reads=(), writes=(), out_final=False, owner=None, **kw):
        reads = [r if isinstance(r, R) else R(r) for r in reads]
        writes = [w if isinstance(w, R) else R(w) for w in writes]
        waits = self._collect(q, reads, writes)
        if owner is None:
            owner = (writes[0] if writes else reads[0]).tile
        if owner.dsem is None:
            owner.dsem = self.new_dsem(owner.name)
        owner.dcnt += 16
        tag = (owner.dsem, owner.dcnt)
        self.cnt[q] += 1
        idx = self.cnt[q]
        self.ins[q].append([lambda e: e.dma_start(out=out_ap, in_=in_ap, **kw), waits, idx, tag])
        self._commit(tag, reads, writes)
        self.pend[tag[0]] = tag[1]
        if out_final:
            self.out_dma.append(tag)
        return tag

    def barrier(self):
        for f in ENGS:
            if self.cnt[f] and (self.ins[f][-1][0] is None or self.ins[f][-1][3] is not None):
                self.cnt[f] += 1
                self.ins[f].append([None, [], self.cnt[f], None])
        for e in ENGS:
            waits = []
            for f in ENGS:
                if f == e or self.cnt[f] == 0:
                    continue
                v = self.cnt[f]
                if self.seen[e].get(("e", f), 0) < v:
                    self.seen[e][("e", f)] = v
                    waits.append((("e", f), v))
                    self.needed[f].add(v)
            for k, v in self.pend.items():
                if self.seen[e].get(k, 0) < v:
                    self.seen[e][k] = v
                    waits.append((k, v))
            if waits:
                self.cnt[e] += 1
                self.ins[e].append([None, waits, self.cnt[e], None])

    def emit(self):
        nc = self.nc
        fwd = {}
        for k, v in self.out_dma:
            fwd[k] = max(fwd.get(k, 0), v)
        fw = list(fwd.items())
        self.cnt["sp"] += 1
        self.ins["sp"].append([None, fw, self.cnt["sp"], None])
        rank = {}
        for e in ENGS:
            s = sorted(self.needed[e])
            rank[e] = {v: i + 1 for i, v in enumerate(s)}
        engmap = {"pe": "tensor", "act": "scalar", "dve": "vector", "pool": "gpsimd", "sp": "sync"}
        with nc.Block() as block:
            for e in ENGS:
                lst = self.ins[e]
                if not lst:
                    continue
                def body(eng, e=e, lst=lst):
                    for fn, waits, idx, dtag in lst:
                        for k, v in waits:
                            if k[0] == "e":
                                eng.wait_ge(self.sems[k], rank[k[1]][v])
                            else:
                                eng.wait_ge(self.sems[k], v)
                        if fn is None:
                            if idx in self.needed[e]:
                                eng.nop().then_inc(self.sems[("e", e)], 1)
                            continue
                        ins = fn(eng) if callable(fn) else getattr(eng, fn[0])(**fn[1])
                        if dtag is not None:
                            ins.then_inc(self.sems[dtag[0]], 16)
                            if idx in self.needed[e]:
                                raise RuntimeError("dma instr needed as engine milestone")
                        elif idx in self.needed[e]:
                            ins.then_inc(self.sems[("e", e)], 1)
                getattr(block, engmap[e])(body)
        self.es.close()

D = 1024; T_SEQ = 2048; DEPTH = 4; FF = 2816; NCH = 8; NPAIR = 22
ALPHA = float((2 * DEPTH) ** 0.25)
LN_EPS = 1e-5

WEIGHT_SHAPES = {
    "rwkv_w_rkv": [2, 3, 1024, 1024], "rwkv_w1": [2, 1024, 64], "rwkv_w2": [2, 64, 1024],
    "rwkv_a1": [2, 1024, 64], "rwkv_a2": [2, 64, 1024], "rwkv_g1": [2, 1024, 160], "rwkv_g2": [2, 160, 1024],
    "rwkv_w_o": [2, 1024, 1024], "ret_w_in": [1, 1024, 6144], "ret_w_o": [1, 2048, 1024],
    "moba_w_qkv": [1, 1024, 3072], "moba_w_o": [1, 1024, 1024],
    "ffn_w_in": [4, 1024, 5632], "ffn_w_out": [4, 2816, 1024],
}


def _fm(v):
    v = np.asarray(v, np.float32).reshape(-1, 128)
    return np.ascontiguousarray(v.T)


def vec_layout():
    off = {}
    n = 0
    def add(name, cols):
        nonlocal n
        off[name] = n
        n += cols
    for i in range(4):
        for nm in ("ln1_g", "ln1_b", "ln2_g", "ln2_b"):
            add("%s%d" % (nm, i), 8)
        for k in range(3):
            add("cw%d_%d" % (k, i), 44)
        add("cb_%d" % i, 44)
    for j in range(2):
        for m in range(6):
            add("mix%d_%d" % (m, j), 8)
        for nm in ("w0", "a0", "k_k", "k_a", "r_k"):
            add("%s_%d" % (nm, j), 8)
    add("ret_gn_g", 16)
    add("ret_gn_b", 16)
    return off, n


def pack_vecs(inp):
    off, n = vec_layout()
    out = np.zeros((128, n), np.float32)
    def put(name, v):
        a = _fm(v)
        out[:, off[name]:off[name] + a.shape[1]] = a
    for i in range(4):
        for nm in ("ln1_g", "ln1_b", "ln2_g", "ln2_b"):
            put("%s%d" % (nm, i), inp[nm][i])
        for k in range(3):
            put("cw%d_%d" % (k, i), inp["ffn_conv_w"][i, k])
        put("cb_%d" % i, inp["ffn_conv_b"][i])
    for j in range(2):
        for m in range(6):
            put("mix%d_%d" % (m, j), inp["rwkv_mix"][j, m])
        put("w0_%d" % j, inp["rwkv_w0"][j]); put("a0_%d" % j, inp["rwkv_a0"][j])
        put("k_k_%d" % j, inp["rwkv_k_k"][j]); put("k_a_%d" % j, inp["rwkv_k_a"][j])
        put("r_k_%d" % j, inp["rwkv_r_k"][j].reshape(-1))
    put("ret_gn_g", inp["ret_gn_g"][0]); put("ret_gn_b", inp["ret_gn_b"][0])
    return out


class Ctx:
    pass


def build_program(plan, x_in_bf=True):
    nc = bass.Bass("TRN2", target_bir_lowering=False)
    S = Sched(nc)
    g = Ctx()
    g.nc, g.S = nc, S
    g.W = {k: nc.dram_tensor(k, shp, F32, kind="ExternalInput").ap() for k, shp in WEIGHT_SHAPES.items()}
    voff, nv = vec_layout()
    g.voff = voff
    xT_d = nc.dram_tensor("xT", [D, T_SEQ], F32, kind="ExternalInput").ap()
    vecs_d = nc.dram_tensor("vecs", [128, nv], F32, kind="ExternalInput").ap()
    consts_d = nc.dram_tensor("consts", [128, 1024], F32, kind="ExternalInput").ap()
    g.rows_d = nc.dram_tensor("rows", [4, 128, 1024], F32, kind="ExternalInput").ap()
    g.rtab_d = nc.dram_tensor("rtab", [2, 128, 2, T_SEQ], F32, kind="ExternalInput").ap()
    g.rmask_d = nc.dram_tensor("rmask", [4, 128, 384], F32, kind="ExternalInput").ap()
    g.ret_sw_d = nc.dram_tensor("ret_sw", [1024, 2048], F32, kind="ExternalInput").ap()
    g.mk_d = nc.dram_tensor("mk", [128, 3, 512], F32, kind="ExternalInput").ap()
    out_d = nc.dram_tensor("outT", [D, T_SEQ], F32, kind="ExternalOutput").ap()

    g.x = S.sbuf("x", [128, NCH, T_SEQ], F32)
    g.xb = S.sbuf("xb", [128, NCH, T_SEQ], BF16)
    g.vecs = S.sbuf("vecs", [128, nv], F32)
    g.cf = S.sbuf("cf", [128, 256], F32)
    g.cb = S.sbuf("cb", [128, 512], BF16)
    xv = xT_d.rearrange("(c p) t -> p c t", p=128)
    for c in range(NCH):
        S.dma("sp", g.x[:, c, :], xv[:, c, :], writes=[R(g.x, None)])
    if plan[0][0] != "rwkv":
        for c in range(NCH):
            S.dma("pool", g.xb[:, c, :], xv[:, c, :], writes=[R(g.xb, None)])
    S.dma("sp", g.vecs[:], vecs_d, writes=[g.vecs])
    S.dma("sp", g.cf[:, 0:128], consts_d[:, 0:128], writes=[R(g.cf, None)])
    S.dma("sp", g.cf[:, 128:256], consts_d[:, 384:512], writes=[R(g.cf, None)])
    S.i("pool", "memset", [], [g.cb], ap=g.cb[:], constant=0.0)
    S.dma("pool", g.cb[:, 0:256], consts_d[:, 128:384], writes=[R(g.cb, None)])
    S.dma("pool", g.cb[:, 256:258], consts_d[:, 768:770], writes=[R(g.cb, None)])
    g.mk = S.sbuf("mk", [128, 3, 512], BF16)
    S.dma("pool", g.mk[:], g.mk_d, writes=[g.mk])
    g.ones32 = g.cf[:, 0:128]
    g.identb = g.cb[:, 0:128]

    for si_, step in enumerate(plan):
        kind = step[0]
        if kind == "ffn":
            nxt = plan[si_ + 1][0] if si_ + 1 < len(plan) else None
            ffn_phase(g, step[1], write_xb=(nxt not in (None, "rwkv")))
        elif kind == "ln":
            S.push()
            layer_norm(g, step[1], range(4))
            S.pop()
        elif kind == "moba":
            moba_phase(g, step[1])
        elif kind == "ret":
            ret_phase(g, step[1])
        elif kind == "rwkv":
            rwkv_phase(g, step[1], step[2])
        else:
            raise ValueError(kind)

    S.barrier()
    ov = out_d.rearrange("(c p) t -> p c t", p=128)
    outsem = T(S, None, "outsem")
    for c in range(NCH):
        S.dma("sp", ov[:, c, :], g.x[:, c, :], reads=[R(g.x, None)], out_final=True)
    S.emit()
    return nc


def vcol(g, name, c=0):
    o = g.voff[name] + c
    return g.vecs[:, o:o + 1]


def layer_norm(g, pref, tts, width=512, write_xb=True):
    S = g.S
    gname, bname = pref
    x, xb = g.x, g.xb
    tts = list(tts)
    nt = len(tts)
    sq = [S.sbuf("lnsq", [128, 512], F32) for _ in range(2)]
    means = [S.sbuf("lnmean", [128, 512], F32) for _ in range(nt)]
    msqs = [S.sbuf("lnmsq", [128, 512], F32) for _ in range(2)]
    rstds = [S.sbuf("lnrstd", [128, 512], F32) for _ in range(nt)]
    tmp = [S.sbuf("lntmp", [128, 512], F32) for _ in range(3)]
    ps_ss = [S.psum("lnps", [128, 512]) for _ in range(2)]
    ps_qs = [S.psum("lnpq", [128, 512]) for _ in range(2)]
    ones = g.ones32
    W_ = width
    def stats(ii, tix):
        ts = slice(tix * W_, (tix + 1) * W_)
        tt = (tix * W_) // 512
        mean, msq, rstd, ps_s, ps_q = means[ii], msqs[ii % 2], rstds[ii], ps_ss[ii % 2], ps_qs[ii % 2]
        for c in range(NCH):
            q = sq[c % 2]
            S.i("act", "activation", [R(x, (c, tt))], [q], out=q[:, 0:W_], in_=x[:, c, ts], func=AF.Square)
            S.i("pe", "matmul", [R(x, (c, tt)), g.cf], [ps_s], out=ps_s[:, 0:W_], lhsT=ones, rhs=x[:, c, ts], start=(c == 0), stop=(c == NCH - 1))
            S.i("pe", "matmul", [q, g.cf], [ps_q], out=ps_q[:, 0:W_], lhsT=ones, rhs=q[:, 0:W_], start=(c == 0), stop=(c == NCH - 1))
        S.i("dve", "tensor_scalar", [ps_s], [mean], out=mean[:, 0:W_], in0=ps_s[:, 0:W_], scalar1=1.0 / D, scalar2=None, op0=ALU.mult)
        S.i("dve", "tensor_tensor", [mean], [msq], out=msq[:, 0:W_], in0=mean[:, 0:W_], in1=mean[:, 0:W_], op=ALU.mult)
        S.i("dve", "scalar_tensor_tensor", [ps_q, msq], [msq], out=msq[:, 0:W_], in0=ps_q[:, 0:W_], scalar=1.0 / D, in1=msq[:, 0:W_], op0=ALU.mult, op1=ALU.subtract)
        S.i("dve", "tensor_scalar", [msq], [msq], out=msq[:, 0:W_], in0=msq[:, 0:W_], scalar1=LN_EPS, scalar2=None, op0=ALU.add)
        S.i("act", "activation", [msq], [rstd], out=rstd[:, 0:W_], in_=msq[:, 0:W_], func=AF.Ln)
        S.i("act", "activation", [rstd], [rstd], out=rstd[:, 0:W_], in_=rstd[:, 0:W_], func=AF.Exp, scale=-0.5)
    def norm(ii, tix):
        ts = slice(tix * W_, (tix + 1) * W_)
        tt = (tix * W_) // 512
        mean, rstd = means[ii], rstds[ii]
        for c in range(NCH):
            tm = tmp[c % 3]
            S.i("dve", "tensor_tensor", [R(x, (c, tt)), mean], [tm], out=tm[:, 0:W_], in0=x[:, c, ts], in1=mean[:, 0:W_], op=ALU.subtract)
            S.i("dve", "tensor_tensor", [tm, rstd], [tm], out=tm[:, 0:W_], in0=tm[:, 0:W_], in1=rstd[:, 0:W_], op=ALU.mult)
            S.i("act", "activation", [tm, g.vecs], [R(x, (c, tt))], out=x[:, c, ts], in_=tm[:, 0:W_], func=AF.Identity,
                bias=vcol(g, bname, c), scale=vcol(g, gname, c))
            if write_xb:
                S.i("dve", "tensor_scalar", [tm, g.vecs], [R(xb, (c, tt))], out=xb[:, c, ts], in0=tm[:, 0:W_], scalar1=vcol(g, gname, c),
                    scalar2=vcol(g, bname, c), op0=ALU.mult, op1=ALU.add)
    stats(0, tts[0])
    for ii in range(nt):
        if ii + 1 < nt:
            stats(ii + 1, tts[ii + 1])
        norm(ii, tts[ii])


def ffn_phase(g, i, write_xb=True):
    S = g.S
    x, xb = g.x, g.xb
    S.push()
    W_in = g.W["ffn_w_in"][i].rearrange("(k p) (two f) -> p k two f", p=128, two=2)
    W_out = g.W["ffn_w_out"][i].rearrange("(k p) n -> p k n", p=128)
    TB = 1024
    a = S.sbuf("ffa", [128, NPAIR, TB], BF16)
    hus = [S.sbuf("hu", [128, TB + 2], F32) for _ in range(2)]
    hgs = [S.sbuf("hg", [128, TB + 2], F32) for _ in range(2)]
    aus = [S.sbuf("au", [128, TB], F32) for _ in range(2)]
    ags = [S.sbuf("ag", [128, TB], F32) for _ in range(2)]
    halo = S.sbuf("halo", [128, 2 * NPAIR, 2], F32)
    wps = [[S.sbuf("wp", [128, NCH, 128], BF16) for _ in range(2)] for _ in range(2)]
    wos = [S.sbuf("wo", [128, NPAIR, 128], BF16) for _ in range(2)]
    pb = [S.psum("ffps", [128, 512]) for _ in range(8)]
    cw = lambda k, j: vcol(g, "cw%d_%d" % (k, i), j)
    cbv = lambda j: vcol(g, "cb_%d" % i, j)
    nw = 0
    pending = []
    for blk in range(2):
        for j in range(NPAIR):
            if j == 0:
                for ug in range(2):
                    S.dma("pool", wps[nw % 2][ug][:], W_in[:, :, ug, 0:128], writes=[wps[nw % 2][ug]])
            wp = wps[nw % 2]; nw += 1
            hu, hg, au, ag = hus[j % 2], hgs[j % 2], aus[j % 2], ags[j % 2]
            if j + 1 < NPAIR:
                for ug in range(2):
                    S.dma("pool", wps[nw % 2][ug][:], W_in[:, :, ug, (j + 1) * 128:(j + 2) * 128], writes=[wps[nw % 2][ug]])
            elif True:
                S.dma("pool", wos[0][:], W_out[:, :, 0:128], writes=[wos[0]])
            banks = pb[(j % 2) * 4:(j % 2) * 4 + 4]
            for ug in range(2):
                for h in range(2):
                    bk = banks[ug * 2 + h]
                    tt = blk * 2 + h
                    for k in range(NCH):
                        S.i("pe", "matmul", [wp[ug], R(xb, (k, tt))], [bk], out=bk[:], lhsT=wp[ug][:, k, :],
                            rhs=xb[:, k, tt * 512:(tt + 1) * 512], start=(k == 0), stop=(k == NCH - 1))
            for ug, hb in ((0, hu), (1, hg)):
                for h in range(2):
                    bk = banks[ug * 2 + h]
                    S.i("act", "activation", [bk], [R(hb, h)], out=hb[:, 2 + h * 512:2 + (h + 1) * 512], in_=bk[:], func=AF.Copy)
                if blk == 0:
                    S.i("dve", "memset", [], [R(hb, "halo")], ap=hb[:, 0:2], constant=0.0)
                else:
                    S.i("dve", "tensor_copy", [R(halo, (ug, j))], [R(hb, "halo")], out=hb[:, 0:2], in_=halo[:, ug * NPAIR + j, :])
            for (hb, ab_, jj) in ((hu, au, j), (hg, ag, NPAIR + j)):
                S.i("act", "activation", [hb, g.vecs], [ab_], out=ab_[:], in_=hb[:, 2:TB + 2], func=AF.Identity, bias=cbv(jj), scale=cw(2, jj))
                if jj == j and pending:
                    pending.pop(0)()
                S.i("dve", "scalar_tensor_tensor", [hb, ab_, g.vecs], [ab_], out=ab_[:], in0=hb[:, 1:TB + 1], scalar=cw(1, jj), in1=ab_[:], op0=ALU.mult, op1=ALU.add)
                S.i("dve", "scalar_tensor_tensor", [hb, ab_, g.vecs], [ab_], out=ab_[:], in0=hb[:, 0:TB], scalar=cw(0, jj), in1=ab_[:], op0=ALU.mult, op1=ALU.add)
            def tail(au=au, ag=ag, j=j):
                S.i("act", "activation", [ag], [ag], out=ag[:], in_=ag[:], func=AF.Silu)
                S.i("dve", "tensor_tensor", [au, ag], [R(a, j)], out=a[:, j, :], in0=au[:], in1=ag[:], op=ALU.mult)
            pending.append(tail)
            if blk == 0:
                for ug, hb in ((0, hu), (1, hg)):
                    S.i("dve", "tensor_copy", [hb], [R(halo, (ug, j))], out=halo[:, ug * NPAIR + j, :], in_=hb[:, TB:TB + 2])
        while pending:
            pending.pop(0)()
        for m in range(NCH):
            wo = wos[m % 2]
            if m + 1 < NCH:
                S.dma("pool", wos[(m + 1) % 2][:], W_out[:, :, (m + 1) * 128:(m + 2) * 128], writes=[wos[(m + 1) % 2]])
            for h in range(2):
                bk = pb[(m * 2 + h) % 8]
                tt = blk * 2 + h
                for k in range(NPAIR):
                    S.i("pe", "matmul", [wo, R(a, k)], [bk], out=bk[:], lhsT=wo[:, k, :], rhs=a[:, k, h * 512:(h + 1) * 512],
                        start=(k == 0), stop=(k == NPAIR - 1))
                xs = x[:, m, tt * 512:(tt + 1) * 512]
                S.i("dve", "scalar_tensor_tensor", [bk, R(x, (m, tt))], [R(x, (m, tt))], out=xs, in0=xs, scalar=ALPHA, in1=bk[:],
                    op0=ALU.mult, op1=ALU.add)
    S.pop()
    S.push()
    layer_norm(g, ("ln2_g%d" % i, "ln2_b%d" % i), range(4), write_xb=write_xb)
    S.pop()

def moba_phase(g, i):
    S = g.S
    x, xb = g.x, g.xb
    S.push()
    Wqkv = g.W["moba_w_qkv"][0].rearrange("(k p) n -> p k n", p=128)
    Wo = g.W["moba_w_o"][0].rearrange("(k p) n -> p k n", p=128)
    ogT = S.sbuf("ogT", [128, NCH, T_SEQ], BF16)
    wsl = [[S.sbuf("mw", [128, NCH, 128], BF16) for _ in range(3)] for _ in range(2)]
    qkv = [[S.sbuf("mqkv", [128, T_SEQ], BF16) for _ in range(3)] for _ in range(2)]
    vtms = [S.sbuf("vtm", [128, 16, 128], BF16) for _ in range(2)]
    ksum = S.sbuf("ksum", [128, 8], F32)
    kmean = S.sbuf("kmean", [128, 8], BF16)
    P = S.sbuf("mP", [128, T_SEQ], BF16)
    PT = S.sbuf("mPT", [128, 16, 128], BF16)
    sd = S.sbuf("msd", [128, 128], F32)
    g8 = S.sbuf("g8", [128, 8], F32)
    top8 = S.sbuf("top8", [128, 8], F32)
    mb = S.sbuf("mb", [128, 8], F32)
    b8 = S.sbuf("b8", [128, 8], F32)
    nm = S.sbuf("nm", [128, 4], F32)
    rs = S.sbuf("rs", [128, 12], F32)
    rinv = S.sbuf("rinv", [128, 2], F32)
    otm = S.sbuf("otm", [128, 128], BF16)
    wos = [S.sbuf("mwo", [128, NCH, 128], BF16) for _ in range(2)]
    sc = S.psum("msc", [128, 2048])
    pT = S.psum("mpT", [128, 1024], BF16)
    gps = S.psum("mgps", [128, 512])
    ov = S.psum("mov", [128, 512])
    pj = S.psum("mpj", [128, 512])
    tri = g.cf[:, 128:256]
    if _DBG.get('dbg_memset'):
        S.i('pool', 'memset', [], [ogT], ap=ogT[:], constant=0.0)
    ident = g.identb
    for c in range(_DBG.get('moba_pairs', NCH)):
        ws = wsl[c % 2]
        qT, kT, vT = qkv[c % 2]
        vtm = vtms[c % 2]
        for w in range(3):
            S.dma("pool", ws[w][:], Wqkv[:, :, w * 1024 + c * 128: w * 1024 + (c + 1) * 128], writes=[ws[w]])
        for w, dst in ((0, qT), (1, kT), (2, vT)):
            for tt in range(4):
                for k in range(NCH):
                    S.i("pe", "matmul", [ws[w], R(xb, (k, tt))], [pj], out=pj[:], lhsT=ws[w][:, k, :], rhs=xb[:, k, tt * 512:(tt + 1) * 512],
                        start=(k == 0), stop=(k == NCH - 1))
                if w == 0:
                    S.i("act", "activation", [pj], [R(dst, tt)], out=dst[:, tt * 512:(tt + 1) * 512], in_=pj[:], func=AF.Copy, scale=0.125)
                else:
                    S.i("act", "activation", [pj], [R(dst, tt)], out=dst[:, tt * 512:(tt + 1) * 512], in_=pj[:], func=AF.Copy)
                if w == 1 and _DBG.get('moba_stage', 9) >= 0.2:
                    S.i("dve", "tensor_reduce", [pj], [R(ksum, tt)], out=ksum[:, 2 * tt:2 * tt + 2],
                        in_=pj[:].rearrange("p (b k) -> p b k", b=2), axis=AX.X, op=ALU.add)
        if _DBG.get('moba_stage', 9) >= 0.2:
            S.i("dve", "tensor_scalar", [ksum], [kmean], out=kmean[:], in0=ksum[:], scalar1=1.0 / 256.0, scalar2=None, op0=ALU.mult)
        for b in range(2 if _DBG.get('moba_stage', 9) >= 0.3 else 0):
            for t8 in range(8):
                kt = b * 8 + t8
                S.i("pe", "transpose", [vT, g.cb], [pT], out=pT[:, t8 * 128:(t8 + 1) * 128], in_=vT[:, kt * 128:(kt + 1) * 128], identity=ident)
            S.i("dve", "tensor_copy", [pT], [R(vtm, b)], out=vtm[:, b * 8:(b + 1) * 8, :].rearrange("p a b -> p (a b)"), in_=pT[:])
        stg = _DBG.get('moba_stage', 9)
        for qt in _DBG.get('moba_qts', range(16)):
            if stg < 2:
                break
            qb = qt // 2
            nkt = qt + 1
            qs = slice(qt * 128, (qt + 1) * 128)
            for hh in range(2):
                ps = slice(hh * 64, (hh + 1) * 64)
                use_thr = qb >= 4
                if use_thr:
                    S.i("pe", "matmul", [qT, kmean], [gps], out=gps[:, 0:8], lhsT=qT[ps, qs], rhs=kmean[ps, 0:8], start=True, stop=True)
                    S.i("pool", "memset", [], [R(g8, "pad")], ap=g8[:, qb:8], constant=-1.0e30)
                    S.i("dve", "tensor_copy", [gps], [R(g8, "val")], out=g8[:, 0:qb], in_=gps[:, 0:qb])
                    S.i("dve", "max", [g8], [top8], out=top8[:], in_=g8[:])
                    S.i("dve", "tensor_scalar", [g8, top8], [mb], out=mb[:], in0=g8[:], scalar1=top8[:, 2:3], scalar2=30000.0,
                        op0=ALU.is_ge, op1=ALU.mult)
                ncol = nkt * 128
                for b in range((ncol + 511) // 512):
                    w_ = min(512, ncol - b * 512)
                    S.i("pe", "matmul", [qT, kT], [sc], out=sc[:, b * 512:b * 512 + w_], lhsT=qT[ps, qs], rhs=kT[ps, b * 512:b * 512 + w_],
                        start=True, stop=True)
                S.i("dve", "tensor_tensor", [sc, g.cf], [sc], out=sc[:, qt * 128:(qt + 1) * 128], in0=sc[:, qt * 128:(qt + 1) * 128], in1=tri, op=ALU.add)
                S.i("dve", "tensor_reduce", [sc], [R(nm, 0)], out=nm[:, 0:1], in_=sc[:, 0:ncol], axis=AX.X, op=ALU.max, negate=True)
                negm = nm[:, 0:1]
                if stg < 3:
                    continue
                if use_thr:
                    S.i("dve", "tensor_scalar", [mb, nm], [b8], out=b8[:], in0=mb[:], scalar1=negm, scalar2=-30000.0, op0=ALU.add, op1=ALU.add)
                npz = 0
                if use_thr:
                    for n in range(qb):
                        S.i("act", "activation", [sc, b8], [R(P, n), R(rs, npz)], out=P[:, n * 256:(n + 1) * 256], in_=sc[:, n * 256:(n + 1) * 256],
                            func=AF.Exp, bias=b8[:, n:n + 1], scale=1.0, accum_out=rs[:, npz:npz + 1])
                        npz += 1
                    S.i("act", "activation", [sc, nm], [R(P, "o"), R(rs, npz)], out=P[:, qb * 256:ncol], in_=sc[:, qb * 256:ncol],
                        func=AF.Exp, bias=negm, scale=1.0, accum_out=rs[:, npz:npz + 1])
                    npz += 1
                else:
                    S.i("act", "activation", [sc, nm], [R(P, "past"), R(rs, npz)], out=P[:, 0:ncol], in_=sc[:, 0:ncol],
                        func=AF.Exp, bias=negm, scale=1.0, accum_out=rs[:, npz:npz + 1])
                    npz += 1
                S.i("dve", "tensor_reduce", [rs], [R(rs, 11)], out=rs[:, 11:12], in_=rs[:, 0:npz], axis=AX.X, op=ALU.add)
                S.i("dve", "reciprocal", [R(rs, 11)], [R(rinv, hh)], out=rinv[:, hh:hh + 1], in_=rs[:, 11:12])
                if stg < 4:
                    continue
                for b in range((nkt + 7) // 8):
                    n8 = min(8, nkt - b * 8)
                    for t8 in range(n8):
                        kt = b * 8 + t8
                        S.i("pe", "transpose", [P, g.cb], [pT], out=pT[:, t8 * 128:(t8 + 1) * 128], in_=P[:, kt * 128:(kt + 1) * 128], identity=ident)
                    S.i("dve", "tensor_copy", [pT], [R(PT, b)], out=PT[:, b * 8:b * 8 + n8, :].rearrange("p a b -> p (a b)"), in_=pT[:, 0:n8 * 128])
                for kt in range(nkt):
                    S.i("pe", "matmul", [PT, vtm], [R(ov, hh)], out=ov[:, hh * 64:(hh + 1) * 64], lhsT=PT[:, kt, :], rhs=vtm[:, kt, hh * 64:(hh + 1) * 64],
                        start=(kt == 0), stop=(kt == nkt - 1))
                S.i("dve", "tensor_scalar", [R(ov, hh), R(rinv, hh)], [R(otm, hh)], out=otm[:, hh * 64:(hh + 1) * 64], in0=ov[:, hh * 64:(hh + 1) * 64],
                    scalar1=rinv[:, hh:hh + 1], scalar2=None, op0=ALU.mult)
            if stg < 5:
                continue
            S.i("pe", "transpose", [otm, g.cb], [pT], out=pT[:, 0:128], in_=otm[:], identity=ident)
            S.i("act", "activation", [pT], [R(ogT, (c, qt))], out=ogT[:, c, qs], in_=pT[:, 0:128], func=AF.Copy)
    for m in range(NCH):
        wo = wos[m % 2]
        S.dma("pool", wo[:], Wo[:, :, m * 128:(m + 1) * 128], writes=[wo])
        for tt in range(4):
            for k in range(NCH):
                S.i("pe", "matmul", [wo, ogT], [pj], out=pj[:], lhsT=wo[:, k, :], rhs=ogT[:, k, tt * 512:(tt + 1) * 512], start=(k == 0), stop=(k == NCH - 1))
            xs = x[:, m, tt * 512:(tt + 1) * 512]
            S.i("dve", "scalar_tensor_tensor", [pj, R(x, (m, tt))], [R(x, (m, tt))], out=xs, in0=xs, scalar=ALPHA, in1=pj[:], op0=ALU.mult, op1=ALU.add)
    S.pop()
    S.push()
    layer_norm(g, ("ln1_g%d" % i, "ln1_b%d" % i), range(4))
    S.pop()

def ret_phase(g, i):
    S = g.S
    x, xb = g.x, g.xb
    S.push()
    Win = g.W["ret_w_in"][0].rearrange("(k p) n -> p k n", p=128)
    Wsw = g.ret_sw_d.rearrange("(k p) n -> p k n", p=128)
    Wo = g.W["ret_w_o"][0].rearrange("(k p) n -> p k n", p=128)
    wq = S.sbuf("rwq", [128, NCH, 256], BF16); wqs = S.sbuf("rwqs", [128, NCH, 256], BF16)
    wk = S.sbuf("rwk", [128, NCH, 256], BF16); wks = S.sbuf("rwks", [128, NCH, 256], BF16)
    wv = S.sbuf("rwv", [128, NCH, 512], BF16); wg = S.sbuf("rwg", [128, NCH, 512], BF16)
    wo = S.sbuf("rwo", [128, 4, 1024], BF16)
    rm = S.sbuf("rrm", [128, 384], F32)
    tabs = [S.sbuf("rtab", [128, 2, 2, 512], F32) for _ in range(2)]
    qrot = S.sbuf("qrot", [128, 2, 512], BF16); krot = S.sbuf("krot", [128, 2, 512], BF16)
    vT = S.sbuf("rvT", [128, 4, 512], BF16); sgT = S.sbuf("rsgT", [128, 4, 512], BF16)
    vtm = S.sbuf("rvtm", [128, 4, 512], BF16); ktm = S.sbuf("rktm", [128, 4, 256], BF16)
    og = S.sbuf("rog", [128, 4, 512], BF16)
    S32 = S.sbuf("rS32", [128, 2, 512], F32); Sb = S.sbuf("rSb", [128, 2, 512], BF16)
    t1 = S.sbuf("rt1", [128, 512], F32); t2 = S.sbuf("rt2", [128, 512], F32)
    ST = S.sbuf("rST", [128, 128], BF16); qcd = S.sbuf("rqcd", [128, 2, 128], BF16)
    osbs = [S.sbuf("rosb", [128, 512], F32) for _ in range(2)]; osqs = [S.sbuf("rosq", [128, 512], F32) for _ in range(2)]
    means = [S.sbuf("rmean", [128, 128], F32) for _ in range(2)]; msqs = [S.sbuf("rmsq", [128, 128], F32) for _ in range(2)]; rstds = [S.sbuf("rrstd", [128, 128], F32) for _ in range(2)]
    nchunk = 0
    pA = S.psum("rpA", [128, 512]); pB = S.psum("rpB", [128, 512])
    opss = [S.psum("rops", [128, 512]) for _ in range(2)]
    ups = [S.psum("rups", [128, 512]) for _ in range(2)]
    pT = S.psum("rpT", [128, 1024], BF16)
    pst = S.psum("rpst", [128, 512])
    sps = pst
    ident = g.identb
    ones = g.ones32
    ntab = 0
    for h in range(4):
        S.dma("pool", wq[:], Win[:, :, h * 256:(h + 1) * 256], writes=[wq])
        S.dma("pool", wqs[:], Wsw[:, :, h * 256:(h + 1) * 256], writes=[wqs])
        S.dma("pool", wk[:], Win[:, :, 1024 + h * 256:1024 + (h + 1) * 256], writes=[wk])
        S.dma("pool", wks[:], Wsw[:, :, 1024 + h * 256:1024 + (h + 1) * 256], writes=[wks])
        S.dma("pool", wv[:], Win[:, :, 2048 + h * 512:2048 + (h + 1) * 512], writes=[wv])
        S.dma("pool", wg[:], Win[:, :, 4096 + h * 512:4096 + (h + 1) * 512], writes=[wg])
        S.dma("pool", wo[:], Wo[:, h * 4:(h + 1) * 4, :], writes=[wo])
        S.dma("sp", rm[:], g.rmask_d[h], writes=[rm])
        S.i("pool", "memset", [], [S32], ap=S32[:], constant=0.0)
        S.i("pool", "memset", [], [Sb], ap=Sb[:], constant=0.0)
        for tt in range(4):
            ts = slice(tt * 512, (tt + 1) * 512)
            tab = tabs[ntab % 2]; ntab += 1
            for cs_ in range(2):
                S.dma("sp", tab[:, cs_, :, :], g.rtab_d[cs_][:, :, ts], writes=[R(tab, None)])
            for (wa, wb, dst) in ((wq, wqs, qrot), (wk, wks, krot)):
                for dc in range(2):
                    for k in range(NCH):
                        S.i("pe", "matmul", [wa, R(xb, (k, tt))], [pA], out=pA[:], lhsT=wa[:, k, dc * 128:(dc + 1) * 128], rhs=xb[:, k, ts],
                            start=(k == 0), stop=(k == NCH - 1))
                    for k in range(NCH):
                        S.i("pe", "matmul", [wb, R(xb, (k, tt))], [pB], out=pB[:], lhsT=wb[:, k, dc * 128:(dc + 1) * 128], rhs=xb[:, k, ts],
                            start=(k == 0), stop=(k == NCH - 1))
                    S.i("dve", "tensor_tensor", [pA, tab], [t1], out=t1[:], in0=pA[:], in1=tab[:, 0, dc, :], op=ALU.mult)
                    S.i("dve", "tensor_tensor", [pB, tab], [t2], out=t2[:], in0=pB[:], in1=tab[:, 1, dc, :], op=ALU.mult)
                    S.i("pool", "tensor_tensor", [t1, t2], [R(dst, dc)], out=dst[:, dc, :], in0=t1[:], in1=t2[:], op=ALU.add)
            for ec in range(4):
                for k in range(NCH):
                    S.i("pe", "matmul", [wv, R(xb, (k, tt))], [pA], out=pA[:], lhsT=wv[:, k, ec * 128:(ec + 1) * 128], rhs=xb[:, k, ts],
                        start=(k == 0), stop=(k == NCH - 1))
                S.i("act", "activation", [pA], [R(vT, ec)], out=vT[:, ec, :], in_=pA[:], func=AF.Copy)
                for k in range(NCH):
                    S.i("pe", "matmul", [wg, R(xb, (k, tt))], [pB], out=pB[:], lhsT=wg[:, k, ec * 128:(ec + 1) * 128], rhs=xb[:, k, ts],
                        start=(k == 0), stop=(k == NCH - 1))
                S.i("act", "activation", [pB], [R(sgT, ec)], out=sgT[:, ec, :], in_=pB[:], func=AF.Silu)
            for n in range(4):
                for ec in range(4):
                    idx = (n % 2) * 4 + ec
                    S.i("pe", "transpose", [vT, g.cb], [pT], out=pT[:, idx * 128:(idx + 1) * 128], in_=vT[:, ec, n * 128:(n + 1) * 128], identity=ident)
                if n % 2 == 1:
                    S.i("dve", "tensor_copy", [pT], [R(vtm, n // 2)], out=vtm[:, n - 1:n + 1, :].rearrange("p a b -> p (a b)"), in_=pT[:])
            for n in range(4):
                for dc in range(2):
                    idx = n * 2 + dc
                    S.i("pe", "transpose", [krot, g.cb], [pT], out=pT[:, idx * 128:(idx + 1) * 128], in_=krot[:, dc, n * 128:(n + 1) * 128], identity=ident)
            S.i("dve", "tensor_scalar", [pT, rm], [ktm], out=ktm[:].rearrange("p a b -> p (a b)"), in0=pT[:], scalar1=rm[:, 256:257], scalar2=None, op0=ALU.mult)
            def partA(n):
                cs = slice(n * 128, (n + 1) * 128)
                opsn = opss[n % 2]
                for dc in range(2):
                    S.i("pe", "matmul", [krot, qrot], [sps], out=sps[:, 256:384], lhsT=krot[:, dc, cs], rhs=qrot[:, dc, cs], start=(dc == 0), stop=(dc == 1))
                S.i("dve", "tensor_tensor", [sps, rm], [ST], out=ST[:], in0=sps[:, 256:384], in1=rm[:, 0:128], op=ALU.mult)
                S.i("pool", "tensor_tensor", [qrot, rm], [qcd], out=qcd[:], in0=qrot[:, :, cs], in1=rm[:, 128:256].unsqueeze(1).to_broadcast([128, 2, 128]), op=ALU.mult)
                for ec in range(4):
                    es = slice(ec * 128, (ec + 1) * 128)
                    S.i("pe", "matmul", [vtm, ST], [opsn], out=opsn[:, es], lhsT=vtm[:, n, es], rhs=ST[:], start=True, stop=False)
                    for dc in range(2):
                        S.i("pe", "matmul", [Sb, qcd], [opsn], out=opsn[:, es], lhsT=Sb[:, dc, es], rhs=qcd[:, dc, :], start=False, stop=(dc == 1))
                for dc in range(2):
                    S.i("pe", "matmul", [ktm, vtm], [ups[dc]], out=ups[dc][:], lhsT=ktm[:, n, dc * 128:(dc + 1) * 128], rhs=vtm[:, n, :], start=True, stop=True)
                    S.i("dve", "scalar_tensor_tensor", [S32, ups[dc], rm], [R(S32, dc)], out=S32[:, dc, :], in0=S32[:, dc, :], scalar=rm[:, 257:258], in1=ups[dc][:],
                        op0=ALU.mult, op1=ALU.add)
                    S.i("act", "activation", [R(S32, dc)], [R(Sb, dc)], out=Sb[:, dc, :], in_=S32[:, dc, :], func=AF.Copy)

            def partB(n):
                cs = slice(n * 128, (n + 1) * 128)
                opsn = opss[n % 2]
                osb, osq, mean, msq, rstd = osbs[n % 2], osqs[n % 2], means[n % 2], msqs[n % 2], rstds[n % 2]
                S.i("act", "activation", [opsn], [osb], out=osb[:], in_=opsn[:], func=AF.Copy)
                S.i("act", "activation", [opsn], [osq], out=osq[:], in_=opsn[:], func=AF.Square)
                for ec in range(4):
                    S.i("pe", "matmul", [osb, g.cf], [R(pst, 0)], out=pst[:, 0:128], lhsT=ones, rhs=osb[:, ec * 128:(ec + 1) * 128], start=(ec == 0), stop=(ec == 3))
                for ec in range(4):
                    S.i("pe", "matmul", [osq, g.cf], [R(pst, 1)], out=pst[:, 128:256], lhsT=ones, rhs=osq[:, ec * 128:(ec + 1) * 128], start=(ec == 0), stop=(ec == 3))
                S.i("dve", "tensor_scalar", [R(pst, 0)], [mean], out=mean[:], in0=pst[:, 0:128], scalar1=1.0 / 512, scalar2=None, op0=ALU.mult)
                S.i("dve", "tensor_tensor", [mean], [msq], out=msq[:], in0=mean[:], in1=mean[:], op=ALU.mult)
                S.i("dve", "scalar_tensor_tensor", [R(pst, 1), msq], [msq], out=msq[:], in0=pst[:, 128:256], scalar=1.0 / 512, in1=msq[:], op0=ALU.mult, op1=ALU.subtract)
                S.i("dve", "tensor_scalar", [msq], [msq], out=msq[:], in0=msq[:], scalar1=1e-5, scalar2=None, op0=ALU.add)
                S.i("act", "activation", [msq], [rstd], out=rstd[:], in_=msq[:], func=AF.Ln)
                S.i("act", "activation", [rstd], [rstd], out=rstd[:], in_=rstd[:], func=AF.Exp, scale=-0.5)
                mb_ = mean[:].unsqueeze(1).to_broadcast([128, 4, 128])
                rb_ = rstd[:].unsqueeze(1).to_broadcast([128, 4, 128])
                o3 = osb[:].rearrange("p (a b) -> p a b", b=128)
                S.i("dve", "tensor_tensor", [osb, mean], [osb], out=o3, in0=o3, in1=mb_, op=ALU.subtract)
                S.i("dve", "tensor_tensor", [osb, rstd], [osb], out=o3, in0=o3, in1=rb_, op=ALU.mult)
                for ec in range(4):
                    col = h * 4 + ec
                    S.i("act", "activation", [osb, g.vecs], [R(osq, ec)], out=osq[:, ec * 128:(ec + 1) * 128], in_=osb[:, ec * 128:(ec + 1) * 128], func=AF.Identity,
                        bias=vcol(g, "ret_gn_b", col), scale=vcol(g, "ret_gn_g", col))
                S.i("dve", "tensor_tensor", [osq, sgT], [R(og, n)], out=og[:, :, cs], in0=osq[:].rearrange("p (a b) -> p a b", b=128), in1=sgT[:, :, cs], op=ALU.mult)

            partA(0)
            for n in range(4):
                if n + 1 < 4:
                    partA(n + 1)
                partB(n)
            for m in range(NCH):
                pw = pA if m % 2 == 0 else pB
                for k in range(4):
                    S.i("pe", "matmul", [wo, og], [pw], out=pw[:], lhsT=wo[:, k, m * 128:(m + 1) * 128], rhs=og[:, k, :], start=(k == 0), stop=(k == 3))
                xs = x[:, m, ts]
                if h == 0:
                    S.i("dve", "scalar_tensor_tensor", [pw, R(x, (m, tt))], [R(x, (m, tt))], out=xs, in0=xs, scalar=ALPHA, in1=pw[:], op0=ALU.mult, op1=ALU.add)
                else:
                    S.i("dve", "tensor_tensor", [pw, R(x, (m, tt))], [R(x, (m, tt))], out=xs, in0=xs, in1=pw[:], op=ALU.add)
    S.pop()
    S.push()
    layer_norm(g, ("ln1_g%d" % i, "ln1_b%d" % i), range(4))
    S.pop()

def rwkv_phase(g, i, j):
    S = g.S
    x, xb = g.x, g.xb
    S.push()
    TW = 256
    NT_ = T_SEQ // TW
    W = g.W
    Wrkv = [W["rwkv_w_rkv"][j, w].rearrange("(k p) n -> p k n", p=128) for w in range(3)]
    Wo = W["rwkv_w_o"][j].rearrange("(k p) n -> p k n", p=128)
    V = lambda name, c=0: vcol(g, "%s_%d" % (name, j), c)
    w1 = S.sbuf("w1", [128, NCH, 64], BF16); a1 = S.sbuf("a1", [128, NCH, 64], BF16); g1 = S.sbuf("g1", [128, NCH, 160], BF16)
    w2 = S.sbuf("w2", [64, 1024], BF16); a2 = S.sbuf("a2", [64, 1024], BF16); g2 = S.sbuf("g2", [128, 2, 1024], BF16)
    S.dma("pool", w1[:], W["rwkv_w1"][j].rearrange("(k p) n -> p k n", p=128), writes=[w1])
    S.dma("pool", a1[:], W["rwkv_a1"][j].rearrange("(k p) n -> p k n", p=128), writes=[a1])
    S.dma("pool", g1[:], W["rwkv_g1"][j].rearrange("(k p) n -> p k n", p=128), writes=[g1])
    S.dma("pool", w2[:], W["rwkv_w2"][j], writes=[w2])
    S.dma("pool", a2[:], W["rwkv_a2"][j], writes=[a2])
    S.dma("pool", g2[:, 0, :], W["rwkv_g2"][j][0:128, :], writes=[R(g2, None)])
    S.dma("pool", g2[0:32, 1, :], W["rwkv_g2"][j][128:160, :], writes=[R(g2, None)])
    rows = S.sbuf("gnrows", [128, 2, 1024], BF16)
    S.dma("pool", rows[:, 0, :], g.rows_d[2 * j], writes=[R(rows, None)])
    S.dma("pool", rows[:, 1, :], g.rows_d[2 * j + 1], writes=[R(rows, None)])
    om = S.sbuf("om", [128, 72], F32)
    mo = g.voff["mix0_%d" % j]
    S.i("dve", "tensor_scalar", [g.vecs], [om], out=om[:, 0:48], in0=g.vecs[:, mo:mo + 48], scalar1=-1.0, scalar2=1.0, op0=ALU.mult, op1=ALU.add)
    ko = g.voff["k_a_%d" % j]
    S.i("dve", "tensor_scalar", [g.vecs], [om], out=om[:, 48:56], in0=g.vecs[:, ko:ko + 8], scalar1=-1.0, scalar2=1.0, op0=ALU.mult, op1=ALU.add)
    ao = g.voff["a0_%d" % j]; wo_ = g.voff["w0_%d" % j]
    S.i("dve", "tensor_scalar", [g.vecs], [om], out=om[:, 56:64], in0=g.vecs[:, ao:ao + 8], scalar1=0.5, scalar2=None, op0=ALU.mult)
    S.i("dve", "tensor_scalar", [g.vecs], [om], out=om[:, 64:72], in0=g.vecs[:, wo_:wo_ + 8], scalar1=0.5, scalar2=None, op0=ALU.mult)
    xx = S.sbuf("xx", [128, NCH, TW], F32)
    xlast = S.sbuf("xlast", [128, NCH, 2], F32)
    S.i("pool", "memset", [], [xlast], ap=xlast[:], constant=0.0)
    hw = S.sbuf("hw", [64, TW], BF16); ha = S.sbuf("ha", [64, TW], BF16); sg = S.sbuf("sg", [128, 2, TW], BF16); sgf = S.sbuf("sgf", [128, 2, TW], BF16)
    NSL = _DBG.get('rw_nsl', 4)
    wsl = [S.sbuf("rwsl", [128, NCH, 128], BF16) for _ in range(NSL)]
    wseq = []
    for tix_ in range(NT_):
        for c_ in range(NCH):
            wseq.append(Wrkv[0][:, :, c_ * 128:(c_ + 1) * 128]); wseq.append(Wrkv[1][:, :, c_ * 128:(c_ + 1) * 128])
        for c_ in range(NCH):
            wseq.append(Wrkv[2][:, :, c_ * 128:(c_ + 1) * 128])
        for m_ in range(NCH):
            wseq.append(Wo[:, :, m_ * 128:(m_ + 1) * 128])
    wst = {"issued": 0, "next": 0}
    def wget():
        i_ = wst["next"]; wst["next"] += 1
        while wst["issued"] < min(len(wseq), i_ + NSL - 1):
            q_ = wst["issued"]
            S.dma("pool", wsl[q_ % NSL][:], wseq[q_], writes=[wsl[q_ % NSL]])
            wst["issued"] += 1
        return wsl[i_ % NSL]
    TS = [dict(), dict()]
    for nm in ("tA", "tsw", "tkk", "tcs", "te1", "te2", "te3"):
        for q_ in range(2):
            TS[q_][nm] = S.sbuf(nm, [128, TW], F32)
    for nm in ("trn", "tt"):
        t_ = S.sbuf(nm, [128, TW], F32)
        TS[0][nm] = t_; TS[1][nm] = t_
    for q_ in range(2):
        TS[q_]["tkk2"] = S.sbuf("tkk2", [128, TW], BF16)
    PC = S.sbuf("PC", [128, NCH, 2], F32)
    def FM(a, c, sl):
        return xb[:, c, a * TW + sl.start:a * TW + sl.stop]
    FMK = lambda a, c: R(xb, ("fm", a, c))
    Lk = [[S.sbuf("Lk", [128, 4, 128], BF16) for _ in range(2)] for _ in range(2)]
    Mk = [[S.sbuf("Mk", [128, 4, 128], BF16) for _ in range(2)] for _ in range(2)]
    NTt = [[S.sbuf("NT", [128, 4, 128], BF16) for _ in range(2)] for _ in range(2)]
    Mak = [S.sbuf("Mak", [128, 4, 128], BF16) for _ in range(2)]
    Mrb = S.sbuf("Mrb", [128, 16, 128], BF16); Mrk = S.sbuf("Mrk", [128, 16, 128], BF16)
    Atm = S.sbuf("Atm", [128, 1024], BF16); Btm = S.sbuf("Btm", [128, 1024], BF16); Ktm = S.sbuf("Ktm", [128, 1024], BF16); Vtm = S.sbuf("Vtm", [128, 1024], BF16)
    AhT = S.sbuf("AhT", [128, NCH, 128], BF16); Xb = S.sbuf("Xb", [128, 256], BF16); Vhat = S.sbuf("Vhat", [128, 1024], BF16)
    Ub = S.sbuf("Ub", [128, 1024], BF16); ST = S.sbuf("ST", [128, NCH, 64], BF16); STs = S.sbuf("STs", [128, NCH, 64], F32)
    yn = S.sbuf("yn", [128, 1024], F32); st4 = S.sbuf("st4", [128, 4, 16], F32); bsum = S.sbuf("bsum", [128, 16], F32)
    ogtm = S.sbuf("ogtm", [128, 1024], BF16); ogT = S.sbuf("ogT", [128, NCH, TW], BF16)
    PB = [S.psum("rp", [128, 512]) for _ in range(7)]
    pT = S.psum("rpT", [128, 1024], BF16)
    ident = g.identb
    blk1 = g.cb[:, 128:256]
    hind = g.cb[:, 256:258]
    ones = g.ones32
    S.i("pool", "memset", [], [ST], ap=ST[:], constant=0.0)
    nws = 0
    stg = _DBG.get('rw_stage', 99)
    for tix in range(_DBG.get('rw_tiles', NT_)):
        t0 = tix * TW
        tt = t0 // 512
        S.i("dve", "tensor_tensor", [R(x, None)], [R(xx, "m")], out=xx[:, :, 1:TW], in0=x[:, :, t0:t0 + TW - 1], in1=x[:, :, t0 + 1:t0 + TW], op=ALU.subtract)
        S.i("dve", "tensor_tensor", [R(x, None), xlast], [R(xx, "0")], out=xx[:, :, 0:1], in0=xlast[:, :, tix % 2:tix % 2 + 1], in1=x[:, :, t0:t0 + 1], op=ALU.subtract)
        S.i("dve", "tensor_copy", [R(x, None)], [xlast], out=xlast[:, :, (tix + 1) % 2:(tix + 1) % 2 + 1], in_=x[:, :, t0 + TW - 1:t0 + TW])
        def mix(m, dst):
            for c in range(NCH):
                mc = vcol(g, "mix%d_%d" % (m, j), c)
                S.i("dve", "scalar_tensor_tensor", [R(x, (c, tt)), xx, g.vecs], [dst.key(c)], out=dst.ap(c), in0=xx[:, c, :], scalar=mc, in1=x[:, c, t0:t0 + TW],
                    op0=ALU.mult, op1=ALU.add)
        nx = [0]
        class XM:
            def __init__(self, slot):
                self.slot = slot
            def ap(self, c):
                return xb[:, c, 1536 + self.slot * TW:1536 + (self.slot + 1) * TW]
            def key(self, c):
                return R(xb, ("xm", self.slot, c))
        def nxm():
            nx[0] += 1
            return XM(nx[0] % 2)
        d_ = nxm(); mix(1, d_)
        for k in range(NCH):
            S.i("pe", "matmul", [w1, d_.key(k)], [PB[0]], out=PB[0][0:64, 0:TW], lhsT=w1[:, k, :], rhs=d_.ap(k), start=(k == 0), stop=(k == NCH - 1))
        S.i("act", "activation", [PB[0]], [hw], out=hw[:], in_=PB[0][0:64, 0:TW], func=AF.Tanh)
        d_ = nxm(); mix(4, d_)
        for k in range(NCH):
            S.i("pe", "matmul", [a1, d_.key(k)], [PB[1]], out=PB[1][0:64, 0:TW], lhsT=a1[:, k, :], rhs=d_.ap(k), start=(k == 0), stop=(k == NCH - 1))
        S.i("act", "activation", [PB[1]], [ha], out=ha[:], in_=PB[1][0:64, 0:TW], func=AF.Copy)
        d_ = nxm(); mix(5, d_)
        for (lo, hi, kc) in ((0, 128, 0), (128, 160, 1)):
            for k in range(NCH):
                S.i("pe", "matmul", [g1, d_.key(k)], [PB[2]], out=PB[2][0:hi - lo, 0:TW], lhsT=g1[:, k, lo:hi], rhs=d_.ap(k), start=(k == 0), stop=(k == NCH - 1))
            S.i("act", "activation", [PB[2]], [R(sgf, kc)], out=sgf[0:hi - lo, kc, :], in_=PB[2][0:hi - lo, 0:TW], func=AF.Tanh, scale=0.5)
            S.i("dve", "tensor_scalar", [R(sgf, kc)], [R(sg, kc)], out=sg[0:hi - lo, kc, :], in0=sgf[0:hi - lo, kc, :], scalar1=0.5, scalar2=0.5, op0=ALU.mult, op1=ALU.add)
        xr = nxm(); mix(0, xr)
        xk = nxm(); mix(2, xk)
        if stg < 3:
            continue
        for c in range(NCH):
            wr = wget()
            wk_ = wget()
            d = TS[c % 2]
            tA, tsw, tcs, te1, te2, te3, tkk, tkk2, trn, tt_ = (d[k_] for k_ in ("tA", "tsw", "tcs", "te1", "te2", "te3", "tkk", "tkk2", "trn", "tt"))
            td3 = tsw; tkkn = tkk; ttb = tkk; tkm = tt_
            if c % 2 == 0:
                rp, kp, zw, za, ssp = PB[0], PB[1], PB[2], PB[3], PB[4]
            else:
                rp, kp, zw, za, ssp = PB[5], PB[6], PB[2], PB[3], PB[4]
            for k in range(NCH):
                S.i("pe", "matmul", [wr, xr.key(k)], [rp], out=rp[:, 0:TW], lhsT=wr[:, k, :], rhs=xr.ap(k), start=(k == 0), stop=(k == NCH - 1))
            for k in range(NCH):
                S.i("pe", "matmul", [wk_, xk.key(k)], [kp], out=kp[:, 0:TW], lhsT=wk_[:, k, :], rhs=xk.ap(k), start=(k == 0), stop=(k == NCH - 1))
            S.i("pe", "matmul", [w2, hw], [zw], out=zw[:, 0:TW], lhsT=w2[:, c * 128:(c + 1) * 128], rhs=hw[:], start=True, stop=True)
            S.i("pe", "matmul", [a2, ha], [za], out=za[:, 0:TW], lhsT=a2[:, c * 128:(c + 1) * 128], rhs=ha[:], start=True, stop=True)
            S.i("act", "activation", [za, om], [tA], out=tA[:], in_=za[:, 0:TW], func=AF.Tanh, bias=om[:, 56 + c:57 + c], scale=0.5)
            S.i("act", "activation", [zw, om], [tsw], out=tsw[:], in_=zw[:, 0:TW], func=AF.Tanh, bias=om[:, 64 + c:65 + c], scale=0.5)
            S.i("dve", "tensor_scalar", [tA], [tA], out=tA[:], in0=tA[:], scalar1=0.5, scalar2=0.5, op0=ALU.mult, op1=ALU.add)
            S.i("dve", "tensor_scalar", [tsw], [tsw], out=tsw[:], in0=tsw[:], scalar1=0.5, scalar2=0.5, op0=ALU.mult, op1=ALU.add)
            for n in range(2):
                cs = slice(n * 128, (n + 1) * 128)
                S.i("dve", "tensor_tensor_scan", [tsw, g.cf], [R(tcs, n)], out=tcs[:, cs], data0=ones, data1=tsw[:, cs], initial=0.0, op0=ALU.mult, op1=ALU.add)
            S.i("dve", "tensor_tensor", [tcs, tsw], [td3], out=td3[:], in0=tcs[:], in1=tsw[:], op=ALU.subtract)
            LD = 0.6065306597126334
            S.i("act", "activation", [tcs], [te1], out=te1[:], in_=tcs[:], func=AF.Exp, scale=-LD)
            S.i("act", "activation", [tcs], [te2], out=te2[:], in_=tcs[:], func=AF.Exp, scale=LD)
            S.i("act", "activation", [td3], [te3], out=te3[:], in_=td3[:], func=AF.Exp, scale=-LD)
            S.i("act", "activation", [te1], [R(PC, c)], out=PC[:, c, :], in_=te1[:, 127:TW:128], func=AF.Copy)
            S.i("dve", "tensor_scalar", [kp, g.vecs], [tkk], out=tkk[:], in0=kp[:, 0:TW], scalar1=V("k_k", c), scalar2=None, op0=ALU.mult)
            S.i("act", "activation", [tkk], [tkk2], out=tkk2[:], in_=tkk[:], func=AF.Square)
            S.i("pe", "matmul", [tkk2, g.cb], [ssp], out=ssp[:, 0:TW], lhsT=blk1, rhs=tkk2[:], start=True, stop=True)
            S.i("dve", "tensor_scalar", [ssp], [trn], out=trn[:], in0=ssp[:, 0:TW], scalar1=1e-24, scalar2=None, op0=ALU.max)
            S.i("act", "activation", [trn], [trn], out=trn[:], in_=trn[:], func=AF.Ln)
            S.i("act", "activation", [trn], [trn], out=trn[:], in_=trn[:], func=AF.Exp, scale=-0.5)
            S.i("dve", "tensor_tensor", [tkk, trn], [tkkn], out=tkkn[:], in0=tkk[:], in1=trn[:], op=ALU.mult)
            S.i("dve", "tensor_scalar", [tA, g.vecs, om], [tt_], out=tt_[:], in0=tA[:], scalar1=V("k_a", c), scalar2=om[:, 48 + c:49 + c], op0=ALU.mult, op1=ALU.add)
            S.i("dve", "tensor_tensor", [kp, tt_], [tkm], out=tkm[:], in0=kp[:, 0:TW], in1=tt_[:], op=ALU.mult)
            full = slice(0, TW)
            S.i("dve", "scalar_tensor_tensor", [tkkn, te3], [FMK(0, c)], out=FM(0, c, full), in0=tkkn[:], scalar=-1.0, in1=te3[:], op0=ALU.mult, op1=ALU.mult)
            S.i("dve", "tensor_tensor", [tkkn, tA], [ttb], out=ttb[:], in0=tkkn[:], in1=tA[:], op=ALU.mult)
            S.i("dve", "tensor_tensor", [ttb, te2], [FMK(1, c)], out=FM(1, c, full), in0=ttb[:], in1=te2[:], op=ALU.mult)
            S.i("dve", "tensor_tensor", [tkm, te2], [FMK(2, c)], out=FM(2, c, full), in0=tkm[:], in1=te2[:], op=ALU.mult)
            S.i("dve", "tensor_tensor", [rp, te1], [FMK(3, c)], out=FM(3, c, full), in0=rp[:, 0:TW], in1=te1[:], op=ALU.mult)
            S.i("dve", "scalar_tensor_tensor", [rp, g.vecs, tkm], [FMK(4, c)], out=FM(4, c, full), in0=rp[:, 0:TW], scalar=V("r_k", c), in1=tkm[:], op0=ALU.mult, op1=ALU.mult)
        xv = nxm(); mix(3, xv)
        for c in range(NCH):
            wv_ = wget()
            vp = PB[5 + (c % 2)]
            for k in range(NCH):
                S.i("pe", "matmul", [wv_, xv.key(k)], [vp], out=vp[:, 0:TW], lhsT=wv_[:, k, :], rhs=xv.ap(k), start=(k == 0), stop=(k == NCH - 1))
            S.i("act", "activation", [vp], [FMK(5, c)], out=FM(5, c, slice(0, TW)), in_=vp[:, 0:TW], func=AF.Copy)
        if stg < 5:
            continue
        for n in range(2):
            cs = slice(n * 128, (n + 1) * 128)
            for a_, dst in ((0, Atm), (1, Btm), (2, Ktm), (5, Vtm)):
                for c in range(NCH):
                    S.i("pe", "transpose", [FMK(a_, c), g.cb], [pT], out=pT[:, c * 128:(c + 1) * 128], in_=FM(a_, c, cs), identity=ident)
                S.i("dve" if a_ in (0, 2) else "act", "tensor_copy" if a_ in (0, 2) else "activation", [pT], [dst],
                    **(dict(out=dst[:], in_=pT[:]) if a_ in (0, 2) else dict(out=dst[:], in_=pT[:], func=AF.Copy)))
            if stg < 6:
                continue
            for hgp in range(2):
                grp = [2 * hgp, 2 * hgp + 1]
                def HP(h):
                    return slice((h % 2) * 64, (h % 2) * 64 + 64), h // 2
                for gi, hg in enumerate(grp):
                    heads = [4 * hg + q for q in range(4)]
                    specs = [(1, 0, 0, Mk[gi][0], None), (0, 1, 2, Lk[gi][0], None), (2, 0, 0, Mak[gi], None), (1, 3, 1, Mrb, hg), (2, 3, 1, Mrk, hg)]
                    for si, (la, ra, mki, dst, full16) in enumerate(specs):
                        dview = dst[:] if full16 is None else dst[:, hg * 4:(hg + 1) * 4, :]
                        wkey = dst if full16 is None else R(dst, hg)
                        for par in range(2):
                            bk = PB[(2 * si + par) % 6]
                            for qq in range(2):
                                h = heads[2 * qq + par]
                                ps_, c = HP(h)
                                S.i("pe", "matmul", [FMK(la, c), FMK(ra, c)], [bk], out=bk[:, qq * 128:(qq + 1) * 128], lhsT=FM(la, c, cs)[ps_, :], rhs=FM(ra, c, cs)[ps_, :],
                                    start=True, stop=True)
                            S.i("dve", "tensor_tensor", [bk, g.mk], [wkey], out=dview[:, par::2, :],
                                in0=bk[:, 0:256].rearrange("p (a b) -> p a b", b=128), in1=g.mk[:, mki, 0:256].rearrange("p (a b) -> p a b", b=128), op=ALU.mult)
                if stg < 7:
                    continue
                cur = [0, 0]
                NTin = [Mk[0][0], Mk[1][0]]
                for r in range(7):
                    for gi in range(2):
                        Lc, Mc = Lk[gi][cur[gi]], Mk[gi][cur[gi]]
                        Ln, Mn = Lk[gi][1 - cur[gi]], Mk[gi][1 - cur[gi]]
                        bN, bL, bM = PB[3 * gi], PB[3 * gi + 1], PB[3 * gi + 2]
                        if r >= 1:
                            NTo = NTt[gi][r % 2]
                            for q in range(4):
                                S.i("pe", "matmul", [NTin[gi], g.cb], [bN], out=bN[:, q * 128:(q + 1) * 128], lhsT=ident, rhs=NTin[gi][:, q, :], start=True, stop=False)
                                S.i("pe", "matmul", [Mc, g.cb], [bN], out=bN[:, q * 128:(q + 1) * 128], lhsT=ident, rhs=Mc[:, q, :], start=False, stop=False)
                                S.i("pe", "matmul", [Lc, NTin[gi]], [bN], out=bN[:, q * 128:(q + 1) * 128], lhsT=Lc[:, q, :], rhs=NTin[gi][:, q, :], start=False, stop=True)
                        if r < 6:
                            for q in range(4):
                                S.i("pe", "matmul", [Mc, Lc], [bL], out=bL[:, q * 128:(q + 1) * 128], lhsT=Mc[:, q, :], rhs=Lc[:, q, :], start=True, stop=True)
                            for q in range(4):
                                S.i("pe", "matmul", [Lc, Mc], [bM], out=bM[:, q * 128:(q + 1) * 128], lhsT=Lc[:, q, :], rhs=Mc[:, q, :], start=True, stop=True)
                        if r >= 1:
                            if gi == 0:
                                S.i("act", "activation", [bN], [NTo], out=NTo[:].rearrange("p a b -> p (a b)"), in_=bN[:], func=AF.Copy)
                            else:
                                S.i("dve", "tensor_copy", [bN], [NTo], out=NTo[:].rearrange("p a b -> p (a b)"), in_=bN[:])
                            NTin[gi] = NTo
                        if r < 6:
                            S.i("act", "activation", [bL], [Ln], out=Ln[:].rearrange("p a b -> p (a b)"), in_=bL[:], func=AF.Copy)
                            S.i("dve", "tensor_copy", [bM], [Mn], out=Mn[:].rearrange("p a b -> p (a b)"), in_=bM[:])
                            cur[gi] = 1 - cur[gi]
                if stg < 8:
                    continue
                for gi, hg in enumerate(grp):
                    heads = [4 * hg + q for q in range(4)]
                    NTf = NTin[gi]
                    for q, h in enumerate(heads):
                        S.i("pe", "matmul", [Mak[gi], Vtm], [PB[6]], out=PB[6][:, q * 64:(q + 1) * 64], lhsT=Mak[gi][:, q, :], rhs=Vtm[:, h * 64:(h + 1) * 64], start=True, stop=True)
                    S.i("act", "activation", [PB[6]], [Xb], out=Xb[:], in_=PB[6][:, 0:256], func=AF.Copy)
                    for q, h in enumerate(heads):
                        S.i("pe", "matmul", [Xb, g.cb], [PB[6]], out=PB[6][:, 256 + q * 64:256 + (q + 1) * 64], lhsT=ident, rhs=Xb[:, q * 64:(q + 1) * 64], start=True, stop=False)
                        S.i("pe", "matmul", [NTf, Xb], [PB[6]], out=PB[6][:, 256 + q * 64:256 + (q + 1) * 64], lhsT=NTf[:, q, :], rhs=Xb[:, q * 64:(q + 1) * 64], start=False, stop=True)
                    S.i("dve", "tensor_copy", [PB[6]], [R(Vhat, hg)], out=Vhat[:, hg * 256:(hg + 1) * 256], in_=PB[6][:, 256:512])
                    bA = PB[gi]
                    for q, h in enumerate(heads):
                        ps_, c = HP(h)
                        cc = (c % 2) * 128
                        S.i("pe", "matmul", [Atm, NTf], [bA], out=bA[ps_, cc:cc + 128], lhsT=Atm[:, h * 64:(h + 1) * 64], rhs=NTf[:, q, :], start=True, stop=True)
                    S.i("dve", "tensor_tensor", [bA, FMK(0, 2 * hg), FMK(0, 2 * hg + 1)], [R(AhT, hg)], out=AhT[:, 2 * hg:2 * hg + 2, :],
                        in0=bA[:, 0:256].rearrange("p (a b) -> p a b", b=128), in1=xb[:, 2 * hg:2 * hg + 2, cs.start:cs.stop], op=ALU.add)
            if stg < 9:
                continue
            Ubv = Ub[:].rearrange("p (h n) -> p h n", n=64)
            Vhv = Vhat[:].rearrange("p (h n) -> p h n", n=64)
            for par in range(2):
                bk = PB[par]
                for hh in range(8):
                    h = 2 * hh + par
                    ps_, c = slice(par * 64, par * 64 + 64), h // 2
                    S.i("pe", "matmul", [AhT, ST], [bk], out=bk[:, hh * 64:(hh + 1) * 64], lhsT=AhT[ps_, c, :], rhs=ST[ps_, c, :], start=True, stop=True)
            for par in range(2):
                S.i("dve", "tensor_tensor", [PB[par], Vhat], [R(Ub, par)], out=Ubv[:, par::2, :], in0=PB[par][:].rearrange("p (h n) -> p h n", n=64),
                    in1=Vhv[:, par::2, :], op=ALU.add)
            for c in range(NCH):
                S.i("act", "activation", [R(ST, c), PC], [R(STs, c)], out=STs[:, c, :], in_=ST[:, c, :], func=AF.Identity, scale=PC[:, c, n:n + 1])
            for par in range(2):
                bk = PB[2 + par]
                for hh in range(8):
                    h = 2 * hh + par
                    ps_, c = slice(par * 64, par * 64 + 64), h // 2
                    o_ = bk[:, hh * 64:(hh + 1) * 64]
                    S.i("pe", "matmul", [FMK(3, c), ST], [bk], out=o_, lhsT=FM(3, c, cs)[ps_, :], rhs=ST[ps_, c, :], start=True, stop=False)
                    S.i("pe", "matmul", [Mrk, Vtm], [bk], out=o_, lhsT=Mrk[:, h, :], rhs=Vtm[:, h * 64:(h + 1) * 64], start=False, stop=False)
                    S.i("pe", "matmul", [Mrb, Ub], [bk], out=o_, lhsT=Mrb[:, h, :], rhs=Ub[:, h * 64:(h + 1) * 64], start=False, stop=True)
            for h in range(16):
                ps_, c = slice((h % 2) * 64, (h % 2) * 64 + 64), h // 2
                o_ = PB[4][ps_, c * 64:(c + 1) * 64]
                S.i("pe", "matmul", [Ktm, Vtm], [PB[4]], out=o_, lhsT=Ktm[:, h * 64:(h + 1) * 64], rhs=Vtm[:, h * 64:(h + 1) * 64], start=True, stop=False)
                S.i("pe", "matmul", [Btm, Ub], [PB[4]], out=o_, lhsT=Btm[:, h * 64:(h + 1) * 64], rhs=Ub[:, h * 64:(h + 1) * 64], start=False, stop=True)
            for c in range(NCH):
                S.i("dve", "scalar_tensor_tensor", [PB[4], PC, R(STs, c)], [R(ST, c)], out=ST[:, c, :], in0=PB[4][:, c * 64:(c + 1) * 64], scalar=PC[:, c, n:n + 1],
                    in1=STs[:, c, :], op0=ALU.mult, op1=ALU.add)
            for b in range(2):
                S.i("dve", "tensor_reduce", [PB[2 + b]], [R(st4, ("s", b))], out=st4[:, 0, b::2], in_=PB[2 + b][:].rearrange("p (h n) -> p h n", n=64), axis=AX.X, op=ALU.add)
                S.i("act", "activation", [PB[2 + b]], [R(yn, None)], out=yn[:, b * 512:(b + 1) * 512], in_=PB[2 + b][:], func=AF.Square)
                S.i("dve", "tensor_reduce", [R(yn, None)], [R(st4, ("q", b))], out=st4[:, 1, b::2], in_=yn[:, b * 512:(b + 1) * 512].rearrange("p (h n) -> p h n", n=64), axis=AX.X, op=ALU.add)
            S.i("dve", "tensor_scalar", [st4], [R(st4, "m")], out=st4[:, 2, :], in0=st4[:, 0, :], scalar1=1.0 / 64, scalar2=None, op0=ALU.mult)
            S.i("dve", "tensor_tensor", [st4], [R(st4, "v")], out=st4[:, 3, :], in0=st4[:, 2, :], in1=st4[:, 2, :], op=ALU.mult)
            S.i("dve", "scalar_tensor_tensor", [st4], [R(st4, "v")], out=st4[:, 3, :], in0=st4[:, 1, :], scalar=1.0 / 64, in1=st4[:, 3, :], op0=ALU.mult, op1=ALU.subtract)
            S.i("dve", "tensor_scalar", [st4], [R(st4, "v")], out=st4[:, 3, :], in0=st4[:, 3, :], scalar1=64e-5, scalar2=None, op0=ALU.add)
            S.i("act", "activation", [st4], [R(st4, "v")], out=st4[:, 3, :], in_=st4[:, 3, :], func=AF.Ln)
            S.i("act", "activation", [st4], [R(st4, "v")], out=st4[:, 3, :], in_=st4[:, 3, :], func=AF.Exp, scale=-0.5)
            for c in range(NCH):
                S.i("pe", "matmul", [FMK(4, c), g.cb], [PB[5]], out=PB[5][:, 2 * c:2 * c + 2], lhsT=FM(4, c, cs), rhs=hind, start=True, stop=True)
            S.i("dve", "tensor_copy", [PB[5]], [bsum], out=bsum[:], in_=PB[5][:, 0:16])
            for h in range(16):
                b = h % 2
                hs = slice(h * 64, (h + 1) * 64)
                S.i("dve", "tensor_scalar", [PB[2 + b], st4], [R(yn, h)], out=yn[:, hs], in0=PB[2 + b][:, (h // 2) * 64:(h // 2 + 1) * 64],
                    scalar1=st4[:, 2, h:h + 1], scalar2=st4[:, 3, h:h + 1], op0=ALU.subtract, op1=ALU.mult)
            S.i("dve", "tensor_tensor", [yn, rows], [yn], out=yn[:], in0=yn[:], in1=rows[:, 0, :], op=ALU.mult)
            S.i("dve", "tensor_tensor", [yn, rows], [yn], out=yn[:], in0=yn[:], in1=rows[:, 1, :], op=ALU.add)
            for h in range(16):
                hs = slice(h * 64, (h + 1) * 64)
                S.i("dve", "scalar_tensor_tensor", [Vtm, bsum, yn], [R(yn, h)], out=yn[:, hs], in0=Vtm[:, hs], scalar=bsum[:, h:h + 1], in1=yn[:, hs], op0=ALU.mult, op1=ALU.add)
            for b in range(2):
                S.i("pe", "matmul", [sg, g2], [PB[b]], out=PB[b][:], lhsT=sg[:, 0, cs], rhs=g2[:, 0, b * 512:(b + 1) * 512], start=True, stop=False)
                S.i("pe", "matmul", [sg, g2], [PB[b]], out=PB[b][:], lhsT=sg[0:32, 1, cs], rhs=g2[0:32, 1, b * 512:(b + 1) * 512], start=False, stop=True)
                S.i("dve", "tensor_tensor", [yn, PB[b]], [R(ogtm, b)], out=ogtm[:, b * 512:(b + 1) * 512], in0=yn[:, b * 512:(b + 1) * 512], in1=PB[b][:], op=ALU.mult)
            for c in range(NCH):
                S.i("pe", "transpose", [ogtm, g.cb], [pT], out=pT[:, c * 128:(c + 1) * 128], in_=ogtm[:, c * 128:(c + 1) * 128], identity=ident)
            for c in range(NCH):
                S.i("act", "activation", [pT], [R(ogT, (c, n))], out=ogT[:, c, cs], in_=pT[:, c * 128:(c + 1) * 128], func=AF.Copy)
        if stg < 11:
            continue
        for m in range(NCH):
            wo = wget()
            bk = PB[5 + (m % 2)]
            for k in range(NCH):
                S.i("pe", "matmul", [wo, ogT], [bk], out=bk[:, 0:TW], lhsT=wo[:, k, :], rhs=ogT[:, k, :], start=(k == 0), stop=(k == NCH - 1))
            xs = x[:, m, t0:t0 + TW]
            S.i("dve", "scalar_tensor_tensor", [bk, R(x, (m, tt))], [R(x, (m, tt))], out=xs, in0=xs, scalar=ALPHA, in1=bk[:, 0:TW], op0=ALU.mult, op1=ALU.add)
    S.pop()
    S.push()
    layer_norm(g, ("ln1_g%d" % i, "ln1_b%d" % i), range(4))
    S.pop()

def make_consts():
    c = np.zeros((128, 1024), np.float32)
    c[:, 0:128] = 1.0
    c[:, 128:256] = np.eye(128, dtype=np.float32)
    bo = np.zeros((128, 128), np.float32); bo[:64, :64] = 1.0; bo[64:, 64:] = 1.0
    c[:, 256:384] = bo
    t = np.arange(128)
    c[:, 384:512] = np.where(t[None, :] <= t[:, None], 0.0, -30000.0)
    c[:, 512:640] = (t[:, None] < t[None, :]).astype(np.float32)
    c[:, 640:768] = (t[:, None] <= t[None, :]).astype(np.float32)
    c[:64, 768] = 1.0; c[64:, 769] = 1.0
    return c


def make_ret_tables():
    dk = 256
    inv = (1.0 / (np.float32(10000.0) ** np.linspace(0.0, 1.0, dk // 2, dtype=np.float32))).astype(np.float32)
    pos = np.arange(T_SEQ, dtype=np.float32)
    ang = (pos[:, None] * inv[None, :]).astype(np.float32)
    cos = np.cos(ang).astype(np.float32); sin = np.sin(ang).astype(np.float32)
    f = np.arange(dk)
    cosf = cos[:, f // 2].T
    sgn = np.where(f % 2 == 0, -1.0, 1.0).astype(np.float32)
    sinf = (sin[:, f // 2].T * sgn[:, None]).astype(np.float32)
    tab = np.stack([cosf.reshape(2, 128, T_SEQ).transpose(1, 0, 2), sinf.reshape(2, 128, T_SEQ).transpose(1, 0, 2)])
    m = np.zeros((4, 128, 384), np.float32)
    idx = np.arange(128, dtype=np.float64)
    for h in range(4):
        lg = np.log(1.0 - 2.0 ** (-5.0 - h))
        rel = idx[None, :] - idx[:, None]
        m[h, :, 0:128] = np.where(rel >= 0, np.exp(lg * np.maximum(rel, 0)), 0.0) / 16.0
        m[h, :, 128:256] = np.exp(lg * (idx + 1.0))[None, :]
        m[h, :, 256] = np.exp(lg * (127.0 - idx)) / 16.0
        m[h, :, 257] = np.exp(lg * 128.0)
    return np.ascontiguousarray(tab.astype(np.float32)), m


FULL_PLAN = [("rwkv", 0, 0), ("ffn", 0), ("ret", 1), ("ffn", 1), ("moba", 2), ("ffn", 2), ("rwkv", 3, 1), ("ffn", 3)]
_PLAN = FULL_PLAN
_NC_CACHE = {}


def host_inputs(inputs):
    shared = {k: np.ascontiguousarray(np.asarray(inputs[k], np.float32)) for k in WEIGHT_SHAPES}
    shared["vecs"] = pack_vecs(inputs)
    shared["consts"] = make_consts()
    rows = np.zeros((4, 128, 1024), np.float32)
    for j in range(2):
        rows[2 * j] = np.broadcast_to(np.asarray(inputs["rwkv_gn_g"][j], np.float32)[None, :], (128, 1024))
        rows[2 * j + 1] = np.broadcast_to(np.asarray(inputs["rwkv_gn_b"][j], np.float32)[None, :], (128, 1024))
    shared["rows"] = rows
    ti = np.arange(128)
    mk = np.zeros((128, 3, 512), np.float32)
    mk[:, 0, :] = np.tile((ti[:, None] < ti[None, :]).astype(np.float32), (1, 4))
    mk[:, 1, :] = np.tile((ti[:, None] <= ti[None, :]).astype(np.float32), (1, 4))
    mk[:, 2, :] = np.tile((ti[:, None] > ti[None, :]).astype(np.float32), (1, 4))
    shared["mk"] = mk
    tab, msk = make_ret_tables()
    shared["rtab"] = tab
    shared["rmask"] = msk
    wqk = np.asarray(inputs["ret_w_in"][0][:, :2048], np.float32)
    shared["ret_sw"] = np.ascontiguousarray(wqk.reshape(1024, 1024, 2)[:, :, ::-1].reshape(1024, 2048))
    return shared


def run_plan(plan, inputs, x_full, n_cores=8):
    key = repr(plan)
    if key not in _NC_CACHE:
        _NC_CACHE[key] = build_program(plan)
    nc = _NC_CACHE[key]
    shared = host_inputs(inputs)
    in_maps = []
    for b in range(n_cores):
        m = dict(shared)
        m["xT"] = np.ascontiguousarray(np.asarray(x_full[b], np.float32).T)
        in_maps.append(m)
    res = run_bass_kernel_spmd(nc, in_maps, core_ids=list(range(n_cores)))
    return np.stack([np.ascontiguousarray(r["outT"].T) for r in res.results]).astype(np.float32)


def kernel(**inputs):
    return run_plan(_PLAN, inputs, inputs["x"], 8)
```

```python
import numpy as np
import concourse.bass as bass
import concourse.mybir as mybir
from concourse.bass_utils import run_bass_kernel_spmd
from contextlib import ExitStack

_DBG = {}
F32 = mybir.dt.float32
BF16 = mybir.dt.bfloat16
AF = mybir.ActivationFunctionType
ALU = mybir.AluOpType
AX = mybir.AxisListType

ENGS = ("pe", "act", "dve", "pool", "sp")


class T:
    def __init__(self, S, t, name):
        self.S = S
        self.t = t
        self.name = name
        self.st = {}
        self.is_psum = False
        self.dsem = None
        self.dcnt = 0

    def __getitem__(self, idx):
        return self.t[idx]


class R:
    __slots__ = ("tile", "key")
    def __init__(self, tile, key=None):
        self.tile = tile
        self.key = key


class Sched:
    def __init__(self, nc):
        self.nc = nc
        self.es = ExitStack()
        self.ins = {e: [] for e in ENGS}
        self.cnt = {e: 0 for e in ENGS}
        self.seen = {e: {} for e in ENGS}
        self.sems = {}
        self.needed = {e: set() for e in ENGS}
        self.nsem = 0
        for e in ENGS:
            self.sems[("e", e)] = self.es.enter_context(nc.semaphore("sem_" + e))
        self.dma_pool = []
        self.out_dma = []
        self.pend = {}
        self.scopes = []

    def _es(self):
        return self.scopes[-1] if self.scopes else self.es

    def push(self):
        self.scopes.append(ExitStack())

    def pop(self):
        self.barrier()
        self.scopes.pop().close()

    def sbuf(self, name, shape, dt):
        self.uid = getattr(self, "uid", 0) + 1
        name = "%s_%d" % (name, self.uid)
        t = self._es().enter_context(self.nc.sbuf_tensor(name, list(shape), dt))
        return T(self, t, name)

    def psum(self, name, shape, dt=F32):
        self.uid = getattr(self, "uid", 0) + 1
        name = "%s_%d" % (name, self.uid)
        t = self._es().enter_context(self.nc.psum_tensor(name, list(shape), dt))
        tt = T(self, t, name)
        tt.is_psum = True
        return tt

    def new_dsem(self, name):
        self.nsem += 1
        key = ("d", self.nsem)
        self.sems[key] = self.es.enter_context(self.nc.semaphore("dsem%d_%s" % (self.nsem, name)))
        return key

    def _states(self, ref, create=True):
        tile, key = ref.tile, ref.key
        if key is None:
            if None not in tile.st:
                tile.st[None] = [None, {}]
            return list(tile.st.values())
        out = []
        if None in tile.st:
            out.append(tile.st[None])
        if key not in tile.st:
            tile.st[key] = [None, {}]
        out.append(tile.st[key])
        return out

    def _collect(self, eng, reads, writes):
        waits = {}
        def need(w, same_ok=False):
            if w is None:
                return
            k, v = w
            if same_ok and k == ("e", eng) and eng == "pe":
                return
            if waits.get(k, 0) < v:
                waits[k] = v
        for r in reads:
            for st in self._states(r):
                need(st[0])
                if r.tile.is_psum:
                    for rk, rv in st[1].items():
                        if rk != ("e", eng):
                            need((rk, rv))
        for w in writes:
            for st in self._states(w):
                need(st[0])
                for rk, rv in st[1].items():
                    if rk == ("e", eng) and eng == "pe":
                        continue
                    need((rk, rv))
        final = []
        for k, v in waits.items():
            if k == ("e", eng) and eng == "pe":
                continue
            if self.seen[eng].get(k, 0) < v:
                self.seen[eng][k] = v
                final.append((k, v))
                if k[0] == "e":
                    self.needed[k[1]].add(v)
        return final

    def _commit(self, tag, reads, writes):
        for r in reads:
            if r.key is None:
                for k, s in r.tile.st.items():
                    if s[1].get(tag[0], 0) < tag[1]:
                        s[1][tag[0]] = tag[1]
            else:
                s = r.tile.st[r.key]
                if s[1].get(tag[0], 0) < tag[1]:
                    s[1][tag[0]] = tag[1]
        for w in writes:
            if w.key is None:
                w.tile.st = {None: [tag, {}]}
            else:
                w.tile.st[w.key] = [tag, {}]

    def i(self, eng, meth, reads=(), writes=(), **kw):
        return self.op(eng, (meth, kw), reads, writes)

    def _norm(self, refs):
        out = []
        for r in refs:
            if not isinstance(r, R):
                r = R(r)
            if r.tile.is_psum and r.key is not None:
                r = R(r.tile)
            out.append(r)
        return out

    def op(self, eng, fn, reads=(), writes=()):
        reads = self._norm(reads)
        writes = self._norm(writes)
        waits = self._collect(eng, reads, writes)
        self.cnt[eng] += 1
        idx = self.cnt[eng]
        self.ins[eng].append([fn, waits, idx, None])
        self._commit((("e", eng), idx), reads, writes)
        return idx

    def dma(self, q, out_ap, in_ap, reads=(), writes=(), out_final=False, owner=None, **kw):
        reads = [r if isinstance(r, R) else R(r) for r in reads]
        writes = [w if isinstance(w, R) else R(w) for w in writes]
        waits = self._collect(q, reads, writes)
        if owner is None:
            owner = (writes[0] if writes else reads[0]).tile
        if owner.dsem is None:
            owner.dsem = self.new_dsem(owner.name)
        owner.dcnt += 16
        tag = (owner.dsem, owner.dcnt)
        self.cnt[q] += 1
        idx = self.cnt[q]
        self.ins[q].append([lambda e: e.dma_start(out=out_ap, in_=in_ap, **kw), waits, idx, tag])
        self._commit(tag, reads, writes)
        self.pend[tag[0]] = tag[1]
        if out_final:
            self.out_dma.append(tag)
        return tag

    def barrier(self):
        for f in ENGS:
            if self.cnt[f] and (self.ins[f][-1][0] is None or self.ins[f][-1][3] is not None):
                self.cnt[f] += 1
                self.ins[f].append([None, [], self.cnt[f], None])
        for e in ENGS:
            waits = []
            for f in ENGS:
                if f == e or self.cnt[f] == 0:
                    continue
                v = self.cnt[f]
                if self.seen[e].get(("e", f), 0) < v:
                    self.seen[e][("e", f)] = v
                    waits.append((("e", f), v))
                    self.needed[f].add(v)
            for k, v in self.pend.items():
                if self.seen[e].get(k, 0) < v:
                    self.seen[e][k] = v
                    waits.append((k, v))
            if waits:
                self.cnt[e] += 1
                self.ins[e].append([None, waits, self.cnt[e], None])

    def emit(self):
        nc = self.nc
        fwd = {}
        for k, v in self.out_dma:
            fwd[k] = max(fwd.get(k, 0), v)
        fw = list(fwd.items())
        self.cnt["sp"] += 1
        self.ins["sp"].append([None, fw, self.cnt["sp"], None])
        rank = {}
        for e in ENGS:
            s = sorted(self.needed[e])
            rank[e] = {v: i + 1 for i, v in enumerate(s)}
        engmap = {"pe": "tensor", "act": "scalar", "dve": "vector", "pool": "gpsimd", "sp": "sync"}
        with nc.Block() as block:
            for e in ENGS:
                lst = self.ins[e]
                if not lst:
                    continue
                def body(eng, e=e, lst=lst):
                    for fn, waits, idx, dtag in lst:
                        for k, v in waits:
                            if k[0] == "e":
                                eng.wait_ge(self.sems[k], rank[k[1]][v])
                            else:
                                eng.wait_ge(self.sems[k], v)
                        if fn is None:
                            if idx in self.needed[e]:
                                eng.nop().then_inc(self.sems[("e", e)], 1)
                            continue
                        ins = fn(eng) if callable(fn) else getattr(eng, fn[0])(**fn[1])
                        if dtag is not None:
                            ins.then_inc(self.sems[dtag[0]], 16)
                            if idx in self.needed[e]:
                                raise RuntimeError("dma instr needed as engine milestone")
                        elif idx in self.needed[e]:
                            ins.then_inc(self.sems[("e", e)], 1)
                getattr(block, engmap[e])(body)
        self.es.close()

D = 1024; T_SEQ = 2048; DEPTH = 4; FF = 2816; NCH = 8; NPAIR = 22
ALPHA = float((2 * DEPTH) ** 0.25)
LN_EPS = 1e-5

WEIGHT_SHAPES = {
    "rwkv_w_rkv": [2, 3, 1024, 1024], "rwkv_w1": [2, 1024, 64], "rwkv_w2": [2, 64, 1024],
    "rwkv_a1": [2, 1024, 64], "rwkv_a2": [2, 64, 1024], "rwkv_g1": [2, 1024, 160], "rwkv_g2": [2, 160, 1024],
    "rwkv_w_o": [2, 1024, 1024], "ret_w_in": [1, 1024, 6144], "ret_w_o": [1, 2048, 1024],
    "moba_w_qkv": [1, 1024, 3072], "moba_w_o": [1, 1024, 1024],
    "ffn_w_in": [4, 1024, 5632], "ffn_w_out": [4, 2816, 1024],
}


def _fm(v):
    v = np.asarray(v, np.float32).reshape(-1, 128)
    return np.ascontiguousarray(v.T)


def vec_layout():
    off = {}
    n = 0
    def add(name, cols):
        nonlocal n
        off[name] = n
        n += cols
    for i in range(4):
        for nm in ("ln1_g", "ln1_b", "ln2_g", "ln2_b"):
            add("%s%d" % (nm, i), 8)
        for k in range(3):
            add("cw%d_%d" % (k, i), 44)
        add("cb_%d" % i, 44)
    for j in range(2):
        for m in range(6):
            add("mix%d_%d" % (m, j), 8)
        for nm in ("w0", "a0", "k_k", "k_a", "r_k"):
            add("%s_%d" % (nm, j), 8)
    add("ret_gn_g", 16)
    add("ret_gn_b", 16)
    return off, n


def pack_vecs(inp):
    off, n = vec_layout()
    out = np.zeros((128, n), np.float32)
    def put(name, v):
        a = _fm(v)
        out[:, off[name]:off[name] + a.shape[1]] = a
    for i in range(4):
        for nm in ("ln1_g", "ln1_b", "ln2_g", "ln2_b"):
            put("%s%d" % (nm, i), inp[nm][i])
        for k in range(3):
            put("cw%d_%d" % (k, i), inp["ffn_conv_w"][i, k])
        put("cb_%d" % i, inp["ffn_conv_b"][i])
    for j in range(2):
        for m in range(6):
            put("mix%d_%d" % (m, j), inp["rwkv_mix"][j, m])
        put("w0_%d" % j, inp["rwkv_w0"][j]); put("a0_%d" % j, inp["rwkv_a0"][j])
        put("k_k_%d" % j, inp["rwkv_k_k"][j]); put("k_a_%d" % j, inp["rwkv_k_a"][j])
        put("r_k_%d" % j, inp["rwkv_r_k"][j].reshape(-1))
    put("ret_gn_g", inp["ret_gn_g"][0]); put("ret_gn_b", inp["ret_gn_b"][0])
    return out


class Ctx:
    pass


def build_program(plan, x_in_bf=True):
    nc = bass.Bass("TRN2", target_bir_lowering=False)
    S = Sched(nc)
    g = Ctx()
    g.nc, g.S = nc, S
    g.W = {k: nc.dram_tensor(k, shp, F32, kind="ExternalInput").ap() for k, shp in WEIGHT_SHAPES.items()}
    voff, nv = vec_layout()
    g.voff = voff
    xT_d = nc.dram_tensor("xT", [D, T_SEQ], F32, kind="ExternalInput").ap()
    vecs_d = nc.dram_tensor("vecs", [128, nv], F32, kind="ExternalInput").ap()
    consts_d = nc.dram_tensor("consts", [128, 1024], F32, kind="ExternalInput").ap()
    g.rows_d = nc.dram_tensor("rows", [4, 128, 1024], F32, kind="ExternalInput").ap()
    g.rtab_d = nc.dram_tensor("rtab", [2, 128, 2, T_SEQ], F32, kind="ExternalInput").ap()
    g.rmask_d = nc.dram_tensor("rmask", [4, 128, 384], F32, kind="ExternalInput").ap()
    g.ret_sw_d = nc.dram_tensor("ret_sw", [1024, 2048], F32, kind="ExternalInput").ap()
    g.mk_d = nc.dram_tensor("mk", [128, 3, 512], F32, kind="ExternalInput").ap()
    out_d = nc.dram_tensor("outT", [D, T_SEQ], F32, kind="ExternalOutput").ap()

    g.x = S.sbuf("x", [128, NCH, T_SEQ], F32)
    g.xb = S.sbuf("xb", [128, NCH, T_SEQ], BF16)
    g.vecs = S.sbuf("vecs", [128, nv], F32)
    g.cf = S.sbuf("cf", [128, 256], F32)
    g.cb = S.sbuf("cb", [128, 512], BF16)
    xv = xT_d.rearrange("(c p) t -> p c t", p=128)
    for c in range(NCH):
        S.dma("sp", g.x[:, c, :], xv[:, c, :], writes=[R(g.x, None)])
    if plan[0][0] != "rwkv":
        for c in range(NCH):
            S.dma("pool", g.xb[:, c, :], xv[:, c, :], writes=[R(g.xb, None)])
    S.dma("sp", g.vecs[:], vecs_d, writes=[g.vecs])
    S.dma("sp", g.cf[:, 0:128], consts_d[:, 0:128], writes=[R(g.cf, None)])
    S.dma("sp", g.cf[:, 128:256], consts_d[:, 384:512], writes=[R(g.cf, None)])
    S.i("pool", "memset", [], [g.cb], ap=g.cb[:], constant=0.0)
    S.dma("pool", g.cb[:, 0:256], consts_d[:, 128:384], writes=[R(g.cb, None)])
    S.dma("pool", g.cb[:, 256:258], consts_d[:, 768:770], writes=[R(g.cb, None)])
    g.mk = S.sbuf("mk", [128, 3, 512], BF16)
    S.dma("pool", g.mk[:], g.mk_d, writes=[g.mk])
    g.ones32 = g.cf[:, 0:128]
    g.identb = g.cb[:, 0:128]

    for si_, step in enumerate(plan):
        kind = step[0]
        if kind == "ffn":
            nxt = plan[si_ + 1][0] if si_ + 1 < len(plan) else None
            ffn_phase(g, step[1], write_xb=(nxt not in (None, "rwkv")))
        elif kind == "ln":
            S.push()
            layer_norm(g, step[1], range(4))
            S.pop()
        elif kind == "moba":
            moba_phase(g, step[1])
        elif kind == "ret":
            ret_phase(g, step[1])
        elif kind == "rwkv":
            rwkv_phase(g, step[1], step[2])
        else:
            raise ValueError(kind)

    S.barrier()
    ov = out_d.rearrange("(c p) t -> p c t", p=128)
    outsem = T(S, None, "outsem")
    for c in range(NCH):
        S.dma("sp", ov[:, c, :], g.x[:, c, :], reads=[R(g.x, None)], out_final=True)
    S.emit()
    return nc


def vcol(g, name, c=0):
    o = g.voff[name] + c
    return g.vecs[:, o:o + 1]


def layer_norm(g, pref, tts, width=512, write_xb=True):
    S = g.S
    gname, bname = pref
    x, xb = g.x, g.xb
    tts = list(tts)
    nt = len(tts)
    sq = [S.sbuf("lnsq", [128, 512], F32) for _ in range(2)]
    means = [S.sbuf("lnmean", [128, 512], F32) for _ in range(nt)]
    msqs = [S.sbuf("lnmsq", [128, 512], F32) for _ in range(2)]
    rstds = [S.sbuf("lnrstd", [128, 512], F32) for _ in range(nt)]
    tmp = [S.sbuf("lntmp", [128, 512], F32) for _ in range(3)]
    ps_ss = [S.psum("lnps", [128, 512]) for _ in range(2)]
    ps_qs = [S.psum("lnpq", [128, 512]) for _ in range(2)]
    ones = g.ones32
    W_ = width
    def stats(ii, tix):
        ts = slice(tix * W_, (tix + 1) * W_)
        tt = (tix * W_) // 512
        mean, msq, rstd, ps_s, ps_q = means[ii], msqs[ii % 2], rstds[ii], ps_ss[ii % 2], ps_qs[ii % 2]
        for c in range(NCH):
            q = sq[c % 2]
            S.i("act", "activation", [R(x, (c, tt))], [q], out=q[:, 0:W_], in_=x[:, c, ts], func=AF.Square)
            S.i("pe", "matmul", [R(x, (c, tt)), g.cf], [ps_s], out=ps_s[:, 0:W_], lhsT=ones, rhs=x[:, c, ts], start=(c == 0), stop=(c == NCH - 1))
            S.i("pe", "matmul", [q, g.cf], [ps_q], out=ps_q[:, 0:W_], lhsT=ones, rhs=q[:, 0:W_], start=(c == 0), stop=(c == NCH - 1))
        S.i("dve", "tensor_scalar", [ps_s], [mean], out=mean[:, 0:W_], in0=ps_s[:, 0:W_], scalar1=1.0 / D, scalar2=None, op0=ALU.mult)
        S.i("dve", "tensor_tensor", [mean], [msq], out=msq[:, 0:W_], in0=mean[:, 0:W_], in1=mean[:, 0:W_], op=ALU.mult)
        S.i("dve", "scalar_tensor_tensor", [ps_q, msq], [msq], out=msq[:, 0:W_], in0=ps_q[:, 0:W_], scalar=1.0 / D, in1=msq[:, 0:W_], op0=ALU.mult, op1=ALU.subtract)
        S.i("dve", "tensor_scalar", [msq], [msq], out=msq[:, 0:W_], in0=msq[:, 0:W_], scalar1=LN_EPS, scalar2=None, op0=ALU.add)
        S.i("act", "activation", [msq], [rstd], out=rstd[:, 0:W_], in_=msq[:, 0:W_], func=AF.Ln)
        S.i("act", "activation", [rstd], [rstd], out=rstd[:, 0:W_], in_=rstd[:, 0:W_], func=AF.Exp, scale=-0.5)
    def norm(ii, tix):
        ts = slice(tix * W_, (tix + 1) * W_)
        tt = (tix * W_) // 512
        mean, rstd = means[ii], rstds[ii]
        for c in range(NCH):
            tm = tmp[c % 3]
            S.i("dve", "tensor_tensor", [R(x, (c, tt)), mean], [tm], out=tm[:, 0:W_], in0=x[:, c, ts], in1=mean[:, 0:W_], op=ALU.subtract)
            S.i("dve", "tensor_tensor", [tm, rstd], [tm], out=tm[:, 0:W_], in0=tm[:, 0:W_], in1=rstd[:, 0:W_], op=ALU.mult)
            S.i("act", "activation", [tm, g.vecs], [R(x, (c, tt))], out=x[:, c, ts], in_=tm[:, 0:W_], func=AF.Identity,
                bias=vcol(g, bname, c), scale=vcol(g, gname, c))
            if write_xb:
                S.i("dve", "tensor_scalar", [tm, g.vecs], [R(xb, (c, tt))], out=xb[:, c, ts], in0=tm[:, 0:W_], scalar1=vcol(g, gname, c),
                    scalar2=vcol(g, bname, c), op0=ALU.mult, op1=ALU.add)
    stats(0, tts[0])
    for ii in range(nt):
        if ii + 1 < nt:
            stats(ii + 1, tts[ii + 1])
        norm(ii, tts[ii])


def ffn_phase(g, i, write_xb=True):
    S = g.S
    x, xb = g.x, g.xb
    S.push()
    W_in = g.W["ffn_w_in"][i].rearrange("(k p) (two f) -> p k two f", p=128, two=2)
    W_out = g.W["ffn_w_out"][i].rearrange("(k p) n -> p k n", p=128)
    TB = 1024
    a = S.sbuf("ffa", [128, NPAIR, TB], BF16)
    hus = [S.sbuf("hu", [128, TB + 2], F32) for _ in range(2)]
    hgs = [S.sbuf("hg", [128, TB + 2], F32) for _ in range(2)]
    aus = [S.sbuf("au", [128, TB], F32) for _ in range(2)]
    ags = [S.sbuf("ag", [128, TB], F32) for _ in range(2)]
    halo = S.sbuf("halo", [128, 2 * NPAIR, 2], F32)
    wps = [[S.sbuf("wp", [128, NCH, 128], BF16) for _ in range(2)] for _ in range(2)]
    wos = [S.sbuf("wo", [128, NPAIR, 128], BF16) for _ in range(2)]
    pb = [S.psum("ffps", [128, 512]) for _ in range(8)]
    cw = lambda k, j: vcol(g, "cw%d_%d" % (k, i), j)
    cbv = lambda j: vcol(g, "cb_%d" % i, j)
    nw = 0
    pending = []
    for blk in range(2):
        for j in range(NPAIR):
            if j == 0:
                for ug in range(2):
                    S.dma("pool", wps[nw % 2][ug][:], W_in[:, :, ug, 0:128], writes=[wps[nw % 2][ug]])
            wp = wps[nw % 2]; nw += 1
            hu, hg, au, ag = hus[j % 2], hgs[j % 2], aus[j % 2], ags[j % 2]
            if j + 1 < NPAIR:
                for ug in range(2):
                    S.dma("pool", wps[nw % 2][ug][:], W_in[:, :, ug, (j + 1) * 128:(j + 2) * 128], writes=[wps[nw % 2][ug]])
            elif True:
                S.dma("pool", wos[0][:], W_out[:, :, 0:128], writes=[wos[0]])
            banks = pb[(j % 2) * 4:(j % 2) * 4 + 4]
            for ug in range(2):
                for h in range(2):
                    bk = banks[ug * 2 + h]
                    tt = blk * 2 + h
                    for k in range(NCH):
                        S.i("pe", "matmul", [wp[ug], R(xb, (k, tt))], [bk], out=bk[:], lhsT=wp[ug][:, k, :],
                            rhs=xb[:, k, tt * 512:(tt + 1) * 512], start=(k == 0), stop=(k == NCH - 1))
            for ug, hb in ((0, hu), (1, hg)):
                for h in range(2):
                    bk = banks[ug * 2 + h]
                    S.i("act", "activation", [bk], [R(hb, h)], out=hb[:, 2 + h * 512:2 + (h + 1) * 512], in_=bk[:], func=AF.Copy)
                if blk == 0:
                    S.i("dve", "memset", [], [R(hb, "halo")], ap=hb[:, 0:2], constant=0.0)
                else:
                    S.i("dve", "tensor_copy", [R(halo, (ug, j))], [R(hb, "halo")], out=hb[:, 0:2], in_=halo[:, ug * NPAIR + j, :])
            for (hb, ab_, jj) in ((hu, au, j), (hg, ag, NPAIR + j)):
                S.i("act", "activation", [hb, g.vecs], [ab_], out=ab_[:], in_=hb[:, 2:TB + 2], func=AF.Identity, bias=cbv(jj), scale=cw(2, jj))
                if jj == j and pending:
                    pending.pop(0)()
                S.i("dve", "scalar_tensor_tensor", [hb, ab_, g.vecs], [ab_], out=ab_[:], in0=hb[:, 1:TB + 1], scalar=cw(1, jj), in1=ab_[:], op0=ALU.mult, op1=ALU.add)
                S.i("dve", "scalar_tensor_tensor", [hb, ab_, g.vecs], [ab_], out=ab_[:], in0=hb[:, 0:TB], scalar=cw(0, jj), in1=ab_[:], op0=ALU.mult, op1=ALU.add)
            def tail(au=au, ag=ag, j=j):
                S.i("act", "activation", [ag], [ag], out=ag[:], in_=ag[:], func=AF.Silu)
                S.i("dve", "tensor_tensor", [au, ag], [R(a, j)], out=a[:, j, :], in0=au[:], in1=ag[:], op=ALU.mult)
            pending.append(tail)
            if blk == 0:
                for ug, hb in ((0, hu), (1, hg)):
                    S.i("dve", "tensor_copy", [hb], [R(halo, (ug, j))], out=halo[:, ug * NPAIR + j, :], in_=hb[:, TB:TB + 2])
        while pending:
            pending.pop(0)()
        for m in range(NCH):
            wo = wos[m % 2]
            if m + 1 < NCH:
                S.dma("pool", wos[(m + 1) % 2][:], W_out[:, :, (m + 1) * 128:(m + 2) * 128], writes=[wos[(m + 1) % 2]])
            for h in range(2):
                bk = pb[(m * 2 + h) % 8]
                tt = blk * 2 + h
                for k in range(NPAIR):
                    S.i("pe", "matmul", [wo, R(a, k)], [bk], out=bk[:], lhsT=wo[:, k, :], rhs=a[:, k, h * 512:(h + 1) * 512],
                        start=(k == 0), stop=(k == NPAIR - 1))
                xs = x[:, m, tt * 512:(tt + 1) * 512]
                S.i("dve", "scalar_tensor_tensor", [bk, R(x, (m, tt))], [R(x, (m, tt))], out=xs, in0=xs, scalar=ALPHA, in1=bk[:],
                    op0=ALU.mult, op1=ALU.add)
    S.pop()
    S.push()
    layer_norm(g, ("ln2_g%d" % i, "ln2_b%d" % i), range(4), write_xb=write_xb)
    S.pop()

def moba_phase(g, i):
    S = g.S
    x, xb = g.x, g.xb
    S.push()
    Wqkv = g.W["moba_w_qkv"][0].rearrange("(k p) n -> p k n", p=128)
    Wo = g.W["moba_w_o"][0].rearrange("(k p) n -> p k n", p=128)
    ogT = S.sbuf("ogT", [128, NCH, T_SEQ], BF16)
    wsl = [[S.sbuf("mw", [128, NCH, 128], BF16) for _ in range(3)] for _ in range(2)]
    qkv = [[S.sbuf("mqkv", [128, T_SEQ], BF16) for _ in range(3)] for _ in range(2)]
    vtms = [S.sbuf("vtm", [128, 16, 128], BF16) for _ in range(2)]
    ksum = S.sbuf("ksum", [128, 8], F32)
    kmean = S.sbuf("kmean", [128, 8], BF16)
    P = S.sbuf("mP", [128, T_SEQ], BF16)
    PT = S.sbuf("mPT", [128, 16, 128], BF16)
    sd = S.sbuf("msd", [128, 128], F32)
    g8 = S.sbuf("g8", [128, 8], F32)
    top8 = S.sbuf("top8", [128, 8], F32)
    mb = S.sbuf("mb", [128, 8], F32)
    b8 = S.sbuf("b8", [128, 8], F32)
    nm = S.sbuf("nm", [128, 4], F32)
    rs = S.sbuf("rs", [128, 12], F32)
    rinv = S.sbuf("rinv", [128, 2], F32)
    otm = S.sbuf("otm", [128, 128], BF16)
    wos = [S.sbuf("mwo", [128, NCH, 128], BF16) for _ in range(2)]
    sc = S.psum("msc", [128, 2048])
    pT = S.psum("mpT", [128, 1024], BF16)
    ov = S.psum("mov", [128, 512])
    gps = ov
    pjs = [S.psum("mpj", [128, 512]) for _ in range(2)]
    npj = 0
    tri = g.cf[:, 128:256]
    if _DBG.get('dbg_memset'):
        S.i('pool', 'memset', [], [ogT], ap=ogT[:], constant=0.0)
    ident = g.identb
    for c in range(_DBG.get('moba_pairs', NCH)):
        ws = wsl[c % 2]
        qT, kT, vT = qkv[c % 2]
        vtm = vtms[c % 2]
        for w in range(3):
            S.dma("pool", ws[w][:], Wqkv[:, :, w * 1024 + c * 128: w * 1024 + (c + 1) * 128], writes=[ws[w]])
        for w, dst in ((0, qT), (1, kT), (2, vT)):
            for tt in range(4):
                pj = pjs[npj % 2]; npj += 1
                for k in range(NCH):
                    S.i("pe", "matmul", [ws[w], R(xb, (k, tt))], [pj], out=pj[:], lhsT=ws[w][:, k, :], rhs=xb[:, k, tt * 512:(tt + 1) * 512],
                        start=(k == 0), stop=(k == NCH - 1))
                if w == 0:
                    S.i("act", "activation", [pj], [R(dst, tt)], out=dst[:, tt * 512:(tt + 1) * 512], in_=pj[:], func=AF.Copy, scale=0.125)
                else:
                    S.i("act", "activation", [pj], [R(dst, tt)], out=dst[:, tt * 512:(tt + 1) * 512], in_=pj[:], func=AF.Copy)
                if w == 1 and _DBG.get('moba_stage', 9) >= 0.2:
                    S.i("dve", "tensor_reduce", [pj], [R(ksum, tt)], out=ksum[:, 2 * tt:2 * tt + 2],
                        in_=pj[:].rearrange("p (b k) -> p b k", b=2), axis=AX.X, op=ALU.add)
        if _DBG.get('moba_stage', 9) >= 0.2:
            S.i("dve", "tensor_scalar", [ksum], [kmean], out=kmean[:], in0=ksum[:], scalar1=1.0 / 256.0, scalar2=None, op0=ALU.mult)
        for b in range(2 if _DBG.get('moba_stage', 9) >= 0.3 else 0):
            for t8 in range(8):
                kt = b * 8 + t8
                S.i("pe", "transpose", [vT, g.cb], [pT], out=pT[:, t8 * 128:(t8 + 1) * 128], in_=vT[:, kt * 128:(kt + 1) * 128], identity=ident)
            S.i("dve", "tensor_copy", [pT], [R(vtm, b)], out=vtm[:, b * 8:(b + 1) * 8, :].rearrange("p a b -> p (a b)"), in_=pT[:])
        stg = _DBG.get('moba_stage', 9)
        for qt in _DBG.get('moba_qts', range(16)):
            if stg < 2:
                break
            qb = qt // 2
            nkt = qt + 1
            qs = slice(qt * 128, (qt + 1) * 128)
            for hh in range(2):
                ps = slice(hh * 64, (hh + 1) * 64)
                use_thr = qb >= 4
                if use_thr:
                    S.i("pe", "matmul", [qT, kmean], [gps], out=gps[:, 256:264], lhsT=qT[ps, qs], rhs=kmean[ps, 0:8], start=True, stop=True)
                    S.i("pool", "memset", [], [R(g8, "pad")], ap=g8[:, qb:8], constant=-1.0e30)
                    S.i("dve", "tensor_copy", [gps], [R(g8, "val")], out=g8[:, 0:qb], in_=gps[:, 256:256 + qb])
                    S.i("dve", "max", [g8], [top8], out=top8[:], in_=g8[:])
                    S.i("dve", "tensor_scalar", [g8, top8], [mb], out=mb[:], in0=g8[:], scalar1=top8[:, 2:3], scalar2=30000.0,
                        op0=ALU.is_ge, op1=ALU.mult)
                ncol = nkt * 128
                for b in range((ncol + 511) // 512):
                    w_ = min(512, ncol - b * 512)
                    S.i("pe", "matmul", [qT, kT], [sc], out=sc[:, b * 512:b * 512 + w_], lhsT=qT[ps, qs], rhs=kT[ps, b * 512:b * 512 + w_],
                        start=True, stop=True)
                S.i("dve", "tensor_tensor", [sc, g.cf], [sc], out=sc[:, qt * 128:(qt + 1) * 128], in0=sc[:, qt * 128:(qt + 1) * 128], in1=tri, op=ALU.add)
                S.i("dve", "tensor_reduce", [sc], [R(nm, 0)], out=nm[:, 0:1], in_=sc[:, 0:ncol], axis=AX.X, op=ALU.max, negate=True)
                negm = nm[:, 0:1]
                if stg < 3:
                    continue
                if use_thr:
                    S.i("dve", "tensor_scalar", [mb, nm], [b8], out=b8[:], in0=mb[:], scalar1=negm, scalar2=-30000.0, op0=ALU.add, op1=ALU.add)
                npz = 0
                if use_thr:
                    for n in range(qb):
                        S.i("act", "activation", [sc, b8], [R(P, n), R(rs, npz)], out=P[:, n * 256:(n + 1) * 256], in_=sc[:, n * 256:(n + 1) * 256],
                            func=AF.Exp, bias=b8[:, n:n + 1], scale=1.0, accum_out=rs[:, npz:npz + 1])
                        npz += 1
                    S.i("act", "activation", [sc, nm], [R(P, "o"), R(rs, npz)], out=P[:, qb * 256:ncol], in_=sc[:, qb * 256:ncol],
                        func=AF.Exp, bias=negm, scale=1.0, accum_out=rs[:, npz:npz + 1])
                    npz += 1
                else:
                    S.i("act", "activation", [sc, nm], [R(P, "past"), R(rs, npz)], out=P[:, 0:ncol], in_=sc[:, 0:ncol],
                        func=AF.Exp, bias=negm, scale=1.0, accum_out=rs[:, npz:npz + 1])
                    npz += 1
                S.i("dve", "tensor_reduce", [rs], [R(rs, 11)], out=rs[:, 11:12], in_=rs[:, 0:npz], axis=AX.X, op=ALU.add)
                S.i("dve", "reciprocal", [R(rs, 11)], [R(rinv, hh)], out=rinv[:, hh:hh + 1], in_=rs[:, 11:12])
                if stg < 4:
                    continue
                for b in range((nkt + 7) // 8):
                    n8 = min(8, nkt - b * 8)
                    for t8 in range(n8):
                        kt = b * 8 + t8
                        S.i("pe", "transpose", [P, g.cb], [pT], out=pT[:, t8 * 128:(t8 + 1) * 128], in_=P[:, kt * 128:(kt + 1) * 128], identity=ident)
                    S.i("dve", "tensor_copy", [pT], [R(PT, b)], out=PT[:, b * 8:b * 8 + n8, :].rearrange("p a b -> p (a b)"), in_=pT[:, 0:n8 * 128])
                for kt in range(nkt):
                    S.i("pe", "matmul", [PT, vtm], [R(ov, hh)], out=ov[:, hh * 64:(hh + 1) * 64], lhsT=PT[:, kt, :], rhs=vtm[:, kt, hh * 64:(hh + 1) * 64],
                        start=(kt == 0), stop=(kt == nkt - 1))
                S.i("dve", "tensor_scalar", [R(ov, hh), R(rinv, hh)], [R(otm, hh)], out=otm[:, hh * 64:(hh + 1) * 64], in0=ov[:, hh * 64:(hh + 1) * 64],
                    scalar1=rinv[:, hh:hh + 1], scalar2=None, op0=ALU.mult)
            if stg < 5:
                continue
            S.i("pe", "transpose", [otm, g.cb], [pT], out=pT[:, 0:128], in_=otm[:], identity=ident)
            S.i("act", "activation", [pT], [R(ogT, (c, qt))], out=ogT[:, c, qs], in_=pT[:, 0:128], func=AF.Copy)
    for m in range(NCH):
        wo = wos[m % 2]
        S.dma("pool", wo[:], Wo[:, :, m * 128:(m + 1) * 128], writes=[wo])
        for tt in range(4):
            pj = pjs[npj % 2]; npj += 1
            for k in range(NCH):
                S.i("pe", "matmul", [wo, ogT], [pj], out=pj[:], lhsT=wo[:, k, :], rhs=ogT[:, k, tt * 512:(tt + 1) * 512], start=(k == 0), stop=(k == NCH - 1))
            xs = x[:, m, tt * 512:(tt + 1) * 512]
            S.i("dve", "scalar_tensor_tensor", [pj, R(x, (m, tt))], [R(x, (m, tt))], out=xs, in0=xs, scalar=ALPHA, in1=pj[:], op0=ALU.mult, op1=ALU.add)
    S.pop()
    S.push()
    layer_norm(g, ("ln1_g%d" % i, "ln1_b%d" % i), range(4))
    S.pop()

def ret_phase(g, i):
    S = g.S
    x, xb = g.x, g.xb
    S.push()
    Win = g.W["ret_w_in"][0].rearrange("(k p) n -> p k n", p=128)
    Wsw = g.ret_sw_d.rearrange("(k p) n -> p k n", p=128)
    Wo = g.W["ret_w_o"][0].rearrange("(k p) n -> p k n", p=128)
    wq = S.sbuf("rwq", [128, NCH, 256], BF16); wqs = S.sbuf("rwqs", [128, NCH, 256], BF16)
    wk = S.sbuf("rwk", [128, NCH, 256], BF16); wks = S.sbuf("rwks", [128, NCH, 256], BF16)
    wv = S.sbuf("rwv", [128, NCH, 512], BF16); wg = S.sbuf("rwg", [128, NCH, 512], BF16)
    wo = S.sbuf("rwo", [128, 4, 1024], BF16)
    rm = S.sbuf("rrm", [128, 384], F32)
    tabs = [S.sbuf("rtab", [128, 2, 2, 512], F32) for _ in range(2)]
    qrot = S.sbuf("qrot", [128, 2, 512], BF16); krot = S.sbuf("krot", [128, 2, 512], BF16)
    vT = S.sbuf("rvT", [128, 4, 512], BF16); sgT = S.sbuf("rsgT", [128, 4, 512], BF16)
    vtm = S.sbuf("rvtm", [128, 4, 512], BF16); ktm = S.sbuf("rktm", [128, 4, 256], BF16)
    og = S.sbuf("rog", [128, 4, 512], BF16)
    S32 = S.sbuf("rS32", [128, 2, 512], F32); Sb = S.sbuf("rSb", [128, 2, 512], BF16)
    t1 = S.sbuf("rt1", [128, 512], F32); t2 = S.sbuf("rt2", [128, 512], F32)
    ST = S.sbuf("rST", [128, 128], BF16); qcd = S.sbuf("rqcd", [128, 2, 128], BF16)
    osbs = [S.sbuf("rosb", [128, 512], F32) for _ in range(2)]; osqs = [S.sbuf("rosq", [128, 512], F32) for _ in range(2)]
    means = [S.sbuf("rmean", [128, 128], F32) for _ in range(2)]; msqs = [S.sbuf("rmsq", [128, 128], F32) for _ in range(2)]; rstds = [S.sbuf("rrstd", [128, 128], F32) for _ in range(2)]
    nchunk = 0
    pA = S.psum("rpA", [128, 512]); pB = S.psum("rpB", [128, 512])
    opss = [S.psum("rops", [128, 512]) for _ in range(2)]
    ups = [S.psum("rups", [128, 512]) for _ in range(2)]
    pT = S.psum("rpT", [128, 1024], BF16)
    pst = S.psum("rpst", [128, 512])
    sps = pst
    ident = g.identb
    ones = g.ones32
    ntab = 0
    for h in range(4):
        S.dma("pool", wq[:], Win[:, :, h * 256:(h + 1) * 256], writes=[wq])
        S.dma("pool", wqs[:], Wsw[:, :, h * 256:(h + 1) * 256], writes=[wqs])
        S.dma("pool", wk[:], Win[:, :, 1024 + h * 256:1024 + (h + 1) * 256], writes=[wk])
        S.dma("pool", wks[:], Wsw[:, :, 1024 + h * 256:1024 + (h + 1) * 256], writes=[wks])
        S.dma("pool", wv[:], Win[:, :, 2048 + h * 512:2048 + (h + 1) * 512], writes=[wv])
        S.dma("pool", wg[:], Win[:, :, 4096 + h * 512:4096 + (h + 1) * 512], writes=[wg])
        S.dma("pool", wo[:], Wo[:, h * 4:(h + 1) * 4, :], writes=[wo])
        S.dma("sp", rm[:], g.rmask_d[h], writes=[rm])
        S.i("pool", "memset", [], [S32], ap=S32[:], constant=0.0)
        S.i("pool", "memset", [], [Sb], ap=Sb[:], constant=0.0)
        for tt in range(4):
            ts = slice(tt * 512, (tt + 1) * 512)
            tab = tabs[ntab % 2]; ntab += 1
            for cs_ in range(2):
                S.dma("sp", tab[:, cs_, :, :], g.rtab_d[cs_][:, :, ts], writes=[R(tab, None)])
            for (wa, wb, dst) in ((wq, wqs, qrot), (wk, wks, krot)):
                for dc in range(2):
                    for k in range(NCH):
                        S.i("pe", "matmul", [wa, R(xb, (k, tt))], [pA], out=pA[:], lhsT=wa[:, k, dc * 128:(dc + 1) * 128], rhs=xb[:, k, ts],
                            start=(k == 0), stop=(k == NCH - 1))
                    for k in range(NCH):
                        S.i("pe", "matmul", [wb, R(xb, (k, tt))], [pB], out=pB[:], lhsT=wb[:, k, dc * 128:(dc + 1) * 128], rhs=xb[:, k, ts],
                            start=(k == 0), stop=(k == NCH - 1))
                    S.i("dve", "tensor_tensor", [pA, tab], [t1], out=t1[:], in0=pA[:], in1=tab[:, 0, dc, :], op=ALU.mult)
                    S.i("dve", "tensor_tensor", [pB, tab], [t2], out=t2[:], in0=pB[:], in1=tab[:, 1, dc, :], op=ALU.mult)
                    S.i("pool", "tensor_tensor", [t1, t2], [R(dst, dc)], out=dst[:, dc, :], in0=t1[:], in1=t2[:], op=ALU.add)
            for ec in range(4):
                for k in range(NCH):
                    S.i("pe", "matmul", [wv, R(xb, (k, tt))], [pA], out=pA[:], lhsT=wv[:, k, ec * 128:(ec + 1) * 128], rhs=xb[:, k, ts],
                        start=(k == 0), stop=(k == NCH - 1))
                S.i("act", "activation", [pA], [R(vT, ec)], out=vT[:, ec, :], in_=pA[:], func=AF.Copy)
                for k in range(NCH):
                    S.i("pe", "matmul", [wg, R(xb, (k, tt))], [pB], out=pB[:], lhsT=wg[:, k, ec * 128:(ec + 1) * 128], rhs=xb[:, k, ts],
                        start=(k == 0), stop=(k == NCH - 1))
                S.i("act", "activation", [pB], [R(sgT, ec)], out=sgT[:, ec, :], in_=pB[:], func=AF.Silu)
            for n in range(4):
                for ec in range(4):
                    idx = (n % 2) * 4 + ec
                    S.i("pe", "transpose", [vT, g.cb], [pT], out=pT[:, idx * 128:(idx + 1) * 128], in_=vT[:, ec, n * 128:(n + 1) * 128], identity=ident)
                if n % 2 == 1:
                    S.i("dve", "tensor_copy", [pT], [R(vtm, n // 2)], out=vtm[:, n - 1:n + 1, :].rearrange("p a b -> p (a b)"), in_=pT[:])
            for n in range(4):
                for dc in range(2):
                    idx = n * 2 + dc
                    S.i("pe", "transpose", [krot, g.cb], [pT], out=pT[:, idx * 128:(idx + 1) * 128], in_=krot[:, dc, n * 128:(n + 1) * 128], identity=ident)
            S.i("dve", "tensor_scalar", [pT, rm], [ktm], out=ktm[:].rearrange("p a b -> p (a b)"), in0=pT[:], scalar1=rm[:, 256:257], scalar2=None, op0=ALU.mult)
            def partA(n):
                cs = slice(n * 128, (n + 1) * 128)
                opsn = opss[n % 2]
                for dc in range(2):
                    S.i("pe", "matmul", [krot, qrot], [sps], out=sps[:, 256:384], lhsT=krot[:, dc, cs], rhs=qrot[:, dc, cs], start=(dc == 0), stop=(dc == 1))
                S.i("dve", "tensor_tensor", [sps, rm], [ST], out=ST[:], in0=sps[:, 256:384], in1=rm[:, 0:128], op=ALU.mult)
                S.i("pool", "tensor_tensor", [qrot, rm], [qcd], out=qcd[:], in0=qrot[:, :, cs], in1=rm[:, 128:256].unsqueeze(1).to_broadcast([128, 2, 128]), op=ALU.mult)
                for ec in range(4):
                    es = slice(ec * 128, (ec + 1) * 128)
                    S.i("pe", "matmul", [vtm, ST], [opsn], out=opsn[:, es], lhsT=vtm[:, n, es], rhs=ST[:], start=True, stop=False)
                    for dc in range(2):
                        S.i("pe", "matmul", [Sb, qcd], [opsn], out=opsn[:, es], lhsT=Sb[:, dc, es], rhs=qcd[:, dc, :], start=False, stop=(dc == 1))
                for dc in range(2):
                    S.i("pe", "matmul", [ktm, vtm], [ups[dc]], out=ups[dc][:], lhsT=ktm[:, n, dc * 128:(dc + 1) * 128], rhs=vtm[:, n, :], start=True, stop=True)
                    S.i("dve", "scalar_tensor_tensor", [S32, ups[dc], rm], [R(S32, dc)], out=S32[:, dc, :], in0=S32[:, dc, :], scalar=rm[:, 257:258], in1=ups[dc][:],
                        op0=ALU.mult, op1=ALU.add)
                    S.i("act", "activation", [R(S32, dc)], [R(Sb, dc)], out=Sb[:, dc, :], in_=S32[:, dc, :], func=AF.Copy)

            def partB(n):
                cs = slice(n * 128, (n + 1) * 128)
                opsn = opss[n % 2]
                osb, osq, mean, msq, rstd = osbs[n % 2], osqs[n % 2], means[n % 2], msqs[n % 2], rstds[n % 2]
                S.i("act", "activation", [opsn], [osb], out=osb[:], in_=opsn[:], func=AF.Copy)
                S.i("act", "activation", [opsn], [osq], out=osq[:], in_=opsn[:], func=AF.Square)
                for ec in range(4):
                    S.i("pe", "matmul", [osb, g.cf], [R(pst, 0)], out=pst[:, 0:128], lhsT=ones, rhs=osb[:, ec * 128:(ec + 1) * 128], start=(ec == 0), stop=(ec == 3))
                for ec in range(4):
                    S.i("pe", "matmul", [osq, g.cf], [R(pst, 1)], out=pst[:, 128:256], lhsT=ones, rhs=osq[:, ec * 128:(ec + 1) * 128], start=(ec == 0), stop=(ec == 3))
                S.i("dve", "tensor_scalar", [R(pst, 0)], [mean], out=mean[:], in0=pst[:, 0:128], scalar1=1.0 / 512, scalar2=None, op0=ALU.mult)
                S.i("dve", "tensor_tensor", [mean], [msq], out=msq[:], in0=mean[:], in1=mean[:], op=ALU.mult)
                S.i("dve", "scalar_tensor_tensor", [R(pst, 1), msq], [msq], out=msq[:], in0=pst[:, 128:256], scalar=1.0 / 512, in1=msq[:], op0=ALU.mult, op1=ALU.subtract)
                S.i("dve", "tensor_scalar", [msq], [msq], out=msq[:], in0=msq[:], scalar1=1e-5, scalar2=None, op0=ALU.add)
                S.i("act", "activation", [msq], [rstd], out=rstd[:], in_=msq[:], func=AF.Ln)
                S.i("act", "activation", [rstd], [rstd], out=rstd[:], in_=rstd[:], func=AF.Exp, scale=-0.5)
                mb_ = mean[:].unsqueeze(1).to_broadcast([128, 4, 128])
                rb_ = rstd[:].unsqueeze(1).to_broadcast([128, 4, 128])
                o3 = osb[:].rearrange("p (a b) -> p a b", b=128)
                S.i("dve", "tensor_tensor", [osb, mean], [osb], out=o3, in0=o3, in1=mb_, op=ALU.subtract)
                S.i("dve", "tensor_tensor", [osb, rstd], [osb], out=o3, in0=o3, in1=rb_, op=ALU.mult)
                for ec in range(4):
                    col = h * 4 + ec
                    S.i("act", "activation", [osb, g.vecs], [R(osq, ec)], out=osq[:, ec * 128:(ec + 1) * 128], in_=osb[:, ec * 128:(ec + 1) * 128], func=AF.Identity,
                        bias=vcol(g, "ret_gn_b", col), scale=vcol(g, "ret_gn_g", col))
                S.i("dve", "tensor_tensor", [osq, sgT], [R(og, n)], out=og[:, :, cs], in0=osq[:].rearrange("p (a b) -> p a b", b=128), in1=sgT[:, :, cs], op=ALU.mult)

            partA(0)
            for n in range(4):
                if n + 1 < 4:
                    partA(n + 1)
                partB(n)
            for m in range(NCH):
                pw = pA if m % 2 == 0 else pB
                for k in range(4):
                    S.i("pe", "matmul", [wo, og], [pw], out=pw[:], lhsT=wo[:, k, m * 128:(m + 1) * 128], rhs=og[:, k, :], start=(k == 0), stop=(k == 3))
                xs = x[:, m, ts]
                if h == 0:
                    S.i("dve", "scalar_tensor_tensor", [pw, R(x, (m, tt))], [R(x, (m, tt))], out=xs, in0=xs, scalar=ALPHA, in1=pw[:], op0=ALU.mult, op1=ALU.add)
                else:
                    S.i("dve", "tensor_tensor", [pw, R(x, (m, tt))], [R(x, (m, tt))], out=xs, in0=xs, in1=pw[:], op=ALU.add)
    S.pop()
    S.push()
    layer_norm(g, ("ln1_g%d" % i, "ln1_b%d" % i), range(4))
    S.pop()

def rwkv_phase(g, i, j):
    S = g.S
    x, xb = g.x, g.xb
    S.push()
    TW = 256
    NT_ = T_SEQ // TW
    W = g.W
    Wrkv = [W["rwkv_w_rkv"][j, w].rearrange("(k p) n -> p k n", p=128) for w in range(3)]
    Wo = W["rwkv_w_o"][j].rearrange("(k p) n -> p k n", p=128)
    V = lambda name, c=0: vcol(g, "%s_%d" % (name, j), c)
    w1 = S.sbuf("w1", [128, NCH, 64], BF16); a1 = S.sbuf("a1", [128, NCH, 64], BF16); g1 = S.sbuf("g1", [128, NCH, 160], BF16)
    w2 = S.sbuf("w2", [64, 1024], BF16); a2 = S.sbuf("a2", [64, 1024], BF16); g2 = S.sbuf("g2", [128, 2, 1024], BF16)
    S.dma("pool", w1[:], W["rwkv_w1"][j].rearrange("(k p) n -> p k n", p=128), writes=[w1])
    S.dma("pool", a1[:], W["rwkv_a1"][j].rearrange("(k p) n -> p k n", p=128), writes=[a1])
    S.dma("pool", g1[:], W["rwkv_g1"][j].rearrange("(k p) n -> p k n", p=128), writes=[g1])
    S.dma("pool", w2[:], W["rwkv_w2"][j], writes=[w2])
    S.dma("pool", a2[:], W["rwkv_a2"][j], writes=[a2])
    S.dma("pool", g2[:, 0, :], W["rwkv_g2"][j][0:128, :], writes=[R(g2, None)])
    S.dma("pool", g2[0:32, 1, :], W["rwkv_g2"][j][128:160, :], writes=[R(g2, None)])
    rows = S.sbuf("gnrows", [128, 2, 1024], BF16)
    S.dma("pool", rows[:, 0, :], g.rows_d[2 * j], writes=[R(rows, None)])
    S.dma("pool", rows[:, 1, :], g.rows_d[2 * j + 1], writes=[R(rows, None)])
    om = S.sbuf("om", [128, 72], F32)
    mo = g.voff["mix0_%d" % j]
    S.i("dve", "tensor_scalar", [g.vecs], [om], out=om[:, 0:48], in0=g.vecs[:, mo:mo + 48], scalar1=-1.0, scalar2=1.0, op0=ALU.mult, op1=ALU.add)
    ko = g.voff["k_a_%d" % j]
    S.i("dve", "tensor_scalar", [g.vecs], [om], out=om[:, 48:56], in0=g.vecs[:, ko:ko + 8], scalar1=-1.0, scalar2=1.0, op0=ALU.mult, op1=ALU.add)
    ao = g.voff["a0_%d" % j]; wo_ = g.voff["w0_%d" % j]
    S.i("dve", "tensor_scalar", [g.vecs], [om], out=om[:, 56:64], in0=g.vecs[:, ao:ao + 8], scalar1=0.5, scalar2=None, op0=ALU.mult)
    S.i("dve", "tensor_scalar", [g.vecs], [om], out=om[:, 64:72], in0=g.vecs[:, wo_:wo_ + 8], scalar1=0.5, scalar2=None, op0=ALU.mult)
    xx = S.sbuf("xx", [128, NCH, TW], F32)
    xlast = S.sbuf("xlast", [128, NCH, 2], F32)
    S.i("pool", "memset", [], [xlast], ap=xlast[:], constant=0.0)
    hw = S.sbuf("hw", [64, TW], BF16); ha = S.sbuf("ha", [64, TW], BF16); sg = S.sbuf("sg", [128, 2, TW], BF16); sgf = S.sbuf("sgf", [128, 2, TW], BF16)
    NSL = _DBG.get('rw_nsl', 4)
    wsl = [S.sbuf("rwsl", [128, NCH, 128], BF16) for _ in range(NSL)]
    wseq = []
    for tix_ in range(NT_):
        for c_ in range(NCH):
            wseq.append(Wrkv[0][:, :, c_ * 128:(c_ + 1) * 128]); wseq.append(Wrkv[1][:, :, c_ * 128:(c_ + 1) * 128])
        for c_ in range(NCH):
            wseq.append(Wrkv[2][:, :, c_ * 128:(c_ + 1) * 128])
        for m_ in range(NCH):
            wseq.append(Wo[:, :, m_ * 128:(m_ + 1) * 128])
    wst = {"issued": 0, "next": 0}
    def wget():
        i_ = wst["next"]; wst["next"] += 1
        while wst["issued"] < min(len(wseq), i_ + NSL - 1):
            q_ = wst["issued"]
            S.dma("pool", wsl[q_ % NSL][:], wseq[q_], writes=[wsl[q_ % NSL]])
            wst["issued"] += 1
        return wsl[i_ % NSL]
    TS = [dict(), dict()]
    for nm in ("tA", "tsw", "tkk", "tcs", "te1", "te2", "te3"):
        for q_ in range(2):
            TS[q_][nm] = S.sbuf(nm, [128, TW], F32)
    for nm in ("trn", "tt"):
        t_ = S.sbuf(nm, [128, TW], F32)
        TS[0][nm] = t_; TS[1][nm] = t_
    for q_ in range(2):
        TS[q_]["tkk2"] = S.sbuf("tkk2", [128, TW], BF16)
    PC = S.sbuf("PC", [128, NCH, 2], F32)
    def FM(a, c, sl):
        return xb[:, c, a * TW + sl.start:a * TW + sl.stop]
    FMK = lambda a, c: R(xb, ("fm", a, c))
    Lk = [[S.sbuf("Lk", [128, 4, 128], BF16) for _ in range(2)] for _ in range(2)]
    Mk = [[S.sbuf("Mk", [128, 4, 128], BF16) for _ in range(2)] for _ in range(2)]
    NTt = [[S.sbuf("NT", [128, 4, 128], BF16) for _ in range(2)] for _ in range(2)]
    Mak = [S.sbuf("Mak", [128, 4, 128], BF16) for _ in range(2)]
    Mrb = S.sbuf("Mrb", [128, 16, 128], BF16); Mrk = S.sbuf("Mrk", [128, 16, 128], BF16)
    Atm = S.sbuf("Atm", [128, 1024], BF16); Btm = S.sbuf("Btm", [128, 1024], BF16); Ktm = S.sbuf("Ktm", [128, 1024], BF16); Vtm = S.sbuf("Vtm", [128, 1024], BF16)
    AhT = S.sbuf("AhT", [128, NCH, 128], BF16); Xb = S.sbuf("Xb", [128, 256], BF16); Vhat = S.sbuf("Vhat", [128, 1024], BF16)
    Ub = S.sbuf("Ub", [128, 1024], BF16); ST = S.sbuf("ST", [128, NCH, 64], BF16); STs = S.sbuf("STs", [128, NCH, 64], F32)
    yn = S.sbuf("yn", [128, 1024], F32); st4 = S.sbuf("st4", [128, 4, 16], F32); bsum = S.sbuf("bsum", [128, 16], F32)
    ogtm = S.sbuf("ogtm", [128, 1024], BF16); ogT = S.sbuf("ogT", [128, NCH, TW], BF16)
    PB = [S.psum("rp", [128, 512]) for _ in range(7)]
    pT = S.psum("rpT", [128, 1024], BF16)
    ident = g.identb
    blk1 = g.cb[:, 128:256]
    hind = g.cb[:, 256:258]
    ones = g.ones32
    S.i("pool", "memset", [], [ST], ap=ST[:], constant=0.0)
    nws = 0
    stg = _DBG.get('rw_stage', 99)
    for tix in range(_DBG.get('rw_tiles', NT_)):
        t0 = tix * TW
        tt = t0 // 512
        S.i("dve", "tensor_tensor", [R(x, None)], [R(xx, "m")], out=xx[:, :, 1:TW], in0=x[:, :, t0:t0 + TW - 1], in1=x[:, :, t0 + 1:t0 + TW], op=ALU.subtract)
        S.i("dve", "tensor_tensor", [R(x, None), xlast], [R(xx, "0")], out=xx[:, :, 0:1], in0=xlast[:, :, tix % 2:tix % 2 + 1], in1=x[:, :, t0:t0 + 1], op=ALU.subtract)
        S.i("dve", "tensor_copy", [R(x, None)], [xlast], out=xlast[:, :, (tix + 1) % 2:(tix + 1) % 2 + 1], in_=x[:, :, t0 + TW - 1:t0 + TW])
        def mix(m, dst):
            for c in range(NCH):
                mc = vcol(g, "mix%d_%d" % (m, j), c)
                S.i("dve", "scalar_tensor_tensor", [R(x, (c, tt)), xx, g.vecs], [dst.key(c)], out=dst.ap(c), in0=xx[:, c, :], scalar=mc, in1=x[:, c, t0:t0 + TW],
                    op0=ALU.mult, op1=ALU.add)
        nx = [0]
        class XM:
            def __init__(self, slot):
                self.slot = slot
            def ap(self, c):
                return xb[:, c, 1536 + self.slot * TW:1536 + (self.slot + 1) * TW]
            def key(self, c):
                return R(xb, ("xm", self.slot, c))
        def nxm():
            nx[0] += 1
            return XM(nx[0] % 2)
        d_ = nxm(); mix(1, d_)
        for k in range(NCH):
            S.i("pe", "matmul", [w1, d_.key(k)], [PB[0]], out=PB[0][0:64, 0:TW], lhsT=w1[:, k, :], rhs=d_.ap(k), start=(k == 0), stop=(k == NCH - 1))
        S.i("act", "activation", [PB[0]], [hw], out=hw[:], in_=PB[0][0:64, 0:TW], func=AF.Tanh)
        d_ = nxm(); mix(4, d_)
        for k in range(NCH):
            S.i("pe", "matmul", [a1, d_.key(k)], [PB[1]], out=PB[1][0:64, 0:TW], lhsT=a1[:, k, :], rhs=d_.ap(k), start=(k == 0), stop=(k == NCH - 1))
        S.i("act", "activation", [PB[1]], [ha], out=ha[:], in_=PB[1][0:64, 0:TW], func=AF.Copy)
        d_ = nxm(); mix(5, d_)
        for (lo, hi, kc) in ((0, 128, 0), (128, 160, 1)):
            for k in range(NCH):
                S.i("pe", "matmul", [g1, d_.key(k)], [PB[2]], out=PB[2][0:hi - lo, 0:TW], lhsT=g1[:, k, lo:hi], rhs=d_.ap(k), start=(k == 0), stop=(k == NCH - 1))
            S.i("act", "activation", [PB[2]], [R(sgf, kc)], out=sgf[0:hi - lo, kc, :], in_=PB[2][0:hi - lo, 0:TW], func=AF.Tanh, scale=0.5)
            S.i("dve", "tensor_scalar", [R(sgf, kc)], [R(sg, kc)], out=sg[0:hi - lo, kc, :], in0=sgf[0:hi - lo, kc, :], scalar1=0.5, scalar2=0.5, op0=ALU.mult, op1=ALU.add)
        xr = nxm(); mix(0, xr)
        xk = nxm(); mix(2, xk)
        if stg < 3:
            continue
        for c in range(NCH):
            wr = wget()
            wk_ = wget()
            d = TS[c % 2]
            tA, tsw, tcs, te1, te2, te3, tkk, tkk2, trn, tt_ = (d[k_] for k_ in ("tA", "tsw", "tcs", "te1", "te2", "te3", "tkk", "tkk2", "trn", "tt"))
            td3 = tsw; tkkn = tkk; ttb = tkk; tkm = tt_
            if c % 2 == 0:
                rp, kp, zw, za, ssp = PB[0], PB[1], PB[2], PB[3], PB[4]
            else:
                rp, kp, zw, za, ssp = PB[5], PB[6], PB[2], PB[3], PB[4]
            for k in range(NCH):
                S.i("pe", "matmul", [wr, xr.key(k)], [rp], out=rp[:, 0:TW], lhsT=wr[:, k, :], rhs=xr.ap(k), start=(k == 0), stop=(k == NCH - 1))
            for k in range(NCH):
                S.i("pe", "matmul", [wk_, xk.key(k)], [kp], out=kp[:, 0:TW], lhsT=wk_[:, k, :], rhs=xk.ap(k), start=(k == 0), stop=(k == NCH - 1))
            S.i("pe", "matmul", [w2, hw], [zw], out=zw[:, 0:TW], lhsT=w2[:, c * 128:(c + 1) * 128], rhs=hw[:], start=True, stop=True)
            S.i("pe", "matmul", [a2, ha], [za], out=za[:, 0:TW], lhsT=a2[:, c * 128:(c + 1) * 128], rhs=ha[:], start=True, stop=True)
            S.i("act", "activation", [za, om], [tA], out=tA[:], in_=za[:, 0:TW], func=AF.Tanh, bias=om[:, 56 + c:57 + c], scale=0.5)
            S.i("act", "activation", [zw, om], [tsw], out=tsw[:], in_=zw[:, 0:TW], func=AF.Tanh, bias=om[:, 64 + c:65 + c], scale=0.5)
            S.i("dve", "tensor_scalar", [tA], [tA], out=tA[:], in0=tA[:], scalar1=0.5, scalar2=0.5, op0=ALU.mult, op1=ALU.add)
            S.i("dve", "tensor_scalar", [tsw], [tsw], out=tsw[:], in0=tsw[:], scalar1=0.5, scalar2=0.5, op0=ALU.mult, op1=ALU.add)
            for n in range(2):
                cs = slice(n * 128, (n + 1) * 128)
                S.i("dve", "tensor_tensor_scan", [tsw, g.cf], [R(tcs, n)], out=tcs[:, cs], data0=ones, data1=tsw[:, cs], initial=0.0, op0=ALU.mult, op1=ALU.add)
            S.i("dve", "tensor_tensor", [tcs, tsw], [td3], out=td3[:], in0=tcs[:], in1=tsw[:], op=ALU.subtract)
            LD = 0.6065306597126334
            S.i("act", "activation", [tcs], [te1], out=te1[:], in_=tcs[:], func=AF.Exp, scale=-LD)
            S.i("act", "activation", [tcs], [te2], out=te2[:], in_=tcs[:], func=AF.Exp, scale=LD)
            S.i("act", "activation", [td3], [te3], out=te3[:], in_=td3[:], func=AF.Exp, scale=-LD)
            S.i("act", "activation", [te1], [R(PC, c)], out=PC[:, c, :], in_=te1[:, 127:TW:128], func=AF.Copy)
            S.i("dve", "tensor_scalar", [kp, g.vecs], [tkk], out=tkk[:], in0=kp[:, 0:TW], scalar1=V("k_k", c), scalar2=None, op0=ALU.mult)
            S.i("act", "activation", [tkk], [tkk2], out=tkk2[:], in_=tkk[:], func=AF.Square)
            S.i("pe", "matmul", [tkk2, g.cb], [ssp], out=ssp[:, 0:TW], lhsT=blk1, rhs=tkk2[:], start=True, stop=True)
            S.i("dve", "tensor_scalar", [ssp], [trn], out=trn[:], in0=ssp[:, 0:TW], scalar1=1e-24, scalar2=None, op0=ALU.max)
            S.i("act", "activation", [trn], [trn], out=trn[:], in_=trn[:], func=AF.Ln)
            S.i("act", "activation", [trn], [trn], out=trn[:], in_=trn[:], func=AF.Exp, scale=-0.5)
            S.i("dve", "tensor_tensor", [tkk, trn], [tkkn], out=tkkn[:], in0=tkk[:], in1=trn[:], op=ALU.mult)
            S.i("dve", "tensor_scalar", [tA, g.vecs, om], [tt_], out=tt_[:], in0=tA[:], scalar1=V("k_a", c), scalar2=om[:, 48 + c:49 + c], op0=ALU.mult, op1=ALU.add)
            S.i("dve", "tensor_tensor", [kp, tt_], [tkm], out=tkm[:], in0=kp[:, 0:TW], in1=tt_[:], op=ALU.mult)
            full = slice(0, TW)
            S.i("dve", "scalar_tensor_tensor", [tkkn, te3], [FMK(0, c)], out=FM(0, c, full), in0=tkkn[:], scalar=-1.0, in1=te3[:], op0=ALU.mult, op1=ALU.mult)
            S.i("dve", "tensor_tensor", [tkkn, tA], [ttb], out=ttb[:], in0=tkkn[:], in1=tA[:], op=ALU.mult)
            S.i("dve", "tensor_tensor", [ttb, te2], [FMK(1, c)], out=FM(1, c, full), in0=ttb[:], in1=te2[:], op=ALU.mult)
            S.i("dve", "tensor_tensor", [tkm, te2], [FMK(2, c)], out=FM(2, c, full), in0=tkm[:], in1=te2[:], op=ALU.mult)
            S.i("dve", "tensor_tensor", [rp, te1], [FMK(3, c)], out=FM(3, c, full), in0=rp[:, 0:TW], in1=te1[:], op=ALU.mult)
            S.i("dve", "scalar_tensor_tensor", [rp, g.vecs, tkm], [FMK(4, c)], out=FM(4, c, full), in0=rp[:, 0:TW], scalar=V("r_k", c), in1=tkm[:], op0=ALU.mult, op1=ALU.mult)
        xv = nxm(); mix(3, xv)
        for c in range(NCH):
            wv_ = wget()
            vp = PB[5 + (c % 2)]
            for k in range(NCH):
                S.i("pe", "matmul", [wv_, xv.key(k)], [vp], out=vp[:, 0:TW], lhsT=wv_[:, k, :], rhs=xv.ap(k), start=(k == 0), stop=(k == NCH - 1))
            S.i("act", "activation", [vp], [FMK(5, c)], out=FM(5, c, slice(0, TW)), in_=vp[:, 0:TW], func=AF.Copy)
        if stg < 5:
            continue
        for n in range(2):
            cs = slice(n * 128, (n + 1) * 128)
            for a_, dst in ((0, Atm), (1, Btm), (2, Ktm), (5, Vtm)):
                for c in range(NCH):
                    S.i("pe", "transpose", [FMK(a_, c), g.cb], [pT], out=pT[:, c * 128:(c + 1) * 128], in_=FM(a_, c, cs), identity=ident)
                S.i("dve" if a_ in (0, 2) else "act", "tensor_copy" if a_ in (0, 2) else "activation", [pT], [dst],
                    **(dict(out=dst[:], in_=pT[:]) if a_ in (0, 2) else dict(out=dst[:], in_=pT[:], func=AF.Copy)))
            if stg < 6:
                continue
            for hgp in range(2):
                grp = [2 * hgp, 2 * hgp + 1]
                def HP(h):
                    return slice((h % 2) * 64, (h % 2) * 64 + 64), h // 2
                for gi, hg in enumerate(grp):
                    heads = [4 * hg + q for q in range(4)]
                    specs = [(1, 0, 0, Mk[gi][0], None), (0, 1, 2, Lk[gi][0], None), (2, 0, 0, Mak[gi], None), (1, 3, 1, Mrb, hg), (2, 3, 1, Mrk, hg)]
                    for si, (la, ra, mki, dst, full16) in enumerate(specs):
                        dview = dst[:] if full16 is None else dst[:, hg * 4:(hg + 1) * 4, :]
                        wkey = dst if full16 is None else R(dst, hg)
                        for par in range(2):
                            bk = PB[(2 * si + par) % 6]
                            for qq in range(2):
                                h = heads[2 * qq + par]
                                ps_, c = HP(h)
                                S.i("pe", "matmul", [FMK(la, c), FMK(ra, c)], [bk], out=bk[:, qq * 128:(qq + 1) * 128], lhsT=FM(la, c, cs)[ps_, :], rhs=FM(ra, c, cs)[ps_, :],
                                    start=True, stop=True)
                            S.i("dve", "tensor_tensor", [bk, g.mk], [wkey], out=dview[:, par::2, :],
                                in0=bk[:, 0:256].rearrange("p (a b) -> p a b", b=128), in1=g.mk[:, mki, 0:256].rearrange("p (a b) -> p a b", b=128), op=ALU.mult)
                if stg < 7:
                    continue
                cur = [0, 0]
                NTin = [Mk[0][0], Mk[1][0]]
                for r in range(7):
                    for gi in range(2):
                        Lc, Mc = Lk[gi][cur[gi]], Mk[gi][cur[gi]]
                        Ln, Mn = Lk[gi][1 - cur[gi]], Mk[gi][1 - cur[gi]]
                        bN, bL, bM = PB[3 * gi], PB[3 * gi + 1], PB[3 * gi + 2]
                        if r >= 1:
                            NTo = NTt[gi][r % 2]
                            for q in range(4):
                                S.i("pe", "matmul", [NTin[gi], g.cb], [bN], out=bN[:, q * 128:(q + 1) * 128], lhsT=ident, rhs=NTin[gi][:, q, :], start=True, stop=False)
                                S.i("pe", "matmul", [Mc, g.cb], [bN], out=bN[:, q * 128:(q + 1) * 128], lhsT=ident, rhs=Mc[:, q, :], start=False, stop=False)
                                S.i("pe", "matmul", [Lc, NTin[gi]], [bN], out=bN[:, q * 128:(q + 1) * 128], lhsT=Lc[:, q, :], rhs=NTin[gi][:, q, :], start=False, stop=True)
                        if r < 6:
                            for q in range(4):
                                S.i("pe", "matmul", [Mc, Lc], [bL], out=bL[:, q * 128:(q + 1) * 128], lhsT=Mc[:, q, :], rhs=Lc[:, q, :], start=True, stop=True)
                            for q in range(4):
                                S.i("pe", "matmul", [Lc, Mc], [bM], out=bM[:, q * 128:(q + 1) * 128], lhsT=Lc[:, q, :], rhs=Mc[:, q, :], start=True, stop=True)
                        if r >= 1:
                            if gi == 0:
                                S.i("act", "activation", [bN], [NTo], out=NTo[:].rearrange("p a b -> p (a b)"), in_=bN[:], func=AF.Copy)
                            else:
                                S.i("dve", "tensor_copy", [bN], [NTo], out=NTo[:].rearrange("p a b -> p (a b)"), in_=bN[:])
                            NTin[gi] = NTo
                        if r < 6:
                            S.i("act", "activation", [bL], [Ln], out=Ln[:].rearrange("p a b -> p (a b)"), in_=bL[:], func=AF.Copy)
                            S.i("dve", "tensor_copy", [bM], [Mn], out=Mn[:].rearrange("p a b -> p (a b)"), in_=bM[:])
                            cur[gi] = 1 - cur[gi]
                if stg < 8:
                    continue
                for gi, hg in enumerate(grp):
                    heads = [4 * hg + q for q in range(4)]
                    NTf = NTin[gi]
                    for q, h in enumerate(heads):
                        S.i("pe", "matmul", [Mak[gi], Vtm], [PB[6]], out=PB[6][:, q * 64:(q + 1) * 64], lhsT=Mak[gi][:, q, :], rhs=Vtm[:, h * 64:(h + 1) * 64], start=True, stop=True)
                    S.i("act", "activation", [PB[6]], [Xb], out=Xb[:], in_=PB[6][:, 0:256], func=AF.Copy)
                    for q, h in enumerate(heads):
                        S.i("pe", "matmul", [Xb, g.cb], [PB[6]], out=PB[6][:, 256 + q * 64:256 + (q + 1) * 64], lhsT=ident, rhs=Xb[:, q * 64:(q + 1) * 64], start=True, stop=False)
                        S.i("pe", "matmul", [NTf, Xb], [PB[6]], out=PB[6][:, 256 + q * 64:256 + (q + 1) * 64], lhsT=NTf[:, q, :], rhs=Xb[:, q * 64:(q + 1) * 64], start=False, stop=True)
                    S.i("dve", "tensor_copy", [PB[6]], [R(Vhat, hg)], out=Vhat[:, hg * 256:(hg + 1) * 256], in_=PB[6][:, 256:512])
                    bA = PB[gi]
                    for q, h in enumerate(heads):
                        ps_, c = HP(h)
                        cc = (c % 2) * 128
                        S.i("pe", "matmul", [Atm, NTf], [bA], out=bA[ps_, cc:cc + 128], lhsT=Atm[:, h * 64:(h + 1) * 64], rhs=NTf[:, q, :], start=True, stop=True)
                    S.i("dve", "tensor_tensor", [bA, FMK(0, 2 * hg), FMK(0, 2 * hg + 1)], [R(AhT, hg)], out=AhT[:, 2 * hg:2 * hg + 2, :],
                        in0=bA[:, 0:256].rearrange("p (a b) -> p a b", b=128), in1=xb[:, 2 * hg:2 * hg + 2, cs.start:cs.stop], op=ALU.add)
            if stg < 9:
                continue
            Ubv = Ub[:].rearrange("p (h n) -> p h n", n=64)
            Vhv = Vhat[:].rearrange("p (h n) -> p h n", n=64)
            for par in range(2):
                bk = PB[par]
                for hh in range(8):
                    h = 2 * hh + par
                    ps_, c = slice(par * 64, par * 64 + 64), h // 2
                    S.i("pe", "matmul", [AhT, ST], [bk], out=bk[:, hh * 64:(hh + 1) * 64], lhsT=AhT[ps_, c, :], rhs=ST[ps_, c, :], start=True, stop=True)
            for par in range(2):
                S.i("dve", "tensor_tensor", [PB[par], Vhat], [R(Ub, par)], out=Ubv[:, par::2, :], in0=PB[par][:].rearrange("p (h n) -> p h n", n=64),
                    in1=Vhv[:, par::2, :], op=ALU.add)
            for c in range(NCH):
                S.i("act", "activation", [R(ST, c), PC], [R(STs, c)], out=STs[:, c, :], in_=ST[:, c, :], func=AF.Identity, scale=PC[:, c, n:n + 1])
            for par in range(2):
                bk = PB[2 + par]
                for hh in range(8):
                    h = 2 * hh + par
                    ps_, c = slice(par * 64, par * 64 + 64), h // 2
                    o_ = bk[:, hh * 64:(hh + 1) * 64]
                    S.i("pe", "matmul", [FMK(3, c), ST], [bk], out=o_, lhsT=FM(3, c, cs)[ps_, :], rhs=ST[ps_, c, :], start=True, stop=False)
                    S.i("pe", "matmul", [Mrk, Vtm], [bk], out=o_, lhsT=Mrk[:, h, :], rhs=Vtm[:, h * 64:(h + 1) * 64], start=False, stop=False)
                    S.i("pe", "matmul", [Mrb, Ub], [bk], out=o_, lhsT=Mrb[:, h, :], rhs=Ub[:, h * 64:(h + 1) * 64], start=False, stop=True)
            for h in range(16):
                ps_, c = slice((h % 2) * 64, (h % 2) * 64 + 64), h // 2
                o_ = PB[4][ps_, c * 64:(c + 1) * 64]
                S.i("pe", "matmul", [Ktm, Vtm], [PB[4]], out=o_, lhsT=Ktm[:, h * 64:(h + 1) * 64], rhs=Vtm[:, h * 64:(h + 1) * 64], start=True, stop=False)
                S.i("pe", "matmul", [Btm, Ub], [PB[4]], out=o_, lhsT=Btm[:, h * 64:(h + 1) * 64], rhs=Ub[:, h * 64:(h + 1) * 64], start=False, stop=True)
            for c in range(NCH):
                S.i("dve", "scalar_tensor_tensor", [PB[4], PC, R(STs, c)], [R(ST, c)], out=ST[:, c, :], in0=PB[4][:, c * 64:(c + 1) * 64], scalar=PC[:, c, n:n + 1],
                    in1=STs[:, c, :], op0=ALU.mult, op1=ALU.add)
            for b in range(2):
                S.i("dve", "tensor_reduce", [PB[2 + b]], [R(st4, ("s", b))], out=st4[:, 0, b::2], in_=PB[2 + b][:].rearrange("p (h n) -> p h n", n=64), axis=AX.X, op=ALU.add)
                S.i("act", "activation", [PB[2 + b]], [R(yn, None)], out=yn[:, b * 512:(b + 1) * 512], in_=PB[2 + b][:], func=AF.Square)
                S.i("dve", "tensor_reduce", [R(yn, None)], [R(st4, ("q", b))], out=st4[:, 1, b::2], in_=yn[:, b * 512:(b + 1) * 512].rearrange("p (h n) -> p h n", n=64), axis=AX.X, op=ALU.add)
            S.i("dve", "tensor_scalar", [st4], [R(st4, "m")], out=st4[:, 2, :], in0=st4[:, 0, :], scalar1=1.0 / 64, scalar2=None, op0=ALU.mult)
            S.i("dve", "tensor_tensor", [st4], [R(st4, "v")], out=st4[:, 3, :], in0=st4[:, 2, :], in1=st4[:, 2, :], op=ALU.mult)
            S.i("dve", "scalar_tensor_tensor", [st4], [R(st4, "v")], out=st4[:, 3, :], in0=st4[:, 1, :], scalar=1.0 / 64, in1=st4[:, 3, :], op0=ALU.mult, op1=ALU.subtract)
            S.i("dve", "tensor_scalar", [st4], [R(st4, "v")], out=st4[:, 3, :], in0=st4[:, 3, :], scalar1=64e-5, scalar2=None, op0=ALU.add)
            S.i("act", "activation", [st4], [R(st4, "v")], out=st4[:, 3, :], in_=st4[:, 3, :], func=AF.Ln)
            S.i("act", "activation", [st4], [R(st4, "v")], out=st4[:, 3, :], in_=st4[:, 3, :], func=AF.Exp, scale=-0.5)
            for c in range(NCH):
                S.i("pe", "matmul", [FMK(4, c), g.cb], [PB[5]], out=PB[5][:, 2 * c:2 * c + 2], lhsT=FM(4, c, cs), rhs=hind, start=True, stop=True)
            S.i("dve", "tensor_copy", [PB[5]], [bsum], out=bsum[:], in_=PB[5][:, 0:16])
            for h in range(16):
                b = h % 2
                hs = slice(h * 64, (h + 1) * 64)
                S.i("dve", "tensor_scalar", [PB[2 + b], st4], [R(yn, h)], out=yn[:, hs], in0=PB[2 + b][:, (h // 2) * 64:(h // 2 + 1) * 64],
                    scalar1=st4[:, 2, h:h + 1], scalar2=st4[:, 3, h:h + 1], op0=ALU.subtract, op1=ALU.mult)
            S.i("dve", "tensor_tensor", [yn, rows], [yn], out=yn[:], in0=yn[:], in1=rows[:, 0, :], op=ALU.mult)
            S.i("dve", "tensor_tensor", [yn, rows], [yn], out=yn[:], in0=yn[:], in1=rows[:, 1, :], op=ALU.add)
            for h in range(16):
                hs = slice(h * 64, (h + 1) * 64)
                S.i("dve", "scalar_tensor_tensor", [Vtm, bsum, yn], [R(yn, h)], out=yn[:, hs], in0=Vtm[:, hs], scalar=bsum[:, h:h + 1], in1=yn[:, hs], op0=ALU.mult, op1=ALU.add)
            for b in range(2):
                S.i("pe", "matmul", [sg, g2], [PB[b]], out=PB[b][:], lhsT=sg[:, 0, cs], rhs=g2[:, 0, b * 512:(b + 1) * 512], start=True, stop=False)
                S.i("pe", "matmul", [sg, g2], [PB[b]], out=PB[b][:], lhsT=sg[0:32, 1, cs], rhs=g2[0:32, 1, b * 512:(b + 1) * 512], start=False, stop=True)
                S.i("dve", "tensor_tensor", [yn, PB[b]], [R(ogtm, b)], out=ogtm[:, b * 512:(b + 1) * 512], in0=yn[:, b * 512:(b + 1) * 512], in1=PB[b][:], op=ALU.mult)
            for c in range(NCH):
                S.i("pe", "transpose", [ogtm, g.cb], [pT], out=pT[:, c * 128:(c + 1) * 128], in_=ogtm[:, c * 128:(c + 1) * 128], identity=ident)
            for c in range(NCH):
                S.i("act", "activation", [pT], [R(ogT, (c, n))], out=ogT[:, c, cs], in_=pT[:, c * 128:(c + 1) * 128], func=AF.Copy)
        if stg < 11:
            continue
        for m in range(NCH):
            wo = wget()
            bk = PB[5 + (m % 2)]
            for k in range(NCH):
                S.i("pe", "matmul", [wo, ogT], [bk], out=bk[:, 0:TW], lhsT=wo[:, k, :], rhs=ogT[:, k, :], start=(k == 0), stop=(k == NCH - 1))
            xs = x[:, m, t0:t0 + TW]
            S.i("dve", "scalar_tensor_tensor", [bk, R(x, (m, tt))], [R(x, (m, tt))], out=xs, in0=xs, scalar=ALPHA, in1=bk[:, 0:TW], op0=ALU.mult, op1=ALU.add)
    S.pop()
    S.push()
    layer_norm(g, ("ln1_g%d" % i, "ln1_b%d" % i), range(4))
    S.pop()

def make_consts():
    c = np.zeros((128, 1024), np.float32)
    c[:, 0:128] = 1.0
    c[:, 128:256] = np.eye(128, dtype=np.float32)
    bo = np.zeros((128, 128), np.float32); bo[:64, :64] = 1.0; bo[64:, 64:] = 1.0
    c[:, 256:384] = bo
    t = np.arange(128)
    c[:, 384:512] = np.where(t[None, :] <= t[:, None], 0.0, -30000.0)
    c[:, 512:640] = (t[:, None] < t[None, :]).astype(np.float32)
    c[:, 640:768] = (t[:, None] <= t[None, :]).astype(np.float32)
    c[:64, 768] = 1.0; c[64:, 769] = 1.0
    return c


def make_ret_tables():
    dk = 256
    inv = (1.0 / (np.float32(10000.0) ** np.linspace(0.0, 1.0, dk // 2, dtype=np.float32))).astype(np.float32)
    pos = np.arange(T_SEQ, dtype=np.float32)
    ang = (pos[:, None] * inv[None, :]).astype(np.float32)
    cos = np.cos(ang).astype(np.float32); sin = np.sin(ang).astype(np.float32)
    f = np.arange(dk)
    cosf = cos[:, f // 2].T
    sgn = np.where(f % 2 == 0, -1.0, 1.0).astype(np.float32)
    sinf = (sin[:, f // 2].T * sgn[:, None]).astype(np.float32)
    tab = np.stack([cosf.reshape(2, 128, T_SEQ).transpose(1, 0, 2), sinf.reshape(2, 128, T_SEQ).transpose(1, 0, 2)])
    m = np.zeros((4, 128, 384), np.float32)
    idx = np.arange(128, dtype=np.float64)
    for h in range(4):
        lg = np.log(1.0 - 2.0 ** (-5.0 - h))
        rel = idx[None, :] - idx[:, None]
        m[h, :, 0:128] = np.where(rel >= 0, np.exp(lg * np.maximum(rel, 0)), 0.0) / 16.0
        m[h, :, 128:256] = np.exp(lg * (idx + 1.0))[None, :]
        m[h, :, 256] = np.exp(lg * (127.0 - idx)) / 16.0
        m[h, :, 257] = np.exp(lg * 128.0)
    return np.ascontiguousarray(tab.astype(np.float32)), m


FULL_PLAN = [("rwkv", 0, 0), ("ffn", 0), ("ret", 1), ("ffn", 1), ("moba", 2), ("ffn", 2), ("rwkv", 3, 1), ("ffn", 3)]
_PLAN = FULL_PLAN
_NC_CACHE = {}


def host_inputs(inputs):
    shared = {k: np.ascontiguousarray(np.asarray(inputs[k], np.float32)) for k in WEIGHT_SHAPES}
    shared["vecs"] = pack_vecs(inputs)
    shared["consts"] = make_consts()
    rows = np.zeros((4, 128, 1024), np.float32)
    for j in range(2):
        rows[2 * j] = np.broadcast_to(np.asarray(inputs["rwkv_gn_g"][j], np.float32)[None, :], (128, 1024))
        rows[2 * j + 1] = np.broadcast_to(np.asarray(inputs["rwkv_gn_b"][j], np.float32)[None, :], (128, 1024))
    shared["rows"] = rows
    ti = np.arange(128)
    mk = np.zeros((128, 3, 512), np.float32)
    mk[:, 0, :] = np.tile((ti[:, None] < ti[None, :]).astype(np.float32), (1, 4))
    mk[:, 1, :] = np.tile((ti[:, None] <= ti[None, :]).astype(np.float32), (1, 4))
    mk[:, 2, :] = np.tile((ti[:, None] > ti[None, :]).astype(np.float32), (1, 4))
    shared["mk"] = mk
    tab, msk = make_ret_tables()
    shared["rtab"] = tab
    shared["rmask"] = msk
    wqk = np.asarray(inputs["ret_w_in"][0][:, :2048], np.float32)
    shared["ret_sw"] = np.ascontiguousarray(wqk.reshape(1024, 1024, 2)[:, :, ::-1].reshape(1024, 2048))
    return shared


def run_plan(plan, inputs, x_full, n_cores=8):
    key = repr(plan)
    if key not in _NC_CACHE:
        _NC_CACHE[key] = build_program(plan)
    nc = _NC_CACHE[key]
    shared = host_inputs(inputs)
    in_maps = []
    for b in range(n_cores):
        m = dict(shared)
        m["xT"] = np.ascontiguousarray(np.asarray(x_full[b], np.float32).T)
        in_maps.append(m)
    res = run_bass_kernel_spmd(nc, in_maps, core_ids=list(range(n_cores)))
    return np.stack([np.ascontiguousarray(r["outT"].T) for r in res.results]).astype(np.float32)


def kernel(**inputs):
    return run_plan(_PLAN, inputs, inputs["x"], 8)
```

```python
import numpy as np
import concourse.bass as bass
import concourse.mybir as mybir
from concourse.bass_utils import run_bass_kernel_spmd
from contextlib import ExitStack

_DBG = {}
F32 = mybir.dt.float32
BF16 = mybir.dt.bfloat16
AF = mybir.ActivationFunctionType
ALU = mybir.AluOpType
AX = mybir.AxisListType

ENGS = ("pe", "act", "dve", "pool", "sp")


class T:
    def __init__(self, S, t, name):
        self.S = S
        self.t = t
        self.name = name
        self.st = {}
        self.is_psum = False
        self.dsem = None
        self.dcnt = 0

    def __getitem__(self, idx):
        return self.t[idx]


class R:
    __slots__ = ("tile", "key")
    def __init__(self, tile, key=None):
        self.tile = tile
        self.key = key


class Sched:
    def __init__(self, nc):
        self.nc = nc
        self.es = ExitStack()
        self.ins = {e: [] for e in ENGS}
        self.cnt = {e: 0 for e in ENGS}
        self.seen = {e: {} for e in ENGS}
        self.sems = {}
        self.needed = {e: set() for e in ENGS}
        self.nsem = 0
        for e in ENGS:
            self.sems[("e", e)] = self.es.enter_context(nc.semaphore("sem_" + e))
        self.dma_pool = []
        self.out_dma = []
        self.pend = {}
        self.scopes = []

    def _es(self):
        return self.scopes[-1] if self.scopes else self.es

    def push(self):
        self.scopes.append(ExitStack())

    def pop(self):
        self.barrier()
        self.scopes.pop().close()

    def sbuf(self, name, shape, dt):
        self.uid = getattr(self, "uid", 0) + 1
        name = "%s_%d" % (name, self.uid)
        t = self._es().enter_context(self.nc.sbuf_tensor(name, list(shape), dt))
        return T(self, t, name)

    def psum(self, name, shape, dt=F32):
        self.uid = getattr(self, "uid", 0) + 1
        name = "%s_%d" % (name, self.uid)
        t = self._es().enter_context(self.nc.psum_tensor(name, list(shape), dt))
        tt = T(self, t, name)
        tt.is_psum = True
        return tt

    def new_dsem(self, name):
        self.nsem += 1
        key = ("d", self.nsem)
        self.sems[key] = self.es.enter_context(self.nc.semaphore("dsem%d_%s" % (self.nsem, name)))
        return key

    def _states(self, ref, create=True):
        tile, key = ref.tile, ref.key
        if key is None:
            if None not in tile.st:
                tile.st[None] = [None, {}]
            return list(tile.st.values())
        out = []
        if None in tile.st:
            out.append(tile.st[None])
        if key not in tile.st:
            tile.st[key] = [None, {}]
        out.append(tile.st[key])
        return out

    def _collect(self, eng, reads, writes):
        waits = {}
        def need(w, same_ok=False):
            if w is None:
                return
            k, v = w
            if same_ok and k == ("e", eng) and eng == "pe":
                return
            if waits.get(k, 0) < v:
                waits[k] = v
        for r in reads:
            for st in self._states(r):
                need(st[0])
                if r.tile.is_psum:
                    for rk, rv in st[1].items():
                        if rk != ("e", eng):
                            need((rk, rv))
        for w in writes:
            for st in self._states(w):
                need(st[0])
                for rk, rv in st[1].items():
                    if rk == ("e", eng) and eng == "pe":
                        continue
                    need((rk, rv))
        final = []
        for k, v in waits.items():
            if k == ("e", eng) and eng == "pe":
                continue
            if self.seen[eng].get(k, 0) < v:
                self.seen[eng][k] = v
                final.append((k, v))
                if k[0] == "e":
                    self.needed[k[1]].add(v)
        return final

    def _commit(self, tag, reads, writes):
        for r in reads:
            if r.key is None:
                for k, s in r.tile.st.items():
                    if s[1].get(tag[0], 0) < tag[1]:
                        s[1][tag[0]] = tag[1]
            else:
                s = r.tile.st[r.key]
                if s[1].get(tag[0], 0) < tag[1]:
                    s[1][tag[0]] = tag[1]
        for w in writes:
            if w.key is None:
                w.tile.st = {None: [tag, {}]}
            else:
                w.tile.st[w.key] = [tag, {}]

    def i(self, eng, meth, reads=(), writes=(), **kw):
        return self.op(eng, (meth, kw), reads, writes)

    def _norm(self, refs):
        out = []
        for r in refs:
            if not isinstance(r, R):
                r = R(r)
            if r.tile.is_psum and r.key is not None:
                r = R(r.tile)
            out.append(r)
        return out

    def op(self, eng, fn, reads=(), writes=()):
        reads = self._norm(reads)
        writes = self._norm(writes)
        waits = self._collect(eng, reads, writes)
        self.cnt[eng] += 1
        idx = self.cnt[eng]
        self.ins[eng].append([fn, waits, idx, None])
        self._commit((("e", eng), idx), reads, writes)
        return idx

    def dma(self, q, out_ap, in_ap, reads=(), writes=(), out_final=False, owner=None, **kw):
        reads = [r if isinstance(r, R) else R(r) for r in reads]
        writes = [w if isinstance(w, R) else R(w) for w in writes]
        waits = self._collect(q, reads, writes)
        if owner is None:
            owner = (writes[0] if writes else reads[0]).tile
        if owner.dsem is None:
            owner.dsem = self.new_dsem(owner.name)
        owner.dcnt += 16
        tag = (owner.dsem, owner.dcnt)
        self.cnt[q] += 1
        idx = self.cnt[q]
        self.ins[q].append([lambda e: e.dma_start(out=out_ap, in_=in_ap, **kw), waits, idx, tag])
        self._commit(tag, reads, writes)
        self.pend[tag[0]] = tag[1]
        if out_final:
            self.out_dma.append(tag)
        return tag

    def barrier(self):
        for f in ENGS:
            if self.cnt[f] and (self.ins[f][-1][0] is None or self.ins[f][-1][3] is not None):
                self.cnt[f] += 1
                self.ins[f].append([None, [], self.cnt[f], None])
        for e in ENGS:
            waits = []
            for f in ENGS:
                if f == e or self.cnt[f] == 0:
                    continue
                v = self.cnt[f]
                if self.seen[e].get(("e", f), 0) < v:
                    self.seen[e][("e", f)] = v
                    waits.append((("e", f), v))
                    self.needed[f].add(v)
            for k, v in self.pend.items():
                if self.seen[e].get(k, 0) < v:
                    self.seen[e][k] = v
                    waits.append((k, v))
            if waits:
                self.cnt[e] += 1
                self.ins[e].append([None, waits, self.cnt[e], None])

    def emit(self):
        nc = self.nc
        fwd = {}
        for k, v in self.out_dma:
            fwd[k] = max(fwd.get(k, 0), v)
        fw = list(fwd.items())
        self.cnt["sp"] += 1
        self.ins["sp"].append([None, fw, self.cnt["sp"], None])
        rank = {}
        for e in ENGS:
            s = sorted(self.needed[e])
            rank[e] = {v: i + 1 for i, v in enumerate(s)}
        engmap = {"pe": "tensor", "act": "scalar", "dve": "vector", "pool": "gpsimd", "sp": "sync"}
        with nc.Block() as block:
            for e in ENGS:
                lst = self.ins[e]
                if not lst:
                    continue
                def body(eng, e=e, lst=lst):
                    for fn, waits, idx, dtag in lst:
                        for k, v in waits:
                            if k[0] == "e":
                                eng.wait_ge(self.sems[k], rank[k[1]][v])
                            else:
                                eng.wait_ge(self.sems[k], v)
                        if fn is None:
                            if idx in self.needed[e]:
                                eng.nop().then_inc(self.sems[("e", e)], 1)
                            continue
                        ins = fn(eng) if callable(fn) else getattr(eng, fn[0])(**fn[1])
                        if dtag is not None:
                            ins.then_inc(self.sems[dtag[0]], 16)
                            if idx in self.needed[e]:
                                raise RuntimeError("dma instr needed as engine milestone")
                        elif idx in self.needed[e]:
                            ins.then_inc(self.sems[("e", e)], 1)
                getattr(block, engmap[e])(body)
        self.es.close()

D = 1024; T_SEQ = 2048; DEPTH = 4; FF = 2816; NCH = 8; NPAIR = 22
ALPHA = float((2 * DEPTH) ** 0.25)
LN_EPS = 1e-5

WEIGHT_SHAPES = {
    "rwkv_w_rkv": [2, 3, 1024, 1024], "rwkv_w1": [2, 1024, 64], "rwkv_w2": [2, 64, 1024],
    "rwkv_a1": [2, 1024, 64], "rwkv_a2": [2, 64, 1024], "rwkv_g1": [2, 1024, 160], "rwkv_g2": [2, 160, 1024],
    "rwkv_w_o": [2, 1024, 1024], "ret_w_in": [1, 1024, 6144], "ret_w_o": [1, 2048, 1024],
    "moba_w_qkv": [1, 1024, 3072], "moba_w_o": [1, 1024, 1024],
    "ffn_w_in": [4, 1024, 5632], "ffn_w_out": [4, 2816, 1024],
}


def _fm(v):
    v = np.asarray(v, np.float32).reshape(-1, 128)
    return np.ascontiguousarray(v.T)


def vec_layout():
    off = {}
    n = 0
    def add(name, cols):
        nonlocal n
        off[name] = n
        n += cols
    for i in range(4):
        for nm in ("ln1_g", "ln1_b", "ln2_g", "ln2_b"):
            add("%s%d" % (nm, i), 8)
        for k in range(3):
            add("cw%d_%d" % (k, i), 44)
        add("cb_%d" % i, 44)
    for j in range(2):
        for m in range(6):
            add("mix%d_%d" % (m, j), 8)
        for nm in ("w0", "a0", "k_k", "k_a", "r_k"):
            add("%s_%d" % (nm, j), 8)
    add("ret_gn_g", 16)
    add("ret_gn_b", 16)
    return off, n


def pack_vecs(inp):
    off, n = vec_layout()
    out = np.zeros((128, n), np.float32)
    def put(name, v):
        a = _fm(v)
        out[:, off[name]:off[name] + a.shape[1]] = a
    for i in range(4):
        for nm in ("ln1_g", "ln1_b", "ln2_g", "ln2_b"):
            put("%s%d" % (nm, i), inp[nm][i])
        for k in range(3):
            put("cw%d_%d" % (k, i), inp["ffn_conv_w"][i, k])
        put("cb_%d" % i, inp["ffn_conv_b"][i])
    for j in range(2):
        for m in range(6):
            put("mix%d_%d" % (m, j), inp["rwkv_mix"][j, m])
        put("w0_%d" % j, inp["rwkv_w0"][j]); put("a0_%d" % j, inp["rwkv_a0"][j])
        put("k_k_%d" % j, inp["rwkv_k_k"][j]); put("k_a_%d" % j, inp["rwkv_k_a"][j])
        put("r_k_%d" % j, inp["rwkv_r_k"][j].reshape(-1))
    put("ret_gn_g", inp["ret_gn_g"][0]); put("ret_gn_b", inp["ret_gn_b"][0])
    return out


class Ctx:
    pass


def build_program(plan, x_in_bf=True):
    nc = bass.Bass("TRN2", target_bir_lowering=False)
    S = Sched(nc)
    g = Ctx()
    g.nc, g.S = nc, S
    g.W = {k: nc.dram_tensor(k, shp, F32, kind="ExternalInput").ap() for k, shp in WEIGHT_SHAPES.items()}
    voff, nv = vec_layout()
    g.voff = voff
    xT_d = nc.dram_tensor("xT", [D, T_SEQ], F32, kind="ExternalInput").ap()
    vecs_d = nc.dram_tensor("vecs", [128, nv], F32, kind="ExternalInput").ap()
    consts_d = nc.dram_tensor("consts", [128, 1024], F32, kind="ExternalInput").ap()
    g.rows_d = nc.dram_tensor("rows", [4, 128, 1024], F32, kind="ExternalInput").ap()
    g.rtab_d = nc.dram_tensor("rtab", [2, 128, 2, T_SEQ], F32, kind="ExternalInput").ap()
    g.rmask_d = nc.dram_tensor("rmask", [4, 128, 384], F32, kind="ExternalInput").ap()
    g.ret_sw_d = nc.dram_tensor("ret_sw", [1024, 2048], F32, kind="ExternalInput").ap()
    g.mk_d = nc.dram_tensor("mk", [128, 3, 512], F32, kind="ExternalInput").ap()
    out_d = nc.dram_tensor("outT", [D, T_SEQ], F32, kind="ExternalOutput").ap()

    g.x = S.sbuf("x", [128, NCH, T_SEQ], F32)
    g.xb = S.sbuf("xb", [128, NCH, T_SEQ], BF16)
    g.vecs = S.sbuf("vecs", [128, nv], F32)
    g.cf = S.sbuf("cf", [128, 256], F32)
    g.cb = S.sbuf("cb", [128, 512], BF16)
    xv = xT_d.rearrange("(c p) t -> p c t", p=128)
    for c in range(NCH):
        S.dma("sp", g.x[:, c, :], xv[:, c, :], writes=[R(g.x, None)])
    if plan[0][0] != "rwkv":
        for c in range(NCH):
            S.dma("pool", g.xb[:, c, :], xv[:, c, :], writes=[R(g.xb, None)])
    S.dma("sp", g.vecs[:], vecs_d, writes=[g.vecs])
    S.dma("sp", g.cf[:, 0:128], consts_d[:, 0:128], writes=[R(g.cf, None)])
    S.dma("sp", g.cf[:, 128:256], consts_d[:, 384:512], writes=[R(g.cf, None)])
    S.i("pool", "memset", [], [g.cb], ap=g.cb[:], constant=0.0)
    S.dma("pool", g.cb[:, 0:256], consts_d[:, 128:384], writes=[R(g.cb, None)])
    S.dma("pool", g.cb[:, 256:258], consts_d[:, 768:770], writes=[R(g.cb, None)])
    g.mk = S.sbuf("mk", [128, 3, 512], BF16)
    S.dma("pool", g.mk[:], g.mk_d, writes=[g.mk])
    g.ones32 = g.cf[:, 0:128]
    g.identb = g.cb[:, 0:128]

    for si_, step in enumerate(plan):
        kind = step[0]
        if kind == "ffn":
            nxt = plan[si_ + 1][0] if si_ + 1 < len(plan) else None
            ffn_phase(g, step[1], write_xb=(nxt not in (None, "rwkv")))
        elif kind == "ln":
            S.push()
            layer_norm(g, step[1], range(4))
            S.pop()
        elif kind == "moba":
            moba_phase(g, step[1])
        elif kind == "ret":
            ret_phase(g, step[1])
        elif kind == "rwkv":
            rwkv_phase(g, step[1], step[2])
        else:
            raise ValueError(kind)

    S.barrier()
    ov = out_d.rearrange("(c p) t -> p c t", p=128)
    outsem = T(S, None, "outsem")
    for c in range(NCH):
        S.dma("sp", ov[:, c, :], g.x[:, c, :], reads=[R(g.x, None)], out_final=True)
    S.emit()
    return nc


def vcol(g, name, c=0):
    o = g.voff[name] + c
    return g.vecs[:, o:o + 1]


def layer_norm(g, pref, tts, width=512, write_xb=True):
    S = g.S
    gname, bname = pref
    x, xb = g.x, g.xb
    tts = list(tts)
    nt = len(tts)
    sq = [S.sbuf("lnsq", [128, 512], F32) for _ in range(2)]
    means = [S.sbuf("lnmean", [128, 512], F32) for _ in range(nt)]
    msqs = [S.sbuf("lnmsq", [128, 512], F32) for _ in range(2)]
    rstds = [S.sbuf("lnrstd", [128, 512], F32) for _ in range(nt)]
    tmp = [S.sbuf("lntmp", [128, 512], F32) for _ in range(3)]
    ps_ss = [S.psum("lnps", [128, 512]) for _ in range(2)]
    ps_qs = [S.psum("lnpq", [128, 512]) for _ in range(2)]
    ones = g.ones32
    W_ = width
    def stats(ii, tix):
        ts = slice(tix * W_, (tix + 1) * W_)
        tt = (tix * W_) // 512
        mean, msq, rstd, ps_s, ps_q = means[ii], msqs[ii % 2], rstds[ii], ps_ss[ii % 2], ps_qs[ii % 2]
        for c in range(NCH):
            q = sq[c % 2]
            S.i("act", "activation", [R(x, (c, tt))], [q], out=q[:, 0:W_], in_=x[:, c, ts], func=AF.Square)
            S.i("pe", "matmul", [R(x, (c, tt)), g.cf], [ps_s], out=ps_s[:, 0:W_], lhsT=ones, rhs=x[:, c, ts], start=(c == 0), stop=(c == NCH - 1))
            S.i("pe", "matmul", [q, g.cf], [ps_q], out=ps_q[:, 0:W_], lhsT=ones, rhs=q[:, 0:W_], start=(c == 0), stop=(c == NCH - 1))
        S.i("dve", "tensor_scalar", [ps_s], [mean], out=mean[:, 0:W_], in0=ps_s[:, 0:W_], scalar1=1.0 / D, scalar2=None, op0=ALU.mult)
        S.i("dve", "tensor_tensor", [mean], [msq], out=msq[:, 0:W_], in0=mean[:, 0:W_], in1=mean[:, 0:W_], op=ALU.mult)
        S.i("dve", "scalar_tensor_tensor", [ps_q, msq], [msq], out=msq[:, 0:W_], in0=ps_q[:, 0:W_], scalar=1.0 / D, in1=msq[:, 0:W_], op0=ALU.mult, op1=ALU.subtract)
        S.i("dve", "tensor_scalar", [msq], [msq], out=msq[:, 0:W_], in0=msq[:, 0:W_], scalar1=LN_EPS, scalar2=None, op0=ALU.add)
        S.i("act", "activation", [msq], [rstd], out=rstd[:, 0:W_], in_=msq[:, 0:W_], func=AF.Ln)
        S.i("act", "activation", [rstd], [rstd], out=rstd[:, 0:W_], in_=rstd[:, 0:W_], func=AF.Exp, scale=-0.5)
    def norm(ii, tix):
        ts = slice(tix * W_, (tix + 1) * W_)
        tt = (tix * W_) // 512
        mean, rstd = means[ii], rstds[ii]
        for c in range(NCH):
            tm = tmp[c % 3]
            S.i("dve", "tensor_tensor", [R(x, (c, tt)), mean], [tm], out=tm[:, 0:W_], in0=x[:, c, ts], in1=mean[:, 0:W_], op=ALU.subtract)
            S.i("dve", "tensor_tensor", [tm, rstd], [tm], out=tm[:, 0:W_], in0=tm[:, 0:W_], in1=rstd[:, 0:W_], op=ALU.mult)
            S.i("act", "activation", [tm, g.vecs], [R(x, (c, tt))], out=x[:, c, ts], in_=tm[:, 0:W_], func=AF.Identity,
                bias=vcol(g, bname, c), scale=vcol(g, gname, c))
            if write_xb:
                S.i("dve", "tensor_scalar", [tm, g.vecs], [R(xb, (c, tt))], out=xb[:, c, ts], in0=tm[:, 0:W_], scalar1=vcol(g, gname, c),
                    scalar2=vcol(g, bname, c), op0=ALU.mult, op1=ALU.add)
    stats(0, tts[0])
    for ii in range(nt):
        if ii + 1 < nt:
            stats(ii + 1, tts[ii + 1])
        norm(ii, tts[ii])


def ffn_phase(g, i, write_xb=True):
    S = g.S
    x, xb = g.x, g.xb
    S.push()
    W_in = g.W["ffn_w_in"][i].rearrange("(k p) (two f) -> p k two f", p=128, two=2)
    W_out = g.W["ffn_w_out"][i].rearrange("(k p) n -> p k n", p=128)
    TB = 1024
    a = S.sbuf("ffa", [128, NPAIR, TB], BF16)
    hus = [S.sbuf("hu", [128, TB + 2], F32) for _ in range(2)]
    hgs = [S.sbuf("hg", [128, TB + 2], F32) for _ in range(2)]
    aus = [S.sbuf("au", [128, TB], F32) for _ in range(2)]
    ags = [S.sbuf("ag", [128, TB], F32) for _ in range(2)]
    halo = S.sbuf("halo", [128, 2 * NPAIR, 2], F32)
    wps = [[S.sbuf("wp", [128, NCH, 128], BF16) for _ in range(2)] for _ in range(2)]
    wos = [S.sbuf("wo", [128, NPAIR, 128], BF16) for _ in range(2)]
    pb = [S.psum("ffps", [128, 512]) for _ in range(8)]
    cw = lambda k, j: vcol(g, "cw%d_%d" % (k, i), j)
    cbv = lambda j: vcol(g, "cb_%d" % i, j)
    nw = 0
    pending = []
    for blk in range(2):
        for j in range(NPAIR):
            if j == 0:
                for ug in range(2):
                    S.dma("pool", wps[nw % 2][ug][:], W_in[:, :, ug, 0:128], writes=[wps[nw % 2][ug]])
            wp = wps[nw % 2]; nw += 1
            hu, hg, au, ag = hus[j % 2], hgs[j % 2], aus[j % 2], ags[j % 2]
            if j + 1 < NPAIR:
                for ug in range(2):
                    S.dma("pool", wps[nw % 2][ug][:], W_in[:, :, ug, (j + 1) * 128:(j + 2) * 128], writes=[wps[nw % 2][ug]])
            elif True:
                S.dma("pool", wos[0][:], W_out[:, :, 0:128], writes=[wos[0]])
            banks = pb[(j % 2) * 4:(j % 2) * 4 + 4]
            for ug in range(2):
                for h in range(2):
                    bk = banks[ug * 2 + h]
                    tt = blk * 2 + h
                    for k in range(NCH):
                        S.i("pe", "matmul", [wp[ug], R(xb, (k, tt))], [bk], out=bk[:], lhsT=wp[ug][:, k, :],
                            rhs=xb[:, k, tt * 512:(tt + 1) * 512], start=(k == 0), stop=(k == NCH - 1))
            for ug, hb in ((0, hu), (1, hg)):
                for h in range(2):
                    bk = banks[ug * 2 + h]
                    S.i("act", "activation", [bk], [R(hb, h)], out=hb[:, 2 + h * 512:2 + (h + 1) * 512], in_=bk[:], func=AF.Copy)
                if blk == 0:
                    S.i("dve", "memset", [], [R(hb, "halo")], ap=hb[:, 0:2], constant=0.0)
                else:
                    S.i("dve", "tensor_copy", [R(halo, (ug, j))], [R(hb, "halo")], out=hb[:, 0:2], in_=halo[:, ug * NPAIR + j, :])
            for (hb, ab_, jj) in ((hu, au, j), (hg, ag, NPAIR + j)):
                if jj == j:
                    S.i("dve", "tensor_scalar", [hb, g.vecs], [ab_], out=ab_[:], in0=hb[:, 2:TB + 2], scalar1=cw(2, jj), scalar2=cbv(jj), op0=ALU.mult, op1=ALU.add)
                else:
                    S.i("act", "activation", [hb, g.vecs], [ab_], out=ab_[:], in_=hb[:, 2:TB + 2], func=AF.Identity, bias=cbv(jj), scale=cw(2, jj))
                if jj == j and pending:
                    pending.pop(0)()
                S.i("dve", "scalar_tensor_tensor", [hb, ab_, g.vecs], [ab_], out=ab_[:], in0=hb[:, 1:TB + 1], scalar=cw(1, jj), in1=ab_[:], op0=ALU.mult, op1=ALU.add)
                S.i("dve", "scalar_tensor_tensor", [hb, ab_, g.vecs], [ab_], out=ab_[:], in0=hb[:, 0:TB], scalar=cw(0, jj), in1=ab_[:], op0=ALU.mult, op1=ALU.add)
            def tail(au=au, ag=ag, j=j):
                S.i("act", "activation", [ag], [ag], out=ag[:], in_=ag[:], func=AF.Silu)
                S.i("dve", "tensor_tensor", [au, ag], [R(a, j)], out=a[:, j, :], in0=au[:], in1=ag[:], op=ALU.mult)
            pending.append(tail)
            if blk == 0:
                for ug, hb in ((0, hu), (1, hg)):
                    S.i("dve", "tensor_copy", [hb], [R(halo, (ug, j))], out=halo[:, ug * NPAIR + j, :], in_=hb[:, TB:TB + 2])
        while pending:
            pending.pop(0)()
        for m in range(NCH):
            wo = wos[m % 2]
            if m + 1 < NCH:
                S.dma("pool", wos[(m + 1) % 2][:], W_out[:, :, (m + 1) * 128:(m + 2) * 128], writes=[wos[(m + 1) % 2]])
            for h in range(2):
                bk = pb[(m * 2 + h) % 8]
                tt = blk * 2 + h
                for k in range(NPAIR):
                    S.i("pe", "matmul", [wo, R(a, k)], [bk], out=bk[:], lhsT=wo[:, k, :], rhs=a[:, k, h * 512:(h + 1) * 512],
                        start=(k == 0), stop=(k == NPAIR - 1))
                xs = x[:, m, tt * 512:(tt + 1) * 512]
                S.i("dve", "scalar_tensor_tensor", [bk, R(x, (m, tt))], [R(x, (m, tt))], out=xs, in0=xs, scalar=ALPHA, in1=bk[:],
                    op0=ALU.mult, op1=ALU.add)
    S.pop()
    S.push()
    layer_norm(g, ("ln2_g%d" % i, "ln2_b%d" % i), range(4), write_xb=write_xb)
    S.pop()

def moba_phase(g, i):
    S = g.S
    x, xb = g.x, g.xb
    S.push()
    Wqkv = g.W["moba_w_qkv"][0].rearrange("(k p) n -> p k n", p=128)
    Wo = g.W["moba_w_o"][0].rearrange("(k p) n -> p k n", p=128)
    ogT = S.sbuf("ogT", [128, NCH, T_SEQ], BF16)
    wsl = [[S.sbuf("mw", [128, NCH, 128], BF16) for _ in range(3)] for _ in range(2)]
    qkv = [[S.sbuf("mqkv", [128, T_SEQ], BF16) for _ in range(3)] for _ in range(2)]
    vtms = [S.sbuf("vtm", [128, 16, 128], BF16) for _ in range(2)]
    ksum = S.sbuf("ksum", [128, 8], F32)
    kmean = S.sbuf("kmean", [128, 8], BF16)
    P = S.sbuf("mP", [128, T_SEQ], BF16)
    PT = S.sbuf("mPT", [128, 16, 128], BF16)
    sd = S.sbuf("msd", [128, 128], F32)
    g8 = S.sbuf("g8", [128, 8], F32)
    top8 = S.sbuf("top8", [128, 8], F32)
    mb = S.sbuf("mb", [128, 8], F32)
    b8 = S.sbuf("b8", [128, 8], F32)
    nm = S.sbuf("nm", [128, 4], F32)
    rs = S.sbuf("rs", [128, 12], F32)
    rinv = S.sbuf("rinv", [128, 2], F32)
    otm = S.sbuf("otm", [128, 128], BF16)
    wos = [S.sbuf("mwo", [128, NCH, 128], BF16) for _ in range(2)]
    sc = S.psum("msc", [128, 2048])
    pT = S.psum("mpT", [128, 1024], BF16)
    ov = S.psum("mov", [128, 512])
    gps = ov
    pjs = [S.psum("mpj", [128, 512]) for _ in range(2)]
    npj = 0
    tri = g.cf[:, 128:256]
    if _DBG.get('dbg_memset'):
        S.i('pool', 'memset', [], [ogT], ap=ogT[:], constant=0.0)
    ident = g.identb
    for c in range(_DBG.get('moba_pairs', NCH)):
        ws = wsl[c % 2]
        qT, kT, vT = qkv[c % 2]
        vtm = vtms[c % 2]
        for w in range(3):
            S.dma("pool", ws[w][:], Wqkv[:, :, w * 1024 + c * 128: w * 1024 + (c + 1) * 128], writes=[ws[w]])
        for w, dst in ((0, qT), (1, kT), (2, vT)):
            for tt in range(4):
                pj = pjs[npj % 2]; npj += 1
                for k in range(NCH):
                    S.i("pe", "matmul", [ws[w], R(xb, (k, tt))], [pj], out=pj[:], lhsT=ws[w][:, k, :], rhs=xb[:, k, tt * 512:(tt + 1) * 512],
                        start=(k == 0), stop=(k == NCH - 1))
                if w == 0:
                    S.i("act", "activation", [pj], [R(dst, tt)], out=dst[:, tt * 512:(tt + 1) * 512], in_=pj[:], func=AF.Copy, scale=0.125)
                else:
                    S.i("act", "activation", [pj], [R(dst, tt)], out=dst[:, tt * 512:(tt + 1) * 512], in_=pj[:], func=AF.Copy)
                if w == 1 and _DBG.get('moba_stage', 9) >= 0.2:
                    S.i("dve", "tensor_reduce", [pj], [R(ksum, tt)], out=ksum[:, 2 * tt:2 * tt + 2],
                        in_=pj[:].rearrange("p (b k) -> p b k", b=2), axis=AX.X, op=ALU.add)
        if _DBG.get('moba_stage', 9) >= 0.2:
            S.i("dve", "tensor_scalar", [ksum], [kmean], out=kmean[:], in0=ksum[:], scalar1=1.0 / 256.0, scalar2=None, op0=ALU.mult)
        for b in range(2 if _DBG.get('moba_stage', 9) >= 0.3 else 0):
            for t8 in range(8):
                kt = b * 8 + t8
                S.i("pe", "transpose", [vT, g.cb], [pT], out=pT[:, t8 * 128:(t8 + 1) * 128], in_=vT[:, kt * 128:(kt + 1) * 128], identity=ident)
            S.i("dve", "tensor_copy", [pT], [R(vtm, b)], out=vtm[:, b * 8:(b + 1) * 8, :].rearrange("p a b -> p (a b)"), in_=pT[:])
        stg = _DBG.get('moba_stage', 9)
        for qt in _DBG.get('moba_qts', range(16)):
            if stg < 2:
                break
            qb = qt // 2
            nkt = qt + 1
            qs = slice(qt * 128, (qt + 1) * 128)
            for hh in range(2):
                ps = slice(hh * 64, (hh + 1) * 64)
                use_thr = qb >= 4
                if use_thr:
                    S.i("pe", "matmul", [qT, kmean], [gps], out=gps[:, 256:264], lhsT=qT[ps, qs], rhs=kmean[ps, 0:8], start=True, stop=True)
                    S.i("pool", "memset", [], [R(g8, "pad")], ap=g8[:, qb:8], constant=-1.0e30)
                    S.i("dve", "tensor_copy", [gps], [R(g8, "val")], out=g8[:, 0:qb], in_=gps[:, 256:256 + qb])
                    S.i("dve", "max", [g8], [top8], out=top8[:], in_=g8[:])
                    S.i("dve", "tensor_scalar", [g8, top8], [mb], out=mb[:], in0=g8[:], scalar1=top8[:, 2:3], scalar2=30000.0,
                        op0=ALU.is_ge, op1=ALU.mult)
                ncol = nkt * 128
                for b in range((ncol + 511) // 512):
                    w_ = min(512, ncol - b * 512)
                    S.i("pe", "matmul", [qT, kT], [sc], out=sc[:, b * 512:b * 512 + w_], lhsT=qT[ps, qs], rhs=kT[ps, b * 512:b * 512 + w_],
                        start=True, stop=True)
                S.i("dve", "tensor_tensor", [sc, g.cf], [sc], out=sc[:, qt * 128:(qt + 1) * 128], in0=sc[:, qt * 128:(qt + 1) * 128], in1=tri, op=ALU.add)
                S.i("dve", "tensor_reduce", [sc], [R(nm, 0)], out=nm[:, 0:1], in_=sc[:, 0:ncol], axis=AX.X, op=ALU.max, negate=True)
                negm = nm[:, 0:1]
                if stg < 3:
                    continue
                if use_thr:
                    S.i("dve", "tensor_scalar", [mb, nm], [b8], out=b8[:], in0=mb[:], scalar1=negm, scalar2=-30000.0, op0=ALU.add, op1=ALU.add)
                npz = 0
                if use_thr:
                    for n in range(qb):
                        S.i("act", "activation", [sc, b8], [R(P, n), R(rs, npz)], out=P[:, n * 256:(n + 1) * 256], in_=sc[:, n * 256:(n + 1) * 256],
                            func=AF.Exp, bias=b8[:, n:n + 1], scale=1.0, accum_out=rs[:, npz:npz + 1])
                        npz += 1
                    S.i("act", "activation", [sc, nm], [R(P, "o"), R(rs, npz)], out=P[:, qb * 256:ncol], in_=sc[:, qb * 256:ncol],
                        func=AF.Exp, bias=negm, scale=1.0, accum_out=rs[:, npz:npz + 1])
                    npz += 1
                else:
                    S.i("act", "activation", [sc, nm], [R(P, "past"), R(rs, npz)], out=P[:, 0:ncol], in_=sc[:, 0:ncol],
                        func=AF.Exp, bias=negm, scale=1.0, accum_out=rs[:, npz:npz + 1])
                    npz += 1
                S.i("dve", "tensor_reduce", [rs], [R(rs, 11)], out=rs[:, 11:12], in_=rs[:, 0:npz], axis=AX.X, op=ALU.add)
                S.i("dve", "reciprocal", [R(rs, 11)], [R(rinv, hh)], out=rinv[:, hh:hh + 1], in_=rs[:, 11:12])
                if stg < 4:
                    continue
                for b in range((nkt + 7) // 8):
                    n8 = min(8, nkt - b * 8)
                    for t8 in range(n8):
                        kt = b * 8 + t8
                        S.i("pe", "transpose", [P, g.cb], [pT], out=pT[:, t8 * 128:(t8 + 1) * 128], in_=P[:, kt * 128:(kt + 1) * 128], identity=ident)
                    S.i("dve", "tensor_copy", [pT], [R(PT, b)], out=PT[:, b * 8:b * 8 + n8, :].rearrange("p a b -> p (a b)"), in_=pT[:, 0:n8 * 128])
                for kt in range(nkt):
                    S.i("pe", "matmul", [PT, vtm], [R(ov, hh)], out=ov[:, hh * 64:(hh + 1) * 64], lhsT=PT[:, kt, :], rhs=vtm[:, kt, hh * 64:(hh + 1) * 64],
                        start=(kt == 0), stop=(kt == nkt - 1))
                S.i("dve", "tensor_scalar", [R(ov, hh), R(rinv, hh)], [R(otm, hh)], out=otm[:, hh * 64:(hh + 1) * 64], in0=ov[:, hh * 64:(hh + 1) * 64],
                    scalar1=rinv[:, hh:hh + 1], scalar2=None, op0=ALU.mult)
            if stg < 5:
                continue
            S.i("pe", "transpose", [otm, g.cb], [pT], out=pT[:, 0:128], in_=otm[:], identity=ident)
            S.i("act", "activation", [pT], [R(ogT, (c, qt))], out=ogT[:, c, qs], in_=pT[:, 0:128], func=AF.Copy)
    for m in range(NCH):
        wo = wos[m % 2]
        S.dma("pool", wo[:], Wo[:, :, m * 128:(m + 1) * 128], writes=[wo])
        for tt in range(4):
            pj = pjs[npj % 2]; npj += 1
            for k in range(NCH):
                S.i("pe", "matmul", [wo, ogT], [pj], out=pj[:], lhsT=wo[:, k, :], rhs=ogT[:, k, tt * 512:(tt + 1) * 512], start=(k == 0), stop=(k == NCH - 1))
            xs = x[:, m, tt * 512:(tt + 1) * 512]
            S.i("dve", "scalar_tensor_tensor", [pj, R(x, (m, tt))], [R(x, (m, tt))], out=xs, in0=xs, scalar=ALPHA, in1=pj[:], op0=ALU.mult, op1=ALU.add)
    S.pop()
    S.push()
    layer_norm(g, ("ln1_g%d" % i, "ln1_b%d" % i), range(4))
    S.pop()

def ret_phase(g, i):
    S = g.S
    x, xb = g.x, g.xb
    S.push()
    Win = g.W["ret_w_in"][0].rearrange("(k p) n -> p k n", p=128)
    Wsw = g.ret_sw_d.rearrange("(k p) n -> p k n", p=128)
    Wo = g.W["ret_w_o"][0].rearrange("(k p) n -> p k n", p=128)
    wq = S.sbuf("rwq", [128, NCH, 256], BF16); wqs = S.sbuf("rwqs", [128, NCH, 256], BF16)
    wk = S.sbuf("rwk", [128, NCH, 256], BF16); wks = S.sbuf("rwks", [128, NCH, 256], BF16)
    wv = S.sbuf("rwv", [128, NCH, 512], BF16); wg = S.sbuf("rwg", [128, NCH, 512], BF16)
    wo = S.sbuf("rwo", [128, 4, 1024], BF16)
    rm = S.sbuf("rrm", [128, 384], F32)
    tabs = [S.sbuf("rtab", [128, 2, 2, 512], F32) for _ in range(2)]
    qrot = S.sbuf("qrot", [128, 2, 512], BF16); krot = S.sbuf("krot", [128, 2, 512], BF16)
    vT = S.sbuf("rvT", [128, 4, 512], BF16); sgT = S.sbuf("rsgT", [128, 4, 512], BF16)
    vtm = S.sbuf("rvtm", [128, 4, 512], BF16); ktm = S.sbuf("rktm", [128, 4, 256], BF16)
    og = S.sbuf("rog", [128, 4, 512], BF16)
    S32 = S.sbuf("rS32", [128, 2, 512], F32); Sb = S.sbuf("rSb", [128, 2, 512], BF16)
    t1 = S.sbuf("rt1", [128, 512], F32); t2 = S.sbuf("rt2", [128, 512], F32)
    ST = S.sbuf("rST", [128, 128], BF16); qcd = S.sbuf("rqcd", [128, 2, 128], BF16)
    osbs = [S.sbuf("rosb", [128, 512], F32) for _ in range(2)]; osqs = [S.sbuf("rosq", [128, 512], F32) for _ in range(2)]
    means = [S.sbuf("rmean", [128, 128], F32) for _ in range(2)]; msqs = [S.sbuf("rmsq", [128, 128], F32) for _ in range(2)]; rstds = [S.sbuf("rrstd", [128, 128], F32) for _ in range(2)]
    nchunk = 0
    pA = S.psum("rpA", [128, 512]); pB = S.psum("rpB", [128, 512])
    opss = [S.psum("rops", [128, 512]) for _ in range(2)]
    ups = [S.psum("rups", [128, 512]) for _ in range(2)]
    pT = S.psum("rpT", [128, 1024], BF16)
    pst = S.psum("rpst", [128, 512])
    sps = pst
    ident = g.identb
    ones = g.ones32
    ntab = 0
    for h in range(4):
        S.dma("pool", wq[:], Win[:, :, h * 256:(h + 1) * 256], writes=[wq])
        S.dma("pool", wqs[:], Wsw[:, :, h * 256:(h + 1) * 256], writes=[wqs])
        S.dma("pool", wk[:], Win[:, :, 1024 + h * 256:1024 + (h + 1) * 256], writes=[wk])
        S.dma("pool", wks[:], Wsw[:, :, 1024 + h * 256:1024 + (h + 1) * 256], writes=[wks])
        S.dma("pool", wv[:], Win[:, :, 2048 + h * 512:2048 + (h + 1) * 512], writes=[wv])
        S.dma("pool", wg[:], Win[:, :, 4096 + h * 512:4096 + (h + 1) * 512], writes=[wg])
        S.dma("pool", wo[:], Wo[:, h * 4:(h + 1) * 4, :], writes=[wo])
        S.dma("sp", rm[:], g.rmask_d[h], writes=[rm])
        S.i("pool", "memset", [], [S32], ap=S32[:], constant=0.0)
        S.i("pool", "memset", [], [Sb], ap=Sb[:], constant=0.0)
        for tt in range(4):
            ts = slice(tt * 512, (tt + 1) * 512)
            tab = tabs[ntab % 2]; ntab += 1
            for cs_ in range(2):
                S.dma("sp", tab[:, cs_, :, :], g.rtab_d[cs_][:, :, ts], writes=[R(tab, None)])
            for (wa, wb, dst) in ((wq, wqs, qrot), (wk, wks, krot)):
                for dc in range(2):
                    for k in range(NCH):
                        S.i("pe", "matmul", [wa, R(xb, (k, tt))], [pA], out=pA[:], lhsT=wa[:, k, dc * 128:(dc + 1) * 128], rhs=xb[:, k, ts],
                            start=(k == 0), stop=(k == NCH - 1))
                    for k in range(NCH):
                        S.i("pe", "matmul", [wb, R(xb, (k, tt))], [pB], out=pB[:], lhsT=wb[:, k, dc * 128:(dc + 1) * 128], rhs=xb[:, k, ts],
                            start=(k == 0), stop=(k == NCH - 1))
                    S.i("dve", "tensor_tensor", [pA, tab], [t1], out=t1[:], in0=pA[:], in1=tab[:, 0, dc, :], op=ALU.mult)
                    S.i("dve", "tensor_tensor", [pB, tab], [t2], out=t2[:], in0=pB[:], in1=tab[:, 1, dc, :], op=ALU.mult)
                    S.i("pool", "tensor_tensor", [t1, t2], [R(dst, dc)], out=dst[:, dc, :], in0=t1[:], in1=t2[:], op=ALU.add)
            for ec in range(4):
                for k in range(NCH):
                    S.i("pe", "matmul", [wv, R(xb, (k, tt))], [pA], out=pA[:], lhsT=wv[:, k, ec * 128:(ec + 1) * 128], rhs=xb[:, k, ts],
                        start=(k == 0), stop=(k == NCH - 1))
                S.i("act", "activation", [pA], [R(vT, ec)], out=vT[:, ec, :], in_=pA[:], func=AF.Copy)
                for k in range(NCH):
                    S.i("pe", "matmul", [wg, R(xb, (k, tt))], [pB], out=pB[:], lhsT=wg[:, k, ec * 128:(ec + 1) * 128], rhs=xb[:, k, ts],
                        start=(k == 0), stop=(k == NCH - 1))
                S.i("act", "activation", [pB], [R(sgT, ec)], out=sgT[:, ec, :], in_=pB[:], func=AF.Silu)
            for n in range(4):
                for ec in range(4):
                    idx = (n % 2) * 4 + ec
                    S.i("pe", "transpose", [vT, g.cb], [pT], out=pT[:, idx * 128:(idx + 1) * 128], in_=vT[:, ec, n * 128:(n + 1) * 128], identity=ident)
                if n % 2 == 1:
                    S.i("dve", "tensor_copy", [pT], [R(vtm, n // 2)], out=vtm[:, n - 1:n + 1, :].rearrange("p a b -> p (a b)"), in_=pT[:])
            for n in range(4):
                for dc in range(2):
                    idx = n * 2 + dc
                    S.i("pe", "transpose", [krot, g.cb], [pT], out=pT[:, idx * 128:(idx + 1) * 128], in_=krot[:, dc, n * 128:(n + 1) * 128], identity=ident)
            S.i("dve", "tensor_scalar", [pT, rm], [ktm], out=ktm[:].rearrange("p a b -> p (a b)"), in0=pT[:], scalar1=rm[:, 256:257], scalar2=None, op0=ALU.mult)
            def partA(n):
                cs = slice(n * 128, (n + 1) * 128)
                opsn = opss[n % 2]
                for dc in range(2):
                    S.i("pe", "matmul", [krot, qrot], [sps], out=sps[:, 256:384], lhsT=krot[:, dc, cs], rhs=qrot[:, dc, cs], start=(dc == 0), stop=(dc == 1))
                S.i("dve", "tensor_tensor", [sps, rm], [ST], out=ST[:], in0=sps[:, 256:384], in1=rm[:, 0:128], op=ALU.mult)
                S.i("pool", "tensor_tensor", [qrot, rm], [qcd], out=qcd[:], in0=qrot[:, :, cs], in1=rm[:, 128:256].unsqueeze(1).to_broadcast([128, 2, 128]), op=ALU.mult)
                for ec in range(4):
                    es = slice(ec * 128, (ec + 1) * 128)
                    S.i("pe", "matmul", [vtm, ST], [opsn], out=opsn[:, es], lhsT=vtm[:, n, es], rhs=ST[:], start=True, stop=False)
                    for dc in range(2):
                        S.i("pe", "matmul", [Sb, qcd], [opsn], out=opsn[:, es], lhsT=Sb[:, dc, es], rhs=qcd[:, dc, :], start=False, stop=(dc == 1))
                for dc in range(2):
                    S.i("pe", "matmul", [ktm, vtm], [ups[dc]], out=ups[dc][:], lhsT=ktm[:, n, dc * 128:(dc + 1) * 128], rhs=vtm[:, n, :], start=True, stop=True)
                    S.i("dve", "scalar_tensor_tensor", [S32, ups[dc], rm], [R(S32, dc)], out=S32[:, dc, :], in0=S32[:, dc, :], scalar=rm[:, 257:258], in1=ups[dc][:],
                        op0=ALU.mult, op1=ALU.add)
                    S.i("act", "activation", [R(S32, dc)], [R(Sb, dc)], out=Sb[:, dc, :], in_=S32[:, dc, :], func=AF.Copy)

            def partB(n):
                cs = slice(n * 128, (n + 1) * 128)
                opsn = opss[n % 2]
                osb, osq, mean, msq, rstd = osbs[n % 2], osqs[n % 2], means[n % 2], msqs[n % 2], rstds[n % 2]
                S.i("act", "activation", [opsn], [osb], out=osb[:], in_=opsn[:], func=AF.Copy)
                S.i("act", "activation", [opsn], [osq], out=osq[:], in_=opsn[:], func=AF.Square)
                for ec in range(4):
                    S.i("pe", "matmul", [osb, g.cf], [R(pst, 0)], out=pst[:, 0:128], lhsT=ones, rhs=osb[:, ec * 128:(ec + 1) * 128], start=(ec == 0), stop=(ec == 3))
                for ec in range(4):
                    S.i("pe", "matmul", [osq, g.cf], [R(pst, 1)], out=pst[:, 128:256], lhsT=ones, rhs=osq[:, ec * 128:(ec + 1) * 128], start=(ec == 0), stop=(ec == 3))
                S.i("dve", "tensor_scalar", [R(pst, 0)], [mean], out=mean[:], in0=pst[:, 0:128], scalar1=1.0 / 512, scalar2=None, op0=ALU.mult)
                S.i("dve", "tensor_tensor", [mean], [msq], out=msq[:], in0=mean[:], in1=mean[:], op=ALU.mult)
                S.i("dve", "scalar_tensor_tensor", [R(pst, 1), msq], [msq], out=msq[:], in0=pst[:, 128:256], scalar=1.0 / 512, in1=msq[:], op0=ALU.mult, op1=ALU.subtract)
                S.i("dve", "tensor_scalar", [msq], [msq], out=msq[:], in0=msq[:], scalar1=1e-5, scalar2=None, op0=ALU.add)
                S.i("act", "activation", [msq], [rstd], out=rstd[:], in_=msq[:], func=AF.Ln)
                S.i("act", "activation", [rstd], [rstd], out=rstd[:], in_=rstd[:], func=AF.Exp, scale=-0.5)
                mb_ = mean[:].unsqueeze(1).to_broadcast([128, 4, 128])
                rb_ = rstd[:].unsqueeze(1).to_broadcast([128, 4, 128])
                o3 = osb[:].rearrange("p (a b) -> p a b", b=128)
                S.i("dve", "tensor_tensor", [osb, mean], [osb], out=o3, in0=o3, in1=mb_, op=ALU.subtract)
                S.i("dve", "tensor_tensor", [osb, rstd], [osb], out=o3, in0=o3, in1=rb_, op=ALU.mult)
                for ec in range(4):
                    col = h * 4 + ec
                    S.i("act", "activation", [osb, g.vecs], [R(osq, ec)], out=osq[:, ec * 128:(ec + 1) * 128], in_=osb[:, ec * 128:(ec + 1) * 128], func=AF.Identity,
                        bias=vcol(g, "ret_gn_b", col), scale=vcol(g, "ret_gn_g", col))
                S.i("dve", "tensor_tensor", [osq, sgT], [R(og, n)], out=og[:, :, cs], in0=osq[:].rearrange("p (a b) -> p a b", b=128), in1=sgT[:, :, cs], op=ALU.mult)

            partA(0)
            for n in range(4):
                if n + 1 < 4:
                    partA(n + 1)
                partB(n)
            for m in range(NCH):
                pw = pA if m % 2 == 0 else pB
                for k in range(4):
                    S.i("pe", "matmul", [wo, og], [pw], out=pw[:], lhsT=wo[:, k, m * 128:(m + 1) * 128], rhs=og[:, k, :], start=(k == 0), stop=(k == 3))
                xs = x[:, m, ts]
                if h == 0:
                    S.i("dve", "scalar_tensor_tensor", [pw, R(x, (m, tt))], [R(x, (m, tt))], out=xs, in0=xs, scalar=ALPHA, in1=pw[:], op0=ALU.mult, op1=ALU.add)
                else:
                    S.i("dve", "tensor_tensor", [pw, R(x, (m, tt))], [R(x, (m, tt))], out=xs, in0=xs, in1=pw[:], op=ALU.add)
    S.pop()
    S.push()
    layer_norm(g, ("ln1_g%d" % i, "ln1_b%d" % i), range(4))
    S.pop()

def rwkv_phase(g, i, j):
    S = g.S
    x, xb = g.x, g.xb
    S.push()
    TW = 256
    NT_ = T_SEQ // TW
    W = g.W
    Wrkv = [W["rwkv_w_rkv"][j, w].rearrange("(k p) n -> p k n", p=128) for w in range(3)]
    Wo = W["rwkv_w_o"][j].rearrange("(k p) n -> p k n", p=128)
    V = lambda name, c=0: vcol(g, "%s_%d" % (name, j), c)
    w1 = S.sbuf("w1", [128, NCH, 64], BF16); a1 = S.sbuf("a1", [128, NCH, 64], BF16); g1 = S.sbuf("g1", [128, NCH, 160], BF16)
    w2 = S.sbuf("w2", [64, 1024], BF16); a2 = S.sbuf("a2", [64, 1024], BF16); g2 = S.sbuf("g2", [128, 2, 1024], BF16)
    S.dma("pool", w1[:], W["rwkv_w1"][j].rearrange("(k p) n -> p k n", p=128), writes=[w1])
    S.dma("pool", a1[:], W["rwkv_a1"][j].rearrange("(k p) n -> p k n", p=128), writes=[a1])
    S.dma("pool", g1[:], W["rwkv_g1"][j].rearrange("(k p) n -> p k n", p=128), writes=[g1])
    S.dma("pool", w2[:], W["rwkv_w2"][j], writes=[w2])
    S.dma("pool", a2[:], W["rwkv_a2"][j], writes=[a2])
    S.dma("pool", g2[:, 0, :], W["rwkv_g2"][j][0:128, :], writes=[R(g2, None)])
    S.dma("pool", g2[0:32, 1, :], W["rwkv_g2"][j][128:160, :], writes=[R(g2, None)])
    rows = S.sbuf("gnrows", [128, 2, 1024], BF16)
    S.dma("pool", rows[:, 0, :], g.rows_d[2 * j], writes=[R(rows, None)])
    S.dma("pool", rows[:, 1, :], g.rows_d[2 * j + 1], writes=[R(rows, None)])
    om = S.sbuf("om", [128, 72], F32)
    mo = g.voff["mix0_%d" % j]
    S.i("dve", "tensor_scalar", [g.vecs], [om], out=om[:, 0:48], in0=g.vecs[:, mo:mo + 48], scalar1=-1.0, scalar2=1.0, op0=ALU.mult, op1=ALU.add)
    ko = g.voff["k_a_%d" % j]
    S.i("dve", "tensor_scalar", [g.vecs], [om], out=om[:, 48:56], in0=g.vecs[:, ko:ko + 8], scalar1=-1.0, scalar2=1.0, op0=ALU.mult, op1=ALU.add)
    ao = g.voff["a0_%d" % j]; wo_ = g.voff["w0_%d" % j]
    S.i("dve", "tensor_scalar", [g.vecs], [om], out=om[:, 56:64], in0=g.vecs[:, ao:ao + 8], scalar1=0.5, scalar2=None, op0=ALU.mult)
    S.i("dve", "tensor_scalar", [g.vecs], [om], out=om[:, 64:72], in0=g.vecs[:, wo_:wo_ + 8], scalar1=0.5, scalar2=None, op0=ALU.mult)
    xx = S.sbuf("xx", [128, NCH, TW], F32)
    xlast = S.sbuf("xlast", [128, NCH, 2], F32)
    S.i("pool", "memset", [], [xlast], ap=xlast[:], constant=0.0)
    hw = S.sbuf("hw", [64, TW], BF16); ha = S.sbuf("ha", [64, TW], BF16); sg = S.sbuf("sg", [128, 2, TW], BF16); sgf = S.sbuf("sgf", [128, 2, TW], BF16)
    NSL = _DBG.get('rw_nsl', 4)
    wsl = [S.sbuf("rwsl", [128, NCH, 128], BF16) for _ in range(NSL)]
    wseq = []
    for tix_ in range(NT_):
        for c_ in range(NCH):
            wseq.append(Wrkv[0][:, :, c_ * 128:(c_ + 1) * 128]); wseq.append(Wrkv[1][:, :, c_ * 128:(c_ + 1) * 128])
        for c_ in range(NCH):
            wseq.append(Wrkv[2][:, :, c_ * 128:(c_ + 1) * 128])
        for m_ in range(NCH):
            wseq.append(Wo[:, :, m_ * 128:(m_ + 1) * 128])
    wst = {"issued": 0, "next": 0}
    def wget():
        i_ = wst["next"]; wst["next"] += 1
        while wst["issued"] < min(len(wseq), i_ + NSL - 1):
            q_ = wst["issued"]
            S.dma("pool", wsl[q_ % NSL][:], wseq[q_], writes=[wsl[q_ % NSL]])
            wst["issued"] += 1
        return wsl[i_ % NSL]
    TS = [dict(), dict()]
    for nm in ("tA", "tsw", "tkk", "tcs", "te1", "te2", "te3"):
        for q_ in range(2):
            TS[q_][nm] = S.sbuf(nm, [128, TW], F32)
    for nm in ("trn", "tt"):
        t_ = S.sbuf(nm, [128, TW], F32)
        TS[0][nm] = t_; TS[1][nm] = t_
    for q_ in range(2):
        TS[q_]["tkk2"] = S.sbuf("tkk2", [128, TW], BF16)
    PC = S.sbuf("PC", [128, NCH, 2], F32)
    def FM(a, c, sl):
        return xb[:, c, a * TW + sl.start:a * TW + sl.stop]
    FMK = lambda a, c: R(xb, ("fm", a, c))
    Lk = [[S.sbuf("Lk", [128, 4, 128], BF16) for _ in range(2)] for _ in range(2)]
    Mk = [[S.sbuf("Mk", [128, 4, 128], BF16) for _ in range(2)] for _ in range(2)]
    NTt = [[S.sbuf("NT", [128, 4, 128], BF16) for _ in range(2)] for _ in range(2)]
    Mak = [S.sbuf("Mak", [128, 4, 128], BF16) for _ in range(2)]
    Mrb = S.sbuf("Mrb", [128, 16, 128], BF16); Mrk = S.sbuf("Mrk", [128, 16, 128], BF16)
    Atm = S.sbuf("Atm", [128, 1024], BF16); Btm = S.sbuf("Btm", [128, 1024], BF16); Ktm = S.sbuf("Ktm", [128, 1024], BF16); Vtm = S.sbuf("Vtm", [128, 1024], BF16)
    AhT = S.sbuf("AhT", [128, NCH, 128], BF16); Xb = S.sbuf("Xb", [128, 256], BF16); Vhat = S.sbuf("Vhat", [128, 1024], BF16)
    Ub = S.sbuf("Ub", [128, 1024], BF16); ST = S.sbuf("ST", [128, NCH, 64], BF16); STs = S.sbuf("STs", [128, NCH, 64], F32)
    yn = S.sbuf("yn", [128, 1024], F32); st4 = S.sbuf("st4", [128, 4, 16], F32); bsum = S.sbuf("bsum", [128, 16], F32)
    ogtm = S.sbuf("ogtm", [128, 1024], BF16); ogT = S.sbuf("ogT", [128, NCH, TW], BF16)
    PB = [S.psum("rp", [128, 512]) for _ in range(7)]
    pT = S.psum("rpT", [128, 1024], BF16)
    ident = g.identb
    blk1 = g.cb[:, 128:256]
    hind = g.cb[:, 256:258]
    ones = g.ones32
    S.i("pool", "memset", [], [ST], ap=ST[:], constant=0.0)
    nws = 0
    stg = _DBG.get('rw_stage', 99)
    for tix in range(_DBG.get('rw_tiles', NT_)):
        t0 = tix * TW
        tt = t0 // 512
        S.i("dve", "tensor_tensor", [R(x, None)], [R(xx, "m")], out=xx[:, :, 1:TW], in0=x[:, :, t0:t0 + TW - 1], in1=x[:, :, t0 + 1:t0 + TW], op=ALU.subtract)
        S.i("dve", "tensor_tensor", [R(x, None), xlast], [R(xx, "0")], out=xx[:, :, 0:1], in0=xlast[:, :, tix % 2:tix % 2 + 1], in1=x[:, :, t0:t0 + 1], op=ALU.subtract)
        S.i("dve", "tensor_copy", [R(x, None)], [xlast], out=xlast[:, :, (tix + 1) % 2:(tix + 1) % 2 + 1], in_=x[:, :, t0 + TW - 1:t0 + TW])
        def mix(m, dst):
            for c in range(NCH):
                mc = vcol(g, "mix%d_%d" % (m, j), c)
                S.i("dve", "scalar_tensor_tensor", [R(x, (c, tt)), xx, g.vecs], [dst.key(c)], out=dst.ap(c), in0=xx[:, c, :], scalar=mc, in1=x[:, c, t0:t0 + TW],
                    op0=ALU.mult, op1=ALU.add)
        nx = [0]
        class XM:
            def __init__(self, slot):
                self.slot = slot
            def ap(self, c):
                return xb[:, c, 1536 + self.slot * TW:1536 + (self.slot + 1) * TW]
            def key(self, c):
                return R(xb, ("xm", self.slot, c))
        def nxm():
            nx[0] += 1
            return XM(nx[0] % 2)
        d_ = nxm(); mix(1, d_)
        for k in range(NCH):
            S.i("pe", "matmul", [w1, d_.key(k)], [PB[0]], out=PB[0][0:64, 0:TW], lhsT=w1[:, k, :], rhs=d_.ap(k), start=(k == 0), stop=(k == NCH - 1))
        S.i("act", "activation", [PB[0]], [hw], out=hw[:], in_=PB[0][0:64, 0:TW], func=AF.Tanh)
        d_ = nxm(); mix(4, d_)
        for k in range(NCH):
            S.i("pe", "matmul", [a1, d_.key(k)], [PB[1]], out=PB[1][0:64, 0:TW], lhsT=a1[:, k, :], rhs=d_.ap(k), start=(k == 0), stop=(k == NCH - 1))
        S.i("act", "activation", [PB[1]], [ha], out=ha[:], in_=PB[1][0:64, 0:TW], func=AF.Copy)
        d_ = nxm(); mix(5, d_)
        for (lo, hi, kc) in ((0, 128, 0), (128, 160, 1)):
            for k in range(NCH):
                S.i("pe", "matmul", [g1, d_.key(k)], [PB[2]], out=PB[2][0:hi - lo, 0:TW], lhsT=g1[:, k, lo:hi], rhs=d_.ap(k), start=(k == 0), stop=(k == NCH - 1))
            S.i("act", "activation", [PB[2]], [R(sgf, kc)], out=sgf[0:hi - lo, kc, :], in_=PB[2][0:hi - lo, 0:TW], func=AF.Tanh, scale=0.5)
            S.i("dve", "tensor_scalar", [R(sgf, kc)], [R(sg, kc)], out=sg[0:hi - lo, kc, :], in0=sgf[0:hi - lo, kc, :], scalar1=0.5, scalar2=0.5, op0=ALU.mult, op1=ALU.add)
        xr = nxm(); mix(0, xr)
        xk = nxm(); mix(2, xk)
        if stg < 3:
            continue
        for c in range(NCH):
            wr = wget()
            wk_ = wget()
            d = TS[c % 2]
            tA, tsw, tcs, te1, te2, te3, tkk, tkk2, trn, tt_ = (d[k_] for k_ in ("tA", "tsw", "tcs", "te1", "te2", "te3", "tkk", "tkk2", "trn", "tt"))
            td3 = tsw; tkkn = tkk; ttb = tkk; tkm = tt_
            if c % 2 == 0:
                rp, kp, zw, za, ssp = PB[0], PB[1], PB[2], PB[3], PB[4]
            else:
                rp, kp, zw, za, ssp = PB[5], PB[6], PB[2], PB[3], PB[4]
            for k in range(NCH):
                S.i("pe", "matmul", [wr, xr.key(k)], [rp], out=rp[:, 0:TW], lhsT=wr[:, k, :], rhs=xr.ap(k), start=(k == 0), stop=(k == NCH - 1))
            for k in range(NCH):
                S.i("pe", "matmul", [wk_, xk.key(k)], [kp], out=kp[:, 0:TW], lhsT=wk_[:, k, :], rhs=xk.ap(k), start=(k == 0), stop=(k == NCH - 1))
            S.i("pe", "matmul", [w2, hw], [zw], out=zw[:, 0:TW], lhsT=w2[:, c * 128:(c + 1) * 128], rhs=hw[:], start=True, stop=True)
            S.i("pe", "matmul", [a2, ha], [za], out=za[:, 0:TW], lhsT=a2[:, c * 128:(c + 1) * 128], rhs=ha[:], start=True, stop=True)
            S.i("act", "activation", [za, om], [tA], out=tA[:], in_=za[:, 0:TW], func=AF.Tanh, bias=om[:, 56 + c:57 + c], scale=0.5)
            S.i("act", "activation", [zw, om], [tsw], out=tsw[:], in_=zw[:, 0:TW], func=AF.Tanh, bias=om[:, 64 + c:65 + c], scale=0.5)
            S.i("dve", "tensor_scalar", [tA], [tA], out=tA[:], in0=tA[:], scalar1=0.5, scalar2=0.5, op0=ALU.mult, op1=ALU.add)
            S.i("dve", "tensor_scalar", [tsw], [tsw], out=tsw[:], in0=tsw[:], scalar1=0.5, scalar2=0.5, op0=ALU.mult, op1=ALU.add)
            for n in range(2):
                cs = slice(n * 128, (n + 1) * 128)
                S.i("dve", "tensor_tensor_scan", [tsw, g.cf], [R(tcs, n)], out=tcs[:, cs], data0=ones, data1=tsw[:, cs], initial=0.0, op0=ALU.mult, op1=ALU.add)
            S.i("dve", "tensor_tensor", [tcs, tsw], [td3], out=td3[:], in0=tcs[:], in1=tsw[:], op=ALU.subtract)
            LD = 0.6065306597126334
            S.i("act", "activation", [tcs], [te1], out=te1[:], in_=tcs[:], func=AF.Exp, scale=-LD)
            S.i("act", "activation", [tcs], [te2], out=te2[:], in_=tcs[:], func=AF.Exp, scale=LD)
            S.i("act", "activation", [td3], [te3], out=te3[:], in_=td3[:], func=AF.Exp, scale=-LD)
            S.i("act", "activation", [te1], [R(PC, c)], out=PC[:, c, :], in_=te1[:, 127:TW:128], func=AF.Copy)
            S.i("dve", "tensor_scalar", [kp, g.vecs], [tkk], out=tkk[:], in0=kp[:, 0:TW], scalar1=V("k_k", c), scalar2=None, op0=ALU.mult)
            S.i("act", "activation", [tkk], [tkk2], out=tkk2[:], in_=tkk[:], func=AF.Square)
            S.i("pe", "matmul", [tkk2, g.cb], [ssp], out=ssp[:, 0:TW], lhsT=blk1, rhs=tkk2[:], start=True, stop=True)
            S.i("dve", "tensor_scalar", [ssp], [trn], out=trn[:], in0=ssp[:, 0:TW], scalar1=1e-24, scalar2=None, op0=ALU.max)
            S.i("act", "activation", [trn], [trn], out=trn[:], in_=trn[:], func=AF.Ln)
            S.i("act", "activation", [trn], [trn], out=trn[:], in_=trn[:], func=AF.Exp, scale=-0.5)
            S.i("dve", "tensor_tensor", [tkk, trn], [tkkn], out=tkkn[:], in0=tkk[:], in1=trn[:], op=ALU.mult)
            S.i("dve", "tensor_scalar", [tA, g.vecs, om], [tt_], out=tt_[:], in0=tA[:], scalar1=V("k_a", c), scalar2=om[:, 48 + c:49 + c], op0=ALU.mult, op1=ALU.add)
            S.i("dve", "tensor_tensor", [kp, tt_], [tkm], out=tkm[:], in0=kp[:, 0:TW], in1=tt_[:], op=ALU.mult)
            full = slice(0, TW)
            S.i("dve", "scalar_tensor_tensor", [tkkn, te3], [FMK(0, c)], out=FM(0, c, full), in0=tkkn[:], scalar=-1.0, in1=te3[:], op0=ALU.mult, op1=ALU.mult)
            S.i("dve", "tensor_tensor", [tkkn, tA], [ttb], out=ttb[:], in0=tkkn[:], in1=tA[:], op=ALU.mult)
            S.i("dve", "tensor_tensor", [ttb, te2], [FMK(1, c)], out=FM(1, c, full), in0=ttb[:], in1=te2[:], op=ALU.mult)
            S.i("dve", "tensor_tensor", [tkm, te2], [FMK(2, c)], out=FM(2, c, full), in0=tkm[:], in1=te2[:], op=ALU.mult)
            S.i("dve", "tensor_tensor", [rp, te1], [FMK(3, c)], out=FM(3, c, full), in0=rp[:, 0:TW], in1=te1[:], op=ALU.mult)
            S.i("dve", "scalar_tensor_tensor", [rp, g.vecs, tkm], [FMK(4, c)], out=FM(4, c, full), in0=rp[:, 0:TW], scalar=V("r_k", c), in1=tkm[:], op0=ALU.mult, op1=ALU.mult)
        xv = nxm(); mix(3, xv)
        for c in range(NCH):
            wv_ = wget()
            vp = PB[5 + (c % 2)]
            for k in range(NCH):
                S.i("pe", "matmul", [wv_, xv.key(k)], [vp], out=vp[:, 0:TW], lhsT=wv_[:, k, :], rhs=xv.ap(k), start=(k == 0), stop=(k == NCH - 1))
            S.i("act", "activation", [vp], [FMK(5, c)], out=FM(5, c, slice(0, TW)), in_=vp[:, 0:TW], func=AF.Copy)
        if stg < 5:
            continue
        for n in range(2):
            cs = slice(n * 128, (n + 1) * 128)
            for a_, dst in ((0, Atm), (1, Btm), (2, Ktm), (5, Vtm)):
                for c in range(NCH):
                    S.i("pe", "transpose", [FMK(a_, c), g.cb], [pT], out=pT[:, c * 128:(c + 1) * 128], in_=FM(a_, c, cs), identity=ident)
                S.i("dve" if a_ in (0, 2) else "act", "tensor_copy" if a_ in (0, 2) else "activation", [pT], [dst],
                    **(dict(out=dst[:], in_=pT[:]) if a_ in (0, 2) else dict(out=dst[:], in_=pT[:], func=AF.Copy)))
            if stg < 6:
                continue
            for hgp in range(2):
                grp = [2 * hgp, 2 * hgp + 1]
                def HP(h):
                    return slice((h % 2) * 64, (h % 2) * 64 + 64), h // 2
                for gi, hg in enumerate(grp):
                    heads = [4 * hg + q for q in range(4)]
                    specs = [(1, 0, 0, Mk[gi][0], None), (0, 1, 2, Lk[gi][0], None), (2, 0, 0, Mak[gi], None), (1, 3, 1, Mrb, hg), (2, 3, 1, Mrk, hg)]
                    for si, (la, ra, mki, dst, full16) in enumerate(specs):
                        dview = dst[:] if full16 is None else dst[:, hg * 4:(hg + 1) * 4, :]
                        wkey = dst if full16 is None else R(dst, hg)
                        for par in range(2):
                            bk = PB[(2 * si + par) % 6]
                            for qq in range(2):
                                h = heads[2 * qq + par]
                                ps_, c = HP(h)
                                S.i("pe", "matmul", [FMK(la, c), FMK(ra, c)], [bk], out=bk[:, qq * 128:(qq + 1) * 128], lhsT=FM(la, c, cs)[ps_, :], rhs=FM(ra, c, cs)[ps_, :],
                                    start=True, stop=True)
                            S.i("dve", "tensor_tensor", [bk, g.mk], [wkey], out=dview[:, par::2, :],
                                in0=bk[:, 0:256].rearrange("p (a b) -> p a b", b=128), in1=g.mk[:, mki, 0:256].rearrange("p (a b) -> p a b", b=128), op=ALU.mult)
                if stg < 7:
                    continue
                cur = [0, 0]
                NTin = [Mk[0][0], Mk[1][0]]
                for r in range(7):
                    for gi in range(2):
                        Lc, Mc = Lk[gi][cur[gi]], Mk[gi][cur[gi]]
                        Ln, Mn = Lk[gi][1 - cur[gi]], Mk[gi][1 - cur[gi]]
                        bN, bL, bM = PB[3 * gi], PB[3 * gi + 1], PB[3 * gi + 2]
                        if r >= 1:
                            NTo = NTt[gi][r % 2]
                            for q in range(4):
                                S.i("pe", "matmul", [NTin[gi], g.cb], [bN], out=bN[:, q * 128:(q + 1) * 128], lhsT=ident, rhs=NTin[gi][:, q, :], start=True, stop=False)
                                S.i("pe", "matmul", [Mc, g.cb], [bN], out=bN[:, q * 128:(q + 1) * 128], lhsT=ident, rhs=Mc[:, q, :], start=False, stop=False)
                                S.i("pe", "matmul", [Lc, NTin[gi]], [bN], out=bN[:, q * 128:(q + 1) * 128], lhsT=Lc[:, q, :], rhs=NTin[gi][:, q, :], start=False, stop=True)
                        if r < 6:
                            for q in range(4):
                                S.i("pe", "matmul", [Mc, Lc], [bL], out=bL[:, q * 128:(q + 1) * 128], lhsT=Mc[:, q, :], rhs=Lc[:, q, :], start=True, stop=True)
                            for q in range(4):
                                S.i("pe", "matmul", [Lc, Mc], [bM], out=bM[:, q * 128:(q + 1) * 128], lhsT=Lc[:, q, :], rhs=Mc[:, q, :], start=True, stop=True)
                        if r >= 1:
                            if gi == 0:
                                S.i("act", "activation", [bN], [NTo], out=NTo[:].rearrange("p a b -> p (a b)"), in_=bN[:], func=AF.Copy)
                            else:
                                S.i("dve", "tensor_copy", [bN], [NTo], out=NTo[:].rearrange("p a b -> p (a b)"), in_=bN[:])
                            NTin[gi] = NTo
                        if r < 6:
                            S.i("act", "activation", [bL], [Ln], out=Ln[:].rearrange("p a b -> p (a b)"), in_=bL[:], func=AF.Copy)
                            S.i("dve", "tensor_copy", [bM], [Mn], out=Mn[:].rearrange("p a b -> p (a b)"), in_=bM[:])
                            cur[gi] = 1 - cur[gi]
                if stg < 8:
                    continue
                for gi, hg in enumerate(grp):
                    heads = [4 * hg + q for q in range(4)]
                    NTf = NTin[gi]
                    for q, h in enumerate(heads):
                        S.i("pe", "matmul", [Mak[gi], Vtm], [PB[6]], out=PB[6][:, q * 64:(q + 1) * 64], lhsT=Mak[gi][:, q, :], rhs=Vtm[:, h * 64:(h + 1) * 64], start=True, stop=True)
                    S.i("act", "activation", [PB[6]], [Xb], out=Xb[:], in_=PB[6][:, 0:256], func=AF.Copy)
                    for q, h in enumerate(heads):
                        S.i("pe", "matmul", [Xb, g.cb], [PB[6]], out=PB[6][:, 256 + q * 64:256 + (q + 1) * 64], lhsT=ident, rhs=Xb[:, q * 64:(q + 1) * 64], start=True, stop=False)
                        S.i("pe", "matmul", [NTf, Xb], [PB[6]], out=PB[6][:, 256 + q * 64:256 + (q + 1) * 64], lhsT=NTf[:, q, :], rhs=Xb[:, q * 64:(q + 1) * 64], start=False, stop=True)
                    S.i("dve", "tensor_copy", [PB[6]], [R(Vhat, hg)], out=Vhat[:, hg * 256:(hg + 1) * 256], in_=PB[6][:, 256:512])
                    bA = PB[gi]
                    for q, h in enumerate(heads):
                        ps_, c = HP(h)
                        cc = (c % 2) * 128
                        S.i("pe", "matmul", [Atm, NTf], [bA], out=bA[ps_, cc:cc + 128], lhsT=Atm[:, h * 64:(h + 1) * 64], rhs=NTf[:, q, :], start=True, stop=True)
                    S.i("dve", "tensor_tensor", [bA, FMK(0, 2 * hg), FMK(0, 2 * hg + 1)], [R(AhT, hg)], out=AhT[:, 2 * hg:2 * hg + 2, :],
                        in0=bA[:, 0:256].rearrange("p (a b) -> p a b", b=128), in1=xb[:, 2 * hg:2 * hg + 2, cs.start:cs.stop], op=ALU.add)
            if stg < 9:
                continue
            Ubv = Ub[:].rearrange("p (h n) -> p h n", n=64)
            Vhv = Vhat[:].rearrange("p (h n) -> p h n", n=64)
            for par in range(2):
                bk = PB[par]
                for hh in range(8):
                    h = 2 * hh + par
                    ps_, c = slice(par * 64, par * 64 + 64), h // 2
                    S.i("pe", "matmul", [AhT, ST], [bk], out=bk[:, hh * 64:(hh + 1) * 64], lhsT=AhT[ps_, c, :], rhs=ST[ps_, c, :], start=True, stop=True)
            for par in range(2):
                S.i("dve", "tensor_tensor", [PB[par], Vhat], [R(Ub, par)], out=Ubv[:, par::2, :], in0=PB[par][:].rearrange("p (h n) -> p h n", n=64),
                    in1=Vhv[:, par::2, :], op=ALU.add)
            for c in range(NCH):
                S.i("act", "activation", [R(ST, c), PC], [R(STs, c)], out=STs[:, c, :], in_=ST[:, c, :], func=AF.Identity, scale=PC[:, c, n:n + 1])
            for par in range(2):
                bk = PB[2 + par]
                for hh in range(8):
                    h = 2 * hh + par
                    ps_, c = slice(par * 64, par * 64 + 64), h // 2
                    o_ = bk[:, hh * 64:(hh + 1) * 64]
                    S.i("pe", "matmul", [FMK(3, c), ST], [bk], out=o_, lhsT=FM(3, c, cs)[ps_, :], rhs=ST[ps_, c, :], start=True, stop=False)
                    S.i("pe", "matmul", [Mrk, Vtm], [bk], out=o_, lhsT=Mrk[:, h, :], rhs=Vtm[:, h * 64:(h + 1) * 64], start=False, stop=False)
                    S.i("pe", "matmul", [Mrb, Ub], [bk], out=o_, lhsT=Mrb[:, h, :], rhs=Ub[:, h * 64:(h + 1) * 64], start=False, stop=True)
            for h in range(16):
                ps_, c = slice((h % 2) * 64, (h % 2) * 64 + 64), h // 2
                o_ = PB[4][ps_, c * 64:(c + 1) * 64]
                S.i("pe", "matmul", [Ktm, Vtm], [PB[4]], out=o_, lhsT=Ktm[:, h * 64:(h + 1) * 64], rhs=Vtm[:, h * 64:(h + 1) * 64], start=True, stop=False)
                S.i("pe", "matmul", [Btm, Ub], [PB[4]], out=o_, lhsT=Btm[:, h * 64:(h + 1) * 64], rhs=Ub[:, h * 64:(h + 1) * 64], start=False, stop=True)
            for c in range(NCH):
                S.i("dve", "scalar_tensor_tensor", [PB[4], PC, R(STs, c)], [R(ST, c)], out=ST[:, c, :], in0=PB[4][:, c * 64:(c + 1) * 64], scalar=PC[:, c, n:n + 1],
                    in1=STs[:, c, :], op0=ALU.mult, op1=ALU.add)
            for b in range(2):
                S.i("dve", "tensor_reduce", [PB[2 + b]], [R(st4, ("s", b))], out=st4[:, 0, b::2], in_=PB[2 + b][:].rearrange("p (h n) -> p h n", n=64), axis=AX.X, op=ALU.add)
                S.i("act", "activation", [PB[2 + b]], [R(yn, None)], out=yn[:, b * 512:(b + 1) * 512], in_=PB[2 + b][:], func=AF.Square)
                S.i("dve", "tensor_reduce", [R(yn, None)], [R(st4, ("q", b))], out=st4[:, 1, b::2], in_=yn[:, b * 512:(b + 1) * 512].rearrange("p (h n) -> p h n", n=64), axis=AX.X, op=ALU.add)
            S.i("dve", "tensor_scalar", [st4], [R(st4, "m")], out=st4[:, 2, :], in0=st4[:, 0, :], scalar1=1.0 / 64, scalar2=None, op0=ALU.mult)
            S.i("dve", "tensor_tensor", [st4], [R(st4, "v")], out=st4[:, 3, :], in0=st4[:, 2, :], in1=st4[:, 2, :], op=ALU.mult)
            S.i("dve", "scalar_tensor_tensor", [st4], [R(st4, "v")], out=st4[:, 3, :], in0=st4[:, 1, :], scalar=1.0 / 64, in1=st4[:, 3, :], op0=ALU.mult, op1=ALU.subtract)
            S.i("dve", "tensor_scalar", [st4], [R(st4, "v")], out=st4[:, 3, :], in0=st4[:, 3, :], scalar1=64e-5, scalar2=None, op0=ALU.add)
            S.i("act", "activation", [st4], [R(st4, "v")], out=st4[:, 3, :], in_=st4[:, 3, :], func=AF.Ln)
            S.i("act", "activation", [st4], [R(st4, "v")], out=st4[:, 3, :], in_=st4[:, 3, :], func=AF.Exp, scale=-0.5)
            for c in range(NCH):
                S.i("pe", "matmul", [FMK(4, c), g.cb], [PB[5]], out=PB[5][:, 2 * c:2 * c + 2], lhsT=FM(4, c, cs), rhs=hind, start=True, stop=True)
            S.i("dve", "tensor_copy", [PB[5]], [bsum], out=bsum[:], in_=PB[5][:, 0:16])
            for h in range(16):
                b = h % 2
                hs = slice(h * 64, (h + 1) * 64)
                S.i("dve", "tensor_scalar", [PB[2 + b], st4], [R(yn, h)], out=yn[:, hs], in0=PB[2 + b][:, (h // 2) * 64:(h // 2 + 1) * 64],
                    scalar1=st4[:, 2, h:h + 1], scalar2=st4[:, 3, h:h + 1], op0=ALU.subtract, op1=ALU.mult)
            S.i("dve", "tensor_tensor", [yn, rows], [yn], out=yn[:], in0=yn[:], in1=rows[:, 0, :], op=ALU.mult)
            S.i("dve", "tensor_tensor", [yn, rows], [yn], out=yn[:], in0=yn[:], in1=rows[:, 1, :], op=ALU.add)
            for h in range(16):
                hs = slice(h * 64, (h + 1) * 64)
                S.i("dve", "scalar_tensor_tensor", [Vtm, bsum, yn], [R(yn, h)], out=yn[:, hs], in0=Vtm[:, hs], scalar=bsum[:, h:h + 1], in1=yn[:, hs], op0=ALU.mult, op1=ALU.add)
            for b in range(2):
                S.i("pe", "matmul", [sg, g2], [PB[b]], out=PB[b][:], lhsT=sg[:, 0, cs], rhs=g2[:, 0, b * 512:(b + 1) * 512], start=True, stop=False)
                S.i("pe", "matmul", [sg, g2], [PB[b]], out=PB[b][:], lhsT=sg[0:32, 1, cs], rhs=g2[0:32, 1, b * 512:(b + 1) * 512], start=False, stop=True)
                S.i("dve", "tensor_tensor", [yn, PB[b]], [R(ogtm, b)], out=ogtm[:, b * 512:(b + 1) * 512], in0=yn[:, b * 512:(b + 1) * 512], in1=PB[b][:], op=ALU.mult)
            for c in range(NCH):
                S.i("pe", "transpose", [ogtm, g.cb], [pT], out=pT[:, c * 128:(c + 1) * 128], in_=ogtm[:, c * 128:(c + 1) * 128], identity=ident)
            for c in range(NCH):
                S.i("act", "activation", [pT], [R(ogT, (c, n))], out=ogT[:, c, cs], in_=pT[:, c * 128:(c + 1) * 128], func=AF.Copy)
        if stg < 11:
            continue
        for m in range(NCH):
            wo = wget()
            bk = PB[5 + (m % 2)]
            for k in range(NCH):
                S.i("pe", "matmul", [wo, ogT], [bk], out=bk[:, 0:TW], lhsT=wo[:, k, :], rhs=ogT[:, k, :], start=(k == 0), stop=(k == NCH - 1))
            xs = x[:, m, t0:t0 + TW]
            S.i("dve", "scalar_tensor_tensor", [bk, R(x, (m, tt))], [R(x, (m, tt))], out=xs, in0=xs, scalar=ALPHA, in1=bk[:, 0:TW], op0=ALU.mult, op1=ALU.add)
    S.pop()
    S.push()
    layer_norm(g, ("ln1_g%d" % i, "ln1_b%d" % i), range(4))
    S.pop()

def make_consts():
    c = np.zeros((128, 1024), np.float32)
    c[:, 0:128] = 1.0
    c[:, 128:256] = np.eye(128, dtype=np.float32)
    bo = np.zeros((128, 128), np.float32); bo[:64, :64] = 1.0; bo[64:, 64:] = 1.0
    c[:, 256:384] = bo
    t = np.arange(128)
    c[:, 384:512] = np.where(t[None, :] <= t[:, None], 0.0, -30000.0)
    c[:, 512:640] = (t[:, None] < t[None, :]).astype(np.float32)
    c[:, 640:768] = (t[:, None] <= t[None, :]).astype(np.float32)
    c[:64, 768] = 1.0; c[64:, 769] = 1.0
    return c


def make_ret_tables():
    dk = 256
    inv = (1.0 / (np.float32(10000.0) ** np.linspace(0.0, 1.0, dk // 2, dtype=np.float32))).astype(np.float32)
    pos = np.arange(T_SEQ, dtype=np.float32)
    ang = (pos[:, None] * inv[None, :]).astype(np.float32)
    cos = np.cos(ang).astype(np.float32); sin = np.sin(ang).astype(np.float32)
    f = np.arange(dk)
    cosf = cos[:, f // 2].T
    sgn = np.where(f % 2 == 0, -1.0, 1.0).astype(np.float32)
    sinf = (sin[:, f // 2].T * sgn[:, None]).astype(np.float32)
    tab = np.stack([cosf.reshape(2, 128, T_SEQ).transpose(1, 0, 2), sinf.reshape(2, 128, T_SEQ).transpose(1, 0, 2)])
    m = np.zeros((4, 128, 384), np.float32)
    idx = np.arange(128, dtype=np.float64)
    for h in range(4):
        lg = np.log(1.0 - 2.0 ** (-5.0 - h))
        rel = idx[None, :] - idx[:, None]
        m[h, :, 0:128] = np.where(rel >= 0, np.exp(lg * np.maximum(rel, 0)), 0.0) / 16.0
        m[h, :, 128:256] = np.exp(lg * (idx + 1.0))[None, :]
        m[h, :, 256] = np.exp(lg * (127.0 - idx)) / 16.0
        m[h, :, 257] = np.exp(lg * 128.0)
    return np.ascontiguousarray(tab.astype(np.float32)), m


FULL_PLAN = [("rwkv", 0, 0), ("ffn", 0), ("ret", 1), ("ffn", 1), ("moba", 2), ("ffn", 2), ("rwkv", 3, 1), ("ffn", 3)]
_PLAN = FULL_PLAN
_NC_CACHE = {}


def host_inputs(inputs):
    shared = {k: np.ascontiguousarray(np.asarray(inputs[k], np.float32)) for k in WEIGHT_SHAPES}
    shared["vecs"] = pack_vecs(inputs)
    shared["consts"] = make_consts()
    rows = np.zeros((4, 128, 1024), np.float32)
    for j in range(2):
        rows[2 * j] = np.broadcast_to(np.asarray(inputs["rwkv_gn_g"][j], np.float32)[None, :], (128, 1024))
        rows[2 * j + 1] = np.broadcast_to(np.asarray(inputs["rwkv_gn_b"][j], np.float32)[None, :], (128, 1024))
    shared["rows"] = rows
    ti = np.arange(128)
    mk = np.zeros((128, 3, 512), np.float32)
    mk[:, 0, :] = np.tile((ti[:, None] < ti[None, :]).astype(np.float32), (1, 4))
    mk[:, 1, :] = np.tile((ti[:, None] <= ti[None, :]).astype(np.float32), (1, 4))
    mk[:, 2, :] = np.tile((ti[:, None] > ti[None, :]).astype(np.float32), (1, 4))
    shared["mk"] = mk
    tab, msk = make_ret_tables()
    shared["rtab"] = tab
    shared["rmask"] = msk
    wqk = np.asarray(inputs["ret_w_in"][0][:, :2048], np.float32)
    shared["ret_sw"] = np.ascontiguousarray(wqk.reshape(1024, 1024, 2)[:, :, ::-1].reshape(1024, 2048))
    return shared


def run_plan(plan, inputs, x_full, n_cores=8):
    key = repr(plan)
    if key not in _NC_CACHE:
        _NC_CACHE[key] = build_program(plan)
    nc = _NC_CACHE[key]
    shared = host_inputs(inputs)
    in_maps = []
    for b in range(n_cores):
        m = dict(shared)
        m["xT"] = np.ascontiguousarray(np.asarray(x_full[b], np.float32).T)
        in_maps.append(m)
    res = run_bass_kernel_spmd(nc, in_maps, core_ids=list(range(n_cores)))
    return np.stack([np.ascontiguousarray(r["outT"].T) for r in res.results]).astype(np.float32)


def kernel(**inputs):
    return run_plan(_PLAN, inputs, inputs["x"], 8)
```
